# Optimizing a Trainium2 kernel written in Bass

```python
import jax, jax.numpy as jnp
from jax import lax
import numpy as np

D_MODEL = 1024
BATCH = 4
SEQ = 8192
DEPTH = 4

N_MIXERS = 4
D_INNER = D_MODEL
RMS_EPS = 1e-6

RWKV_HEAD = 64
RWKV_HEADS = D_INNER // RWKV_HEAD
RWKV_DECAY_LORA = 64
RWKV_AAA_LORA = 64
RWKV_GN_EPS = 64e-5
RWKV_COLS = 4 * D_INNER + RWKV_DECAY_LORA + RWKV_AAA_LORA
RWKV_SPLITS = (D_INNER, 2 * D_INNER, 3 * D_INNER, 3 * D_INNER + RWKV_DECAY_LORA,
               3 * D_INNER + RWKV_DECAY_LORA + RWKV_AAA_LORA)

HGRN_EXPAND = 128
HGRN_HEADS = D_INNER // HGRN_EXPAND
HGRN_HEAD_V = D_INNER // HGRN_HEADS
HGRN_CHUNK = 32
HGRN_COLS = 4 * D_INNER

CONV_WIDTH = 3
CONV_COLS = 4 * D_INNER

GMLP_CHUNK = 128
GMLP_GROUPS = 8
GMLP_GROUP_CH = D_INNER // GMLP_GROUPS
GMLP_COLS = 3 * D_INNER

N_RWKV = (DEPTH + 3) // 4
N_HGRN = (DEPTH + 2) // 4
N_CONV = (DEPTH + 1) // 4
N_GMLP = DEPTH // 4

kernel_name = "hybrid_rwkv7_hgrn2_shortconv_gmlp_interleaved"


def rmsnorm(x, g):
    xf = x.astype(jnp.float32)
    y = xf * lax.rsqrt(jnp.mean(xf * xf, axis=-1, keepdims=True) + RMS_EPS)
    return (y * g.astype(jnp.float32)).astype(x.dtype)


def shift1(z):
    return jnp.pad(z, ((0, 0), (1, 0), (0, 0)))[:, :-1]


def rwkv7_time_mix(h, w_in, mu, w0, w_w2, a0, w_a2, k_k, k_a, r_k, gn_g, gn_b, w_out):
    B, T, _ = h.shape
    H, N = RWKV_HEADS, RWKV_HEAD
    f32 = jnp.float32
    p = h @ w_in
    p = p + (shift1(p) - p) * mu
    r, k, v, wd, ad, gate = jnp.split(p, RWKV_SPLITS, axis=-1)
    w_log = -jax.nn.softplus(-(w0 + jnp.tanh(wd) @ w_w2).astype(f32)) - 0.5
    decay = jnp.exp(-jnp.exp(w_log))
    a = jax.nn.sigmoid((a0 + ad @ w_a2).astype(f32))
    r = r.astype(f32)
    k = k.astype(f32)
    v = v.astype(f32)
    kk = (k * k_k).reshape(B, T, H, N)
    kk = kk / jnp.maximum(jnp.linalg.norm(kk, axis=-1, keepdims=True), 1e-12)
    k = k * (1.0 + (a - 1.0) * k_a)
    r_h, k_h, v_h, w_h, a_h = (z.reshape(B, T, H, N) for z in (r, k, v, decay, a))
    aa = -kk
    bb = kk * a_h

    def step(S, inp):
        r_t, w_t, k_t, v_t, aa_t, bb_t = inp
        sa = jnp.einsum('bhvk,bhk->bhv', S, aa_t)
        S = S * w_t[:, :, None, :] + sa[..., None] * bb_t[:, :, None, :] + v_t[..., None] * k_t[:, :, None, :]
        return S, jnp.einsum('bhvk,bhk->bhv', S, r_t)

    xs = tuple(jnp.moveaxis(z, 1, 0) for z in (r_h, w_h, k_h, v_h, aa, bb))
    _, y = lax.scan(step, jnp.zeros((B, H, N, N), f32), xs)
    y = jnp.moveaxis(y, 0, 1)
    mean = jnp.mean(y, axis=-1, keepdims=True)
    var = jnp.mean(jnp.square(y - mean), axis=-1, keepdims=True)
    y = ((y - mean) * lax.rsqrt(var + RWKV_GN_EPS)).reshape(B, T, D_INNER) * gn_g + gn_b
    bonus = jnp.sum(r_h * k_h * r_k, axis=-1, keepdims=True) * v_h
    y = (y + bonus.reshape(B, T, D_INNER)) * jax.nn.silu(gate.astype(f32))
    return y.astype(h.dtype) @ w_out


def chunk_gated_linear(q, k, v, g, chunk):
    B, T, H, DK = q.shape
    DV = v.shape[-1]
    NC = T // chunk

    def blk(z):
        return z.reshape(B, NC, chunk, H, z.shape[-1]).transpose(0, 3, 1, 2, 4)

    q, k, v, g = (blk(z.astype(jnp.float32)) for z in (q, k, v, g))
    b = jnp.cumsum(g, axis=3)
    m = b[:, :, :, chunk // 2:chunk // 2 + 1]
    qm = q * jnp.exp(b - m)
    km = k * jnp.exp(m - b)
    att = jnp.einsum('bhnck,bhnsk->bhncs', qm, km)
    causal = jnp.tril(jnp.ones((chunk, chunk), dtype=bool))
    att = jnp.where(causal, att, 0.0)
    o_intra = jnp.einsum('bhncs,bhnsv->bhncv', att, v)
    b_last = b[:, :, :, -1]
    q_in = q * jnp.exp(b)
    k_out = k * jnp.exp(b_last[:, :, :, None, :] - b)

    def step(S, inp):
        q_c, k_c, v_c, d_c = inp
        o = jnp.einsum('bhck,bhkv->bhcv', q_c, S)
        S = S * jnp.exp(d_c)[..., None] + jnp.einsum('bhck,bhcv->bhkv', k_c, v_c)
        return S, o

    xs = tuple(jnp.moveaxis(z, 2, 0) for z in (q_in, k_out, v, b_last))
    _, o_inter = lax.scan(step, jnp.zeros((B, H, DK, DV), jnp.float32), xs)
    o = o_intra + jnp.moveaxis(o_inter, 0, 2)
    return o.transpose(0, 2, 3, 1, 4).reshape(B, T, H, DV)


def hgrn2_mix(h, lb, w_in, gn_g, w_out):
    B, T, _ = h.shape
    f32 = jnp.float32
    p = h @ w_in
    q, f_pre, i_in, gate = jnp.split(p, 4, axis=-1)
    f_pre = f_pre.astype(f32)
    lb = lb.astype(f32)
    log_f = jnp.logaddexp(jnp.log(lb), jnp.log1p(-lb) + jax.nn.log_sigmoid(f_pre))
    k = (1.0 - lb) * jax.nn.sigmoid(-f_pre)
    heads = lambda z: z.reshape(B, T, HGRN_HEADS, -1)
    o = chunk_gated_linear(heads(q), heads(k), heads(i_in), heads(log_f), HGRN_CHUNK)
    o = o * lax.rsqrt(jnp.mean(o * o, axis=-1, keepdims=True) + RMS_EPS)
    o = o.reshape(B, T, D_INNER) * gn_g * jax.nn.silu(gate.astype(f32))
    return o.astype(h.dtype) @ w_out


def hgrn_lower_bound(lb_logits, layer):
    cb = jnp.cumsum(jax.nn.softmax(lb_logits.astype(jnp.float32), axis=0), axis=0)
    return cb[layer] - cb[0]


def short_conv_mix(h, w_in, conv_w, w_out):
    T = h.shape[1]
    p = h @ w_in
    b_gate, c_gate, z, gate = jnp.split(p, 4, axis=-1)
    y = c_gate * z
    yp = jnp.pad(y, ((0, 0), (CONV_WIDTH - 1, 0), (0, 0)))
    yc = sum(conv_w[j] * yp[:, j:j + T] for j in range(CONV_WIDTH))
    return (b_gate * yc * jax.nn.silu(gate)) @ w_out


def gmlp_chunk_mix(h, w_in, v_g, w_s, b_s, w_out):
    B, T, _ = h.shape
    NC = T // GMLP_CHUNK
    p = h @ w_in
    u, v, gate = jnp.split(p, 3, axis=-1)
    v = rmsnorm(v, v_g)
    vb = v.reshape(B, NC, GMLP_CHUNK, GMLP_GROUPS, GMLP_GROUP_CH)
    ws = w_s * jnp.tril(jnp.ones((GMLP_CHUNK, GMLP_CHUNK), w_s.dtype))
    s = jnp.einsum('gts,bnsgc->bntgc', ws, vb) + b_s.T[None, None, :, :, None]
    s = s.reshape(B, T, D_INNER)
    return (u * s * jax.nn.silu(gate)) @ w_out


def setup_inputs(seed: int = 0) -> dict:
    key = jax.random.key(seed)
    ks = iter(jax.random.split(key, 40))
    f32 = jnp.float32

    def nrm(shape, scale):
        return jax.random.normal(next(ks), shape, f32) * scale

    D, DI = D_MODEL, D_INNER
    return {
        "x": nrm((BATCH, SEQ, D), 1.0),
        "norm_g": 1.0 + nrm((DEPTH, D), 0.02),
        "final_g": 1.0 + nrm((D,), 0.02),
        "rwkv_w_in": nrm((N_RWKV, D, RWKV_COLS), D ** -0.5),
        "rwkv_mu": jax.random.uniform(next(ks), (N_RWKV, RWKV_COLS), f32),
        "rwkv_w0": nrm((N_RWKV, DI), 0.5),
        "rwkv_w_w2": nrm((N_RWKV, RWKV_DECAY_LORA, DI), 0.5 * RWKV_DECAY_LORA ** -0.5),
        "rwkv_a0": nrm((N_RWKV, DI), 0.5),
        "rwkv_w_a2": nrm((N_RWKV, RWKV_AAA_LORA, DI), RWKV_AAA_LORA ** -0.5),
        "rwkv_k_k": 0.85 + nrm((N_RWKV, DI), 0.05),
        "rwkv_k_a": 1.0 + nrm((N_RWKV, DI), 0.05),
        "rwkv_r_k": nrm((N_RWKV, RWKV_HEADS, RWKV_HEAD), 0.1),
        "rwkv_gn_g": 1.0 + nrm((N_RWKV, DI), 0.02),
        "rwkv_gn_b": nrm((N_RWKV, DI), 0.02),
        "rwkv_w_out": nrm((N_RWKV, DI, D), DI ** -0.5),
        "hgrn_lb_logits": nrm((DEPTH, DI), 0.1),
        "hgrn_w_in": nrm((N_HGRN, D, HGRN_COLS), D ** -0.5),
        "hgrn_gn_g": 1.0 + nrm((N_HGRN, DI), 0.02),
        "hgrn_w_out": nrm((N_HGRN, DI, D), DI ** -0.5),
        "conv_w_in": nrm((N_CONV, D, CONV_COLS), D ** -0.5),
        "conv_w": nrm((N_CONV, CONV_WIDTH, DI), CONV_WIDTH ** -0.5),
        "conv_w_out": nrm((N_CONV, DI, D), DI ** -0.5),
        "gmlp_w_in": nrm((N_GMLP, D, GMLP_COLS), D ** -0.5),
        "gmlp_v_g": 1.0 + nrm((N_GMLP, DI), 0.02),
        "gmlp_w_s": nrm((N_GMLP, GMLP_GROUPS, GMLP_CHUNK, GMLP_CHUNK), GMLP_CHUNK ** -0.5),
        "gmlp_b_s": 1.0 + nrm((N_GMLP, GMLP_GROUPS, GMLP_CHUNK), 0.1),
        "gmlp_w_out": nrm((N_GMLP, DI, D), DI ** -0.5),
    }


def reference(x, norm_g, final_g,
              rwkv_w_in, rwkv_mu, rwkv_w0, rwkv_w_w2, rwkv_a0, rwkv_w_a2, rwkv_k_k, rwkv_k_a,
              rwkv_r_k, rwkv_gn_g, rwkv_gn_b, rwkv_w_out,
              hgrn_lb_logits, hgrn_w_in, hgrn_gn_g, hgrn_w_out,
              conv_w_in, conv_w, conv_w_out,
              gmlp_w_in, gmlp_v_g, gmlp_w_s, gmlp_b_s, gmlp_w_out):
    h = x
    for i in range(DEPTH):
        hn = rmsnorm(h, norm_g[i])
        m, j = i % N_MIXERS, i // N_MIXERS
        if m == 0:
            y = rwkv7_time_mix(hn, rwkv_w_in[j], rwkv_mu[j], rwkv_w0[j], rwkv_w_w2[j], rwkv_a0[j],
                               rwkv_w_a2[j], rwkv_k_k[j], rwkv_k_a[j], rwkv_r_k[j],
                               rwkv_gn_g[j], rwkv_gn_b[j], rwkv_w_out[j])
        elif m == 1:
            y = hgrn2_mix(hn, hgrn_lower_bound(hgrn_lb_logits, i), hgrn_w_in[j], hgrn_gn_g[j], hgrn_w_out[j])
        elif m == 2:
            y = short_conv_mix(hn, conv_w_in[j], conv_w[j], conv_w_out[j])
        else:
            y = gmlp_chunk_mix(hn, gmlp_w_in[j], gmlp_v_g[j], gmlp_w_s[j], gmlp_b_s[j], gmlp_w_out[j])
        h = h + y.astype(h.dtype)
    return rmsnorm(h, final_g)
```

```python
import numpy as np
from contextlib import ExitStack
import concourse.bass as bass
import concourse.mybir as mybir
from concourse.bass_utils import run_bass_kernel_spmd

F32 = mybir.dt.float32
BF16 = mybir.dt.bfloat16
ALU = mybir.AluOpType
AF = mybir.ActivationFunctionType
AX = mybir.AxisListType

D = 1024
NKC = 8
RMS_EPS = 1e-6
GN_EPS = 64e-5


class Prog:
    LIMIT = 30000

    def __init__(self, nc, es, n_dma_sems=24):
        self.nc = nc
        self.es = es
        self.engs = {'pe': nc.tensor, 'act': nc.scalar, 'dve': nc.vector,
                     'pool': nc.gpsimd, 'sp': nc.sync}
        self.sems = {}
        self.epoch = {k: 0 for k in self.engs}
        self.cnt = {k: 0 for k in self.engs}
        for k in self.engs:
            self.sems[(k, 0)] = es.enter_context(nc.semaphore(f"s_{k}_0"))
        self.dma_sems = []
        for i in range(n_dma_sems):
            key = ('dma', i)
            self.sems[key] = es.enter_context(nc.semaphore(f"s_dma_{i}"))
            self.cnt[key] = 0
            self.dma_sems.append(key)
        self.dma_rr = 0
        self.waited = {k: {} for k in self.engs}
        self.bufs = {}
        self.n_wait = 0
        self.n_ins = 0

    def _deps(self, reads, writes):
        deps = set()
        for k in reads:
            b = self.bufs.get(k)
            if b and b['w']:
                deps.add(b['w'])
        for k in writes:
            b = self.bufs.get(k)
            if b:
                if b['w']:
                    deps.add(b['w'])
                deps.update(b['r'])
        return deps

    def _wait(self, eng, deps):
        e = self.engs[eng]
        best = {}
        for (sk, v) in deps:
            if sk[0] == eng and eng == 'pe':
                continue
            if best.get(sk, 0) < v:
                best[sk] = v
        for sk, v in best.items():
            if self.waited[eng].get(sk, 0) >= v:
                continue
            e.wait_ge(self.sems[sk], v)
            self.waited[eng][sk] = v
            self.n_wait += 1

    def _record(self, tok, reads, writes):
        for k in reads:
            b = self.bufs.setdefault(k, {'w': None, 'r': []})
            b['r'].append(tok)
            if len(b['r']) > 64:
                best = {}
                for (sk, v) in b['r']:
                    if best.get(sk, 0) < v:
                        best[sk] = v
                b['r'] = list(best.items())
        for k in writes:
            b = self.bufs.setdefault(k, {'w': None, 'r': []})
            b['w'] = tok
            b['r'] = []

    def op(self, eng, fn, reads=(), writes=()):
        deps = self._deps(reads, writes)
        self._wait(eng, deps)
        ins = fn(self.engs[eng])
        if self.cnt[eng] >= self.LIMIT:
            self.epoch[eng] += 1
            ep = self.epoch[eng]
            self.sems[(eng, ep)] = self.es.enter_context(self.nc.semaphore(f"s_{eng}_{ep}"))
            self.cnt[eng] = 0
        sk = (eng, self.epoch[eng])
        self.cnt[eng] += 1
        ins.then_inc(self.sems[sk], 1)
        self._record((sk, self.cnt[eng]), reads, writes)
        self.n_ins += 1
        return ins

    def dma(self, eng, out, in_, reads=(), writes=(), **kw):
        deps = self._deps(reads, writes)
        sk = self.dma_sems[self.dma_rr]
        self.dma_rr = (self.dma_rr + 1) % len(self.dma_sems)
        if self.cnt[sk] > 0:
            deps.add((sk, self.cnt[sk]))
        self._wait(eng, deps)
        ins = self.engs[eng].dma_start(out=out, in_=in_, **kw)
        self.cnt[sk] += 16
        ins.then_inc(self.sems[sk], 16)
        self._record((sk, self.cnt[sk]), reads, writes)
        self.n_ins += 1
        return ins

    def all_tokens(self):
        deps = set()
        for k, b in self.bufs.items():
            if b['w']:
                deps.add(b['w'])
            deps.update(b['r'])
        return deps

    def barrier(self):
        deps = self.all_tokens()
        for eng in self.engs:
            d = set(x for x in deps)
            self._wait(eng, d)

    def finish(self, eng='sp'):
        self._wait(eng, self.all_tokens())


class Builder:
    def __init__(self, T, layers, do_final=True):
        self.T = T
        self.layers = layers
        self.do_final = do_final
        self.nc = bass.Bass("TRN2", target_bir_lowering=False)
        self.inputs = {}

    def din(self, name, shape):
        t = self.nc.dram_tensor(name, list(shape), F32, kind="ExternalInput").ap()
        self.inputs[name] = t
        return t

    def sb(self, name, shape, dt=F32):
        return self.es.enter_context(self.nc.sbuf_tensor(name, list(shape), dt))

    def lsb(self, name, shape, dt=F32):
        return self.les.enter_context(self.nc.sbuf_tensor(f"{name}_{self.lname}", list(shape), dt))

    def next_ps(self):
        pool = self.ps_pool
        i = pool[self.ps_rr % len(pool)]
        self.ps_rr += 1
        return self.psums[i], f"ps{i}"

    def tt(self, eng, out, in0, in1, op, reads, writes):
        return self.p.op(eng, lambda e: e.tensor_tensor(out=out, in0=in0, in1=in1, op=op), reads, writes)

    def ts(self, eng, out, in0, s1, s2, op0, op1, reads, writes):
        if s2 is None:
            return self.p.op(eng, lambda e: e.tensor_scalar(out=out, in0=in0, scalar1=s1, scalar2=None, op0=op0), reads, writes)
        return self.p.op(eng, lambda e: e.tensor_scalar(out=out, in0=in0, scalar1=s1, scalar2=s2, op0=op0, op1=op1), reads, writes)

    def stt(self, eng, out, in0, scalar, in1, op0, op1, reads, writes):
        eng = 'dve'
        return self.p.op(eng, lambda e: e.scalar_tensor_tensor(out=out, in0=in0, scalar=scalar, in1=in1, op0=op0, op1=op1), reads, writes)

    def act(self, out, in_, func, reads, writes, bias=None, scale=1.0, accum_out=None):
        kw = {}
        if bias is not None:
            kw['bias'] = bias
        if accum_out is not None:
            kw['accum_out'] = accum_out
        return self.p.op('act', lambda e: e.activation(out=out, in_=in_, func=func, scale=scale, **kw), reads, writes)

    def mm(self, out, lhsT, rhs, start, stop, reads, writes):
        return self.p.op('pe', lambda e: e.matmul(out, lhsT=lhsT, rhs=rhs, start=start, stop=stop), reads, writes)

    def copy(self, eng, out, in_, reads, writes):
        if eng == 'act':
            return self.p.op('act', lambda e: e.copy(out=out, in_=in_), reads, writes)
        return self.p.op(eng, lambda e: e.tensor_copy(out=out, in_=in_), reads, writes)

    def load_weight_bf16(self, dst, dst_key, src, ncols, src_c0=0, dst_c0=0, scale_bc=None):
        p = self.p
        CH = 1056 if ncols % 1056 == 0 else (1024 if ncols % 1024 == 0 else ncols)
        for kc in range(NKC):
            for c0 in range(0, ncols, CH):
                i = self.stage_i
                self.stage_i += 1
                st = self.stage[i % 2]
                sk = f"stage{i % 2}"
                p.dma('sp', st[:, 0:CH], src[kc * 128:(kc + 1) * 128, src_c0 + c0:src_c0 + c0 + CH], reads=[], writes=[sk])
                eng = ['dve', 'pool'][i % 2]
                if scale_bc is None:
                    self.copy(eng, dst[:, kc, dst_c0 + c0:dst_c0 + c0 + CH], st[:, 0:CH], [sk], [dst_key])
                else:
                    self.tt(eng, dst[:, kc, dst_c0 + c0:dst_c0 + c0 + CH], st[:, 0:CH], scale_bc[:, c0:c0 + CH], ALU.mult,
                            [sk, 'bc_tiles'], [dst_key])

    def rms_rstd(self, src, src_key, TT, tag):
        sqb, rstd = self.sqb, self.rstd
        for kc in range(NKC):
            if kc % 2 == 0:
                self.act(sqb[:, kc, :TT], src[:, kc, :TT], AF.Square, [src_key], ['sqb'])
            else:
                self.tt('pool', sqb[:, kc, :TT], src[:, kc, :TT], src[:, kc, :TT], ALU.mult, [src_key], ['sqb'])
        ps, pk = self.next_ps()
        for kc in range(NKC):
            self.mm(ps[:, :TT], self.ones_bf[:], sqb[:, kc, :TT], kc == 0, kc == NKC - 1, ['sqb', 'ones_bf'], [pk])
        self.act(rstd[:, :TT], ps[:, :TT], AF.Ln, [pk, 'consts'], ['rstd'], bias=self.epsc[:, 0:1], scale=1.0 / D)
        self.act(rstd[:, :TT], rstd[:, :TT], AF.Exp, ['rstd'], ['rstd'], scale=-0.5)
        return rstd, 'rstd'

    def run_layer(self, li, kind, TT, w_in_cols, mixer_setup, mixer_tile, is_last):
        p = self.p
        T = self.T
        ntiles = T // TT
        with ExitStack() as les:
            self.les = les
            self.lname = f"L{li}"
            self.TT = TT
            self.W_in = self.lsb("W_in", [128, NKC, w_in_cols], BF16)
            self.W_out = self.lsb("W_out", [128, NKC, D], BF16)
            self.hT = [self.lsb(f"hT{i}", [128, NKC, TT], F32) for i in range(2)]
            self.sqb = self.lsb("sqb", [128, NKC, TT], BF16)
            self.rstd = self.lsb("rstd", [128, TT], F32)
            self.hn = self.lsb("hn", [128, NKC, TT + 1], BF16)
            self.yTt = self.lsb("yTt", [128, NKC, TT], BF16)
            self.stage = [self.lsb(f"stage{i}", [128, 1056], F32) for i in range(2)]
            self.stage_i = 0
            if mixer_setup() is None:
                self.load_weight_bf16(self.W_in, 'W_in', self.w_in_dram[kind], w_in_cols)
            self.load_weight_bf16(self.W_out, 'W_out', self.w_out_dram[kind], D)
            p.op('pool', lambda e: e.memset(self.hn[:, :, 0:1], 0.0), [], ['hn'])

            def load(ti):
                buf = self.hT[ti % 2]
                src = self.xT if self.first_layer else self.yT
                p.dma('sp', buf[:], src.rearrange("(c p) t -> p c t", p=128)[:, :, ti * TT:(ti + 1) * TT],
                      reads=[('hd', ti * TT // 128 + i) for i in range(TT // 128)], writes=[f"hT{ti % 2}"])

            load(0)
            for ti in range(ntiles):
                if ti + 1 < ntiles:
                    load(ti + 1)
                h = self.hT[ti % 2]
                hk = f"hT{ti % 2}"
                rstd, rk = self.rms_rstd(h, hk, TT, 'in')
                g = self.norm_g
                if ti > 0:
                    self.copy('pool', self.hn[:, :, 0:1], self.hn[:, :, TT:TT + 1], ['hn'], ['hn'])
                for kc in range(NKC):
                    self.stt('dve', self.hn[:, kc, 1:TT + 1], h[:, kc, :], g[:, li, kc:kc + 1], rstd[:, :TT],
                             ALU.mult, ALU.mult, [hk, rk, 'consts'], ['hn'])
                mixer_tile(ti)
                for j in range(NKC):
                    ps, pk = self.next_ps()
                    for kc in range(NKC):
                        self.mm(ps[:, :TT], self.W_out[:, kc, j * 128:(j + 1) * 128], self.yTt[:, kc, :TT],
                                kc == 0, kc == NKC - 1, ['W_out', 'yT'], [pk])
                    self.tt('dve', h[:, j, :], h[:, j, :], ps[:, :TT], ALU.add, [hk, pk], [hk])
                if is_last and self.do_final:
                    rstd, rk = self.rms_rstd(h, hk, TT, 'fin')
                    for kc in range(NKC):
                        self.stt('dve' if kc % 2 == 0 else 'pool', h[:, kc, :], h[:, kc, :], self.final_g[:, kc:kc + 1], rstd[:, :TT],
                                 ALU.mult, ALU.mult, [hk, rk, 'consts'], [hk])
                p.dma('sp', self.yT.rearrange("(c p) t -> p c t", p=128)[:, :, ti * TT:(ti + 1) * TT], h[:],
                      reads=[hk], writes=[('hd', (ti * TT) // 128 + i) for i in range(max(1, TT // 128))])
            self.first_layer = False
            p.barrier()
        self.les = None

    def conv_layer(self, li, is_last):
        TT = 512

        def setup():
            self.yext = self.lsb("yext", [128, NKC, TT + 2], F32)
            self.zs = [self.lsb(f"zs{i}", [128, TT], F32) for i in range(2)]
            self.acc = [self.lsb(f"acc{i}", [128, TT], F32) for i in range(2)]
            self.sg = [self.lsb(f"sg{i}", [128, TT], F32) for i in range(2)]
            self.p.op('pool', lambda e: e.memset(self.yext[:, :, 0:2], 0.0), [], [('yext', j) for j in range(NKC)])

        def tile(ti):
            W = self.W_in
            cw = self.conv_w
            for j in range(NKC):
                zs, acc, sg = self.zs[j % 2], self.acc[j % 2], self.sg[j % 2]
                zk, ak, gk = f"zs{j % 2}", f"acc{j % 2}", f"sg{j % 2}"
                pss = []
                for blk in range(4):
                    ps, pk = self.next_ps()
                    col0 = blk * D + j * 128
                    for kc in range(NKC):
                        self.mm(ps[:, :TT], W[:, kc, col0:col0 + 128], self.hn[:, kc, 1:TT + 1], kc == 0, kc == NKC - 1,
                                ['W_in', 'hn'], [pk])
                    pss.append((ps, pk))
                (pb, pbk), (pc, pck), (pz, pzk), (pg, pgk) = pss
                yk = ('yext', j)
                self.copy('act', zs[:], pz[:, :TT], [pzk], [zk])
                if ti > 0:
                    self.copy('pool', self.yext[:, j, 0:2], self.yext[:, j, TT:TT + 2], [yk], [yk])
                self.tt('dve', self.yext[:, j, 2:TT + 2], pc[:, :TT], zs[:], ALU.mult, [pck, zk], [yk])
                self.ts('pool', acc[:], self.yext[:, j, 2:TT + 2], cw[:, j, 2:3], None, ALU.mult, None, [yk, 'consts'], [ak])
                self.stt('pool', acc[:], self.yext[:, j, 1:TT + 1], cw[:, j, 1:2], acc[:], ALU.mult, ALU.add, [yk, ak, 'consts'], [ak])
                self.stt('pool', acc[:], self.yext[:, j, 0:TT], cw[:, j, 0:1], acc[:], ALU.mult, ALU.add, [yk, ak, 'consts'], [ak])
                self.act(sg[:], pg[:, :TT], AF.Silu, [pgk], [gk])
                self.tt('dve', acc[:], pb[:, :TT], acc[:], ALU.mult, [pbk, ak], [ak])
                self.tt('pool', self.yTt[:, j, :], acc[:], sg[:], ALU.mult, [ak, gk], ['yT'])

        self.run_layer(li, 'conv', TT, 4 * D, setup, tile, is_last)

    def gmlp_layer(self, li, is_last):
        TT = 512

        def setup():
            p = self.p
            self.wsT = self.lsb("wsT", [128, 8, 128], F32)
            self.bs_bc = self.lsb("bs_bc", [128, 8, TT], F32)
            self.vg_bc = self.lsb("vg_bc", [128, D], F32)
            self.vn = [self.lsb(f"vn{i}", [128, D], F32) for i in range(TT // 128)]
            self.vss = self.lsb("vss", [128, 4], F32)
            self.junk = self.lsb("junk", [128, 512], F32)
            self.s_sb = [self.lsb(f"s_sb{i}", [128, TT], F32) for i in range(2)]
            self.sg = [self.lsb(f"sg{i}", [128, TT], F32) for i in range(2)]
            p.dma('sp', self.wsT[:], self.inputs['gmlp_wsT'], [], ['wsT'])
            for g in range(8):
                p.op('pool', lambda e: e.affine_select(out=self.wsT[:, g, :], in_=self.wsT[:, g, :], pattern=[[1, 128]],
                                                       compare_op=ALU.is_ge, fill=0.0, base=0, channel_multiplier=-1),
                     ['wsT'], ['wsT'])
            for r in range(TT // 128):
                p.dma('sp', self.bs_bc[:, :, r * 128:(r + 1) * 128],
                      self.inputs['gmlp_bs'].partition_broadcast(128), [], ['bs_bc'])
            p.dma('sp', self.vg_bc[:], self.inputs['gmlp_vg'].partition_broadcast(128), [], ['vg_bc'])

        def tile(ti):
            W = self.W_in
            nblk = TT // 128
            for blk in range(nblk):
                vn = self.vn[blk]
                vk = f"vn{blk}"
                halves = []
                for hf in range(2):
                    ps, pk = self.next_ps()
                    for kc in range(NKC):
                        self.mm(ps[:, :512], self.hn[:, kc, 1 + blk * 128:1 + (blk + 1) * 128],
                                W[:, kc, D + hf * 512:D + (hf + 1) * 512], kc == 0, kc == NKC - 1, ['W_in', 'hn'], [pk])
                    halves.append((ps, pk))
                for hf, (ps, pk) in enumerate(halves):
                    self.act(self.junk[:], ps[:, :512], AF.Square, [pk], ['junk', 'vss'], accum_out=self.vss[:, hf:hf + 1])
                self.tt('dve', self.vss[:, 2:3], self.vss[:, 0:1], self.vss[:, 1:2], ALU.add, ['vss'], ['vss'])
                self.act(self.vss[:, 3:4], self.vss[:, 2:3], AF.Ln, ['vss', 'consts'], ['vss'], bias=self.epsc[:, 0:1], scale=1.0 / D)
                self.act(self.vss[:, 3:4], self.vss[:, 3:4], AF.Exp, ['vss'], ['vss'], scale=-0.5)
                for hf, (ps, pk) in enumerate(halves):
                    self.stt('dve', vn[:, hf * 512:(hf + 1) * 512], ps[:, :512], self.vss[:, 3:4],
                             self.vg_bc[:, hf * 512:(hf + 1) * 512], ALU.mult, ALU.mult, [pk, 'vss', 'vg_bc'], [vk])
            for j in range(NKC):
                s_sb, sg = self.s_sb[j % 2], self.sg[j % 2]
                sk, gk = f"s_sb{j % 2}", f"sg{j % 2}"
                ps, pk = self.next_ps()
                for blk in range(nblk):
                    self.mm(ps[:, blk * 128:(blk + 1) * 128], self.vn[blk][:, j * 128:(j + 1) * 128], self.wsT[:, j, :], True, True,
                            [f"vn{blk}", 'wsT'], [pk])
                self.tt('dve', s_sb[:], ps[:, :TT], self.bs_bc[:, j, :], ALU.add, [pk, 'bs_bc'], [sk])
                pu, puk = self.next_ps()
                for kc in range(NKC):
                    self.mm(pu[:, :TT], W[:, kc, j * 128:(j + 1) * 128], self.hn[:, kc, 1:TT + 1], kc == 0, kc == NKC - 1,
                            ['W_in', 'hn'], [puk])
                pg, pgk = self.next_ps()
                for kc in range(NKC):
                    self.mm(pg[:, :TT], W[:, kc, 2 * D + j * 128:2 * D + (j + 1) * 128], self.hn[:, kc, 1:TT + 1], kc == 0,
                            kc == NKC - 1, ['W_in', 'hn'], [pgk])
                self.act(sg[:], pg[:, :TT], AF.Silu, [pgk], [gk])
                self.tt('dve', s_sb[:], pu[:, :TT], s_sb[:], ALU.mult, [puk, sk], [sk])
                self.tt('pool', self.yTt[:, j, :], s_sb[:], sg[:], ALU.mult, [sk, gk], ['yT'])

        self.run_layer(li, 'gmlp', TT, 3 * D, setup, tile, is_last)


    def make_ident(self, ident, key):
        p = self.p
        p.op('pool', lambda e: e.memset(ident[:], 1.0), [], [key])
        p.op('pool', lambda e: e.affine_select(out=ident[:], in_=ident[:], pattern=[[-1, 128]], compare_op=ALU.is_equal,
                                               fill=0.0, base=0, channel_multiplier=1), [key], [key])

    def make_block_masks(self, C, maskT, colmask, rowmask, strict=False):
        p = self.p
        nch = 128 // C
        if maskT is not None:
            p.op('pool', lambda e: e.memset(maskT[:], 1.0), [], ['masks'])
            p.op('pool', lambda e: e.affine_select(out=maskT[:], in_=maskT[:], pattern=[[1, 128]], compare_op=ALU.is_ge if not strict else ALU.is_gt,
                                                   fill=0.0, base=0, channel_multiplier=-1), ['masks'], ['masks'])
            for c in range(1, nch):
                p.op('pool', lambda e, c=c: e.affine_select(out=maskT[:, c * C:(c + 1) * C], in_=maskT[:, c * C:(c + 1) * C], pattern=[[0, C]],
                                                            compare_op=ALU.is_ge, fill=0.0, base=-c * C, channel_multiplier=1), ['masks'], ['masks'])
        if colmask is not None:
            p.op('pool', lambda e: e.memset(colmask[:], 0.0), [], ['masks'])
            for c in range(nch):
                p.op('pool', lambda e, c=c: e.memset(colmask[:, c, c * C:(c + 1) * C], 1.0), ['masks'], ['masks'])
        if rowmask is not None:
            p.op('pool', lambda e: e.memset(rowmask[:], 1.0), [], ['masks'])
            for c in range(nch):
                p.op('pool', lambda e, c=c: e.affine_select(out=rowmask[:, c:c + 1], in_=rowmask[:, c:c + 1], pattern=[[0, 1]],
                                                            compare_op=ALU.is_ge, fill=0.0, base=-c * C, channel_multiplier=1), ['masks'], ['masks'])
                p.op('pool', lambda e, c=c: e.affine_select(out=rowmask[:, c:c + 1], in_=rowmask[:, c:c + 1], pattern=[[0, 1]],
                                                            compare_op=ALU.is_ge, fill=0.0, base=c * C + C - 1, channel_multiplier=-1), ['masks'], ['masks'])

    def hgrn_layer(self, li, is_last):
        TT = 512
        C = 32
        NB = TT // 128
        NCH = TT // C

        def setup():
            p = self.p
            L = self.lsb
            self.ident = L("ident", [128, 128], F32)
            self.make_ident(self.ident, 'ident')
            self.maskT = L("maskT", [128, 128], F32)
            self.colmask = L("colmask", [128, 4, 128], F32)
            self.rowmask = L("rowmask", [128, 4], F32)
            self.make_block_masks(C, self.maskT, self.colmask, self.rowmask)
            self.resetm = L("resetm", [128, TT], F32)
            p.op('pool', lambda e: e.memset(self.resetm[:], 1.0), [], ['masks'])
            p.op('pool', lambda e: e.memset(self.resetm[:].rearrange("p (n c) -> p n c", c=C)[:, :, 0:1], 0.0), ['masks'], ['masks'])
            self.gn_bc = L("gn_bc", [128, D], F32)
            p.dma('sp', self.gn_bc[:], self.inputs['hgrn_gn_g'].partition_broadcast(128), [], ['gn_bc'])
            self.lbl = L("lbl", [128, 4, NKC], F32)
            self.lbt = L("lbt", [128, 4, NKC], F32)
            p.dma('sp', self.lbl[:], self.inputs['hgrn_lbl'], [], ['lbl'])
            self.act(self.lbl[:], self.lbl[:], AF.Exp, ['lbl'], ['lbl'])
            self.tt('dve', self.lbt[:, 0, :], self.lbl[:, 0, :], self.lbl[:, 1, :], ALU.add, ['lbl'], ['lbt'])
            self.tt('dve', self.lbt[:, 0, :], self.lbt[:, 0, :], self.lbl[:, 2, :], ALU.add, ['lbl', 'lbt'], ['lbt'])
            self.tt('dve', self.lbt[:, 0, :], self.lbt[:, 0, :], self.lbl[:, 3, :], ALU.add, ['lbl', 'lbt'], ['lbt'])
            p.op('dve', lambda e: e.reciprocal(out=self.lbt[:, 3, :], in_=self.lbt[:, 0, :]), ['lbt'], ['lbt'])
            p.op('dve', lambda e: e.memset(self.lbt[:, 1, :], 0.0), ['lbt'], ['lbt'])
            for i in range(1, li + 1):
                self.tt('dve', self.lbt[:, 1, :], self.lbt[:, 1, :], self.lbl[:, i, :], ALU.add, ['lbl', 'lbt'], ['lbt'])
            self.tt('dve', self.lbt[:, 1, :], self.lbt[:, 1, :], self.lbt[:, 3, :], ALU.mult, ['lbt'], ['lbt'])
            self.ts('dve', self.lbt[:, 2, :], self.lbt[:, 1, :], -1.0, 1.0, ALU.mult, ALU.add, ['lbt'], ['lbt'])
            self.S = L("S_hgrn", [128, NKC, 2, 128], F32)
            p.op('pool', lambda e: e.memset(self.S[:], 0.0), [], [('S', j, b) for j in range(NKC) for b in range(2)])
            names = ['f', 'kk', 'bb', 'eb', 'enb', 'qe', 'ke', 'ko', 'dd', 'sgate']
            self.tmp = {n: L(f"h_{n}", [128, TT], F32) for n in names}
            self.qem = L("qem", [128, 4, TT], F32)
            self.v_sb = L("v_sb", [128, NB, 128], F32)
            self.attm = [L(f"attm{i}", [128, 128], F32) for i in range(2)]
            self.kom = [L(f"kom{i}", [128, 4, 128], F32) for i in range(2)]
            self.on = [L(f"on{i}", [128, 128], F32) for i in range(2)]
            self.dec = L("dec", [128, NCH], F32)
            self.oss = L("oss", [128, 4], F32)
            self.junk = L("junk", [128, 128], F32)
            self.ps_pool = [0, 1, 2, 3]

        def tile(ti):
            W = self.W_in
            t = self.tmp
            P = self.psums
            lb, oml = self.lbt[:, 1, :], self.lbt[:, 2, :]
            for j in range(NKC):
                pq, pqk = self.next_ps()
                for kc in range(NKC):
                    self.mm(pq[:, :TT], W[:, kc, j * 128:(j + 1) * 128], self.hn[:, kc, 1:TT + 1], kc == 0, kc == NKC - 1, ['W_in', 'hn'], [pqk])
                pf, pfk = self.next_ps()
                for kc in range(NKC):
                    self.mm(pf[:, :TT], W[:, kc, D + j * 128:D + (j + 1) * 128], self.hn[:, kc, 1:TT + 1], kc == 0, kc == NKC - 1, ['W_in', 'hn'], [pfk])
                pg, pgk = self.next_ps()
                for kc in range(NKC):
                    self.mm(pg[:, :TT], W[:, kc, 3 * D + j * 128:3 * D + (j + 1) * 128], self.hn[:, kc, 1:TT + 1], kc == 0, kc == NKC - 1, ['W_in', 'hn'], [pgk])
                pv, pvk = self.next_ps()
                for blk in range(NB):
                    for kc in range(NKC):
                        self.mm(pv[:, blk * 128:(blk + 1) * 128], self.hn[:, kc, 1 + blk * 128:1 + (blk + 1) * 128],
                                W[:, kc, 2 * D + j * 128:2 * D + (j + 1) * 128], kc == 0, kc == NKC - 1, ['W_in', 'hn'], [pvk])
                self.act(t['f'][:], pf[:, :TT], AF.Sigmoid, [pfk], ['t_f'])
                self.act(t['sgate'][:], pg[:, :TT], AF.Silu, [pgk], ['t_sgate'])
                self.copy('act', self.v_sb[:].rearrange("p b v -> p (b v)"), pv[:, :TT], [pvk], ['v_sb'])
                self.ts('dve', t['f'][:], t['f'][:], oml[:, j:j + 1], lb[:, j:j + 1], ALU.mult, ALU.add, ['t_f', 'lbt'], ['t_f'])
                self.ts('pool', t['kk'][:], t['f'][:], -1.0, 1.0, ALU.mult, ALU.add, ['t_f'], ['t_kk'])
                self.act(t['dd'][:], t['f'][:], AF.Ln, ['t_f'], ['t_dd'])
                self.p.op('dve', lambda e: e.tensor_tensor_scan(out=t['bb'][:], data0=self.resetm[:], data1=t['dd'][:], initial=0.0,
                                                                op0=ALU.mult, op1=ALU.add), ['t_dd', 'masks'], ['t_bb'])
                self.act(t['eb'][:], t['bb'][:], AF.Exp, ['t_bb'], ['t_eb'])
                self.act(t['enb'][:], t['bb'][:], AF.Exp, ['t_bb'], ['t_enb'], scale=-1.0)
                b3 = t['bb'][:].rearrange("p (n c) -> p n c", c=C)
                self.act(self.dec[:], b3[:, :, C - 1], AF.Exp, ['t_bb'], ['dec'])
                self.tt('dve', t['qe'][:], pq[:, :TT], t['eb'][:], ALU.mult, [pqk, 't_eb'], ['t_qe'])
                self.tt('pool', t['ke'][:], t['kk'][:], t['enb'][:], ALU.mult, ['t_kk', 't_enb'], ['t_ke'])
                self.tt('pool', t['dd'][:].rearrange("p (n c) -> p n c", c=C), b3[:, :, C - 1:C].to_broadcast([128, NCH, C]), b3, ALU.subtract,
                        ['t_bb'], ['t_dd'])
                self.act(t['dd'][:], t['dd'][:], AF.Exp, ['t_dd'], ['t_dd'])
                self.tt('pool', t['ko'][:], t['kk'][:], t['dd'][:], ALU.mult, ['t_kk', 't_dd'], ['t_ko'])
                qe4 = t['qe'][:].rearrange("p (b t) -> p b t", t=128)
                for c in range(4):
                    self.tt('pool' if c % 2 else 'dve', self.qem[:, c, :].rearrange("p (b t) -> p b t", t=128), qe4,
                            self.colmask[:, c:c + 1, :].to_broadcast([128, NB, 128]), ALU.mult, ['t_qe', 'masks'], ['qem'])
                py, pyk = P[4], 'ps4'
                for blk in range(NB):
                    par = blk % 2
                    ps_a, ps_t, ps_o = P[5][:, 0:128], P[5][:, 128:256], P[7][:, 0:128]
                    ka, kt, ko_ = 'ps5', 'ps5', 'ps7'
                    cs = slice(blk * 128, (blk + 1) * 128)
                    attm, kom, on = self.attm[par], self.kom[par], self.on[par]
                    self.mm(ps_a, t['ke'][:, cs], t['qe'][:, cs], True, True, ['t_ke', 't_qe'], [ka])
                    self.tt('dve', attm[:], ps_a, self.maskT[:], ALU.mult, [ka, 'masks'], [f'attm{par}'])
                    self.p.op('pe', lambda e: e.transpose(ps_t, t['ko'][:, cs], self.ident[:]), ['t_ko', 'ident'], [kt])
                    for c in range(4):
                        self.act(kom[:, c, :], ps_t, AF.Copy, [kt, 'masks'], [f'kom{par}'], scale=self.rowmask[:, c:c + 1])
                    self.mm(ps_o, attm[:], self.v_sb[:, blk, :], True, False, [f'attm{par}', 'v_sb'], [ko_])
                    for c in range(4):
                        Sc, Sn = self.S[:, j, c % 2, :], self.S[:, j, (c + 1) % 2, :]
                        kSc, kSn = ('S', j, c % 2), ('S', j, (c + 1) % 2)
                        self.mm(ps_o, self.qem[:, c, cs], Sc, False, c == 3, ['qem', kSc], [ko_])
                        ps_u = P[6][:, c * 128:(c + 1) * 128]
                        ku = 'ps6'
                        self.mm(ps_u, kom[:, c, :], self.v_sb[:, blk, :], True, True, [f'kom{par}', 'v_sb'], [ku])
                        self.stt('dve', Sn, Sc, self.dec[:, blk * 4 + c:blk * 4 + c + 1], ps_u, ALU.mult, ALU.add, [kSc, 'dec', ku], [kSn])
                    self.act(self.junk[:], ps_o, AF.Square, [ko_], ['junk', 'oss'], accum_out=self.oss[:, 0:1])
                    self.act(self.oss[:, 1:2], self.oss[:, 0:1], AF.Ln, ['oss', 'consts'], ['oss'], bias=self.epsc[:, 0:1], scale=1.0 / 128)
                    self.act(self.oss[:, 1:2], self.oss[:, 1:2], AF.Exp, ['oss'], ['oss'], scale=-0.5)
                    self.stt('dve', on[:], ps_o, self.oss[:, 1:2], self.gn_bc[:, j * 128:(j + 1) * 128], ALU.mult, ALU.mult,
                             [ko_, 'oss', 'gn_bc'], [f'on{par}'])
                    self.p.op('pe', lambda e: e.transpose(py[:, cs], on[:], self.ident[:]), [f'on{par}', 'ident'], [pyk])
                self.tt('dve', self.yTt[:, j, :], py[:, :TT], t['sgate'][:], ALU.mult, [pyk, 't_sgate'], ['yT'])

        self.run_layer(li, 'hgrn', TT, 4 * D, setup, tile, is_last)
        self.ps_pool = list(range(8))


    def rwkv_layer(self, li, is_last):
        TT = 256
        NB = TT // 128
        WC = 3200

        def setup():
            p = self.p
            L = self.lsb
            self.ident = L("ident", [128, 128], F32)
            self.make_ident(self.ident, 'ident')
            self.maskS = L("maskS", [128, 128], F32)
            self.maskI = L("maskI", [128, 128], F32)
            self.maskSL = L("maskSL", [128, 128], F32)
            for (m, pat, cm, cmp_) in ((self.maskS, 1, -1, ALU.is_gt), (self.maskI, 1, -1, ALU.is_ge), (self.maskSL, -1, 1, ALU.is_gt)):
                p.op('pool', lambda e, m=m: e.memset(m[:], 1.0), [], ['masks'])
                p.op('pool', lambda e, m=m, pat=pat, cm=cm, cmp_=cmp_: e.affine_select(
                    out=m[:], in_=m[:], pattern=[[pat, 128]], compare_op=cmp_, fill=0.0, base=0, channel_multiplier=cm), ['masks'], ['masks'])
            self.blockones = L("blockones", [128, 128], F32)
            p.op('pool', lambda e: e.memset(self.blockones[:], 1.0), [], ['masks'])
            p.op('pool', lambda e: e.memset(self.blockones[0:64, 64:128], 0.0), ['masks'], ['masks'])
            p.op('pool', lambda e: e.memset(self.blockones[64:128, 0:64], 0.0), ['masks'], ['masks'])
            self.ones_col = L("ones_col", [128, 1], F32)
            p.op('pool', lambda e: e.memset(self.ones_col[:], 1.0), [], ['masks'])
            self.resetm = L("resetm", [128, TT], F32)
            p.op('pool', lambda e: e.memset(self.resetm[:], 1.0), [], ['masks'])
            p.op('pool', lambda e: e.memset(self.resetm[:].rearrange("p (n c) -> p n c", c=128)[:, :, 0:1], 0.0), ['masks'], ['masks'])
            p.op('pool', lambda e: e.memset(self.epsc[:, 1:2], GN_EPS), [], ['consts'])
            self.mu_fm = L("mu_fm", [128, 33], F32)
            self.omu_fm = L("omu_fm", [128, 33], F32)
            p.dma('sp', self.mu_fm[:], self.inputs['rwkv_mu_fm'], [], ['rw_vecs'])
            self.ts('dve', self.omu_fm[:], self.mu_fm[:], -1.0, 1.0, ALU.mult, ALU.add, ['rw_vecs'], ['rw_vecs'])
            self.vecs = L("rw_vecs", [128, 5, NKC], F32)
            p.dma('sp', self.vecs[:], self.inputs['rwkv_vecs'], [], ['rw_vecs'])
            self.lw2 = L("lw2", [128, D], F32)
            p.dma('sp', self.lw2[:], self.inputs['rwkv_lw2'], [], ['lw2'])
            self.gng_bc = L("gng_bc", [128, D], F32)
            self.gnb_bc = L("gnb_bc", [128, D], F32)
            p.dma('sp', self.gng_bc[:], self.inputs['rwkv_gn_g'].partition_broadcast(128), [], ['bc_tiles'])
            p.dma('sp', self.gnb_bc[:], self.inputs['rwkv_gn_b'].partition_broadcast(128), [], ['bc_tiles'])
            self.Wva = L("Wva", [128, NKC, D], BF16)
            self.Wvb = L("Wvb", [128, NKC, D], BF16)
            with ExitStack() as tes:
                muv = tes.enter_context(self.nc.sbuf_tensor("muv_bc", [128, D], F32))
                omuv = tes.enter_context(self.nc.sbuf_tensor("omuv_bc", [128, D], F32))
                p.dma('sp', muv[:], self.inputs['rwkv_mu'][2 * D:3 * D].partition_broadcast(128), [], ['bc_tiles'])
                self.ts('dve', omuv[:], muv[:], -1.0, 1.0, ALU.mult, ALU.add, ['bc_tiles'], ['bc_tiles'])
                src = self.w_in_dram['rwkv']
                self.load_weight_bf16(self.Wva, 'Wv', src, D, src_c0=2 * D, scale_bc=omuv)
                self.load_weight_bf16(self.Wvb, 'Wv', src, D, src_c0=2 * D, scale_bc=muv)
                p.barrier()
            src = self.w_in_dram['rwkv']
            self.load_weight_bf16(self.W_in, 'W_in', src, 2 * D, src_c0=0, dst_c0=0)
            self.load_weight_bf16(self.W_in, 'W_in', src, 128, src_c0=3 * D, dst_c0=2 * D)
            self.load_weight_bf16(self.W_in, 'W_in', src, D, src_c0=3 * D + 128, dst_c0=2 * D + 128)
            self.S = L("S_rwkv", [128, NKC, 64], F32)
            p.op('pool', lambda e: e.memset(self.S[:], 0.0), [], [('S', j) for j in range(NKC)])
            self.pcar = L("pcar", [128, 25], F32)
            p.op('pool', lambda e: e.memset(self.pcar[:], 0.0), [], [('pcar', i) for i in range(25)])
            self.pm_ext = [L(f"pm_ext{i}", [128, TT + 1], F32) for i in range(2)]
            self.pm_i = 0
            names = ['lo', 'r', 'k', 'sgate', 'sigw', 'a', 'lw', 'kk', 'kk2', 'rn', 'kkn', 'kmod', 'bbv', 'c', 'ec', 'enc', 'd1',
                     'at', 'rt', 'kt', 'bt', 'e3', 'khat', 'bhat', 'rkr', 'tmp']
            self.tmp = {n: L(f"w_{n}", [128, TT], F32) for n in names}
            self.v_sb = L("v_sb", [128, NB, 128], F32)
            self.dec = L("dec", [128, NB], F32)
            self.Pb = [[L(f"Pb{h}{i}", [128, 128], F32) for i in range(2)] for h in range(2)]
            self.Qb = [[L(f"Qb{h}{i}", [128, 128], F32) for i in range(2)] for h in range(2)]
            self.NT = [L(f"NT{h}", [128, 128], F32) for h in range(2)]
            self.Aak = [L(f"Aak{h}", [128, 128], F32) for h in range(2)]
            self.Ark = [L(f"Ark{h}", [128, 128], F32) for h in range(2)]
            self.Arb = [L(f"Arb{h}", [128, 128], F32) for h in range(2)]
            self.khm = L("khm", [128, 128], F32)
            self.bhm = L("bhm", [128, 128], F32)
            self.Z_sb = L("Z_sb", [128, 128], F32)
            self.U_sb = L("U_sb", [128, 128], F32)
            self.yn = L("yn", [128, 128], F32)
            self.bon = L("bon", [128, 128], F32)
            self.gst = L("gst", [128, 12], F32)
            self.junk = L("junk", [128, 64], F32)
            self.ps_pool = [0, 1, 2]
            self.nm_rr = 0
            return True

        def nm_ps():
            i = 3 + (self.nm_rr % 2)
            self.nm_rr += 1
            return self.psums[i], f"ps{i}"

        def shift(ps, pk, dst, dk, idx, mt):
            pm = self.pm_ext[self.pm_i % 2]
            pmk = f"pm_ext{self.pm_i % 2}"
            self.pm_i += 1
            ck = ('pcar', idx)
            self.copy('pool', pm[:, 0:1], self.pcar[:, idx:idx + 1], [ck], [pmk])
            self.act(pm[:, 1:TT + 1], ps[:, :TT], AF.Copy, [pk, 'rw_vecs'], [pmk], scale=self.mu_fm[:, mt:mt + 1])
            self.copy('pool', self.pcar[:, idx:idx + 1], pm[:, TT:TT + 1], [pmk], [ck])
            self.act(dst, ps[:, :TT], AF.Copy, [pk, 'rw_vecs'], [dk], scale=self.omu_fm[:, mt:mt + 1])
            self.tt('dve', dst, dst, pm[:, 0:TT], ALU.add, [dk, pmk], [dk])

        def proj(col0):
            ps, pk = self.next_ps()
            for kc in range(NKC):
                self.mm(ps[:, :TT], self.W_in[:, kc, col0:col0 + 128], self.hn[:, kc, 1:TT + 1], kc == 0, kc == NKC - 1, ['W_in', 'hn'], [pk])
            return ps, pk

        def tile(ti):
            t = self.tmp
            P = self.psums
            V = self.vecs
            ps, pk = proj(2 * D)
            shift(ps, pk, t['lo'][:], 't_lo', 24, 24)
            self.act(t['lo'][0:64, :], t['lo'][0:64, :], AF.Tanh, ['t_lo'], ['t_lo'])
            for j in range(NKC):
                jc = slice(j * 128, (j + 1) * 128)
                ps, pk = proj(j * 128)
                shift(ps, pk, t['r'][:], 't_r', j, j)
                ps, pk = proj(D + j * 128)
                shift(ps, pk, t['k'][:], 't_k', 8 + j, 8 + j)
                ps, pk = proj(2 * D + 128 + j * 128)
                shift(ps, pk, t['tmp'][:], 't_tmp', 16 + j, 25 + j)
                self.act(t['sgate'][:], t['tmp'][:], AF.Silu, ['t_tmp'], ['t_sgate'])
                pv, pvk = self.next_ps()
                for blk in range(NB):
                    n = 0
                    for kc in range(NKC):
                        for (Wv, off) in ((self.Wva, 1), (self.Wvb, 0)):
                            self.mm(pv[:, blk * 128:(blk + 1) * 128], self.hn[:, kc, off + blk * 128:off + (blk + 1) * 128], Wv[:, kc, jc],
                                    n == 0, n == 2 * NKC - 1, ['Wv', 'hn'], [pvk])
                            n += 1
                self.copy('act', self.v_sb[:].rearrange("p b v -> p (b v)"), pv[:, :TT], [pvk], ['v_sb'])
                pw, pwk = self.next_ps()
                self.mm(pw[:, :TT], self.lw2[0:64, jc], t['lo'][0:64, :], True, True, ['lw2', 't_lo'], [pwk])
                self.act(t['sigw'][:], pw[:, :TT], AF.Sigmoid, [pwk, 'rw_vecs'], ['t_sigw'], bias=V[:, 0, j:j + 1])
                pa, pak = self.next_ps()
                self.mm(pa[:, :TT], self.lw2[64:128, jc], t['lo'][64:128, :], True, True, ['lw2', 't_lo'], [pak])
                self.act(t['a'][:], pa[:, :TT], AF.Sigmoid, [pak, 'rw_vecs'], ['t_a'], bias=V[:, 1, j:j + 1])
                self.ts('pool', t['lw'][:], t['sigw'][:], -0.6065306597126334, None, ALU.mult, None, ['t_sigw'], ['t_lw'])
                self.ts('pool', t['kk'][:], t['k'][:], V[:, 2, j:j + 1], None, ALU.mult, None, ['t_k', 'rw_vecs'], ['t_kk'])
                self.tt('pool', t['kk2'][:], t['kk'][:], t['kk'][:], ALU.mult, ['t_kk'], ['t_kk2'])
                pn, pnk = self.next_ps()
                self.mm(pn[:, :TT], self.blockones[:], t['kk2'][:], True, True, ['masks', 't_kk2'], [pnk])
                self.ts('dve', t['rn'][:], pn[:, :TT], 1e-24, None, ALU.max, None, [pnk], ['t_rn'])
                self.act(t['rn'][:], t['rn'][:], AF.Ln, ['t_rn'], ['t_rn'])
                self.act(t['rn'][:], t['rn'][:], AF.Exp, ['t_rn'], ['t_rn'], scale=-0.5)
                self.tt('pool', t['kkn'][:], t['kk'][:], t['rn'][:], ALU.mult, ['t_kk', 't_rn'], ['t_kkn'])
                self.ts('dve', t['tmp'][:], t['a'][:], -1.0, V[:, 3, j:j + 1], ALU.add, ALU.mult, ['t_a', 'rw_vecs'], ['t_tmp'])
                self.stt('dve', t['kmod'][:], t['tmp'][:], 1.0, t['k'][:], ALU.add, ALU.mult, ['t_tmp', 't_k'], ['t_kmod'])
                self.tt('pool', t['bbv'][:], t['kkn'][:], t['a'][:], ALU.mult, ['t_kkn', 't_a'], ['t_bbv'])
                self.p.op('dve', lambda e: e.tensor_tensor_scan(out=t['c'][:], data0=self.resetm[:], data1=t['lw'][:], initial=0.0,
                                                                op0=ALU.mult, op1=ALU.add), ['t_lw', 'masks'], ['t_c'])
                self.act(t['ec'][:], t['c'][:], AF.Exp, ['t_c'], ['t_ec'])
                self.act(t['enc'][:], t['c'][:], AF.Exp, ['t_c'], ['t_enc'], scale=-1.0)
                self.tt('pool', t['d1'][:], t['c'][:], t['lw'][:], ALU.subtract, ['t_c', 't_lw'], ['t_d1'])
                self.act(t['d1'][:], t['d1'][:], AF.Exp, ['t_d1'], ['t_d1'])
                self.stt('dve', t['at'][:], t['kkn'][:], -1.0, t['d1'][:], ALU.mult, ALU.mult, ['t_kkn', 't_d1'], ['t_at'])
                self.tt('pool', t['rt'][:], t['r'][:], t['ec'][:], ALU.mult, ['t_r', 't_ec'], ['t_rt'])
                self.tt('pool', t['kt'][:], t['kmod'][:], t['enc'][:], ALU.mult, ['t_kmod', 't_enc'], ['t_kt'])
                self.tt('pool', t['bt'][:], t['bbv'][:], t['enc'][:], ALU.mult, ['t_bbv', 't_enc'], ['t_bt'])
                c3 = t['c'][:].rearrange("p (n c) -> p n c", c=128)
                self.tt('pool', t['e3'][:].rearrange("p (n c) -> p n c", c=128), c3[:, :, 127:128].to_broadcast([128, NB, 128]), c3,
                        ALU.subtract, ['t_c'], ['t_e3'])
                self.act(t['e3'][:], t['e3'][:], AF.Exp, ['t_e3'], ['t_e3'])
                self.act(self.dec[:], c3[:, :, 127], AF.Exp, ['t_c'], ['dec'])
                self.tt('pool', t['khat'][:], t['kmod'][:], t['e3'][:], ALU.mult, ['t_kmod', 't_e3'], ['t_khat'])
                self.tt('pool', t['bhat'][:], t['bbv'][:], t['e3'][:], ALU.mult, ['t_bbv', 't_e3'], ['t_bhat'])
                self.stt('dve', t['rkr'][:], t['r'][:], V[:, 4, j:j + 1], t['kmod'][:], ALU.mult, ALU.mult, ['t_r', 'rw_vecs', 't_kmod'], ['t_rkr'])
                kS = ('S', j)
                for blk in range(NB):
                    cs = slice(blk * 128, (blk + 1) * 128)
                    for (srcn, dst, dk) in (('khat', self.khm, 'khm'), ('bhat', self.bhm, 'bhm')):
                        ps, pk = nm_ps()
                        self.p.op('pe', lambda e, ps=ps, srcn=srcn: e.transpose(ps[:, 0:128], t[srcn][:, cs], self.ident[:]), ['t_' + srcn, 'ident'], [pk])
                        self.copy('act', dst[:], ps[:, 0:128], [pk], [dk])
                    hsl = [slice(0, 64), slice(64, 128)]
                    for hd in range(2):
                        hs = hsl[hd]
                        for (lh, rh, mask, dst, dk, eng) in (
                                ('bt', 'at', self.maskS, self.Pb[hd][0], f'Pb{hd}0', 'dve'),
                                ('at', 'bt', self.maskSL, self.Qb[hd][0], f'Qb{hd}0', 'dve'),
                                ('kt', 'at', self.maskS, self.Aak[hd], f'Aak{hd}', 'dve'),
                                ('kt', 'rt', self.maskI, self.Ark[hd], f'Ark{hd}', 'dve'),
                                ('bt', 'rt', self.maskI, self.Arb[hd], f'Arb{hd}', 'dve')):
                            ps, pk = nm_ps()
                            self.mm(ps[:, 0:128], t[lh][hs, cs], t[rh][hs, cs], True, True, ['t_' + lh, 't_' + rh], [pk])
                            self.tt(eng, dst[:], ps[:, 0:128], mask[:], ALU.mult, [pk, 'masks'], [dk])
                        self.tt('pool', self.NT[hd][:], self.ident[:], self.Pb[hd][0][:], ALU.add, ['ident', f'Pb{hd}0'], [f'NT{hd}'])
                    for i in range(6):
                        a_, b_ = i % 2, (i + 1) % 2
                        for hd in range(2):
                            Pa, Qa, Pn, Qn = self.Pb[hd][a_], self.Qb[hd][a_], self.Pb[hd][b_], self.Qb[hd][b_]
                            kPa, kQa, kPn, kQn = f'Pb{hd}{a_}', f'Qb{hd}{a_}', f'Pb{hd}{b_}', f'Qb{hd}{b_}'
                            if i < 5:
                                ps, pk = nm_ps()
                                self.mm(ps[:, 0:128], Qa[:], Pa[:], True, True, [kQa, kPa], [pk])
                                self.copy('act', Pn[:], ps[:, 0:128], [pk], [kPn])
                            ps, pk = nm_ps()
                            self.mm(ps[:, 0:128], Pa[:], Qa[:], True, True, [kPa, kQa], [pk])
                            self.copy('act', Qn[:], ps[:, 0:128], [pk], [kQn])
                            ps, pk = nm_ps()
                            self.mm(ps[:, 0:128], Qn[:], self.NT[hd][:], True, True, [kQn, f'NT{hd}'], [pk])
                            self.tt('dve', self.NT[hd][:], self.NT[hd][:], ps[:, 0:128], ALU.add, [f'NT{hd}', pk], [f'NT{hd}'])
                    pz, pzk = P[5], 'ps5'
                    for hd in range(2):
                        hs = hsl[hd]
                        hc = slice(hd * 64, (hd + 1) * 64)
                        self.mm(pz[:, hc], self.Aak[hd][:], self.v_sb[:, blk, hc], True, False, [f'Aak{hd}', 'v_sb'], [pzk])
                        self.mm(pz[:, hc], t['at'][hs, cs], self.S[hs, j, :], False, True, ['t_at', kS], [pzk])
                    self.copy('act', self.Z_sb[:], pz[:, 0:128], [pzk], ['Z_sb'])
                    for hd in range(2):
                        hc = slice(hd * 64, (hd + 1) * 64)
                        self.mm(pz[:, hc], self.NT[hd][:], self.Z_sb[:, hc], True, True, [f'NT{hd}', 'Z_sb'], [pzk])
                    self.copy('act', self.U_sb[:], pz[:, 0:128], [pzk], ['U_sb'])
                    for hd in range(2):
                        hs = hsl[hd]
                        hc = slice(hd * 64, (hd + 1) * 64)
                        py, pyk = P[6 + hd], f'ps{6 + hd}'
                        self.mm(py[:, 0:64], self.Ark[hd][:], self.v_sb[:, blk, hc], True, False, [f'Ark{hd}', 'v_sb'], [pyk])
                        self.mm(py[:, 0:64], t['rt'][hs, cs], self.S[hs, j, :], False, False, ['t_rt', kS], [pyk])
                        self.mm(py[:, 0:64], self.Arb[hd][:], self.U_sb[:, hc], False, True, [f'Arb{hd}', 'U_sb'], [pyk])
                        self.mm(py[:, 64:128], t['rkr'][hs, cs], self.blockones[hs, hs], True, True, ['t_rkr', 'masks'], [pyk])
                    self.mm(pz[:, 0:128], self.khm[:], self.v_sb[:, blk, :], True, False, ['khm', 'v_sb'], [pzk])
                    self.mm(pz[:, 0:128], self.bhm[:], self.U_sb[:], False, True, ['bhm', 'U_sb'], [pzk])
                    for hd in range(2):
                        hs = hsl[hd]
                        hc = slice(hd * 64, (hd + 1) * 64)
                        self.stt('dve', self.S[hs, j, :], self.S[hs, j, :], self.dec[hs, blk:blk + 1], pz[hs, hc], ALU.mult, ALU.add,
                                 [kS, 'dec', pzk], [kS])
                    g = self.gst
                    for hd in range(2):
                        hc = slice(hd * 64, (hd + 1) * 64)
                        py, pyk = P[6 + hd], f'ps{6 + hd}'
                        self.act(self.junk[:], py[:, 0:64], AF.Identity, [pyk], ['junk', 'gst'], accum_out=g[:, hd:hd + 1])
                        self.act(self.junk[:], py[:, 0:64], AF.Square, [pyk], ['junk', 'gst'], accum_out=g[:, 2 + hd:3 + hd])
                        self.tt('dve', self.bon[:, hc], py[:, 64:128], self.v_sb[:, blk, hc], ALU.mult, [pyk, 'v_sb'], ['bon'])
                    self.ts('dve', g[:, 4:6], g[:, 0:2], 1.0 / 64, None, ALU.mult, None, ['gst'], ['gst'])
                    self.tt('dve', g[:, 6:8], g[:, 4:6], g[:, 4:6], ALU.mult, ['gst'], ['gst'])
                    self.stt('dve', g[:, 8:10], g[:, 2:4], 1.0 / 64, g[:, 6:8], ALU.mult, ALU.subtract, ['gst'], ['gst'])
                    self.act(g[:, 8:10], g[:, 8:10], AF.Ln, ['gst', 'consts'], ['gst'], bias=self.epsc[:, 1:2])
                    self.act(g[:, 8:10], g[:, 8:10], AF.Exp, ['gst'], ['gst'], scale=-0.5)
                    for hd in range(2):
                        hc = slice(hd * 64, (hd + 1) * 64)
                        py, pyk = P[6 + hd], f'ps{6 + hd}'
                        self.ts('dve', self.yn[:, hc], py[:, 0:64], g[:, 4 + hd:5 + hd], g[:, 8 + hd:9 + hd], ALU.subtract, ALU.mult,
                                [pyk, 'gst'], ['yn'])
                    self.tt('pool', self.yn[:], self.yn[:], self.gng_bc[:, jc], ALU.mult, ['yn', 'bc_tiles'], ['yn'])
                    self.tt('pool', self.yn[:], self.yn[:], self.gnb_bc[:, jc], ALU.add, ['yn', 'bc_tiles'], ['yn'])
                    self.tt('pool', self.yn[:], self.yn[:], self.bon[:], ALU.add, ['yn', 'bon'], ['yn'])
                    ps, pk = nm_ps()
                    self.p.op('pe', lambda e, ps=ps: e.transpose(ps[:, 0:128], self.yn[:], self.ident[:]), ['yn', 'ident'], [pk])
                    self.tt('dve', self.yTt[:, j, cs], ps[:, 0:128], t['sgate'][:, cs], ALU.mult, [pk, 't_sgate'], ['yT'])

        self.run_layer(li, 'rwkv', TT, WC, setup, tile, is_last)
        self.ps_pool = list(range(8))

    def build(self):
        nc = self.nc
        T = self.T
        self.xT = self.din("xT", [D, T])
        self.yT = nc.dram_tensor("yT", [D, T], F32, kind="ExternalOutput").ap()
        d_norm_g = self.din("norm_g", [128, 4, NKC])
        d_final_g = self.din("final_g", [128, NKC])
        self.w_in_dram, self.w_out_dram = {}, {}
        kinds = [k for (_, k) in self.layers]
        if 'conv' in kinds:
            self.w_in_dram['conv'] = self.din("conv_w_in", [D, 4 * D])
            self.w_out_dram['conv'] = self.din("conv_w_out", [D, D])
            d_conv_w = self.din("conv_w", [128, NKC, 3])
        if 'rwkv' in kinds:
            self.w_in_dram['rwkv'] = self.din("rwkv_w_in", [D, 4 * D + 128])
            self.w_out_dram['rwkv'] = self.din("rwkv_w_out", [D, D])
            self.din("rwkv_mu_fm", [128, 33])
            self.din("rwkv_mu", [4 * D + 128])
            self.din("rwkv_vecs", [128, 5, NKC])
            self.din("rwkv_lw2", [128, D])
            self.din("rwkv_gn_g", [D])
            self.din("rwkv_gn_b", [D])
        if 'hgrn' in kinds:
            self.w_in_dram['hgrn'] = self.din("hgrn_w_in", [D, 4 * D])
            self.w_out_dram['hgrn'] = self.din("hgrn_w_out", [D, D])
            self.din("hgrn_gn_g", [D])
            self.din("hgrn_lbl", [128, 4, NKC])
        if 'gmlp' in kinds:
            self.w_in_dram['gmlp'] = self.din("gmlp_w_in", [D, 3 * D])
            self.w_out_dram['gmlp'] = self.din("gmlp_w_out", [D, D])
            self.din("gmlp_wsT", [128, 8, 128])
            self.din("gmlp_bs", [8, 128])
            self.din("gmlp_vg", [D])
        with ExitStack() as es:
            self.es = es
            nc.allow_low_precision("bf16 matmul operands, fp32 accumulation")
            self.p = p = Prog(nc, es)
            self.psums = [es.enter_context(nc.psum_tensor(f"ps{i}", [128, 512], F32)) for i in range(8)]
            self.ps_rr = 0
            self.ps_pool = list(range(8))
            self.ones_bf = self.sb("ones_bf", [128, 128], BF16)
            self.epsc = self.sb("epsc", [128, 2], F32)
            self.norm_g = self.sb("norm_g_sb", [128, 4, NKC], F32)
            self.final_g = self.sb("final_g_sb", [128, NKC], F32)
            p.op('pool', lambda e: e.memset(self.ones_bf[:], 1.0), [], ['ones_bf'])
            p.op('pool', lambda e: e.memset(self.epsc[:, 0:1], RMS_EPS), [], ['consts'])
            p.dma('sp', self.norm_g[:], d_norm_g, [], ['consts'])
            p.dma('sp', self.final_g[:], d_final_g, [], ['consts'])
            if 'conv' in kinds:
                self.conv_w = self.sb("conv_w_sb", [128, NKC, 3], F32)
                p.dma('sp', self.conv_w[:], d_conv_w, [], ['consts'])
            self.first_layer = True
            for n, (li, kind) in enumerate(self.layers):
                is_last = n == len(self.layers) - 1
                if kind == 'conv':
                    self.conv_layer(li, is_last)
                elif kind == 'gmlp':
                    self.gmlp_layer(li, is_last)
                elif kind == 'hgrn':
                    self.hgrn_layer(li, is_last)
                elif kind == 'rwkv':
                    self.rwkv_layer(li, is_last)
                else:
                    raise ValueError(kind)
            p.finish('sp')
            self.stats = (p.n_ins, p.n_wait)
        return nc


def prep_inputs(inp, b, layers):
    f = np.float32
    m = {}
    m["xT"] = np.ascontiguousarray(np.asarray(inp["x"][b], f).T)
    m["norm_g"] = np.ascontiguousarray(np.asarray(inp["norm_g"], f).reshape(4, NKC, 128).transpose(2, 0, 1))
    m["final_g"] = np.ascontiguousarray(np.asarray(inp["final_g"], f).reshape(NKC, 128).T)
    kinds = [k for (_, k) in layers]
    if 'conv' in kinds:
        m["conv_w_in"] = np.ascontiguousarray(np.asarray(inp["conv_w_in"][0], f))
        m["conv_w_out"] = np.ascontiguousarray(np.asarray(inp["conv_w_out"][0], f))
        m["conv_w"] = np.ascontiguousarray(np.asarray(inp["conv_w"][0], f).reshape(3, NKC, 128).transpose(2, 1, 0))
    if 'rwkv' in kinds:
        m["rwkv_w_in"] = np.ascontiguousarray(np.asarray(inp["rwkv_w_in"][0], f))
        m["rwkv_w_out"] = np.ascontiguousarray(np.asarray(inp["rwkv_w_out"][0], f))
        mu = np.asarray(inp["rwkv_mu"][0], f)
        m["rwkv_mu"] = np.ascontiguousarray(mu)
        m["rwkv_mu_fm"] = np.ascontiguousarray(mu.reshape(33, 128).T)
        vecs = np.stack([np.asarray(inp[k][0], f).reshape(NKC, 128) for k in
                         ("rwkv_w0", "rwkv_a0", "rwkv_k_k", "rwkv_k_a", "rwkv_r_k")], axis=0)
        m["rwkv_vecs"] = np.ascontiguousarray(vecs.transpose(2, 0, 1))
        m["rwkv_lw2"] = np.ascontiguousarray(np.concatenate([np.asarray(inp["rwkv_w_w2"][0], f), np.asarray(inp["rwkv_w_a2"][0], f)], axis=0))
        m["rwkv_gn_g"] = np.ascontiguousarray(np.asarray(inp["rwkv_gn_g"][0], f))
        m["rwkv_gn_b"] = np.ascontiguousarray(np.asarray(inp["rwkv_gn_b"][0], f))
    if 'hgrn' in kinds:
        m["hgrn_w_in"] = np.ascontiguousarray(np.asarray(inp["hgrn_w_in"][0], f))
        m["hgrn_w_out"] = np.ascontiguousarray(np.asarray(inp["hgrn_w_out"][0], f))
        m["hgrn_gn_g"] = np.ascontiguousarray(np.asarray(inp["hgrn_gn_g"][0], f))
        m["hgrn_lbl"] = np.ascontiguousarray(np.asarray(inp["hgrn_lb_logits"], f).reshape(4, NKC, 128).transpose(2, 0, 1))
    if 'gmlp' in kinds:
        m["gmlp_w_in"] = np.ascontiguousarray(np.asarray(inp["gmlp_w_in"][0], f))
        m["gmlp_w_out"] = np.ascontiguousarray(np.asarray(inp["gmlp_w_out"][0], f))
        m["gmlp_wsT"] = np.ascontiguousarray(np.asarray(inp["gmlp_w_s"][0], f).transpose(2, 0, 1))
        m["gmlp_bs"] = np.ascontiguousarray(np.asarray(inp["gmlp_b_s"][0], f))
        m["gmlp_vg"] = np.ascontiguousarray(np.asarray(inp["gmlp_v_g"][0], f))
    return m


FULL_LAYERS = [(0, 'rwkv'), (1, 'hgrn'), (2, 'conv'), (3, 'gmlp')]


def kernel(**inputs):
    x = np.asarray(inputs["x"])
    B, T, _ = x.shape
    layers = FULL_LAYERS
    bld = Builder(T, layers)
    nc = bld.build()
    in_maps = []
    for c in range(8):
        in_maps.append(prep_inputs(inputs, c // 2, layers))
    res = run_bass_kernel_spmd(nc, in_maps, core_ids=list(range(8)))
    out = np.stack([np.asarray(res.results[2 * b]["yT"]).T for b in range(B)], axis=0)
    return out.astype(np.float32)
```

```python
import numpy as np
from contextlib import ExitStack
import concourse.bass as bass
import concourse.mybir as mybir
from concourse.bass_utils import run_bass_kernel_spmd

F32 = mybir.dt.float32
BF16 = mybir.dt.bfloat16
ALU = mybir.AluOpType
AF = mybir.ActivationFunctionType
AX = mybir.AxisListType

D = 1024
NKC = 8
RMS_EPS = 1e-6
GN_EPS = 64e-5


class Prog:
    LIMIT = 30000

    def __init__(self, nc, es, n_dma_sems=24):
        self.nc = nc
        self.es = es
        self.engs = {'pe': nc.tensor, 'act': nc.scalar, 'dve': nc.vector,
                     'pool': nc.gpsimd, 'sp': nc.sync}
        self.sems = {}
        self.epoch = {k: 0 for k in self.engs}
        self.cnt = {k: 0 for k in self.engs}
        for k in self.engs:
            self.sems[(k, 0)] = es.enter_context(nc.semaphore(f"s_{k}_0"))
        self.dma_sems = []
        for i in range(n_dma_sems):
            key = ('dma', i)
            self.sems[key] = es.enter_context(nc.semaphore(f"s_dma_{i}"))
            self.cnt[key] = 0
            self.dma_sems.append(key)
        self.dma_rr = 0
        self.waited = {k: {} for k in self.engs}
        self.bufs = {}
        self.n_wait = 0
        self.n_ins = 0

    def _deps(self, reads, writes):
        deps = set()
        for k in reads:
            b = self.bufs.get(k)
            if b and b['w']:
                deps.add(b['w'])
        for k in writes:
            b = self.bufs.get(k)
            if b:
                if b['w']:
                    deps.add(b['w'])
                deps.update(b['r'])
        return deps

    def _wait(self, eng, deps):
        e = self.engs[eng]
        best = {}
        for (sk, v) in deps:
            if sk[0] == eng and eng == 'pe':
                continue
            if best.get(sk, 0) < v:
                best[sk] = v
        for sk, v in best.items():
            if self.waited[eng].get(sk, 0) >= v:
                continue
            e.wait_ge(self.sems[sk], v)
            self.waited[eng][sk] = v
            self.n_wait += 1

    def _record(self, tok, reads, writes):
        for k in reads:
            b = self.bufs.setdefault(k, {'w': None, 'r': []})
            b['r'].append(tok)
            if len(b['r']) > 64:
                best = {}
                for (sk, v) in b['r']:
                    if best.get(sk, 0) < v:
                        best[sk] = v
                b['r'] = list(best.items())
        for k in writes:
            b = self.bufs.setdefault(k, {'w': None, 'r': []})
            b['w'] = tok
            b['r'] = []

    @staticmethod
    def _excl(reads, writes):
        ps = [k for k in reads if isinstance(k, str) and k.startswith('ps')]
        if ps:
            reads = [k for k in reads if k not in ps]
            writes = list(writes) + ps
        return reads, writes

    disabled = False

    def op(self, eng, fn, reads=(), writes=()):
        if self.disabled:
            return None
        reads, writes = self._excl(reads, writes)
        deps = self._deps(reads, writes)
        self._wait(eng, deps)
        ins = fn(self.engs[eng])
        if self.cnt[eng] >= self.LIMIT:
            self.epoch[eng] += 1
            ep = self.epoch[eng]
            self.sems[(eng, ep)] = self.es.enter_context(self.nc.semaphore(f"s_{eng}_{ep}"))
            self.cnt[eng] = 0
        sk = (eng, self.epoch[eng])
        self.cnt[eng] += 1
        ins.then_inc(self.sems[sk], 1)
        self._record((sk, self.cnt[eng]), reads, writes)
        self.n_ins += 1
        return ins

    def dma(self, eng, out, in_, reads=(), writes=(), **kw):
        if self.disabled:
            return None
        deps = self._deps(reads, writes)
        sk = self.dma_sems[self.dma_rr]
        self.dma_rr = (self.dma_rr + 1) % len(self.dma_sems)
        if self.cnt[sk] > 0:
            deps.add((sk, self.cnt[sk]))
        self._wait(eng, deps)
        ins = self.engs[eng].dma_start(out=out, in_=in_, **kw)
        self.cnt[sk] += 16
        ins.then_inc(self.sems[sk], 16)
        self._record((sk, self.cnt[sk]), reads, writes)
        self.n_ins += 1
        return ins

    def all_tokens(self):
        deps = set()
        for k, b in self.bufs.items():
            if b['w']:
                deps.add(b['w'])
            deps.update(b['r'])
        return deps

    def barrier(self):
        deps = self.all_tokens()
        for eng in self.engs:
            d = set(x for x in deps)
            self._wait(eng, d)

    def finish(self, eng='sp'):
        self._wait(eng, self.all_tokens())


class Builder:
    def __init__(self, T, layers, do_final=True, neu_dt=None):
        self.neu_dt = neu_dt if neu_dt is not None else BF16
        self.T = T
        self.layers = layers
        self.do_final = do_final
        self.nc = bass.Bass("TRN2", target_bir_lowering=False)
        self.inputs = {}

    def din(self, name, shape):
        t = self.nc.dram_tensor(name, list(shape), F32, kind="ExternalInput").ap()
        self.inputs[name] = t
        return t

    def sb(self, name, shape, dt=F32):
        return self.es.enter_context(self.nc.sbuf_tensor(name, list(shape), dt))

    def lsb(self, name, shape, dt=F32):
        return self.les.enter_context(self.nc.sbuf_tensor(f"{name}_{self.lname}", list(shape), dt))

    def next_ps(self):
        pool = self.ps_pool
        i = pool[self.ps_rr % len(pool)]
        self.ps_rr += 1
        return self.psums[i], f"ps{i}"

    def tt(self, eng, out, in0, in1, op, reads, writes):
        return self.p.op(eng, lambda e: e.tensor_tensor(out=out, in0=in0, in1=in1, op=op), reads, writes)

    def ts(self, eng, out, in0, s1, s2, op0, op1, reads, writes):
        if s2 is None:
            return self.p.op(eng, lambda e: e.tensor_scalar(out=out, in0=in0, scalar1=s1, scalar2=None, op0=op0), reads, writes)
        return self.p.op(eng, lambda e: e.tensor_scalar(out=out, in0=in0, scalar1=s1, scalar2=s2, op0=op0, op1=op1), reads, writes)

    def stt(self, eng, out, in0, scalar, in1, op0, op1, reads, writes):
        eng = 'dve'
        return self.p.op(eng, lambda e: e.scalar_tensor_tensor(out=out, in0=in0, scalar=scalar, in1=in1, op0=op0, op1=op1), reads, writes)

    def act(self, out, in_, func, reads, writes, bias=None, scale=1.0, accum_out=None):
        kw = {}
        if bias is not None:
            kw['bias'] = bias
        if accum_out is not None:
            kw['accum_out'] = accum_out
        return self.p.op('act', lambda e: e.activation(out=out, in_=in_, func=func, scale=scale, **kw), reads, writes)

    def mm(self, out, lhsT, rhs, start, stop, reads, writes):
        return self.p.op('pe', lambda e: e.matmul(out, lhsT=lhsT, rhs=rhs, start=start, stop=stop), reads, writes)

    def copy(self, eng, out, in_, reads, writes):
        if eng == 'act':
            return self.p.op('act', lambda e: e.copy(out=out, in_=in_), reads, writes)
        return self.p.op(eng, lambda e: e.tensor_copy(out=out, in_=in_), reads, writes)

    def load_weight_bf16(self, dst, dst_key, src, ncols, src_c0=0, dst_c0=0, scale_bc=None):
        p = self.p
        CH = 1056 if ncols % 1056 == 0 else (1024 if ncols % 1024 == 0 else ncols)
        for kc in range(NKC):
            for c0 in range(0, ncols, CH):
                i = self.stage_i
                self.stage_i += 1
                st = self.stage[i % 2]
                sk = f"stage{i % 2}"
                p.dma('sp', st[:, 0:CH], src[kc * 128:(kc + 1) * 128, src_c0 + c0:src_c0 + c0 + CH], reads=[], writes=[sk])
                eng = ['dve', 'pool'][i % 2]
                if scale_bc is None:
                    self.copy(eng, dst[:, kc, dst_c0 + c0:dst_c0 + c0 + CH], st[:, 0:CH], [sk], [dst_key])
                else:
                    self.tt(eng, dst[:, kc, dst_c0 + c0:dst_c0 + c0 + CH], st[:, 0:CH], scale_bc[:, c0:c0 + CH], ALU.mult,
                            [sk, 'bc_tiles'], [dst_key])

    def rms_rstd(self, src, src_key, TT, tag):
        sqb, rstd = self.sqb, self.rstd
        for kc in range(NKC):
            if kc % 2 == 0:
                self.act(sqb[:, kc, :TT], src[:, kc, :TT], AF.Square, [src_key], ['sqb'])
            else:
                self.tt('pool', sqb[:, kc, :TT], src[:, kc, :TT], src[:, kc, :TT], ALU.mult, [src_key], ['sqb'])
        ps, pk = self.next_ps()
        for kc in range(NKC):
            self.mm(ps[:, :TT], self.ones_bf[:], sqb[:, kc, :TT], kc == 0, kc == NKC - 1, ['sqb', 'ones_bf'], [pk])
        self.act(rstd[:, :TT], ps[:, :TT], AF.Ln, [pk, 'consts'], ['rstd'], bias=self.epsc[:, 0:1], scale=1.0 / D)
        self.act(rstd[:, :TT], rstd[:, :TT], AF.Exp, ['rstd'], ['rstd'], scale=-0.5)
        return rstd, 'rstd'

    def run_layer(self, li, kind, TT, w_in_cols, mixer_setup, mixer_tile, is_last):
        p = self.p
        T = self.T
        ntiles = T // TT
        with ExitStack() as les:
            self.les = les
            self.lname = f"L{li}"
            self.TT = TT
            self.W_in = self.lsb("W_in", [128, NKC, w_in_cols], BF16)
            self.W_out = self.lsb("W_out", [128, NKC, D], BF16)
            self.hT = [self.lsb(f"hT{i}", [128, NKC, TT], F32) for i in range(2)]
            self.sqb = self.lsb("sqb", [128, NKC, TT], BF16)
            self.rstd = self.lsb("rstd", [128, TT], F32)
            self.hn = self.lsb("hn", [128, NKC, TT + 1], BF16)
            self.yTt = self.lsb("yTt", [128, NKC, TT], BF16)
            self.stage_i = 0
            loader = mixer_setup()
            with ExitStack() as ses:
                self.stage = [ses.enter_context(self.nc.sbuf_tensor(f"stage{i}_{self.lname}", [128, 1056], F32)) for i in range(2)]
                if loader is None:
                    self.load_weight_bf16(self.W_in, 'W_in', self.w_in_dram[kind], w_in_cols)
                    self.load_weight_bf16(self.W_out, 'W_out', self.w_out_dram[kind], D)
                else:
                    loader(ses)
                p.barrier()
            p.op('pool', lambda e: e.memset(self.hn[:, :, 0:1], 0.0), [], ['hn'])

            def load(ti):
                buf = self.hT[ti % 2]
                src = self.xT if self.first_layer else self.yT
                p.dma('sp', buf[:], src.rearrange("(c p) t -> p c t", p=128)[:, :, ti * TT:(ti + 1) * TT],
                      reads=[('hd', ti * TT // 128 + i) for i in range(TT // 128)], writes=[f"hT{ti % 2}"])

            load(0)
            for ti in range(ntiles):
                if ti + 1 < ntiles:
                    load(ti + 1)
                h = self.hT[ti % 2]
                hk = f"hT{ti % 2}"
                rstd, rk = self.rms_rstd(h, hk, TT, 'in')
                g = self.norm_g
                if ti > 0:
                    self.copy('pool', self.hn[:, :, 0:1], self.hn[:, :, TT:TT + 1], ['hn'], ['hn'])
                for kc in range(NKC):
                    self.stt('dve', self.hn[:, kc, 1:TT + 1], h[:, kc, :], g[:, li, kc:kc + 1], rstd[:, :TT],
                             ALU.mult, ALU.mult, [hk, rk, 'consts'], ['hn'])
                mixer_tile(ti)
                for j in range(NKC):
                    ps, pk = self.next_ps()
                    for kc in range(NKC):
                        self.mm(ps[:, :TT], self.W_out[:, kc, j * 128:(j + 1) * 128], self.yTt[:, kc, :TT],
                                kc == 0, kc == NKC - 1, ['W_out', 'yT'], [pk])
                    self.tt('dve', h[:, j, :], h[:, j, :], ps[:, :TT], ALU.add, [hk, pk], [hk])
                if is_last and self.do_final:
                    rstd, rk = self.rms_rstd(h, hk, TT, 'fin')
                    for kc in range(NKC):
                        self.stt('dve' if kc % 2 == 0 else 'pool', h[:, kc, :], h[:, kc, :], self.final_g[:, kc:kc + 1], rstd[:, :TT],
                                 ALU.mult, ALU.mult, [hk, rk, 'consts'], [hk])
                p.dma('sp', self.yT.rearrange("(c p) t -> p c t", p=128)[:, :, ti * TT:(ti + 1) * TT], h[:],
                      reads=[hk], writes=[('hd', (ti * TT) // 128 + i) for i in range(max(1, TT // 128))])
            self.first_layer = False
            p.barrier()
        self.les = None

    def conv_layer(self, li, is_last):
        TT = 512

        def setup():
            self.yext = self.lsb("yext", [128, NKC, TT + 2], F32)
            self.zs = [self.lsb(f"zs{i}", [128, TT], F32) for i in range(2)]
            self.acc = [self.lsb(f"acc{i}", [128, TT], F32) for i in range(2)]
            self.sg = [self.lsb(f"sg{i}", [128, TT], F32) for i in range(2)]
            self.p.op('pool', lambda e: e.memset(self.yext[:, :, 0:2], 0.0), [], [('yext', j) for j in range(NKC)])

        def tile(ti):
            W = self.W_in
            cw = self.conv_w
            for j in range(NKC):
                zs, acc, sg = self.zs[j % 2], self.acc[j % 2], self.sg[j % 2]
                zk, ak, gk = f"zs{j % 2}", f"acc{j % 2}", f"sg{j % 2}"
                pss = []
                for blk in range(4):
                    ps, pk = self.next_ps()
                    col0 = blk * D + j * 128
                    for kc in range(NKC):
                        self.mm(ps[:, :TT], W[:, kc, col0:col0 + 128], self.hn[:, kc, 1:TT + 1], kc == 0, kc == NKC - 1,
                                ['W_in', 'hn'], [pk])
                    pss.append((ps, pk))
                (pb, pbk), (pc, pck), (pz, pzk), (pg, pgk) = pss
                yk = ('yext', j)
                self.copy('act', zs[:], pz[:, :TT], [pzk], [zk])
                if ti > 0:
                    self.copy('pool', self.yext[:, j, 0:2], self.yext[:, j, TT:TT + 2], [yk], [yk])
                self.tt('dve', self.yext[:, j, 2:TT + 2], pc[:, :TT], zs[:], ALU.mult, [pck, zk], [yk])
                self.ts('pool', acc[:], self.yext[:, j, 2:TT + 2], cw[:, j, 2:3], None, ALU.mult, None, [yk, 'consts'], [ak])
                self.stt('pool', acc[:], self.yext[:, j, 1:TT + 1], cw[:, j, 1:2], acc[:], ALU.mult, ALU.add, [yk, ak, 'consts'], [ak])
                self.stt('pool', acc[:], self.yext[:, j, 0:TT], cw[:, j, 0:1], acc[:], ALU.mult, ALU.add, [yk, ak, 'consts'], [ak])
                self.act(sg[:], pg[:, :TT], AF.Silu, [pgk], [gk])
                self.tt('dve', acc[:], pb[:, :TT], acc[:], ALU.mult, [pbk, ak], [ak])
                self.tt('pool', self.yTt[:, j, :], acc[:], sg[:], ALU.mult, [ak, gk], ['yT'])

        self.run_layer(li, 'conv', TT, 4 * D, setup, tile, is_last)

    def gmlp_layer(self, li, is_last):
        TT = 512

        def setup():
            p = self.p
            self.wsT = self.lsb("wsT", [128, 8, 128], F32)
            self.bs_bc = self.lsb("bs_bc", [128, 8, TT], F32)
            self.vg_bc = self.lsb("vg_bc", [128, D], F32)
            self.vn = [self.lsb(f"vn{i}", [128, D], F32) for i in range(TT // 128)]
            self.vss = self.lsb("vss", [128, 4], F32)
            self.junk = self.lsb("junk", [128, 512], F32)
            self.s_sb = [self.lsb(f"s_sb{i}", [128, TT], F32) for i in range(2)]
            self.sg = [self.lsb(f"sg{i}", [128, TT], F32) for i in range(2)]
            p.dma('sp', self.wsT[:], self.inputs['gmlp_wsT'], [], ['wsT'])
            for g in range(8):
                p.op('pool', lambda e: e.affine_select(out=self.wsT[:, g, :], in_=self.wsT[:, g, :], pattern=[[1, 128]],
                                                       compare_op=ALU.is_ge, fill=0.0, base=0, channel_multiplier=-1),
                     ['wsT'], ['wsT'])
            for r in range(TT // 128):
                p.dma('sp', self.bs_bc[:, :, r * 128:(r + 1) * 128],
                      self.inputs['gmlp_bs'].partition_broadcast(128), [], ['bs_bc'])
            p.dma('sp', self.vg_bc[:], self.inputs['gmlp_vg'].partition_broadcast(128), [], ['vg_bc'])

        def tile(ti):
            W = self.W_in
            nblk = TT // 128
            for blk in range(nblk):
                vn = self.vn[blk]
                vk = f"vn{blk}"
                halves = []
                for hf in range(2):
                    ps, pk = self.next_ps()
                    for kc in range(NKC):
                        self.mm(ps[:, :512], self.hn[:, kc, 1 + blk * 128:1 + (blk + 1) * 128],
                                W[:, kc, D + hf * 512:D + (hf + 1) * 512], kc == 0, kc == NKC - 1, ['W_in', 'hn'], [pk])
                    halves.append((ps, pk))
                for hf, (ps, pk) in enumerate(halves):
                    self.act(self.junk[:], ps[:, :512], AF.Square, [pk], ['junk', 'vss'], accum_out=self.vss[:, hf:hf + 1])
                self.tt('dve', self.vss[:, 2:3], self.vss[:, 0:1], self.vss[:, 1:2], ALU.add, ['vss'], ['vss'])
                self.act(self.vss[:, 3:4], self.vss[:, 2:3], AF.Ln, ['vss', 'consts'], ['vss'], bias=self.epsc[:, 0:1], scale=1.0 / D)
                self.act(self.vss[:, 3:4], self.vss[:, 3:4], AF.Exp, ['vss'], ['vss'], scale=-0.5)
                for hf, (ps, pk) in enumerate(halves):
                    self.stt('dve', vn[:, hf * 512:(hf + 1) * 512], ps[:, :512], self.vss[:, 3:4],
                             self.vg_bc[:, hf * 512:(hf + 1) * 512], ALU.mult, ALU.mult, [pk, 'vss', 'vg_bc'], [vk])
            for j in range(NKC):
                s_sb, sg = self.s_sb[j % 2], self.sg[j % 2]
                sk, gk = f"s_sb{j % 2}", f"sg{j % 2}"
                ps, pk = self.next_ps()
                for blk in range(nblk):
                    self.mm(ps[:, blk * 128:(blk + 1) * 128], self.vn[blk][:, j * 128:(j + 1) * 128], self.wsT[:, j, :], True, True,
                            [f"vn{blk}", 'wsT'], [pk])
                self.tt('dve', s_sb[:], ps[:, :TT], self.bs_bc[:, j, :], ALU.add, [pk, 'bs_bc'], [sk])
                pu, puk = self.next_ps()
                for kc in range(NKC):
                    self.mm(pu[:, :TT], W[:, kc, j * 128:(j + 1) * 128], self.hn[:, kc, 1:TT + 1], kc == 0, kc == NKC - 1,
                            ['W_in', 'hn'], [puk])
                pg, pgk = self.next_ps()
                for kc in range(NKC):
                    self.mm(pg[:, :TT], W[:, kc, 2 * D + j * 128:2 * D + (j + 1) * 128], self.hn[:, kc, 1:TT + 1], kc == 0,
                            kc == NKC - 1, ['W_in', 'hn'], [pgk])
                self.act(sg[:], pg[:, :TT], AF.Silu, [pgk], [gk])
                self.tt('dve', s_sb[:], pu[:, :TT], s_sb[:], ALU.mult, [puk, sk], [sk])
                self.tt('pool', self.yTt[:, j, :], s_sb[:], sg[:], ALU.mult, [sk, gk], ['yT'])

        self.run_layer(li, 'gmlp', TT, 3 * D, setup, tile, is_last)


    def make_ident(self, ident, key):
        p = self.p
        p.op('pool', lambda e: e.memset(ident[:], 1.0), [], [key])
        p.op('pool', lambda e: e.affine_select(out=ident[:], in_=ident[:], pattern=[[-1, 128]], compare_op=ALU.is_equal,
                                               fill=0.0, base=0, channel_multiplier=1), [key], [key])

    def make_block_masks(self, C, maskT, colmask, rowmask, strict=False):
        p = self.p
        nch = 128 // C
        if maskT is not None:
            p.op('pool', lambda e: e.memset(maskT[:], 1.0), [], ['masks'])
            p.op('pool', lambda e: e.affine_select(out=maskT[:], in_=maskT[:], pattern=[[1, 128]], compare_op=ALU.is_ge if not strict else ALU.is_gt,
                                                   fill=0.0, base=0, channel_multiplier=-1), ['masks'], ['masks'])
            for c in range(1, nch):
                p.op('pool', lambda e, c=c: e.affine_select(out=maskT[:, c * C:(c + 1) * C], in_=maskT[:, c * C:(c + 1) * C], pattern=[[0, C]],
                                                            compare_op=ALU.is_ge, fill=0.0, base=-c * C, channel_multiplier=1), ['masks'], ['masks'])
        if colmask is not None:
            p.op('pool', lambda e: e.memset(colmask[:], 0.0), [], ['masks'])
            for c in range(nch):
                p.op('pool', lambda e, c=c: e.memset(colmask[:, c, c * C:(c + 1) * C], 1.0), ['masks'], ['masks'])
        if rowmask is not None:
            p.op('pool', lambda e: e.memset(rowmask[:], 1.0), [], ['masks'])
            for c in range(nch):
                p.op('pool', lambda e, c=c: e.affine_select(out=rowmask[:, c:c + 1], in_=rowmask[:, c:c + 1], pattern=[[0, 1]],
                                                            compare_op=ALU.is_ge, fill=0.0, base=-c * C, channel_multiplier=1), ['masks'], ['masks'])
                p.op('pool', lambda e, c=c: e.affine_select(out=rowmask[:, c:c + 1], in_=rowmask[:, c:c + 1], pattern=[[0, 1]],
                                                            compare_op=ALU.is_ge, fill=0.0, base=c * C + C - 1, channel_multiplier=-1), ['masks'], ['masks'])

    def hgrn_layer(self, li, is_last):
        TT = 512
        C = 32
        NB = TT // 128
        NCH = TT // C

        def setup():
            p = self.p
            L = self.lsb
            self.ident = L("ident", [128, 128], F32)
            self.make_ident(self.ident, 'ident')
            self.maskT = L("maskT", [128, 128], F32)
            self.colmask = L("colmask", [128, 4, 128], F32)
            self.rowmask = L("rowmask", [128, 4], F32)
            self.make_block_masks(C, self.maskT, self.colmask, self.rowmask)
            self.resetm = L("resetm", [128, TT], F32)
            p.op('pool', lambda e: e.memset(self.resetm[:], 1.0), [], ['masks'])
            p.op('pool', lambda e: e.memset(self.resetm[:].rearrange("p (n c) -> p n c", c=C)[:, :, 0:1], 0.0), ['masks'], ['masks'])
            self.gn_bc = L("gn_bc", [128, D], F32)
            p.dma('sp', self.gn_bc[:], self.inputs['hgrn_gn_g'].partition_broadcast(128), [], ['gn_bc'])
            self.lbl = L("lbl", [128, 4, NKC], F32)
            self.lbt = L("lbt", [128, 4, NKC], F32)
            p.dma('sp', self.lbl[:], self.inputs['hgrn_lbl'], [], ['lbl'])
            self.act(self.lbl[:], self.lbl[:], AF.Exp, ['lbl'], ['lbl'])
            self.tt('dve', self.lbt[:, 0, :], self.lbl[:, 0, :], self.lbl[:, 1, :], ALU.add, ['lbl'], ['lbt'])
            self.tt('dve', self.lbt[:, 0, :], self.lbt[:, 0, :], self.lbl[:, 2, :], ALU.add, ['lbl', 'lbt'], ['lbt'])
            self.tt('dve', self.lbt[:, 0, :], self.lbt[:, 0, :], self.lbl[:, 3, :], ALU.add, ['lbl', 'lbt'], ['lbt'])
            p.op('dve', lambda e: e.reciprocal(out=self.lbt[:, 3, :], in_=self.lbt[:, 0, :]), ['lbt'], ['lbt'])
            p.op('dve', lambda e: e.memset(self.lbt[:, 1, :], 0.0), ['lbt'], ['lbt'])
            for i in range(1, li + 1):
                self.tt('dve', self.lbt[:, 1, :], self.lbt[:, 1, :], self.lbl[:, i, :], ALU.add, ['lbl', 'lbt'], ['lbt'])
            self.tt('dve', self.lbt[:, 1, :], self.lbt[:, 1, :], self.lbt[:, 3, :], ALU.mult, ['lbt'], ['lbt'])
            self.ts('dve', self.lbt[:, 2, :], self.lbt[:, 1, :], -1.0, 1.0, ALU.mult, ALU.add, ['lbt'], ['lbt'])
            self.S = L("S_hgrn", [128, NKC, 2, 128], F32)
            p.op('pool', lambda e: e.memset(self.S[:], 0.0), [], [('S', j, b) for j in range(NKC) for b in range(2)])
            names = ['f', 'kk', 'bb', 'eb', 'enb', 'qe', 'ke', 'ko', 'dd', 'sgate']
            self.tmp = {n: L(f"h_{n}", [128, TT], F32) for n in names}
            self.qem = L("qem", [128, 4, TT], F32)
            self.v_sb = L("v_sb", [128, NB, 128], F32)
            self.attm = [L(f"attm{i}", [128, 128], F32) for i in range(2)]
            self.kom = [L(f"kom{i}", [128, 4, 128], F32) for i in range(2)]
            self.on = [L(f"on{i}", [128, 128], F32) for i in range(2)]
            self.dec = L("dec", [128, NCH], F32)
            self.oss = L("oss", [128, 4], F32)
            self.junk = L("junk", [128, 128], F32)
            self.ps_pool = [0, 1, 2, 3]

        def tile(ti):
            W = self.W_in
            t = self.tmp
            P = self.psums
            lb, oml = self.lbt[:, 1, :], self.lbt[:, 2, :]
            for j in range(NKC):
                pq, pqk = self.next_ps()
                for kc in range(NKC):
                    self.mm(pq[:, :TT], W[:, kc, j * 128:(j + 1) * 128], self.hn[:, kc, 1:TT + 1], kc == 0, kc == NKC - 1, ['W_in', 'hn'], [pqk])
                pf, pfk = self.next_ps()
                for kc in range(NKC):
                    self.mm(pf[:, :TT], W[:, kc, D + j * 128:D + (j + 1) * 128], self.hn[:, kc, 1:TT + 1], kc == 0, kc == NKC - 1, ['W_in', 'hn'], [pfk])
                pg, pgk = self.next_ps()
                for kc in range(NKC):
                    self.mm(pg[:, :TT], W[:, kc, 3 * D + j * 128:3 * D + (j + 1) * 128], self.hn[:, kc, 1:TT + 1], kc == 0, kc == NKC - 1, ['W_in', 'hn'], [pgk])
                pv, pvk = self.next_ps()
                for blk in range(NB):
                    for kc in range(NKC):
                        self.mm(pv[:, blk * 128:(blk + 1) * 128], self.hn[:, kc, 1 + blk * 128:1 + (blk + 1) * 128],
                                W[:, kc, 2 * D + j * 128:2 * D + (j + 1) * 128], kc == 0, kc == NKC - 1, ['W_in', 'hn'], [pvk])
                self.act(t['f'][:], pf[:, :TT], AF.Sigmoid, [pfk], ['t_f'])
                self.act(t['sgate'][:], pg[:, :TT], AF.Silu, [pgk], ['t_sgate'])
                self.copy('act', self.v_sb[:].rearrange("p b v -> p (b v)"), pv[:, :TT], [pvk], ['v_sb'])
                self.ts('dve', t['f'][:], t['f'][:], oml[:, j:j + 1], lb[:, j:j + 1], ALU.mult, ALU.add, ['t_f', 'lbt'], ['t_f'])
                self.ts('pool', t['kk'][:], t['f'][:], -1.0, 1.0, ALU.mult, ALU.add, ['t_f'], ['t_kk'])
                self.act(t['dd'][:], t['f'][:], AF.Ln, ['t_f'], ['t_dd'])
                self.p.op('dve', lambda e: e.tensor_tensor_scan(out=t['bb'][:], data0=self.resetm[:], data1=t['dd'][:], initial=0.0,
                                                                op0=ALU.mult, op1=ALU.add), ['t_dd', 'masks'], ['t_bb'])
                self.act(t['eb'][:], t['bb'][:], AF.Exp, ['t_bb'], ['t_eb'])
                self.act(t['enb'][:], t['bb'][:], AF.Exp, ['t_bb'], ['t_enb'], scale=-1.0)
                b3 = t['bb'][:].rearrange("p (n c) -> p n c", c=C)
                self.act(self.dec[:], b3[:, :, C - 1], AF.Exp, ['t_bb'], ['dec'])
                self.tt('dve', t['qe'][:], pq[:, :TT], t['eb'][:], ALU.mult, [pqk, 't_eb'], ['t_qe'])
                self.tt('pool', t['ke'][:], t['kk'][:], t['enb'][:], ALU.mult, ['t_kk', 't_enb'], ['t_ke'])
                self.tt('pool', t['dd'][:].rearrange("p (n c) -> p n c", c=C), b3[:, :, C - 1:C].to_broadcast([128, NCH, C]), b3, ALU.subtract,
                        ['t_bb'], ['t_dd'])
                self.act(t['dd'][:], t['dd'][:], AF.Exp, ['t_dd'], ['t_dd'])
                self.tt('pool', t['ko'][:], t['kk'][:], t['dd'][:], ALU.mult, ['t_kk', 't_dd'], ['t_ko'])
                qe4 = t['qe'][:].rearrange("p (b t) -> p b t", t=128)
                for c in range(4):
                    self.tt('pool' if c % 2 else 'dve', self.qem[:, c, :].rearrange("p (b t) -> p b t", t=128), qe4,
                            self.colmask[:, c:c + 1, :].to_broadcast([128, NB, 128]), ALU.mult, ['t_qe', 'masks'], ['qem'])
                py, pyk = P[4], 'ps4'
                for blk in range(NB):
                    par = blk % 2
                    ps_a, ps_t, ps_o = P[5][:, 0:128], P[5][:, 128:256], P[7][:, 0:128]
                    ka, kt, ko_ = 'ps5', 'ps5', 'ps7'
                    cs = slice(blk * 128, (blk + 1) * 128)
                    attm, kom, on = self.attm[par], self.kom[par], self.on[par]
                    self.mm(ps_a, t['ke'][:, cs], t['qe'][:, cs], True, True, ['t_ke', 't_qe'], [ka])
                    self.tt('dve', attm[:], ps_a, self.maskT[:], ALU.mult, [ka, 'masks'], [f'attm{par}'])
                    self.p.op('pe', lambda e: e.transpose(ps_t, t['ko'][:, cs], self.ident[:]), ['t_ko', 'ident'], [kt])
                    for c in range(4):
                        self.act(kom[:, c, :], ps_t, AF.Copy, [kt, 'masks'], [f'kom{par}'], scale=self.rowmask[:, c:c + 1])
                    self.mm(ps_o, attm[:], self.v_sb[:, blk, :], True, False, [f'attm{par}', 'v_sb'], [ko_])
                    for c in range(4):
                        Sc, Sn = self.S[:, j, c % 2, :], self.S[:, j, (c + 1) % 2, :]
                        kSc, kSn = ('S', j, c % 2), ('S', j, (c + 1) % 2)
                        self.mm(ps_o, self.qem[:, c, cs], Sc, False, c == 3, ['qem', kSc], [ko_])
                        ps_u = P[6][:, c * 128:(c + 1) * 128]
                        ku = 'ps6'
                        self.mm(ps_u, kom[:, c, :], self.v_sb[:, blk, :], True, True, [f'kom{par}', 'v_sb'], [ku])
                        self.stt('dve', Sn, Sc, self.dec[:, blk * 4 + c:blk * 4 + c + 1], ps_u, ALU.mult, ALU.add, [kSc, 'dec', ku], [kSn])
                    self.act(self.junk[:], ps_o, AF.Square, [ko_], ['junk', 'oss'], accum_out=self.oss[:, 0:1])
                    self.act(self.oss[:, 1:2], self.oss[:, 0:1], AF.Ln, ['oss', 'consts'], ['oss'], bias=self.epsc[:, 0:1], scale=1.0 / 128)
                    self.act(self.oss[:, 1:2], self.oss[:, 1:2], AF.Exp, ['oss'], ['oss'], scale=-0.5)
                    self.stt('dve', on[:], ps_o, self.oss[:, 1:2], self.gn_bc[:, j * 128:(j + 1) * 128], ALU.mult, ALU.mult,
                             [ko_, 'oss', 'gn_bc'], [f'on{par}'])
                    self.p.op('pe', lambda e: e.transpose(py[:, cs], on[:], self.ident[:]), [f'on{par}', 'ident'], [pyk])
                self.tt('dve', self.yTt[:, j, :], py[:, :TT], t['sgate'][:], ALU.mult, [pyk, 't_sgate'], ['yT'])

        self.run_layer(li, 'hgrn', TT, 4 * D, setup, tile, is_last)
        self.ps_pool = list(range(8))


    def rwkv_layer(self, li, is_last):
        TT = 256
        NB = TT // 128
        WC = 3200
        NDT = self.neu_dt
        LC = -0.6065306597126334

        def setup():
            p = self.p
            L = self.lsb
            self.ident = L("ident", [128, 128], F32)
            self.make_ident(self.ident, 'ident')
            self.ident_n = L("ident_n", [128, 128], NDT)
            self.copy('dve', self.ident_n[:], self.ident[:], ['ident'], ['ident'])
            self.maskS = L("maskS", [128, 128], F32)
            self.maskI = L("maskI", [128, 128], F32)
            self.maskSL = L("maskSL", [128, 128], F32)
            for (m, pat, cm, cmp_) in ((self.maskS, 1, -1, ALU.is_gt), (self.maskI, 1, -1, ALU.is_ge), (self.maskSL, -1, 1, ALU.is_gt)):
                p.op('pool', lambda e, m=m: e.memset(m[:], 1.0), [], ['masks'])
                p.op('pool', lambda e, m=m, pat=pat, cm=cm, cmp_=cmp_: e.affine_select(
                    out=m[:], in_=m[:], pattern=[[pat, 128]], compare_op=cmp_, fill=0.0, base=0, channel_multiplier=cm), ['masks'], ['masks'])
            self.blockones = L("blockones", [128, 128], F32)
            p.op('pool', lambda e: e.memset(self.blockones[:], 1.0), [], ['masks'])
            p.op('pool', lambda e: e.memset(self.blockones[0:64, 64:128], 0.0), ['masks'], ['masks'])
            p.op('pool', lambda e: e.memset(self.blockones[64:128, 0:64], 0.0), ['masks'], ['masks'])
            self.resetm = L("resetm", [128, TT], F32)
            p.op('pool', lambda e: e.memset(self.resetm[:], 1.0), [], ['masks'])
            p.op('pool', lambda e: e.memset(self.resetm[:].rearrange("p (n c) -> p n c", c=128)[:, :, 0:1], 0.0), ['masks'], ['masks'])
            p.op('pool', lambda e: e.memset(self.epsc[:, 1:2], GN_EPS), [], ['consts'])
            self.mu_fm = L("mu_fm", [128, 33], F32)
            self.omu_fm = L("omu_fm", [128, 33], F32)
            p.dma('sp', self.mu_fm[:], self.inputs['rwkv_mu_fm'], [], ['rw_vecs'])
            self.ts('dve', self.omu_fm[:], self.mu_fm[:], -1.0, 1.0, ALU.mult, ALU.add, ['rw_vecs'], ['rw_vecs'])
            self.vecs = L("rw_vecs", [128, 5, NKC], F32)
            p.dma('sp', self.vecs[:], self.inputs['rwkv_vecs'], [], ['rw_vecs'])
            self.lw2 = L("lw2", [128, D], F32)
            p.dma('sp', self.lw2[:], self.inputs['rwkv_lw2'], [], ['lw2'])
            self.gng_bc = L("gng_bc", [128, D], F32)
            self.gnb_bc = L("gnb_bc", [128, D], F32)
            p.dma('sp', self.gng_bc[:], self.inputs['rwkv_gn_g'].partition_broadcast(128), [], ['bc_tiles'])
            p.dma('sp', self.gnb_bc[:], self.inputs['rwkv_gn_b'].partition_broadcast(128), [], ['bc_tiles'])
            self.Wva = L("Wva", [128, NKC, D], BF16)
            self.Wvb = L("Wvb", [128, NKC, D], BF16)
            src = self.w_in_dram['rwkv']

            def loader(tes):
                muv = tes.enter_context(self.nc.sbuf_tensor("muv_bc", [128, D], F32))
                p.dma('sp', muv[:], self.inputs['rwkv_mu'][2 * D:3 * D].partition_broadcast(128), [], ['bc_tiles'])
                self.load_weight_bf16(self.Wvb, 'Wv', src, D, src_c0=2 * D, scale_bc=muv)
                self.ts('dve', muv[:], muv[:], -1.0, 1.0, ALU.mult, ALU.add, ['bc_tiles'], ['bc_tiles'])
                self.load_weight_bf16(self.Wva, 'Wv', src, D, src_c0=2 * D, scale_bc=muv)
                self.load_weight_bf16(self.W_in, 'W_in', src, 2 * D, src_c0=0, dst_c0=0)
                self.load_weight_bf16(self.W_in, 'W_in', src, 128, src_c0=3 * D, dst_c0=2 * D)
                self.load_weight_bf16(self.W_in, 'W_in', src, D, src_c0=3 * D + 128, dst_c0=2 * D + 128)
                self.load_weight_bf16(self.W_out, 'W_out', self.w_out_dram['rwkv'], D)
            self.S = L("S_rwkv", [128, NKC, 64], F32)
            p.op('pool', lambda e: e.memset(self.S[:], 0.0), [], [('S', j) for j in range(NKC)])
            self.pcar = L("pcar", [128, 25], F32)
            p.op('pool', lambda e: e.memset(self.pcar[:], 0.0), [], [('pcar', i) for i in range(25)])
            self.pm_ext = [L(f"pm_ext{i}", [128, TT + 1], F32) for i in range(2)]
            self.pm_i = 0
            names = ['lo', 'r', 'k', 'tmp', 'sigw', 'a', 'kk', 'rn', 'kmod', 'bbv', 'c', 'e1', 'e2', 'khat', 'bhat']
            self.tmp = {n: L(f"w_{n}", [128, TT], F32) for n in names}
            for n in ['rt_bf', 'bt_bf', 'at_bf', 'kt_h0', 'kt_h1', 'bt_h0', 'bt_h1', 'at_h0', 'at_h1']:
                self.tmp[n] = L(f"w_{n}", [128, TT], BF16)
            self.hm = L("hm", [128, 2], F32)
            p.op('pool', lambda e: e.memset(self.hm[:], 0.0), [], ['masks'])
            p.op('pool', lambda e: e.memset(self.hm[0:64, 0:1], 1.0), ['masks'], ['masks'])
            p.op('pool', lambda e: e.memset(self.hm[64:128, 1:2], 1.0), ['masks'], ['masks'])
            self.pt = {n: [L(f"wp_{n}{q}", [128, TT], F32) for q in range(2)] for n in ['at', 'rt', 'rkr', 'sgate']}
            self.v_sb = [L(f"v_sb{q}", [128, NB, 128], F32) for q in range(2)]
            self.v_bf = [L(f"v_bf{q}", [128, NB, 128], BF16) for q in range(2)]
            self.dec = [L(f"dec{q}", [128, NB], F32) for q in range(2)]
            NCHN = NB * 2
            self.Pb = [L(f"Pb{i}", [128, NCHN, 128], NDT) for i in range(2)]
            self.Qb = [L(f"Qb{i}", [128, NCHN, 128], NDT) for i in range(2)]
            self.NT = [L(f"NT{q}", [128, NCHN, 128], NDT) for q in range(2)]
            self.Aak = [L(f"Aak{q}", [128, NCHN, 128], BF16) for q in range(2)]
            self.Ark = [L(f"Ark{q}", [128, NCHN, 128], BF16) for q in range(2)]
            self.Arb = [L(f"Arb{q}", [128, NCHN, 128], BF16) for q in range(2)]
            self.ident4 = L("ident4", [128, NCHN, 128], NDT)
            for c in range(NCHN):
                self.copy('dve', self.ident4[:, c, :], self.ident[:], ['ident'], ['ident'])
            self.khm = [[L(f"khm{q}{b}", [128, 128], BF16) for b in range(NB)] for q in range(2)]
            self.bhm = [[L(f"bhm{q}{b}", [128, 128], BF16) for b in range(NB)] for q in range(2)]
            self.Z_sb = L("Z_sb", [128, 128], NDT)
            self.U_bf = L("U_bf", [128, 128], BF16)
            self.yn = L("yn", [128, 128], F32)
            self.bon = L("bon", [128, 128], F32)
            self.gst = L("gst", [128, 12], F32)
            self.junk = L("junk", [128, 64], F32)
            self.ps_pool = [0, 1, 2]
            self.nm_rr = 0
            return loader

        NMB = [3, 4, 7]

        def nm_ps():
            i = NMB[self.nm_rr % len(NMB)]
            self.nm_rr += 1
            return self.psums[i], f"ps{i}"

        def shift(ps, pk, dst, dk, idx, mt):
            pm = self.pm_ext[self.pm_i % 2]
            pmk = f"pm_ext{self.pm_i % 2}"
            self.pm_i += 1
            ck = ('pcar', idx)
            self.copy('pool', pm[:, 0:1], self.pcar[:, idx:idx + 1], [ck], [pmk])
            self.act(pm[:, 1:TT + 1], ps[:, :TT], AF.Copy, [pk, 'rw_vecs'], [pmk], scale=self.mu_fm[:, mt:mt + 1])
            self.copy('pool', self.pcar[:, idx:idx + 1], pm[:, TT:TT + 1], [pmk], [ck])
            self.act(dst, ps[:, :TT], AF.Copy, [pk, 'rw_vecs'], [dk], scale=self.omu_fm[:, mt:mt + 1])
            self.tt('dve', dst, dst, pm[:, 0:TT], ALU.add, [dk, pmk], [dk])

        def proj(col0):
            ps, pk = self.next_ps()
            for kc in range(NKC):
                self.mm(ps[:, :TT], self.W_in[:, kc, col0:col0 + 128], self.hn[:, kc, 1:TT + 1], kc == 0, kc == NKC - 1, ['W_in', 'hn'], [pk])
            return ps, pk

        hsl = [slice(0, 64), slice(64, 128)]

        def A_gen(j):
            t = self.tmp
            V = self.vecs
            q = j % 2
            jc = slice(j * 128, (j + 1) * 128)
            at, rt, rkr, sgate = (self.pt[n][q] for n in ('at', 'rt', 'rkr', 'sgate'))
            kat, krt, krkr, ksg = (f'p_{n}{q}' for n in ('at', 'rt', 'rkr', 'sgate'))
            v_sb, v_bf, dec = self.v_sb[q], self.v_bf[q], self.dec[q]
            kv, kvb, kdec = f'v_sb{q}', f'v_bf{q}', f'dec{q}'
            ps, pk = proj(j * 128)
            shift(ps, pk, t['r'][:], 't_r', j, j)
            ps, pk = proj(D + j * 128)
            shift(ps, pk, t['k'][:], 't_k', 8 + j, 8 + j)
            yield
            ps, pk = proj(2 * D + 128 + j * 128)
            shift(ps, pk, t['tmp'][:], 't_tmp', 16 + j, 25 + j)
            self.act(sgate[:], t['tmp'][:], AF.Silu, ['t_tmp'], [ksg])
            pv, pvk = self.next_ps()
            for blk in range(NB):
                n = 0
                for kc in range(NKC):
                    for (Wv, off) in ((self.Wva, 1), (self.Wvb, 0)):
                        self.mm(pv[:, blk * 128:(blk + 1) * 128], self.hn[:, kc, off + blk * 128:off + (blk + 1) * 128], Wv[:, kc, jc],
                                n == 0, n == 2 * NKC - 1, ['Wv', 'hn'], [pvk])
                        n += 1
            self.copy('act', v_sb[:].rearrange("p b v -> p (b v)"), pv[:, :TT], [pvk], [kv])
            self.copy('dve', v_bf[:].rearrange("p b v -> p (b v)"), pv[:, :TT], [pvk], [kvb])
            yield
            pw, pwk = self.next_ps()
            self.mm(pw[:, :TT], self.lw2[0:64, jc], t['lo'][0:64, :], True, True, ['lw2', 't_lo'], [pwk])
            self.act(t['sigw'][:], pw[:, :TT], AF.Sigmoid, [pwk, 'rw_vecs'], ['t_sigw'], bias=V[:, 0, j:j + 1])
            pa, pak = self.next_ps()
            self.mm(pa[:, :TT], self.lw2[64:128, jc], t['lo'][64:128, :], True, True, ['lw2', 't_lo'], [pak])
            self.act(t['a'][:], pa[:, :TT], AF.Sigmoid, [pak, 'rw_vecs'], ['t_a'], bias=V[:, 1, j:j + 1])
            self.ts('dve', t['kk'][:], t['k'][:], V[:, 2, j:j + 1], None, ALU.mult, None, ['t_k', 'rw_vecs'], ['t_kk'])
            self.tt('pool', t['tmp'][:], t['kk'][:], t['kk'][:], ALU.mult, ['t_kk'], ['t_tmp'])
            pn, pnk = self.next_ps()
            self.mm(pn[:, :TT], self.blockones[:], t['tmp'][:], True, True, ['masks', 't_tmp'], [pnk])
            self.ts('dve', t['rn'][:], pn[:, :TT], 1e-24, None, ALU.max, None, [pnk], ['t_rn'])
            self.act(t['rn'][:], t['rn'][:], AF.Ln, ['t_rn'], ['t_rn'])
            self.act(t['rn'][:], t['rn'][:], AF.Exp, ['t_rn'], ['t_rn'], scale=-0.5)
            self.tt('pool', t['kk'][:], t['kk'][:], t['rn'][:], ALU.mult, ['t_kk', 't_rn'], ['t_kk'])
            self.ts('dve', t['tmp'][:], t['a'][:], -1.0, V[:, 3, j:j + 1], ALU.add, ALU.mult, ['t_a', 'rw_vecs'], ['t_tmp'])
            self.stt('dve', t['kmod'][:], t['tmp'][:], 1.0, t['k'][:], ALU.add, ALU.mult, ['t_tmp', 't_k'], ['t_kmod'])
            self.tt('pool', t['bbv'][:], t['kk'][:], t['a'][:], ALU.mult, ['t_kk', 't_a'], ['t_bbv'])
            yield
            self.p.op('dve', lambda e: e.tensor_tensor_scan(out=t['c'][:], data0=self.resetm[:], data1=t['sigw'][:], initial=0.0,
                                                            op0=ALU.mult, op1=ALU.add), ['t_sigw', 'masks'], ['t_c'])
            self.act(t['e1'][:], t['c'][:], AF.Exp, ['t_c'], ['t_e1'], scale=LC)
            self.tt('pool', rt[:], t['r'][:], t['e1'][:], ALU.mult, ['t_r', 't_e1'], [krt])
            self.copy('act', t['rt_bf'][:], rt[:], [krt], ['t_rt_bf'])
            self.act(t['e2'][:], t['c'][:], AF.Exp, ['t_c'], ['t_e2'], scale=-LC)
            for hd in range(2):
                self.stt('dve', t[f'kt_h{hd}'][:], t['kmod'][:], self.hm[:, hd:hd + 1], t['e2'][:], ALU.mult, ALU.mult,
                         ['t_kmod', 't_e2', 'masks'], [f't_kt_h{hd}'])
                self.stt('dve', t[f'bt_h{hd}'][:], t['bbv'][:], self.hm[:, hd:hd + 1], t['e2'][:], ALU.mult, ALU.mult,
                         ['t_bbv', 't_e2', 'masks'], [f't_bt_h{hd}'])
            self.tt('pool', t['bt_bf'][:], t['bbv'][:], t['e2'][:], ALU.mult, ['t_bbv', 't_e2'], ['t_bt_bf'])
            self.tt('pool', t['e1'][:], t['c'][:], t['sigw'][:], ALU.subtract, ['t_c', 't_sigw'], ['t_e1'])
            self.act(t['e1'][:], t['e1'][:], AF.Exp, ['t_e1'], ['t_e1'], scale=LC)
            self.stt('dve', at[:], t['kk'][:], -1.0, t['e1'][:], ALU.mult, ALU.mult, ['t_kk', 't_e1'], [kat])
            self.copy('act', t['at_bf'][:], at[:], [kat], ['t_at_bf'])
            for hd in range(2):
                self.act(t[f'at_h{hd}'][:], at[:], AF.Copy, [kat, 'masks'], [f't_at_h{hd}'], scale=self.hm[:, hd:hd + 1])
            yield
            c3 = t['c'][:].rearrange("p (n c) -> p n c", c=128)
            self.tt('pool', t['e2'][:].rearrange("p (n c) -> p n c", c=128), c3[:, :, 127:128].to_broadcast([128, NB, 128]), c3,
                    ALU.subtract, ['t_c'], ['t_e2'])
            self.act(t['e2'][:], t['e2'][:], AF.Exp, ['t_e2'], ['t_e2'], scale=LC)
            self.act(dec[:], c3[:, :, 127], AF.Exp, ['t_c'], [kdec], scale=LC)
            self.tt('pool', t['khat'][:], t['kmod'][:], t['e2'][:], ALU.mult, ['t_kmod', 't_e2'], ['t_khat'])
            self.tt('dve', t['bhat'][:], t['bbv'][:], t['e2'][:], ALU.mult, ['t_bbv', 't_e2'], ['t_bhat'])
            self.stt('dve', rkr[:], t['r'][:], V[:, 4, j:j + 1], t['kmod'][:], ALU.mult, ALU.mult, ['t_r', 'rw_vecs', 't_kmod'], [krkr])
            yield
            NCH = NB * 2
            specs = {'P': ('bt_h', 'at_bf', self.maskS, self.Pb[0], 'Pb0'),
                     'Q': ('at_h', 'bt_bf', self.maskSL, self.Qb[0], 'Qb0'),
                     'ak': ('kt_h', 'at_bf', self.maskS, self.Aak[q], f'Aak{q}'),
                     'rk': ('kt_h', 'rt_bf', self.maskI, self.Ark[q], f'Ark{q}'),
                     'rb': ('bt_h', 'rt_bf', self.maskI, self.Arb[q], f'Arb{q}')}
            for name in ('P', 'Q', 'ak', 'rk', 'rb'):
                lh, rh, mask, dst, dk = specs[name]
                ps, pk = nm_ps()
                for c in range(NCH):
                    blk, hd = c // 2, c % 2
                    cs = slice(blk * 128, (blk + 1) * 128)
                    self.mm(ps[:, c * 128:(c + 1) * 128], t[f'{lh}{hd}'][:, cs], t[rh][:, cs], True, True, [f't_{lh}{hd}', 't_' + rh], [pk])
                self.tt('dve', dst[:], ps[:, 0:NCH * 128].rearrange("p (c t) -> p c t", c=NCH),
                        mask[:, None, :].to_broadcast([128, NCH, 128]), ALU.mult, [pk, 'masks'], [dk])
                if name == 'Q':
                    self.tt('pool', self.NT[q][:], self.ident4[:], self.Pb[0][:], ALU.add, ['ident', 'Pb0'], [f'NT{q}'])
                    yield
            for blk in range(NB):
                cs = slice(blk * 128, (blk + 1) * 128)
                for (srcn, dst, dk) in (('khat', self.khm[q][blk], f'khm{q}{blk}'), ('bhat', self.bhm[q][blk], f'bhm{q}{blk}')):
                    ps, pk = nm_ps()
                    self.p.op('pe', lambda e, ps=ps, srcn=srcn, cs=cs: e.transpose(ps[:, 0:128], t[srcn][:, cs], self.ident[:]),
                              ['t_' + srcn, 'ident'], [pk])
                    self.copy('act', dst[:], ps[:, 0:128], [pk], [dk])
            yield
            NTq, kNT = self.NT[q], f'NT{q}'
            for i in range(6):
                a_, b_ = i % 2, (i + 1) % 2
                Pa, Qa, Pn, Qn = self.Pb[a_], self.Qb[a_], self.Pb[b_], self.Qb[b_]
                kPa, kQa, kPn, kQn = f'Pb{a_}', f'Qb{a_}', f'Pb{b_}', f'Qb{b_}'
                if i < 5:
                    ps, pk = nm_ps()
                    for c in range(NCH):
                        self.mm(ps[:, c * 128:(c + 1) * 128], Qa[:, c, :], Pa[:, c, :], True, True, [kQa, kPa], [pk])
                    self.copy('act', Pn[:].rearrange("p c t -> p (c t)"), ps[:, 0:NCH * 128], [pk], [kPn])
                ps, pk = nm_ps()
                for c in range(NCH):
                    self.mm(ps[:, c * 128:(c + 1) * 128], Pa[:, c, :], Qa[:, c, :], True, True, [kPa, kQa], [pk])
                self.copy('dve' if i % 2 == 0 else 'act', Qn[:].rearrange("p c t -> p (c t)"), ps[:, 0:NCH * 128], [pk], [kQn])
                yield
                ps, pk = nm_ps()
                for c in range(NCH):
                    self.mm(ps[:, c * 128:(c + 1) * 128], Qn[:, c, :], NTq[:, c, :], True, True, [kQn, kNT], [pk])
                self.tt('dve', NTq[:].rearrange("p c t -> p (c t)"), NTq[:].rearrange("p c t -> p (c t)"), ps[:, 0:NCH * 128], ALU.add,
                        [kNT, pk], [kNT])
                yield

        def B_gen(j):
            t = self.tmp
            P = self.psums
            q = j % 2
            jc = slice(j * 128, (j + 1) * 128)
            at, rt, rkr, sgate = (self.pt[n][q] for n in ('at', 'rt', 'rkr', 'sgate'))
            kat, krt, krkr, ksg = (f'p_{n}{q}' for n in ('at', 'rt', 'rkr', 'sgate'))
            v_sb, v_bf, dec = self.v_sb[q], self.v_bf[q], self.dec[q]
            kv, kvb, kdec = f'v_sb{q}', f'v_bf{q}', f'dec{q}'
            kS = ('S', j)
            for blk in range(NB):
                cs = slice(blk * 128, (blk + 1) * 128)
                khm, bhm = self.khm[q][blk], self.bhm[q][blk]
                kkh, kbh = f'khm{q}{blk}', f'bhm{q}{blk}'
                pz, pzk = P[5], 'ps5'
                for hd in range(2):
                    hs, hc, c = hsl[hd], slice(hd * 64, (hd + 1) * 64), blk * 2 + hd
                    self.mm(pz[:, hc], self.Aak[q][:, c, :], v_bf[:, blk, hc], True, False, [f'Aak{q}', kvb], [pzk])
                    self.mm(pz[:, hc], at[hs, cs], self.S[hs, j, :], False, True, [kat, kS], [pzk])
                self.copy('act', self.Z_sb[:], pz[:, 0:128], [pzk], ['Z_sb'])
                yield
                for hd in range(2):
                    hc, c = slice(hd * 64, (hd + 1) * 64), blk * 2 + hd
                    self.mm(pz[:, hc], self.NT[q][:, c, :], self.Z_sb[:, hc], True, True, [f'NT{q}', 'Z_sb'], [pzk])
                self.copy('act', self.U_bf[:], pz[:, 0:128], [pzk], ['U_bf'])
                yield
                self.mm(pz[:, 0:128], khm[:], v_bf[:, blk, :], True, False, [kkh, kvb], [pzk])
                self.mm(pz[:, 0:128], bhm[:], self.U_bf[:], False, True, [kbh, 'U_bf'], [pzk])
                py, pyk = P[6], 'ps6'
                for hd in range(2):
                    hs, hc, c = hsl[hd], slice(hd * 64, (hd + 1) * 64), blk * 2 + hd
                    yc = slice(hd * 128, hd * 128 + 64)
                    bc_ = slice(hd * 128 + 64, hd * 128 + 128)
                    self.mm(py[:, yc], self.Ark[q][:, c, :], v_bf[:, blk, hc], True, False, [f'Ark{q}', kvb], [pyk])
                    self.mm(py[:, yc], rt[hs, cs], self.S[hs, j, :], False, False, [krt, kS], [pyk])
                    self.mm(py[:, yc], self.Arb[q][:, c, :], self.U_bf[:, hc], False, True, [f'Arb{q}', 'U_bf'], [pyk])
                    self.mm(py[:, bc_], rkr[hs, cs], self.blockones[hs, hs], True, True, [krkr, 'masks'], [pyk])
                for hd in range(2):
                    hs, hc = hsl[hd], slice(hd * 64, (hd + 1) * 64)
                    self.stt('dve', self.S[hs, j, :], self.S[hs, j, :], dec[hs, blk:blk + 1], pz[hs, hc], ALU.mult, ALU.add,
                             [kS, kdec, pzk], [kS])
                yield
                g = self.gst
                for hd in range(2):
                    hc = slice(hd * 64, (hd + 1) * 64)
                    yc = slice(hd * 128, hd * 128 + 64)
                    bc_ = slice(hd * 128 + 64, hd * 128 + 128)
                    self.act(self.junk[:], py[:, yc], AF.Identity, [pyk], ['junk', 'gst'], accum_out=g[:, hd:hd + 1])
                    self.act(self.junk[:], py[:, yc], AF.Square, [pyk], ['junk', 'gst'], accum_out=g[:, 2 + hd:3 + hd])
                    self.tt('dve', self.bon[:, hc], py[:, bc_], v_sb[:, blk, hc], ALU.mult, [pyk, kv], ['bon'])
                self.ts('dve', g[:, 4:6], g[:, 0:2], 1.0 / 64, None, ALU.mult, None, ['gst'], ['gst'])
                self.tt('dve', g[:, 6:8], g[:, 4:6], g[:, 4:6], ALU.mult, ['gst'], ['gst'])
                self.stt('dve', g[:, 8:10], g[:, 2:4], 1.0 / 64, g[:, 6:8], ALU.mult, ALU.subtract, ['gst'], ['gst'])
                self.act(g[:, 8:10], g[:, 8:10], AF.Ln, ['gst', 'consts'], ['gst'], bias=self.epsc[:, 1:2])
                self.act(g[:, 8:10], g[:, 8:10], AF.Exp, ['gst'], ['gst'], scale=-0.5)
                for hd in range(2):
                    hc = slice(hd * 64, (hd + 1) * 64)
                    yc = slice(hd * 128, hd * 128 + 64)
                    self.ts('dve', self.yn[:, hc], py[:, yc], g[:, 4 + hd:5 + hd], g[:, 8 + hd:9 + hd], ALU.subtract, ALU.mult,
                            [pyk, 'gst'], ['yn'])
                yield
                self.tt('pool', self.yn[:], self.yn[:], self.gng_bc[:, jc], ALU.mult, ['yn', 'bc_tiles'], ['yn'])
                self.tt('pool', self.yn[:], self.yn[:], self.gnb_bc[:, jc], ALU.add, ['yn', 'bc_tiles'], ['yn'])
                self.tt('pool', self.yn[:], self.yn[:], self.bon[:], ALU.add, ['yn', 'bon'], ['yn'])
                ps, pk = nm_ps()
                self.p.op('pe', lambda e, ps=ps: e.transpose(ps[:, 0:128], self.yn[:], self.ident[:]), ['yn', 'ident'], [pk])
                self.tt('dve', self.yTt[:, j, cs], ps[:, 0:128], sgate[:, cs], ALU.mult, [pk, ksg], ['yT'])
                yield

        def drive(gens):
            gens = [g for g in gens if g is not None]
            while gens:
                for g in list(gens):
                    try:
                        next(g)
                    except StopIteration:
                        gens.remove(g)

        def tile(ti):
            t = self.tmp
            ps, pk = proj(2 * D)
            shift(ps, pk, t['lo'][:], 't_lo', 24, 24)
            self.act(t['lo'][0:64, :], t['lo'][0:64, :], AF.Tanh, ['t_lo'], ['t_lo'])
            drive([A_gen(0)])
            for j in range(NKC):
                drive([B_gen(j), A_gen(j + 1) if j + 1 < NKC else None])

        self.run_layer(li, 'rwkv', TT, WC, setup, tile, is_last)
        self.ps_pool = list(range(8))

    def build(self):
        nc = self.nc
        T = self.T
        self.xT = self.din("xT", [D, T])
        self.yT = nc.dram_tensor("yT", [D, T], F32, kind="ExternalOutput").ap()
        d_norm_g = self.din("norm_g", [128, 4, NKC])
        d_final_g = self.din("final_g", [128, NKC])
        self.w_in_dram, self.w_out_dram = {}, {}
        kinds = [k for (_, k) in self.layers]
        if 'conv' in kinds:
            self.w_in_dram['conv'] = self.din("conv_w_in", [D, 4 * D])
            self.w_out_dram['conv'] = self.din("conv_w_out", [D, D])
            d_conv_w = self.din("conv_w", [128, NKC, 3])
        if 'rwkv' in kinds:
            self.w_in_dram['rwkv'] = self.din("rwkv_w_in", [D, 4 * D + 128])
            self.w_out_dram['rwkv'] = self.din("rwkv_w_out", [D, D])
            self.din("rwkv_mu_fm", [128, 33])
            self.din("rwkv_mu", [4 * D + 128])
            self.din("rwkv_vecs", [128, 5, NKC])
            self.din("rwkv_lw2", [128, D])
            self.din("rwkv_gn_g", [D])
            self.din("rwkv_gn_b", [D])
        if 'hgrn' in kinds:
            self.w_in_dram['hgrn'] = self.din("hgrn_w_in", [D, 4 * D])
            self.w_out_dram['hgrn'] = self.din("hgrn_w_out", [D, D])
            self.din("hgrn_gn_g", [D])
            self.din("hgrn_lbl", [128, 4, NKC])
        if 'gmlp' in kinds:
            self.w_in_dram['gmlp'] = self.din("gmlp_w_in", [D, 3 * D])
            self.w_out_dram['gmlp'] = self.din("gmlp_w_out", [D, D])
            self.din("gmlp_wsT", [128, 8, 128])
            self.din("gmlp_bs", [8, 128])
            self.din("gmlp_vg", [D])
        with ExitStack() as es:
            self.es = es
            nc.allow_low_precision("bf16 matmul operands, fp32 accumulation")
            self.p = p = Prog(nc, es)
            self.psums = [es.enter_context(nc.psum_tensor(f"ps{i}", [128, 512], F32)) for i in range(8)]
            self.ps_rr = 0
            self.ps_pool = list(range(8))
            self.ones_bf = self.sb("ones_bf", [128, 128], BF16)
            self.epsc = self.sb("epsc", [128, 2], F32)
            self.norm_g = self.sb("norm_g_sb", [128, 4, NKC], F32)
            self.final_g = self.sb("final_g_sb", [128, NKC], F32)
            p.op('pool', lambda e: e.memset(self.ones_bf[:], 1.0), [], ['ones_bf'])
            p.op('pool', lambda e: e.memset(self.epsc[:, 0:1], RMS_EPS), [], ['consts'])
            p.dma('sp', self.norm_g[:], d_norm_g, [], ['consts'])
            p.dma('sp', self.final_g[:], d_final_g, [], ['consts'])
            if 'conv' in kinds:
                self.conv_w = self.sb("conv_w_sb", [128, NKC, 3], F32)
                p.dma('sp', self.conv_w[:], d_conv_w, [], ['consts'])
            self.first_layer = True
            for n, (li, kind) in enumerate(self.layers):
                is_last = n == len(self.layers) - 1
                if kind == 'conv':
                    self.conv_layer(li, is_last)
                elif kind == 'gmlp':
                    self.gmlp_layer(li, is_last)
                elif kind == 'hgrn':
                    self.hgrn_layer(li, is_last)
                elif kind == 'rwkv':
                    self.rwkv_layer(li, is_last)
                else:
                    raise ValueError(kind)
            p.finish('sp')
            self.stats = (p.n_ins, p.n_wait)
        return nc


def prep_inputs(inp, b, layers):
    f = np.float32
    m = {}
    m["xT"] = np.ascontiguousarray(np.asarray(inp["x"][b], f).T)
    m["norm_g"] = np.ascontiguousarray(np.asarray(inp["norm_g"], f).reshape(4, NKC, 128).transpose(2, 0, 1))
    m["final_g"] = np.ascontiguousarray(np.asarray(inp["final_g"], f).reshape(NKC, 128).T)
    kinds = [k for (_, k) in layers]
    if 'conv' in kinds:
        m["conv_w_in"] = np.ascontiguousarray(np.asarray(inp["conv_w_in"][0], f))
        m["conv_w_out"] = np.ascontiguousarray(np.asarray(inp["conv_w_out"][0], f))
        m["conv_w"] = np.ascontiguousarray(np.asarray(inp["conv_w"][0], f).reshape(3, NKC, 128).transpose(2, 1, 0))
    if 'rwkv' in kinds:
        m["rwkv_w_in"] = np.ascontiguousarray(np.asarray(inp["rwkv_w_in"][0], f))
        m["rwkv_w_out"] = np.ascontiguousarray(np.asarray(inp["rwkv_w_out"][0], f))
        mu = np.asarray(inp["rwkv_mu"][0], f)
        m["rwkv_mu"] = np.ascontiguousarray(mu)
        m["rwkv_mu_fm"] = np.ascontiguousarray(mu.reshape(33, 128).T)
        vecs = np.stack([np.asarray(inp[k][0], f).reshape(NKC, 128) for k in
                         ("rwkv_w0", "rwkv_a0", "rwkv_k_k", "rwkv_k_a", "rwkv_r_k")], axis=0)
        m["rwkv_vecs"] = np.ascontiguousarray(vecs.transpose(2, 0, 1))
        m["rwkv_lw2"] = np.ascontiguousarray(np.concatenate([np.asarray(inp["rwkv_w_w2"][0], f), np.asarray(inp["rwkv_w_a2"][0], f)], axis=0))
        m["rwkv_gn_g"] = np.ascontiguousarray(np.asarray(inp["rwkv_gn_g"][0], f))
        m["rwkv_gn_b"] = np.ascontiguousarray(np.asarray(inp["rwkv_gn_b"][0], f))
    if 'hgrn' in kinds:
        m["hgrn_w_in"] = np.ascontiguousarray(np.asarray(inp["hgrn_w_in"][0], f))
        m["hgrn_w_out"] = np.ascontiguousarray(np.asarray(inp["hgrn_w_out"][0], f))
        m["hgrn_gn_g"] = np.ascontiguousarray(np.asarray(inp["hgrn_gn_g"][0], f))
        m["hgrn_lbl"] = np.ascontiguousarray(np.asarray(inp["hgrn_lb_logits"], f).reshape(4, NKC, 128).transpose(2, 0, 1))
    if 'gmlp' in kinds:
        m["gmlp_w_in"] = np.ascontiguousarray(np.asarray(inp["gmlp_w_in"][0], f))
        m["gmlp_w_out"] = np.ascontiguousarray(np.asarray(inp["gmlp_w_out"][0], f))
        m["gmlp_wsT"] = np.ascontiguousarray(np.asarray(inp["gmlp_w_s"][0], f).transpose(2, 0, 1))
        m["gmlp_bs"] = np.ascontiguousarray(np.asarray(inp["gmlp_b_s"][0], f))
        m["gmlp_vg"] = np.ascontiguousarray(np.asarray(inp["gmlp_v_g"][0], f))
    return m


FULL_LAYERS = [(0, 'rwkv'), (1, 'hgrn'), (2, 'conv'), (3, 'gmlp')]


def kernel(**inputs):
    x = np.asarray(inputs["x"])
    B, T, _ = x.shape
    layers = FULL_LAYERS
    bld = Builder(T, layers)
    nc = bld.build()
    in_maps = []
    for c in range(8):
        in_maps.append(prep_inputs(inputs, c // 2, layers))
    res = run_bass_kernel_spmd(nc, in_maps, core_ids=list(range(8)))
    out = np.stack([np.asarray(res.results[2 * b]["yT"]).T for b in range(B)], axis=0)
    return out.astype(np.float32)
```

```python
import numpy as np
from contextlib import ExitStack
import concourse.bass as bass
import concourse.mybir as mybir
from concourse.bass_utils import run_bass_kernel_spmd

F32 = mybir.dt.float32
BF16 = mybir.dt.bfloat16
ALU = mybir.AluOpType
AF = mybir.ActivationFunctionType
AX = mybir.AxisListType

D = 1024
NKC = 8
RMS_EPS = 1e-6
GN_EPS = 64e-5


class Prog:
    LIMIT = 30000

    def __init__(self, nc, es, n_dma_sems=24):
        self.nc = nc
        self.es = es
        self.engs = {'pe': nc.tensor, 'act': nc.scalar, 'dve': nc.vector,
                     'pool': nc.gpsimd, 'sp': nc.sync}
        self.sems = {}
        self.epoch = {k: 0 for k in self.engs}
        self.cnt = {k: 0 for k in self.engs}
        for k in self.engs:
            self.sems[(k, 0)] = es.enter_context(nc.semaphore(f"s_{k}_0"))
        self.dma_sems = []
        for i in range(n_dma_sems):
            key = ('dma', i)
            self.sems[key] = es.enter_context(nc.semaphore(f"s_dma_{i}"))
            self.cnt[key] = 0
            self.dma_sems.append(key)
        self.dma_rr = 0
        self.waited = {k: {} for k in self.engs}
        self.bufs = {}
        self.n_wait = 0
        self.n_ins = 0

    def _deps(self, reads, writes):
        deps = set()
        for k in reads:
            b = self.bufs.get(k)
            if b and b['w']:
                deps.add(b['w'])
        for k in writes:
            b = self.bufs.get(k)
            if b:
                if b['w']:
                    deps.add(b['w'])
                deps.update(b['r'])
        return deps

    def _wait(self, eng, deps):
        e = self.engs[eng]
        best = {}
        for (sk, v) in deps:
            if sk[0] == eng and eng == 'pe':
                continue
            if best.get(sk, 0) < v:
                best[sk] = v
        for sk, v in best.items():
            if self.waited[eng].get(sk, 0) >= v:
                continue
            e.wait_ge(self.sems[sk], v)
            self.waited[eng][sk] = v
            self.n_wait += 1

    def _record(self, tok, reads, writes):
        for k in reads:
            b = self.bufs.setdefault(k, {'w': None, 'r': []})
            b['r'].append(tok)
            if len(b['r']) > 64:
                best = {}
                for (sk, v) in b['r']:
                    if best.get(sk, 0) < v:
                        best[sk] = v
                b['r'] = list(best.items())
        for k in writes:
            b = self.bufs.setdefault(k, {'w': None, 'r': []})
            b['w'] = tok
            b['r'] = []

    @staticmethod
    def _excl(reads, writes):
        ps = [k for k in reads if isinstance(k, str) and k.startswith('ps')]
        if ps:
            reads = [k for k in reads if k not in ps]
            writes = list(writes) + ps
        return reads, writes

    disabled = False
    recording = None
    SYNC_LAT = 0.45

    def begin_record(self):
        self.recording = []

    def flush(self):
        rec = self.recording
        self.recording = None
        if not rec:
            return
        n = len(rec)
        preds = [None] * n
        succs = [[] for _ in range(n)]
        last_w = {}
        readers = {}
        for i, (kind, eng, fn, reads, writes, cost, lat) in enumerate(rec):
            ps = set()
            for k in reads:
                w = last_w.get(k)
                if w is not None:
                    ps.add(w)
            for k in writes:
                w = last_w.get(k)
                if w is not None:
                    ps.add(w)
                ps.update(readers.get(k, ()))
            ps.discard(i)
            preds[i] = ps
            for pi in ps:
                succs[pi].append(i)
            for k in reads:
                readers.setdefault(k, []).append(i)
            for k in writes:
                last_w[k] = i
                readers[k] = []
        npred = [len(p_) for p_ in preds]
        ready = [i for i in range(n) if npred[i] == 0]
        eng_free = {}
        end_t = [0.0] * n
        done_t = [0.0] * n
        order = []
        import heapq
        def est(i):
            kind, eng, fn, reads, writes, cost, lat = rec[i]
            t = eng_free.get(eng, 0.0)
            for pi in preds[i]:
                tp = done_t[pi] + (self.SYNC_LAT if rec[pi][1] != eng else 0.0)
                if tp > t:
                    t = tp
            return t
        heap = [(est(i), i) for i in ready]
        heapq.heapify(heap)
        while heap:
            t0, i = heapq.heappop(heap)
            t1 = est(i)
            if t1 > t0 + 1e-9 and heap and (t1, i) > heap[0]:
                heapq.heappush(heap, (t1, i))
                continue
            kind, eng, fn, reads, writes, cost, lat = rec[i]
            end_t[i] = t1 + cost
            done_t[i] = t1 + cost + lat
            eng_free[eng] = end_t[i]
            order.append(i)
            for si in succs[i]:
                npred[si] -= 1
                if npred[si] == 0:
                    heapq.heappush(heap, (est(si), si))
        assert len(order) == n, (len(order), n)
        self.sched_span = max(done_t) if done_t else 0.0
        for i in order:
            kind, eng, fn, reads, writes, cost, lat = rec[i]
            if kind == 'op':
                self.op(eng, fn, reads, writes)
            else:
                out, in_, kw = fn
                self.dma(eng, out, in_, reads, writes, **kw)

    def op(self, eng, fn, reads=(), writes=(), cost=None):
        if self.disabled:
            return None
        if self.recording is not None:
            reads, writes = self._excl(reads, writes)
            if cost is None:
                cost = {'pe': 0.2, 'act': 0.45, 'dve': 0.45, 'pool': 0.7, 'sp': 0.1}[eng]
            self.recording.append(('op', eng, fn, list(reads), list(writes), cost, 0.0))
            return None
        reads, writes = self._excl(reads, writes)
        deps = self._deps(reads, writes)
        self._wait(eng, deps)
        ins = fn(self.engs[eng])
        if self.cnt[eng] >= self.LIMIT:
            self.epoch[eng] += 1
            ep = self.epoch[eng]
            self.sems[(eng, ep)] = self.es.enter_context(self.nc.semaphore(f"s_{eng}_{ep}"))
            self.cnt[eng] = 0
        sk = (eng, self.epoch[eng])
        self.cnt[eng] += 1
        ins.then_inc(self.sems[sk], 1)
        self._record((sk, self.cnt[eng]), reads, writes)
        self.n_ins += 1
        return ins

    def dma(self, eng, out, in_, reads=(), writes=(), **kw):
        if self.disabled:
            return None
        if self.recording is not None:
            self.recording.append(('dma', eng, (out, in_, kw), list(reads), list(writes), 0.15, 6.0))
            return None
        deps = self._deps(reads, writes)
        sk = self.dma_sems[self.dma_rr]
        self.dma_rr = (self.dma_rr + 1) % len(self.dma_sems)
        if self.cnt[sk] > 0:
            deps.add((sk, self.cnt[sk]))
        self._wait(eng, deps)
        ins = self.engs[eng].dma_start(out=out, in_=in_, **kw)
        self.cnt[sk] += 16
        ins.then_inc(self.sems[sk], 16)
        self._record((sk, self.cnt[sk]), reads, writes)
        self.n_ins += 1
        return ins

    def all_tokens(self):
        deps = set()
        for k, b in self.bufs.items():
            if b['w']:
                deps.add(b['w'])
            deps.update(b['r'])
        return deps

    def barrier(self):
        deps = self.all_tokens()
        for eng in self.engs:
            d = set(x for x in deps)
            self._wait(eng, d)

    def finish(self, eng='sp'):
        self._wait(eng, self.all_tokens())


class Builder:
    def __init__(self, T, layers, do_final=True, neu_dt=None):
        self.neu_dt = neu_dt if neu_dt is not None else BF16
        self.use_sched = True
        self.T = T
        self.layers = layers
        self.do_final = do_final
        self.nc = bass.Bass("TRN2", target_bir_lowering=False)
        self.inputs = {}

    def din(self, name, shape):
        t = self.nc.dram_tensor(name, list(shape), F32, kind="ExternalInput").ap()
        self.inputs[name] = t
        return t

    def sb(self, name, shape, dt=F32):
        return self.es.enter_context(self.nc.sbuf_tensor(name, list(shape), dt))

    def lsb(self, name, shape, dt=F32):
        return self.les.enter_context(self.nc.sbuf_tensor(f"{name}_{self.lname}", list(shape), dt))

    def next_ps(self):
        pool = self.ps_pool
        i = pool[self.ps_rr % len(pool)]
        self.ps_rr += 1
        return self.psums[i], f"ps{i}"

    @staticmethod
    def ecost(eng, ap):
        try:
            n = ap.free_size()
        except Exception:
            n = 256
        if eng == 'act':
            return 0.22 + n * 0.00075
        if eng == 'dve':
            return 0.2 + n * 0.00095
        if eng == 'pool':
            return 0.2 + n * 0.0021
        return 0.2

    def tt(self, eng, out, in0, in1, op, reads, writes):
        return self.p.op(eng, lambda e: e.tensor_tensor(out=out, in0=in0, in1=in1, op=op), reads, writes, cost=self.ecost(eng, out))

    def ts(self, eng, out, in0, s1, s2, op0, op1, reads, writes):
        if s2 is None:
            return self.p.op(eng, lambda e: e.tensor_scalar(out=out, in0=in0, scalar1=s1, scalar2=None, op0=op0), reads, writes, cost=self.ecost(eng, out))
        return self.p.op(eng, lambda e: e.tensor_scalar(out=out, in0=in0, scalar1=s1, scalar2=s2, op0=op0, op1=op1), reads, writes, cost=self.ecost(eng, out))

    def stt(self, eng, out, in0, scalar, in1, op0, op1, reads, writes):
        eng = 'dve'
        return self.p.op(eng, lambda e: e.scalar_tensor_tensor(out=out, in0=in0, scalar=scalar, in1=in1, op0=op0, op1=op1), reads, writes, cost=self.ecost(eng, out))

    def act(self, out, in_, func, reads, writes, bias=None, scale=1.0, accum_out=None):
        kw = {}
        if bias is not None:
            kw['bias'] = bias
        if accum_out is not None:
            kw['accum_out'] = accum_out
        return self.p.op('act', lambda e: e.activation(out=out, in_=in_, func=func, scale=scale, **kw), reads, writes, cost=self.ecost('act', in_))

    def mm(self, out, lhsT, rhs, start, stop, reads, writes):
        try:
            n = rhs.free_size()
        except Exception:
            n = 128
        c = 0.06 + n / 2400.0 * (1.0 if lhsT.dtype == BF16 else 2.4)
        return self.p.op('pe', lambda e: e.matmul(out, lhsT=lhsT, rhs=rhs, start=start, stop=stop), reads, writes, cost=c)

    def copy(self, eng, out, in_, reads, writes):
        if eng == 'act':
            return self.p.op('act', lambda e: e.copy(out=out, in_=in_), reads, writes, cost=self.ecost('act', out))
        return self.p.op(eng, lambda e: e.tensor_copy(out=out, in_=in_), reads, writes, cost=self.ecost(eng, out))

    def load_weight_bf16(self, dst, dst_key, src, ncols, src_c0=0, dst_c0=0, scale_bc=None):
        p = self.p
        CH = 1024 if ncols % 1024 == 0 else ncols
        for kc in range(NKC):
            for c0 in range(0, ncols, CH):
                i = self.stage_i
                self.stage_i += 1
                st = self.stage[i % 2]
                sk = f"stage{i % 2}"
                p.dma('sp', st[:, 0:CH], src[kc * 128:(kc + 1) * 128, src_c0 + c0:src_c0 + c0 + CH], reads=[], writes=[sk])
                eng = ['dve', 'pool'][i % 2]
                if scale_bc is None:
                    self.copy(eng, dst[:, kc, dst_c0 + c0:dst_c0 + c0 + CH], st[:, 0:CH], [sk], [dst_key])
                else:
                    self.tt(eng, dst[:, kc, dst_c0 + c0:dst_c0 + c0 + CH], st[:, 0:CH], scale_bc[:, c0:c0 + CH], ALU.mult,
                            [sk, 'bc_tiles'], [dst_key])

    def rms_rstd(self, src, src_key, TT, tag):
        sqb, rstd = self.sqb, self.rstd
        for kc in range(NKC):
            if kc % 2 == 0:
                self.act(sqb[:, kc, :TT], src[:, kc, :TT], AF.Square, [src_key], ['sqb'])
            else:
                self.tt('pool', sqb[:, kc, :TT], src[:, kc, :TT], src[:, kc, :TT], ALU.mult, [src_key], ['sqb'])
        ps, pk = self.next_ps()
        for kc in range(NKC):
            self.mm(ps[:, :TT], self.ones_bf[:], sqb[:, kc, :TT], kc == 0, kc == NKC - 1, ['sqb', 'ones_bf'], [pk])
        self.act(rstd[:, :TT], ps[:, :TT], AF.Ln, [pk, 'consts'], ['rstd'], bias=self.epsc[:, 0:1], scale=1.0 / D)
        self.act(rstd[:, :TT], rstd[:, :TT], AF.Exp, ['rstd'], ['rstd'], scale=-0.5)
        return rstd, 'rstd'

    def run_layer(self, li, kind, TT, w_in_cols, mixer_setup, mixer_tile, is_last):
        p = self.p
        T = self.T
        ntiles = T // TT
        with ExitStack() as les:
            self.les = les
            self.lname = f"L{li}"
            self.TT = TT
            self.W_in = self.lsb("W_in", [128, NKC, w_in_cols], BF16)
            self.W_out = self.lsb("W_out", [128, NKC, D], BF16)
            self.hT = [self.lsb(f"hT{i}", [128, NKC, TT], F32) for i in range(2)]
            self.sqb = self.lsb("sqb", [128, NKC, TT], BF16)
            self.rstd = self.lsb("rstd", [128, TT], F32)
            self.hn = self.lsb("hn", [128, NKC, TT + 1], BF16)
            self.yTt = self.lsb("yTt", [128, NKC, TT], BF16)
            self.stage_i = 0
            loader = mixer_setup()
            with ExitStack() as ses:
                self.stage = [ses.enter_context(self.nc.sbuf_tensor(f"stage{i}_{self.lname}", [128, 1024], F32)) for i in range(2)]
                if loader is None:
                    self.load_weight_bf16(self.W_in, 'W_in', self.w_in_dram[kind], w_in_cols)
                    self.load_weight_bf16(self.W_out, 'W_out', self.w_out_dram[kind], D)
                else:
                    loader(ses)
                p.barrier()
            p.op('pool', lambda e: e.memset(self.hn[:, :, 0:1], 0.0), [], ['hn'])

            def load(ti):
                buf = self.hT[ti % 2]
                src = self.xT if self.first_layer else self.yT
                p.dma('sp', buf[:], src.rearrange("(c p) t -> p c t", p=128)[:, :, ti * TT:(ti + 1) * TT],
                      reads=[('hd', ti * TT // 128 + i) for i in range(TT // 128)], writes=[f"hT{ti % 2}"])

            if self.use_sched:
                p.begin_record()
            load(0)
            for ti in range(ntiles):
                if ti + 1 < ntiles:
                    load(ti + 1)
                h = self.hT[ti % 2]
                hk = f"hT{ti % 2}"
                rstd, rk = self.rms_rstd(h, hk, TT, 'in')
                g = self.norm_g
                if ti > 0:
                    self.copy('pool', self.hn[:, :, 0:1], self.hn[:, :, TT:TT + 1], ['hn'], ['hn'])
                for kc in range(NKC):
                    self.stt('dve', self.hn[:, kc, 1:TT + 1], h[:, kc, :], g[:, li, kc:kc + 1], rstd[:, :TT],
                             ALU.mult, ALU.mult, [hk, rk, 'consts'], ['hn'])
                mixer_tile(ti)
                for j in range(NKC):
                    ps, pk = self.next_ps()
                    for kc in range(NKC):
                        self.mm(ps[:, :TT], self.W_out[:, kc, j * 128:(j + 1) * 128], self.yTt[:, kc, :TT],
                                kc == 0, kc == NKC - 1, ['W_out', 'yT'], [pk])
                    self.tt('dve', h[:, j, :], h[:, j, :], ps[:, :TT], ALU.add, [hk, pk], [hk])
                if is_last and self.do_final:
                    rstd, rk = self.rms_rstd(h, hk, TT, 'fin')
                    for kc in range(NKC):
                        self.stt('dve' if kc % 2 == 0 else 'pool', h[:, kc, :], h[:, kc, :], self.final_g[:, kc:kc + 1], rstd[:, :TT],
                                 ALU.mult, ALU.mult, [hk, rk, 'consts'], [hk])
                p.dma('sp', self.yT.rearrange("(c p) t -> p c t", p=128)[:, :, ti * TT:(ti + 1) * TT], h[:],
                      reads=[hk], writes=[('hd', (ti * TT) // 128 + i) for i in range(max(1, TT // 128))])
            if self.use_sched:
                p.flush()
            self.first_layer = False
            p.barrier()
        self.les = None

    def conv_layer(self, li, is_last):
        TT = 512

        def setup():
            self.yext = self.lsb("yext", [128, NKC, TT + 2], F32)
            self.zs = [self.lsb(f"zs{i}", [128, TT], F32) for i in range(2)]
            self.acc = [self.lsb(f"acc{i}", [128, TT], F32) for i in range(2)]
            self.sg = [self.lsb(f"sg{i}", [128, TT], F32) for i in range(2)]
            self.p.op('pool', lambda e: e.memset(self.yext[:, :, 0:2], 0.0), [], [('yext', j) for j in range(NKC)])

        def tile(ti):
            W = self.W_in
            cw = self.conv_w
            for j in range(NKC):
                zs, acc, sg = self.zs[j % 2], self.acc[j % 2], self.sg[j % 2]
                zk, ak, gk = f"zs{j % 2}", f"acc{j % 2}", f"sg{j % 2}"
                pss = []
                for blk in range(4):
                    ps, pk = self.next_ps()
                    col0 = blk * D + j * 128
                    for kc in range(NKC):
                        self.mm(ps[:, :TT], W[:, kc, col0:col0 + 128], self.hn[:, kc, 1:TT + 1], kc == 0, kc == NKC - 1,
                                ['W_in', 'hn'], [pk])
                    pss.append((ps, pk))
                (pb, pbk), (pc, pck), (pz, pzk), (pg, pgk) = pss
                yk = ('yext', j)
                self.copy('act', zs[:], pz[:, :TT], [pzk], [zk])
                if ti > 0:
                    self.copy('pool', self.yext[:, j, 0:2], self.yext[:, j, TT:TT + 2], [yk], [yk])
                self.tt('dve', self.yext[:, j, 2:TT + 2], pc[:, :TT], zs[:], ALU.mult, [pck, zk], [yk])
                self.ts('pool', acc[:], self.yext[:, j, 2:TT + 2], cw[:, j, 2:3], None, ALU.mult, None, [yk, 'consts'], [ak])
                self.stt('pool', acc[:], self.yext[:, j, 1:TT + 1], cw[:, j, 1:2], acc[:], ALU.mult, ALU.add, [yk, ak, 'consts'], [ak])
                self.stt('pool', acc[:], self.yext[:, j, 0:TT], cw[:, j, 0:1], acc[:], ALU.mult, ALU.add, [yk, ak, 'consts'], [ak])
                self.act(sg[:], pg[:, :TT], AF.Silu, [pgk], [gk])
                self.tt('dve', acc[:], pb[:, :TT], acc[:], ALU.mult, [pbk, ak], [ak])
                self.tt('pool', self.yTt[:, j, :], acc[:], sg[:], ALU.mult, [ak, gk], ['yT'])

        self.run_layer(li, 'conv', TT, 4 * D, setup, tile, is_last)

    def gmlp_layer(self, li, is_last):
        TT = 512

        def setup():
            p = self.p
            self.wsT = self.lsb("wsT", [128, 8, 128], F32)
            self.bs_bc = self.lsb("bs_bc", [128, 8, TT], F32)
            self.vg_bc = self.lsb("vg_bc", [128, D], F32)
            self.vn = [self.lsb(f"vn{i}", [128, D], F32) for i in range(TT // 128)]
            self.vss = self.lsb("vss", [128, 4], F32)
            self.junk = self.lsb("junk", [128, 512], F32)
            self.s_sb = [self.lsb(f"s_sb{i}", [128, TT], F32) for i in range(2)]
            self.sg = [self.lsb(f"sg{i}", [128, TT], F32) for i in range(2)]
            p.dma('sp', self.wsT[:], self.inputs['gmlp_wsT'], [], ['wsT'])
            for g in range(8):
                p.op('pool', lambda e: e.affine_select(out=self.wsT[:, g, :], in_=self.wsT[:, g, :], pattern=[[1, 128]],
                                                       compare_op=ALU.is_ge, fill=0.0, base=0, channel_multiplier=-1),
                     ['wsT'], ['wsT'])
            for r in range(TT // 128):
                p.dma('sp', self.bs_bc[:, :, r * 128:(r + 1) * 128],
                      self.inputs['gmlp_bs'].partition_broadcast(128), [], ['bs_bc'])
            p.dma('sp', self.vg_bc[:], self.inputs['gmlp_vg'].partition_broadcast(128), [], ['vg_bc'])

        def tile(ti):
            W = self.W_in
            nblk = TT // 128
            for blk in range(nblk):
                vn = self.vn[blk]
                vk = f"vn{blk}"
                halves = []
                for hf in range(2):
                    ps, pk = self.next_ps()
                    for kc in range(NKC):
                        self.mm(ps[:, :512], self.hn[:, kc, 1 + blk * 128:1 + (blk + 1) * 128],
                                W[:, kc, D + hf * 512:D + (hf + 1) * 512], kc == 0, kc == NKC - 1, ['W_in', 'hn'], [pk])
                    halves.append((ps, pk))
                for hf, (ps, pk) in enumerate(halves):
                    self.act(self.junk[:], ps[:, :512], AF.Square, [pk], ['junk', 'vss'], accum_out=self.vss[:, hf:hf + 1])
                self.tt('dve', self.vss[:, 2:3], self.vss[:, 0:1], self.vss[:, 1:2], ALU.add, ['vss'], ['vss'])
                self.act(self.vss[:, 3:4], self.vss[:, 2:3], AF.Ln, ['vss', 'consts'], ['vss'], bias=self.epsc[:, 0:1], scale=1.0 / D)
                self.act(self.vss[:, 3:4], self.vss[:, 3:4], AF.Exp, ['vss'], ['vss'], scale=-0.5)
                for hf, (ps, pk) in enumerate(halves):
                    self.stt('dve', vn[:, hf * 512:(hf + 1) * 512], ps[:, :512], self.vss[:, 3:4],
                             self.vg_bc[:, hf * 512:(hf + 1) * 512], ALU.mult, ALU.mult, [pk, 'vss', 'vg_bc'], [vk])
            for j in range(NKC):
                s_sb, sg = self.s_sb[j % 2], self.sg[j % 2]
                sk, gk = f"s_sb{j % 2}", f"sg{j % 2}"
                ps, pk = self.next_ps()
                for blk in range(nblk):
                    self.mm(ps[:, blk * 128:(blk + 1) * 128], self.vn[blk][:, j * 128:(j + 1) * 128], self.wsT[:, j, :], True, True,
                            [f"vn{blk}", 'wsT'], [pk])
                self.tt('dve', s_sb[:], ps[:, :TT], self.bs_bc[:, j, :], ALU.add, [pk, 'bs_bc'], [sk])
                pu, puk = self.next_ps()
                for kc in range(NKC):
                    self.mm(pu[:, :TT], W[:, kc, j * 128:(j + 1) * 128], self.hn[:, kc, 1:TT + 1], kc == 0, kc == NKC - 1,
                            ['W_in', 'hn'], [puk])
                pg, pgk = self.next_ps()
                for kc in range(NKC):
                    self.mm(pg[:, :TT], W[:, kc, 2 * D + j * 128:2 * D + (j + 1) * 128], self.hn[:, kc, 1:TT + 1], kc == 0,
                            kc == NKC - 1, ['W_in', 'hn'], [pgk])
                self.act(sg[:], pg[:, :TT], AF.Silu, [pgk], [gk])
                self.tt('dve', s_sb[:], pu[:, :TT], s_sb[:], ALU.mult, [puk, sk], [sk])
                self.tt('pool', self.yTt[:, j, :], s_sb[:], sg[:], ALU.mult, [sk, gk], ['yT'])

        self.run_layer(li, 'gmlp', TT, 3 * D, setup, tile, is_last)


    def make_ident(self, ident, key):
        p = self.p
        p.op('pool', lambda e: e.memset(ident[:], 1.0), [], [key])
        p.op('pool', lambda e: e.affine_select(out=ident[:], in_=ident[:], pattern=[[-1, 128]], compare_op=ALU.is_equal,
                                               fill=0.0, base=0, channel_multiplier=1), [key], [key])

    def make_block_masks(self, C, maskT, colmask, rowmask, strict=False):
        p = self.p
        nch = 128 // C
        if maskT is not None:
            p.op('pool', lambda e: e.memset(maskT[:], 1.0), [], ['masks'])
            p.op('pool', lambda e: e.affine_select(out=maskT[:], in_=maskT[:], pattern=[[1, 128]], compare_op=ALU.is_ge if not strict else ALU.is_gt,
                                                   fill=0.0, base=0, channel_multiplier=-1), ['masks'], ['masks'])
            for c in range(1, nch):
                p.op('pool', lambda e, c=c: e.affine_select(out=maskT[:, c * C:(c + 1) * C], in_=maskT[:, c * C:(c + 1) * C], pattern=[[0, C]],
                                                            compare_op=ALU.is_ge, fill=0.0, base=-c * C, channel_multiplier=1), ['masks'], ['masks'])
        if colmask is not None:
            p.op('pool', lambda e: e.memset(colmask[:], 0.0), [], ['masks'])
            for c in range(nch):
                p.op('pool', lambda e, c=c: e.memset(colmask[:, c, c * C:(c + 1) * C], 1.0), ['masks'], ['masks'])
        if rowmask is not None:
            p.op('pool', lambda e: e.memset(rowmask[:], 1.0), [], ['masks'])
            for c in range(nch):
                p.op('pool', lambda e, c=c: e.affine_select(out=rowmask[:, c:c + 1], in_=rowmask[:, c:c + 1], pattern=[[0, 1]],
                                                            compare_op=ALU.is_ge, fill=0.0, base=-c * C, channel_multiplier=1), ['masks'], ['masks'])
                p.op('pool', lambda e, c=c: e.affine_select(out=rowmask[:, c:c + 1], in_=rowmask[:, c:c + 1], pattern=[[0, 1]],
                                                            compare_op=ALU.is_ge, fill=0.0, base=c * C + C - 1, channel_multiplier=-1), ['masks'], ['masks'])

    def hgrn_layer(self, li, is_last):
        TT = 256
        C = 32
        NB = TT // 128
        NCH = TT // C

        def setup():
            p = self.p
            L = self.lsb
            self.ident = L("ident", [128, 128], F32)
            self.make_ident(self.ident, 'ident')
            self.maskT = L("maskT", [128, 128], F32)
            self.colmask = L("colmask", [128, 4, 128], F32)
            self.rowmask = L("rowmask", [128, 4], F32)
            self.make_block_masks(C, self.maskT, self.colmask, self.rowmask)
            self.resetm = L("resetm", [128, TT], F32)
            p.op('pool', lambda e: e.memset(self.resetm[:], 1.0), [], ['masks'])
            p.op('pool', lambda e: e.memset(self.resetm[:].rearrange("p (n c) -> p n c", c=C)[:, :, 0:1], 0.0), ['masks'], ['masks'])
            self.gn_bc = L("gn_bc", [128, D], F32)
            p.dma('sp', self.gn_bc[:], self.inputs['hgrn_gn_g'].partition_broadcast(128), [], ['gn_bc'])
            self.lbl = L("lbl", [128, 4, NKC], F32)
            self.lbt = L("lbt", [128, 4, NKC], F32)
            p.dma('sp', self.lbl[:], self.inputs['hgrn_lbl'], [], ['lbl'])
            self.act(self.lbl[:], self.lbl[:], AF.Exp, ['lbl'], ['lbl'])
            self.tt('dve', self.lbt[:, 0, :], self.lbl[:, 0, :], self.lbl[:, 1, :], ALU.add, ['lbl'], ['lbt'])
            self.tt('dve', self.lbt[:, 0, :], self.lbt[:, 0, :], self.lbl[:, 2, :], ALU.add, ['lbl', 'lbt'], ['lbt'])
            self.tt('dve', self.lbt[:, 0, :], self.lbt[:, 0, :], self.lbl[:, 3, :], ALU.add, ['lbl', 'lbt'], ['lbt'])
            p.op('dve', lambda e: e.reciprocal(out=self.lbt[:, 3, :], in_=self.lbt[:, 0, :]), ['lbt'], ['lbt'])
            p.op('dve', lambda e: e.memset(self.lbt[:, 1, :], 0.0), ['lbt'], ['lbt'])
            for i in range(1, li + 1):
                self.tt('dve', self.lbt[:, 1, :], self.lbt[:, 1, :], self.lbl[:, i, :], ALU.add, ['lbl', 'lbt'], ['lbt'])
            self.tt('dve', self.lbt[:, 1, :], self.lbt[:, 1, :], self.lbt[:, 3, :], ALU.mult, ['lbt'], ['lbt'])
            self.ts('dve', self.lbt[:, 2, :], self.lbt[:, 1, :], -1.0, 1.0, ALU.mult, ALU.add, ['lbt'], ['lbt'])
            self.S = L("S_hgrn", [128, NKC, 128], F32)
            p.op('pool', lambda e: e.memset(self.S[:], 0.0), [], [('S', j) for j in range(NKC)])
            names = ['f', 'kk', 'bb', 'qe', 'dd']
            self.tmps = []
            for q in range(2):
                tm = {n: L(f"h_{n}{q}", [128, TT], F32) for n in names}
                tm['e1'] = tm['f']
                tm['ko'] = tm['dd']
                tm['ke_bf'] = L(f"h_ke_bf{q}", [128, TT], BF16)
                tm['qe_bf'] = L(f"h_qe_bf{q}", [128, TT], BF16)
                tm['kom'] = L(f"kom{q}", [128, 4, NB, 128], BF16)
                self.tmps.append(tm)
            self.sgate = [L(f"sgate{q}", [128, TT], BF16) for q in range(3)]
            self.qem = [L(f"qem{q}", [128, 4, TT], BF16) for q in range(3)]
            self.v_bf = [L(f"v_bf{q}", [128, NB, 128], BF16) for q in range(3)]
            self.attm = [L(f"attm{q}", [128, NB, 128], BF16) for q in range(3)]
            self.u_sb = [L(f"u_sb{q}", [128, NCH, 128], F32) for q in range(3)]
            self.dec = [L(f"dec{q}", [128, NCH], F32) for q in range(3)]
            self.S_all = L("S_all", [128, 5, 128], F32)
            self.S_bf = L("S_bf", [128, NCH, 128], BF16)
            self.on = L("on", [128, NB, 128], F32)
            self.oss = L("oss", [128, 2 * NB], F32)
            self.junk = L("junk", [128, 128], F32)
            self.ps_pool = [0, 1, 2, 3]
            self.nm_rr = 0

        def nm_ps():
            i = [4, 5][self.nm_rr % 2]
            self.nm_rr += 1
            return self.psums[i], f"ps{i}"

        def proj(col0):
            ps, pk = self.next_ps()
            for kc in range(NKC):
                self.mm(ps[:, :TT], self.W_in[:, kc, col0:col0 + 128], self.hn[:, kc, 1:TT + 1], kc == 0, kc == NKC - 1, ['W_in', 'hn'], [pk])
            return ps, pk

        def A_gen(j):
            q2 = j % 2
            t = self.tmps[q2]
            kom = t['kom']
            W = self.W_in
            q = j % 3
            lb, oml = self.lbt[:, 1, :], self.lbt[:, 2, :]
            sgate, qem, v_bf, attm, u_sb, dec = self.sgate[q], self.qem[q], self.v_bf[q], self.attm[q], self.u_sb[q], self.dec[q]
            ksg, kqem, kv, katt, ku, kdec = f'sgate{q}', f'qem{q}', f'v_bf{q}', f'attm{q}', f'u_sb{q}', f'dec{q}'
            pf, pfk = proj(D + j * 128)
            self.act(t['f'][:], pf[:, :TT], AF.Sigmoid, [pfk], [f't_f{q2}'])
            self.ts('dve', t['f'][:], t['f'][:], oml[:, j:j + 1], lb[:, j:j + 1], ALU.mult, ALU.add, [f't_f{q2}', 'lbt'], [f't_f{q2}'])
            self.act(t['kk'][:], t['f'][:], AF.Identity, [f't_f{q2}'], [f't_kk{q2}'], scale=-1.0, bias=self.epsc[:, 2:3])
            self.act(t['dd'][:], t['f'][:], AF.Ln, [f't_f{q2}'], [f't_dd{q2}'])
            self.p.op('dve', lambda e: e.tensor_tensor_scan(out=t['bb'][:], data0=self.resetm[:], data1=t['dd'][:], initial=0.0,
                                                            op0=ALU.mult, op1=ALU.add), [f't_dd{q2}', 'masks'], [f't_bb{q2}'])
            yield
            pq, pqk = proj(j * 128)
            self.act(t['e1'][:], t['bb'][:], AF.Exp, [f't_bb{q2}'], [f't_f{q2}'])
            self.tt('dve', t['qe'][:], pq[:, :TT], t['e1'][:], ALU.mult, [pqk, f't_f{q2}'], [f't_qe{q2}'])
            self.copy('act', t['qe_bf'][:], t['qe'][:], [f't_qe{q2}'], [f't_qe_bf{q2}'])
            qe4 = t['qe'][:].rearrange("p (b t) -> p b t", t=128)
            for c in range(4):
                self.tt('pool' if c % 2 else 'dve', qem[:, c, :].rearrange("p (b t) -> p b t", t=128), qe4,
                        self.colmask[:, c:c + 1, :].to_broadcast([128, NB, 128]), ALU.mult, [f't_qe{q2}', 'masks'], [kqem])
            yield
            self.act(t['e1'][:], t['bb'][:], AF.Exp, [f't_bb{q2}'], [f't_f{q2}'], scale=-1.0)
            self.tt('pool', t['ke_bf'][:], t['kk'][:], t['e1'][:], ALU.mult, [f't_kk{q2}', f't_f{q2}'], [f't_ke_bf{q2}'])
            b3 = t['bb'][:].rearrange("p (n c) -> p n c", c=C)
            self.act(dec[:], b3[:, :, C - 1], AF.Exp, [f't_bb{q2}'], [kdec])
            self.tt('pool', t['dd'][:].rearrange("p (n c) -> p n c", c=C), b3[:, :, C - 1:C].to_broadcast([128, NCH, C]), b3, ALU.subtract,
                    [f't_bb{q2}'], [f't_dd{q2}'])
            self.act(t['dd'][:], t['dd'][:], AF.Exp, [f't_dd{q2}'], [f't_dd{q2}'])
            self.tt('dve', t['ko'][:], t['kk'][:], t['dd'][:], ALU.mult, [f't_kk{q2}', f't_dd{q2}'], [f't_dd{q2}'])
            pg, pgk = proj(3 * D + j * 128)
            self.act(sgate[:], pg[:, :TT], AF.Silu, [pgk], [ksg])
            yield
            pv, pvk = self.next_ps()
            for blk in range(NB):
                for kc in range(NKC):
                    self.mm(pv[:, blk * 128:(blk + 1) * 128], self.hn[:, kc, 1 + blk * 128:1 + (blk + 1) * 128],
                            W[:, kc, 2 * D + j * 128:2 * D + (j + 1) * 128], kc == 0, kc == NKC - 1, ['W_in', 'hn'], [pvk])
            self.copy('act', v_bf[:].rearrange("p b v -> p (b v)"), pv[:, :TT], [pvk], [kv])
            yield
            ps, pk = nm_ps()
            for blk in range(NB):
                cs = slice(blk * 128, (blk + 1) * 128)
                self.mm(ps[:, cs], t['ke_bf'][:, cs], t['qe_bf'][:, cs], True, True, [f't_ke_bf{q2}', f't_qe_bf{q2}'], [pk])
            self.tt('dve', attm[:], ps[:, :TT].rearrange("p (b t) -> p b t", t=128), self.maskT[:, None, :].to_broadcast([128, NB, 128]),
                    ALU.mult, [pk, 'masks'], [katt])
            ps, pk = nm_ps()
            for blk in range(NB):
                cs = slice(blk * 128, (blk + 1) * 128)
                self.p.op('pe', lambda e, ps=ps, cs=cs: e.transpose(ps[:, cs], t['ko'][:, cs], self.ident[:]), [f't_dd{q2}', 'ident'], [pk])
            for c in range(4):
                self.act(kom[:, c, :, :].rearrange("p b k -> p (b k)"), ps[:, :TT], AF.Copy, [pk, 'masks'], [f'kom{q2}'],
                         scale=self.rowmask[:, c:c + 1])
            yield
            for blk in range(NB):
                ps, pk = nm_ps()
                for c in range(4):
                    self.mm(ps[:, c * 128:(c + 1) * 128], kom[:, c, blk, :], v_bf[:, blk, :], True, True, [f'kom{q2}', kv], [pk])
                self.copy('act' if blk % 2 else 'dve', u_sb[:, blk * 4:(blk + 1) * 4, :].rearrange("p c v -> p (c v)"), ps[:, 0:512], [pk], [ku])
                if blk % 2:
                    yield

        def B_gen(j):
            P = self.psums
            q = j % 3
            sgate, qem, v_bf, attm, u_sb, dec = self.sgate[q], self.qem[q], self.v_bf[q], self.attm[q], self.u_sb[q], self.dec[q]
            ksg, kqem, kv, katt, ku, kdec = f'sgate{q}', f'qem{q}', f'v_bf{q}', f'attm{q}', f'u_sb{q}', f'dec{q}'
            kS = ('S', j)
            SA = self.S_all
            self.copy('pool', SA[:, 0, :], self.S[:, j, :], [kS], ['S_all'])
            for blk in range(NB):
                for c in range(4):
                    n = blk * 4 + c
                    self.stt('dve', SA[:, c + 1, :], SA[:, c, :], dec[:, n:n + 1], u_sb[:, n, :], ALU.mult, ALU.add, ['S_all', kdec, ku], ['S_all'])
                self.copy('act', self.S_bf[:, blk * 4:(blk + 1) * 4, :].rearrange("p c v -> p (c v)"),
                          SA[:, 0:4, :].rearrange("p c v -> p (c v)"), ['S_all'], ['S_bf'])
                if blk < NB - 1:
                    self.copy('dve', SA[:, 0, :], SA[:, 4, :], ['S_all'], ['S_all'])
                yield
            self.copy('pool', self.S[:, j, :], SA[:, 4, :], ['S_all'], [kS])
            po, pok = P[6], 'ps6'
            for blk in range(NB):
                cs = slice(blk * 128, (blk + 1) * 128)
                self.mm(po[:, cs], attm[:, blk, :], v_bf[:, blk, :], True, False, [katt, kv], [pok])
                for c in range(4):
                    self.mm(po[:, cs], qem[:, c, cs], self.S_bf[:, blk * 4 + c, :], False, c == 3, [kqem, 'S_bf'], [pok])
                if blk % 2:
                    yield
            for blk in range(NB):
                cs = slice(blk * 128, (blk + 1) * 128)
                self.act(self.junk[:], po[:, cs], AF.Square, [pok], ['junk', 'oss'], accum_out=self.oss[:, blk:blk + 1])
            self.act(self.oss[:, NB:2 * NB], self.oss[:, 0:NB], AF.Ln, ['oss', 'consts'], ['oss'], bias=self.epsc[:, 0:1], scale=1.0 / 128)
            self.act(self.oss[:, NB:2 * NB], self.oss[:, NB:2 * NB], AF.Exp, ['oss'], ['oss'], scale=-0.5)
            self.tt('dve', self.on[:], po[:, :TT].rearrange("p (b v) -> p b v", v=128),
                    self.oss[:, NB:2 * NB, None].to_broadcast([128, NB, 128]), ALU.mult, [pok, 'oss'], ['on'])
            self.tt('pool', self.on[:], self.on[:], self.gn_bc[:, None, j * 128:(j + 1) * 128].to_broadcast([128, NB, 128]), ALU.mult,
                    ['on', 'gn_bc'], ['on'])
            yield
            py, pyk = P[7], 'ps7'
            for blk in range(NB):
                cs = slice(blk * 128, (blk + 1) * 128)
                self.p.op('pe', lambda e, cs=cs, blk=blk: e.transpose(py[:, cs], self.on[:, blk, :], self.ident[:]), ['on', 'ident'], [pyk])
            self.tt('dve', self.yTt[:, j, :], py[:, :TT], sgate[:], ALU.mult, [pyk, ksg], ['yT'])
            yield

        def drive(gens):
            gens = [g for g in gens if g is not None]
            while gens:
                for g in list(gens):
                    try:
                        next(g)
                    except StopIteration:
                        gens.remove(g)

        def step(g):
            try:
                next(g)
                return True
            except StopIteration:
                return False

        def tile(ti):
            A = {0: A_gen(0), 1: A_gen(1)}
            while step(A[0]):
                step(A[1])
            for sl in range(NKC):
                must = [B_gen(sl)]
                if sl + 1 < NKC:
                    must.append(A[sl + 1])
                opt = None
                if sl + 2 < NKC:
                    A[sl + 2] = A_gen(sl + 2)
                    opt = A[sl + 2]
                while must:
                    for g in list(must):
                        if not step(g):
                            must.remove(g)
                    if opt is not None and not step(opt):
                        opt = None

        self.run_layer(li, 'hgrn', TT, 4 * D, setup, tile, is_last)
        self.ps_pool = list(range(8))

    def rwkv_layer(self, li, is_last):
        TT = 256
        NB = TT // 128
        WC = 3200
        NDT = self.neu_dt
        LC = -0.6065306597126334

        def setup():
            p = self.p
            L = self.lsb
            self.ident = L("ident", [128, 128], F32)
            self.make_ident(self.ident, 'ident')
            self.ident_n = L("ident_n", [128, 128], NDT)
            self.copy('dve', self.ident_n[:], self.ident[:], ['ident'], ['ident'])
            self.maskS = L("maskS", [128, 128], F32)
            self.maskI = L("maskI", [128, 128], F32)
            self.maskSL = L("maskSL", [128, 128], F32)
            for (m, pat, cm, cmp_) in ((self.maskS, 1, -1, ALU.is_gt), (self.maskI, 1, -1, ALU.is_ge), (self.maskSL, -1, 1, ALU.is_gt)):
                p.op('pool', lambda e, m=m: e.memset(m[:], 1.0), [], ['masks'])
                p.op('pool', lambda e, m=m, pat=pat, cm=cm, cmp_=cmp_: e.affine_select(
                    out=m[:], in_=m[:], pattern=[[pat, 128]], compare_op=cmp_, fill=0.0, base=0, channel_multiplier=cm), ['masks'], ['masks'])
            self.blockones = L("blockones", [128, 128], F32)
            p.op('pool', lambda e: e.memset(self.blockones[:], 1.0), [], ['masks'])
            p.op('pool', lambda e: e.memset(self.blockones[0:64, 64:128], 0.0), ['masks'], ['masks'])
            p.op('pool', lambda e: e.memset(self.blockones[64:128, 0:64], 0.0), ['masks'], ['masks'])
            self.resetm = L("resetm", [128, TT], F32)
            p.op('pool', lambda e: e.memset(self.resetm[:], 1.0), [], ['masks'])
            p.op('pool', lambda e: e.memset(self.resetm[:].rearrange("p (n c) -> p n c", c=128)[:, :, 0:1], 0.0), ['masks'], ['masks'])
            p.op('pool', lambda e: e.memset(self.epsc[:, 1:2], GN_EPS), [], ['consts'])
            self.mu_fm = L("mu_fm", [128, 33], F32)
            self.omu_fm = L("omu_fm", [128, 33], F32)
            p.dma('sp', self.mu_fm[:], self.inputs['rwkv_mu_fm'], [], ['rw_vecs'])
            self.ts('dve', self.omu_fm[:], self.mu_fm[:], -1.0, 1.0, ALU.mult, ALU.add, ['rw_vecs'], ['rw_vecs'])
            self.vecs = L("rw_vecs", [128, 5, NKC], F32)
            p.dma('sp', self.vecs[:], self.inputs['rwkv_vecs'], [], ['rw_vecs'])
            self.lw2 = L("lw2", [128, D], F32)
            p.dma('sp', self.lw2[:], self.inputs['rwkv_lw2'], [], ['lw2'])
            self.gng_bc = L("gng_bc", [128, D], F32)
            self.gnb_bc = L("gnb_bc", [128, D], F32)
            p.dma('sp', self.gng_bc[:], self.inputs['rwkv_gn_g'].partition_broadcast(128), [], ['bc_tiles'])
            p.dma('sp', self.gnb_bc[:], self.inputs['rwkv_gn_b'].partition_broadcast(128), [], ['bc_tiles'])
            self.Wva = L("Wva", [128, NKC, D], BF16)
            self.Wvb = L("Wvb", [128, NKC, D], BF16)
            src = self.w_in_dram['rwkv']

            def loader(tes):
                muv = tes.enter_context(self.nc.sbuf_tensor("muv_bc", [128, D], F32))
                p.dma('sp', muv[:], self.inputs['rwkv_mu'][2 * D:3 * D].partition_broadcast(128), [], ['bc_tiles'])
                self.load_weight_bf16(self.Wvb, 'Wv', src, D, src_c0=2 * D, scale_bc=muv)
                self.ts('dve', muv[:], muv[:], -1.0, 1.0, ALU.mult, ALU.add, ['bc_tiles'], ['bc_tiles'])
                self.load_weight_bf16(self.Wva, 'Wv', src, D, src_c0=2 * D, scale_bc=muv)
                self.load_weight_bf16(self.W_in, 'W_in', src, 2 * D, src_c0=0, dst_c0=0)
                self.load_weight_bf16(self.W_in, 'W_in', src, 128, src_c0=3 * D, dst_c0=2 * D)
                self.load_weight_bf16(self.W_in, 'W_in', src, D, src_c0=3 * D + 128, dst_c0=2 * D + 128)
                self.load_weight_bf16(self.W_out, 'W_out', self.w_out_dram['rwkv'], D)
            self.S = L("S_rwkv", [128, NKC, 64], F32)
            p.op('pool', lambda e: e.memset(self.S[:], 0.0), [], [('S', j) for j in range(NKC)])
            self.pcar = L("pcar", [128, 25], F32)
            p.op('pool', lambda e: e.memset(self.pcar[:], 0.0), [], [('pcar', i) for i in range(25)])
            self.pm_ext = [L(f"pm_ext{i}", [128, TT + 1], F32) for i in range(2)]
            self.pm_i = 0
            names = ['lo', 'r', 'k', 'tmp', 'sigw', 'a', 'kk', 'rn', 'kmod', 'bbv', 'c', 'e1', 'e2', 'khat', 'bhat']
            self.tmp = {n: L(f"w_{n}", [128, TT], F32) for n in names}
            for n in ['rt_bf', 'bt_bf', 'at_bf', 'kt_h0', 'kt_h1', 'bt_h0', 'bt_h1', 'at_h0', 'at_h1']:
                self.tmp[n] = L(f"w_{n}", [128, TT], BF16)
            self.hm = L("hm", [128, 2], F32)
            p.op('pool', lambda e: e.memset(self.hm[:], 0.0), [], ['masks'])
            p.op('pool', lambda e: e.memset(self.hm[0:64, 0:1], 1.0), ['masks'], ['masks'])
            p.op('pool', lambda e: e.memset(self.hm[64:128, 1:2], 1.0), ['masks'], ['masks'])
            self.pt = {n: [L(f"wp_{n}{q}", [128, TT], F32) for q in range(2)] for n in ['at', 'rt', 'rkr', 'sgate']}
            self.v_sb = [L(f"v_sb{q}", [128, NB, 128], F32) for q in range(2)]
            self.v_bf = [L(f"v_bf{q}", [128, NB, 128], BF16) for q in range(2)]
            self.dec = [L(f"dec{q}", [128, NB], F32) for q in range(2)]
            NCHN = NB * 2
            self.Pb = [L(f"Pb{i}", [128, NCHN, 128], NDT) for i in range(2)]
            self.Qb = [L(f"Qb{i}", [128, NCHN, 128], NDT) for i in range(2)]
            self.NT = [L(f"NT{q}", [128, NCHN, 128], NDT) for q in range(2)]
            self.Aak = [L(f"Aak{q}", [128, NCHN, 128], BF16) for q in range(2)]
            self.Ark = [L(f"Ark{q}", [128, NCHN, 128], BF16) for q in range(2)]
            self.Arb = [L(f"Arb{q}", [128, NCHN, 128], BF16) for q in range(2)]
            self.ident4 = L("ident4", [128, NCHN, 128], NDT)
            for c in range(NCHN):
                self.copy('dve', self.ident4[:, c, :], self.ident[:], ['ident'], ['ident'])
            self.khm = [[L(f"khm{q}{b}", [128, 128], BF16) for b in range(NB)] for q in range(2)]
            self.bhm = [[L(f"bhm{q}{b}", [128, 128], BF16) for b in range(NB)] for q in range(2)]
            self.Z_sb = L("Z_sb", [128, 128], NDT)
            self.U_bf = L("U_bf", [128, 128], BF16)
            self.yn = L("yn", [128, 128], F32)
            self.bon = L("bon", [128, 128], F32)
            self.gst = L("gst", [128, 12], F32)
            self.junk = L("junk", [128, 64], F32)
            self.ps_pool = [0, 1, 2]
            self.nm_rr = 0
            return loader

        NMB = [3, 4, 7]

        def nm_ps():
            i = NMB[self.nm_rr % len(NMB)]
            self.nm_rr += 1
            return self.psums[i], f"ps{i}"

        def shift(ps, pk, dst, dk, idx, mt):
            pm = self.pm_ext[self.pm_i % 2]
            pmk = f"pm_ext{self.pm_i % 2}"
            self.pm_i += 1
            ck = ('pcar', idx)
            self.copy('pool', pm[:, 0:1], self.pcar[:, idx:idx + 1], [ck], [pmk])
            self.act(pm[:, 1:TT + 1], ps[:, :TT], AF.Copy, [pk, 'rw_vecs'], [pmk], scale=self.mu_fm[:, mt:mt + 1])
            self.copy('pool', self.pcar[:, idx:idx + 1], pm[:, TT:TT + 1], [pmk], [ck])
            self.act(dst, ps[:, :TT], AF.Copy, [pk, 'rw_vecs'], [dk], scale=self.omu_fm[:, mt:mt + 1])
            self.tt('dve', dst, dst, pm[:, 0:TT], ALU.add, [dk, pmk], [dk])

        def proj(col0):
            ps, pk = self.next_ps()
            for kc in range(NKC):
                self.mm(ps[:, :TT], self.W_in[:, kc, col0:col0 + 128], self.hn[:, kc, 1:TT + 1], kc == 0, kc == NKC - 1, ['W_in', 'hn'], [pk])
            return ps, pk

        hsl = [slice(0, 64), slice(64, 128)]

        def A_gen(j):
            t = self.tmp
            V = self.vecs
            q = j % 2
            jc = slice(j * 128, (j + 1) * 128)
            at, rt, rkr, sgate = (self.pt[n][q] for n in ('at', 'rt', 'rkr', 'sgate'))
            kat, krt, krkr, ksg = (f'p_{n}{q}' for n in ('at', 'rt', 'rkr', 'sgate'))
            v_sb, v_bf, dec = self.v_sb[q], self.v_bf[q], self.dec[q]
            kv, kvb, kdec = f'v_sb{q}', f'v_bf{q}', f'dec{q}'
            ps, pk = proj(j * 128)
            shift(ps, pk, t['r'][:], 't_r', j, j)
            ps, pk = proj(D + j * 128)
            shift(ps, pk, t['k'][:], 't_k', 8 + j, 8 + j)
            yield
            ps, pk = proj(2 * D + 128 + j * 128)
            shift(ps, pk, t['tmp'][:], 't_tmp', 16 + j, 25 + j)
            self.act(sgate[:], t['tmp'][:], AF.Silu, ['t_tmp'], [ksg])
            pv, pvk = self.next_ps()
            for blk in range(NB):
                n = 0
                for kc in range(NKC):
                    for (Wv, off) in ((self.Wva, 1), (self.Wvb, 0)):
                        self.mm(pv[:, blk * 128:(blk + 1) * 128], self.hn[:, kc, off + blk * 128:off + (blk + 1) * 128], Wv[:, kc, jc],
                                n == 0, n == 2 * NKC - 1, ['Wv', 'hn'], [pvk])
                        n += 1
            self.copy('act', v_sb[:].rearrange("p b v -> p (b v)"), pv[:, :TT], [pvk], [kv])
            self.copy('dve', v_bf[:].rearrange("p b v -> p (b v)"), pv[:, :TT], [pvk], [kvb])
            yield
            pw, pwk = self.next_ps()
            self.mm(pw[:, :TT], self.lw2[0:64, jc], t['lo'][0:64, :], True, True, ['lw2', 't_lo'], [pwk])
            self.act(t['sigw'][:], pw[:, :TT], AF.Sigmoid, [pwk, 'rw_vecs'], ['t_sigw'], bias=V[:, 0, j:j + 1])
            pa, pak = self.next_ps()
            self.mm(pa[:, :TT], self.lw2[64:128, jc], t['lo'][64:128, :], True, True, ['lw2', 't_lo'], [pak])
            self.act(t['a'][:], pa[:, :TT], AF.Sigmoid, [pak, 'rw_vecs'], ['t_a'], bias=V[:, 1, j:j + 1])
            self.ts('dve', t['kk'][:], t['k'][:], V[:, 2, j:j + 1], None, ALU.mult, None, ['t_k', 'rw_vecs'], ['t_kk'])
            self.tt('pool', t['tmp'][:], t['kk'][:], t['kk'][:], ALU.mult, ['t_kk'], ['t_tmp'])
            pn, pnk = self.next_ps()
            self.mm(pn[:, :TT], self.blockones[:], t['tmp'][:], True, True, ['masks', 't_tmp'], [pnk])
            self.ts('dve', t['rn'][:], pn[:, :TT], 1e-24, None, ALU.max, None, [pnk], ['t_rn'])
            self.act(t['rn'][:], t['rn'][:], AF.Ln, ['t_rn'], ['t_rn'])
            self.act(t['rn'][:], t['rn'][:], AF.Exp, ['t_rn'], ['t_rn'], scale=-0.5)
            self.tt('pool', t['kk'][:], t['kk'][:], t['rn'][:], ALU.mult, ['t_kk', 't_rn'], ['t_kk'])
            self.ts('dve', t['tmp'][:], t['a'][:], -1.0, V[:, 3, j:j + 1], ALU.add, ALU.mult, ['t_a', 'rw_vecs'], ['t_tmp'])
            self.stt('dve', t['kmod'][:], t['tmp'][:], 1.0, t['k'][:], ALU.add, ALU.mult, ['t_tmp', 't_k'], ['t_kmod'])
            self.tt('pool', t['bbv'][:], t['kk'][:], t['a'][:], ALU.mult, ['t_kk', 't_a'], ['t_bbv'])
            yield
            self.p.op('dve', lambda e: e.tensor_tensor_scan(out=t['c'][:], data0=self.resetm[:], data1=t['sigw'][:], initial=0.0,
                                                            op0=ALU.mult, op1=ALU.add), ['t_sigw', 'masks'], ['t_c'])
            self.act(t['e1'][:], t['c'][:], AF.Exp, ['t_c'], ['t_e1'], scale=LC)
            self.tt('pool', rt[:], t['r'][:], t['e1'][:], ALU.mult, ['t_r', 't_e1'], [krt])
            self.copy('act', t['rt_bf'][:], rt[:], [krt], ['t_rt_bf'])
            self.act(t['e2'][:], t['c'][:], AF.Exp, ['t_c'], ['t_e2'], scale=-LC)
            for hd in range(2):
                self.stt('dve', t[f'kt_h{hd}'][:], t['kmod'][:], self.hm[:, hd:hd + 1], t['e2'][:], ALU.mult, ALU.mult,
                         ['t_kmod', 't_e2', 'masks'], [f't_kt_h{hd}'])
                self.stt('dve', t[f'bt_h{hd}'][:], t['bbv'][:], self.hm[:, hd:hd + 1], t['e2'][:], ALU.mult, ALU.mult,
                         ['t_bbv', 't_e2', 'masks'], [f't_bt_h{hd}'])
            self.tt('pool', t['bt_bf'][:], t['bbv'][:], t['e2'][:], ALU.mult, ['t_bbv', 't_e2'], ['t_bt_bf'])
            self.tt('pool', t['e1'][:], t['c'][:], t['sigw'][:], ALU.subtract, ['t_c', 't_sigw'], ['t_e1'])
            self.act(t['e1'][:], t['e1'][:], AF.Exp, ['t_e1'], ['t_e1'], scale=LC)
            self.stt('dve', at[:], t['kk'][:], -1.0, t['e1'][:], ALU.mult, ALU.mult, ['t_kk', 't_e1'], [kat])
            self.copy('act', t['at_bf'][:], at[:], [kat], ['t_at_bf'])
            for hd in range(2):
                self.act(t[f'at_h{hd}'][:], at[:], AF.Copy, [kat, 'masks'], [f't_at_h{hd}'], scale=self.hm[:, hd:hd + 1])
            yield
            c3 = t['c'][:].rearrange("p (n c) -> p n c", c=128)
            self.tt('pool', t['e2'][:].rearrange("p (n c) -> p n c", c=128), c3[:, :, 127:128].to_broadcast([128, NB, 128]), c3,
                    ALU.subtract, ['t_c'], ['t_e2'])
            self.act(t['e2'][:], t['e2'][:], AF.Exp, ['t_e2'], ['t_e2'], scale=LC)
            self.act(dec[:], c3[:, :, 127], AF.Exp, ['t_c'], [kdec], scale=LC)
            self.tt('pool', t['khat'][:], t['kmod'][:], t['e2'][:], ALU.mult, ['t_kmod', 't_e2'], ['t_khat'])
            self.tt('dve', t['bhat'][:], t['bbv'][:], t['e2'][:], ALU.mult, ['t_bbv', 't_e2'], ['t_bhat'])
            self.stt('dve', rkr[:], t['r'][:], V[:, 4, j:j + 1], t['kmod'][:], ALU.mult, ALU.mult, ['t_r', 'rw_vecs', 't_kmod'], [krkr])
            yield
            NCH = NB * 2
            specs = {'P': ('bt_h', 'at_bf', self.maskS, self.Pb[0], 'Pb0'),
                     'Q': ('at_h', 'bt_bf', self.maskSL, self.Qb[0], 'Qb0'),
                     'ak': ('kt_h', 'at_bf', self.maskS, self.Aak[q], f'Aak{q}'),
                     'rk': ('kt_h', 'rt_bf', self.maskI, self.Ark[q], f'Ark{q}'),
                     'rb': ('bt_h', 'rt_bf', self.maskI, self.Arb[q], f'Arb{q}')}
            for name in ('P', 'Q', 'ak', 'rk', 'rb'):
                lh, rh, mask, dst, dk = specs[name]
                ps, pk = nm_ps()
                for c in range(NCH):
                    blk, hd = c // 2, c % 2
                    cs = slice(blk * 128, (blk + 1) * 128)
                    self.mm(ps[:, c * 128:(c + 1) * 128], t[f'{lh}{hd}'][:, cs], t[rh][:, cs], True, True, [f't_{lh}{hd}', 't_' + rh], [pk])
                self.tt('dve', dst[:], ps[:, 0:NCH * 128].rearrange("p (c t) -> p c t", c=NCH),
                        mask[:, None, :].to_broadcast([128, NCH, 128]), ALU.mult, [pk, 'masks'], [dk])
                if name == 'Q':
                    self.tt('pool', self.NT[q][:], self.ident4[:], self.Pb[0][:], ALU.add, ['ident', 'Pb0'], [f'NT{q}'])
                    yield
            for blk in range(NB):
                cs = slice(blk * 128, (blk + 1) * 128)
                for (srcn, dst, dk) in (('khat', self.khm[q][blk], f'khm{q}{blk}'), ('bhat', self.bhm[q][blk], f'bhm{q}{blk}')):
                    ps, pk = nm_ps()
                    self.p.op('pe', lambda e, ps=ps, srcn=srcn, cs=cs: e.transpose(ps[:, 0:128], t[srcn][:, cs], self.ident[:]),
                              ['t_' + srcn, 'ident'], [pk])
                    self.copy('act', dst[:], ps[:, 0:128], [pk], [dk])
            yield
            NTq, kNT = self.NT[q], f'NT{q}'
            for i in range(6):
                a_, b_ = i % 2, (i + 1) % 2
                Pa, Qa, Pn, Qn = self.Pb[a_], self.Qb[a_], self.Pb[b_], self.Qb[b_]
                kPa, kQa, kPn, kQn = f'Pb{a_}', f'Qb{a_}', f'Pb{b_}', f'Qb{b_}'
                if i < 5:
                    ps, pk = nm_ps()
                    for c in range(NCH):
                        self.mm(ps[:, c * 128:(c + 1) * 128], Qa[:, c, :], Pa[:, c, :], True, True, [kQa, kPa], [pk])
                    self.copy('act', Pn[:].rearrange("p c t -> p (c t)"), ps[:, 0:NCH * 128], [pk], [kPn])
                ps, pk = nm_ps()
                for c in range(NCH):
                    self.mm(ps[:, c * 128:(c + 1) * 128], Pa[:, c, :], Qa[:, c, :], True, True, [kPa, kQa], [pk])
                self.copy('dve' if i % 2 == 0 else 'act', Qn[:].rearrange("p c t -> p (c t)"), ps[:, 0:NCH * 128], [pk], [kQn])
                yield
                ps, pk = nm_ps()
                for c in range(NCH):
                    self.mm(ps[:, c * 128:(c + 1) * 128], Qn[:, c, :], NTq[:, c, :], True, True, [kQn, kNT], [pk])
                self.tt('dve', NTq[:].rearrange("p c t -> p (c t)"), NTq[:].rearrange("p c t -> p (c t)"), ps[:, 0:NCH * 128], ALU.add,
                        [kNT, pk], [kNT])
                yield

        def B_gen(j):
            t = self.tmp
            P = self.psums
            q = j % 2
            jc = slice(j * 128, (j + 1) * 128)
            at, rt, rkr, sgate = (self.pt[n][q] for n in ('at', 'rt', 'rkr', 'sgate'))
            kat, krt, krkr, ksg = (f'p_{n}{q}' for n in ('at', 'rt', 'rkr', 'sgate'))
            v_sb, v_bf, dec = self.v_sb[q], self.v_bf[q], self.dec[q]
            kv, kvb, kdec = f'v_sb{q}', f'v_bf{q}', f'dec{q}'
            kS = ('S', j)
            for blk in range(NB):
                cs = slice(blk * 128, (blk + 1) * 128)
                khm, bhm = self.khm[q][blk], self.bhm[q][blk]
                kkh, kbh = f'khm{q}{blk}', f'bhm{q}{blk}'
                pz, pzk = P[5], 'ps5'
                for hd in range(2):
                    hs, hc, c = hsl[hd], slice(hd * 64, (hd + 1) * 64), blk * 2 + hd
                    self.mm(pz[:, hc], self.Aak[q][:, c, :], v_bf[:, blk, hc], True, False, [f'Aak{q}', kvb], [pzk])
                    self.mm(pz[:, hc], at[hs, cs], self.S[hs, j, :], False, True, [kat, kS], [pzk])
                self.copy('act', self.Z_sb[:], pz[:, 0:128], [pzk], ['Z_sb'])
                yield
                for hd in range(2):
                    hc, c = slice(hd * 64, (hd + 1) * 64), blk * 2 + hd
                    self.mm(pz[:, hc], self.NT[q][:, c, :], self.Z_sb[:, hc], True, True, [f'NT{q}', 'Z_sb'], [pzk])
                self.copy('act', self.U_bf[:], pz[:, 0:128], [pzk], ['U_bf'])
                yield
                self.mm(pz[:, 0:128], khm[:], v_bf[:, blk, :], True, False, [kkh, kvb], [pzk])
                self.mm(pz[:, 0:128], bhm[:], self.U_bf[:], False, True, [kbh, 'U_bf'], [pzk])
                py, pyk = P[6], 'ps6'
                for hd in range(2):
                    hs, hc, c = hsl[hd], slice(hd * 64, (hd + 1) * 64), blk * 2 + hd
                    yc = slice(hd * 128, hd * 128 + 64)
                    bc_ = slice(hd * 128 + 64, hd * 128 + 128)
                    self.mm(py[:, yc], self.Ark[q][:, c, :], v_bf[:, blk, hc], True, False, [f'Ark{q}', kvb], [pyk])
                    self.mm(py[:, yc], rt[hs, cs], self.S[hs, j, :], False, False, [krt, kS], [pyk])
                    self.mm(py[:, yc], self.Arb[q][:, c, :], self.U_bf[:, hc], False, True, [f'Arb{q}', 'U_bf'], [pyk])
                    self.mm(py[:, bc_], rkr[hs, cs], self.blockones[hs, hs], True, True, [krkr, 'masks'], [pyk])
                for hd in range(2):
                    hs, hc = hsl[hd], slice(hd * 64, (hd + 1) * 64)
                    self.stt('dve', self.S[hs, j, :], self.S[hs, j, :], dec[hs, blk:blk + 1], pz[hs, hc], ALU.mult, ALU.add,
                             [kS, kdec, pzk], [kS])
                yield
                g = self.gst
                for hd in range(2):
                    hc = slice(hd * 64, (hd + 1) * 64)
                    yc = slice(hd * 128, hd * 128 + 64)
                    bc_ = slice(hd * 128 + 64, hd * 128 + 128)
                    self.act(self.junk[:], py[:, yc], AF.Identity, [pyk], ['junk', 'gst'], accum_out=g[:, hd:hd + 1])
                    self.act(self.junk[:], py[:, yc], AF.Square, [pyk], ['junk', 'gst'], accum_out=g[:, 2 + hd:3 + hd])
                    self.tt('dve', self.bon[:, hc], py[:, bc_], v_sb[:, blk, hc], ALU.mult, [pyk, kv], ['bon'])
                self.ts('dve', g[:, 4:6], g[:, 0:2], 1.0 / 64, None, ALU.mult, None, ['gst'], ['gst'])
                self.tt('dve', g[:, 6:8], g[:, 4:6], g[:, 4:6], ALU.mult, ['gst'], ['gst'])
                self.stt('dve', g[:, 8:10], g[:, 2:4], 1.0 / 64, g[:, 6:8], ALU.mult, ALU.subtract, ['gst'], ['gst'])
                self.act(g[:, 8:10], g[:, 8:10], AF.Ln, ['gst', 'consts'], ['gst'], bias=self.epsc[:, 1:2])
                self.act(g[:, 8:10], g[:, 8:10], AF.Exp, ['gst'], ['gst'], scale=-0.5)
                for hd in range(2):
                    hc = slice(hd * 64, (hd + 1) * 64)
                    yc = slice(hd * 128, hd * 128 + 64)
                    self.ts('dve', self.yn[:, hc], py[:, yc], g[:, 4 + hd:5 + hd], g[:, 8 + hd:9 + hd], ALU.subtract, ALU.mult,
                            [pyk, 'gst'], ['yn'])
                yield
                self.tt('pool', self.yn[:], self.yn[:], self.gng_bc[:, jc], ALU.mult, ['yn', 'bc_tiles'], ['yn'])
                self.tt('pool', self.yn[:], self.yn[:], self.gnb_bc[:, jc], ALU.add, ['yn', 'bc_tiles'], ['yn'])
                self.tt('pool', self.yn[:], self.yn[:], self.bon[:], ALU.add, ['yn', 'bon'], ['yn'])
                ps, pk = nm_ps()
                self.p.op('pe', lambda e, ps=ps: e.transpose(ps[:, 0:128], self.yn[:], self.ident[:]), ['yn', 'ident'], [pk])
                self.tt('dve', self.yTt[:, j, cs], ps[:, 0:128], sgate[:, cs], ALU.mult, [pk, ksg], ['yT'])
                yield

        def drive(gens):
            gens = [g for g in gens if g is not None]
            while gens:
                for g in list(gens):
                    try:
                        next(g)
                    except StopIteration:
                        gens.remove(g)

        def tile(ti):
            t = self.tmp
            ps, pk = proj(2 * D)
            shift(ps, pk, t['lo'][:], 't_lo', 24, 24)
            self.act(t['lo'][0:64, :], t['lo'][0:64, :], AF.Tanh, ['t_lo'], ['t_lo'])
            drive([A_gen(0)])
            for j in range(NKC):
                drive([B_gen(j), A_gen(j + 1) if j + 1 < NKC else None])

        self.run_layer(li, 'rwkv', TT, WC, setup, tile, is_last)
        self.ps_pool = list(range(8))

    def build(self):
        nc = self.nc
        T = self.T
        self.xT = self.din("xT", [D, T])
        self.yT = nc.dram_tensor("yT", [D, T], F32, kind="ExternalOutput").ap()
        d_norm_g = self.din("norm_g", [128, 4, NKC])
        d_final_g = self.din("final_g", [128, NKC])
        self.w_in_dram, self.w_out_dram = {}, {}
        kinds = [k for (_, k) in self.layers]
        if 'conv' in kinds:
            self.w_in_dram['conv'] = self.din("conv_w_in", [D, 4 * D])
            self.w_out_dram['conv'] = self.din("conv_w_out", [D, D])
            d_conv_w = self.din("conv_w", [128, NKC, 3])
        if 'rwkv' in kinds:
            self.w_in_dram['rwkv'] = self.din("rwkv_w_in", [D, 4 * D + 128])
            self.w_out_dram['rwkv'] = self.din("rwkv_w_out", [D, D])
            self.din("rwkv_mu_fm", [128, 33])
            self.din("rwkv_mu", [4 * D + 128])
            self.din("rwkv_vecs", [128, 5, NKC])
            self.din("rwkv_lw2", [128, D])
            self.din("rwkv_gn_g", [D])
            self.din("rwkv_gn_b", [D])
        if 'hgrn' in kinds:
            self.w_in_dram['hgrn'] = self.din("hgrn_w_in", [D, 4 * D])
            self.w_out_dram['hgrn'] = self.din("hgrn_w_out", [D, D])
            self.din("hgrn_gn_g", [D])
            self.din("hgrn_lbl", [128, 4, NKC])
        if 'gmlp' in kinds:
            self.w_in_dram['gmlp'] = self.din("gmlp_w_in", [D, 3 * D])
            self.w_out_dram['gmlp'] = self.din("gmlp_w_out", [D, D])
            self.din("gmlp_wsT", [128, 8, 128])
            self.din("gmlp_bs", [8, 128])
            self.din("gmlp_vg", [D])
        with ExitStack() as es:
            self.es = es
            nc.allow_low_precision("bf16 matmul operands, fp32 accumulation")
            self.p = p = Prog(nc, es)
            self.psums = [es.enter_context(nc.psum_tensor(f"ps{i}", [128, 512], F32)) for i in range(8)]
            self.ps_rr = 0
            self.ps_pool = list(range(8))
            self.ones_bf = self.sb("ones_bf", [128, 128], BF16)
            self.epsc = self.sb("epsc", [128, 4], F32)
            self.norm_g = self.sb("norm_g_sb", [128, 4, NKC], F32)
            self.final_g = self.sb("final_g_sb", [128, NKC], F32)
            p.op('pool', lambda e: e.memset(self.ones_bf[:], 1.0), [], ['ones_bf'])
            p.op('pool', lambda e: e.memset(self.epsc[:, 0:1], RMS_EPS), [], ['consts'])
            p.op('pool', lambda e: e.memset(self.epsc[:, 2:3], 1.0), ['consts'], ['consts'])
            p.dma('sp', self.norm_g[:], d_norm_g, [], ['consts'])
            p.dma('sp', self.final_g[:], d_final_g, [], ['consts'])
            if 'conv' in kinds:
                self.conv_w = self.sb("conv_w_sb", [128, NKC, 3], F32)
                p.dma('sp', self.conv_w[:], d_conv_w, [], ['consts'])
            self.first_layer = True
            for n, (li, kind) in enumerate(self.layers):
                is_last = n == len(self.layers) - 1
                if kind == 'conv':
                    self.conv_layer(li, is_last)
                elif kind == 'gmlp':
                    self.gmlp_layer(li, is_last)
                elif kind == 'hgrn':
                    self.hgrn_layer(li, is_last)
                elif kind == 'rwkv':
                    self.rwkv_layer(li, is_last)
                else:
                    raise ValueError(kind)
            p.finish('sp')
            self.stats = (p.n_ins, p.n_wait)
        return nc


def prep_inputs(inp, b, layers):
    f = np.float32
    m = {}
    m["xT"] = np.ascontiguousarray(np.asarray(inp["x"][b], f).T)
    m["norm_g"] = np.ascontiguousarray(np.asarray(inp["norm_g"], f).reshape(4, NKC, 128).transpose(2, 0, 1))
    m["final_g"] = np.ascontiguousarray(np.asarray(inp["final_g"], f).reshape(NKC, 128).T)
    kinds = [k for (_, k) in layers]
    if 'conv' in kinds:
        m["conv_w_in"] = np.ascontiguousarray(np.asarray(inp["conv_w_in"][0], f))
        m["conv_w_out"] = np.ascontiguousarray(np.asarray(inp["conv_w_out"][0], f))
        m["conv_w"] = np.ascontiguousarray(np.asarray(inp["conv_w"][0], f).reshape(3, NKC, 128).transpose(2, 1, 0))
    if 'rwkv' in kinds:
        m["rwkv_w_in"] = np.ascontiguousarray(np.asarray(inp["rwkv_w_in"][0], f))
        m["rwkv_w_out"] = np.ascontiguousarray(np.asarray(inp["rwkv_w_out"][0], f))
        mu = np.asarray(inp["rwkv_mu"][0], f)
        m["rwkv_mu"] = np.ascontiguousarray(mu)
        m["rwkv_mu_fm"] = np.ascontiguousarray(mu.reshape(33, 128).T)
        vecs = np.stack([np.asarray(inp[k][0], f).reshape(NKC, 128) for k in
                         ("rwkv_w0", "rwkv_a0", "rwkv_k_k", "rwkv_k_a", "rwkv_r_k")], axis=0)
        m["rwkv_vecs"] = np.ascontiguousarray(vecs.transpose(2, 0, 1))
        m["rwkv_lw2"] = np.ascontiguousarray(np.concatenate([np.asarray(inp["rwkv_w_w2"][0], f), np.asarray(inp["rwkv_w_a2"][0], f)], axis=0))
        m["rwkv_gn_g"] = np.ascontiguousarray(np.asarray(inp["rwkv_gn_g"][0], f))
        m["rwkv_gn_b"] = np.ascontiguousarray(np.asarray(inp["rwkv_gn_b"][0], f))
    if 'hgrn' in kinds:
        m["hgrn_w_in"] = np.ascontiguousarray(np.asarray(inp["hgrn_w_in"][0], f))
        m["hgrn_w_out"] = np.ascontiguousarray(np.asarray(inp["hgrn_w_out"][0], f))
        m["hgrn_gn_g"] = np.ascontiguousarray(np.asarray(inp["hgrn_gn_g"][0], f))
        m["hgrn_lbl"] = np.ascontiguousarray(np.asarray(inp["hgrn_lb_logits"], f).reshape(4, NKC, 128).transpose(2, 0, 1))
    if 'gmlp' in kinds:
        m["gmlp_w_in"] = np.ascontiguousarray(np.asarray(inp["gmlp_w_in"][0], f))
        m["gmlp_w_out"] = np.ascontiguousarray(np.asarray(inp["gmlp_w_out"][0], f))
        m["gmlp_wsT"] = np.ascontiguousarray(np.asarray(inp["gmlp_w_s"][0], f).transpose(2, 0, 1))
        m["gmlp_bs"] = np.ascontiguousarray(np.asarray(inp["gmlp_b_s"][0], f))
        m["gmlp_vg"] = np.ascontiguousarray(np.asarray(inp["gmlp_v_g"][0], f))
    return m


FULL_LAYERS = [(0, 'rwkv'), (1, 'hgrn'), (2, 'conv'), (3, 'gmlp')]


def kernel(**inputs):
    x = np.asarray(inputs["x"])
    B, T, _ = x.shape
    layers = FULL_LAYERS
    bld = Builder(T, layers)
    nc = bld.build()
    in_maps = []
    for c in range(8):
        in_maps.append(prep_inputs(inputs, c // 2, layers))
    res = run_bass_kernel_spmd(nc, in_maps, core_ids=list(range(8)))
    out = np.stack([np.asarray(res.results[2 * b]["yT"]).T for b in range(B)], axis=0)
    return out.astype(np.float32)
```

```python
import numpy as np
from contextlib import ExitStack
import concourse.bass as bass
import concourse.mybir as mybir
from concourse.bass_utils import run_bass_kernel_spmd

F32 = mybir.dt.float32
BF16 = mybir.dt.bfloat16
ALU = mybir.AluOpType
AF = mybir.ActivationFunctionType
AX = mybir.AxisListType

D = 1024
NKC = 8
RMS_EPS = 1e-6
GN_EPS = 64e-5


class Prog:
    LIMIT = 30000

    def __init__(self, nc, es, n_dma_sems=24):
        self.nc = nc
        self.es = es
        self.engs = {'pe': nc.tensor, 'act': nc.scalar, 'dve': nc.vector,
                     'pool': nc.gpsimd, 'sp': nc.sync}
        self.sems = {}
        self.epoch = {k: 0 for k in self.engs}
        self.cnt = {k: 0 for k in self.engs}
        for k in self.engs:
            self.sems[(k, 0)] = es.enter_context(nc.semaphore(f"s_{k}_0"))
        self.dma_sems = []
        for i in range(n_dma_sems):
            key = ('dma', i)
            self.sems[key] = es.enter_context(nc.semaphore(f"s_dma_{i}"))
            self.cnt[key] = 0
            self.dma_sems.append(key)
        self.dma_rr = 0
        self.waited = {k: {} for k in self.engs}
        self.bufs = {}
        self.n_wait = 0
        self.n_ins = 0

    def _deps(self, reads, writes):
        deps = set()
        for k in reads:
            b = self.bufs.get(k)
            if b and b['w']:
                deps.add(b['w'])
        for k in writes:
            b = self.bufs.get(k)
            if b:
                if b['w']:
                    deps.add(b['w'])
                deps.update(b['r'])
        return deps

    def _wait(self, eng, deps):
        e = self.engs[eng]
        best = {}
        for (sk, v) in deps:
            if sk[0] == eng and eng == 'pe':
                continue
            if best.get(sk, 0) < v:
                best[sk] = v
        for sk, v in best.items():
            if self.waited[eng].get(sk, 0) >= v:
                continue
            e.wait_ge(self.sems[sk], v)
            self.waited[eng][sk] = v
            self.n_wait += 1

    def _record(self, tok, reads, writes):
        for k in reads:
            b = self.bufs.setdefault(k, {'w': None, 'r': []})
            b['r'].append(tok)
            if len(b['r']) > 64:
                best = {}
                for (sk, v) in b['r']:
                    if best.get(sk, 0) < v:
                        best[sk] = v
                b['r'] = list(best.items())
        for k in writes:
            b = self.bufs.setdefault(k, {'w': None, 'r': []})
            b['w'] = tok
            b['r'] = []

    @staticmethod
    def _excl(reads, writes):
        ps = [k for k in reads if isinstance(k, str) and k.startswith('ps')]
        if ps:
            reads = [k for k in reads if k not in ps]
            writes = list(writes) + ps
        return reads, writes

    disabled = False
    recording = None
    SYNC_LAT = 0.45

    def begin_record(self):
        self.recording = []

    def flush(self):
        rec = self.recording
        self.recording = None
        if not rec:
            return
        n = len(rec)
        preds = [None] * n
        succs = [[] for _ in range(n)]
        last_w = {}
        readers = {}
        for i, (kind, eng, fn, reads, writes, cost, lat) in enumerate(rec):
            ps = set()
            for k in reads:
                w = last_w.get(k)
                if w is not None:
                    ps.add(w)
            for k in writes:
                w = last_w.get(k)
                if w is not None:
                    ps.add(w)
                ps.update(readers.get(k, ()))
            ps.discard(i)
            preds[i] = ps
            for pi in ps:
                succs[pi].append(i)
            for k in reads:
                readers.setdefault(k, []).append(i)
            for k in writes:
                last_w[k] = i
                readers[k] = []
        npred = [len(p_) for p_ in preds]
        ready = [i for i in range(n) if npred[i] == 0]
        eng_free = {}
        end_t = [0.0] * n
        done_t = [0.0] * n
        order = []
        blevel = [0.0] * n
        for i in range(n - 1, -1, -1):
            kind, eng, fn, reads, writes, cost, lat = rec[i]
            b = 0.0
            for si in succs[i]:
                v = blevel[si] + (self.SYNC_LAT if rec[si][1] != eng else 0.0)
                if v > b:
                    b = v
            blevel[i] = b + cost + lat

        def est(i):
            kind, eng, fn, reads, writes, cost, lat = rec[i]
            t = eng_free.get(eng, 0.0)
            for pi in preds[i]:
                tp = done_t[pi] + (self.SYNC_LAT if rec[pi][1] != eng else 0.0)
                if tp > t:
                    t = tp
            return t
        EPS = 0.25
        while ready:
            ests = [(est(i), i) for i in ready]
            tmin = min(ests)[0]
            best = None
            for (t, i) in ests:
                if t <= tmin + EPS:
                    if best is None or blevel[i] > blevel[best[1]] or (blevel[i] == blevel[best[1]] and i < best[1]):
                        best = (t, i)
            t1, i = best
            ready.remove(i)
            kind, eng, fn, reads, writes, cost, lat = rec[i]
            end_t[i] = t1 + cost
            done_t[i] = t1 + cost + lat
            eng_free[eng] = end_t[i]
            order.append(i)
            for si in succs[i]:
                npred[si] -= 1
                if npred[si] == 0:
                    ready.append(si)
        assert len(order) == n, (len(order), n)
        self.sched_span = max(done_t) if done_t else 0.0
        for i in order:
            kind, eng, fn, reads, writes, cost, lat = rec[i]
            if kind == 'op':
                self.op(eng, fn, reads, writes)
            else:
                out, in_, kw = fn
                self.dma(eng, out, in_, reads, writes, **kw)

    def op(self, eng, fn, reads=(), writes=(), cost=None):
        if self.disabled:
            return None
        if self.recording is not None:
            reads, writes = self._excl(reads, writes)
            if cost is None:
                cost = {'pe': 0.2, 'act': 0.45, 'dve': 0.45, 'pool': 0.7, 'sp': 0.1}[eng]
            self.recording.append(('op', eng, fn, list(reads), list(writes), cost, 0.0))
            return None
        reads, writes = self._excl(reads, writes)
        deps = self._deps(reads, writes)
        self._wait(eng, deps)
        ins = fn(self.engs[eng])
        if self.cnt[eng] >= self.LIMIT:
            self.epoch[eng] += 1
            ep = self.epoch[eng]
            self.sems[(eng, ep)] = self.es.enter_context(self.nc.semaphore(f"s_{eng}_{ep}"))
            self.cnt[eng] = 0
        sk = (eng, self.epoch[eng])
        self.cnt[eng] += 1
        ins.then_inc(self.sems[sk], 1)
        self._record((sk, self.cnt[eng]), reads, writes)
        self.n_ins += 1
        return ins

    def dma(self, eng, out, in_, reads=(), writes=(), **kw):
        if self.disabled:
            return None
        if self.recording is not None:
            self.recording.append(('dma', eng, (out, in_, kw), list(reads), list(writes), 0.15, 6.0))
            return None
        deps = self._deps(reads, writes)
        sk = self.dma_sems[self.dma_rr]
        self.dma_rr = (self.dma_rr + 1) % len(self.dma_sems)
        if self.cnt[sk] > 0:
            deps.add((sk, self.cnt[sk]))
        self._wait(eng, deps)
        ins = self.engs[eng].dma_start(out=out, in_=in_, **kw)
        self.cnt[sk] += 16
        ins.then_inc(self.sems[sk], 16)
        self._record((sk, self.cnt[sk]), reads, writes)
        self.n_ins += 1
        return ins

    def all_tokens(self):
        deps = set()
        for k, b in self.bufs.items():
            if b['w']:
                deps.add(b['w'])
            deps.update(b['r'])
        return deps

    def barrier(self):
        deps = self.all_tokens()
        for eng in self.engs:
            d = set(x for x in deps)
            self._wait(eng, d)

    def finish(self, eng='sp'):
        self._wait(eng, self.all_tokens())


class Builder:
    def __init__(self, T, layers, do_final=True, neu_dt=None):
        self.neu_dt = neu_dt if neu_dt is not None else BF16
        self.use_sched = True
        self.T = T
        self.layers = layers
        self.do_final = do_final
        self.nc = bass.Bass("TRN2", target_bir_lowering=False)
        self.inputs = {}

    def din(self, name, shape):
        t = self.nc.dram_tensor(name, list(shape), F32, kind="ExternalInput").ap()
        self.inputs[name] = t
        return t

    def sb(self, name, shape, dt=F32):
        return self.es.enter_context(self.nc.sbuf_tensor(name, list(shape), dt))

    def lsb(self, name, shape, dt=F32):
        return self.les.enter_context(self.nc.sbuf_tensor(f"{name}_{self.lname}", list(shape), dt))

    def next_ps(self):
        pool = self.ps_pool
        i = pool[self.ps_rr % len(pool)]
        self.ps_rr += 1
        return self.psums[i], f"ps{i}"

    @staticmethod
    def ecost(eng, ap):
        try:
            n = ap.free_size()
        except Exception:
            n = 256
        if eng == 'act':
            return 0.22 + n * 0.00075
        if eng == 'dve':
            return 0.2 + n * 0.00095
        if eng == 'pool':
            return 0.2 + n * 0.0021
        return 0.2

    def tt(self, eng, out, in0, in1, op, reads, writes):
        return self.p.op(eng, lambda e: e.tensor_tensor(out=out, in0=in0, in1=in1, op=op), reads, writes, cost=self.ecost(eng, out))

    def ts(self, eng, out, in0, s1, s2, op0, op1, reads, writes):
        if s2 is None:
            return self.p.op(eng, lambda e: e.tensor_scalar(out=out, in0=in0, scalar1=s1, scalar2=None, op0=op0), reads, writes, cost=self.ecost(eng, out))
        return self.p.op(eng, lambda e: e.tensor_scalar(out=out, in0=in0, scalar1=s1, scalar2=s2, op0=op0, op1=op1), reads, writes, cost=self.ecost(eng, out))

    def stt(self, eng, out, in0, scalar, in1, op0, op1, reads, writes):
        eng = 'dve'
        return self.p.op(eng, lambda e: e.scalar_tensor_tensor(out=out, in0=in0, scalar=scalar, in1=in1, op0=op0, op1=op1), reads, writes, cost=self.ecost(eng, out))

    def act(self, out, in_, func, reads, writes, bias=None, scale=1.0, accum_out=None):
        kw = {}
        if bias is not None:
            kw['bias'] = bias
        if accum_out is not None:
            kw['accum_out'] = accum_out
        return self.p.op('act', lambda e: e.activation(out=out, in_=in_, func=func, scale=scale, **kw), reads, writes, cost=self.ecost('act', in_))

    def mm(self, out, lhsT, rhs, start, stop, reads, writes):
        try:
            n = rhs.free_size()
        except Exception:
            n = 128
        c = 0.06 + n / 2400.0 * (1.0 if lhsT.dtype == BF16 else 2.4)
        return self.p.op('pe', lambda e: e.matmul(out, lhsT=lhsT, rhs=rhs, start=start, stop=stop), reads, writes, cost=c)

    def copy(self, eng, out, in_, reads, writes):
        if eng == 'act':
            return self.p.op('act', lambda e: e.copy(out=out, in_=in_), reads, writes, cost=self.ecost('act', out))
        return self.p.op(eng, lambda e: e.tensor_copy(out=out, in_=in_), reads, writes, cost=self.ecost(eng, out))

    def load_weight_bf16(self, dst, dst_key, src, ncols, src_c0=0, dst_c0=0, scale_bc=None):
        p = self.p
        CH = 1024 if ncols % 1024 == 0 else ncols
        for kc in range(NKC):
            for c0 in range(0, ncols, CH):
                i = self.stage_i
                self.stage_i += 1
                st = self.stage[i % 2]
                sk = f"stage{i % 2}"
                p.dma('sp', st[:, 0:CH], src[kc * 128:(kc + 1) * 128, src_c0 + c0:src_c0 + c0 + CH], reads=[], writes=[sk])
                eng = ['dve', 'pool'][i % 2]
                if scale_bc is None:
                    self.copy(eng, dst[:, kc, dst_c0 + c0:dst_c0 + c0 + CH], st[:, 0:CH], [sk], [dst_key])
                else:
                    self.tt(eng, dst[:, kc, dst_c0 + c0:dst_c0 + c0 + CH], st[:, 0:CH], scale_bc[:, c0:c0 + CH], ALU.mult,
                            [sk, 'bc_tiles'], [dst_key])

    def rms_rstd(self, src, src_key, TT, tag):
        bi = self.rms_i % len(self.sqb_l)
        self.rms_i += 1
        sqb, rstd = self.sqb_l[bi], self.rstd_l[bi]
        ksq, krs = f'sqb{bi}', f'rstd{bi}'
        for kc in range(NKC):
            if kc % 2 == 0:
                self.act(sqb[:, kc, :TT], src[:, kc, :TT], AF.Square, [src_key], [(ksq, kc)])
            else:
                self.tt('dve', sqb[:, kc, :TT], src[:, kc, :TT], src[:, kc, :TT], ALU.mult, [src_key], [(ksq, kc)])
        ps, pk = self.next_ps()
        for kc in range(NKC):
            self.mm(ps[:, :TT], self.ones_bf[:], sqb[:, kc, :TT], kc == 0, kc == NKC - 1, [(ksq, kc), 'ones_bf'], [pk])
        self.act(rstd[:, :TT], ps[:, :TT], AF.Ln, [pk, 'consts'], [krs], bias=self.epsc[:, 0:1], scale=1.0 / D)
        self.act(rstd[:, :TT], rstd[:, :TT], AF.Exp, [krs], [krs], scale=-0.5)
        return rstd, krs

    def run_layer(self, li, kind, TT, w_in_cols, mixer_setup, mixer_tile, is_last):
        p = self.p
        T = self.T
        ntiles = T // TT
        with ExitStack() as les:
            self.les = les
            self.lname = f"L{li}"
            self.TT = TT
            self.W_in = self.lsb("W_in", [128, NKC, w_in_cols], BF16)
            self.W_out = self.lsb("W_out", [128, NKC, D], BF16)
            self.hT = [self.lsb(f"hT{i}", [128, NKC, TT], F32) for i in range(2)]
            ndb = 2 if kind != 'rwkv' else 1
            self.sqb_l = [self.lsb(f"sqb{i}", [128, NKC, TT], BF16) for i in range(ndb)]
            self.rstd_l = [self.lsb(f"rstd{i}", [128, TT], F32) for i in range(ndb)]
            self.sqb, self.rstd = self.sqb_l[0], self.rstd_l[0]
            self.rms_i = 0
            self.hn = self.lsb("hn", [128, NKC, TT + 1], BF16)
            self.yTt = self.lsb("yTt", [128, NKC, TT], BF16)
            self.stage_i = 0
            loader = mixer_setup()
            with ExitStack() as ses:
                self.stage = [ses.enter_context(self.nc.sbuf_tensor(f"stage{i}_{self.lname}", [128, 1024], F32)) for i in range(2)]
                if loader is None:
                    self.load_weight_bf16(self.W_in, 'W_in', self.w_in_dram[kind], w_in_cols)
                    self.load_weight_bf16(self.W_out, 'W_out', self.w_out_dram[kind], D)
                else:
                    loader(ses)
                p.barrier()
            if getattr(self, 'post_setup', None) is not None:
                self.post_setup()
                self.post_setup = None
            p.op('pool', lambda e: e.memset(self.hn[:, :, 0:1], 0.0), [], ['hn'])

            def load(ti):
                buf = self.hT[ti % 2]
                src = self.xT if self.first_layer else self.yT
                p.dma('sp', buf[:], src.rearrange("(c p) t -> p c t", p=128)[:, :, ti * TT:(ti + 1) * TT],
                      reads=[('hd', ti * TT // 128 + i) for i in range(TT // 128)], writes=[f"hT{ti % 2}"])

            if self.use_sched:
                p.begin_record()
            load(0)
            for ti in range(ntiles):
                if ti + 1 < ntiles:
                    load(ti + 1)
                h = self.hT[ti % 2]
                hk = f"hT{ti % 2}"
                rstd, rk = self.rms_rstd(h, hk, TT, 'in')
                g = self.norm_g
                if ti > 0:
                    self.copy('pool', self.hn[:, :, 0:1], self.hn[:, :, TT:TT + 1], ['hn'], ['hn'])
                for kc in range(NKC):
                    self.stt('dve', self.hn[:, kc, 1:TT + 1], h[:, kc, :], g[:, li, kc:kc + 1], rstd[:, :TT],
                             ALU.mult, ALU.mult, [hk, rk, 'consts'], ['hn'])
                mixer_tile(ti)
                for j in range(NKC):
                    ps, pk = self.next_ps()
                    for kc in range(NKC):
                        self.mm(ps[:, :TT], self.W_out[:, kc, j * 128:(j + 1) * 128], self.yTt[:, kc, :TT],
                                kc == 0, kc == NKC - 1, ['W_out', 'yT'], [pk])
                    self.tt('dve', h[:, j, :], h[:, j, :], ps[:, :TT], ALU.add, [hk, pk], [hk])
                if is_last and self.do_final:
                    rstd, rk = self.rms_rstd(h, hk, TT, 'fin')
                    for kc in range(NKC):
                        self.stt('dve' if kc % 2 == 0 else 'pool', h[:, kc, :], h[:, kc, :], self.final_g[:, kc:kc + 1], rstd[:, :TT],
                                 ALU.mult, ALU.mult, [hk, rk, 'consts'], [hk])
                p.dma('sp', self.yT.rearrange("(c p) t -> p c t", p=128)[:, :, ti * TT:(ti + 1) * TT], h[:],
                      reads=[hk], writes=[('hd', (ti * TT) // 128 + i) for i in range(max(1, TT // 128))])
            if self.use_sched:
                p.flush()
            self.first_layer = False
            p.barrier()
        self.les = None

    def conv_layer(self, li, is_last):
        TT = 512

        def setup():
            self.yext = self.lsb("yext", [128, NKC, TT + 2], F32)
            self.zs = [self.lsb(f"zs{i}", [128, TT], F32) for i in range(2)]
            self.acc = [self.lsb(f"acc{i}", [128, TT], F32) for i in range(2)]
            self.sg = [self.lsb(f"sg{i}", [128, TT], F32) for i in range(2)]
            self.p.op('pool', lambda e: e.memset(self.yext[:, :, 0:2], 0.0), [], [('yext', j) for j in range(NKC)])

        def tile(ti):
            W = self.W_in
            cw = self.conv_w
            for j in range(NKC):
                zs, acc, sg = self.zs[j % 2], self.acc[j % 2], self.sg[j % 2]
                zk, ak, gk = f"zs{j % 2}", f"acc{j % 2}", f"sg{j % 2}"
                pss = []
                for blk in range(4):
                    ps, pk = self.next_ps()
                    col0 = blk * D + j * 128
                    for kc in range(NKC):
                        self.mm(ps[:, :TT], W[:, kc, col0:col0 + 128], self.hn[:, kc, 1:TT + 1], kc == 0, kc == NKC - 1,
                                ['W_in', 'hn'], [pk])
                    pss.append((ps, pk))
                (pb, pbk), (pc, pck), (pz, pzk), (pg, pgk) = pss
                yk = ('yext', j)
                self.copy('act', zs[:], pz[:, :TT], [pzk], [zk])
                if ti > 0:
                    self.copy('pool', self.yext[:, j, 0:2], self.yext[:, j, TT:TT + 2], [yk], [yk])
                self.tt('dve', self.yext[:, j, 2:TT + 2], pc[:, :TT], zs[:], ALU.mult, [pck, zk], [yk])
                self.act(acc[:], self.yext[:, j, 2:TT + 2], AF.Copy, [yk, 'consts'], [ak], scale=cw[:, j, 2:3])
                self.stt('pool', acc[:], self.yext[:, j, 1:TT + 1], cw[:, j, 1:2], acc[:], ALU.mult, ALU.add, [yk, ak, 'consts'], [ak])
                self.stt('pool', acc[:], self.yext[:, j, 0:TT], cw[:, j, 0:1], acc[:], ALU.mult, ALU.add, [yk, ak, 'consts'], [ak])
                self.act(sg[:], pg[:, :TT], AF.Silu, [pgk], [gk])
                self.tt('dve', acc[:], pb[:, :TT], acc[:], ALU.mult, [pbk, ak], [ak])
                self.tt('pool', self.yTt[:, j, :], acc[:], sg[:], ALU.mult, [ak, gk], ['yT'])

        self.run_layer(li, 'conv', TT, 4 * D, setup, tile, is_last)

    def gmlp_layer(self, li, is_last):
        TT = 512

        def setup():
            p = self.p
            self.wsT = self.lsb("wsT", [128, 8, 128], F32)
            self.bs_bc = self.lsb("bs_bc", [128, 8, TT], F32)
            self.vg_bc = self.lsb("vg_bc", [128, D], F32)
            self.vn = [self.lsb(f"vn{i}", [128, D], F32) for i in range(TT // 128)]
            self.vss = self.lsb("vss", [128, 4], F32)
            self.junk = self.lsb("junk", [128, 512], F32)
            self.s_sb = [self.lsb(f"s_sb{i}", [128, TT], F32) for i in range(2)]
            self.sg = [self.lsb(f"sg{i}", [128, TT], F32) for i in range(2)]
            p.dma('sp', self.wsT[:], self.inputs['gmlp_wsT'], [], ['wsT'])
            for g in range(8):
                p.op('pool', lambda e: e.affine_select(out=self.wsT[:, g, :], in_=self.wsT[:, g, :], pattern=[[1, 128]],
                                                       compare_op=ALU.is_ge, fill=0.0, base=0, channel_multiplier=-1),
                     ['wsT'], ['wsT'])
            for r in range(TT // 128):
                p.dma('sp', self.bs_bc[:, :, r * 128:(r + 1) * 128],
                      self.inputs['gmlp_bs'].partition_broadcast(128), [], ['bs_bc'])
            p.dma('sp', self.vg_bc[:], self.inputs['gmlp_vg'].partition_broadcast(128), [], ['vg_bc'])

        def tile(ti):
            W = self.W_in
            nblk = TT // 128
            for blk in range(nblk):
                vn = self.vn[blk]
                vk = f"vn{blk}"
                halves = []
                for hf in range(2):
                    ps, pk = self.next_ps()
                    for kc in range(NKC):
                        self.mm(ps[:, :512], self.hn[:, kc, 1 + blk * 128:1 + (blk + 1) * 128],
                                W[:, kc, D + hf * 512:D + (hf + 1) * 512], kc == 0, kc == NKC - 1, ['W_in', 'hn'], [pk])
                    halves.append((ps, pk))
                for hf, (ps, pk) in enumerate(halves):
                    self.act(self.junk[:], ps[:, :512], AF.Square, [pk], ['junk', 'vss'], accum_out=self.vss[:, hf:hf + 1])
                self.tt('dve', self.vss[:, 2:3], self.vss[:, 0:1], self.vss[:, 1:2], ALU.add, ['vss'], ['vss'])
                self.act(self.vss[:, 3:4], self.vss[:, 2:3], AF.Ln, ['vss', 'consts'], ['vss'], bias=self.epsc[:, 0:1], scale=1.0 / D)
                self.act(self.vss[:, 3:4], self.vss[:, 3:4], AF.Exp, ['vss'], ['vss'], scale=-0.5)
                for hf, (ps, pk) in enumerate(halves):
                    self.stt('dve', vn[:, hf * 512:(hf + 1) * 512], ps[:, :512], self.vss[:, 3:4],
                             self.vg_bc[:, hf * 512:(hf + 1) * 512], ALU.mult, ALU.mult, [pk, 'vss', 'vg_bc'], [vk])
            for j in range(NKC):
                s_sb, sg = self.s_sb[j % 2], self.sg[j % 2]
                sk, gk = f"s_sb{j % 2}", f"sg{j % 2}"
                ps, pk = self.next_ps()
                for blk in range(nblk):
                    self.mm(ps[:, blk * 128:(blk + 1) * 128], self.vn[blk][:, j * 128:(j + 1) * 128], self.wsT[:, j, :], True, True,
                            [f"vn{blk}", 'wsT'], [pk])
                self.tt('dve', s_sb[:], ps[:, :TT], self.bs_bc[:, j, :], ALU.add, [pk, 'bs_bc'], [sk])
                pu, puk = self.next_ps()
                for kc in range(NKC):
                    self.mm(pu[:, :TT], W[:, kc, j * 128:(j + 1) * 128], self.hn[:, kc, 1:TT + 1], kc == 0, kc == NKC - 1,
                            ['W_in', 'hn'], [puk])
                pg, pgk = self.next_ps()
                for kc in range(NKC):
                    self.mm(pg[:, :TT], W[:, kc, 2 * D + j * 128:2 * D + (j + 1) * 128], self.hn[:, kc, 1:TT + 1], kc == 0,
                            kc == NKC - 1, ['W_in', 'hn'], [pgk])
                self.act(sg[:], pg[:, :TT], AF.Silu, [pgk], [gk])
                self.tt('dve', s_sb[:], pu[:, :TT], s_sb[:], ALU.mult, [puk, sk], [sk])
                self.tt('pool', self.yTt[:, j, :], s_sb[:], sg[:], ALU.mult, [sk, gk], ['yT'])

        self.run_layer(li, 'gmlp', TT, 3 * D, setup, tile, is_last)


    def make_ident(self, ident, key):
        p = self.p
        p.op('pool', lambda e: e.memset(ident[:], 1.0), [], [key])
        p.op('pool', lambda e: e.affine_select(out=ident[:], in_=ident[:], pattern=[[-1, 128]], compare_op=ALU.is_equal,
                                               fill=0.0, base=0, channel_multiplier=1), [key], [key])

    def make_block_masks(self, C, maskT, colmask, rowmask, strict=False):
        p = self.p
        nch = 128 // C
        if maskT is not None:
            p.op('pool', lambda e: e.memset(maskT[:], 1.0), [], ['masks'])
            p.op('pool', lambda e: e.affine_select(out=maskT[:], in_=maskT[:], pattern=[[1, 128]], compare_op=ALU.is_ge if not strict else ALU.is_gt,
                                                   fill=0.0, base=0, channel_multiplier=-1), ['masks'], ['masks'])
            for c in range(1, nch):
                p.op('pool', lambda e, c=c: e.affine_select(out=maskT[:, c * C:(c + 1) * C], in_=maskT[:, c * C:(c + 1) * C], pattern=[[0, C]],
                                                            compare_op=ALU.is_ge, fill=0.0, base=-c * C, channel_multiplier=1), ['masks'], ['masks'])
        if colmask is not None:
            p.op('pool', lambda e: e.memset(colmask[:], 0.0), [], ['masks'])
            for c in range(nch):
                p.op('pool', lambda e, c=c: e.memset(colmask[:, c, c * C:(c + 1) * C], 1.0), ['masks'], ['masks'])
        if rowmask is not None:
            p.op('pool', lambda e: e.memset(rowmask[:], 1.0), [], ['masks'])
            for c in range(nch):
                p.op('pool', lambda e, c=c: e.affine_select(out=rowmask[:, c:c + 1], in_=rowmask[:, c:c + 1], pattern=[[0, 1]],
                                                            compare_op=ALU.is_ge, fill=0.0, base=-c * C, channel_multiplier=1), ['masks'], ['masks'])
                p.op('pool', lambda e, c=c: e.affine_select(out=rowmask[:, c:c + 1], in_=rowmask[:, c:c + 1], pattern=[[0, 1]],
                                                            compare_op=ALU.is_ge, fill=0.0, base=c * C + C - 1, channel_multiplier=-1), ['masks'], ['masks'])

    def hgrn_layer(self, li, is_last):
        TT = 256
        C = 32
        NB = TT // 128
        NCH = TT // C

        def setup():
            p = self.p
            L = self.lsb
            self.ident = L("ident", [128, 128], F32)
            self.make_ident(self.ident, 'ident')
            self.maskT = L("maskT", [128, 128], F32)
            self.colmask = L("colmask", [128, 4, 128], F32)
            self.rowmask = L("rowmask", [128, 4], F32)
            self.make_block_masks(C, self.maskT, self.colmask, self.rowmask)
            self.resetm = L("resetm", [128, TT], F32)
            p.op('pool', lambda e: e.memset(self.resetm[:], 1.0), [], ['masks'])
            p.op('pool', lambda e: e.memset(self.resetm[:].rearrange("p (n c) -> p n c", c=C)[:, :, 0:1], 0.0), ['masks'], ['masks'])
            self.gn_bc = L("gn_bc", [128, D], F32)
            p.dma('sp', self.gn_bc[:], self.inputs['hgrn_gn_g'].partition_broadcast(128), [], ['gn_bc'])
            self.lbl = L("lbl", [128, 4, NKC], F32)
            self.lbt = L("lbt", [128, 4, NKC], F32)
            p.dma('sp', self.lbl[:], self.inputs['hgrn_lbl'], [], ['lbl'])
            self.act(self.lbl[:], self.lbl[:], AF.Exp, ['lbl'], ['lbl'])
            self.tt('dve', self.lbt[:, 0, :], self.lbl[:, 0, :], self.lbl[:, 1, :], ALU.add, ['lbl'], ['lbt'])
            self.tt('dve', self.lbt[:, 0, :], self.lbt[:, 0, :], self.lbl[:, 2, :], ALU.add, ['lbl', 'lbt'], ['lbt'])
            self.tt('dve', self.lbt[:, 0, :], self.lbt[:, 0, :], self.lbl[:, 3, :], ALU.add, ['lbl', 'lbt'], ['lbt'])
            p.op('dve', lambda e: e.reciprocal(out=self.lbt[:, 3, :], in_=self.lbt[:, 0, :]), ['lbt'], ['lbt'])
            p.op('dve', lambda e: e.memset(self.lbt[:, 1, :], 0.0), ['lbt'], ['lbt'])
            for i in range(1, li + 1):
                self.tt('dve', self.lbt[:, 1, :], self.lbt[:, 1, :], self.lbl[:, i, :], ALU.add, ['lbl', 'lbt'], ['lbt'])
            self.tt('dve', self.lbt[:, 1, :], self.lbt[:, 1, :], self.lbt[:, 3, :], ALU.mult, ['lbt'], ['lbt'])
            self.ts('dve', self.lbt[:, 2, :], self.lbt[:, 1, :], -1.0, 1.0, ALU.mult, ALU.add, ['lbt'], ['lbt'])
            self.S = L("S_hgrn", [128, NKC, 128], F32)
            p.op('pool', lambda e: e.memset(self.S[:], 0.0), [], [('S', j) for j in range(NKC)])
            names = ['f', 'kk', 'bb', 'qe', 'dd', 'sg']
            self.tmps = []
            for q in range(2):
                tm = {n: L(f"h_{n}{q}", [128, TT], F32) for n in names}
                tm['e1'] = tm['f']
                tm['ko'] = tm['dd']
                tm['ke_bf'] = L(f"h_ke_bf{q}", [128, TT], BF16)
                tm['qe_bf'] = L(f"h_qe_bf{q}", [128, TT], BF16)
                tm['kom'] = L(f"kom{q}", [128, 4, NB, 128], BF16)
                self.tmps.append(tm)
            self.sgate = [L(f"sgate{q}", [128, TT], BF16) for q in range(3)]
            self.qem = [L(f"qem{q}", [128, 4, TT], BF16) for q in range(3)]
            self.v_bf = [L(f"v_bf{q}", [128, NB, 128], BF16) for q in range(3)]
            self.attm = [L(f"attm{q}", [128, NB, 128], BF16) for q in range(3)]
            self.u_sb = [L(f"u_sb{q}", [128, NCH, 128], F32) for q in range(3)]
            self.dec = [L(f"dec{q}", [128, NCH], F32) for q in range(3)]
            self.S_all2 = [L(f"S_all{q}", [128, 5, 128], F32) for q in range(2)]
            self.S_bf2 = [L(f"S_bf{q}", [128, NCH, 128], BF16) for q in range(2)]
            self.on2 = [L(f"on{q}", [128, NB, 128], F32) for q in range(2)]
            self.oss2 = [L(f"oss{q}", [128, 2 * NB], F32) for q in range(2)]
            self.junk2 = [L(f"junk{q}", [128, 128], F32) for q in range(2)]
            self.ps_pool = [0, 1, 2, 3]
            self.nm_rr = 0

        def nm_ps():
            i = [4, 5][self.nm_rr % 2]
            self.nm_rr += 1
            return self.psums[i], f"ps{i}"

        def proj(col0):
            ps, pk = self.next_ps()
            for kc in range(NKC):
                self.mm(ps[:, :TT], self.W_in[:, kc, col0:col0 + 128], self.hn[:, kc, 1:TT + 1], kc == 0, kc == NKC - 1, ['W_in', 'hn'], [pk])
            return ps, pk

        def A_gen(j):
            q2 = j % 2
            t = self.tmps[q2]
            kom = t['kom']
            W = self.W_in
            q = j % 3
            lb, oml = self.lbt[:, 1, :], self.lbt[:, 2, :]
            sgate, qem, v_bf, attm, u_sb, dec = self.sgate[q], self.qem[q], self.v_bf[q], self.attm[q], self.u_sb[q], self.dec[q]
            ksg, kqem, kv, katt, ku, kdec = f'sgate{q}', f'qem{q}', f'v_bf{q}', f'attm{q}', f'u_sb{q}', f'dec{q}'
            pf, pfk = proj(D + j * 128)
            self.act(t['f'][:], pf[:, :TT], AF.Exp, [pfk], [f't_f{q2}'], scale=-1.0)
            self.ts('dve', t['f'][:], t['f'][:], 1.0, None, ALU.add, None, [f't_f{q2}'], [f't_f{q2}'])
            self.p.op('dve', lambda e: e.reciprocal(out=t['f'][:], in_=t['f'][:]), [f't_f{q2}'], [f't_f{q2}'], cost=0.45)
            self.ts('dve', t['f'][:], t['f'][:], oml[:, j:j + 1], lb[:, j:j + 1], ALU.mult, ALU.add, [f't_f{q2}', 'lbt'], [f't_f{q2}'])
            self.act(t['kk'][:], t['f'][:], AF.Identity, [f't_f{q2}'], [f't_kk{q2}'], scale=-1.0, bias=self.epsc[:, 2:3])
            self.act(t['dd'][:], t['f'][:], AF.Ln, [f't_f{q2}'], [f't_dd{q2}'])
            self.p.op('dve', lambda e: e.tensor_tensor_scan(out=t['bb'][:], data0=self.resetm[:], data1=t['dd'][:], initial=0.0,
                                                            op0=ALU.mult, op1=ALU.add), [f't_dd{q2}', 'masks'], [f't_bb{q2}'])
            yield
            pq, pqk = proj(j * 128)
            self.act(t['e1'][:], t['bb'][:], AF.Exp, [f't_bb{q2}'], [f't_f{q2}'])
            self.tt('dve', t['qe'][:], pq[:, :TT], t['e1'][:], ALU.mult, [pqk, f't_f{q2}'], [f't_qe{q2}'])
            self.copy('act', t['qe_bf'][:], t['qe'][:], [f't_qe{q2}'], [f't_qe_bf{q2}'])
            qe4 = t['qe'][:].rearrange("p (b t) -> p b t", t=128)
            for c in range(4):
                self.tt('pool' if c % 2 else 'dve', qem[:, c, :].rearrange("p (b t) -> p b t", t=128), qe4,
                        self.colmask[:, c:c + 1, :].to_broadcast([128, NB, 128]), ALU.mult, [f't_qe{q2}', 'masks'], [kqem])
            yield
            self.act(t['e1'][:], t['bb'][:], AF.Exp, [f't_bb{q2}'], [f't_f{q2}'], scale=-1.0)
            self.tt('pool', t['ke_bf'][:], t['kk'][:], t['e1'][:], ALU.mult, [f't_kk{q2}', f't_f{q2}'], [f't_ke_bf{q2}'])
            b3 = t['bb'][:].rearrange("p (n c) -> p n c", c=C)
            self.act(dec[:], b3[:, :, C - 1], AF.Exp, [f't_bb{q2}'], [kdec])
            self.tt('pool', t['dd'][:].rearrange("p (n c) -> p n c", c=C), b3[:, :, C - 1:C].to_broadcast([128, NCH, C]), b3, ALU.subtract,
                    [f't_bb{q2}'], [f't_dd{q2}'])
            self.act(t['dd'][:], t['dd'][:], AF.Exp, [f't_dd{q2}'], [f't_dd{q2}'])
            self.tt('dve', t['ko'][:], t['kk'][:], t['dd'][:], ALU.mult, [f't_kk{q2}', f't_dd{q2}'], [f't_dd{q2}'])
            pg, pgk = proj(3 * D + j * 128)
            self.act(t['sg'][:], pg[:, :TT], AF.Exp, [pgk], [f't_sg{q2}'], scale=-1.0)
            self.ts('dve', t['sg'][:], t['sg'][:], 1.0, None, ALU.add, None, [f't_sg{q2}'], [f't_sg{q2}'])
            self.p.op('dve', lambda e: e.reciprocal(out=t['sg'][:], in_=t['sg'][:]), [f't_sg{q2}'], [f't_sg{q2}'], cost=0.45)
            self.tt('dve', sgate[:], pg[:, :TT], t['sg'][:], ALU.mult, [pgk, f't_sg{q2}'], [ksg])
            yield
            pv, pvk = self.next_ps()
            for blk in range(NB):
                for kc in range(NKC):
                    self.mm(pv[:, blk * 128:(blk + 1) * 128], self.hn[:, kc, 1 + blk * 128:1 + (blk + 1) * 128],
                            W[:, kc, 2 * D + j * 128:2 * D + (j + 1) * 128], kc == 0, kc == NKC - 1, ['W_in', 'hn'], [pvk])
            self.copy('act', v_bf[:].rearrange("p b v -> p (b v)"), pv[:, :TT], [pvk], [kv])
            yield
            ps, pk = nm_ps()
            for blk in range(NB):
                cs = slice(blk * 128, (blk + 1) * 128)
                self.mm(ps[:, cs], t['ke_bf'][:, cs], t['qe_bf'][:, cs], True, True, [f't_ke_bf{q2}', f't_qe_bf{q2}'], [pk])
            self.tt('dve', attm[:], ps[:, :TT].rearrange("p (b t) -> p b t", t=128), self.maskT[:, None, :].to_broadcast([128, NB, 128]),
                    ALU.mult, [pk, 'masks'], [katt])
            ps, pk = nm_ps()
            for blk in range(NB):
                cs = slice(blk * 128, (blk + 1) * 128)
                self.p.op('pe', lambda e, ps=ps, cs=cs: e.transpose(ps[:, cs], t['ko'][:, cs], self.ident[:]), [f't_dd{q2}', 'ident'], [pk])
            for c in range(4):
                self.act(kom[:, c, :, :].rearrange("p b k -> p (b k)"), ps[:, :TT], AF.Copy, [pk, 'masks'], [f'kom{q2}'],
                         scale=self.rowmask[:, c:c + 1])
            yield
            for blk in range(NB):
                ps, pk = nm_ps()
                for c in range(4):
                    self.mm(ps[:, c * 128:(c + 1) * 128], kom[:, c, blk, :], v_bf[:, blk, :], True, True, [f'kom{q2}', kv], [pk])
                self.copy('act' if blk % 2 else 'dve', u_sb[:, blk * 4:(blk + 1) * 4, :].rearrange("p c v -> p (c v)"), ps[:, 0:512], [pk], [ku])
                if blk % 2:
                    yield

        def B_gen(j):
            P = self.psums
            q = j % 3
            sgate, qem, v_bf, attm, u_sb, dec = self.sgate[q], self.qem[q], self.v_bf[q], self.attm[q], self.u_sb[q], self.dec[q]
            ksg, kqem, kv, katt, ku, kdec = f'sgate{q}', f'qem{q}', f'v_bf{q}', f'attm{q}', f'u_sb{q}', f'dec{q}'
            kS = ('S', j)
            q2 = j % 2
            SA = self.S_all2[q2]
            S_bf, on, oss, junk = self.S_bf2[q2], self.on2[q2], self.oss2[q2], self.junk2[q2]
            kSA, kSbf, kon, koss, kjunk = f'S_all{q2}', f'S_bf{q2}', f'on{q2}', f'oss{q2}', f'junk{q2}'
            self.copy('pool', SA[:, 0, :], self.S[:, j, :], [kS], [kSA])
            for blk in range(NB):
                for c in range(4):
                    n = blk * 4 + c
                    self.stt('dve', SA[:, c + 1, :], SA[:, c, :], dec[:, n:n + 1], u_sb[:, n, :], ALU.mult, ALU.add, [kSA, kdec, ku], [kSA])
                self.copy('act', S_bf[:, blk * 4:(blk + 1) * 4, :].rearrange("p c v -> p (c v)"),
                          SA[:, 0:4, :].rearrange("p c v -> p (c v)"), [kSA], [kSbf])
                if blk < NB - 1:
                    self.copy('dve', SA[:, 0, :], SA[:, 4, :], [kSA], [kSA])
                yield
            self.copy('pool', self.S[:, j, :], SA[:, 4, :], [kSA], [kS])
            po, pok = P[6], 'ps6'
            for blk in range(NB):
                cs = slice(blk * 128, (blk + 1) * 128)
                self.mm(po[:, cs], attm[:, blk, :], v_bf[:, blk, :], True, False, [katt, kv], [pok])
                for c in range(4):
                    self.mm(po[:, cs], qem[:, c, cs], S_bf[:, blk * 4 + c, :], False, c == 3, [kqem, kSbf], [pok])
                if blk % 2:
                    yield
            for blk in range(NB):
                cs = slice(blk * 128, (blk + 1) * 128)
                self.act(junk[:], po[:, cs], AF.Square, [pok], [kjunk, koss], accum_out=oss[:, blk:blk + 1])
            self.act(oss[:, NB:2 * NB], oss[:, 0:NB], AF.Ln, [koss, 'consts'], [koss], bias=self.epsc[:, 0:1], scale=1.0 / 128)
            self.act(oss[:, NB:2 * NB], oss[:, NB:2 * NB], AF.Exp, [koss], [koss], scale=-0.5)
            self.tt('dve', on[:], po[:, :TT].rearrange("p (b v) -> p b v", v=128),
                    oss[:, NB:2 * NB, None].to_broadcast([128, NB, 128]), ALU.mult, [pok, koss], [kon])
            self.tt('pool', on[:], on[:], self.gn_bc[:, None, j * 128:(j + 1) * 128].to_broadcast([128, NB, 128]), ALU.mult,
                    [kon, 'gn_bc'], [kon])
            yield
            py, pyk = P[7], 'ps7'
            for blk in range(NB):
                cs = slice(blk * 128, (blk + 1) * 128)
                self.p.op('pe', lambda e, cs=cs, blk=blk: e.transpose(py[:, cs], on[:, blk, :], self.ident[:]), [kon, 'ident'], [pyk])
            self.tt('dve', self.yTt[:, j, :], py[:, :TT], sgate[:], ALU.mult, [pyk, ksg], ['yT'])
            yield

        def drive(gens):
            gens = [g for g in gens if g is not None]
            while gens:
                for g in list(gens):
                    try:
                        next(g)
                    except StopIteration:
                        gens.remove(g)

        def step(g):
            try:
                next(g)
                return True
            except StopIteration:
                return False

        def tile(ti):
            A = {0: A_gen(0), 1: A_gen(1)}
            while step(A[0]):
                step(A[1])
            for sl in range(NKC):
                must = [B_gen(sl)]
                if sl + 1 < NKC:
                    must.append(A[sl + 1])
                opt = None
                if sl + 2 < NKC:
                    A[sl + 2] = A_gen(sl + 2)
                    opt = A[sl + 2]
                while must:
                    for g in list(must):
                        if not step(g):
                            must.remove(g)
                    if opt is not None and not step(opt):
                        opt = None

        self.run_layer(li, 'hgrn', TT, 4 * D, setup, tile, is_last)
        self.ps_pool = list(range(8))

    def rwkv_layer(self, li, is_last):
        TT = 256
        NB = TT // 128
        WC = 3200
        NDT = self.neu_dt
        LC = -0.6065306597126334

        def setup():
            p = self.p
            L = self.lsb
            self.ident = L("ident", [128, 128], F32)
            self.make_ident(self.ident, 'ident')
            self.ident_n = L("ident_n", [128, 128], NDT)
            self.copy('dve', self.ident_n[:], self.ident[:], ['ident'], ['ident'])
            self.maskS = L("maskS", [128, 128], F32)
            self.maskI = L("maskI", [128, 128], F32)
            self.maskSL = L("maskSL", [128, 128], F32)
            for (m, pat, cm, cmp_) in ((self.maskS, 1, -1, ALU.is_gt), (self.maskI, 1, -1, ALU.is_ge), (self.maskSL, -1, 1, ALU.is_gt)):
                p.op('pool', lambda e, m=m: e.memset(m[:], 1.0), [], ['masks'])
                p.op('pool', lambda e, m=m, pat=pat, cm=cm, cmp_=cmp_: e.affine_select(
                    out=m[:], in_=m[:], pattern=[[pat, 128]], compare_op=cmp_, fill=0.0, base=0, channel_multiplier=cm), ['masks'], ['masks'])
            self.blockones = L("blockones", [128, 128], F32)
            p.op('pool', lambda e: e.memset(self.blockones[:], 1.0), [], ['masks'])
            p.op('pool', lambda e: e.memset(self.blockones[0:64, 64:128], 0.0), ['masks'], ['masks'])
            p.op('pool', lambda e: e.memset(self.blockones[64:128, 0:64], 0.0), ['masks'], ['masks'])
            self.resetm = L("resetm", [128, TT], F32)
            p.op('pool', lambda e: e.memset(self.resetm[:], 1.0), [], ['masks'])
            p.op('pool', lambda e: e.memset(self.resetm[:].rearrange("p (n c) -> p n c", c=128)[:, :, 0:1], 0.0), ['masks'], ['masks'])
            p.op('pool', lambda e: e.memset(self.epsc[:, 1:2], GN_EPS), [], ['consts'])
            self.mu_fm = L("mu_fm", [128, 33], F32)
            self.omu_fm = L("omu_fm", [128, 33], F32)
            p.dma('sp', self.mu_fm[:], self.inputs['rwkv_mu_fm'], [], ['rw_vecs'])
            self.ts('dve', self.omu_fm[:], self.mu_fm[:], -1.0, 1.0, ALU.mult, ALU.add, ['rw_vecs'], ['rw_vecs'])
            self.vecs = L("rw_vecs", [128, 5, NKC], F32)
            p.dma('sp', self.vecs[:], self.inputs['rwkv_vecs'], [], ['rw_vecs'])
            self.lw2 = L("lw2", [128, D], F32)
            p.dma('sp', self.lw2[:], self.inputs['rwkv_lw2'], [], ['lw2'])
            self.gng_bc = L("gng_bc", [128, D], BF16)
            self.gnb_bc = L("gnb_bc", [128, D], BF16)
            self.Wva = L("Wva", [128, NKC, D], BF16)
            self.Wvb = L("Wvb", [128, NKC, D], BF16)
            src = self.w_in_dram['rwkv']

            def loader(tes):
                muv = tes.enter_context(self.nc.sbuf_tensor("muv_bc", [128, D], F32))
                for (dst, nm) in ((self.gng_bc, 'rwkv_gn_g'), (self.gnb_bc, 'rwkv_gn_b')):
                    p.dma('sp', muv[:], self.inputs[nm].partition_broadcast(128), [], ['muv'])
                    self.copy('dve', dst[:], muv[:], ['muv'], ['bc_tiles'])
                p.dma('sp', muv[:], self.inputs['rwkv_mu'][2 * D:3 * D].partition_broadcast(128), ['muv'], ['bc_tiles', 'muv'])
                self.load_weight_bf16(self.Wvb, 'Wv', src, D, src_c0=2 * D, scale_bc=muv)
                self.ts('dve', muv[:], muv[:], -1.0, 1.0, ALU.mult, ALU.add, ['bc_tiles'], ['bc_tiles'])
                self.load_weight_bf16(self.Wva, 'Wv', src, D, src_c0=2 * D, scale_bc=muv)
                self.load_weight_bf16(self.W_in, 'W_in', src, 2 * D, src_c0=0, dst_c0=0)
                self.load_weight_bf16(self.W_in, 'W_in', src, 128, src_c0=3 * D, dst_c0=2 * D)
                self.load_weight_bf16(self.W_in, 'W_in', src, D, src_c0=3 * D + 128, dst_c0=2 * D + 128)
                self.load_weight_bf16(self.W_out, 'W_out', self.w_out_dram['rwkv'], D)
            self.S = L("S_rwkv", [128, NKC, 64], F32)
            p.op('pool', lambda e: e.memset(self.S[:], 0.0), [], [('S', j) for j in range(NKC)])
            self.pcar = L("pcar", [128, 25], F32)
            p.op('pool', lambda e: e.memset(self.pcar[:], 0.0), [], [('pcar', i) for i in range(25)])
            self.pm_ext = [L(f"pm_ext{i}", [128, TT + 1], F32) for i in range(2)]
            self.pm_i = 0
            names = ['lo', 'r', 'k', 'tmp', 'sigw', 'a', 'kk', 'rn', 'kmod', 'bbv', 'c', 'e1', 'e2']
            self.tmp = {n: L(f"w_{n}", [128, TT], F32) for n in names}
            self.tmp2 = [dict(), dict()]
            for n in ['khat', 'bhat']:
                self.tmp2[0][n] = L(f"w_{n}0", [128, TT], F32)
            for n in ['rt_bf', 'bt_bf', 'at_bf', 'kt_h0', 'kt_h1', 'bt_h0', 'bt_h1', 'at_h0', 'at_h1']:
                self.tmp2[0][n] = L(f"w_{n}0", [128, TT], BF16)
            self.hm = L("hm", [128, 2], F32)
            p.op('pool', lambda e: e.memset(self.hm[:], 0.0), [], ['masks'])
            p.op('pool', lambda e: e.memset(self.hm[0:64, 0:1], 1.0), ['masks'], ['masks'])
            p.op('pool', lambda e: e.memset(self.hm[64:128, 1:2], 1.0), ['masks'], ['masks'])
            self.pt = {n: [L(f"wp_{n}{q}", [128, TT], F32 if n in ('at', 'rt') else BF16) for q in range(2)] for n in ['at', 'rt', 'rkr', 'sgate']}
            self.blockones_bf = L("blockones_bf", [128, 128], BF16)
            self.copy('dve', self.blockones_bf[:], self.blockones[:], ['masks'], ['masks'])
            self.v_sb = [L(f"v_sb{q}", [128, NB, 128], F32) for q in range(2)]
            self.v_bf = [L(f"v_bf{q}", [128, NB, 128], BF16) for q in range(2)]
            self.dec = [L(f"dec{q}", [128, NB], F32) for q in range(3)]

            def post_setup():
                for n in ['khat', 'bhat']:
                    self.tmp2[1][n] = L(f"w_{n}1", [128, TT], F32)
                for n in ['rt_bf', 'bt_bf', 'at_bf', 'kt_h0', 'kt_h1', 'bt_h0', 'bt_h1', 'at_h0', 'at_h1']:
                    self.tmp2[1][n] = L(f"w_{n}1", [128, TT], BF16)
                for n in ['at', 'rt', 'rkr', 'sgate']:
                    self.pt[n].append(L(f"wp_{n}2", [128, TT], F32 if n in ('at', 'rt') else BF16))
                self.ysb = [L(f"ysb{i}", [128, 256], F32) for i in range(2)]
                self.yn2 = [self.yn, L("yn1", [128, 128], F32)]
                self.bon2 = [self.bon, L("bon1", [128, 128], F32)]
                self.gst2 = [self.gst, L("gst1", [128, 12], F32)]
                self.junk2 = [self.junk, L("junk1", [128, 64], F32)]
                self.bcount = 0
                self.v_sb.append(L("v_sb2", [128, NB, 128], F32))
                self.v_bf.append(L("v_bf2", [128, NB, 128], BF16))
            self.post_setup = post_setup
            NCHN = NB * 2
            self.Pb = [L(f"Pb{i}", [128, NCHN, 128], NDT) for i in range(2)]
            self.Qb = [L(f"Qb{i}", [128, NCHN, 128], NDT) for i in range(2)]
            self.NT = [L(f"NT{q}", [128, NCHN, 128], NDT) for q in range(2)]
            self.Aak = [L(f"Aak{q}", [128, NCHN, 128], BF16) for q in range(2)]
            self.Ark = [L(f"Ark{q}", [128, NCHN, 128], BF16) for q in range(2)]
            self.Arb = [L(f"Arb{q}", [128, NCHN, 128], BF16) for q in range(2)]
            self.ident4 = L("ident4", [128, NCHN, 128], NDT)
            for c in range(NCHN):
                self.copy('dve', self.ident4[:, c, :], self.ident[:], ['ident'], ['ident'])
            self.khm = [[L(f"khm{q}{b}", [128, 128], BF16) for b in range(NB)] for q in range(2)]
            self.bhm = [[L(f"bhm{q}{b}", [128, 128], BF16) for b in range(NB)] for q in range(2)]
            self.Z_sb = L("Z_sb", [128, 128], NDT)
            self.U_bf = L("U_bf", [128, 128], BF16)
            self.yn = L("yn", [128, 128], F32)
            self.bon = L("bon", [128, 128], F32)
            self.gst = L("gst", [128, 12], F32)
            self.junk = L("junk", [128, 64], F32)
            self.ps_pool = [0, 1, 2]
            self.nm_rr = 0
            return loader

        NMB = [3, 4, 7]

        def nm_ps():
            i = NMB[self.nm_rr % len(NMB)]
            self.nm_rr += 1
            return self.psums[i], f"ps{i}"

        def shift(ps, pk, dst, dk, idx, mt):
            pm = self.pm_ext[self.pm_i % 2]
            pmk = f"pm_ext{self.pm_i % 2}"
            self.pm_i += 1
            ck = ('pcar', idx)
            self.copy('pool', pm[:, 0:1], self.pcar[:, idx:idx + 1], [ck], [pmk])
            self.act(pm[:, 1:TT + 1], ps[:, :TT], AF.Copy, [pk, 'rw_vecs'], [pmk], scale=self.mu_fm[:, mt:mt + 1])
            self.copy('pool', self.pcar[:, idx:idx + 1], pm[:, TT:TT + 1], [pmk], [ck])
            self.act(dst, ps[:, :TT], AF.Copy, [pk, 'rw_vecs'], [dk], scale=self.omu_fm[:, mt:mt + 1])
            self.tt('dve', dst, dst, pm[:, 0:TT], ALU.add, [dk, pmk], [dk])

        def proj(col0):
            ps, pk = self.next_ps()
            for kc in range(NKC):
                self.mm(ps[:, :TT], self.W_in[:, kc, col0:col0 + 128], self.hn[:, kc, 1:TT + 1], kc == 0, kc == NKC - 1, ['W_in', 'hn'], [pk])
            return ps, pk

        hsl = [slice(0, 64), slice(64, 128)]

        def A_gen(j):
            V = self.vecs
            q = j % 2
            q3 = j % 3
            t = dict(self.tmp)
            t.update(self.tmp2[q])
            PAR = set(self.tmp2[0].keys())
            jc = slice(j * 128, (j + 1) * 128)
            at, rt, rkr, sgate = (self.pt[n][q3] for n in ('at', 'rt', 'rkr', 'sgate'))
            kat, krt, krkr, ksg = (f'p_{n}{q3}' for n in ('at', 'rt', 'rkr', 'sgate'))
            v_sb, v_bf, dec = self.v_sb[q3], self.v_bf[q3], self.dec[q3]
            kv, kvb, kdec = f'v_sb{q3}', f'v_bf{q3}', f'dec{q3}'
            ps, pk = proj(j * 128)
            shift(ps, pk, t['r'][:], 't_r', j, j)
            ps, pk = proj(D + j * 128)
            shift(ps, pk, t['k'][:], 't_k', 8 + j, 8 + j)
            yield
            ps, pk = proj(2 * D + 128 + j * 128)
            shift(ps, pk, t['tmp'][:], 't_tmp', 16 + j, 25 + j)
            self.act(sgate[:], t['tmp'][:], AF.Silu, ['t_tmp'], [ksg])
            pv, pvk = self.next_ps()
            for blk in range(NB):
                n = 0
                for kc in range(NKC):
                    for (Wv, off) in ((self.Wva, 1), (self.Wvb, 0)):
                        self.mm(pv[:, blk * 128:(blk + 1) * 128], self.hn[:, kc, off + blk * 128:off + (blk + 1) * 128], Wv[:, kc, jc],
                                n == 0, n == 2 * NKC - 1, ['Wv', 'hn'], [pvk])
                        n += 1
            self.copy('act', v_sb[:].rearrange("p b v -> p (b v)"), pv[:, :TT], [pvk], [kv])
            self.copy('dve', v_bf[:].rearrange("p b v -> p (b v)"), pv[:, :TT], [pvk], [kvb])
            yield
            pw, pwk = self.next_ps()
            self.mm(pw[:, :TT], self.lw2[0:64, jc], t['lo'][0:64, :], True, True, ['lw2', 't_lo'], [pwk])
            self.act(t['sigw'][:], pw[:, :TT], AF.Sigmoid, [pwk, 'rw_vecs'], ['t_sigw'], bias=V[:, 0, j:j + 1])
            pa, pak = self.next_ps()
            self.mm(pa[:, :TT], self.lw2[64:128, jc], t['lo'][64:128, :], True, True, ['lw2', 't_lo'], [pak])
            self.act(t['a'][:], pa[:, :TT], AF.Sigmoid, [pak, 'rw_vecs'], ['t_a'], bias=V[:, 1, j:j + 1])
            self.ts('dve', t['kk'][:], t['k'][:], V[:, 2, j:j + 1], None, ALU.mult, None, ['t_k', 'rw_vecs'], ['t_kk'])
            self.tt('pool', t['tmp'][:], t['kk'][:], t['kk'][:], ALU.mult, ['t_kk'], ['t_tmp'])
            pn, pnk = self.next_ps()
            self.mm(pn[:, :TT], self.blockones[:], t['tmp'][:], True, True, ['masks', 't_tmp'], [pnk])
            self.ts('dve', t['rn'][:], pn[:, :TT], 1e-24, None, ALU.max, None, [pnk], ['t_rn'])
            self.act(t['rn'][:], t['rn'][:], AF.Ln, ['t_rn'], ['t_rn'])
            self.act(t['rn'][:], t['rn'][:], AF.Exp, ['t_rn'], ['t_rn'], scale=-0.5)
            self.tt('pool', t['kk'][:], t['kk'][:], t['rn'][:], ALU.mult, ['t_kk', 't_rn'], ['t_kk'])
            self.ts('dve', t['tmp'][:], t['a'][:], -1.0, V[:, 3, j:j + 1], ALU.add, ALU.mult, ['t_a', 'rw_vecs'], ['t_tmp'])
            self.stt('dve', t['kmod'][:], t['tmp'][:], 1.0, t['k'][:], ALU.add, ALU.mult, ['t_tmp', 't_k'], ['t_kmod'])
            self.tt('pool', t['bbv'][:], t['kk'][:], t['a'][:], ALU.mult, ['t_kk', 't_a'], ['t_bbv'])
            yield
            self.p.op('dve', lambda e: e.tensor_tensor_scan(out=t['c'][:], data0=self.resetm[:], data1=t['sigw'][:], initial=0.0,
                                                            op0=ALU.mult, op1=ALU.add), ['t_sigw', 'masks'], ['t_c'])
            self.act(t['e1'][:], t['c'][:], AF.Exp, ['t_c'], ['t_e1'], scale=LC)
            self.tt('pool', rt[:], t['r'][:], t['e1'][:], ALU.mult, ['t_r', 't_e1'], [krt])
            self.copy('act', t['rt_bf'][:], rt[:], [krt], [f't_rt_bf{q}'])
            self.act(t['e2'][:], t['c'][:], AF.Exp, ['t_c'], ['t_e2'], scale=-LC)
            for hd in range(2):
                self.stt('dve', t[f'kt_h{hd}'][:], t['kmod'][:], self.hm[:, hd:hd + 1], t['e2'][:], ALU.mult, ALU.mult,
                         ['t_kmod', 't_e2', 'masks'], [f't_kt_h{hd}_{q}'])
                self.stt('dve', t[f'bt_h{hd}'][:], t['bbv'][:], self.hm[:, hd:hd + 1], t['e2'][:], ALU.mult, ALU.mult,
                         ['t_bbv', 't_e2', 'masks'], [f't_bt_h{hd}_{q}'])
            self.tt('pool', t['bt_bf'][:], t['bbv'][:], t['e2'][:], ALU.mult, ['t_bbv', 't_e2'], [f't_bt_bf{q}'])
            self.tt('pool', t['e1'][:], t['c'][:], t['sigw'][:], ALU.subtract, ['t_c', 't_sigw'], ['t_e1'])
            self.act(t['e1'][:], t['e1'][:], AF.Exp, ['t_e1'], ['t_e1'], scale=LC)
            self.stt('dve', at[:], t['kk'][:], -1.0, t['e1'][:], ALU.mult, ALU.mult, ['t_kk', 't_e1'], [kat])
            self.copy('act', t['at_bf'][:], at[:], [kat], [f't_at_bf{q}'])
            for hd in range(2):
                self.act(t[f'at_h{hd}'][:], at[:], AF.Copy, [kat, 'masks'], [f't_at_h{hd}_{q}'], scale=self.hm[:, hd:hd + 1])
            yield
            c3 = t['c'][:].rearrange("p (n c) -> p n c", c=128)
            self.tt('pool', t['e2'][:].rearrange("p (n c) -> p n c", c=128), c3[:, :, 127:128].to_broadcast([128, NB, 128]), c3,
                    ALU.subtract, ['t_c'], ['t_e2'])
            self.act(t['e2'][:], t['e2'][:], AF.Exp, ['t_e2'], ['t_e2'], scale=LC)
            self.act(dec[:], c3[:, :, 127], AF.Exp, ['t_c'], [kdec], scale=LC)
            self.tt('pool', t['khat'][:], t['kmod'][:], t['e2'][:], ALU.mult, ['t_kmod', 't_e2'], [f't_khat{q}'])
            self.tt('dve', t['bhat'][:], t['bbv'][:], t['e2'][:], ALU.mult, ['t_bbv', 't_e2'], [f't_bhat{q}'])
            self.stt('dve', rkr[:], t['r'][:], V[:, 4, j:j + 1], t['kmod'][:], ALU.mult, ALU.mult, ['t_r', 'rw_vecs', 't_kmod'], [krkr])
            yield
            NCH = NB * 2
            specs = {'P': ('bt_h', 'at_bf', self.maskS, self.Pb[0], 'Pb0'),
                     'Q': ('at_h', 'bt_bf', self.maskSL, self.Qb[0], 'Qb0'),
                     'ak': ('kt_h', 'at_bf', self.maskS, self.Aak[q], f'Aak{q}'),
                     'rk': ('kt_h', 'rt_bf', self.maskI, self.Ark[q], f'Ark{q}'),
                     'rb': ('bt_h', 'rt_bf', self.maskI, self.Arb[q], f'Arb{q}')}
            for name in ('P', 'Q', 'ak', 'rk', 'rb'):
                lh, rh, mask, dst, dk = specs[name]
                ps, pk = nm_ps()
                for c in range(NCH):
                    blk, hd = c // 2, c % 2
                    cs = slice(blk * 128, (blk + 1) * 128)
                    self.mm(ps[:, c * 128:(c + 1) * 128], t[f'{lh}{hd}'][:, cs], t[rh][:, cs], True, True, [f't_{lh}{hd}_{q}', f't_{rh}{q}'], [pk])
                self.tt('dve', dst[:], ps[:, 0:NCH * 128].rearrange("p (c t) -> p c t", c=NCH),
                        mask[:, None, :].to_broadcast([128, NCH, 128]), ALU.mult, [pk, 'masks'], [dk])
                if name == 'Q':
                    self.tt('pool', self.NT[q][:], self.ident4[:], self.Pb[0][:], ALU.add, ['ident', 'Pb0'], [f'NT{q}'])
                    yield
            for blk in range(NB):
                cs = slice(blk * 128, (blk + 1) * 128)
                for (srcn, dst, dk) in (('khat', self.khm[q][blk], f'khm{q}{blk}'), ('bhat', self.bhm[q][blk], f'bhm{q}{blk}')):
                    ps, pk = nm_ps()
                    self.p.op('pe', lambda e, ps=ps, srcn=srcn, cs=cs: e.transpose(ps[:, 0:128], t[srcn][:, cs], self.ident[:]),
                              [f't_{srcn}{q}', 'ident'], [pk])
                    self.copy('act', dst[:], ps[:, 0:128], [pk], [dk])
            yield
            NTq, kNT = self.NT[q], f'NT{q}'
            for i in range(6):
                a_, b_ = i % 2, (i + 1) % 2
                Pa, Qa, Pn, Qn = self.Pb[a_], self.Qb[a_], self.Pb[b_], self.Qb[b_]
                kPa, kQa, kPn, kQn = f'Pb{a_}', f'Qb{a_}', f'Pb{b_}', f'Qb{b_}'
                if i < 5:
                    ps, pk = nm_ps()
                    for c in range(NCH):
                        self.mm(ps[:, c * 128:(c + 1) * 128], Qa[:, c, :], Pa[:, c, :], True, True, [kQa, kPa], [pk])
                    self.copy('act', Pn[:].rearrange("p c t -> p (c t)"), ps[:, 0:NCH * 128], [pk], [kPn])
                ps, pk = nm_ps()
                for c in range(NCH):
                    self.mm(ps[:, c * 128:(c + 1) * 128], Pa[:, c, :], Qa[:, c, :], True, True, [kPa, kQa], [pk])
                self.copy('dve' if i % 2 == 0 else 'act', Qn[:].rearrange("p c t -> p (c t)"), ps[:, 0:NCH * 128], [pk], [kQn])
                yield
                ps, pk = nm_ps()
                for c in range(NCH):
                    self.mm(ps[:, c * 128:(c + 1) * 128], Qn[:, c, :], NTq[:, c, :], True, True, [kQn, kNT], [pk])
                self.tt('dve', NTq[:].rearrange("p c t -> p (c t)"), NTq[:].rearrange("p c t -> p (c t)"), ps[:, 0:NCH * 128], ALU.add,
                        [kNT, pk], [kNT])
                yield

        def B_gen(j):
            t = self.tmp
            P = self.psums
            q = j % 2
            q3 = j % 3
            jc = slice(j * 128, (j + 1) * 128)
            at, rt, rkr, sgate = (self.pt[n][q3] for n in ('at', 'rt', 'rkr', 'sgate'))
            kat, krt, krkr, ksg = (f'p_{n}{q3}' for n in ('at', 'rt', 'rkr', 'sgate'))
            v_sb, v_bf, dec = self.v_sb[q3], self.v_bf[q3], self.dec[q3]
            kv, kvb, kdec = f'v_sb{q3}', f'v_bf{q3}', f'dec{q3}'
            kS = ('S', j)
            for blk in range(NB):
                cs = slice(blk * 128, (blk + 1) * 128)
                khm, bhm = self.khm[q][blk], self.bhm[q][blk]
                kkh, kbh = f'khm{q}{blk}', f'bhm{q}{blk}'
                pz, pzk = P[5], 'ps5'
                for hd in range(2):
                    hs, hc, c = hsl[hd], slice(hd * 64, (hd + 1) * 64), blk * 2 + hd
                    self.mm(pz[:, hc], self.Aak[q][:, c, :], v_bf[:, blk, hc], True, False, [f'Aak{q}', kvb], [pzk])
                    self.mm(pz[:, hc], at[hs, cs], self.S[hs, j, :], False, True, [kat, kS], [pzk])
                self.copy('act', self.Z_sb[:], pz[:, 0:128], [pzk], ['Z_sb'])
                yield
                for hd in range(2):
                    hc, c = slice(hd * 64, (hd + 1) * 64), blk * 2 + hd
                    self.mm(pz[:, hc], self.NT[q][:, c, :], self.Z_sb[:, hc], True, True, [f'NT{q}', 'Z_sb'], [pzk])
                self.copy('act', self.U_bf[:], pz[:, 0:128], [pzk], ['U_bf'])
                yield
                self.mm(pz[:, 0:128], khm[:], v_bf[:, blk, :], True, False, [kkh, kvb], [pzk])
                self.mm(pz[:, 0:128], bhm[:], self.U_bf[:], False, True, [kbh, 'U_bf'], [pzk])
                py, pyk = P[6], 'ps6'
                for hd in range(2):
                    hs, hc, c = hsl[hd], slice(hd * 64, (hd + 1) * 64), blk * 2 + hd
                    yc = slice(hd * 128, hd * 128 + 64)
                    bc_ = slice(hd * 128 + 64, hd * 128 + 128)
                    self.mm(py[:, yc], self.Ark[q][:, c, :], v_bf[:, blk, hc], True, False, [f'Ark{q}', kvb], [pyk])
                    self.mm(py[:, yc], rt[hs, cs], self.S[hs, j, :], False, False, [krt, kS], [pyk])
                    self.mm(py[:, yc], self.Arb[q][:, c, :], self.U_bf[:, hc], False, True, [f'Arb{q}', 'U_bf'], [pyk])
                    self.mm(py[:, bc_], rkr[hs, cs], self.blockones_bf[hs, hs], True, True, [krkr, 'masks'], [pyk])
                for hd in range(2):
                    hs, hc = hsl[hd], slice(hd * 64, (hd + 1) * 64)
                    self.stt('dve', self.S[hs, j, :], self.S[hs, j, :], dec[hs, blk:blk + 1], pz[hs, hc], ALU.mult, ALU.add,
                             [kS, kdec, pzk], [kS])
                yield
                bp = self.bcount % 2
                self.bcount += 1
                ysb, kys = self.ysb[bp], f'ysb{bp}'
                g, kg = self.gst2[bp], f'gst{bp}'
                yn, kyn = self.yn2[bp], f'yn{bp}'
                bon, kbon = self.bon2[bp], f'bon{bp}'
                junk, kjunk = self.junk2[bp], f'junk{bp}'
                self.copy('act', ysb[:], py[:, 0:256], [pyk], [kys])
                for hd in range(2):
                    hc = slice(hd * 64, (hd + 1) * 64)
                    yc = slice(hd * 128, hd * 128 + 64)
                    bc_ = slice(hd * 128 + 64, hd * 128 + 128)
                    self.act(junk[:], ysb[:, yc], AF.Identity, [kys], [kjunk, kg], accum_out=g[:, hd:hd + 1])
                    self.act(junk[:], ysb[:, yc], AF.Square, [kys], [kjunk, kg], accum_out=g[:, 2 + hd:3 + hd])
                    self.tt('pool', bon[:, hc], ysb[:, bc_], v_sb[:, blk, hc], ALU.mult, [kys, kv], [kbon])
                self.ts('dve', g[:, 4:6], g[:, 0:2], 1.0 / 64, None, ALU.mult, None, [kg], [kg])
                self.tt('dve', g[:, 6:8], g[:, 4:6], g[:, 4:6], ALU.mult, [kg], [kg])
                self.stt('dve', g[:, 8:10], g[:, 2:4], 1.0 / 64, g[:, 6:8], ALU.mult, ALU.subtract, [kg], [kg])
                self.act(g[:, 8:10], g[:, 8:10], AF.Ln, [kg, 'consts'], [kg], bias=self.epsc[:, 1:2])
                self.act(g[:, 8:10], g[:, 8:10], AF.Exp, [kg], [kg], scale=-0.5)
                for hd in range(2):
                    hc = slice(hd * 64, (hd + 1) * 64)
                    yc = slice(hd * 128, hd * 128 + 64)
                    self.ts('dve', yn[:, hc], ysb[:, yc], g[:, 4 + hd:5 + hd], g[:, 8 + hd:9 + hd], ALU.subtract, ALU.mult,
                            [kys, kg], [kyn])
                yield
                self.tt('pool', yn[:], yn[:], self.gng_bc[:, jc], ALU.mult, [kyn, 'bc_tiles'], [kyn])
                self.tt('pool', yn[:], yn[:], self.gnb_bc[:, jc], ALU.add, [kyn, 'bc_tiles'], [kyn])
                self.tt('pool', yn[:], yn[:], bon[:], ALU.add, [kyn, kbon], [kyn])
                ps, pk = nm_ps()
                self.p.op('pe', lambda e, ps=ps, yn=yn: e.transpose(ps[:, 0:128], yn[:], self.ident[:]), [kyn, 'ident'], [pk])
                self.tt('dve', self.yTt[:, j, cs], ps[:, 0:128], sgate[:, cs], ALU.mult, [pk, ksg], ['yT'])
                yield

        def drive(gens):
            gens = [g for g in gens if g is not None]
            while gens:
                for g in list(gens):
                    try:
                        next(g)
                    except StopIteration:
                        gens.remove(g)

        def tile(ti):
            t = self.tmp
            ps, pk = proj(2 * D)
            shift(ps, pk, t['lo'][:], 't_lo', 24, 24)
            self.act(t['lo'][0:64, :], t['lo'][0:64, :], AF.Tanh, ['t_lo'], ['t_lo'])
            for j in range(NKC):
                drive([A_gen(j)])
                drive([B_gen(j)])

        self.run_layer(li, 'rwkv', TT, WC, setup, tile, is_last)
        self.ps_pool = list(range(8))

    def build(self):
        nc = self.nc
        T = self.T
        self.xT = self.din("xT", [D, T])
        self.yT = nc.dram_tensor("yT", [D, T], F32, kind="ExternalOutput").ap()
        d_norm_g = self.din("norm_g", [128, 4, NKC])
        d_final_g = self.din("final_g", [128, NKC])
        self.w_in_dram, self.w_out_dram = {}, {}
        kinds = [k for (_, k) in self.layers]
        if 'conv' in kinds:
            self.w_in_dram['conv'] = self.din("conv_w_in", [D, 4 * D])
            self.w_out_dram['conv'] = self.din("conv_w_out", [D, D])
            d_conv_w = self.din("conv_w", [128, NKC, 3])
        if 'rwkv' in kinds:
            self.w_in_dram['rwkv'] = self.din("rwkv_w_in", [D, 4 * D + 128])
            self.w_out_dram['rwkv'] = self.din("rwkv_w_out", [D, D])
            self.din("rwkv_mu_fm", [128, 33])
            self.din("rwkv_mu", [4 * D + 128])
            self.din("rwkv_vecs", [128, 5, NKC])
            self.din("rwkv_lw2", [128, D])
            self.din("rwkv_gn_g", [D])
            self.din("rwkv_gn_b", [D])
        if 'hgrn' in kinds:
            self.w_in_dram['hgrn'] = self.din("hgrn_w_in", [D, 4 * D])
            self.w_out_dram['hgrn'] = self.din("hgrn_w_out", [D, D])
            self.din("hgrn_gn_g", [D])
            self.din("hgrn_lbl", [128, 4, NKC])
        if 'gmlp' in kinds:
            self.w_in_dram['gmlp'] = self.din("gmlp_w_in", [D, 3 * D])
            self.w_out_dram['gmlp'] = self.din("gmlp_w_out", [D, D])
            self.din("gmlp_wsT", [128, 8, 128])
            self.din("gmlp_bs", [8, 128])
            self.din("gmlp_vg", [D])
        with ExitStack() as es:
            self.es = es
            nc.allow_low_precision("bf16 matmul operands, fp32 accumulation")
            self.p = p = Prog(nc, es)
            self.psums = [es.enter_context(nc.psum_tensor(f"ps{i}", [128, 512], F32)) for i in range(8)]
            self.ps_rr = 0
            self.ps_pool = list(range(8))
            self.ones_bf = self.sb("ones_bf", [128, 128], BF16)
            self.epsc = self.sb("epsc", [128, 4], F32)
            self.norm_g = self.sb("norm_g_sb", [128, 4, NKC], F32)
            self.final_g = self.sb("final_g_sb", [128, NKC], F32)
            p.op('pool', lambda e: e.memset(self.ones_bf[:], 1.0), [], ['ones_bf'])
            p.op('pool', lambda e: e.memset(self.epsc[:, 0:1], RMS_EPS), [], ['consts'])
            p.op('pool', lambda e: e.memset(self.epsc[:, 2:3], 1.0), ['consts'], ['consts'])
            p.dma('sp', self.norm_g[:], d_norm_g, [], ['consts'])
            p.dma('sp', self.final_g[:], d_final_g, [], ['consts'])
            if 'conv' in kinds:
                self.conv_w = self.sb("conv_w_sb", [128, NKC, 3], F32)
                p.dma('sp', self.conv_w[:], d_conv_w, [], ['consts'])
            self.first_layer = True
            for n, (li, kind) in enumerate(self.layers):
                is_last = n == len(self.layers) - 1
                if kind == 'conv':
                    self.conv_layer(li, is_last)
                elif kind == 'gmlp':
                    self.gmlp_layer(li, is_last)
                elif kind == 'hgrn':
                    self.hgrn_layer(li, is_last)
                elif kind == 'rwkv':
                    self.rwkv_layer(li, is_last)
                else:
                    raise ValueError(kind)
            p.finish('sp')
            self.stats = (p.n_ins, p.n_wait)
        return nc


def prep_inputs(inp, b, layers):
    f = np.float32
    m = {}
    m["xT"] = np.ascontiguousarray(np.asarray(inp["x"][b], f).T)
    m["norm_g"] = np.ascontiguousarray(np.asarray(inp["norm_g"], f).reshape(4, NKC, 128).transpose(2, 0, 1))
    m["final_g"] = np.ascontiguousarray(np.asarray(inp["final_g"], f).reshape(NKC, 128).T)
    kinds = [k for (_, k) in layers]
    if 'conv' in kinds:
        m["conv_w_in"] = np.ascontiguousarray(np.asarray(inp["conv_w_in"][0], f))
        m["conv_w_out"] = np.ascontiguousarray(np.asarray(inp["conv_w_out"][0], f))
        m["conv_w"] = np.ascontiguousarray(np.asarray(inp["conv_w"][0], f).reshape(3, NKC, 128).transpose(2, 1, 0))
    if 'rwkv' in kinds:
        m["rwkv_w_in"] = np.ascontiguousarray(np.asarray(inp["rwkv_w_in"][0], f))
        m["rwkv_w_out"] = np.ascontiguousarray(np.asarray(inp["rwkv_w_out"][0], f))
        mu = np.asarray(inp["rwkv_mu"][0], f)
        m["rwkv_mu"] = np.ascontiguousarray(mu)
        m["rwkv_mu_fm"] = np.ascontiguousarray(mu.reshape(33, 128).T)
        vecs = np.stack([np.asarray(inp[k][0], f).reshape(NKC, 128) for k in
                         ("rwkv_w0", "rwkv_a0", "rwkv_k_k", "rwkv_k_a", "rwkv_r_k")], axis=0)
        m["rwkv_vecs"] = np.ascontiguousarray(vecs.transpose(2, 0, 1))
        m["rwkv_lw2"] = np.ascontiguousarray(np.concatenate([np.asarray(inp["rwkv_w_w2"][0], f), np.asarray(inp["rwkv_w_a2"][0], f)], axis=0))
        m["rwkv_gn_g"] = np.ascontiguousarray(np.asarray(inp["rwkv_gn_g"][0], f))
        m["rwkv_gn_b"] = np.ascontiguousarray(np.asarray(inp["rwkv_gn_b"][0], f))
    if 'hgrn' in kinds:
        m["hgrn_w_in"] = np.ascontiguousarray(np.asarray(inp["hgrn_w_in"][0], f))
        m["hgrn_w_out"] = np.ascontiguousarray(np.asarray(inp["hgrn_w_out"][0], f))
        m["hgrn_gn_g"] = np.ascontiguousarray(np.asarray(inp["hgrn_gn_g"][0], f))
        m["hgrn_lbl"] = np.ascontiguousarray(np.asarray(inp["hgrn_lb_logits"], f).reshape(4, NKC, 128).transpose(2, 0, 1))
    if 'gmlp' in kinds:
        m["gmlp_w_in"] = np.ascontiguousarray(np.asarray(inp["gmlp_w_in"][0], f))
        m["gmlp_w_out"] = np.ascontiguousarray(np.asarray(inp["gmlp_w_out"][0], f))
        m["gmlp_wsT"] = np.ascontiguousarray(np.asarray(inp["gmlp_w_s"][0], f).transpose(2, 0, 1))
        m["gmlp_bs"] = np.ascontiguousarray(np.asarray(inp["gmlp_b_s"][0], f))
        m["gmlp_vg"] = np.ascontiguousarray(np.asarray(inp["gmlp_v_g"][0], f))
    return m


FULL_LAYERS = [(0, 'rwkv'), (1, 'hgrn'), (2, 'conv'), (3, 'gmlp')]


def kernel(**inputs):
    x = np.asarray(inputs["x"])
    B, T, _ = x.shape
    layers = FULL_LAYERS
    bld = Builder(T, layers)
    nc = bld.build()
    in_maps = []
    for c in range(8):
        in_maps.append(prep_inputs(inputs, c // 2, layers))
    res = run_bass_kernel_spmd(nc, in_maps, core_ids=list(range(8)))
    out = np.stack([np.asarray(res.results[2 * b]["yT"]).T for b in range(B)], axis=0)
    return out.astype(np.float32)
```

```python
import numpy as np
from contextlib import ExitStack
import concourse.bass as bass
import concourse.mybir as mybir
from concourse.bass_utils import run_bass_kernel_spmd

F32 = mybir.dt.float32
BF16 = mybir.dt.bfloat16
ALU = mybir.AluOpType
AF = mybir.ActivationFunctionType
AX = mybir.AxisListType

D = 1024
NKC = 8
RMS_EPS = 1e-6
GN_EPS = 64e-5


class Prog:
    LIMIT = 30000

    def __init__(self, nc, es, n_dma_sems=24):
        self.nc = nc
        self.es = es
        self.engs = {'pe': nc.tensor, 'act': nc.scalar, 'dve': nc.vector,
                     'pool': nc.gpsimd, 'sp': nc.sync}
        self.sems = {}
        self.epoch = {k: 0 for k in self.engs}
        self.cnt = {k: 0 for k in self.engs}
        for k in self.engs:
            self.sems[(k, 0)] = es.enter_context(nc.semaphore(f"s_{k}_0"))
        self.dma_sems = []
        for i in range(n_dma_sems):
            key = ('dma', i)
            self.sems[key] = es.enter_context(nc.semaphore(f"s_dma_{i}"))
            self.cnt[key] = 0
            self.dma_sems.append(key)
        self.dma_rr = 0
        self.waited = {k: {} for k in self.engs}
        self.bufs = {}
        self.n_wait = 0
        self.n_ins = 0

    def _deps(self, reads, writes):
        deps = set()
        for k in reads:
            b = self.bufs.get(k)
            if b and b['w']:
                deps.add(b['w'])
        for k in writes:
            b = self.bufs.get(k)
            if b:
                if b['w']:
                    deps.add(b['w'])
                deps.update(b['r'])
        return deps

    def _wait(self, eng, deps):
        e = self.engs[eng]
        best = {}
        for (sk, v) in deps:
            if sk[0] == eng and eng == 'pe':
                continue
            if best.get(sk, 0) < v:
                best[sk] = v
        for sk, v in best.items():
            if self.waited[eng].get(sk, 0) >= v:
                continue
            e.wait_ge(self.sems[sk], v)
            self.waited[eng][sk] = v
            self.n_wait += 1

    def _record(self, tok, reads, writes):
        for k in reads:
            b = self.bufs.setdefault(k, {'w': None, 'r': []})
            b['r'].append(tok)
            if len(b['r']) > 64:
                best = {}
                for (sk, v) in b['r']:
                    if best.get(sk, 0) < v:
                        best[sk] = v
                b['r'] = list(best.items())
        for k in writes:
            b = self.bufs.setdefault(k, {'w': None, 'r': []})
            b['w'] = tok
            b['r'] = []

    @staticmethod
    def _excl(reads, writes):
        ps = [k for k in reads if isinstance(k, str) and k.startswith('ps')]
        if ps:
            reads = [k for k in reads if k not in ps]
            writes = list(writes) + ps
        return reads, writes

    disabled = False
    recording = None
    SYNC_LAT = 0.45

    def begin_record(self):
        self.recording = []

    def flush(self):
        rec = self.recording
        self.recording = None
        if not rec:
            return
        n = len(rec)
        preds = [None] * n
        succs = [[] for _ in range(n)]
        last_w = {}
        readers = {}
        for i, (kind, eng, fn, reads, writes, cost, lat) in enumerate(rec):
            ps = set()
            for k in reads:
                w = last_w.get(k)
                if w is not None:
                    ps.add(w)
            for k in writes:
                w = last_w.get(k)
                if w is not None:
                    ps.add(w)
                ps.update(readers.get(k, ()))
            ps.discard(i)
            preds[i] = ps
            for pi in ps:
                succs[pi].append(i)
            for k in reads:
                readers.setdefault(k, []).append(i)
            for k in writes:
                last_w[k] = i
                readers[k] = []
        npred = [len(p_) for p_ in preds]
        ready = [i for i in range(n) if npred[i] == 0]
        eng_free = {}
        end_t = [0.0] * n
        done_t = [0.0] * n
        order = []
        blevel = [0.0] * n
        for i in range(n - 1, -1, -1):
            kind, eng, fn, reads, writes, cost, lat = rec[i]
            b = 0.0
            for si in succs[i]:
                v = blevel[si] + (self.SYNC_LAT if rec[si][1] != eng else 0.0)
                if v > b:
                    b = v
            blevel[i] = b + cost + lat

        def est(i):
            kind, eng, fn, reads, writes, cost, lat = rec[i]
            t = eng_free.get(eng, 0.0)
            for pi in preds[i]:
                tp = done_t[pi] + (self.SYNC_LAT if rec[pi][1] != eng else 0.0)
                if tp > t:
                    t = tp
            return t
        EPS = 0.25
        while ready:
            ests = [(est(i), i) for i in ready]
            tmin = min(ests)[0]
            best = None
            for (t, i) in ests:
                if t <= tmin + EPS:
                    if best is None or blevel[i] > blevel[best[1]] or (blevel[i] == blevel[best[1]] and i < best[1]):
                        best = (t, i)
            t1, i = best
            ready.remove(i)
            kind, eng, fn, reads, writes, cost, lat = rec[i]
            end_t[i] = t1 + cost
            done_t[i] = t1 + cost + lat
            eng_free[eng] = end_t[i]
            order.append(i)
            for si in succs[i]:
                npred[si] -= 1
                if npred[si] == 0:
                    ready.append(si)
        assert len(order) == n, (len(order), n)
        self.sched_span = max(done_t) if done_t else 0.0
        for i in order:
            kind, eng, fn, reads, writes, cost, lat = rec[i]
            if kind == 'op':
                self.op(eng, fn, reads, writes)
            else:
                out, in_, kw = fn
                self.dma(eng, out, in_, reads, writes, **kw)

    def op(self, eng, fn, reads=(), writes=(), cost=None):
        if self.disabled:
            return None
        if self.recording is not None:
            reads, writes = self._excl(reads, writes)
            if cost is None:
                cost = {'pe': 0.2, 'act': 0.45, 'dve': 0.45, 'pool': 0.7, 'sp': 0.1}[eng]
            self.recording.append(('op', eng, fn, list(reads), list(writes), cost, 0.0))
            return None
        reads, writes = self._excl(reads, writes)
        deps = self._deps(reads, writes)
        self._wait(eng, deps)
        ins = fn(self.engs[eng])
        if self.cnt[eng] >= self.LIMIT:
            self.epoch[eng] += 1
            ep = self.epoch[eng]
            self.sems[(eng, ep)] = self.es.enter_context(self.nc.semaphore(f"s_{eng}_{ep}"))
            self.cnt[eng] = 0
        sk = (eng, self.epoch[eng])
        self.cnt[eng] += 1
        ins.then_inc(self.sems[sk], 1)
        self._record((sk, self.cnt[eng]), reads, writes)
        self.n_ins += 1
        return ins

    def dma(self, eng, out, in_, reads=(), writes=(), **kw):
        if self.disabled:
            return None
        if self.recording is not None:
            self.recording.append(('dma', eng, (out, in_, kw), list(reads), list(writes), 0.15, 6.0))
            return None
        deps = self._deps(reads, writes)
        sk = self.dma_sems[self.dma_rr]
        self.dma_rr = (self.dma_rr + 1) % len(self.dma_sems)
        if self.cnt[sk] > 0:
            deps.add((sk, self.cnt[sk]))
        self._wait(eng, deps)
        ins = self.engs[eng].dma_start(out=out, in_=in_, **kw)
        self.cnt[sk] += 16
        ins.then_inc(self.sems[sk], 16)
        self._record((sk, self.cnt[sk]), reads, writes)
        self.n_ins += 1
        return ins

    def all_tokens(self):
        deps = set()
        for k, b in self.bufs.items():
            if b['w']:
                deps.add(b['w'])
            deps.update(b['r'])
        return deps

    def barrier(self):
        deps = self.all_tokens()
        for eng in self.engs:
            d = set(x for x in deps)
            self._wait(eng, d)

    def finish(self, eng='sp'):
        self._wait(eng, self.all_tokens())


class Builder:
    def __init__(self, T, layers, do_final=True, neu_dt=None):
        self.neu_dt = neu_dt if neu_dt is not None else BF16
        self.use_sched = True
        self.T = T
        self.layers = layers
        self.do_final = do_final
        self.nc = bass.Bass("TRN2", target_bir_lowering=False)
        self.inputs = {}

    def din(self, name, shape):
        t = self.nc.dram_tensor(name, list(shape), F32, kind="ExternalInput").ap()
        self.inputs[name] = t
        return t

    def sb(self, name, shape, dt=F32):
        return self.es.enter_context(self.nc.sbuf_tensor(name, list(shape), dt))

    def lsb(self, name, shape, dt=F32):
        return self.les.enter_context(self.nc.sbuf_tensor(f"{name}_{self.lname}", list(shape), dt))

    def next_ps(self):
        pool = self.ps_pool
        i = pool[self.ps_rr % len(pool)]
        self.ps_rr += 1
        return self.psums[i], f"ps{i}"

    @staticmethod
    def ecost(eng, ap):
        try:
            n = ap.free_size()
        except Exception:
            n = 256
        if eng == 'act':
            return 0.22 + n * 0.00075
        if eng == 'dve':
            return 0.2 + n * 0.00095
        if eng == 'pool':
            return 0.2 + n * 0.0021
        return 0.2

    def tt(self, eng, out, in0, in1, op, reads, writes):
        return self.p.op(eng, lambda e: e.tensor_tensor(out=out, in0=in0, in1=in1, op=op), reads, writes, cost=self.ecost(eng, out))

    def ts(self, eng, out, in0, s1, s2, op0, op1, reads, writes):
        if s2 is None:
            return self.p.op(eng, lambda e: e.tensor_scalar(out=out, in0=in0, scalar1=s1, scalar2=None, op0=op0), reads, writes, cost=self.ecost(eng, out))
        return self.p.op(eng, lambda e: e.tensor_scalar(out=out, in0=in0, scalar1=s1, scalar2=s2, op0=op0, op1=op1), reads, writes, cost=self.ecost(eng, out))

    def stt(self, eng, out, in0, scalar, in1, op0, op1, reads, writes):
        eng = 'dve'
        return self.p.op(eng, lambda e: e.scalar_tensor_tensor(out=out, in0=in0, scalar=scalar, in1=in1, op0=op0, op1=op1), reads, writes, cost=self.ecost(eng, out))

    def act(self, out, in_, func, reads, writes, bias=None, scale=1.0, accum_out=None):
        kw = {}
        if bias is not None:
            kw['bias'] = bias
        if accum_out is not None:
            kw['accum_out'] = accum_out
        return self.p.op('act', lambda e: e.activation(out=out, in_=in_, func=func, scale=scale, **kw), reads, writes, cost=self.ecost('act', in_))

    def mm(self, out, lhsT, rhs, start, stop, reads, writes):
        try:
            n = rhs.free_size()
        except Exception:
            n = 128
        c = 0.06 + n / 2400.0 * (1.0 if lhsT.dtype == BF16 else 2.4)
        return self.p.op('pe', lambda e: e.matmul(out, lhsT=lhsT, rhs=rhs, start=start, stop=stop), reads, writes, cost=c)

    def copy(self, eng, out, in_, reads, writes):
        if eng == 'act':
            return self.p.op('act', lambda e: e.copy(out=out, in_=in_), reads, writes, cost=self.ecost('act', out))
        return self.p.op(eng, lambda e: e.tensor_copy(out=out, in_=in_), reads, writes, cost=self.ecost(eng, out))

    def load_weight_bf16(self, dst, dst_key, src, ncols, src_c0=0, dst_c0=0, scale_bc=None):
        p = self.p
        CH = 1024 if ncols % 1024 == 0 else ncols
        for kc in range(NKC):
            for c0 in range(0, ncols, CH):
                i = self.stage_i
                self.stage_i += 1
                nst = len(self.stage)
                st = self.stage[i % nst]
                sk = f"stage{i % nst}"
                p.dma('sp', st[:, 0:CH], src[kc * 128:(kc + 1) * 128, src_c0 + c0:src_c0 + c0 + CH], reads=[], writes=[sk])
                eng = ['dve', 'act'][i % 2] if scale_bc is None else ['dve', 'pool'][i % 2]
                if scale_bc is None:
                    self.copy(eng, dst[:, kc, dst_c0 + c0:dst_c0 + c0 + CH], st[:, 0:CH], [sk], [dst_key])
                else:
                    self.tt(eng, dst[:, kc, dst_c0 + c0:dst_c0 + c0 + CH], st[:, 0:CH], scale_bc[:, c0:c0 + CH], ALU.mult,
                            [sk, 'bc_tiles'], [dst_key])

    def rms_rstd(self, src, src_key, TT, tag):
        bi = self.rms_i % len(self.sqb_l)
        self.rms_i += 1
        sqb, rstd = self.sqb_l[bi], self.rstd_l[bi]
        ksq, krs = f'sqb{bi}', f'rstd{bi}'
        for kc in range(NKC):
            if kc % 2 == 0:
                self.act(sqb[:, kc, :TT], src[:, kc, :TT], AF.Square, [src_key], [(ksq, kc)])
            else:
                self.tt('dve', sqb[:, kc, :TT], src[:, kc, :TT], src[:, kc, :TT], ALU.mult, [src_key], [(ksq, kc)])
        ps, pk = self.next_ps()
        for kc in range(NKC):
            self.mm(ps[:, :TT], self.ones_bf[:], sqb[:, kc, :TT], kc == 0, kc == NKC - 1, [(ksq, kc), 'ones_bf'], [pk])
        self.act(rstd[:, :TT], ps[:, :TT], AF.Ln, [pk, 'consts'], [krs], bias=self.epsc[:, 0:1], scale=1.0 / D)
        self.act(rstd[:, :TT], rstd[:, :TT], AF.Exp, [krs], [krs], scale=-0.5)
        return rstd, krs

    def run_layer(self, li, kind, TT, w_in_cols, mixer_setup, mixer_tile, is_last):
        p = self.p
        T = self.T
        ntiles = T // TT
        with ExitStack() as les:
            self.les = les
            self.lname = f"L{li}"
            self.TT = TT
            self.W_in = self.lsb("W_in", [128, NKC, w_in_cols], BF16)
            self.W_out = self.lsb("W_out", [128, NKC, D], BF16)
            self.hT = [self.lsb(f"hT{i}", [128, NKC, TT], F32) for i in range(2)]
            ndb = 2 if kind != 'rwkv' else 1
            self.sqb_l = [self.lsb(f"sqb{i}", [128, NKC, TT], BF16) for i in range(ndb)]
            self.rstd_l = [self.lsb(f"rstd{i}", [128, TT], F32) for i in range(ndb)]
            self.sqb, self.rstd = self.sqb_l[0], self.rstd_l[0]
            self.rms_i = 0
            self.hn_l = [self.lsb(f"hn{i}", [128, NKC, TT + 1], BF16) for i in range(ndb)]
            self.yTt_l = [self.lsb(f"yTt{i}", [128, NKC, TT], BF16) for i in range(ndb)]
            self.hn, self.hnk = self.hn_l[0], 'hn0'
            self.yTt, self.yTk = self.yTt_l[0], 'yT0'
            self.stage_i = 0
            loader = mixer_setup()
            with ExitStack() as ses:
                nst = 2 if kind == 'rwkv' else max(2, min(4, (self.nc.sbuf_bytes_remaining - 512) // 4096))
                self.stage = [ses.enter_context(self.nc.sbuf_tensor(f"stage{i}_{self.lname}", [128, 1024], F32)) for i in range(nst)]
                if loader is None:
                    self.load_weight_bf16(self.W_in, 'W_in', self.w_in_dram[kind], w_in_cols)
                    self.load_weight_bf16(self.W_out, 'W_out', self.w_out_dram[kind], D)
                else:
                    loader(ses)
                p.barrier()
            if getattr(self, 'post_setup', None) is not None:
                self.post_setup()
                self.post_setup = None
            hn0 = self.hn_l[0]
            p.op('pool', lambda e: e.memset(hn0[:, :, 0:1], 0.0), [], ['hn0'])

            def load(ti):
                buf = self.hT[ti % 2]
                src = self.xT if self.first_layer else self.yT
                p.dma('sp', buf[:], src.rearrange("(c p) t -> p c t", p=128)[:, :, ti * TT:(ti + 1) * TT],
                      reads=[('hd', ti * TT // 128 + i) for i in range(TT // 128)], writes=[f"hT{ti % 2}"])

            if self.use_sched:
                p.begin_record()
            load(0)
            for ti in range(ntiles):
                if ti + 1 < ntiles:
                    load(ti + 1)
                h = self.hT[ti % 2]
                hk = f"hT{ti % 2}"
                rstd, rk = self.rms_rstd(h, hk, TT, 'in')
                g = self.norm_g
                prev_hn, prev_hnk = self.hn, self.hnk
                bi = ti % ndb
                self.hn, self.hnk = self.hn_l[bi], f'hn{bi}'
                self.yTt, self.yTk = self.yTt_l[bi], f'yT{bi}'
                if ti > 0:
                    self.copy('pool', self.hn[:, :, 0:1], prev_hn[:, :, TT:TT + 1], [prev_hnk], [self.hnk])
                for kc in range(NKC):
                    self.stt('dve', self.hn[:, kc, 1:TT + 1], h[:, kc, :], g[:, li, kc:kc + 1], rstd[:, :TT],
                             ALU.mult, ALU.mult, [hk, rk, 'consts'], [self.hnk])
                mixer_tile(ti)
                for j in range(NKC):
                    ps, pk = self.next_ps()
                    for kc in range(NKC):
                        self.mm(ps[:, :TT], self.W_out[:, kc, j * 128:(j + 1) * 128], self.yTt[:, kc, :TT],
                                kc == 0, kc == NKC - 1, ['W_out', self.yTk], [pk])
                    self.tt('dve', h[:, j, :], h[:, j, :], ps[:, :TT], ALU.add, [hk, pk], [hk])
                if is_last and self.do_final:
                    rstd, rk = self.rms_rstd(h, hk, TT, 'fin')
                    for kc in range(NKC):
                        self.stt('dve' if kc % 2 == 0 else 'pool', h[:, kc, :], h[:, kc, :], self.final_g[:, kc:kc + 1], rstd[:, :TT],
                                 ALU.mult, ALU.mult, [hk, rk, 'consts'], [hk])
                p.dma('sp', self.yT.rearrange("(c p) t -> p c t", p=128)[:, :, ti * TT:(ti + 1) * TT], h[:],
                      reads=[hk], writes=[('hd', (ti * TT) // 128 + i) for i in range(max(1, TT // 128))])
            if self.use_sched:
                p.flush()
            self.first_layer = False
            p.barrier()
        self.les = None

    def conv_layer(self, li, is_last):
        TT = 512

        def setup():
            self.yext = self.lsb("yext", [128, NKC, TT + 2], F32)
            self.zs = [self.lsb(f"zs{i}", [128, TT], F32) for i in range(2)]
            self.acc = [self.lsb(f"acc{i}", [128, TT], F32) for i in range(2)]
            self.sg = [self.lsb(f"sg{i}", [128, TT], F32) for i in range(2)]
            self.p.op('pool', lambda e: e.memset(self.yext[:, :, 0:2], 0.0), [], [('yext', j) for j in range(NKC)])

        def tile(ti):
            W = self.W_in
            cw = self.conv_w
            for j in range(NKC):
                zs, acc, sg = self.zs[j % 2], self.acc[j % 2], self.sg[j % 2]
                zk, ak, gk = f"zs{j % 2}", f"acc{j % 2}", f"sg{j % 2}"
                pss = []
                for blk in range(4):
                    ps, pk = self.next_ps()
                    col0 = blk * D + j * 128
                    for kc in range(NKC):
                        self.mm(ps[:, :TT], W[:, kc, col0:col0 + 128], self.hn[:, kc, 1:TT + 1], kc == 0, kc == NKC - 1,
                                ['W_in', self.hnk], [pk])
                    pss.append((ps, pk))
                (pb, pbk), (pc, pck), (pz, pzk), (pg, pgk) = pss
                yk = ('yext', j)
                self.copy('act', zs[:], pz[:, :TT], [pzk], [zk])
                if ti > 0:
                    self.copy('pool', self.yext[:, j, 0:2], self.yext[:, j, TT:TT + 2], [yk], [yk])
                self.tt('dve', self.yext[:, j, 2:TT + 2], pc[:, :TT], zs[:], ALU.mult, [pck, zk], [yk])
                self.act(acc[:], self.yext[:, j, 2:TT + 2], AF.Copy, [yk, 'consts'], [ak], scale=cw[:, j, 2:3])
                self.stt('pool', acc[:], self.yext[:, j, 1:TT + 1], cw[:, j, 1:2], acc[:], ALU.mult, ALU.add, [yk, ak, 'consts'], [ak])
                self.stt('pool', acc[:], self.yext[:, j, 0:TT], cw[:, j, 0:1], acc[:], ALU.mult, ALU.add, [yk, ak, 'consts'], [ak])
                self.act(sg[:], pg[:, :TT], AF.Silu, [pgk], [gk])
                self.tt('dve', acc[:], pb[:, :TT], acc[:], ALU.mult, [pbk, ak], [ak])
                self.tt('pool', self.yTt[:, j, :], acc[:], sg[:], ALU.mult, [ak, gk], [self.yTk])

        self.run_layer(li, 'conv', TT, 4 * D, setup, tile, is_last)

    def gmlp_layer(self, li, is_last):
        TT = 512

        def setup():
            p = self.p
            self.wsT = self.lsb("wsT", [128, 8, 128], F32)
            self.bs_bc = self.lsb("bs_bc", [128, 8, TT], F32)
            self.vg_bc = self.lsb("vg_bc", [128, D], F32)
            self.vn = [self.lsb(f"vn{i}", [128, D], F32) for i in range(TT // 128)]
            self.vss = self.lsb("vss", [128, 4], F32)
            self.junk = self.lsb("junk", [128, 512], F32)
            self.s_sb = [self.lsb(f"s_sb{i}", [128, TT], F32) for i in range(2)]
            self.sg = [self.lsb(f"sg{i}", [128, TT], F32) for i in range(2)]
            p.dma('sp', self.wsT[:], self.inputs['gmlp_wsT'], [], ['wsT'])
            for g in range(8):
                p.op('pool', lambda e: e.affine_select(out=self.wsT[:, g, :], in_=self.wsT[:, g, :], pattern=[[1, 128]],
                                                       compare_op=ALU.is_ge, fill=0.0, base=0, channel_multiplier=-1),
                     ['wsT'], ['wsT'])
            for r in range(TT // 128):
                p.dma('sp', self.bs_bc[:, :, r * 128:(r + 1) * 128],
                      self.inputs['gmlp_bs'].partition_broadcast(128), [], ['bs_bc'])
            p.dma('sp', self.vg_bc[:], self.inputs['gmlp_vg'].partition_broadcast(128), [], ['vg_bc'])

        def tile(ti):
            W = self.W_in
            nblk = TT // 128
            for blk in range(nblk):
                vn = self.vn[blk]
                vk = f"vn{blk}"
                halves = []
                for hf in range(2):
                    ps, pk = self.next_ps()
                    for kc in range(NKC):
                        self.mm(ps[:, :512], self.hn[:, kc, 1 + blk * 128:1 + (blk + 1) * 128],
                                W[:, kc, D + hf * 512:D + (hf + 1) * 512], kc == 0, kc == NKC - 1, ['W_in', self.hnk], [pk])
                    halves.append((ps, pk))
                for hf, (ps, pk) in enumerate(halves):
                    self.act(self.junk[:], ps[:, :512], AF.Square, [pk], ['junk', 'vss'], accum_out=self.vss[:, hf:hf + 1])
                self.tt('dve', self.vss[:, 2:3], self.vss[:, 0:1], self.vss[:, 1:2], ALU.add, ['vss'], ['vss'])
                self.act(self.vss[:, 3:4], self.vss[:, 2:3], AF.Ln, ['vss', 'consts'], ['vss'], bias=self.epsc[:, 0:1], scale=1.0 / D)
                self.act(self.vss[:, 3:4], self.vss[:, 3:4], AF.Exp, ['vss'], ['vss'], scale=-0.5)
                for hf, (ps, pk) in enumerate(halves):
                    self.stt('dve', vn[:, hf * 512:(hf + 1) * 512], ps[:, :512], self.vss[:, 3:4],
                             self.vg_bc[:, hf * 512:(hf + 1) * 512], ALU.mult, ALU.mult, [pk, 'vss', 'vg_bc'], [vk])
            for j in range(NKC):
                s_sb, sg = self.s_sb[j % 2], self.sg[j % 2]
                sk, gk = f"s_sb{j % 2}", f"sg{j % 2}"
                ps, pk = self.next_ps()
                for blk in range(nblk):
                    self.mm(ps[:, blk * 128:(blk + 1) * 128], self.vn[blk][:, j * 128:(j + 1) * 128], self.wsT[:, j, :], True, True,
                            [f"vn{blk}", 'wsT'], [pk])
                self.tt('dve', s_sb[:], ps[:, :TT], self.bs_bc[:, j, :], ALU.add, [pk, 'bs_bc'], [sk])
                pu, puk = self.next_ps()
                for kc in range(NKC):
                    self.mm(pu[:, :TT], W[:, kc, j * 128:(j + 1) * 128], self.hn[:, kc, 1:TT + 1], kc == 0, kc == NKC - 1,
                            ['W_in', self.hnk], [puk])
                pg, pgk = self.next_ps()
                for kc in range(NKC):
                    self.mm(pg[:, :TT], W[:, kc, 2 * D + j * 128:2 * D + (j + 1) * 128], self.hn[:, kc, 1:TT + 1], kc == 0,
                            kc == NKC - 1, ['W_in', self.hnk], [pgk])
                self.act(sg[:], pg[:, :TT], AF.Silu, [pgk], [gk])
                self.tt('dve', s_sb[:], pu[:, :TT], s_sb[:], ALU.mult, [puk, sk], [sk])
                self.tt('pool', self.yTt[:, j, :], s_sb[:], sg[:], ALU.mult, [sk, gk], [self.yTk])

        self.run_layer(li, 'gmlp', TT, 3 * D, setup, tile, is_last)


    def make_ident(self, ident, key):
        p = self.p
        p.op('pool', lambda e: e.memset(ident[:], 1.0), [], [key])
        p.op('pool', lambda e: e.affine_select(out=ident[:], in_=ident[:], pattern=[[-1, 128]], compare_op=ALU.is_equal,
                                               fill=0.0, base=0, channel_multiplier=1), [key], [key])

    def make_block_masks(self, C, maskT, colmask, rowmask, strict=False):
        p = self.p
        nch = 128 // C
        if maskT is not None:
            p.op('pool', lambda e: e.memset(maskT[:], 1.0), [], ['masks'])
            p.op('pool', lambda e: e.affine_select(out=maskT[:], in_=maskT[:], pattern=[[1, 128]], compare_op=ALU.is_ge if not strict else ALU.is_gt,
                                                   fill=0.0, base=0, channel_multiplier=-1), ['masks'], ['masks'])
            for c in range(1, nch):
                p.op('pool', lambda e, c=c: e.affine_select(out=maskT[:, c * C:(c + 1) * C], in_=maskT[:, c * C:(c + 1) * C], pattern=[[0, C]],
                                                            compare_op=ALU.is_ge, fill=0.0, base=-c * C, channel_multiplier=1), ['masks'], ['masks'])
        if colmask is not None:
            p.op('pool', lambda e: e.memset(colmask[:], 0.0), [], ['masks'])
            for c in range(nch):
                p.op('pool', lambda e, c=c: e.memset(colmask[:, c, c * C:(c + 1) * C], 1.0), ['masks'], ['masks'])
        if rowmask is not None:
            p.op('pool', lambda e: e.memset(rowmask[:], 1.0), [], ['masks'])
            for c in range(nch):
                p.op('pool', lambda e, c=c: e.affine_select(out=rowmask[:, c:c + 1], in_=rowmask[:, c:c + 1], pattern=[[0, 1]],
                                                            compare_op=ALU.is_ge, fill=0.0, base=-c * C, channel_multiplier=1), ['masks'], ['masks'])
                p.op('pool', lambda e, c=c: e.affine_select(out=rowmask[:, c:c + 1], in_=rowmask[:, c:c + 1], pattern=[[0, 1]],
                                                            compare_op=ALU.is_ge, fill=0.0, base=c * C + C - 1, channel_multiplier=-1), ['masks'], ['masks'])

    def hgrn_layer(self, li, is_last):
        TT = 256
        C = 32
        NB = TT // 128
        NCH = TT // C

        def setup():
            p = self.p
            L = self.lsb
            self.ident = L("ident", [128, 128], F32)
            self.make_ident(self.ident, 'ident')
            self.maskT = L("maskT", [128, 128], F32)
            self.colmask = L("colmask", [128, 4, 128], F32)
            self.rowmask = L("rowmask", [128, 4], F32)
            self.make_block_masks(C, self.maskT, self.colmask, self.rowmask)
            self.resetm = L("resetm", [128, TT], F32)
            p.op('pool', lambda e: e.memset(self.resetm[:], 1.0), [], ['masks'])
            p.op('pool', lambda e: e.memset(self.resetm[:].rearrange("p (n c) -> p n c", c=C)[:, :, 0:1], 0.0), ['masks'], ['masks'])
            self.gn_bc = L("gn_bc", [128, D], F32)
            p.dma('sp', self.gn_bc[:], self.inputs['hgrn_gn_g'].partition_broadcast(128), [], ['gn_bc'])
            self.lbl = L("lbl", [128, 4, NKC], F32)
            self.lbt = L("lbt", [128, 4, NKC], F32)
            p.dma('sp', self.lbl[:], self.inputs['hgrn_lbl'], [], ['lbl'])
            self.act(self.lbl[:], self.lbl[:], AF.Exp, ['lbl'], ['lbl'])
            self.tt('dve', self.lbt[:, 0, :], self.lbl[:, 0, :], self.lbl[:, 1, :], ALU.add, ['lbl'], ['lbt'])
            self.tt('dve', self.lbt[:, 0, :], self.lbt[:, 0, :], self.lbl[:, 2, :], ALU.add, ['lbl', 'lbt'], ['lbt'])
            self.tt('dve', self.lbt[:, 0, :], self.lbt[:, 0, :], self.lbl[:, 3, :], ALU.add, ['lbl', 'lbt'], ['lbt'])
            p.op('dve', lambda e: e.reciprocal(out=self.lbt[:, 3, :], in_=self.lbt[:, 0, :]), ['lbt'], ['lbt'])
            p.op('dve', lambda e: e.memset(self.lbt[:, 1, :], 0.0), ['lbt'], ['lbt'])
            for i in range(1, li + 1):
                self.tt('dve', self.lbt[:, 1, :], self.lbt[:, 1, :], self.lbl[:, i, :], ALU.add, ['lbl', 'lbt'], ['lbt'])
            self.tt('dve', self.lbt[:, 1, :], self.lbt[:, 1, :], self.lbt[:, 3, :], ALU.mult, ['lbt'], ['lbt'])
            self.ts('dve', self.lbt[:, 2, :], self.lbt[:, 1, :], -1.0, 1.0, ALU.mult, ALU.add, ['lbt'], ['lbt'])
            self.S = L("S_hgrn", [128, NKC, 128], F32)
            p.op('pool', lambda e: e.memset(self.S[:], 0.0), [], [('S', j) for j in range(NKC)])
            names = ['f', 'kk', 'bb', 'qe', 'dd', 'sg']
            self.tmps = []
            for q in range(2):
                tm = {n: L(f"h_{n}{q}", [128, TT], F32) for n in names}
                tm['e1'] = tm['f']
                tm['ko'] = tm['dd']
                tm['ke_bf'] = L(f"h_ke_bf{q}", [128, TT], BF16)
                tm['qe_bf'] = L(f"h_qe_bf{q}", [128, TT], BF16)
                tm['kom'] = L(f"kom{q}", [128, 4, NB, 128], BF16)
                self.tmps.append(tm)
            self.sgate = [L(f"sgate{q}", [128, TT], BF16) for q in range(3)]
            self.qem = [L(f"qem{q}", [128, 4, TT], BF16) for q in range(3)]
            self.v_bf = [L(f"v_bf{q}", [128, NB, 128], BF16) for q in range(3)]
            self.attm = [L(f"attm{q}", [128, NB, 128], BF16) for q in range(3)]
            self.u_sb = [L(f"u_sb{q}", [128, NCH, 128], F32) for q in range(3)]
            self.dec = [L(f"dec{q}", [128, NCH], F32) for q in range(3)]
            self.S_all2 = [L(f"S_all{q}", [128, 5, 128], F32) for q in range(2)]
            self.S_bf2 = [L(f"S_bf{q}", [128, NCH, 128], BF16) for q in range(2)]
            self.on2 = [L(f"on{q}", [128, NB, 128], F32) for q in range(2)]
            self.oss2 = [L(f"oss{q}", [128, 2 * NB], F32) for q in range(2)]
            self.junk2 = [L(f"junk{q}", [128, 128], F32) for q in range(2)]
            self.ps_pool = [0, 1, 2, 3]
            self.nm_rr = 0

        def nm_ps():
            i = [4, 5][self.nm_rr % 2]
            self.nm_rr += 1
            return self.psums[i], f"ps{i}"

        def proj(col0):
            ps, pk = self.next_ps()
            for kc in range(NKC):
                self.mm(ps[:, :TT], self.W_in[:, kc, col0:col0 + 128], self.hn[:, kc, 1:TT + 1], kc == 0, kc == NKC - 1, ['W_in', self.hnk], [pk])
            return ps, pk

        def A_gen(j):
            q2 = j % 2
            t = self.tmps[q2]
            kom = t['kom']
            W = self.W_in
            q = j % 3
            lb, oml = self.lbt[:, 1, :], self.lbt[:, 2, :]
            sgate, qem, v_bf, attm, u_sb, dec = self.sgate[q], self.qem[q], self.v_bf[q], self.attm[q], self.u_sb[q], self.dec[q]
            ksg, kqem, kv, katt, ku, kdec = f'sgate{q}', f'qem{q}', f'v_bf{q}', f'attm{q}', f'u_sb{q}', f'dec{q}'
            pf, pfk = proj(D + j * 128)
            self.act(t['f'][:], pf[:, :TT], AF.Exp, [pfk], [f't_f{q2}'], scale=-1.0)
            self.ts('dve', t['f'][:], t['f'][:], 1.0, None, ALU.add, None, [f't_f{q2}'], [f't_f{q2}'])
            self.p.op('dve', lambda e: e.reciprocal(out=t['f'][:], in_=t['f'][:]), [f't_f{q2}'], [f't_f{q2}'], cost=0.45)
            self.ts('dve', t['f'][:], t['f'][:], oml[:, j:j + 1], lb[:, j:j + 1], ALU.mult, ALU.add, [f't_f{q2}', 'lbt'], [f't_f{q2}'])
            self.act(t['kk'][:], t['f'][:], AF.Identity, [f't_f{q2}'], [f't_kk{q2}'], scale=-1.0, bias=self.epsc[:, 2:3])
            self.act(t['dd'][:], t['f'][:], AF.Ln, [f't_f{q2}'], [f't_dd{q2}'])
            self.p.op('dve', lambda e: e.tensor_tensor_scan(out=t['bb'][:], data0=self.resetm[:], data1=t['dd'][:], initial=0.0,
                                                            op0=ALU.mult, op1=ALU.add), [f't_dd{q2}', 'masks'], [f't_bb{q2}'])
            yield
            pq, pqk = proj(j * 128)
            self.act(t['e1'][:], t['bb'][:], AF.Exp, [f't_bb{q2}'], [f't_f{q2}'])
            self.tt('dve', t['qe'][:], pq[:, :TT], t['e1'][:], ALU.mult, [pqk, f't_f{q2}'], [f't_qe{q2}'])
            self.copy('act', t['qe_bf'][:], t['qe'][:], [f't_qe{q2}'], [f't_qe_bf{q2}'])
            qe4 = t['qe'][:].rearrange("p (b t) -> p b t", t=128)
            for c in range(4):
                self.tt('pool' if c % 2 else 'dve', qem[:, c, :].rearrange("p (b t) -> p b t", t=128), qe4,
                        self.colmask[:, c:c + 1, :].to_broadcast([128, NB, 128]), ALU.mult, [f't_qe{q2}', 'masks'], [kqem])
            yield
            self.act(t['e1'][:], t['bb'][:], AF.Exp, [f't_bb{q2}'], [f't_f{q2}'], scale=-1.0)
            self.tt('pool', t['ke_bf'][:], t['kk'][:], t['e1'][:], ALU.mult, [f't_kk{q2}', f't_f{q2}'], [f't_ke_bf{q2}'])
            b3 = t['bb'][:].rearrange("p (n c) -> p n c", c=C)
            self.act(dec[:], b3[:, :, C - 1], AF.Exp, [f't_bb{q2}'], [kdec])
            self.tt('pool', t['dd'][:].rearrange("p (n c) -> p n c", c=C), b3[:, :, C - 1:C].to_broadcast([128, NCH, C]), b3, ALU.subtract,
                    [f't_bb{q2}'], [f't_dd{q2}'])
            self.act(t['dd'][:], t['dd'][:], AF.Exp, [f't_dd{q2}'], [f't_dd{q2}'])
            self.tt('dve', t['ko'][:], t['kk'][:], t['dd'][:], ALU.mult, [f't_kk{q2}', f't_dd{q2}'], [f't_dd{q2}'])
            pg, pgk = proj(3 * D + j * 128)
            self.act(t['sg'][:], pg[:, :TT], AF.Exp, [pgk], [f't_sg{q2}'], scale=-1.0)
            self.ts('dve', t['sg'][:], t['sg'][:], 1.0, None, ALU.add, None, [f't_sg{q2}'], [f't_sg{q2}'])
            self.p.op('dve', lambda e: e.reciprocal(out=t['sg'][:], in_=t['sg'][:]), [f't_sg{q2}'], [f't_sg{q2}'], cost=0.45)
            self.tt('dve', sgate[:], pg[:, :TT], t['sg'][:], ALU.mult, [pgk, f't_sg{q2}'], [ksg])
            yield
            pv, pvk = self.next_ps()
            for blk in range(NB):
                for kc in range(NKC):
                    self.mm(pv[:, blk * 128:(blk + 1) * 128], self.hn[:, kc, 1 + blk * 128:1 + (blk + 1) * 128],
                            W[:, kc, 2 * D + j * 128:2 * D + (j + 1) * 128], kc == 0, kc == NKC - 1, ['W_in', self.hnk], [pvk])
            self.copy('act', v_bf[:].rearrange("p b v -> p (b v)"), pv[:, :TT], [pvk], [kv])
            yield
            ps, pk = nm_ps()
            for blk in range(NB):
                cs = slice(blk * 128, (blk + 1) * 128)
                self.mm(ps[:, cs], t['ke_bf'][:, cs], t['qe_bf'][:, cs], True, True, [f't_ke_bf{q2}', f't_qe_bf{q2}'], [pk])
            self.tt('dve', attm[:], ps[:, :TT].rearrange("p (b t) -> p b t", t=128), self.maskT[:, None, :].to_broadcast([128, NB, 128]),
                    ALU.mult, [pk, 'masks'], [katt])
            ps, pk = nm_ps()
            for blk in range(NB):
                cs = slice(blk * 128, (blk + 1) * 128)
                self.p.op('pe', lambda e, ps=ps, cs=cs: e.transpose(ps[:, cs], t['ko'][:, cs], self.ident[:]), [f't_dd{q2}', 'ident'], [pk])
            for c in range(4):
                self.act(kom[:, c, :, :].rearrange("p b k -> p (b k)"), ps[:, :TT], AF.Copy, [pk, 'masks'], [f'kom{q2}'],
                         scale=self.rowmask[:, c:c + 1])
            yield
            for blk in range(NB):
                ps, pk = nm_ps()
                for c in range(4):
                    self.mm(ps[:, c * 128:(c + 1) * 128], kom[:, c, blk, :], v_bf[:, blk, :], True, True, [f'kom{q2}', kv], [pk])
                self.copy('act' if blk % 2 else 'dve', u_sb[:, blk * 4:(blk + 1) * 4, :].rearrange("p c v -> p (c v)"), ps[:, 0:512], [pk], [ku])
                if blk % 2:
                    yield

        def B_gen(j):
            P = self.psums
            q = j % 3
            sgate, qem, v_bf, attm, u_sb, dec = self.sgate[q], self.qem[q], self.v_bf[q], self.attm[q], self.u_sb[q], self.dec[q]
            ksg, kqem, kv, katt, ku, kdec = f'sgate{q}', f'qem{q}', f'v_bf{q}', f'attm{q}', f'u_sb{q}', f'dec{q}'
            kS = ('S', j)
            q2 = j % 2
            SA = self.S_all2[q2]
            S_bf, on, oss, junk = self.S_bf2[q2], self.on2[q2], self.oss2[q2], self.junk2[q2]
            kSA, kSbf, kon, koss, kjunk = f'S_all{q2}', f'S_bf{q2}', f'on{q2}', f'oss{q2}', f'junk{q2}'
            self.copy('pool', SA[:, 0, :], self.S[:, j, :], [kS], [kSA])
            for blk in range(NB):
                for c in range(4):
                    n = blk * 4 + c
                    self.stt('dve', SA[:, c + 1, :], SA[:, c, :], dec[:, n:n + 1], u_sb[:, n, :], ALU.mult, ALU.add, [kSA, kdec, ku], [kSA])
                self.copy('act', S_bf[:, blk * 4:(blk + 1) * 4, :].rearrange("p c v -> p (c v)"),
                          SA[:, 0:4, :].rearrange("p c v -> p (c v)"), [kSA], [kSbf])
                if blk < NB - 1:
                    self.copy('dve', SA[:, 0, :], SA[:, 4, :], [kSA], [kSA])
                yield
            self.copy('pool', self.S[:, j, :], SA[:, 4, :], [kSA], [kS])
            po, pok = P[6], 'ps6'
            for blk in range(NB):
                cs = slice(blk * 128, (blk + 1) * 128)
                self.mm(po[:, cs], attm[:, blk, :], v_bf[:, blk, :], True, False, [katt, kv], [pok])
                for c in range(4):
                    self.mm(po[:, cs], qem[:, c, cs], S_bf[:, blk * 4 + c, :], False, c == 3, [kqem, kSbf], [pok])
                if blk % 2:
                    yield
            for blk in range(NB):
                cs = slice(blk * 128, (blk + 1) * 128)
                self.act(junk[:], po[:, cs], AF.Square, [pok], [kjunk, koss], accum_out=oss[:, blk:blk + 1])
            self.act(oss[:, NB:2 * NB], oss[:, 0:NB], AF.Ln, [koss, 'consts'], [koss], bias=self.epsc[:, 0:1], scale=1.0 / 128)
            self.act(oss[:, NB:2 * NB], oss[:, NB:2 * NB], AF.Exp, [koss], [koss], scale=-0.5)
            self.tt('dve', on[:], po[:, :TT].rearrange("p (b v) -> p b v", v=128),
                    oss[:, NB:2 * NB, None].to_broadcast([128, NB, 128]), ALU.mult, [pok, koss], [kon])
            self.tt('pool', on[:], on[:], self.gn_bc[:, None, j * 128:(j + 1) * 128].to_broadcast([128, NB, 128]), ALU.mult,
                    [kon, 'gn_bc'], [kon])
            yield
            py, pyk = P[7], 'ps7'
            for blk in range(NB):
                cs = slice(blk * 128, (blk + 1) * 128)
                self.p.op('pe', lambda e, cs=cs, blk=blk: e.transpose(py[:, cs], on[:, blk, :], self.ident[:]), [kon, 'ident'], [pyk])
            self.tt('dve', self.yTt[:, j, :], py[:, :TT], sgate[:], ALU.mult, [pyk, ksg], [self.yTk])
            yield

        def drive(gens):
            gens = [g for g in gens if g is not None]
            while gens:
                for g in list(gens):
                    try:
                        next(g)
                    except StopIteration:
                        gens.remove(g)

        def step(g):
            try:
                next(g)
                return True
            except StopIteration:
                return False

        def tile(ti):
            A = {0: A_gen(0), 1: A_gen(1)}
            while step(A[0]):
                step(A[1])
            for sl in range(NKC):
                must = [B_gen(sl)]
                if sl + 1 < NKC:
                    must.append(A[sl + 1])
                opt = None
                if sl + 2 < NKC:
                    A[sl + 2] = A_gen(sl + 2)
                    opt = A[sl + 2]
                while must:
                    for g in list(must):
                        if not step(g):
                            must.remove(g)
                    if opt is not None and not step(opt):
                        opt = None

        self.run_layer(li, 'hgrn', TT, 4 * D, setup, tile, is_last)
        self.ps_pool = list(range(8))

    def rwkv_layer(self, li, is_last):
        TT = 256
        NB = TT // 128
        WC = 3200
        NDT = self.neu_dt
        LC = -0.6065306597126334

        def setup():
            p = self.p
            L = self.lsb
            self.ident = L("ident", [128, 128], F32)
            self.make_ident(self.ident, 'ident')
            self.ident_n = L("ident_n", [128, 128], NDT)
            self.copy('dve', self.ident_n[:], self.ident[:], ['ident'], ['ident'])
            self.maskS = L("maskS", [128, 128], F32)
            self.maskI = L("maskI", [128, 128], F32)
            self.maskSL = L("maskSL", [128, 128], F32)
            for (m, pat, cm, cmp_) in ((self.maskS, 1, -1, ALU.is_gt), (self.maskI, 1, -1, ALU.is_ge), (self.maskSL, -1, 1, ALU.is_gt)):
                p.op('pool', lambda e, m=m: e.memset(m[:], 1.0), [], ['masks'])
                p.op('pool', lambda e, m=m, pat=pat, cm=cm, cmp_=cmp_: e.affine_select(
                    out=m[:], in_=m[:], pattern=[[pat, 128]], compare_op=cmp_, fill=0.0, base=0, channel_multiplier=cm), ['masks'], ['masks'])
            self.blockones = L("blockones", [128, 128], F32)
            p.op('pool', lambda e: e.memset(self.blockones[:], 1.0), [], ['masks'])
            p.op('pool', lambda e: e.memset(self.blockones[0:64, 64:128], 0.0), ['masks'], ['masks'])
            p.op('pool', lambda e: e.memset(self.blockones[64:128, 0:64], 0.0), ['masks'], ['masks'])
            self.resetm = L("resetm", [128, TT], F32)
            p.op('pool', lambda e: e.memset(self.resetm[:], 1.0), [], ['masks'])
            p.op('pool', lambda e: e.memset(self.resetm[:].rearrange("p (n c) -> p n c", c=128)[:, :, 0:1], 0.0), ['masks'], ['masks'])
            p.op('pool', lambda e: e.memset(self.epsc[:, 1:2], GN_EPS), [], ['consts'])
            self.mu_fm = L("mu_fm", [128, 33], F32)
            self.omu_fm = L("omu_fm", [128, 33], F32)
            p.dma('sp', self.mu_fm[:], self.inputs['rwkv_mu_fm'], [], ['rw_vecs'])
            self.ts('dve', self.omu_fm[:], self.mu_fm[:], -1.0, 1.0, ALU.mult, ALU.add, ['rw_vecs'], ['rw_vecs'])
            self.vecs = L("rw_vecs", [128, 5, NKC], F32)
            p.dma('sp', self.vecs[:], self.inputs['rwkv_vecs'], [], ['rw_vecs'])
            self.lw2 = L("lw2", [128, D], F32)
            p.dma('sp', self.lw2[:], self.inputs['rwkv_lw2'], [], ['lw2'])
            self.gng_bc = L("gng_bc", [128, D], BF16)
            self.gnb_bc = L("gnb_bc", [128, D], BF16)
            self.Wva = L("Wva", [128, NKC, D], BF16)
            self.Wvb = L("Wvb", [128, NKC, D], BF16)
            src = self.w_in_dram['rwkv']

            def loader(tes):
                muv = tes.enter_context(self.nc.sbuf_tensor("muv_bc", [128, D], F32))
                for (dst, nm) in ((self.gng_bc, 'rwkv_gn_g'), (self.gnb_bc, 'rwkv_gn_b')):
                    p.dma('sp', muv[:], self.inputs[nm].partition_broadcast(128), [], ['muv'])
                    self.copy('dve', dst[:], muv[:], ['muv'], ['bc_tiles'])
                p.dma('sp', muv[:], self.inputs['rwkv_mu'][2 * D:3 * D].partition_broadcast(128), ['muv'], ['bc_tiles', 'muv'])
                self.load_weight_bf16(self.Wvb, 'Wv', src, D, src_c0=2 * D, scale_bc=muv)
                self.ts('dve', muv[:], muv[:], -1.0, 1.0, ALU.mult, ALU.add, ['bc_tiles'], ['bc_tiles'])
                self.load_weight_bf16(self.Wva, 'Wv', src, D, src_c0=2 * D, scale_bc=muv)
                self.load_weight_bf16(self.W_in, 'W_in', src, 2 * D, src_c0=0, dst_c0=0)
                self.load_weight_bf16(self.W_in, 'W_in', src, 128, src_c0=3 * D, dst_c0=2 * D)
                self.load_weight_bf16(self.W_in, 'W_in', src, D, src_c0=3 * D + 128, dst_c0=2 * D + 128)
                self.load_weight_bf16(self.W_out, 'W_out', self.w_out_dram['rwkv'], D)
            self.S = L("S_rwkv", [128, NKC, 64], F32)
            p.op('pool', lambda e: e.memset(self.S[:], 0.0), [], [('S', j) for j in range(NKC)])
            self.pcar = L("pcar", [128, 25], F32)
            p.op('pool', lambda e: e.memset(self.pcar[:], 0.0), [], [('pcar', i) for i in range(25)])
            self.pm_ext = [L(f"pm_ext{i}", [128, TT + 1], F32) for i in range(2)]
            self.pm_i = 0
            names = ['lo', 'r', 'k', 'tmp', 'sigw', 'a', 'kk', 'rn', 'kmod', 'bbv', 'c', 'e1', 'e2']
            self.tmp = {n: L(f"w_{n}", [128, TT], F32) for n in names}
            self.tmp2 = [dict(), dict()]
            for n in ['khat', 'bhat']:
                self.tmp2[0][n] = L(f"w_{n}0", [128, TT], F32)
            for n in ['rt_bf', 'bt_bf', 'at_bf', 'kt_h0', 'kt_h1', 'bt_h0', 'bt_h1', 'at_h0', 'at_h1']:
                self.tmp2[0][n] = L(f"w_{n}0", [128, TT], BF16)
            self.hm = L("hm", [128, 2], F32)
            p.op('pool', lambda e: e.memset(self.hm[:], 0.0), [], ['masks'])
            p.op('pool', lambda e: e.memset(self.hm[0:64, 0:1], 1.0), ['masks'], ['masks'])
            p.op('pool', lambda e: e.memset(self.hm[64:128, 1:2], 1.0), ['masks'], ['masks'])
            self.pt = {n: [L(f"wp_{n}{q}", [128, TT], F32 if n in ('at', 'rt') else BF16) for q in range(2)] for n in ['at', 'rt', 'rkr', 'sgate']}
            self.blockones_bf = L("blockones_bf", [128, 128], BF16)
            self.copy('dve', self.blockones_bf[:], self.blockones[:], ['masks'], ['masks'])
            self.v_sb = [L(f"v_sb{q}", [128, NB, 128], F32) for q in range(2)]
            self.v_bf = [L(f"v_bf{q}", [128, NB, 128], BF16) for q in range(2)]
            self.dec = [L(f"dec{q}", [128, NB], F32) for q in range(3)]

            def post_setup():
                for n in ['khat', 'bhat']:
                    self.tmp2[1][n] = L(f"w_{n}1", [128, TT], F32)
                for n in ['rt_bf', 'bt_bf', 'at_bf', 'kt_h0', 'kt_h1', 'bt_h0', 'bt_h1', 'at_h0', 'at_h1']:
                    self.tmp2[1][n] = L(f"w_{n}1", [128, TT], BF16)
                for n in ['at', 'rt', 'rkr', 'sgate']:
                    self.pt[n].append(L(f"wp_{n}2", [128, TT], F32 if n in ('at', 'rt') else BF16))
                self.ysb = [L(f"ysb{i}", [128, 256], F32) for i in range(2)]
                self.yn2 = [self.yn, L("yn1", [128, 128], F32)]
                self.bon2 = [self.bon, L("bon1", [128, 128], F32)]
                self.gst2 = [self.gst, L("gst1", [128, 12], F32)]
                self.junk2 = [self.junk, L("junk1", [128, 64], F32)]
                self.bcount = 0
                self.v_sb.append(L("v_sb2", [128, NB, 128], F32))
                self.v_bf.append(L("v_bf2", [128, NB, 128], BF16))
            self.post_setup = post_setup
            NCHN = NB * 2
            self.Pb = [L(f"Pb{i}", [128, NCHN, 128], NDT) for i in range(2)]
            self.Qb = [L(f"Qb{i}", [128, NCHN, 128], NDT) for i in range(2)]
            self.NT = [L(f"NT{q}", [128, NCHN, 128], NDT) for q in range(2)]
            self.Aak = [L(f"Aak{q}", [128, NCHN, 128], BF16) for q in range(2)]
            self.Ark = [L(f"Ark{q}", [128, NCHN, 128], BF16) for q in range(2)]
            self.Arb = [L(f"Arb{q}", [128, NCHN, 128], BF16) for q in range(2)]
            self.ident4 = L("ident4", [128, NCHN, 128], NDT)
            for c in range(NCHN):
                self.copy('dve', self.ident4[:, c, :], self.ident[:], ['ident'], ['ident'])
            self.khm = [[L(f"khm{q}{b}", [128, 128], BF16) for b in range(NB)] for q in range(2)]
            self.bhm = [[L(f"bhm{q}{b}", [128, 128], BF16) for b in range(NB)] for q in range(2)]
            self.Z_sb = L("Z_sb", [128, 128], NDT)
            self.U_bf = L("U_bf", [128, 128], BF16)
            self.yn = L("yn", [128, 128], F32)
            self.bon = L("bon", [128, 128], F32)
            self.gst = L("gst", [128, 12], F32)
            self.junk = L("junk", [128, 64], F32)
            self.ps_pool = [0, 1, 2]
            self.nm_rr = 0
            return loader

        NMB = [3, 4, 7]

        def nm_ps():
            i = NMB[self.nm_rr % len(NMB)]
            self.nm_rr += 1
            return self.psums[i], f"ps{i}"

        def shift(ps, pk, dst, dk, idx, mt):
            pm = self.pm_ext[self.pm_i % 2]
            pmk = f"pm_ext{self.pm_i % 2}"
            self.pm_i += 1
            ck = ('pcar', idx)
            self.copy('pool', pm[:, 0:1], self.pcar[:, idx:idx + 1], [ck], [pmk])
            self.act(pm[:, 1:TT + 1], ps[:, :TT], AF.Copy, [pk, 'rw_vecs'], [pmk], scale=self.mu_fm[:, mt:mt + 1])
            self.copy('pool', self.pcar[:, idx:idx + 1], pm[:, TT:TT + 1], [pmk], [ck])
            self.act(dst, ps[:, :TT], AF.Copy, [pk, 'rw_vecs'], [dk], scale=self.omu_fm[:, mt:mt + 1])
            self.tt('dve', dst, dst, pm[:, 0:TT], ALU.add, [dk, pmk], [dk])

        def proj(col0):
            ps, pk = self.next_ps()
            for kc in range(NKC):
                self.mm(ps[:, :TT], self.W_in[:, kc, col0:col0 + 128], self.hn[:, kc, 1:TT + 1], kc == 0, kc == NKC - 1, ['W_in', self.hnk], [pk])
            return ps, pk

        hsl = [slice(0, 64), slice(64, 128)]

        def A_gen(j):
            V = self.vecs
            q = j % 2
            q3 = j % 3
            t = dict(self.tmp)
            t.update(self.tmp2[q])
            PAR = set(self.tmp2[0].keys())
            jc = slice(j * 128, (j + 1) * 128)
            at, rt, rkr, sgate = (self.pt[n][q3] for n in ('at', 'rt', 'rkr', 'sgate'))
            kat, krt, krkr, ksg = (f'p_{n}{q3}' for n in ('at', 'rt', 'rkr', 'sgate'))
            v_sb, v_bf, dec = self.v_sb[q3], self.v_bf[q3], self.dec[q3]
            kv, kvb, kdec = f'v_sb{q3}', f'v_bf{q3}', f'dec{q3}'
            ps, pk = proj(j * 128)
            shift(ps, pk, t['r'][:], 't_r', j, j)
            ps, pk = proj(D + j * 128)
            shift(ps, pk, t['k'][:], 't_k', 8 + j, 8 + j)
            yield
            ps, pk = proj(2 * D + 128 + j * 128)
            shift(ps, pk, t['tmp'][:], 't_tmp', 16 + j, 25 + j)
            self.act(sgate[:], t['tmp'][:], AF.Silu, ['t_tmp'], [ksg])
            pv, pvk = self.next_ps()
            for blk in range(NB):
                n = 0
                for kc in range(NKC):
                    for (Wv, off) in ((self.Wva, 1), (self.Wvb, 0)):
                        self.mm(pv[:, blk * 128:(blk + 1) * 128], self.hn[:, kc, off + blk * 128:off + (blk + 1) * 128], Wv[:, kc, jc],
                                n == 0, n == 2 * NKC - 1, ['Wv', self.hnk], [pvk])
                        n += 1
            self.copy('act', v_sb[:].rearrange("p b v -> p (b v)"), pv[:, :TT], [pvk], [kv])
            self.copy('dve', v_bf[:].rearrange("p b v -> p (b v)"), pv[:, :TT], [pvk], [kvb])
            yield
            pw, pwk = self.next_ps()
            self.mm(pw[:, :TT], self.lw2[0:64, jc], t['lo'][0:64, :], True, True, ['lw2', 't_lo'], [pwk])
            self.act(t['sigw'][:], pw[:, :TT], AF.Sigmoid, [pwk, 'rw_vecs'], ['t_sigw'], bias=V[:, 0, j:j + 1])
            pa, pak = self.next_ps()
            self.mm(pa[:, :TT], self.lw2[64:128, jc], t['lo'][64:128, :], True, True, ['lw2', 't_lo'], [pak])
            self.act(t['a'][:], pa[:, :TT], AF.Sigmoid, [pak, 'rw_vecs'], ['t_a'], bias=V[:, 1, j:j + 1])
            self.ts('dve', t['kk'][:], t['k'][:], V[:, 2, j:j + 1], None, ALU.mult, None, ['t_k', 'rw_vecs'], ['t_kk'])
            self.tt('pool', t['tmp'][:], t['kk'][:], t['kk'][:], ALU.mult, ['t_kk'], ['t_tmp'])
            pn, pnk = self.next_ps()
            self.mm(pn[:, :TT], self.blockones[:], t['tmp'][:], True, True, ['masks', 't_tmp'], [pnk])
            self.ts('dve', t['rn'][:], pn[:, :TT], 1e-24, None, ALU.max, None, [pnk], ['t_rn'])
            self.act(t['rn'][:], t['rn'][:], AF.Ln, ['t_rn'], ['t_rn'])
            self.act(t['rn'][:], t['rn'][:], AF.Exp, ['t_rn'], ['t_rn'], scale=-0.5)
            self.tt('pool', t['kk'][:], t['kk'][:], t['rn'][:], ALU.mult, ['t_kk', 't_rn'], ['t_kk'])
            self.ts('dve', t['tmp'][:], t['a'][:], -1.0, V[:, 3, j:j + 1], ALU.add, ALU.mult, ['t_a', 'rw_vecs'], ['t_tmp'])
            self.stt('dve', t['kmod'][:], t['tmp'][:], 1.0, t['k'][:], ALU.add, ALU.mult, ['t_tmp', 't_k'], ['t_kmod'])
            self.tt('pool', t['bbv'][:], t['kk'][:], t['a'][:], ALU.mult, ['t_kk', 't_a'], ['t_bbv'])
            yield
            self.p.op('dve', lambda e: e.tensor_tensor_scan(out=t['c'][:], data0=self.resetm[:], data1=t['sigw'][:], initial=0.0,
                                                            op0=ALU.mult, op1=ALU.add), ['t_sigw', 'masks'], ['t_c'])
            self.act(t['e1'][:], t['c'][:], AF.Exp, ['t_c'], ['t_e1'], scale=LC)
            self.tt('pool', rt[:], t['r'][:], t['e1'][:], ALU.mult, ['t_r', 't_e1'], [krt])
            self.copy('act', t['rt_bf'][:], rt[:], [krt], [f't_rt_bf{q}'])
            self.act(t['e2'][:], t['c'][:], AF.Exp, ['t_c'], ['t_e2'], scale=-LC)
            for hd in range(2):
                self.stt('dve', t[f'kt_h{hd}'][:], t['kmod'][:], self.hm[:, hd:hd + 1], t['e2'][:], ALU.mult, ALU.mult,
                         ['t_kmod', 't_e2', 'masks'], [f't_kt_h{hd}_{q}'])
                self.stt('dve', t[f'bt_h{hd}'][:], t['bbv'][:], self.hm[:, hd:hd + 1], t['e2'][:], ALU.mult, ALU.mult,
                         ['t_bbv', 't_e2', 'masks'], [f't_bt_h{hd}_{q}'])
            self.tt('pool', t['bt_bf'][:], t['bbv'][:], t['e2'][:], ALU.mult, ['t_bbv', 't_e2'], [f't_bt_bf{q}'])
            self.tt('pool', t['e1'][:], t['c'][:], t['sigw'][:], ALU.subtract, ['t_c', 't_sigw'], ['t_e1'])
            self.act(t['e1'][:], t['e1'][:], AF.Exp, ['t_e1'], ['t_e1'], scale=LC)
            self.stt('dve', at[:], t['kk'][:], -1.0, t['e1'][:], ALU.mult, ALU.mult, ['t_kk', 't_e1'], [kat])
            self.copy('act', t['at_bf'][:], at[:], [kat], [f't_at_bf{q}'])
            for hd in range(2):
                self.act(t[f'at_h{hd}'][:], at[:], AF.Copy, [kat, 'masks'], [f't_at_h{hd}_{q}'], scale=self.hm[:, hd:hd + 1])
            yield
            c3 = t['c'][:].rearrange("p (n c) -> p n c", c=128)
            self.tt('pool', t['e2'][:].rearrange("p (n c) -> p n c", c=128), c3[:, :, 127:128].to_broadcast([128, NB, 128]), c3,
                    ALU.subtract, ['t_c'], ['t_e2'])
            self.act(t['e2'][:], t['e2'][:], AF.Exp, ['t_e2'], ['t_e2'], scale=LC)
            self.act(dec[:], c3[:, :, 127], AF.Exp, ['t_c'], [kdec], scale=LC)
            self.tt('pool', t['khat'][:], t['kmod'][:], t['e2'][:], ALU.mult, ['t_kmod', 't_e2'], [f't_khat{q}'])
            self.tt('dve', t['bhat'][:], t['bbv'][:], t['e2'][:], ALU.mult, ['t_bbv', 't_e2'], [f't_bhat{q}'])
            self.stt('dve', rkr[:], t['r'][:], V[:, 4, j:j + 1], t['kmod'][:], ALU.mult, ALU.mult, ['t_r', 'rw_vecs', 't_kmod'], [krkr])
            yield
            NCH = NB * 2
            specs = {'P': ('bt_h', 'at_bf', self.maskS, self.Pb[0], 'Pb0'),
                     'Q': ('at_h', 'bt_bf', self.maskSL, self.Qb[0], 'Qb0'),
                     'ak': ('kt_h', 'at_bf', self.maskS, self.Aak[q], f'Aak{q}'),
                     'rk': ('kt_h', 'rt_bf', self.maskI, self.Ark[q], f'Ark{q}'),
                     'rb': ('bt_h', 'rt_bf', self.maskI, self.Arb[q], f'Arb{q}')}
            for name in ('P', 'Q', 'ak', 'rk', 'rb'):
                lh, rh, mask, dst, dk = specs[name]
                ps, pk = nm_ps()
                for c in range(NCH):
                    blk, hd = c // 2, c % 2
                    cs = slice(blk * 128, (blk + 1) * 128)
                    self.mm(ps[:, c * 128:(c + 1) * 128], t[f'{lh}{hd}'][:, cs], t[rh][:, cs], True, True, [f't_{lh}{hd}_{q}', f't_{rh}{q}'], [pk])
                self.tt('dve', dst[:], ps[:, 0:NCH * 128].rearrange("p (c t) -> p c t", c=NCH),
                        mask[:, None, :].to_broadcast([128, NCH, 128]), ALU.mult, [pk, 'masks'], [dk])
                if name == 'Q':
                    self.tt('pool', self.NT[q][:], self.ident4[:], self.Pb[0][:], ALU.add, ['ident', 'Pb0'], [f'NT{q}'])
                    yield
            for blk in range(NB):
                cs = slice(blk * 128, (blk + 1) * 128)
                for (srcn, dst, dk) in (('khat', self.khm[q][blk], f'khm{q}{blk}'), ('bhat', self.bhm[q][blk], f'bhm{q}{blk}')):
                    ps, pk = nm_ps()
                    self.p.op('pe', lambda e, ps=ps, srcn=srcn, cs=cs: e.transpose(ps[:, 0:128], t[srcn][:, cs], self.ident[:]),
                              [f't_{srcn}{q}', 'ident'], [pk])
                    self.copy('act', dst[:], ps[:, 0:128], [pk], [dk])
            yield
            NTq, kNT = self.NT[q], f'NT{q}'
            for i in range(6):
                a_, b_ = i % 2, (i + 1) % 2
                Pa, Qa, Pn, Qn = self.Pb[a_], self.Qb[a_], self.Pb[b_], self.Qb[b_]
                kPa, kQa, kPn, kQn = f'Pb{a_}', f'Qb{a_}', f'Pb{b_}', f'Qb{b_}'
                if i < 5:
                    ps, pk = nm_ps()
                    for c in range(NCH):
                        self.mm(ps[:, c * 128:(c + 1) * 128], Qa[:, c, :], Pa[:, c, :], True, True, [kQa, kPa], [pk])
                    self.copy('act', Pn[:].rearrange("p c t -> p (c t)"), ps[:, 0:NCH * 128], [pk], [kPn])
                ps, pk = nm_ps()
                for c in range(NCH):
                    self.mm(ps[:, c * 128:(c + 1) * 128], Pa[:, c, :], Qa[:, c, :], True, True, [kPa, kQa], [pk])
                self.copy('dve' if i % 2 == 0 else 'act', Qn[:].rearrange("p c t -> p (c t)"), ps[:, 0:NCH * 128], [pk], [kQn])
                yield
                ps, pk = nm_ps()
                for c in range(NCH):
                    self.mm(ps[:, c * 128:(c + 1) * 128], Qn[:, c, :], NTq[:, c, :], True, True, [kQn, kNT], [pk])
                self.tt('dve', NTq[:].rearrange("p c t -> p (c t)"), NTq[:].rearrange("p c t -> p (c t)"), ps[:, 0:NCH * 128], ALU.add,
                        [kNT, pk], [kNT])
                yield

        def B_gen(j):
            t = self.tmp
            P = self.psums
            q = j % 2
            q3 = j % 3
            jc = slice(j * 128, (j + 1) * 128)
            at, rt, rkr, sgate = (self.pt[n][q3] for n in ('at', 'rt', 'rkr', 'sgate'))
            kat, krt, krkr, ksg = (f'p_{n}{q3}' for n in ('at', 'rt', 'rkr', 'sgate'))
            v_sb, v_bf, dec = self.v_sb[q3], self.v_bf[q3], self.dec[q3]
            kv, kvb, kdec = f'v_sb{q3}', f'v_bf{q3}', f'dec{q3}'
            kS = ('S', j)
            for blk in range(NB):
                cs = slice(blk * 128, (blk + 1) * 128)
                khm, bhm = self.khm[q][blk], self.bhm[q][blk]
                kkh, kbh = f'khm{q}{blk}', f'bhm{q}{blk}'
                pz, pzk = P[5], 'ps5'
                for hd in range(2):
                    hs, hc, c = hsl[hd], slice(hd * 64, (hd + 1) * 64), blk * 2 + hd
                    self.mm(pz[:, hc], self.Aak[q][:, c, :], v_bf[:, blk, hc], True, False, [f'Aak{q}', kvb], [pzk])
                    self.mm(pz[:, hc], at[hs, cs], self.S[hs, j, :], False, True, [kat, kS], [pzk])
                self.copy('act', self.Z_sb[:], pz[:, 0:128], [pzk], ['Z_sb'])
                yield
                for hd in range(2):
                    hc, c = slice(hd * 64, (hd + 1) * 64), blk * 2 + hd
                    self.mm(pz[:, hc], self.NT[q][:, c, :], self.Z_sb[:, hc], True, True, [f'NT{q}', 'Z_sb'], [pzk])
                self.copy('act', self.U_bf[:], pz[:, 0:128], [pzk], ['U_bf'])
                yield
                self.mm(pz[:, 0:128], khm[:], v_bf[:, blk, :], True, False, [kkh, kvb], [pzk])
                self.mm(pz[:, 0:128], bhm[:], self.U_bf[:], False, True, [kbh, 'U_bf'], [pzk])
                py, pyk = P[6], 'ps6'
                for hd in range(2):
                    hs, hc, c = hsl[hd], slice(hd * 64, (hd + 1) * 64), blk * 2 + hd
                    yc = slice(hd * 128, hd * 128 + 64)
                    bc_ = slice(hd * 128 + 64, hd * 128 + 128)
                    self.mm(py[:, yc], self.Ark[q][:, c, :], v_bf[:, blk, hc], True, False, [f'Ark{q}', kvb], [pyk])
                    self.mm(py[:, yc], rt[hs, cs], self.S[hs, j, :], False, False, [krt, kS], [pyk])
                    self.mm(py[:, yc], self.Arb[q][:, c, :], self.U_bf[:, hc], False, True, [f'Arb{q}', 'U_bf'], [pyk])
                    self.mm(py[:, bc_], rkr[hs, cs], self.blockones_bf[hs, hs], True, True, [krkr, 'masks'], [pyk])
                for hd in range(2):
                    hs, hc = hsl[hd], slice(hd * 64, (hd + 1) * 64)
                    self.stt('dve', self.S[hs, j, :], self.S[hs, j, :], dec[hs, blk:blk + 1], pz[hs, hc], ALU.mult, ALU.add,
                             [kS, kdec, pzk], [kS])
                yield
                bp = self.bcount % 2
                self.bcount += 1
                ysb, kys = self.ysb[bp], f'ysb{bp}'
                g, kg = self.gst2[bp], f'gst{bp}'
                yn, kyn = self.yn2[bp], f'yn{bp}'
                bon, kbon = self.bon2[bp], f'bon{bp}'
                junk, kjunk = self.junk2[bp], f'junk{bp}'
                self.copy('act', ysb[:], py[:, 0:256], [pyk], [kys])
                for hd in range(2):
                    hc = slice(hd * 64, (hd + 1) * 64)
                    yc = slice(hd * 128, hd * 128 + 64)
                    bc_ = slice(hd * 128 + 64, hd * 128 + 128)
                    self.act(junk[:], ysb[:, yc], AF.Identity, [kys], [kjunk, kg], accum_out=g[:, hd:hd + 1])
                    self.act(junk[:], ysb[:, yc], AF.Square, [kys], [kjunk, kg], accum_out=g[:, 2 + hd:3 + hd])
                    self.tt('pool', bon[:, hc], ysb[:, bc_], v_sb[:, blk, hc], ALU.mult, [kys, kv], [kbon])
                self.ts('dve', g[:, 4:6], g[:, 0:2], 1.0 / 64, None, ALU.mult, None, [kg], [kg])
                self.tt('dve', g[:, 6:8], g[:, 4:6], g[:, 4:6], ALU.mult, [kg], [kg])
                self.stt('dve', g[:, 8:10], g[:, 2:4], 1.0 / 64, g[:, 6:8], ALU.mult, ALU.subtract, [kg], [kg])
                self.act(g[:, 8:10], g[:, 8:10], AF.Ln, [kg, 'consts'], [kg], bias=self.epsc[:, 1:2])
                self.act(g[:, 8:10], g[:, 8:10], AF.Exp, [kg], [kg], scale=-0.5)
                for hd in range(2):
                    hc = slice(hd * 64, (hd + 1) * 64)
                    yc = slice(hd * 128, hd * 128 + 64)
                    self.ts('dve', yn[:, hc], ysb[:, yc], g[:, 4 + hd:5 + hd], g[:, 8 + hd:9 + hd], ALU.subtract, ALU.mult,
                            [kys, kg], [kyn])
                yield
                self.tt('pool', yn[:], yn[:], self.gng_bc[:, jc], ALU.mult, [kyn, 'bc_tiles'], [kyn])
                self.tt('pool', yn[:], yn[:], self.gnb_bc[:, jc], ALU.add, [kyn, 'bc_tiles'], [kyn])
                self.tt('pool', yn[:], yn[:], bon[:], ALU.add, [kyn, kbon], [kyn])
                ps, pk = nm_ps()
                self.p.op('pe', lambda e, ps=ps, yn=yn: e.transpose(ps[:, 0:128], yn[:], self.ident[:]), [kyn, 'ident'], [pk])
                self.tt('dve', self.yTt[:, j, cs], ps[:, 0:128], sgate[:, cs], ALU.mult, [pk, ksg], [self.yTk])
                yield

        def drive(gens):
            gens = [g for g in gens if g is not None]
            while gens:
                for g in list(gens):
                    try:
                        next(g)
                    except StopIteration:
                        gens.remove(g)

        def tile(ti):
            t = self.tmp
            ps, pk = proj(2 * D)
            shift(ps, pk, t['lo'][:], 't_lo', 24, 24)
            self.act(t['lo'][0:64, :], t['lo'][0:64, :], AF.Tanh, ['t_lo'], ['t_lo'])
            for j in range(NKC):
                drive([A_gen(j)])
                drive([B_gen(j)])

        self.run_layer(li, 'rwkv', TT, WC, setup, tile, is_last)
        self.ps_pool = list(range(8))

    def build(self):
        nc = self.nc
        T = self.T
        self.xT = self.din("xT", [D, T])
        self.yT = nc.dram_tensor("yT", [D, T], F32, kind="ExternalOutput").ap()
        d_norm_g = self.din("norm_g", [128, 4, NKC])
        d_final_g = self.din("final_g", [128, NKC])
        self.w_in_dram, self.w_out_dram = {}, {}
        kinds = [k for (_, k) in self.layers]
        if 'conv' in kinds:
            self.w_in_dram['conv'] = self.din("conv_w_in", [D, 4 * D])
            self.w_out_dram['conv'] = self.din("conv_w_out", [D, D])
            d_conv_w = self.din("conv_w", [128, NKC, 3])
        if 'rwkv' in kinds:
            self.w_in_dram['rwkv'] = self.din("rwkv_w_in", [D, 4 * D + 128])
            self.w_out_dram['rwkv'] = self.din("rwkv_w_out", [D, D])
            self.din("rwkv_mu_fm", [128, 33])
            self.din("rwkv_mu", [4 * D + 128])
            self.din("rwkv_vecs", [128, 5, NKC])
            self.din("rwkv_lw2", [128, D])
            self.din("rwkv_gn_g", [D])
            self.din("rwkv_gn_b", [D])
        if 'hgrn' in kinds:
            self.w_in_dram['hgrn'] = self.din("hgrn_w_in", [D, 4 * D])
            self.w_out_dram['hgrn'] = self.din("hgrn_w_out", [D, D])
            self.din("hgrn_gn_g", [D])
            self.din("hgrn_lbl", [128, 4, NKC])
        if 'gmlp' in kinds:
            self.w_in_dram['gmlp'] = self.din("gmlp_w_in", [D, 3 * D])
            self.w_out_dram['gmlp'] = self.din("gmlp_w_out", [D, D])
            self.din("gmlp_wsT", [128, 8, 128])
            self.din("gmlp_bs", [8, 128])
            self.din("gmlp_vg", [D])
        with ExitStack() as es:
            self.es = es
            nc.allow_low_precision("bf16 matmul operands, fp32 accumulation")
            self.p = p = Prog(nc, es)
            self.psums = [es.enter_context(nc.psum_tensor(f"ps{i}", [128, 512], F32)) for i in range(8)]
            self.ps_rr = 0
            self.ps_pool = list(range(8))
            self.ones_bf = self.sb("ones_bf", [128, 128], BF16)
            self.epsc = self.sb("epsc", [128, 4], F32)
            self.norm_g = self.sb("norm_g_sb", [128, 4, NKC], F32)
            self.final_g = self.sb("final_g_sb", [128, NKC], F32)
            p.op('pool', lambda e: e.memset(self.ones_bf[:], 1.0), [], ['ones_bf'])
            p.op('pool', lambda e: e.memset(self.epsc[:, 0:1], RMS_EPS), [], ['consts'])
            p.op('pool', lambda e: e.memset(self.epsc[:, 2:3], 1.0), ['consts'], ['consts'])
            p.dma('sp', self.norm_g[:], d_norm_g, [], ['consts'])
            p.dma('sp', self.final_g[:], d_final_g, [], ['consts'])
            if 'conv' in kinds:
                self.conv_w = self.sb("conv_w_sb", [128, NKC, 3], F32)
                p.dma('sp', self.conv_w[:], d_conv_w, [], ['consts'])
            self.first_layer = True
            for n, (li, kind) in enumerate(self.layers):
                is_last = n == len(self.layers) - 1
                if kind == 'conv':
                    self.conv_layer(li, is_last)
                elif kind == 'gmlp':
                    self.gmlp_layer(li, is_last)
                elif kind == 'hgrn':
                    self.hgrn_layer(li, is_last)
                elif kind == 'rwkv':
                    self.rwkv_layer(li, is_last)
                else:
                    raise ValueError(kind)
            p.finish('sp')
            self.stats = (p.n_ins, p.n_wait)
        return nc


def prep_inputs(inp, b, layers):
    f = np.float32
    m = {}
    m["xT"] = np.ascontiguousarray(np.asarray(inp["x"][b], f).T)
    m["norm_g"] = np.ascontiguousarray(np.asarray(inp["norm_g"], f).reshape(4, NKC, 128).transpose(2, 0, 1))
    m["final_g"] = np.ascontiguousarray(np.asarray(inp["final_g"], f).reshape(NKC, 128).T)
    kinds = [k for (_, k) in layers]
    if 'conv' in kinds:
        m["conv_w_in"] = np.ascontiguousarray(np.asarray(inp["conv_w_in"][0], f))
        m["conv_w_out"] = np.ascontiguousarray(np.asarray(inp["conv_w_out"][0], f))
        m["conv_w"] = np.ascontiguousarray(np.asarray(inp["conv_w"][0], f).reshape(3, NKC, 128).transpose(2, 1, 0))
    if 'rwkv' in kinds:
        m["rwkv_w_in"] = np.ascontiguousarray(np.asarray(inp["rwkv_w_in"][0], f))
        m["rwkv_w_out"] = np.ascontiguousarray(np.asarray(inp["rwkv_w_out"][0], f))
        mu = np.asarray(inp["rwkv_mu"][0], f)
        m["rwkv_mu"] = np.ascontiguousarray(mu)
        m["rwkv_mu_fm"] = np.ascontiguousarray(mu.reshape(33, 128).T)
        vecs = np.stack([np.asarray(inp[k][0], f).reshape(NKC, 128) for k in
                         ("rwkv_w0", "rwkv_a0", "rwkv_k_k", "rwkv_k_a", "rwkv_r_k")], axis=0)
        m["rwkv_vecs"] = np.ascontiguousarray(vecs.transpose(2, 0, 1))
        m["rwkv_lw2"] = np.ascontiguousarray(np.concatenate([np.asarray(inp["rwkv_w_w2"][0], f), np.asarray(inp["rwkv_w_a2"][0], f)], axis=0))
        m["rwkv_gn_g"] = np.ascontiguousarray(np.asarray(inp["rwkv_gn_g"][0], f))
        m["rwkv_gn_b"] = np.ascontiguousarray(np.asarray(inp["rwkv_gn_b"][0], f))
    if 'hgrn' in kinds:
        m["hgrn_w_in"] = np.ascontiguousarray(np.asarray(inp["hgrn_w_in"][0], f))
        m["hgrn_w_out"] = np.ascontiguousarray(np.asarray(inp["hgrn_w_out"][0], f))
        m["hgrn_gn_g"] = np.ascontiguousarray(np.asarray(inp["hgrn_gn_g"][0], f))
        m["hgrn_lbl"] = np.ascontiguousarray(np.asarray(inp["hgrn_lb_logits"], f).reshape(4, NKC, 128).transpose(2, 0, 1))
    if 'gmlp' in kinds:
        m["gmlp_w_in"] = np.ascontiguousarray(np.asarray(inp["gmlp_w_in"][0], f))
        m["gmlp_w_out"] = np.ascontiguousarray(np.asarray(inp["gmlp_w_out"][0], f))
        m["gmlp_wsT"] = np.ascontiguousarray(np.asarray(inp["gmlp_w_s"][0], f).transpose(2, 0, 1))
        m["gmlp_bs"] = np.ascontiguousarray(np.asarray(inp["gmlp_b_s"][0], f))
        m["gmlp_vg"] = np.ascontiguousarray(np.asarray(inp["gmlp_v_g"][0], f))
    return m


FULL_LAYERS = [(0, 'rwkv'), (1, 'hgrn'), (2, 'conv'), (3, 'gmlp')]


def kernel(**inputs):
    x = np.asarray(inputs["x"])
    B, T, _ = x.shape
    layers = FULL_LAYERS
    bld = Builder(T, layers)
    nc = bld.build()
    in_maps = []
    for c in range(8):
        in_maps.append(prep_inputs(inputs, c // 2, layers))
    res = run_bass_kernel_spmd(nc, in_maps, core_ids=list(range(8)))
    out = np.stack([np.asarray(res.results[2 * b]["yT"]).T for b in range(B)], axis=0)
    return out.astype(np.float32)
```

```python
import numpy as np
from contextlib import ExitStack
import concourse.bass as bass
import concourse.mybir as mybir
from concourse.bass_utils import run_bass_kernel_spmd

F32 = mybir.dt.float32
BF16 = mybir.dt.bfloat16
ALU = mybir.AluOpType
AF = mybir.ActivationFunctionType
AX = mybir.AxisListType

D = 1024
NKC = 8
RMS_EPS = 1e-6
GN_EPS = 64e-5


class Prog:
    LIMIT = 30000

    def __init__(self, nc, es, n_dma_sems=24):
        self.nc = nc
        self.es = es
        self.engs = {'pe': nc.tensor, 'act': nc.scalar, 'dve': nc.vector,
                     'pool': nc.gpsimd, 'sp': nc.sync}
        self.sems = {}
        self.epoch = {k: 0 for k in self.engs}
        self.cnt = {k: 0 for k in self.engs}
        for k in self.engs:
            self.sems[(k, 0)] = es.enter_context(nc.semaphore(f"s_{k}_0"))
        self.dma_sems = []
        for i in range(n_dma_sems):
            key = ('dma', i)
            self.sems[key] = es.enter_context(nc.semaphore(f"s_dma_{i}"))
            self.cnt[key] = 0
            self.dma_sems.append(key)
        self.dma_rr = 0
        self.waited = {k: {} for k in self.engs}
        self.bufs = {}
        self.n_wait = 0
        self.n_ins = 0

    def _deps(self, reads, writes):
        deps = set()
        for k in reads:
            b = self.bufs.get(k)
            if b and b['w']:
                deps.add(b['w'])
        for k in writes:
            b = self.bufs.get(k)
            if b:
                if b['w']:
                    deps.add(b['w'])
                deps.update(b['r'])
        return deps

    def _wait(self, eng, deps):
        e = self.engs[eng]
        best = {}
        for (sk, v) in deps:
            if sk[0] == eng and eng == 'pe':
                continue
            if best.get(sk, 0) < v:
                best[sk] = v
        for sk, v in best.items():
            if self.waited[eng].get(sk, 0) >= v:
                continue
            e.wait_ge(self.sems[sk], v)
            self.waited[eng][sk] = v
            self.n_wait += 1

    def _record(self, tok, reads, writes):
        for k in reads:
            b = self.bufs.setdefault(k, {'w': None, 'r': []})
            b['r'].append(tok)
            if len(b['r']) > 64:
                best = {}
                for (sk, v) in b['r']:
                    if best.get(sk, 0) < v:
                        best[sk] = v
                b['r'] = list(best.items())
        for k in writes:
            b = self.bufs.setdefault(k, {'w': None, 'r': []})
            b['w'] = tok
            b['r'] = []

    @staticmethod
    def _excl(reads, writes):
        ps = [k for k in reads if isinstance(k, str) and k.startswith('ps')]
        if ps:
            reads = [k for k in reads if k not in ps]
            writes = list(writes) + ps
        return reads, writes

    disabled = False
    recording = None
    SYNC_LAT = 0.45

    def begin_record(self):
        self.recording = []

    def flush(self):
        rec = self.recording
        self.recording = None
        if not rec:
            return
        n = len(rec)
        preds = [None] * n
        succs = [[] for _ in range(n)]
        last_w = {}
        readers = {}
        for i, (kind, eng, fn, reads, writes, cost, lat) in enumerate(rec):
            ps = set()
            for k in reads:
                w = last_w.get(k)
                if w is not None:
                    ps.add(w)
            for k in writes:
                w = last_w.get(k)
                if w is not None:
                    ps.add(w)
                ps.update(readers.get(k, ()))
            ps.discard(i)
            preds[i] = ps
            for pi in ps:
                succs[pi].append(i)
            for k in reads:
                readers.setdefault(k, []).append(i)
            for k in writes:
                last_w[k] = i
                readers[k] = []
        npred = [len(p_) for p_ in preds]
        ready = [i for i in range(n) if npred[i] == 0]
        eng_free = {}
        end_t = [0.0] * n
        done_t = [0.0] * n
        order = []
        blevel = [0.0] * n
        for i in range(n - 1, -1, -1):
            kind, eng, fn, reads, writes, cost, lat = rec[i]
            b = 0.0
            for si in succs[i]:
                v = blevel[si] + (self.SYNC_LAT if rec[si][1] != eng else 0.0)
                if v > b:
                    b = v
            blevel[i] = b + cost + lat

        def est(i):
            kind, eng, fn, reads, writes, cost, lat = rec[i]
            t = eng_free.get(eng, 0.0)
            for pi in preds[i]:
                tp = done_t[pi] + (self.SYNC_LAT if rec[pi][1] != eng else 0.0)
                if tp > t:
                    t = tp
            return t
        EPS = 0.25
        while ready:
            ests = [(est(i), i) for i in ready]
            tmin = min(ests)[0]
            best = None
            for (t, i) in ests:
                if t <= tmin + EPS:
                    if best is None or blevel[i] > blevel[best[1]] or (blevel[i] == blevel[best[1]] and i < best[1]):
                        best = (t, i)
            t1, i = best
            ready.remove(i)
            kind, eng, fn, reads, writes, cost, lat = rec[i]
            end_t[i] = t1 + cost
            done_t[i] = t1 + cost + lat
            eng_free[eng] = end_t[i]
            order.append(i)
            for si in succs[i]:
                npred[si] -= 1
                if npred[si] == 0:
                    ready.append(si)
        assert len(order) == n, (len(order), n)
        self.sched_span = max(done_t) if done_t else 0.0
        for i in order:
            kind, eng, fn, reads, writes, cost, lat = rec[i]
            if kind == 'op':
                self.op(eng, fn, reads, writes)
            else:
                out, in_, kw = fn
                self.dma(eng, out, in_, reads, writes, **kw)

    def op(self, eng, fn, reads=(), writes=(), cost=None):
        if self.disabled:
            return None
        if self.recording is not None:
            reads, writes = self._excl(reads, writes)
            if cost is None:
                cost = {'pe': 0.2, 'act': 0.45, 'dve': 0.45, 'pool': 0.7, 'sp': 0.1}[eng]
            self.recording.append(('op', eng, fn, list(reads), list(writes), cost, 0.0))
            return None
        reads, writes = self._excl(reads, writes)
        deps = self._deps(reads, writes)
        self._wait(eng, deps)
        ins = fn(self.engs[eng])
        if self.cnt[eng] >= self.LIMIT:
            self.epoch[eng] += 1
            ep = self.epoch[eng]
            self.sems[(eng, ep)] = self.es.enter_context(self.nc.semaphore(f"s_{eng}_{ep}"))
            self.cnt[eng] = 0
        sk = (eng, self.epoch[eng])
        self.cnt[eng] += 1
        ins.then_inc(self.sems[sk], 1)
        self._record((sk, self.cnt[eng]), reads, writes)
        self.n_ins += 1
        return ins

    def dma(self, eng, out, in_, reads=(), writes=(), **kw):
        if self.disabled:
            return None
        if self.recording is not None:
            self.recording.append(('dma', eng, (out, in_, kw), list(reads), list(writes), 0.15, 6.0))
            return None
        deps = self._deps(reads, writes)
        sk = self.dma_sems[self.dma_rr]
        self.dma_rr = (self.dma_rr + 1) % len(self.dma_sems)
        if self.cnt[sk] > 0:
            deps.add((sk, self.cnt[sk]))
        self._wait(eng, deps)
        ins = self.engs[eng].dma_start(out=out, in_=in_, **kw)
        self.cnt[sk] += 16
        ins.then_inc(self.sems[sk], 16)
        self._record((sk, self.cnt[sk]), reads, writes)
        self.n_ins += 1
        return ins

    def all_tokens(self):
        deps = set()
        for k, b in self.bufs.items():
            if b['w']:
                deps.add(b['w'])
            deps.update(b['r'])
        return deps

    def barrier(self):
        deps = self.all_tokens()
        for eng in self.engs:
            d = set(x for x in deps)
            self._wait(eng, d)

    def finish(self, eng='sp'):
        self._wait(eng, self.all_tokens())


class Builder:
    def __init__(self, T, layers, do_final=True, neu_dt=None):
        self.neu_dt = neu_dt if neu_dt is not None else BF16
        self.use_sched = True
        self.T = T
        self.layers = layers
        self.do_final = do_final
        self.nc = bass.Bass("TRN2", target_bir_lowering=False)
        self.inputs = {}

    def din(self, name, shape):
        t = self.nc.dram_tensor(name, list(shape), F32, kind="ExternalInput").ap()
        self.inputs[name] = t
        return t

    def sb(self, name, shape, dt=F32):
        return self.es.enter_context(self.nc.sbuf_tensor(name, list(shape), dt))

    def lsb(self, name, shape, dt=F32):
        return self.les.enter_context(self.nc.sbuf_tensor(f"{name}_{self.lname}", list(shape), dt))

    def next_ps(self):
        pool = self.ps_pool
        i = pool[self.ps_rr % len(pool)]
        self.ps_rr += 1
        return self.psums[i], f"ps{i}"

    @staticmethod
    def ecost(eng, ap):
        try:
            n = ap.free_size()
        except Exception:
            n = 256
        if eng == 'act':
            return 0.22 + n * 0.00075
        if eng == 'dve':
            return 0.2 + n * 0.00095
        if eng == 'pool':
            return 0.2 + n * 0.0021
        return 0.2

    def tt(self, eng, out, in0, in1, op, reads, writes):
        return self.p.op(eng, lambda e: e.tensor_tensor(out=out, in0=in0, in1=in1, op=op), reads, writes, cost=self.ecost(eng, out))

    def ts(self, eng, out, in0, s1, s2, op0, op1, reads, writes):
        if s2 is None:
            return self.p.op(eng, lambda e: e.tensor_scalar(out=out, in0=in0, scalar1=s1, scalar2=None, op0=op0), reads, writes, cost=self.ecost(eng, out))
        return self.p.op(eng, lambda e: e.tensor_scalar(out=out, in0=in0, scalar1=s1, scalar2=s2, op0=op0, op1=op1), reads, writes, cost=self.ecost(eng, out))

    def stt(self, eng, out, in0, scalar, in1, op0, op1, reads, writes):
        eng = 'dve'
        return self.p.op(eng, lambda e: e.scalar_tensor_tensor(out=out, in0=in0, scalar=scalar, in1=in1, op0=op0, op1=op1), reads, writes, cost=self.ecost(eng, out))

    def act(self, out, in_, func, reads, writes, bias=None, scale=1.0, accum_out=None):
        kw = {}
        if bias is not None:
            kw['bias'] = bias
        if accum_out is not None:
            kw['accum_out'] = accum_out
        return self.p.op('act', lambda e: e.activation(out=out, in_=in_, func=func, scale=scale, **kw), reads, writes, cost=self.ecost('act', in_))

    def mm(self, out, lhsT, rhs, start, stop, reads, writes):
        try:
            n = rhs.free_size()
        except Exception:
            n = 128
        c = 0.06 + n / 2400.0 * (1.0 if lhsT.dtype == BF16 else 2.4)
        return self.p.op('pe', lambda e: e.matmul(out, lhsT=lhsT, rhs=rhs, start=start, stop=stop), reads, writes, cost=c)

    def copy(self, eng, out, in_, reads, writes):
        if eng == 'act':
            return self.p.op('act', lambda e: e.copy(out=out, in_=in_), reads, writes, cost=self.ecost('act', out))
        return self.p.op(eng, lambda e: e.tensor_copy(out=out, in_=in_), reads, writes, cost=self.ecost(eng, out))

    def load_weight_bf16(self, dst, dst_key, src, ncols, src_c0=0, dst_c0=0, scale_bc=None):
        p = self.p
        CH = 1024 if ncols % 1024 == 0 else ncols
        for kc in range(NKC):
            for c0 in range(0, ncols, CH):
                i = self.stage_i
                self.stage_i += 1
                nst = len(self.stage)
                st = self.stage[i % nst]
                sk = f"stage{i % nst}"
                p.dma('sp', st[:, 0:CH], src[kc * 128:(kc + 1) * 128, src_c0 + c0:src_c0 + c0 + CH], reads=[], writes=[sk])
                eng = ['dve', 'act'][i % 2] if scale_bc is None else ['dve', 'pool'][i % 2]
                if scale_bc is None:
                    self.copy(eng, dst[:, kc, dst_c0 + c0:dst_c0 + c0 + CH], st[:, 0:CH], [sk], [dst_key])
                else:
                    self.tt(eng, dst[:, kc, dst_c0 + c0:dst_c0 + c0 + CH], st[:, 0:CH], scale_bc[:, c0:c0 + CH], ALU.mult,
                            [sk, 'bc_tiles'], [dst_key])

    def rms_rstd(self, src, src_key, TT, tag):
        bi = self.rms_i % len(self.sqb_l)
        self.rms_i += 1
        sqb, rstd = self.sqb_l[bi], self.rstd_l[bi]
        ksq, krs = f'sqb{bi}', f'rstd{bi}'
        if self.sqb_alias:
            ksq = 'yT0'
        for kc in range(NKC):
            if kc % 2 == 0:
                self.act(sqb[:, kc, :TT], src[:, kc, :TT], AF.Square, [src_key], [(ksq, kc) if not self.sqb_alias else ksq])
            else:
                self.tt('dve', sqb[:, kc, :TT], src[:, kc, :TT], src[:, kc, :TT], ALU.mult, [src_key], [(ksq, kc) if not self.sqb_alias else ksq])
        ps, pk = self.next_ps()
        for kc in range(NKC):
            self.mm(ps[:, :TT], self.ones_bf[:], sqb[:, kc, :TT], kc == 0, kc == NKC - 1, [(ksq, kc) if not self.sqb_alias else ksq, 'ones_bf'], [pk])
        self.act(rstd[:, :TT], ps[:, :TT], AF.Ln, [pk, 'consts'], [krs], bias=self.epsc[:, 0:1], scale=1.0 / D)
        self.act(rstd[:, :TT], rstd[:, :TT], AF.Exp, [krs], [krs], scale=-0.5)
        return rstd, krs

    def run_layer(self, li, kind, TT, w_in_cols, mixer_setup, mixer_tile, is_last):
        p = self.p
        T = self.T
        ntiles = T // TT
        with ExitStack() as les:
            self.les = les
            self.lname = f"L{li}"
            self.TT = TT
            self.W_in = self.lsb("W_in", [128, NKC, w_in_cols], BF16)
            self.W_out = self.lsb("W_out", [128, NKC, D], BF16)
            ndb = 2 if kind != 'rwkv' else 1
            self.hT = [self.lsb(f"hT{i}", [128, NKC, TT], F32) for i in range(ndb)]
            self.sqb_alias = (kind == 'rwkv')
            if not self.sqb_alias:
                self.sqb_l = [self.lsb(f"sqb{i}", [128, NKC, TT], BF16) for i in range(ndb)]
            self.rstd_l = [self.lsb(f"rstd{i}", [128, TT], F32) for i in range(ndb)]
            self.rms_i = 0
            self.hn_l = [self.lsb(f"hn{i}", [128, NKC, TT + 1], BF16) for i in range(ndb)]
            self.yTt_l = [self.lsb(f"yTt{i}", [128, NKC, TT], BF16) for i in range(ndb)]
            self.hn, self.hnk = self.hn_l[0], 'hn0'
            self.yTt, self.yTk = self.yTt_l[0], 'yT0'
            if self.sqb_alias:
                self.sqb_l = [self.yTt_l[0]]
            self.stage_i = 0
            loader = mixer_setup()
            with ExitStack() as ses:
                nst = 2 if kind == 'rwkv' else max(2, min(4, (self.nc.sbuf_bytes_remaining - 512) // 4096))
                self.stage = [ses.enter_context(self.nc.sbuf_tensor(f"stage{i}_{self.lname}", [128, 1024], F32)) for i in range(nst)]
                if loader is None:
                    self.load_weight_bf16(self.W_in, 'W_in', self.w_in_dram[kind], w_in_cols)
                    self.load_weight_bf16(self.W_out, 'W_out', self.w_out_dram[kind], D)
                else:
                    loader(ses)
                p.barrier()
            if getattr(self, 'post_setup', None) is not None:
                self.post_setup()
                self.post_setup = None
            hn0 = self.hn_l[0]
            p.op('pool', lambda e: e.memset(hn0[:, :, 0:1], 0.0), [], ['hn0'])

            def load(ti):
                buf = self.hT[ti % len(self.hT)]
                src = self.xT if self.first_layer else self.yT
                p.dma('sp', buf[:], src.rearrange("(c p) t -> p c t", p=128)[:, :, ti * TT:(ti + 1) * TT],
                      reads=[('hd', ti * TT // 128 + i) for i in range(TT // 128)], writes=[f"hT{ti % len(self.hT)}"])

            if self.use_sched:
                p.begin_record()
            load(0)
            for ti in range(ntiles):
                if len(self.hT) > 1:
                    if ti + 1 < ntiles:
                        load(ti + 1)
                elif ti > 0:
                    load(ti)
                h = self.hT[ti % len(self.hT)]
                hk = f"hT{ti % len(self.hT)}"
                rstd, rk = self.rms_rstd(h, hk, TT, 'in')
                g = self.norm_g
                prev_hn, prev_hnk = self.hn, self.hnk
                bi = ti % ndb
                self.hn, self.hnk = self.hn_l[bi], f'hn{bi}'
                self.yTt, self.yTk = self.yTt_l[bi], f'yT{bi}'
                if ti > 0:
                    self.copy('pool', self.hn[:, :, 0:1], prev_hn[:, :, TT:TT + 1], [prev_hnk], [self.hnk])
                for kc in range(NKC):
                    self.stt('dve', self.hn[:, kc, 1:TT + 1], h[:, kc, :], g[:, li, kc:kc + 1], rstd[:, :TT],
                             ALU.mult, ALU.mult, [hk, rk, 'consts'], [self.hnk])
                mixer_tile(ti)
                for j in range(NKC):
                    ps, pk = self.next_ps()
                    for kc in range(NKC):
                        self.mm(ps[:, :TT], self.W_out[:, kc, j * 128:(j + 1) * 128], self.yTt[:, kc, :TT],
                                kc == 0, kc == NKC - 1, ['W_out', self.yTk], [pk])
                    self.tt('dve', h[:, j, :], h[:, j, :], ps[:, :TT], ALU.add, [hk, pk], [hk])
                if is_last and self.do_final:
                    rstd, rk = self.rms_rstd(h, hk, TT, 'fin')
                    for kc in range(NKC):
                        self.stt('dve' if kc % 2 == 0 else 'pool', h[:, kc, :], h[:, kc, :], self.final_g[:, kc:kc + 1], rstd[:, :TT],
                                 ALU.mult, ALU.mult, [hk, rk, 'consts'], [hk])
                p.dma('sp', self.yT.rearrange("(c p) t -> p c t", p=128)[:, :, ti * TT:(ti + 1) * TT], h[:],
                      reads=[hk], writes=[('hd', (ti * TT) // 128 + i) for i in range(max(1, TT // 128))])
            if self.use_sched:
                p.flush()
            self.first_layer = False
            p.barrier()
        self.les = None

    def conv_layer(self, li, is_last):
        TT = 512

        def setup():
            self.yext = self.lsb("yext", [128, NKC, TT + 2], F32)
            self.zs = [self.lsb(f"zs{i}", [128, TT], F32) for i in range(2)]
            self.acc = [self.lsb(f"acc{i}", [128, TT], F32) for i in range(2)]
            self.sg = [self.lsb(f"sg{i}", [128, TT], F32) for i in range(2)]
            self.p.op('pool', lambda e: e.memset(self.yext[:, :, 0:2], 0.0), [], [('yext', j) for j in range(NKC)])

        def tile(ti):
            W = self.W_in
            cw = self.conv_w
            for j in range(NKC):
                zs, acc, sg = self.zs[j % 2], self.acc[j % 2], self.sg[j % 2]
                zk, ak, gk = f"zs{j % 2}", f"acc{j % 2}", f"sg{j % 2}"
                pss = []
                for blk in range(4):
                    ps, pk = self.next_ps()
                    col0 = blk * D + j * 128
                    for kc in range(NKC):
                        self.mm(ps[:, :TT], W[:, kc, col0:col0 + 128], self.hn[:, kc, 1:TT + 1], kc == 0, kc == NKC - 1,
                                ['W_in', self.hnk], [pk])
                    pss.append((ps, pk))
                (pb, pbk), (pc, pck), (pz, pzk), (pg, pgk) = pss
                yk = ('yext', j)
                self.copy('act', zs[:], pz[:, :TT], [pzk], [zk])
                if ti > 0:
                    self.copy('pool', self.yext[:, j, 0:2], self.yext[:, j, TT:TT + 2], [yk], [yk])
                self.tt('dve', self.yext[:, j, 2:TT + 2], pc[:, :TT], zs[:], ALU.mult, [pck, zk], [yk])
                self.act(acc[:], self.yext[:, j, 2:TT + 2], AF.Copy, [yk, 'consts'], [ak], scale=cw[:, j, 2:3])
                self.stt('pool', acc[:], self.yext[:, j, 1:TT + 1], cw[:, j, 1:2], acc[:], ALU.mult, ALU.add, [yk, ak, 'consts'], [ak])
                self.stt('pool', acc[:], self.yext[:, j, 0:TT], cw[:, j, 0:1], acc[:], ALU.mult, ALU.add, [yk, ak, 'consts'], [ak])
                self.act(sg[:], pg[:, :TT], AF.Silu, [pgk], [gk])
                self.tt('dve', acc[:], pb[:, :TT], acc[:], ALU.mult, [pbk, ak], [ak])
                self.tt('pool', self.yTt[:, j, :], acc[:], sg[:], ALU.mult, [ak, gk], [self.yTk])

        self.run_layer(li, 'conv', TT, 4 * D, setup, tile, is_last)

    def gmlp_layer(self, li, is_last):
        TT = 512

        def setup():
            p = self.p
            self.wsT = self.lsb("wsT", [128, 8, 128], F32)
            self.bs_bc = self.lsb("bs_bc", [128, 8, TT], F32)
            self.vg_bc = self.lsb("vg_bc", [128, D], F32)
            self.vn = [self.lsb(f"vn{i}", [128, D], F32) for i in range(TT // 128)]
            self.vss = self.lsb("vss", [128, 4], F32)
            self.junk = self.lsb("junk", [128, 512], F32)
            self.s_sb = [self.lsb(f"s_sb{i}", [128, TT], F32) for i in range(2)]
            self.sg = [self.lsb(f"sg{i}", [128, TT], F32) for i in range(2)]
            p.dma('sp', self.wsT[:], self.inputs['gmlp_wsT'], [], ['wsT'])
            for g in range(8):
                p.op('pool', lambda e: e.affine_select(out=self.wsT[:, g, :], in_=self.wsT[:, g, :], pattern=[[1, 128]],
                                                       compare_op=ALU.is_ge, fill=0.0, base=0, channel_multiplier=-1),
                     ['wsT'], ['wsT'])
            for r in range(TT // 128):
                p.dma('sp', self.bs_bc[:, :, r * 128:(r + 1) * 128],
                      self.inputs['gmlp_bs'].partition_broadcast(128), [], ['bs_bc'])
            p.dma('sp', self.vg_bc[:], self.inputs['gmlp_vg'].partition_broadcast(128), [], ['vg_bc'])

        def tile(ti):
            W = self.W_in
            nblk = TT // 128
            for blk in range(nblk):
                vn = self.vn[blk]
                vk = f"vn{blk}"
                halves = []
                for hf in range(2):
                    ps, pk = self.next_ps()
                    for kc in range(NKC):
                        self.mm(ps[:, :512], self.hn[:, kc, 1 + blk * 128:1 + (blk + 1) * 128],
                                W[:, kc, D + hf * 512:D + (hf + 1) * 512], kc == 0, kc == NKC - 1, ['W_in', self.hnk], [pk])
                    halves.append((ps, pk))
                for hf, (ps, pk) in enumerate(halves):
                    self.act(self.junk[:], ps[:, :512], AF.Square, [pk], ['junk', 'vss'], accum_out=self.vss[:, hf:hf + 1])
                self.tt('dve', self.vss[:, 2:3], self.vss[:, 0:1], self.vss[:, 1:2], ALU.add, ['vss'], ['vss'])
                self.act(self.vss[:, 3:4], self.vss[:, 2:3], AF.Ln, ['vss', 'consts'], ['vss'], bias=self.epsc[:, 0:1], scale=1.0 / D)
                self.act(self.vss[:, 3:4], self.vss[:, 3:4], AF.Exp, ['vss'], ['vss'], scale=-0.5)
                for hf, (ps, pk) in enumerate(halves):
                    self.stt('dve', vn[:, hf * 512:(hf + 1) * 512], ps[:, :512], self.vss[:, 3:4],
                             self.vg_bc[:, hf * 512:(hf + 1) * 512], ALU.mult, ALU.mult, [pk, 'vss', 'vg_bc'], [vk])
            for j in range(NKC):
                s_sb, sg = self.s_sb[j % 2], self.sg[j % 2]
                sk, gk = f"s_sb{j % 2}", f"sg{j % 2}"
                ps, pk = self.next_ps()
                for blk in range(nblk):
                    self.mm(ps[:, blk * 128:(blk + 1) * 128], self.vn[blk][:, j * 128:(j + 1) * 128], self.wsT[:, j, :], True, True,
                            [f"vn{blk}", 'wsT'], [pk])
                self.tt('dve', s_sb[:], ps[:, :TT], self.bs_bc[:, j, :], ALU.add, [pk, 'bs_bc'], [sk])
                pu, puk = self.next_ps()
                for kc in range(NKC):
                    self.mm(pu[:, :TT], W[:, kc, j * 128:(j + 1) * 128], self.hn[:, kc, 1:TT + 1], kc == 0, kc == NKC - 1,
                            ['W_in', self.hnk], [puk])
                pg, pgk = self.next_ps()
                for kc in range(NKC):
                    self.mm(pg[:, :TT], W[:, kc, 2 * D + j * 128:2 * D + (j + 1) * 128], self.hn[:, kc, 1:TT + 1], kc == 0,
                            kc == NKC - 1, ['W_in', self.hnk], [pgk])
                self.act(sg[:], pg[:, :TT], AF.Silu, [pgk], [gk])
                self.tt('dve', s_sb[:], pu[:, :TT], s_sb[:], ALU.mult, [puk, sk], [sk])
                self.tt('pool', self.yTt[:, j, :], s_sb[:], sg[:], ALU.mult, [sk, gk], [self.yTk])

        self.run_layer(li, 'gmlp', TT, 3 * D, setup, tile, is_last)


    def make_ident(self, ident, key):
        p = self.p
        p.op('pool', lambda e: e.memset(ident[:], 1.0), [], [key])
        p.op('pool', lambda e: e.affine_select(out=ident[:], in_=ident[:], pattern=[[-1, 128]], compare_op=ALU.is_equal,
                                               fill=0.0, base=0, channel_multiplier=1), [key], [key])

    def make_block_masks(self, C, maskT, colmask, rowmask, strict=False):
        p = self.p
        nch = 128 // C
        if maskT is not None:
            p.op('pool', lambda e: e.memset(maskT[:], 1.0), [], ['masks'])
            p.op('pool', lambda e: e.affine_select(out=maskT[:], in_=maskT[:], pattern=[[1, 128]], compare_op=ALU.is_ge if not strict else ALU.is_gt,
                                                   fill=0.0, base=0, channel_multiplier=-1), ['masks'], ['masks'])
            for c in range(1, nch):
                p.op('pool', lambda e, c=c: e.affine_select(out=maskT[:, c * C:(c + 1) * C], in_=maskT[:, c * C:(c + 1) * C], pattern=[[0, C]],
                                                            compare_op=ALU.is_ge, fill=0.0, base=-c * C, channel_multiplier=1), ['masks'], ['masks'])
        if colmask is not None:
            p.op('pool', lambda e: e.memset(colmask[:], 0.0), [], ['masks'])
            for c in range(nch):
                p.op('pool', lambda e, c=c: e.memset(colmask[:, c, c * C:(c + 1) * C], 1.0), ['masks'], ['masks'])
        if rowmask is not None:
            p.op('pool', lambda e: e.memset(rowmask[:], 1.0), [], ['masks'])
            for c in range(nch):
                p.op('pool', lambda e, c=c: e.affine_select(out=rowmask[:, c:c + 1], in_=rowmask[:, c:c + 1], pattern=[[0, 1]],
                                                            compare_op=ALU.is_ge, fill=0.0, base=-c * C, channel_multiplier=1), ['masks'], ['masks'])
                p.op('pool', lambda e, c=c: e.affine_select(out=rowmask[:, c:c + 1], in_=rowmask[:, c:c + 1], pattern=[[0, 1]],
                                                            compare_op=ALU.is_ge, fill=0.0, base=c * C + C - 1, channel_multiplier=-1), ['masks'], ['masks'])

    def hgrn_layer(self, li, is_last):
        TT = 256
        C = 32
        NB = TT // 128
        NCH = TT // C

        def setup():
            p = self.p
            L = self.lsb
            self.ident = L("ident", [128, 128], F32)
            self.make_ident(self.ident, 'ident')
            self.maskT = L("maskT", [128, 128], F32)
            self.colmask = L("colmask", [128, 4, 128], F32)
            self.rowmask = L("rowmask", [128, 4], F32)
            self.make_block_masks(C, self.maskT, self.colmask, self.rowmask)
            self.resetm = L("resetm", [128, TT], F32)
            p.op('pool', lambda e: e.memset(self.resetm[:], 1.0), [], ['masks'])
            p.op('pool', lambda e: e.memset(self.resetm[:].rearrange("p (n c) -> p n c", c=C)[:, :, 0:1], 0.0), ['masks'], ['masks'])
            self.gn_bc = L("gn_bc", [128, D], F32)
            p.dma('sp', self.gn_bc[:], self.inputs['hgrn_gn_g'].partition_broadcast(128), [], ['gn_bc'])
            self.lbl = L("lbl", [128, 4, NKC], F32)
            self.lbt = L("lbt", [128, 4, NKC], F32)
            p.dma('sp', self.lbl[:], self.inputs['hgrn_lbl'], [], ['lbl'])
            self.act(self.lbl[:], self.lbl[:], AF.Exp, ['lbl'], ['lbl'])
            self.tt('dve', self.lbt[:, 0, :], self.lbl[:, 0, :], self.lbl[:, 1, :], ALU.add, ['lbl'], ['lbt'])
            self.tt('dve', self.lbt[:, 0, :], self.lbt[:, 0, :], self.lbl[:, 2, :], ALU.add, ['lbl', 'lbt'], ['lbt'])
            self.tt('dve', self.lbt[:, 0, :], self.lbt[:, 0, :], self.lbl[:, 3, :], ALU.add, ['lbl', 'lbt'], ['lbt'])
            p.op('dve', lambda e: e.reciprocal(out=self.lbt[:, 3, :], in_=self.lbt[:, 0, :]), ['lbt'], ['lbt'])
            p.op('dve', lambda e: e.memset(self.lbt[:, 1, :], 0.0), ['lbt'], ['lbt'])
            for i in range(1, li + 1):
                self.tt('dve', self.lbt[:, 1, :], self.lbt[:, 1, :], self.lbl[:, i, :], ALU.add, ['lbl', 'lbt'], ['lbt'])
            self.tt('dve', self.lbt[:, 1, :], self.lbt[:, 1, :], self.lbt[:, 3, :], ALU.mult, ['lbt'], ['lbt'])
            self.ts('dve', self.lbt[:, 2, :], self.lbt[:, 1, :], -1.0, 1.0, ALU.mult, ALU.add, ['lbt'], ['lbt'])
            self.S = L("S_hgrn", [128, NKC, 128], F32)
            p.op('pool', lambda e: e.memset(self.S[:], 0.0), [], [('S', j) for j in range(NKC)])
            names = ['f', 'kk', 'bb', 'qe', 'dd', 'sg']
            self.tmps = []
            for q in range(2):
                tm = {n: L(f"h_{n}{q}", [128, TT], F32) for n in names}
                tm['e1'] = tm['f']
                tm['ko'] = tm['dd']
                tm['ke_bf'] = L(f"h_ke_bf{q}", [128, TT], BF16)
                tm['qe_bf'] = L(f"h_qe_bf{q}", [128, TT], BF16)
                tm['kom'] = L(f"kom{q}", [128, 4, NB, 128], BF16)
                self.tmps.append(tm)
            self.sgate = [L(f"sgate{q}", [128, TT], BF16) for q in range(3)]
            self.qem = [L(f"qem{q}", [128, 4, TT], BF16) for q in range(3)]
            self.v_bf = [L(f"v_bf{q}", [128, NB, 128], BF16) for q in range(3)]
            self.attm = [L(f"attm{q}", [128, NB, 128], BF16) for q in range(3)]
            self.u_sb = [L(f"u_sb{q}", [128, NCH, 128], F32) for q in range(3)]
            self.dec = [L(f"dec{q}", [128, NCH], F32) for q in range(3)]
            self.S_all2 = [L(f"S_all{q}", [128, 5, 128], F32) for q in range(2)]
            self.S_bf2 = [L(f"S_bf{q}", [128, NCH, 128], BF16) for q in range(2)]
            self.on2 = [L(f"on{q}", [128, NB, 128], F32) for q in range(2)]
            self.oss2 = [L(f"oss{q}", [128, 2 * NB], F32) for q in range(2)]
            self.junk2 = [L(f"junk{q}", [128, 128], F32) for q in range(2)]
            self.ps_pool = [0, 1, 2, 3]
            self.nm_rr = 0

        def nm_ps():
            i = [4, 5][self.nm_rr % 2]
            self.nm_rr += 1
            return self.psums[i], f"ps{i}"

        def proj(col0):
            ps, pk = self.next_ps()
            for kc in range(NKC):
                self.mm(ps[:, :TT], self.W_in[:, kc, col0:col0 + 128], self.hn[:, kc, 1:TT + 1], kc == 0, kc == NKC - 1, ['W_in', self.hnk], [pk])
            return ps, pk

        def A_gen(j):
            q2 = j % 2
            t = self.tmps[q2]
            kom = t['kom']
            W = self.W_in
            q = j % 3
            lb, oml = self.lbt[:, 1, :], self.lbt[:, 2, :]
            sgate, qem, v_bf, attm, u_sb, dec = self.sgate[q], self.qem[q], self.v_bf[q], self.attm[q], self.u_sb[q], self.dec[q]
            ksg, kqem, kv, katt, ku, kdec = f'sgate{q}', f'qem{q}', f'v_bf{q}', f'attm{q}', f'u_sb{q}', f'dec{q}'
            pf, pfk = proj(D + j * 128)
            self.act(t['f'][:], pf[:, :TT], AF.Exp, [pfk], [f't_f{q2}'], scale=-1.0)
            self.ts('dve', t['f'][:], t['f'][:], 1.0, None, ALU.add, None, [f't_f{q2}'], [f't_f{q2}'])
            self.p.op('dve', lambda e: e.reciprocal(out=t['f'][:], in_=t['f'][:]), [f't_f{q2}'], [f't_f{q2}'], cost=0.45)
            self.ts('dve', t['f'][:], t['f'][:], oml[:, j:j + 1], lb[:, j:j + 1], ALU.mult, ALU.add, [f't_f{q2}', 'lbt'], [f't_f{q2}'])
            self.act(t['kk'][:], t['f'][:], AF.Identity, [f't_f{q2}'], [f't_kk{q2}'], scale=-1.0, bias=self.epsc[:, 2:3])
            self.act(t['dd'][:], t['f'][:], AF.Ln, [f't_f{q2}'], [f't_dd{q2}'])
            self.p.op('dve', lambda e: e.tensor_tensor_scan(out=t['bb'][:], data0=self.resetm[:], data1=t['dd'][:], initial=0.0,
                                                            op0=ALU.mult, op1=ALU.add), [f't_dd{q2}', 'masks'], [f't_bb{q2}'])
            yield
            pq, pqk = proj(j * 128)
            self.act(t['e1'][:], t['bb'][:], AF.Exp, [f't_bb{q2}'], [f't_f{q2}'])
            self.tt('dve', t['qe'][:], pq[:, :TT], t['e1'][:], ALU.mult, [pqk, f't_f{q2}'], [f't_qe{q2}'])
            self.copy('act', t['qe_bf'][:], t['qe'][:], [f't_qe{q2}'], [f't_qe_bf{q2}'])
            qe4 = t['qe'][:].rearrange("p (b t) -> p b t", t=128)
            for c in range(4):
                self.tt('pool' if c % 2 else 'dve', qem[:, c, :].rearrange("p (b t) -> p b t", t=128), qe4,
                        self.colmask[:, c:c + 1, :].to_broadcast([128, NB, 128]), ALU.mult, [f't_qe{q2}', 'masks'], [kqem])
            yield
            self.act(t['e1'][:], t['bb'][:], AF.Exp, [f't_bb{q2}'], [f't_f{q2}'], scale=-1.0)
            self.tt('pool', t['ke_bf'][:], t['kk'][:], t['e1'][:], ALU.mult, [f't_kk{q2}', f't_f{q2}'], [f't_ke_bf{q2}'])
            b3 = t['bb'][:].rearrange("p (n c) -> p n c", c=C)
            self.act(dec[:], b3[:, :, C - 1], AF.Exp, [f't_bb{q2}'], [kdec])
            self.tt('pool', t['dd'][:].rearrange("p (n c) -> p n c", c=C), b3[:, :, C - 1:C].to_broadcast([128, NCH, C]), b3, ALU.subtract,
                    [f't_bb{q2}'], [f't_dd{q2}'])
            self.act(t['dd'][:], t['dd'][:], AF.Exp, [f't_dd{q2}'], [f't_dd{q2}'])
            self.tt('dve', t['ko'][:], t['kk'][:], t['dd'][:], ALU.mult, [f't_kk{q2}', f't_dd{q2}'], [f't_dd{q2}'])
            pg, pgk = proj(3 * D + j * 128)
            self.act(t['sg'][:], pg[:, :TT], AF.Exp, [pgk], [f't_sg{q2}'], scale=-1.0)
            self.ts('dve', t['sg'][:], t['sg'][:], 1.0, None, ALU.add, None, [f't_sg{q2}'], [f't_sg{q2}'])
            self.p.op('dve', lambda e: e.reciprocal(out=t['sg'][:], in_=t['sg'][:]), [f't_sg{q2}'], [f't_sg{q2}'], cost=0.45)
            self.tt('dve', sgate[:], pg[:, :TT], t['sg'][:], ALU.mult, [pgk, f't_sg{q2}'], [ksg])
            yield
            pv, pvk = self.next_ps()
            for blk in range(NB):
                for kc in range(NKC):
                    self.mm(pv[:, blk * 128:(blk + 1) * 128], self.hn[:, kc, 1 + blk * 128:1 + (blk + 1) * 128],
                            W[:, kc, 2 * D + j * 128:2 * D + (j + 1) * 128], kc == 0, kc == NKC - 1, ['W_in', self.hnk], [pvk])
            self.copy('act', v_bf[:].rearrange("p b v -> p (b v)"), pv[:, :TT], [pvk], [kv])
            yield
            ps, pk = nm_ps()
            for blk in range(NB):
                cs = slice(blk * 128, (blk + 1) * 128)
                self.mm(ps[:, cs], t['ke_bf'][:, cs], t['qe_bf'][:, cs], True, True, [f't_ke_bf{q2}', f't_qe_bf{q2}'], [pk])
            self.tt('dve', attm[:], ps[:, :TT].rearrange("p (b t) -> p b t", t=128), self.maskT[:, None, :].to_broadcast([128, NB, 128]),
                    ALU.mult, [pk, 'masks'], [katt])
            ps, pk = nm_ps()
            for blk in range(NB):
                cs = slice(blk * 128, (blk + 1) * 128)
                self.p.op('pe', lambda e, ps=ps, cs=cs: e.transpose(ps[:, cs], t['ko'][:, cs], self.ident[:]), [f't_dd{q2}', 'ident'], [pk])
            for c in range(4):
                self.act(kom[:, c, :, :].rearrange("p b k -> p (b k)"), ps[:, :TT], AF.Copy, [pk, 'masks'], [f'kom{q2}'],
                         scale=self.rowmask[:, c:c + 1])
            yield
            for blk in range(NB):
                ps, pk = nm_ps()
                for c in range(4):
                    self.mm(ps[:, c * 128:(c + 1) * 128], kom[:, c, blk, :], v_bf[:, blk, :], True, True, [f'kom{q2}', kv], [pk])
                self.copy('act' if blk % 2 else 'dve', u_sb[:, blk * 4:(blk + 1) * 4, :].rearrange("p c v -> p (c v)"), ps[:, 0:512], [pk], [ku])
                if blk % 2:
                    yield

        def B_gen(j):
            P = self.psums
            q = j % 3
            sgate, qem, v_bf, attm, u_sb, dec = self.sgate[q], self.qem[q], self.v_bf[q], self.attm[q], self.u_sb[q], self.dec[q]
            ksg, kqem, kv, katt, ku, kdec = f'sgate{q}', f'qem{q}', f'v_bf{q}', f'attm{q}', f'u_sb{q}', f'dec{q}'
            kS = ('S', j)
            q2 = j % 2
            SA = self.S_all2[q2]
            S_bf, on, oss, junk = self.S_bf2[q2], self.on2[q2], self.oss2[q2], self.junk2[q2]
            kSA, kSbf, kon, koss, kjunk = f'S_all{q2}', f'S_bf{q2}', f'on{q2}', f'oss{q2}', f'junk{q2}'
            self.copy('pool', SA[:, 0, :], self.S[:, j, :], [kS], [kSA])
            for blk in range(NB):
                for c in range(4):
                    n = blk * 4 + c
                    self.stt('dve', SA[:, c + 1, :], SA[:, c, :], dec[:, n:n + 1], u_sb[:, n, :], ALU.mult, ALU.add, [kSA, kdec, ku], [kSA])
                self.copy('act', S_bf[:, blk * 4:(blk + 1) * 4, :].rearrange("p c v -> p (c v)"),
                          SA[:, 0:4, :].rearrange("p c v -> p (c v)"), [kSA], [kSbf])
                if blk < NB - 1:
                    self.copy('dve', SA[:, 0, :], SA[:, 4, :], [kSA], [kSA])
                yield
            self.copy('pool', self.S[:, j, :], SA[:, 4, :], [kSA], [kS])
            po, pok = P[6], 'ps6'
            for blk in range(NB):
                cs = slice(blk * 128, (blk + 1) * 128)
                self.mm(po[:, cs], attm[:, blk, :], v_bf[:, blk, :], True, False, [katt, kv], [pok])
                for c in range(4):
                    self.mm(po[:, cs], qem[:, c, cs], S_bf[:, blk * 4 + c, :], False, c == 3, [kqem, kSbf], [pok])
                if blk % 2:
                    yield
            for blk in range(NB):
                cs = slice(blk * 128, (blk + 1) * 128)
                self.act(junk[:], po[:, cs], AF.Square, [pok], [kjunk, koss], accum_out=oss[:, blk:blk + 1])
            self.act(oss[:, NB:2 * NB], oss[:, 0:NB], AF.Ln, [koss, 'consts'], [koss], bias=self.epsc[:, 0:1], scale=1.0 / 128)
            self.act(oss[:, NB:2 * NB], oss[:, NB:2 * NB], AF.Exp, [koss], [koss], scale=-0.5)
            self.tt('dve', on[:], po[:, :TT].rearrange("p (b v) -> p b v", v=128),
                    oss[:, NB:2 * NB, None].to_broadcast([128, NB, 128]), ALU.mult, [pok, koss], [kon])
            self.tt('pool', on[:], on[:], self.gn_bc[:, None, j * 128:(j + 1) * 128].to_broadcast([128, NB, 128]), ALU.mult,
                    [kon, 'gn_bc'], [kon])
            yield
            py, pyk = P[7], 'ps7'
            for blk in range(NB):
                cs = slice(blk * 128, (blk + 1) * 128)
                self.p.op('pe', lambda e, cs=cs, blk=blk: e.transpose(py[:, cs], on[:, blk, :], self.ident[:]), [kon, 'ident'], [pyk])
            self.tt('dve', self.yTt[:, j, :], py[:, :TT], sgate[:], ALU.mult, [pyk, ksg], [self.yTk])
            yield

        def drive(gens):
            gens = [g for g in gens if g is not None]
            while gens:
                for g in list(gens):
                    try:
                        next(g)
                    except StopIteration:
                        gens.remove(g)

        def step(g):
            try:
                next(g)
                return True
            except StopIteration:
                return False

        def tile(ti):
            A = {0: A_gen(0), 1: A_gen(1)}
            while step(A[0]):
                step(A[1])
            for sl in range(NKC):
                must = [B_gen(sl)]
                if sl + 1 < NKC:
                    must.append(A[sl + 1])
                opt = None
                if sl + 2 < NKC:
                    A[sl + 2] = A_gen(sl + 2)
                    opt = A[sl + 2]
                while must:
                    for g in list(must):
                        if not step(g):
                            must.remove(g)
                    if opt is not None and not step(opt):
                        opt = None

        self.run_layer(li, 'hgrn', TT, 4 * D, setup, tile, is_last)
        self.ps_pool = list(range(8))

    def rwkv_layer(self, li, is_last):
        TT = 256
        NB = TT // 128
        WC = 3200
        NDT = self.neu_dt
        LC = -0.6065306597126334

        def setup():
            p = self.p
            L = self.lsb
            self.ident = L("ident", [128, 128], F32)
            self.make_ident(self.ident, 'ident')
            self.ident_n = L("ident_n", [128, 128], NDT)
            self.copy('dve', self.ident_n[:], self.ident[:], ['ident'], ['ident'])
            self.maskS = L("maskS", [128, 128], F32)
            self.maskI = L("maskI", [128, 128], F32)
            self.maskSL = L("maskSL", [128, 128], F32)
            for (m, pat, cm, cmp_) in ((self.maskS, 1, -1, ALU.is_gt), (self.maskI, 1, -1, ALU.is_ge), (self.maskSL, -1, 1, ALU.is_gt)):
                p.op('pool', lambda e, m=m: e.memset(m[:], 1.0), [], ['masks'])
                p.op('pool', lambda e, m=m, pat=pat, cm=cm, cmp_=cmp_: e.affine_select(
                    out=m[:], in_=m[:], pattern=[[pat, 128]], compare_op=cmp_, fill=0.0, base=0, channel_multiplier=cm), ['masks'], ['masks'])
            self.blockones = L("blockones", [128, 128], F32)
            p.op('pool', lambda e: e.memset(self.blockones[:], 1.0), [], ['masks'])
            p.op('pool', lambda e: e.memset(self.blockones[0:64, 64:128], 0.0), ['masks'], ['masks'])
            p.op('pool', lambda e: e.memset(self.blockones[64:128, 0:64], 0.0), ['masks'], ['masks'])
            self.resetm = L("resetm", [128, TT], F32)
            p.op('pool', lambda e: e.memset(self.resetm[:], 1.0), [], ['masks'])
            p.op('pool', lambda e: e.memset(self.resetm[:].rearrange("p (n c) -> p n c", c=128)[:, :, 0:1], 0.0), ['masks'], ['masks'])
            p.op('pool', lambda e: e.memset(self.epsc[:, 1:2], GN_EPS), [], ['consts'])
            self.mu_fm = L("mu_fm", [128, 33], F32)
            self.omu_fm = L("omu_fm", [128, 33], F32)
            p.dma('sp', self.mu_fm[:], self.inputs['rwkv_mu_fm'], [], ['rw_vecs'])
            self.ts('dve', self.omu_fm[:], self.mu_fm[:], -1.0, 1.0, ALU.mult, ALU.add, ['rw_vecs'], ['rw_vecs'])
            self.vecs = L("rw_vecs", [128, 5, NKC], F32)
            p.dma('sp', self.vecs[:], self.inputs['rwkv_vecs'], [], ['rw_vecs'])
            self.lw2 = L("lw2", [128, D], F32)
            p.dma('sp', self.lw2[:], self.inputs['rwkv_lw2'], [], ['lw2'])
            self.gng_bc = L("gng_bc", [128, D], BF16)
            self.gnb_bc = L("gnb_bc", [128, D], BF16)
            self.Wva = L("Wva", [128, NKC, D], BF16)
            self.Wvb = L("Wvb", [128, NKC, D], BF16)
            src = self.w_in_dram['rwkv']

            def loader(tes):
                muv = tes.enter_context(self.nc.sbuf_tensor("muv_bc", [128, D], F32))
                for (dst, nm) in ((self.gng_bc, 'rwkv_gn_g'), (self.gnb_bc, 'rwkv_gn_b')):
                    p.dma('sp', muv[:], self.inputs[nm].partition_broadcast(128), [], ['muv'])
                    self.copy('dve', dst[:], muv[:], ['muv'], ['bc_tiles'])
                p.dma('sp', muv[:], self.inputs['rwkv_mu'][2 * D:3 * D].partition_broadcast(128), ['muv'], ['bc_tiles', 'muv'])
                self.load_weight_bf16(self.Wvb, 'Wv', src, D, src_c0=2 * D, scale_bc=muv)
                self.ts('dve', muv[:], muv[:], -1.0, 1.0, ALU.mult, ALU.add, ['bc_tiles'], ['bc_tiles'])
                self.load_weight_bf16(self.Wva, 'Wv', src, D, src_c0=2 * D, scale_bc=muv)
                self.load_weight_bf16(self.W_in, 'W_in', src, 2 * D, src_c0=0, dst_c0=0)
                self.load_weight_bf16(self.W_in, 'W_in', src, 128, src_c0=3 * D, dst_c0=2 * D)
                self.load_weight_bf16(self.W_in, 'W_in', src, D, src_c0=3 * D + 128, dst_c0=2 * D + 128)
                self.load_weight_bf16(self.W_out, 'W_out', self.w_out_dram['rwkv'], D)
            self.S = L("S_rwkv", [128, NKC, 64], F32)
            p.op('pool', lambda e: e.memset(self.S[:], 0.0), [], [('S', j) for j in range(NKC)])
            self.pcar = L("pcar", [128, 25], F32)
            p.op('pool', lambda e: e.memset(self.pcar[:], 0.0), [], [('pcar', i) for i in range(25)])
            self.pm_ext = [L(f"pm_ext{i}", [128, TT + 1], F32) for i in range(2)]
            self.pm_i = 0
            names = ['r', 'k', 'tmp', 'sigw', 'a', 'kk', 'rn', 'kmod', 'bbv', 'c']
            self.tmp = {'lo': L("w_lo", [128, TT], F32)}
            self.tmpP = [{n: L(f"w_{n}0", [128, TT], F32) for n in names}, None]
            self.tmp2 = [dict(), dict()]
            for n in ['khat', 'bhat']:
                self.tmp2[0][n] = L(f"w_{n}0", [128, TT], F32)
            for n in ['rt_bf', 'bt_bf', 'at_bf', 'kt_h0', 'kt_h1', 'bt_h0', 'bt_h1', 'at_h0', 'at_h1']:
                self.tmp2[0][n] = L(f"w_{n}0", [128, TT], BF16)
            self.hm = L("hm", [128, 2], F32)
            p.op('pool', lambda e: e.memset(self.hm[:], 0.0), [], ['masks'])
            p.op('pool', lambda e: e.memset(self.hm[0:64, 0:1], 1.0), ['masks'], ['masks'])
            p.op('pool', lambda e: e.memset(self.hm[64:128, 1:2], 1.0), ['masks'], ['masks'])
            self.pt = {n: [L(f"wp_{n}{q}", [128, TT], F32 if n in ('at', 'rt') else BF16) for q in range(2)] for n in ['at', 'rt', 'rkr', 'sgate']}
            self.blockones_bf = L("blockones_bf", [128, 128], BF16)
            self.copy('dve', self.blockones_bf[:], self.blockones[:], ['masks'], ['masks'])
            self.v_sb = [L(f"v_sb{q}", [128, NB, 128], F32) for q in range(2)]
            self.v_bf = [L(f"v_bf{q}", [128, NB, 128], BF16) for q in range(2)]
            self.dec = [L(f"dec{q}", [128, NB], F32) for q in range(3)]

            def post_setup():
                self.tmpP[1] = {n: L(f"w_{n}1", [128, TT], F32) for n in names}
                self.PbP[1] = [L(f"Pb1{i}", [128, NCHN, 128], NDT) for i in range(2)]
                self.QbP[1] = [L(f"Qb1{i}", [128, NCHN, 128], NDT) for i in range(2)]
                for n in ['khat', 'bhat']:
                    self.tmp2[1][n] = L(f"w_{n}1", [128, TT], F32)
                for n in ['rt_bf', 'bt_bf', 'at_bf', 'kt_h0', 'kt_h1', 'bt_h0', 'bt_h1', 'at_h0', 'at_h1']:
                    self.tmp2[1][n] = L(f"w_{n}1", [128, TT], BF16)
                for n in ['rkr', 'sgate']:
                    self.pt[n].append(L(f"wp_{n}2", [128, TT], BF16))
                for n in ['at', 'rt']:
                    self.pt[n].append(self.pt[n][0])
                self.ysb = [L(f"ysb{i}", [128, 256], F32) for i in range(2)]
                self.yn2 = [self.yn, L("yn1", [128, 128], F32)]
                self.bon2 = [self.bon, L("bon1", [128, 128], F32)]
                self.gst2 = [self.gst, L("gst1", [128, 12], F32)]
                self.junk2 = [self.junk, L("junk1", [128, 64], F32)]
                self.bcount = 0
                self.v_sb.append(L("v_sb2", [128, NB, 128], F32))
                self.v_bf.append(L("v_bf2", [128, NB, 128], BF16))
            self.post_setup = post_setup
            NCHN = NB * 2
            self.PbP = [[L(f"Pb0{i}", [128, NCHN, 128], NDT) for i in range(2)], None]
            self.QbP = [[L(f"Qb0{i}", [128, NCHN, 128], NDT) for i in range(2)], None]
            self.NT = [L(f"NT{q}", [128, NCHN, 128], NDT) for q in range(2)]
            self.Aak = [L(f"Aak{q}", [128, NCHN, 128], BF16) for q in range(2)]
            self.Ark = [L(f"Ark{q}", [128, NCHN, 128], BF16) for q in range(2)]
            self.Arb = [L(f"Arb{q}", [128, NCHN, 128], BF16) for q in range(2)]
            self.ident4 = L("ident4", [128, NCHN, 128], NDT)
            for c in range(NCHN):
                self.copy('dve', self.ident4[:, c, :], self.ident[:], ['ident'], ['ident'])
            self.khm = [[L(f"khm{q}{b}", [128, 128], BF16) for b in range(NB)] for q in range(2)]
            self.bhm = [[L(f"bhm{q}{b}", [128, 128], BF16) for b in range(NB)] for q in range(2)]
            self.Z_sb = L("Z_sb", [128, 128], NDT)
            self.U_bf = L("U_bf", [128, 128], BF16)
            self.yn = L("yn", [128, 128], F32)
            self.bon = L("bon", [128, 128], F32)
            self.gst = L("gst", [128, 12], F32)
            self.junk = L("junk", [128, 64], F32)
            self.ps_pool = [0, 1]
            return loader

        NMB = [[2, 3], [4, 7]]
        self.nm_rrs = [0, 0]

        def nm_ps(q=0):
            i = NMB[q][self.nm_rrs[q] % 2]
            self.nm_rrs[q] += 1
            return self.psums[i], f"ps{i}"

        def shift(ps, pk, dst, dk, idx, mt):
            pm = self.pm_ext[self.pm_i % 2]
            pmk = f"pm_ext{self.pm_i % 2}"
            self.pm_i += 1
            ck = ('pcar', idx)
            self.copy('pool', pm[:, 0:1], self.pcar[:, idx:idx + 1], [ck], [pmk])
            self.act(pm[:, 1:TT + 1], ps[:, :TT], AF.Copy, [pk, 'rw_vecs'], [pmk], scale=self.mu_fm[:, mt:mt + 1])
            self.copy('pool', self.pcar[:, idx:idx + 1], pm[:, TT:TT + 1], [pmk], [ck])
            self.act(dst, ps[:, :TT], AF.Copy, [pk, 'rw_vecs'], [dk], scale=self.omu_fm[:, mt:mt + 1])
            self.tt('dve', dst, dst, pm[:, 0:TT], ALU.add, [dk, pmk], [dk])

        def proj(col0):
            ps, pk = self.next_ps()
            for kc in range(NKC):
                self.mm(ps[:, :TT], self.W_in[:, kc, col0:col0 + 128], self.hn[:, kc, 1:TT + 1], kc == 0, kc == NKC - 1, ['W_in', self.hnk], [pk])
            return ps, pk

        hsl = [slice(0, 64), slice(64, 128)]

        def A_gen(j):
            V = self.vecs
            q = j % 2
            q3 = j % 3
            t = dict(self.tmp)
            t.update(self.tmpP[q])
            t['e1'] = t['rn']
            t['e2'] = t['tmp']
            t.update(self.tmp2[q])
            self.Pb, self.Qb = self.PbP[q], self.QbP[q]
            PAR = set(self.tmp2[0].keys())
            jc = slice(j * 128, (j + 1) * 128)
            at, rt, rkr, sgate = self.pt['at'][q], self.pt['rt'][q], self.pt['rkr'][q3], self.pt['sgate'][q3]
            kat, krt, krkr, ksg = f'p_at{q}', f'p_rt{q}', f'p_rkr{q3}', f'p_sgate{q3}'
            v_sb, v_bf, dec = self.v_sb[q3], self.v_bf[q3], self.dec[q3]
            kv, kvb, kdec = f'v_sb{q3}', f'v_bf{q3}', f'dec{q3}'
            ps, pk = proj(j * 128)
            shift(ps, pk, t['r'][:], f't_r{q}', j, j)
            ps, pk = proj(D + j * 128)
            shift(ps, pk, t['k'][:], f't_k{q}', 8 + j, 8 + j)
            yield
            ps, pk = proj(2 * D + 128 + j * 128)
            shift(ps, pk, t['tmp'][:], f't_tmp{q}', 16 + j, 25 + j)
            self.act(sgate[:], t['tmp'][:], AF.Silu, [f't_tmp{q}'], [ksg])
            pv, pvk = self.next_ps()
            for blk in range(NB):
                n = 0
                for kc in range(NKC):
                    for (Wv, off) in ((self.Wva, 1), (self.Wvb, 0)):
                        self.mm(pv[:, blk * 128:(blk + 1) * 128], self.hn[:, kc, off + blk * 128:off + (blk + 1) * 128], Wv[:, kc, jc],
                                n == 0, n == 2 * NKC - 1, ['Wv', self.hnk], [pvk])
                        n += 1
            self.copy('act', v_sb[:].rearrange("p b v -> p (b v)"), pv[:, :TT], [pvk], [kv])
            self.copy('dve', v_bf[:].rearrange("p b v -> p (b v)"), pv[:, :TT], [pvk], [kvb])
            yield
            pw, pwk = self.next_ps()
            self.mm(pw[:, :TT], self.lw2[0:64, jc], t['lo'][0:64, :], True, True, ['lw2', 't_lo'], [pwk])
            self.act(t['sigw'][:], pw[:, :TT], AF.Sigmoid, [pwk, 'rw_vecs'], [f't_sigw{q}'], bias=V[:, 0, j:j + 1])
            pa, pak = self.next_ps()
            self.mm(pa[:, :TT], self.lw2[64:128, jc], t['lo'][64:128, :], True, True, ['lw2', 't_lo'], [pak])
            self.act(t['a'][:], pa[:, :TT], AF.Sigmoid, [pak, 'rw_vecs'], [f't_a{q}'], bias=V[:, 1, j:j + 1])
            self.ts('dve', t['kk'][:], t['k'][:], V[:, 2, j:j + 1], None, ALU.mult, None, [f't_k{q}', 'rw_vecs'], [f't_kk{q}'])
            self.tt('pool', t['tmp'][:], t['kk'][:], t['kk'][:], ALU.mult, [f't_kk{q}'], [f't_tmp{q}'])
            pn, pnk = self.next_ps()
            self.mm(pn[:, :TT], self.blockones[:], t['tmp'][:], True, True, ['masks', f't_tmp{q}'], [pnk])
            self.ts('dve', t['rn'][:], pn[:, :TT], 1e-24, None, ALU.max, None, [pnk], [f't_rn{q}'])
            self.act(t['rn'][:], t['rn'][:], AF.Ln, [f't_rn{q}'], [f't_rn{q}'])
            self.act(t['rn'][:], t['rn'][:], AF.Exp, [f't_rn{q}'], [f't_rn{q}'], scale=-0.5)
            self.tt('pool', t['kk'][:], t['kk'][:], t['rn'][:], ALU.mult, [f't_kk{q}', f't_rn{q}'], [f't_kk{q}'])
            self.ts('dve', t['tmp'][:], t['a'][:], -1.0, V[:, 3, j:j + 1], ALU.add, ALU.mult, [f't_a{q}', 'rw_vecs'], [f't_tmp{q}'])
            self.stt('dve', t['kmod'][:], t['tmp'][:], 1.0, t['k'][:], ALU.add, ALU.mult, [f't_tmp{q}', f't_k{q}'], [f't_kmod{q}'])
            self.tt('pool', t['bbv'][:], t['kk'][:], t['a'][:], ALU.mult, [f't_kk{q}', f't_a{q}'], [f't_bbv{q}'])
            yield
            self.p.op('dve', lambda e: e.tensor_tensor_scan(out=t['c'][:], data0=self.resetm[:], data1=t['sigw'][:], initial=0.0,
                                                            op0=ALU.mult, op1=ALU.add), [f't_sigw{q}', 'masks'], [f't_c{q}'])
            self.act(t['e1'][:], t['c'][:], AF.Exp, [f't_c{q}'], [f't_rn{q}'], scale=LC)
            self.tt('pool', rt[:], t['r'][:], t['e1'][:], ALU.mult, [f't_r{q}', f't_rn{q}'], [krt])
            self.copy('act', t['rt_bf'][:], rt[:], [krt], [f't_rt_bf{q}'])
            self.act(t['e2'][:], t['c'][:], AF.Exp, [f't_c{q}'], [f't_tmp{q}'], scale=-LC)
            for hd in range(2):
                self.stt('dve', t[f'kt_h{hd}'][:], t['kmod'][:], self.hm[:, hd:hd + 1], t['e2'][:], ALU.mult, ALU.mult,
                         [f't_kmod{q}', f't_tmp{q}', 'masks'], [f't_kt_h{hd}_{q}'])
                self.stt('dve', t[f'bt_h{hd}'][:], t['bbv'][:], self.hm[:, hd:hd + 1], t['e2'][:], ALU.mult, ALU.mult,
                         [f't_bbv{q}', f't_tmp{q}', 'masks'], [f't_bt_h{hd}_{q}'])
            self.tt('pool', t['bt_bf'][:], t['bbv'][:], t['e2'][:], ALU.mult, [f't_bbv{q}', f't_tmp{q}'], [f't_bt_bf{q}'])
            self.tt('pool', t['e1'][:], t['c'][:], t['sigw'][:], ALU.subtract, [f't_c{q}', f't_sigw{q}'], [f't_rn{q}'])
            self.act(t['e1'][:], t['e1'][:], AF.Exp, [f't_rn{q}'], [f't_rn{q}'], scale=LC)
            self.stt('dve', at[:], t['kk'][:], -1.0, t['e1'][:], ALU.mult, ALU.mult, [f't_kk{q}', f't_rn{q}'], [kat])
            self.copy('act', t['at_bf'][:], at[:], [kat], [f't_at_bf{q}'])
            for hd in range(2):
                self.act(t[f'at_h{hd}'][:], at[:], AF.Copy, [kat, 'masks'], [f't_at_h{hd}_{q}'], scale=self.hm[:, hd:hd + 1])
            yield
            c3 = t['c'][:].rearrange("p (n c) -> p n c", c=128)
            self.tt('pool', t['e2'][:].rearrange("p (n c) -> p n c", c=128), c3[:, :, 127:128].to_broadcast([128, NB, 128]), c3,
                    ALU.subtract, [f't_c{q}'], [f't_tmp{q}'])
            self.act(t['e2'][:], t['e2'][:], AF.Exp, [f't_tmp{q}'], [f't_tmp{q}'], scale=LC)
            self.act(dec[:], c3[:, :, 127], AF.Exp, [f't_c{q}'], [kdec], scale=LC)
            self.tt('pool', t['khat'][:], t['kmod'][:], t['e2'][:], ALU.mult, [f't_kmod{q}', f't_tmp{q}'], [f't_khat{q}'])
            self.tt('dve', t['bhat'][:], t['bbv'][:], t['e2'][:], ALU.mult, [f't_bbv{q}', f't_tmp{q}'], [f't_bhat{q}'])
            self.stt('dve', rkr[:], t['r'][:], V[:, 4, j:j + 1], t['kmod'][:], ALU.mult, ALU.mult, [f't_r{q}', 'rw_vecs', f't_kmod{q}'], [krkr])
            yield
            NCH = NB * 2
            specs = {'P': ('bt_h', 'at_bf', self.maskS, self.Pb[0], f'Pb{q}0'),
                     'Q': ('at_h', 'bt_bf', self.maskSL, self.Qb[0], f'Qb{q}0'),
                     'ak': ('kt_h', 'at_bf', self.maskS, self.Aak[q], f'Aak{q}'),
                     'rk': ('kt_h', 'rt_bf', self.maskI, self.Ark[q], f'Ark{q}'),
                     'rb': ('bt_h', 'rt_bf', self.maskI, self.Arb[q], f'Arb{q}')}
            for name in ('P', 'Q', 'ak', 'rk', 'rb'):
                lh, rh, mask, dst, dk = specs[name]
                ps, pk = nm_ps(q)
                for c in range(NCH):
                    blk, hd = c // 2, c % 2
                    cs = slice(blk * 128, (blk + 1) * 128)
                    self.mm(ps[:, c * 128:(c + 1) * 128], t[f'{lh}{hd}'][:, cs], t[rh][:, cs], True, True, [f't_{lh}{hd}_{q}', f't_{rh}{q}'], [pk])
                self.tt('dve', dst[:], ps[:, 0:NCH * 128].rearrange("p (c t) -> p c t", c=NCH),
                        mask[:, None, :].to_broadcast([128, NCH, 128]), ALU.mult, [pk, 'masks'], [dk])
                if name == 'Q':
                    self.tt('pool', self.NT[q][:], self.ident4[:], self.Pb[0][:], ALU.add, ['ident', f'Pb{q}0'], [f'NT{q}'])
                    yield
            for blk in range(NB):
                cs = slice(blk * 128, (blk + 1) * 128)
                for (srcn, dst, dk) in (('khat', self.khm[q][blk], f'khm{q}{blk}'), ('bhat', self.bhm[q][blk], f'bhm{q}{blk}')):
                    ps, pk = nm_ps(q)
                    self.p.op('pe', lambda e, ps=ps, srcn=srcn, cs=cs: e.transpose(ps[:, 0:128], t[srcn][:, cs], self.ident[:]),
                              [f't_{srcn}{q}', 'ident'], [pk])
                    self.copy('act', dst[:], ps[:, 0:128], [pk], [dk])
            yield
            NTq, kNT = self.NT[q], f'NT{q}'
            for i in range(6):
                a_, b_ = i % 2, (i + 1) % 2
                Pa, Qa, Pn, Qn = self.Pb[a_], self.Qb[a_], self.Pb[b_], self.Qb[b_]
                kPa, kQa, kPn, kQn = f'Pb{q}{a_}', f'Qb{q}{a_}', f'Pb{q}{b_}', f'Qb{q}{b_}'
                if i < 5:
                    ps, pk = nm_ps(q)
                    for c in range(NCH):
                        self.mm(ps[:, c * 128:(c + 1) * 128], Qa[:, c, :], Pa[:, c, :], True, True, [kQa, kPa], [pk])
                    self.copy('act', Pn[:].rearrange("p c t -> p (c t)"), ps[:, 0:NCH * 128], [pk], [kPn])
                ps, pk = nm_ps(q)
                for c in range(NCH):
                    self.mm(ps[:, c * 128:(c + 1) * 128], Pa[:, c, :], Qa[:, c, :], True, True, [kPa, kQa], [pk])
                self.copy('dve' if i % 2 == 0 else 'act', Qn[:].rearrange("p c t -> p (c t)"), ps[:, 0:NCH * 128], [pk], [kQn])
                yield
                ps, pk = nm_ps(q)
                for c in range(NCH):
                    self.mm(ps[:, c * 128:(c + 1) * 128], Qn[:, c, :], NTq[:, c, :], True, True, [kQn, kNT], [pk])
                self.tt('dve', NTq[:].rearrange("p c t -> p (c t)"), NTq[:].rearrange("p c t -> p (c t)"), ps[:, 0:NCH * 128], ALU.add,
                        [kNT, pk], [kNT])
                yield

        def B_gen(j):
            t = self.tmp
            P = self.psums
            q = j % 2
            q3 = j % 3
            jc = slice(j * 128, (j + 1) * 128)
            at, rt, rkr, sgate = self.pt['at'][q], self.pt['rt'][q], self.pt['rkr'][q3], self.pt['sgate'][q3]
            kat, krt, krkr, ksg = f'p_at{q}', f'p_rt{q}', f'p_rkr{q3}', f'p_sgate{q3}'
            v_sb, v_bf, dec = self.v_sb[q3], self.v_bf[q3], self.dec[q3]
            kv, kvb, kdec = f'v_sb{q3}', f'v_bf{q3}', f'dec{q3}'
            kS = ('S', j)
            for blk in range(NB):
                cs = slice(blk * 128, (blk + 1) * 128)
                khm, bhm = self.khm[q][blk], self.bhm[q][blk]
                kkh, kbh = f'khm{q}{blk}', f'bhm{q}{blk}'
                pz, pzk = P[5], 'ps5'
                for hd in range(2):
                    hs, hc, c = hsl[hd], slice(hd * 64, (hd + 1) * 64), blk * 2 + hd
                    self.mm(pz[:, hc], self.Aak[q][:, c, :], v_bf[:, blk, hc], True, False, [f'Aak{q}', kvb], [pzk])
                    self.mm(pz[:, hc], at[hs, cs], self.S[hs, j, :], False, True, [kat, kS], [pzk])
                self.copy('act', self.Z_sb[:], pz[:, 0:128], [pzk], ['Z_sb'])
                yield
                for hd in range(2):
                    hc, c = slice(hd * 64, (hd + 1) * 64), blk * 2 + hd
                    self.mm(pz[:, hc], self.NT[q][:, c, :], self.Z_sb[:, hc], True, True, [f'NT{q}', 'Z_sb'], [pzk])
                self.copy('act', self.U_bf[:], pz[:, 0:128], [pzk], ['U_bf'])
                yield
                self.mm(pz[:, 0:128], khm[:], v_bf[:, blk, :], True, False, [kkh, kvb], [pzk])
                self.mm(pz[:, 0:128], bhm[:], self.U_bf[:], False, True, [kbh, 'U_bf'], [pzk])
                py, pyk = P[6], 'ps6'
                for hd in range(2):
                    hs, hc, c = hsl[hd], slice(hd * 64, (hd + 1) * 64), blk * 2 + hd
                    yc = slice(hd * 128, hd * 128 + 64)
                    bc_ = slice(hd * 128 + 64, hd * 128 + 128)
                    self.mm(py[:, yc], self.Ark[q][:, c, :], v_bf[:, blk, hc], True, False, [f'Ark{q}', kvb], [pyk])
                    self.mm(py[:, yc], rt[hs, cs], self.S[hs, j, :], False, False, [krt, kS], [pyk])
                    self.mm(py[:, yc], self.Arb[q][:, c, :], self.U_bf[:, hc], False, True, [f'Arb{q}', 'U_bf'], [pyk])
                    self.mm(py[:, bc_], rkr[hs, cs], self.blockones_bf[hs, hs], True, True, [krkr, 'masks'], [pyk])
                for hd in range(2):
                    hs, hc = hsl[hd], slice(hd * 64, (hd + 1) * 64)
                    self.stt('dve', self.S[hs, j, :], self.S[hs, j, :], dec[hs, blk:blk + 1], pz[hs, hc], ALU.mult, ALU.add,
                             [kS, kdec, pzk], [kS])
                yield
                bp = self.bcount % 2
                self.bcount += 1
                ysb, kys = self.ysb[bp], f'ysb{bp}'
                g, kg = self.gst2[bp], f'gst{bp}'
                yn, kyn = self.yn2[bp], f'yn{bp}'
                bon, kbon = self.bon2[bp], f'bon{bp}'
                junk, kjunk = self.junk2[bp], f'junk{bp}'
                self.copy('act', ysb[:], py[:, 0:256], [pyk], [kys])
                for hd in range(2):
                    hc = slice(hd * 64, (hd + 1) * 64)
                    yc = slice(hd * 128, hd * 128 + 64)
                    bc_ = slice(hd * 128 + 64, hd * 128 + 128)
                    self.act(junk[:], ysb[:, yc], AF.Identity, [kys], [kjunk, kg], accum_out=g[:, hd:hd + 1])
                    self.act(junk[:], ysb[:, yc], AF.Square, [kys], [kjunk, kg], accum_out=g[:, 2 + hd:3 + hd])
                    self.tt('pool', bon[:, hc], ysb[:, bc_], v_sb[:, blk, hc], ALU.mult, [kys, kv], [kbon])
                self.ts('dve', g[:, 4:6], g[:, 0:2], 1.0 / 64, None, ALU.mult, None, [kg], [kg])
                self.tt('dve', g[:, 6:8], g[:, 4:6], g[:, 4:6], ALU.mult, [kg], [kg])
                self.stt('dve', g[:, 8:10], g[:, 2:4], 1.0 / 64, g[:, 6:8], ALU.mult, ALU.subtract, [kg], [kg])
                self.act(g[:, 8:10], g[:, 8:10], AF.Ln, [kg, 'consts'], [kg], bias=self.epsc[:, 1:2])
                self.act(g[:, 8:10], g[:, 8:10], AF.Exp, [kg], [kg], scale=-0.5)
                for hd in range(2):
                    hc = slice(hd * 64, (hd + 1) * 64)
                    yc = slice(hd * 128, hd * 128 + 64)
                    self.ts('dve', yn[:, hc], ysb[:, yc], g[:, 4 + hd:5 + hd], g[:, 8 + hd:9 + hd], ALU.subtract, ALU.mult,
                            [kys, kg], [kyn])
                yield
                self.tt('pool', yn[:], yn[:], self.gng_bc[:, jc], ALU.mult, [kyn, 'bc_tiles'], [kyn])
                self.tt('pool', yn[:], yn[:], self.gnb_bc[:, jc], ALU.add, [kyn, 'bc_tiles'], [kyn])
                self.tt('pool', yn[:], yn[:], bon[:], ALU.add, [kyn, kbon], [kyn])
                ps, pk = nm_ps(q)
                self.p.op('pe', lambda e, ps=ps, yn=yn: e.transpose(ps[:, 0:128], yn[:], self.ident[:]), [kyn, 'ident'], [pk])
                self.tt('dve', self.yTt[:, j, cs], ps[:, 0:128], sgate[:, cs], ALU.mult, [pk, ksg], [self.yTk])
                yield

        def drive(gens):
            gens = [g for g in gens if g is not None]
            while gens:
                for g in list(gens):
                    try:
                        next(g)
                    except StopIteration:
                        gens.remove(g)

        def tile(ti):
            t = self.tmp
            ps, pk = proj(2 * D)
            shift(ps, pk, t['lo'][:], 't_lo', 24, 24)
            self.act(t['lo'][0:64, :], t['lo'][0:64, :], AF.Tanh, ['t_lo'], ['t_lo'])
            for j in range(NKC):
                drive([A_gen(j)])
                drive([B_gen(j)])

        self.run_layer(li, 'rwkv', TT, WC, setup, tile, is_last)
        self.ps_pool = list(range(8))

    def build(self):
        nc = self.nc
        T = self.T
        self.xT = self.din("xT", [D, T])
        self.yT = nc.dram_tensor("yT", [D, T], F32, kind="ExternalOutput").ap()
        d_norm_g = self.din("norm_g", [128, 4, NKC])
        d_final_g = self.din("final_g", [128, NKC])
        self.w_in_dram, self.w_out_dram = {}, {}
        kinds = [k for (_, k) in self.layers]
        if 'conv' in kinds:
            self.w_in_dram['conv'] = self.din("conv_w_in", [D, 4 * D])
            self.w_out_dram['conv'] = self.din("conv_w_out", [D, D])
            d_conv_w = self.din("conv_w", [128, NKC, 3])
        if 'rwkv' in kinds:
            self.w_in_dram['rwkv'] = self.din("rwkv_w_in", [D, 4 * D + 128])
            self.w_out_dram['rwkv'] = self.din("rwkv_w_out", [D, D])
            self.din("rwkv_mu_fm", [128, 33])
            self.din("rwkv_mu", [4 * D + 128])
            self.din("rwkv_vecs", [128, 5, NKC])
            self.din("rwkv_lw2", [128, D])
            self.din("rwkv_gn_g", [D])
            self.din("rwkv_gn_b", [D])
        if 'hgrn' in kinds:
            self.w_in_dram['hgrn'] = self.din("hgrn_w_in", [D, 4 * D])
            self.w_out_dram['hgrn'] = self.din("hgrn_w_out", [D, D])
            self.din("hgrn_gn_g", [D])
            self.din("hgrn_lbl", [128, 4, NKC])
        if 'gmlp' in kinds:
            self.w_in_dram['gmlp'] = self.din("gmlp_w_in", [D, 3 * D])
            self.w_out_dram['gmlp'] = self.din("gmlp_w_out", [D, D])
            self.din("gmlp_wsT", [128, 8, 128])
            self.din("gmlp_bs", [8, 128])
            self.din("gmlp_vg", [D])
        with ExitStack() as es:
            self.es = es
            nc.allow_low_precision("bf16 matmul operands, fp32 accumulation")
            self.p = p = Prog(nc, es)
            self.psums = [es.enter_context(nc.psum_tensor(f"ps{i}", [128, 512], F32)) for i in range(8)]
            self.ps_rr = 0
            self.ps_pool = list(range(8))
            self.ones_bf = self.sb("ones_bf", [128, 128], BF16)
            self.epsc = self.sb("epsc", [128, 4], F32)
            self.norm_g = self.sb("norm_g_sb", [128, 4, NKC], F32)
            self.final_g = self.sb("final_g_sb", [128, NKC], F32)
            p.op('pool', lambda e: e.memset(self.ones_bf[:], 1.0), [], ['ones_bf'])
            p.op('pool', lambda e: e.memset(self.epsc[:, 0:1], RMS_EPS), [], ['consts'])
            p.op('pool', lambda e: e.memset(self.epsc[:, 2:3], 1.0), ['consts'], ['consts'])
            p.dma('sp', self.norm_g[:], d_norm_g, [], ['consts'])
            p.dma('sp', self.final_g[:], d_final_g, [], ['consts'])
            if 'conv' in kinds:
                self.conv_w = self.sb("conv_w_sb", [128, NKC, 3], F32)
                p.dma('sp', self.conv_w[:], d_conv_w, [], ['consts'])
            self.first_layer = True
            for n, (li, kind) in enumerate(self.layers):
                is_last = n == len(self.layers) - 1
                if kind == 'conv':
                    self.conv_layer(li, is_last)
                elif kind == 'gmlp':
                    self.gmlp_layer(li, is_last)
                elif kind == 'hgrn':
                    self.hgrn_layer(li, is_last)
                elif kind == 'rwkv':
                    self.rwkv_layer(li, is_last)
                else:
                    raise ValueError(kind)
            p.finish('sp')
            self.stats = (p.n_ins, p.n_wait)
        return nc


def prep_inputs(inp, b, layers):
    f = np.float32
    m = {}
    m["xT"] = np.ascontiguousarray(np.asarray(inp["x"][b], f).T)
    m["norm_g"] = np.ascontiguousarray(np.asarray(inp["norm_g"], f).reshape(4, NKC, 128).transpose(2, 0, 1))
    m["final_g"] = np.ascontiguousarray(np.asarray(inp["final_g"], f).reshape(NKC, 128).T)
    kinds = [k for (_, k) in layers]
    if 'conv' in kinds:
        m["conv_w_in"] = np.ascontiguousarray(np.asarray(inp["conv_w_in"][0], f))
        m["conv_w_out"] = np.ascontiguousarray(np.asarray(inp["conv_w_out"][0], f))
        m["conv_w"] = np.ascontiguousarray(np.asarray(inp["conv_w"][0], f).reshape(3, NKC, 128).transpose(2, 1, 0))
    if 'rwkv' in kinds:
        m["rwkv_w_in"] = np.ascontiguousarray(np.asarray(inp["rwkv_w_in"][0], f))
        m["rwkv_w_out"] = np.ascontiguousarray(np.asarray(inp["rwkv_w_out"][0], f))
        mu = np.asarray(inp["rwkv_mu"][0], f)
        m["rwkv_mu"] = np.ascontiguousarray(mu)
        m["rwkv_mu_fm"] = np.ascontiguousarray(mu.reshape(33, 128).T)
        vecs = np.stack([np.asarray(inp[k][0], f).reshape(NKC, 128) for k in
                         ("rwkv_w0", "rwkv_a0", "rwkv_k_k", "rwkv_k_a", "rwkv_r_k")], axis=0)
        m["rwkv_vecs"] = np.ascontiguousarray(vecs.transpose(2, 0, 1))
        m["rwkv_lw2"] = np.ascontiguousarray(np.concatenate([np.asarray(inp["rwkv_w_w2"][0], f), np.asarray(inp["rwkv_w_a2"][0], f)], axis=0))
        m["rwkv_gn_g"] = np.ascontiguousarray(np.asarray(inp["rwkv_gn_g"][0], f))
        m["rwkv_gn_b"] = np.ascontiguousarray(np.asarray(inp["rwkv_gn_b"][0], f))
    if 'hgrn' in kinds:
        m["hgrn_w_in"] = np.ascontiguousarray(np.asarray(inp["hgrn_w_in"][0], f))
        m["hgrn_w_out"] = np.ascontiguousarray(np.asarray(inp["hgrn_w_out"][0], f))
        m["hgrn_gn_g"] = np.ascontiguousarray(np.asarray(inp["hgrn_gn_g"][0], f))
        m["hgrn_lbl"] = np.ascontiguousarray(np.asarray(inp["hgrn_lb_logits"], f).reshape(4, NKC, 128).transpose(2, 0, 1))
    if 'gmlp' in kinds:
        m["gmlp_w_in"] = np.ascontiguousarray(np.asarray(inp["gmlp_w_in"][0], f))
        m["gmlp_w_out"] = np.ascontiguousarray(np.asarray(inp["gmlp_w_out"][0], f))
        m["gmlp_wsT"] = np.ascontiguousarray(np.asarray(inp["gmlp_w_s"][0], f).transpose(2, 0, 1))
        m["gmlp_bs"] = np.ascontiguousarray(np.asarray(inp["gmlp_b_s"][0], f))
        m["gmlp_vg"] = np.ascontiguousarray(np.asarray(inp["gmlp_v_g"][0], f))
    return m


FULL_LAYERS = [(0, 'rwkv'), (1, 'hgrn'), (2, 'conv'), (3, 'gmlp')]


def kernel(**inputs):
    x = np.asarray(inputs["x"])
    B, T, _ = x.shape
    layers = FULL_LAYERS
    bld = Builder(T, layers)
    nc = bld.build()
    in_maps = []
    for c in range(8):
        in_maps.append(prep_inputs(inputs, c // 2, layers))
    res = run_bass_kernel_spmd(nc, in_maps, core_ids=list(range(8)))
    out = np.stack([np.asarray(res.results[2 * b]["yT"]).T for b in range(B)], axis=0)
    return out.astype(np.float32)
```

```python
import numpy as np
from contextlib import ExitStack
import concourse.bass as bass
import concourse.mybir as mybir
from concourse.bass_utils import run_bass_kernel_spmd

F32 = mybir.dt.float32
BF16 = mybir.dt.bfloat16
ALU = mybir.AluOpType
AF = mybir.ActivationFunctionType
AX = mybir.AxisListType

D = 1024
NKC = 8
RMS_EPS = 1e-6
GN_EPS = 64e-5


class Prog:
    LIMIT = 30000

    def __init__(self, nc, es, n_dma_sems=24):
        self.nc = nc
        self.es = es
        self.engs = {'pe': nc.tensor, 'act': nc.scalar, 'dve': nc.vector,
                     'pool': nc.gpsimd, 'sp': nc.sync}
        self.sems = {}
        self.epoch = {k: 0 for k in self.engs}
        self.cnt = {k: 0 for k in self.engs}
        for k in self.engs:
            self.sems[(k, 0)] = es.enter_context(nc.semaphore(f"s_{k}_0"))
        self.dma_sems = []
        for i in range(n_dma_sems):
            key = ('dma', i)
            self.sems[key] = es.enter_context(nc.semaphore(f"s_dma_{i}"))
            self.cnt[key] = 0
            self.dma_sems.append(key)
        self.dma_rr = 0
        self.waited = {k: {} for k in self.engs}
        self.bufs = {}
        self.n_wait = 0
        self.n_ins = 0

    def _deps(self, reads, writes):
        deps = set()
        for k in reads:
            b = self.bufs.get(k)
            if b and b['w']:
                deps.add(b['w'])
        for k in writes:
            b = self.bufs.get(k)
            if b:
                if b['w']:
                    deps.add(b['w'])
                deps.update(b['r'])
        return deps

    def _wait(self, eng, deps):
        e = self.engs[eng]
        best = {}
        for (sk, v) in deps:
            if sk[0] == eng and eng == 'pe':
                continue
            if best.get(sk, 0) < v:
                best[sk] = v
        for sk, v in best.items():
            if self.waited[eng].get(sk, 0) >= v:
                continue
            e.wait_ge(self.sems[sk], v)
            self.waited[eng][sk] = v
            self.n_wait += 1

    def _record(self, tok, reads, writes):
        for k in reads:
            b = self.bufs.setdefault(k, {'w': None, 'r': []})
            b['r'].append(tok)
            if len(b['r']) > 64:
                best = {}
                for (sk, v) in b['r']:
                    if best.get(sk, 0) < v:
                        best[sk] = v
                b['r'] = list(best.items())
        for k in writes:
            b = self.bufs.setdefault(k, {'w': None, 'r': []})
            b['w'] = tok
            b['r'] = []

    @staticmethod
    def _excl(reads, writes):
        ps = [k for k in reads if isinstance(k, str) and k.startswith('ps')]
        if ps:
            reads = [k for k in reads if k not in ps]
            writes = list(writes) + ps
        return reads, writes

    disabled = False
    recording = None
    SYNC_LAT = 0.45

    def begin_record(self):
        self.recording = []

    def flush(self):
        rec = self.recording
        self.recording = None
        if not rec:
            return
        n = len(rec)
        preds = [None] * n
        succs = [[] for _ in range(n)]
        last_w = {}
        readers = {}
        for i, (kind, eng, fn, reads, writes, cost, lat) in enumerate(rec):
            ps = set()
            for k in reads:
                w = last_w.get(k)
                if w is not None:
                    ps.add(w)
            for k in writes:
                w = last_w.get(k)
                if w is not None:
                    ps.add(w)
                ps.update(readers.get(k, ()))
            ps.discard(i)
            preds[i] = ps
            for pi in ps:
                succs[pi].append(i)
            for k in reads:
                readers.setdefault(k, []).append(i)
            for k in writes:
                last_w[k] = i
                readers[k] = []
        npred = [len(p_) for p_ in preds]
        ready = [i for i in range(n) if npred[i] == 0]
        eng_free = {}
        end_t = [0.0] * n
        done_t = [0.0] * n
        order = []
        blevel = [0.0] * n
        for i in range(n - 1, -1, -1):
            kind, eng, fn, reads, writes, cost, lat = rec[i]
            b = 0.0
            for si in succs[i]:
                v = blevel[si] + (self.SYNC_LAT if rec[si][1] != eng else 0.0)
                if v > b:
                    b = v
            blevel[i] = b + cost + lat

        def est(i):
            kind, eng, fn, reads, writes, cost, lat = rec[i]
            t = eng_free.get(eng, 0.0)
            for pi in preds[i]:
                tp = done_t[pi] + (self.SYNC_LAT if rec[pi][1] != eng else 0.0)
                if tp > t:
                    t = tp
            return t
        EPS = 0.25
        while ready:
            ests = [(est(i), i) for i in ready]
            tmin = min(ests)[0]
            best = None
            for (t, i) in ests:
                if t <= tmin + EPS:
                    if best is None or blevel[i] > blevel[best[1]] or (blevel[i] == blevel[best[1]] and i < best[1]):
                        best = (t, i)
            t1, i = best
            ready.remove(i)
            kind, eng, fn, reads, writes, cost, lat = rec[i]
            end_t[i] = t1 + cost
            done_t[i] = t1 + cost + lat
            eng_free[eng] = end_t[i]
            order.append(i)
            for si in succs[i]:
                npred[si] -= 1
                if npred[si] == 0:
                    ready.append(si)
        assert len(order) == n, (len(order), n)
        self.sched_span = max(done_t) if done_t else 0.0
        for i in order:
            kind, eng, fn, reads, writes, cost, lat = rec[i]
            if kind == 'op':
                self.op(eng, fn, reads, writes)
            else:
                out, in_, kw = fn
                self.dma(eng, out, in_, reads, writes, **kw)

    def op(self, eng, fn, reads=(), writes=(), cost=None):
        if self.disabled:
            return None
        if self.recording is not None:
            reads, writes = self._excl(reads, writes)
            if cost is None:
                cost = {'pe': 0.2, 'act': 0.45, 'dve': 0.45, 'pool': 0.7, 'sp': 0.1}[eng]
            self.recording.append(('op', eng, fn, list(reads), list(writes), cost, 0.0))
            return None
        reads, writes = self._excl(reads, writes)
        deps = self._deps(reads, writes)
        self._wait(eng, deps)
        ins = fn(self.engs[eng])
        if self.cnt[eng] >= self.LIMIT:
            self.epoch[eng] += 1
            ep = self.epoch[eng]
            self.sems[(eng, ep)] = self.es.enter_context(self.nc.semaphore(f"s_{eng}_{ep}"))
            self.cnt[eng] = 0
        sk = (eng, self.epoch[eng])
        self.cnt[eng] += 1
        ins.then_inc(self.sems[sk], 1)
        self._record((sk, self.cnt[eng]), reads, writes)
        self.n_ins += 1
        return ins

    def dma(self, eng, out, in_, reads=(), writes=(), **kw):
        if self.disabled:
            return None
        if self.recording is not None:
            self.recording.append(('dma', eng, (out, in_, kw), list(reads), list(writes), 0.15, 6.0))
            return None
        deps = self._deps(reads, writes)
        sk = self.dma_sems[self.dma_rr]
        self.dma_rr = (self.dma_rr + 1) % len(self.dma_sems)
        if self.cnt[sk] > 0:
            deps.add((sk, self.cnt[sk]))
        self._wait(eng, deps)
        ins = self.engs[eng].dma_start(out=out, in_=in_, **kw)
        self.cnt[sk] += 16
        ins.then_inc(self.sems[sk], 16)
        self._record((sk, self.cnt[sk]), reads, writes)
        self.n_ins += 1
        return ins

    def all_tokens(self):
        deps = set()
        for k, b in self.bufs.items():
            if b['w']:
                deps.add(b['w'])
            deps.update(b['r'])
        return deps

    def barrier(self):
        deps = self.all_tokens()
        for eng in self.engs:
            d = set(x for x in deps)
            self._wait(eng, d)

    def finish(self, eng='sp'):
        self._wait(eng, self.all_tokens())


class Builder:
    def __init__(self, T, layers, do_final=True, neu_dt=None):
        self.neu_dt = neu_dt if neu_dt is not None else BF16
        self.use_sched = True
        self.T = T
        self.layers = layers
        self.do_final = do_final
        self.nc = bass.Bass("TRN2", target_bir_lowering=False)
        self.inputs = {}

    def din(self, name, shape):
        t = self.nc.dram_tensor(name, list(shape), F32, kind="ExternalInput").ap()
        self.inputs[name] = t
        return t

    def sb(self, name, shape, dt=F32):
        return self.es.enter_context(self.nc.sbuf_tensor(name, list(shape), dt))

    def lsb(self, name, shape, dt=F32):
        return self.les.enter_context(self.nc.sbuf_tensor(f"{name}_{self.lname}", list(shape), dt))

    def next_ps(self):
        pool = self.ps_pool
        i = pool[self.ps_rr % len(pool)]
        self.ps_rr += 1
        return self.psums[i], f"ps{i}"

    @staticmethod
    def ecost(eng, ap):
        try:
            n = ap.free_size()
        except Exception:
            n = 256
        if eng == 'act':
            return 0.22 + n * 0.00075
        if eng == 'dve':
            return 0.2 + n * 0.00095
        if eng == 'pool':
            return 0.2 + n * 0.0021
        return 0.2

    def tt(self, eng, out, in0, in1, op, reads, writes):
        return self.p.op(eng, lambda e: e.tensor_tensor(out=out, in0=in0, in1=in1, op=op), reads, writes, cost=self.ecost(eng, out))

    def ts(self, eng, out, in0, s1, s2, op0, op1, reads, writes):
        if s2 is None:
            return self.p.op(eng, lambda e: e.tensor_scalar(out=out, in0=in0, scalar1=s1, scalar2=None, op0=op0), reads, writes, cost=self.ecost(eng, out))
        return self.p.op(eng, lambda e: e.tensor_scalar(out=out, in0=in0, scalar1=s1, scalar2=s2, op0=op0, op1=op1), reads, writes, cost=self.ecost(eng, out))

    def stt(self, eng, out, in0, scalar, in1, op0, op1, reads, writes):
        eng = 'dve'
        return self.p.op(eng, lambda e: e.scalar_tensor_tensor(out=out, in0=in0, scalar=scalar, in1=in1, op0=op0, op1=op1), reads, writes, cost=self.ecost(eng, out))

    def act(self, out, in_, func, reads, writes, bias=None, scale=1.0, accum_out=None):
        kw = {}
        if bias is not None:
            kw['bias'] = bias
        if accum_out is not None:
            kw['accum_out'] = accum_out
        return self.p.op('act', lambda e: e.activation(out=out, in_=in_, func=func, scale=scale, **kw), reads, writes, cost=self.ecost('act', in_))

    def mm(self, out, lhsT, rhs, start, stop, reads, writes):
        try:
            n = rhs.free_size()
        except Exception:
            n = 128
        c = 0.06 + n / 2400.0 * (1.0 if lhsT.dtype == BF16 else 2.4)
        return self.p.op('pe', lambda e: e.matmul(out, lhsT=lhsT, rhs=rhs, start=start, stop=stop), reads, writes, cost=c)

    def copy(self, eng, out, in_, reads, writes):
        if eng == 'act':
            return self.p.op('act', lambda e: e.copy(out=out, in_=in_), reads, writes, cost=self.ecost('act', out))
        return self.p.op(eng, lambda e: e.tensor_copy(out=out, in_=in_), reads, writes, cost=self.ecost(eng, out))

    def load_weight_bf16(self, dst, dst_key, src, ncols, src_c0=0, dst_c0=0, scale_bc=None):
        p = self.p
        CH = 1024 if ncols % 1024 == 0 else ncols
        for kc in range(NKC):
            for c0 in range(0, ncols, CH):
                i = self.stage_i
                self.stage_i += 1
                nst = len(self.stage)
                st = self.stage[i % nst]
                sk = f"stage{i % nst}"
                p.dma('sp', st[:, 0:CH], src[kc * 128:(kc + 1) * 128, src_c0 + c0:src_c0 + c0 + CH], reads=[], writes=[sk])
                eng = ['dve', 'act'][i % 2] if scale_bc is None else ['dve', 'pool'][i % 2]
                if scale_bc is None:
                    self.copy(eng, dst[:, kc, dst_c0 + c0:dst_c0 + c0 + CH], st[:, 0:CH], [sk], [dst_key])
                else:
                    self.tt(eng, dst[:, kc, dst_c0 + c0:dst_c0 + c0 + CH], st[:, 0:CH], scale_bc[:, c0:c0 + CH], ALU.mult,
                            [sk, 'bc_tiles'], [dst_key])

    def rms_rstd(self, src, src_key, TT, tag):
        bi = self.rms_i % len(self.sqb_l)
        self.rms_i += 1
        sqb, rstd = self.sqb_l[bi], self.rstd_l[bi]
        ksq, krs = f'sqb{bi}', f'rstd{bi}'
        if self.sqb_alias:
            ksq = 'yT0'
        for kc in range(NKC):
            if kc % 2 == 0:
                self.act(sqb[:, kc, :TT], src[:, kc, :TT], AF.Square, [src_key], [(ksq, kc) if not self.sqb_alias else ksq])
            else:
                self.tt('dve', sqb[:, kc, :TT], src[:, kc, :TT], src[:, kc, :TT], ALU.mult, [src_key], [(ksq, kc) if not self.sqb_alias else ksq])
        ps, pk = self.next_ps()
        for kc in range(NKC):
            self.mm(ps[:, :TT], self.ones_bf[:], sqb[:, kc, :TT], kc == 0, kc == NKC - 1, [(ksq, kc) if not self.sqb_alias else ksq, 'ones_bf'], [pk])
        self.act(rstd[:, :TT], ps[:, :TT], AF.Ln, [pk, 'consts'], [krs], bias=self.epsc[:, 0:1], scale=1.0 / D)
        self.act(rstd[:, :TT], rstd[:, :TT], AF.Exp, [krs], [krs], scale=-0.5)
        return rstd, krs

    def run_layer(self, li, kind, TT, w_in_cols, mixer_setup, mixer_tile, is_last):
        p = self.p
        T = self.T
        ntiles = T // TT
        with ExitStack() as les:
            self.les = les
            self.lname = f"L{li}"
            self.TT = TT
            self.W_in = self.lsb("W_in", [128, NKC, w_in_cols], BF16)
            self.W_out = self.lsb("W_out", [128, NKC, D], BF16)
            ndb = 2 if kind != 'rwkv' else 1
            self.hT = [self.lsb(f"hT{i}", [128, NKC, TT], F32) for i in range(ndb)]
            self.sqb_alias = (kind == 'rwkv')
            if not self.sqb_alias:
                self.sqb_l = [self.lsb(f"sqb{i}", [128, NKC, TT], BF16) for i in range(ndb)]
            self.rstd_l = [self.lsb(f"rstd{i}", [128, TT], F32) for i in range(ndb)]
            self.rms_i = 0
            self.hn_l = [self.lsb(f"hn{i}", [128, NKC, TT + 1], BF16) for i in range(ndb)]
            self.yTt_l = [self.lsb(f"yTt{i}", [128, NKC, TT], BF16) for i in range(ndb)]
            self.hn, self.hnk = self.hn_l[0], 'hn0'
            self.yTt, self.yTk = self.yTt_l[0], 'yT0'
            if self.sqb_alias:
                self.sqb_l = [self.yTt_l[0]]
            self.stage_i = 0
            loader = mixer_setup()
            with ExitStack() as ses:
                nst = 2 if kind == 'rwkv' else max(2, min(4, (self.nc.sbuf_bytes_remaining - 512) // 4096))
                self.stage = [ses.enter_context(self.nc.sbuf_tensor(f"stage{i}_{self.lname}", [128, 1024], F32)) for i in range(nst)]
                if loader is None:
                    self.load_weight_bf16(self.W_in, 'W_in', self.w_in_dram[kind], w_in_cols)
                    self.load_weight_bf16(self.W_out, 'W_out', self.w_out_dram[kind], D)
                else:
                    loader(ses)
                p.barrier()
            if getattr(self, 'post_setup', None) is not None:
                self.post_setup()
                self.post_setup = None
            hn0 = self.hn_l[0]
            p.op('pool', lambda e: e.memset(hn0[:, :, 0:1], 0.0), [], ['hn0'])

            def load(ti):
                buf = self.hT[ti % len(self.hT)]
                src = self.xT if self.first_layer else self.yT
                p.dma('sp', buf[:], src.rearrange("(c p) t -> p c t", p=128)[:, :, ti * TT:(ti + 1) * TT],
                      reads=[('hd', ti * TT // 128 + i) for i in range(TT // 128)], writes=[f"hT{ti % len(self.hT)}"])

            if self.use_sched:
                p.begin_record()
            load(0)
            for ti in range(ntiles):
                if len(self.hT) > 1:
                    if ti + 1 < ntiles:
                        load(ti + 1)
                elif ti > 0:
                    load(ti)
                h = self.hT[ti % len(self.hT)]
                hk = f"hT{ti % len(self.hT)}"
                rstd, rk = self.rms_rstd(h, hk, TT, 'in')
                g = self.norm_g
                prev_hn, prev_hnk = self.hn, self.hnk
                bi = ti % ndb
                self.hn, self.hnk = self.hn_l[bi], f'hn{bi}'
                self.yTt, self.yTk = self.yTt_l[bi], f'yT{bi}'
                if ti > 0:
                    self.copy('pool', self.hn[:, :, 0:1], prev_hn[:, :, TT:TT + 1], [prev_hnk], [self.hnk])
                for kc in range(NKC):
                    self.stt('dve', self.hn[:, kc, 1:TT + 1], h[:, kc, :], g[:, li, kc:kc + 1], rstd[:, :TT],
                             ALU.mult, ALU.mult, [hk, rk, 'consts'], [self.hnk])
                mixer_tile(ti)
                for j in range(NKC):
                    ps, pk = self.next_ps()
                    for kc in range(NKC):
                        self.mm(ps[:, :TT], self.W_out[:, kc, j * 128:(j + 1) * 128], self.yTt[:, kc, :TT],
                                kc == 0, kc == NKC - 1, ['W_out', self.yTk], [pk])
                    self.tt('dve', h[:, j, :], h[:, j, :], ps[:, :TT], ALU.add, [hk, pk], [hk])
                if is_last and self.do_final:
                    rstd, rk = self.rms_rstd(h, hk, TT, 'fin')
                    for kc in range(NKC):
                        self.stt('dve' if kc % 2 == 0 else 'pool', h[:, kc, :], h[:, kc, :], self.final_g[:, kc:kc + 1], rstd[:, :TT],
                                 ALU.mult, ALU.mult, [hk, rk, 'consts'], [hk])
                p.dma('sp', self.yT.rearrange("(c p) t -> p c t", p=128)[:, :, ti * TT:(ti + 1) * TT], h[:],
                      reads=[hk], writes=[('hd', (ti * TT) // 128 + i) for i in range(max(1, TT // 128))])
            if self.use_sched:
                p.flush()
            self.first_layer = False
            p.barrier()
        self.les = None

    def conv_layer(self, li, is_last):
        TT = 512

        def setup():
            self.yext = self.lsb("yext", [128, NKC, TT + 2], F32)
            self.zs = [self.lsb(f"zs{i}", [128, TT], F32) for i in range(2)]
            self.acc = [self.lsb(f"acc{i}", [128, TT], F32) for i in range(2)]
            self.sg = [self.lsb(f"sg{i}", [128, TT], F32) for i in range(2)]
            self.p.op('pool', lambda e: e.memset(self.yext[:, :, 0:2], 0.0), [], [('yext', j) for j in range(NKC)])

        def tile(ti):
            W = self.W_in
            cw = self.conv_w
            for j in range(NKC):
                zs, acc, sg = self.zs[j % 2], self.acc[j % 2], self.sg[j % 2]
                zk, ak, gk = f"zs{j % 2}", f"acc{j % 2}", f"sg{j % 2}"
                pss = []
                for blk in range(4):
                    ps, pk = self.next_ps()
                    col0 = blk * D + j * 128
                    for kc in range(NKC):
                        self.mm(ps[:, :TT], W[:, kc, col0:col0 + 128], self.hn[:, kc, 1:TT + 1], kc == 0, kc == NKC - 1,
                                ['W_in', self.hnk], [pk])
                    pss.append((ps, pk))
                (pb, pbk), (pc, pck), (pz, pzk), (pg, pgk) = pss
                yk = ('yext', j)
                self.copy('act', zs[:], pz[:, :TT], [pzk], [zk])
                if ti > 0:
                    self.copy('pool', self.yext[:, j, 0:2], self.yext[:, j, TT:TT + 2], [yk], [yk])
                self.tt('dve', self.yext[:, j, 2:TT + 2], pc[:, :TT], zs[:], ALU.mult, [pck, zk], [yk])
                self.act(acc[:], self.yext[:, j, 2:TT + 2], AF.Copy, [yk, 'consts'], [ak], scale=cw[:, j, 2:3])
                self.stt('pool', acc[:], self.yext[:, j, 1:TT + 1], cw[:, j, 1:2], acc[:], ALU.mult, ALU.add, [yk, ak, 'consts'], [ak])
                self.stt('pool', acc[:], self.yext[:, j, 0:TT], cw[:, j, 0:1], acc[:], ALU.mult, ALU.add, [yk, ak, 'consts'], [ak])
                self.act(sg[:], pg[:, :TT], AF.Silu, [pgk], [gk])
                self.tt('dve', acc[:], pb[:, :TT], acc[:], ALU.mult, [pbk, ak], [ak])
                self.tt('pool', self.yTt[:, j, :], acc[:], sg[:], ALU.mult, [ak, gk], [self.yTk])

        self.run_layer(li, 'conv', TT, 4 * D, setup, tile, is_last)

    def gmlp_layer(self, li, is_last):
        TT = 512

        def setup():
            p = self.p
            self.wsT = self.lsb("wsT", [128, 8, 128], F32)
            self.bs_bc = self.lsb("bs_bc", [128, 8, TT], F32)
            self.vg_bc = self.lsb("vg_bc", [128, D], F32)
            self.vn = [self.lsb(f"vn{i}", [128, D], F32) for i in range(TT // 128)]
            self.vss = self.lsb("vss", [128, 4], F32)
            self.junk = self.lsb("junk", [128, 512], F32)
            self.s_sb = [self.lsb(f"s_sb{i}", [128, TT], F32) for i in range(2)]
            self.sg = [self.lsb(f"sg{i}", [128, TT], F32) for i in range(2)]
            p.dma('sp', self.wsT[:], self.inputs['gmlp_wsT'], [], ['wsT'])
            for g in range(8):
                p.op('pool', lambda e: e.affine_select(out=self.wsT[:, g, :], in_=self.wsT[:, g, :], pattern=[[1, 128]],
                                                       compare_op=ALU.is_ge, fill=0.0, base=0, channel_multiplier=-1),
                     ['wsT'], ['wsT'])
            for r in range(TT // 128):
                p.dma('sp', self.bs_bc[:, :, r * 128:(r + 1) * 128],
                      self.inputs['gmlp_bs'].partition_broadcast(128), [], ['bs_bc'])
            p.dma('sp', self.vg_bc[:], self.inputs['gmlp_vg'].partition_broadcast(128), [], ['vg_bc'])

        def tile(ti):
            W = self.W_in
            nblk = TT // 128
            for blk in range(nblk):
                vn = self.vn[blk]
                vk = f"vn{blk}"
                halves = []
                for hf in range(2):
                    ps, pk = self.next_ps()
                    for kc in range(NKC):
                        self.mm(ps[:, :512], self.hn[:, kc, 1 + blk * 128:1 + (blk + 1) * 128],
                                W[:, kc, D + hf * 512:D + (hf + 1) * 512], kc == 0, kc == NKC - 1, ['W_in', self.hnk], [pk])
                    halves.append((ps, pk))
                for hf, (ps, pk) in enumerate(halves):
                    self.act(self.junk[:], ps[:, :512], AF.Square, [pk], ['junk', 'vss'], accum_out=self.vss[:, hf:hf + 1])
                self.tt('dve', self.vss[:, 2:3], self.vss[:, 0:1], self.vss[:, 1:2], ALU.add, ['vss'], ['vss'])
                self.act(self.vss[:, 3:4], self.vss[:, 2:3], AF.Ln, ['vss', 'consts'], ['vss'], bias=self.epsc[:, 0:1], scale=1.0 / D)
                self.act(self.vss[:, 3:4], self.vss[:, 3:4], AF.Exp, ['vss'], ['vss'], scale=-0.5)
                for hf, (ps, pk) in enumerate(halves):
                    self.stt('dve', vn[:, hf * 512:(hf + 1) * 512], ps[:, :512], self.vss[:, 3:4],
                             self.vg_bc[:, hf * 512:(hf + 1) * 512], ALU.mult, ALU.mult, [pk, 'vss', 'vg_bc'], [vk])
            for j in range(NKC):
                s_sb, sg = self.s_sb[j % 2], self.sg[j % 2]
                sk, gk = f"s_sb{j % 2}", f"sg{j % 2}"
                ps, pk = self.next_ps()
                for blk in range(nblk):
                    self.mm(ps[:, blk * 128:(blk + 1) * 128], self.vn[blk][:, j * 128:(j + 1) * 128], self.wsT[:, j, :], True, True,
                            [f"vn{blk}", 'wsT'], [pk])
                self.tt('dve', s_sb[:], ps[:, :TT], self.bs_bc[:, j, :], ALU.add, [pk, 'bs_bc'], [sk])
                pu, puk = self.next_ps()
                for kc in range(NKC):
                    self.mm(pu[:, :TT], W[:, kc, j * 128:(j + 1) * 128], self.hn[:, kc, 1:TT + 1], kc == 0, kc == NKC - 1,
                            ['W_in', self.hnk], [puk])
                pg, pgk = self.next_ps()
                for kc in range(NKC):
                    self.mm(pg[:, :TT], W[:, kc, 2 * D + j * 128:2 * D + (j + 1) * 128], self.hn[:, kc, 1:TT + 1], kc == 0,
                            kc == NKC - 1, ['W_in', self.hnk], [pgk])
                self.act(sg[:], pg[:, :TT], AF.Silu, [pgk], [gk])
                self.tt('dve', s_sb[:], pu[:, :TT], s_sb[:], ALU.mult, [puk, sk], [sk])
                self.tt('pool', self.yTt[:, j, :], s_sb[:], sg[:], ALU.mult, [sk, gk], [self.yTk])

        self.run_layer(li, 'gmlp', TT, 3 * D, setup, tile, is_last)


    def make_ident(self, ident, key):
        p = self.p
        p.op('pool', lambda e: e.memset(ident[:], 1.0), [], [key])
        p.op('pool', lambda e: e.affine_select(out=ident[:], in_=ident[:], pattern=[[-1, 128]], compare_op=ALU.is_equal,
                                               fill=0.0, base=0, channel_multiplier=1), [key], [key])

    def make_block_masks(self, C, maskT, colmask, rowmask, strict=False):
        p = self.p
        nch = 128 // C
        if maskT is not None:
            p.op('pool', lambda e: e.memset(maskT[:], 1.0), [], ['masks'])
            p.op('pool', lambda e: e.affine_select(out=maskT[:], in_=maskT[:], pattern=[[1, 128]], compare_op=ALU.is_ge if not strict else ALU.is_gt,
                                                   fill=0.0, base=0, channel_multiplier=-1), ['masks'], ['masks'])
            for c in range(1, nch):
                p.op('pool', lambda e, c=c: e.affine_select(out=maskT[:, c * C:(c + 1) * C], in_=maskT[:, c * C:(c + 1) * C], pattern=[[0, C]],
                                                            compare_op=ALU.is_ge, fill=0.0, base=-c * C, channel_multiplier=1), ['masks'], ['masks'])
        if colmask is not None:
            p.op('pool', lambda e: e.memset(colmask[:], 0.0), [], ['masks'])
            for c in range(nch):
                p.op('pool', lambda e, c=c: e.memset(colmask[:, c, c * C:(c + 1) * C], 1.0), ['masks'], ['masks'])
        if rowmask is not None:
            p.op('pool', lambda e: e.memset(rowmask[:], 1.0), [], ['masks'])
            for c in range(nch):
                p.op('pool', lambda e, c=c: e.affine_select(out=rowmask[:, c:c + 1], in_=rowmask[:, c:c + 1], pattern=[[0, 1]],
                                                            compare_op=ALU.is_ge, fill=0.0, base=-c * C, channel_multiplier=1), ['masks'], ['masks'])
                p.op('pool', lambda e, c=c: e.affine_select(out=rowmask[:, c:c + 1], in_=rowmask[:, c:c + 1], pattern=[[0, 1]],
                                                            compare_op=ALU.is_ge, fill=0.0, base=c * C + C - 1, channel_multiplier=-1), ['masks'], ['masks'])

    def hgrn_layer(self, li, is_last):
        TT = 256
        C = 32
        NB = TT // 128
        NCH = TT // C

        def setup():
            p = self.p
            L = self.lsb
            self.ident = L("ident", [128, 128], F32)
            self.make_ident(self.ident, 'ident')
            self.maskT = L("maskT", [128, 128], F32)
            self.colmask = L("colmask", [128, 4, 128], F32)
            self.rowmask = L("rowmask", [128, 4], F32)
            self.make_block_masks(C, self.maskT, self.colmask, self.rowmask)
            self.resetm = L("resetm", [128, TT], F32)
            self.ones_t = L("ones_t", [128, TT], F32)
            p.op('pool', lambda e: e.memset(self.ones_t[:], 1.0), [], ['masks'])
            p.op('pool', lambda e: e.memset(self.resetm[:], 1.0), [], ['masks'])
            p.op('pool', lambda e: e.memset(self.resetm[:].rearrange("p (n c) -> p n c", c=C)[:, :, 0:1], 0.0), ['masks'], ['masks'])
            self.gn_bc = L("gn_bc", [128, D], F32)
            p.dma('sp', self.gn_bc[:], self.inputs['hgrn_gn_g'].partition_broadcast(128), [], ['gn_bc'])
            self.lbl = L("lbl", [128, 4, NKC], F32)
            self.lbt = L("lbt", [128, 4, NKC], F32)
            p.dma('sp', self.lbl[:], self.inputs['hgrn_lbl'], [], ['lbl'])
            self.act(self.lbl[:], self.lbl[:], AF.Exp, ['lbl'], ['lbl'])
            self.tt('dve', self.lbt[:, 0, :], self.lbl[:, 0, :], self.lbl[:, 1, :], ALU.add, ['lbl'], ['lbt'])
            self.tt('dve', self.lbt[:, 0, :], self.lbt[:, 0, :], self.lbl[:, 2, :], ALU.add, ['lbl', 'lbt'], ['lbt'])
            self.tt('dve', self.lbt[:, 0, :], self.lbt[:, 0, :], self.lbl[:, 3, :], ALU.add, ['lbl', 'lbt'], ['lbt'])
            p.op('dve', lambda e: e.reciprocal(out=self.lbt[:, 3, :], in_=self.lbt[:, 0, :]), ['lbt'], ['lbt'])
            p.op('dve', lambda e: e.memset(self.lbt[:, 1, :], 0.0), ['lbt'], ['lbt'])
            for i in range(1, li + 1):
                self.tt('dve', self.lbt[:, 1, :], self.lbt[:, 1, :], self.lbl[:, i, :], ALU.add, ['lbl', 'lbt'], ['lbt'])
            self.tt('dve', self.lbt[:, 1, :], self.lbt[:, 1, :], self.lbt[:, 3, :], ALU.mult, ['lbt'], ['lbt'])
            self.ts('dve', self.lbt[:, 2, :], self.lbt[:, 1, :], -1.0, 1.0, ALU.mult, ALU.add, ['lbt'], ['lbt'])
            self.S = L("S_hgrn", [128, NKC, 128], F32)
            p.op('pool', lambda e: e.memset(self.S[:], 0.0), [], [('S', j) for j in range(NKC)])
            names = ['f', 'kk', 'bb', 'qe', 'dd', 'sg']
            self.tmps = []
            for q in range(2):
                tm = {n: L(f"h_{n}{q}", [128, TT], F32) for n in names}
                tm['e1'] = tm['f']
                tm['ko'] = tm['dd']
                tm['ke_bf'] = L(f"h_ke_bf{q}", [128, TT], BF16)
                tm['qe_bf'] = L(f"h_qe_bf{q}", [128, TT], BF16)
                tm['kom'] = L(f"kom{q}", [128, 4, NB, 128], BF16)
                self.tmps.append(tm)
            self.sgate = [L(f"sgate{q}", [128, TT], BF16) for q in range(3)]
            self.qem = [L(f"qem{q}", [128, 4, TT], BF16) for q in range(3)]
            self.v_bf = [L(f"v_bf{q}", [128, NB, 128], BF16) for q in range(3)]
            self.attm = [L(f"attm{q}", [128, NB, 128], BF16) for q in range(3)]
            self.u_sb = [L(f"u_sb{q}", [128, NCH, 128], F32) for q in range(3)]
            self.dec = [L(f"dec{q}", [128, NCH], F32) for q in range(3)]
            self.S_all2 = [L(f"S_all{q}", [128, 5, 128], F32) for q in range(2)]
            self.S_bf2 = [L(f"S_bf{q}", [128, NCH, 128], BF16) for q in range(2)]
            self.on2 = [L(f"on{q}", [128, NB, 128], F32) for q in range(2)]
            self.oss2 = [L(f"oss{q}", [128, 2 * NB], F32) for q in range(2)]
            self.junk2 = [L(f"junk{q}", [128, 128], F32) for q in range(2)]
            self.ps_pool = [0, 1, 2, 3]
            self.nm_rr = 0

        def nm_ps():
            i = [4, 5][self.nm_rr % 2]
            self.nm_rr += 1
            return self.psums[i], f"ps{i}"

        def proj(col0):
            ps, pk = self.next_ps()
            for kc in range(NKC):
                self.mm(ps[:, :TT], self.W_in[:, kc, col0:col0 + 128], self.hn[:, kc, 1:TT + 1], kc == 0, kc == NKC - 1, ['W_in', self.hnk], [pk])
            return ps, pk

        def A_gen(j):
            q2 = j % 2
            t = self.tmps[q2]
            kom = t['kom']
            W = self.W_in
            q = j % 3
            lb, oml = self.lbt[:, 1, :], self.lbt[:, 2, :]
            sgate, qem, v_bf, attm, u_sb, dec = self.sgate[q], self.qem[q], self.v_bf[q], self.attm[q], self.u_sb[q], self.dec[q]
            ksg, kqem, kv, katt, ku, kdec = f'sgate{q}', f'qem{q}', f'v_bf{q}', f'attm{q}', f'u_sb{q}', f'dec{q}'
            pf, pfk = proj(D + j * 128)
            self.act(t['f'][:], pf[:, :TT], AF.Exp, [pfk], [f't_f{q2}'], scale=-1.0)
            self.tt('pool', t['f'][:], t['f'][:], self.ones_t[:], ALU.add, [f't_f{q2}', 'masks'], [f't_f{q2}'])
            self.p.op('dve', lambda e: e.reciprocal(out=t['f'][:], in_=t['f'][:]), [f't_f{q2}'], [f't_f{q2}'], cost=0.45)
            self.ts('dve', t['f'][:], t['f'][:], oml[:, j:j + 1], lb[:, j:j + 1], ALU.mult, ALU.add, [f't_f{q2}', 'lbt'], [f't_f{q2}'])
            self.act(t['kk'][:], t['f'][:], AF.Identity, [f't_f{q2}'], [f't_kk{q2}'], scale=-1.0, bias=self.epsc[:, 2:3])
            self.act(t['dd'][:], t['f'][:], AF.Ln, [f't_f{q2}'], [f't_dd{q2}'])
            self.p.op('dve', lambda e: e.tensor_tensor_scan(out=t['bb'][:], data0=self.resetm[:], data1=t['dd'][:], initial=0.0,
                                                            op0=ALU.mult, op1=ALU.add), [f't_dd{q2}', 'masks'], [f't_bb{q2}'])
            yield
            pq, pqk = proj(j * 128)
            self.act(t['e1'][:], t['bb'][:], AF.Exp, [f't_bb{q2}'], [f't_f{q2}'])
            self.tt('dve', t['qe'][:], pq[:, :TT], t['e1'][:], ALU.mult, [pqk, f't_f{q2}'], [f't_qe{q2}'])
            self.copy('act', t['qe_bf'][:], t['qe'][:], [f't_qe{q2}'], [f't_qe_bf{q2}'])
            qe4 = t['qe'][:].rearrange("p (b t) -> p b t", t=128)
            for c in range(4):
                self.tt('pool', qem[:, c, :].rearrange("p (b t) -> p b t", t=128), qe4,
                        self.colmask[:, c:c + 1, :].to_broadcast([128, NB, 128]), ALU.mult, [f't_qe{q2}', 'masks'], [kqem])
            yield
            self.act(t['e1'][:], t['bb'][:], AF.Exp, [f't_bb{q2}'], [f't_f{q2}'], scale=-1.0)
            self.tt('pool', t['ke_bf'][:], t['kk'][:], t['e1'][:], ALU.mult, [f't_kk{q2}', f't_f{q2}'], [f't_ke_bf{q2}'])
            b3 = t['bb'][:].rearrange("p (n c) -> p n c", c=C)
            self.act(dec[:], b3[:, :, C - 1], AF.Exp, [f't_bb{q2}'], [kdec])
            self.tt('pool', t['dd'][:].rearrange("p (n c) -> p n c", c=C), b3[:, :, C - 1:C].to_broadcast([128, NCH, C]), b3, ALU.subtract,
                    [f't_bb{q2}'], [f't_dd{q2}'])
            self.act(t['dd'][:], t['dd'][:], AF.Exp, [f't_dd{q2}'], [f't_dd{q2}'])
            self.tt('pool', t['ko'][:], t['kk'][:], t['dd'][:], ALU.mult, [f't_kk{q2}', f't_dd{q2}'], [f't_dd{q2}'])
            pg, pgk = proj(3 * D + j * 128)
            self.act(t['sg'][:], pg[:, :TT], AF.Exp, [pgk], [f't_sg{q2}'], scale=-1.0)
            self.tt('pool', t['sg'][:], t['sg'][:], self.ones_t[:], ALU.add, [f't_sg{q2}', 'masks'], [f't_sg{q2}'])
            self.p.op('dve', lambda e: e.reciprocal(out=t['sg'][:], in_=t['sg'][:]), [f't_sg{q2}'], [f't_sg{q2}'], cost=0.45)
            self.tt('dve', sgate[:], pg[:, :TT], t['sg'][:], ALU.mult, [pgk, f't_sg{q2}'], [ksg])
            yield
            pv, pvk = self.next_ps()
            for blk in range(NB):
                for kc in range(NKC):
                    self.mm(pv[:, blk * 128:(blk + 1) * 128], self.hn[:, kc, 1 + blk * 128:1 + (blk + 1) * 128],
                            W[:, kc, 2 * D + j * 128:2 * D + (j + 1) * 128], kc == 0, kc == NKC - 1, ['W_in', self.hnk], [pvk])
            self.copy('act', v_bf[:].rearrange("p b v -> p (b v)"), pv[:, :TT], [pvk], [kv])
            yield
            ps, pk = nm_ps()
            for blk in range(NB):
                cs = slice(blk * 128, (blk + 1) * 128)
                self.mm(ps[:, cs], t['ke_bf'][:, cs], t['qe_bf'][:, cs], True, True, [f't_ke_bf{q2}', f't_qe_bf{q2}'], [pk])
            self.tt('dve', attm[:], ps[:, :TT].rearrange("p (b t) -> p b t", t=128), self.maskT[:, None, :].to_broadcast([128, NB, 128]),
                    ALU.mult, [pk, 'masks'], [katt])
            ps, pk = nm_ps()
            for blk in range(NB):
                cs = slice(blk * 128, (blk + 1) * 128)
                self.p.op('pe', lambda e, ps=ps, cs=cs: e.transpose(ps[:, cs], t['ko'][:, cs], self.ident[:]), [f't_dd{q2}', 'ident'], [pk])
            for c in range(4):
                self.act(kom[:, c, :, :].rearrange("p b k -> p (b k)"), ps[:, :TT], AF.Copy, [pk, 'masks'], [f'kom{q2}'],
                         scale=self.rowmask[:, c:c + 1])
            yield
            for blk in range(NB):
                ps, pk = nm_ps()
                for c in range(4):
                    self.mm(ps[:, c * 128:(c + 1) * 128], kom[:, c, blk, :], v_bf[:, blk, :], True, True, [f'kom{q2}', kv], [pk])
                self.copy('act' if blk % 2 else 'dve', u_sb[:, blk * 4:(blk + 1) * 4, :].rearrange("p c v -> p (c v)"), ps[:, 0:512], [pk], [ku])
                if blk % 2:
                    yield

        def B_gen(j):
            P = self.psums
            q = j % 3
            sgate, qem, v_bf, attm, u_sb, dec = self.sgate[q], self.qem[q], self.v_bf[q], self.attm[q], self.u_sb[q], self.dec[q]
            ksg, kqem, kv, katt, ku, kdec = f'sgate{q}', f'qem{q}', f'v_bf{q}', f'attm{q}', f'u_sb{q}', f'dec{q}'
            kS = ('S', j)
            q2 = j % 2
            SA = self.S_all2[q2]
            S_bf, on, oss, junk = self.S_bf2[q2], self.on2[q2], self.oss2[q2], self.junk2[q2]
            kSA, kSbf, kon, koss, kjunk = f'S_all{q2}', f'S_bf{q2}', f'on{q2}', f'oss{q2}', f'junk{q2}'
            self.copy('pool', SA[:, 0, :], self.S[:, j, :], [kS], [kSA])
            for blk in range(NB):
                for c in range(4):
                    n = blk * 4 + c
                    self.stt('dve', SA[:, c + 1, :], SA[:, c, :], dec[:, n:n + 1], u_sb[:, n, :], ALU.mult, ALU.add, [kSA, kdec, ku], [kSA])
                self.copy('act', S_bf[:, blk * 4:(blk + 1) * 4, :].rearrange("p c v -> p (c v)"),
                          SA[:, 0:4, :].rearrange("p c v -> p (c v)"), [kSA], [kSbf])
                if blk < NB - 1:
                    self.copy('dve', SA[:, 0, :], SA[:, 4, :], [kSA], [kSA])
                yield
            self.copy('pool', self.S[:, j, :], SA[:, 4, :], [kSA], [kS])
            po, pok = P[6], 'ps6'
            for blk in range(NB):
                cs = slice(blk * 128, (blk + 1) * 128)
                self.mm(po[:, cs], attm[:, blk, :], v_bf[:, blk, :], True, False, [katt, kv], [pok])
                for c in range(4):
                    self.mm(po[:, cs], qem[:, c, cs], S_bf[:, blk * 4 + c, :], False, c == 3, [kqem, kSbf], [pok])
                if blk % 2:
                    yield
            for blk in range(NB):
                cs = slice(blk * 128, (blk + 1) * 128)
                self.act(junk[:], po[:, cs], AF.Square, [pok], [kjunk, koss], accum_out=oss[:, blk:blk + 1])
            self.act(oss[:, NB:2 * NB], oss[:, 0:NB], AF.Ln, [koss, 'consts'], [koss], bias=self.epsc[:, 0:1], scale=1.0 / 128)
            self.act(oss[:, NB:2 * NB], oss[:, NB:2 * NB], AF.Exp, [koss], [koss], scale=-0.5)
            self.tt('dve', on[:], po[:, :TT].rearrange("p (b v) -> p b v", v=128),
                    oss[:, NB:2 * NB, None].to_broadcast([128, NB, 128]), ALU.mult, [pok, koss], [kon])
            self.tt('pool', on[:], on[:], self.gn_bc[:, None, j * 128:(j + 1) * 128].to_broadcast([128, NB, 128]), ALU.mult,
                    [kon, 'gn_bc'], [kon])
            yield
            py, pyk = P[7], 'ps7'
            for blk in range(NB):
                cs = slice(blk * 128, (blk + 1) * 128)
                self.p.op('pe', lambda e, cs=cs, blk=blk: e.transpose(py[:, cs], on[:, blk, :], self.ident[:]), [kon, 'ident'], [pyk])
            self.tt('dve', self.yTt[:, j, :], py[:, :TT], sgate[:], ALU.mult, [pyk, ksg], [self.yTk])
            yield

        def drive(gens):
            gens = [g for g in gens if g is not None]
            while gens:
                for g in list(gens):
                    try:
                        next(g)
                    except StopIteration:
                        gens.remove(g)

        def step(g):
            try:
                next(g)
                return True
            except StopIteration:
                return False

        def tile(ti):
            A = {0: A_gen(0), 1: A_gen(1)}
            while step(A[0]):
                step(A[1])
            for sl in range(NKC):
                must = [B_gen(sl)]
                if sl + 1 < NKC:
                    must.append(A[sl + 1])
                opt = None
                if sl + 2 < NKC:
                    A[sl + 2] = A_gen(sl + 2)
                    opt = A[sl + 2]
                while must:
                    for g in list(must):
                        if not step(g):
                            must.remove(g)
                    if opt is not None and not step(opt):
                        opt = None

        self.run_layer(li, 'hgrn', TT, 4 * D, setup, tile, is_last)
        self.ps_pool = list(range(8))

    def rwkv_layer(self, li, is_last):
        TT = 256
        NB = TT // 128
        WC = 3200
        NDT = self.neu_dt
        LC = -0.6065306597126334

        def setup():
            p = self.p
            L = self.lsb
            self.ident = L("ident", [128, 128], F32)
            self.make_ident(self.ident, 'ident')
            self.ident_n = L("ident_n", [128, 128], NDT)
            self.copy('dve', self.ident_n[:], self.ident[:], ['ident'], ['ident'])
            self.maskS = L("maskS", [128, 128], F32)
            self.maskI = L("maskI", [128, 128], F32)
            self.maskSL = L("maskSL", [128, 128], F32)
            for (m, pat, cm, cmp_) in ((self.maskS, 1, -1, ALU.is_gt), (self.maskI, 1, -1, ALU.is_ge), (self.maskSL, -1, 1, ALU.is_gt)):
                p.op('pool', lambda e, m=m: e.memset(m[:], 1.0), [], ['masks'])
                p.op('pool', lambda e, m=m, pat=pat, cm=cm, cmp_=cmp_: e.affine_select(
                    out=m[:], in_=m[:], pattern=[[pat, 128]], compare_op=cmp_, fill=0.0, base=0, channel_multiplier=cm), ['masks'], ['masks'])
            self.blockones = L("blockones", [128, 128], F32)
            p.op('pool', lambda e: e.memset(self.blockones[:], 1.0), [], ['masks'])
            p.op('pool', lambda e: e.memset(self.blockones[0:64, 64:128], 0.0), ['masks'], ['masks'])
            p.op('pool', lambda e: e.memset(self.blockones[64:128, 0:64], 0.0), ['masks'], ['masks'])
            self.resetm = L("resetm", [128, TT], F32)
            p.op('pool', lambda e: e.memset(self.resetm[:], 1.0), [], ['masks'])
            p.op('pool', lambda e: e.memset(self.resetm[:].rearrange("p (n c) -> p n c", c=128)[:, :, 0:1], 0.0), ['masks'], ['masks'])
            p.op('pool', lambda e: e.memset(self.epsc[:, 1:2], GN_EPS), [], ['consts'])
            self.mu_fm = L("mu_fm", [128, 33], F32)
            self.omu_fm = L("omu_fm", [128, 33], F32)
            p.dma('sp', self.mu_fm[:], self.inputs['rwkv_mu_fm'], [], ['rw_vecs'])
            self.ts('dve', self.omu_fm[:], self.mu_fm[:], -1.0, 1.0, ALU.mult, ALU.add, ['rw_vecs'], ['rw_vecs'])
            self.vecs = L("rw_vecs", [128, 5, NKC], F32)
            p.dma('sp', self.vecs[:], self.inputs['rwkv_vecs'], [], ['rw_vecs'])
            self.lw2 = L("lw2", [128, D], F32)
            p.dma('sp', self.lw2[:], self.inputs['rwkv_lw2'], [], ['lw2'])
            self.gng_bc = L("gng_bc", [128, D], BF16)
            self.gnb_bc = L("gnb_bc", [128, D], BF16)
            self.Wva = L("Wva", [128, NKC, D], BF16)
            self.Wvb = L("Wvb", [128, NKC, D], BF16)
            src = self.w_in_dram['rwkv']

            def loader(tes):
                muv = tes.enter_context(self.nc.sbuf_tensor("muv_bc", [128, D], F32))
                for (dst, nm) in ((self.gng_bc, 'rwkv_gn_g'), (self.gnb_bc, 'rwkv_gn_b')):
                    p.dma('sp', muv[:], self.inputs[nm].partition_broadcast(128), [], ['muv'])
                    self.copy('dve', dst[:], muv[:], ['muv'], ['bc_tiles'])
                p.dma('sp', muv[:], self.inputs['rwkv_mu'][2 * D:3 * D].partition_broadcast(128), ['muv'], ['bc_tiles', 'muv'])
                self.load_weight_bf16(self.Wvb, 'Wv', src, D, src_c0=2 * D, scale_bc=muv)
                self.ts('dve', muv[:], muv[:], -1.0, 1.0, ALU.mult, ALU.add, ['bc_tiles'], ['bc_tiles'])
                self.load_weight_bf16(self.Wva, 'Wv', src, D, src_c0=2 * D, scale_bc=muv)
                self.load_weight_bf16(self.W_in, 'W_in', src, 2 * D, src_c0=0, dst_c0=0)
                self.load_weight_bf16(self.W_in, 'W_in', src, 128, src_c0=3 * D, dst_c0=2 * D)
                self.load_weight_bf16(self.W_in, 'W_in', src, D, src_c0=3 * D + 128, dst_c0=2 * D + 128)
                self.load_weight_bf16(self.W_out, 'W_out', self.w_out_dram['rwkv'], D)
            self.S = L("S_rwkv", [128, NKC, 64], F32)
            p.op('pool', lambda e: e.memset(self.S[:], 0.0), [], [('S', j) for j in range(NKC)])
            self.pcar = L("pcar", [128, 25], F32)
            p.op('pool', lambda e: e.memset(self.pcar[:], 0.0), [], [('pcar', i) for i in range(25)])
            self.pm_ext = [L(f"pm_ext{i}", [128, TT + 1], F32) for i in range(2)]
            self.pm_i = 0
            names = ['r', 'k', 'tmp', 'sigw', 'a', 'kk', 'rn', 'kmod', 'bbv', 'c']
            self.tmp = {'lo': L("w_lo", [128, TT], F32)}
            self.tmpP = [{n: L(f"w_{n}0", [128, TT], F32) for n in names}, None]
            self.tmp2 = [dict(), dict()]
            for n in ['khat', 'bhat']:
                self.tmp2[0][n] = L(f"w_{n}0", [128, TT], F32)
            for n in ['rt_bf', 'bt_bf', 'at_bf', 'kt_h0', 'kt_h1', 'bt_h0', 'bt_h1', 'at_h0', 'at_h1']:
                self.tmp2[0][n] = L(f"w_{n}0", [128, TT], BF16)
            self.hm = L("hm", [128, 2], F32)
            p.op('pool', lambda e: e.memset(self.hm[:], 0.0), [], ['masks'])
            p.op('pool', lambda e: e.memset(self.hm[0:64, 0:1], 1.0), ['masks'], ['masks'])
            p.op('pool', lambda e: e.memset(self.hm[64:128, 1:2], 1.0), ['masks'], ['masks'])
            self.pt = {n: [L(f"wp_{n}{q}", [128, TT], F32 if n in ('at', 'rt') else BF16) for q in range(2)] for n in ['at', 'rt', 'rkr', 'sgate']}
            self.blockones_bf = L("blockones_bf", [128, 128], BF16)
            self.copy('dve', self.blockones_bf[:], self.blockones[:], ['masks'], ['masks'])
            self.v_sb = [L(f"v_sb{q}", [128, NB, 128], F32) for q in range(2)]
            self.v_bf = [L(f"v_bf{q}", [128, NB, 128], BF16) for q in range(2)]
            self.dec = [L(f"dec{q}", [128, NB], F32) for q in range(3)]

            def post_setup():
                self.tmpP[1] = {n: L(f"w_{n}1", [128, TT], F32) for n in names}
                self.PbP[1] = [L(f"Pb1{i}", [128, NCHN, 128], NDT) for i in range(2)]
                self.QbP[1] = [L(f"Qb1{i}", [128, NCHN, 128], NDT) for i in range(2)]
                for n in ['khat', 'bhat']:
                    self.tmp2[1][n] = L(f"w_{n}1", [128, TT], F32)
                for n in ['rt_bf', 'bt_bf', 'at_bf', 'kt_h0', 'kt_h1', 'bt_h0', 'bt_h1', 'at_h0', 'at_h1']:
                    self.tmp2[1][n] = L(f"w_{n}1", [128, TT], BF16)
                for n in ['rkr', 'sgate']:
                    self.pt[n].append(L(f"wp_{n}2", [128, TT], BF16))
                for n in ['at', 'rt']:
                    self.pt[n].append(self.pt[n][0])
                self.ysb = [L(f"ysb{i}", [128, 256], F32) for i in range(2)]
                self.yn2 = [self.yn, L("yn1", [128, 128], F32)]
                self.bon2 = [self.bon, L("bon1", [128, 128], F32)]
                self.gst2 = [self.gst, L("gst1", [128, 12], F32)]
                self.junk2 = [self.junk, L("junk1", [128, 64], F32)]
                self.bcount = 0
                self.v_sb.append(L("v_sb2", [128, NB, 128], F32))
                self.v_bf.append(L("v_bf2", [128, NB, 128], BF16))
            self.post_setup = post_setup
            NCHN = NB * 2
            self.PbP = [[L(f"Pb0{i}", [128, NCHN, 128], NDT) for i in range(2)], None]
            self.QbP = [[L(f"Qb0{i}", [128, NCHN, 128], NDT) for i in range(2)], None]
            self.NT = [L(f"NT{q}", [128, NCHN, 128], NDT) for q in range(2)]
            self.Aak = [L(f"Aak{q}", [128, NCHN, 128], BF16) for q in range(2)]
            self.Ark = [L(f"Ark{q}", [128, NCHN, 128], BF16) for q in range(2)]
            self.Arb = [L(f"Arb{q}", [128, NCHN, 128], BF16) for q in range(2)]
            self.ident4 = L("ident4", [128, NCHN, 128], NDT)
            for c in range(NCHN):
                self.copy('dve', self.ident4[:, c, :], self.ident[:], ['ident'], ['ident'])
            self.khm = [[L(f"khm{q}{b}", [128, 128], BF16) for b in range(NB)] for q in range(2)]
            self.bhm = [[L(f"bhm{q}{b}", [128, 128], BF16) for b in range(NB)] for q in range(2)]
            self.Z_sb = L("Z_sb", [128, 128], NDT)
            self.U_bf = L("U_bf", [128, 128], BF16)
            self.yn = L("yn", [128, 128], F32)
            self.bon = L("bon", [128, 128], F32)
            self.gst = L("gst", [128, 12], F32)
            self.junk = L("junk", [128, 64], F32)
            self.ps_pool = [0, 1]
            return loader

        NMB = [[2, 3], [4, 7]]
        self.nm_rrs = [0, 0]

        def nm_ps(q=0):
            i = NMB[q][self.nm_rrs[q] % 2]
            self.nm_rrs[q] += 1
            return self.psums[i], f"ps{i}"

        def shift(ps, pk, dst, dk, idx, mt):
            pm = self.pm_ext[self.pm_i % 2]
            pmk = f"pm_ext{self.pm_i % 2}"
            self.pm_i += 1
            ck = ('pcar', idx)
            self.copy('pool', pm[:, 0:1], self.pcar[:, idx:idx + 1], [ck], [pmk])
            self.act(pm[:, 1:TT + 1], ps[:, :TT], AF.Copy, [pk, 'rw_vecs'], [pmk], scale=self.mu_fm[:, mt:mt + 1])
            self.copy('pool', self.pcar[:, idx:idx + 1], pm[:, TT:TT + 1], [pmk], [ck])
            self.act(dst, ps[:, :TT], AF.Copy, [pk, 'rw_vecs'], [dk], scale=self.omu_fm[:, mt:mt + 1])
            self.tt('dve', dst, dst, pm[:, 0:TT], ALU.add, [dk, pmk], [dk])

        def proj(col0):
            ps, pk = self.next_ps()
            for kc in range(NKC):
                self.mm(ps[:, :TT], self.W_in[:, kc, col0:col0 + 128], self.hn[:, kc, 1:TT + 1], kc == 0, kc == NKC - 1, ['W_in', self.hnk], [pk])
            return ps, pk

        hsl = [slice(0, 64), slice(64, 128)]

        def A_gen(j):
            V = self.vecs
            q = j % 2
            q3 = j % 3
            t = dict(self.tmp)
            t.update(self.tmpP[q])
            t['e1'] = t['rn']
            t['e2'] = t['tmp']
            t.update(self.tmp2[q])
            self.Pb, self.Qb = self.PbP[q], self.QbP[q]
            PAR = set(self.tmp2[0].keys())
            jc = slice(j * 128, (j + 1) * 128)
            at, rt, rkr, sgate = self.pt['at'][q], self.pt['rt'][q], self.pt['rkr'][q3], self.pt['sgate'][q3]
            kat, krt, krkr, ksg = f'p_at{q}', f'p_rt{q}', f'p_rkr{q3}', f'p_sgate{q3}'
            v_sb, v_bf, dec = self.v_sb[q3], self.v_bf[q3], self.dec[q3]
            kv, kvb, kdec = f'v_sb{q3}', f'v_bf{q3}', f'dec{q3}'
            ps, pk = proj(j * 128)
            shift(ps, pk, t['r'][:], f't_r{q}', j, j)
            ps, pk = proj(D + j * 128)
            shift(ps, pk, t['k'][:], f't_k{q}', 8 + j, 8 + j)
            yield
            ps, pk = proj(2 * D + 128 + j * 128)
            shift(ps, pk, t['tmp'][:], f't_tmp{q}', 16 + j, 25 + j)
            self.act(sgate[:], t['tmp'][:], AF.Silu, [f't_tmp{q}'], [ksg])
            pv, pvk = self.next_ps()
            for blk in range(NB):
                n = 0
                for kc in range(NKC):
                    for (Wv, off) in ((self.Wva, 1), (self.Wvb, 0)):
                        self.mm(pv[:, blk * 128:(blk + 1) * 128], self.hn[:, kc, off + blk * 128:off + (blk + 1) * 128], Wv[:, kc, jc],
                                n == 0, n == 2 * NKC - 1, ['Wv', self.hnk], [pvk])
                        n += 1
            self.copy('act', v_sb[:].rearrange("p b v -> p (b v)"), pv[:, :TT], [pvk], [kv])
            self.copy('dve', v_bf[:].rearrange("p b v -> p (b v)"), pv[:, :TT], [pvk], [kvb])
            yield
            pw, pwk = self.next_ps()
            self.mm(pw[:, :TT], self.lw2[0:64, jc], t['lo'][0:64, :], True, True, ['lw2', 't_lo'], [pwk])
            self.act(t['sigw'][:], pw[:, :TT], AF.Sigmoid, [pwk, 'rw_vecs'], [f't_sigw{q}'], bias=V[:, 0, j:j + 1])
            pa, pak = self.next_ps()
            self.mm(pa[:, :TT], self.lw2[64:128, jc], t['lo'][64:128, :], True, True, ['lw2', 't_lo'], [pak])
            self.act(t['a'][:], pa[:, :TT], AF.Sigmoid, [pak, 'rw_vecs'], [f't_a{q}'], bias=V[:, 1, j:j + 1])
            self.ts('dve', t['kk'][:], t['k'][:], V[:, 2, j:j + 1], None, ALU.mult, None, [f't_k{q}', 'rw_vecs'], [f't_kk{q}'])
            self.tt('pool', t['tmp'][:], t['kk'][:], t['kk'][:], ALU.mult, [f't_kk{q}'], [f't_tmp{q}'])
            pn, pnk = self.next_ps()
            self.mm(pn[:, :TT], self.blockones[:], t['tmp'][:], True, True, ['masks', f't_tmp{q}'], [pnk])
            self.ts('dve', t['rn'][:], pn[:, :TT], 1e-24, None, ALU.max, None, [pnk], [f't_rn{q}'])
            self.act(t['rn'][:], t['rn'][:], AF.Ln, [f't_rn{q}'], [f't_rn{q}'])
            self.act(t['rn'][:], t['rn'][:], AF.Exp, [f't_rn{q}'], [f't_rn{q}'], scale=-0.5)
            self.tt('pool', t['kk'][:], t['kk'][:], t['rn'][:], ALU.mult, [f't_kk{q}', f't_rn{q}'], [f't_kk{q}'])
            self.ts('dve', t['tmp'][:], t['a'][:], -1.0, V[:, 3, j:j + 1], ALU.add, ALU.mult, [f't_a{q}', 'rw_vecs'], [f't_tmp{q}'])
            self.stt('dve', t['kmod'][:], t['tmp'][:], 1.0, t['k'][:], ALU.add, ALU.mult, [f't_tmp{q}', f't_k{q}'], [f't_kmod{q}'])
            self.tt('pool', t['bbv'][:], t['kk'][:], t['a'][:], ALU.mult, [f't_kk{q}', f't_a{q}'], [f't_bbv{q}'])
            yield
            self.p.op('dve', lambda e: e.tensor_tensor_scan(out=t['c'][:], data0=self.resetm[:], data1=t['sigw'][:], initial=0.0,
                                                            op0=ALU.mult, op1=ALU.add), [f't_sigw{q}', 'masks'], [f't_c{q}'])
            self.act(t['e1'][:], t['c'][:], AF.Exp, [f't_c{q}'], [f't_rn{q}'], scale=LC)
            self.tt('pool', rt[:], t['r'][:], t['e1'][:], ALU.mult, [f't_r{q}', f't_rn{q}'], [krt])
            self.copy('act', t['rt_bf'][:], rt[:], [krt], [f't_rt_bf{q}'])
            self.act(t['e2'][:], t['c'][:], AF.Exp, [f't_c{q}'], [f't_tmp{q}'], scale=-LC)
            for hd in range(2):
                self.stt('dve', t[f'kt_h{hd}'][:], t['kmod'][:], self.hm[:, hd:hd + 1], t['e2'][:], ALU.mult, ALU.mult,
                         [f't_kmod{q}', f't_tmp{q}', 'masks'], [f't_kt_h{hd}_{q}'])
                self.stt('dve', t[f'bt_h{hd}'][:], t['bbv'][:], self.hm[:, hd:hd + 1], t['e2'][:], ALU.mult, ALU.mult,
                         [f't_bbv{q}', f't_tmp{q}', 'masks'], [f't_bt_h{hd}_{q}'])
            self.tt('pool', t['bt_bf'][:], t['bbv'][:], t['e2'][:], ALU.mult, [f't_bbv{q}', f't_tmp{q}'], [f't_bt_bf{q}'])
            self.tt('pool', t['e1'][:], t['c'][:], t['sigw'][:], ALU.subtract, [f't_c{q}', f't_sigw{q}'], [f't_rn{q}'])
            self.act(t['e1'][:], t['e1'][:], AF.Exp, [f't_rn{q}'], [f't_rn{q}'], scale=LC)
            self.stt('dve', at[:], t['kk'][:], -1.0, t['e1'][:], ALU.mult, ALU.mult, [f't_kk{q}', f't_rn{q}'], [kat])
            self.copy('act', t['at_bf'][:], at[:], [kat], [f't_at_bf{q}'])
            for hd in range(2):
                self.act(t[f'at_h{hd}'][:], at[:], AF.Copy, [kat, 'masks'], [f't_at_h{hd}_{q}'], scale=self.hm[:, hd:hd + 1])
            yield
            c3 = t['c'][:].rearrange("p (n c) -> p n c", c=128)
            self.tt('pool', t['e2'][:].rearrange("p (n c) -> p n c", c=128), c3[:, :, 127:128].to_broadcast([128, NB, 128]), c3,
                    ALU.subtract, [f't_c{q}'], [f't_tmp{q}'])
            self.act(t['e2'][:], t['e2'][:], AF.Exp, [f't_tmp{q}'], [f't_tmp{q}'], scale=LC)
            self.act(dec[:], c3[:, :, 127], AF.Exp, [f't_c{q}'], [kdec], scale=LC)
            self.tt('pool', t['khat'][:], t['kmod'][:], t['e2'][:], ALU.mult, [f't_kmod{q}', f't_tmp{q}'], [f't_khat{q}'])
            self.tt('dve', t['bhat'][:], t['bbv'][:], t['e2'][:], ALU.mult, [f't_bbv{q}', f't_tmp{q}'], [f't_bhat{q}'])
            self.stt('dve', rkr[:], t['r'][:], V[:, 4, j:j + 1], t['kmod'][:], ALU.mult, ALU.mult, [f't_r{q}', 'rw_vecs', f't_kmod{q}'], [krkr])
            yield
            NCH = NB * 2
            specs = {'P': ('bt_h', 'at_bf', self.maskS, self.Pb[0], f'Pb{q}0'),
                     'Q': ('at_h', 'bt_bf', self.maskSL, self.Qb[0], f'Qb{q}0'),
                     'ak': ('kt_h', 'at_bf', self.maskS, self.Aak[q], f'Aak{q}'),
                     'rk': ('kt_h', 'rt_bf', self.maskI, self.Ark[q], f'Ark{q}'),
                     'rb': ('bt_h', 'rt_bf', self.maskI, self.Arb[q], f'Arb{q}')}
            for name in ('P', 'Q', 'ak', 'rk', 'rb'):
                lh, rh, mask, dst, dk = specs[name]
                ps, pk = nm_ps(q)
                for c in range(NCH):
                    blk, hd = c // 2, c % 2
                    cs = slice(blk * 128, (blk + 1) * 128)
                    self.mm(ps[:, c * 128:(c + 1) * 128], t[f'{lh}{hd}'][:, cs], t[rh][:, cs], True, True, [f't_{lh}{hd}_{q}', f't_{rh}{q}'], [pk])
                self.tt('dve', dst[:], ps[:, 0:NCH * 128].rearrange("p (c t) -> p c t", c=NCH),
                        mask[:, None, :].to_broadcast([128, NCH, 128]), ALU.mult, [pk, 'masks'], [dk])
                if name == 'Q':
                    self.tt('pool', self.NT[q][:], self.ident4[:], self.Pb[0][:], ALU.add, ['ident', f'Pb{q}0'], [f'NT{q}'])
                    yield
            for blk in range(NB):
                cs = slice(blk * 128, (blk + 1) * 128)
                for (srcn, dst, dk) in (('khat', self.khm[q][blk], f'khm{q}{blk}'), ('bhat', self.bhm[q][blk], f'bhm{q}{blk}')):
                    ps, pk = nm_ps(q)
                    self.p.op('pe', lambda e, ps=ps, srcn=srcn, cs=cs: e.transpose(ps[:, 0:128], t[srcn][:, cs], self.ident[:]),
                              [f't_{srcn}{q}', 'ident'], [pk])
                    self.copy('act', dst[:], ps[:, 0:128], [pk], [dk])
            yield
            NTq, kNT = self.NT[q], f'NT{q}'
            for i in range(6):
                a_, b_ = i % 2, (i + 1) % 2
                Pa, Qa, Pn, Qn = self.Pb[a_], self.Qb[a_], self.Pb[b_], self.Qb[b_]
                kPa, kQa, kPn, kQn = f'Pb{q}{a_}', f'Qb{q}{a_}', f'Pb{q}{b_}', f'Qb{q}{b_}'
                if i < 5:
                    ps, pk = nm_ps(q)
                    for c in range(NCH):
                        self.mm(ps[:, c * 128:(c + 1) * 128], Qa[:, c, :], Pa[:, c, :], True, True, [kQa, kPa], [pk])
                    self.copy('act', Pn[:].rearrange("p c t -> p (c t)"), ps[:, 0:NCH * 128], [pk], [kPn])
                ps, pk = nm_ps(q)
                for c in range(NCH):
                    self.mm(ps[:, c * 128:(c + 1) * 128], Pa[:, c, :], Qa[:, c, :], True, True, [kPa, kQa], [pk])
                self.copy('dve' if i % 2 == 0 else 'act', Qn[:].rearrange("p c t -> p (c t)"), ps[:, 0:NCH * 128], [pk], [kQn])
                yield
                ps, pk = nm_ps(q)
                for c in range(NCH):
                    self.mm(ps[:, c * 128:(c + 1) * 128], Qn[:, c, :], NTq[:, c, :], True, True, [kQn, kNT], [pk])
                self.tt('dve', NTq[:].rearrange("p c t -> p (c t)"), NTq[:].rearrange("p c t -> p (c t)"), ps[:, 0:NCH * 128], ALU.add,
                        [kNT, pk], [kNT])
                yield

        def B_gen(j):
            t = self.tmp
            P = self.psums
            q = j % 2
            q3 = j % 3
            jc = slice(j * 128, (j + 1) * 128)
            at, rt, rkr, sgate = self.pt['at'][q], self.pt['rt'][q], self.pt['rkr'][q3], self.pt['sgate'][q3]
            kat, krt, krkr, ksg = f'p_at{q}', f'p_rt{q}', f'p_rkr{q3}', f'p_sgate{q3}'
            v_sb, v_bf, dec = self.v_sb[q3], self.v_bf[q3], self.dec[q3]
            kv, kvb, kdec = f'v_sb{q3}', f'v_bf{q3}', f'dec{q3}'
            kS = ('S', j)
            for blk in range(NB):
                cs = slice(blk * 128, (blk + 1) * 128)
                khm, bhm = self.khm[q][blk], self.bhm[q][blk]
                kkh, kbh = f'khm{q}{blk}', f'bhm{q}{blk}'
                pz, pzk = P[5], 'ps5'
                for hd in range(2):
                    hs, hc, c = hsl[hd], slice(hd * 64, (hd + 1) * 64), blk * 2 + hd
                    self.mm(pz[:, hc], self.Aak[q][:, c, :], v_bf[:, blk, hc], True, False, [f'Aak{q}', kvb], [pzk])
                    self.mm(pz[:, hc], at[hs, cs], self.S[hs, j, :], False, True, [kat, kS], [pzk])
                self.copy('act', self.Z_sb[:], pz[:, 0:128], [pzk], ['Z_sb'])
                yield
                for hd in range(2):
                    hc, c = slice(hd * 64, (hd + 1) * 64), blk * 2 + hd
                    self.mm(pz[:, hc], self.NT[q][:, c, :], self.Z_sb[:, hc], True, True, [f'NT{q}', 'Z_sb'], [pzk])
                self.copy('act', self.U_bf[:], pz[:, 0:128], [pzk], ['U_bf'])
                yield
                self.mm(pz[:, 0:128], khm[:], v_bf[:, blk, :], True, False, [kkh, kvb], [pzk])
                self.mm(pz[:, 0:128], bhm[:], self.U_bf[:], False, True, [kbh, 'U_bf'], [pzk])
                py, pyk = P[6], 'ps6'
                for hd in range(2):
                    hs, hc, c = hsl[hd], slice(hd * 64, (hd + 1) * 64), blk * 2 + hd
                    yc = slice(hd * 128, hd * 128 + 64)
                    bc_ = slice(hd * 128 + 64, hd * 128 + 128)
                    self.mm(py[:, yc], self.Ark[q][:, c, :], v_bf[:, blk, hc], True, False, [f'Ark{q}', kvb], [pyk])
                    self.mm(py[:, yc], rt[hs, cs], self.S[hs, j, :], False, False, [krt, kS], [pyk])
                    self.mm(py[:, yc], self.Arb[q][:, c, :], self.U_bf[:, hc], False, True, [f'Arb{q}', 'U_bf'], [pyk])
                    self.mm(py[:, bc_], rkr[hs, cs], self.blockones_bf[hs, hs], True, True, [krkr, 'masks'], [pyk])
                for hd in range(2):
                    hs, hc = hsl[hd], slice(hd * 64, (hd + 1) * 64)
                    self.stt('dve', self.S[hs, j, :], self.S[hs, j, :], dec[hs, blk:blk + 1], pz[hs, hc], ALU.mult, ALU.add,
                             [kS, kdec, pzk], [kS])
                yield
                bp = self.bcount % 2
                self.bcount += 1
                ysb, kys = self.ysb[bp], f'ysb{bp}'
                g, kg = self.gst2[bp], f'gst{bp}'
                yn, kyn = self.yn2[bp], f'yn{bp}'
                bon, kbon = self.bon2[bp], f'bon{bp}'
                junk, kjunk = self.junk2[bp], f'junk{bp}'
                self.copy('act', ysb[:], py[:, 0:256], [pyk], [kys])
                for hd in range(2):
                    hc = slice(hd * 64, (hd + 1) * 64)
                    yc = slice(hd * 128, hd * 128 + 64)
                    bc_ = slice(hd * 128 + 64, hd * 128 + 128)
                    self.act(junk[:], ysb[:, yc], AF.Identity, [kys], [kjunk, kg], accum_out=g[:, hd:hd + 1])
                    self.act(junk[:], ysb[:, yc], AF.Square, [kys], [kjunk, kg], accum_out=g[:, 2 + hd:3 + hd])
                    self.tt('pool', bon[:, hc], ysb[:, bc_], v_sb[:, blk, hc], ALU.mult, [kys, kv], [kbon])
                self.ts('dve', g[:, 4:6], g[:, 0:2], 1.0 / 64, None, ALU.mult, None, [kg], [kg])
                self.tt('dve', g[:, 6:8], g[:, 4:6], g[:, 4:6], ALU.mult, [kg], [kg])
                self.stt('dve', g[:, 8:10], g[:, 2:4], 1.0 / 64, g[:, 6:8], ALU.mult, ALU.subtract, [kg], [kg])
                self.act(g[:, 8:10], g[:, 8:10], AF.Ln, [kg, 'consts'], [kg], bias=self.epsc[:, 1:2])
                self.act(g[:, 8:10], g[:, 8:10], AF.Exp, [kg], [kg], scale=-0.5)
                for hd in range(2):
                    hc = slice(hd * 64, (hd + 1) * 64)
                    yc = slice(hd * 128, hd * 128 + 64)
                    self.ts('dve', yn[:, hc], ysb[:, yc], g[:, 4 + hd:5 + hd], g[:, 8 + hd:9 + hd], ALU.subtract, ALU.mult,
                            [kys, kg], [kyn])
                yield
                self.tt('pool', yn[:], yn[:], self.gng_bc[:, jc], ALU.mult, [kyn, 'bc_tiles'], [kyn])
                self.tt('pool', yn[:], yn[:], self.gnb_bc[:, jc], ALU.add, [kyn, 'bc_tiles'], [kyn])
                self.tt('pool', yn[:], yn[:], bon[:], ALU.add, [kyn, kbon], [kyn])
                ps, pk = nm_ps(q)
                self.p.op('pe', lambda e, ps=ps, yn=yn: e.transpose(ps[:, 0:128], yn[:], self.ident[:]), [kyn, 'ident'], [pk])
                self.tt('dve', self.yTt[:, j, cs], ps[:, 0:128], sgate[:, cs], ALU.mult, [pk, ksg], [self.yTk])
                yield

        def drive(gens):
            gens = [g for g in gens if g is not None]
            while gens:
                for g in list(gens):
                    try:
                        next(g)
                    except StopIteration:
                        gens.remove(g)

        def tile(ti):
            t = self.tmp
            ps, pk = proj(2 * D)
            shift(ps, pk, t['lo'][:], 't_lo', 24, 24)
            self.act(t['lo'][0:64, :], t['lo'][0:64, :], AF.Tanh, ['t_lo'], ['t_lo'])
            for j in range(NKC):
                drive([A_gen(j)])
                drive([B_gen(j)])

        self.run_layer(li, 'rwkv', TT, WC, setup, tile, is_last)
        self.ps_pool = list(range(8))

    def build(self):
        nc = self.nc
        T = self.T
        self.xT = self.din("xT", [D, T])
        self.yT = nc.dram_tensor("yT", [D, T], F32, kind="ExternalOutput").ap()
        d_norm_g = self.din("norm_g", [128, 4, NKC])
        d_final_g = self.din("final_g", [128, NKC])
        self.w_in_dram, self.w_out_dram = {}, {}
        kinds = [k for (_, k) in self.layers]
        if 'conv' in kinds:
            self.w_in_dram['conv'] = self.din("conv_w_in", [D, 4 * D])
            self.w_out_dram['conv'] = self.din("conv_w_out", [D, D])
            d_conv_w = self.din("conv_w", [128, NKC, 3])
        if 'rwkv' in kinds:
            self.w_in_dram['rwkv'] = self.din("rwkv_w_in", [D, 4 * D + 128])
            self.w_out_dram['rwkv'] = self.din("rwkv_w_out", [D, D])
            self.din("rwkv_mu_fm", [128, 33])
            self.din("rwkv_mu", [4 * D + 128])
            self.din("rwkv_vecs", [128, 5, NKC])
            self.din("rwkv_lw2", [128, D])
            self.din("rwkv_gn_g", [D])
            self.din("rwkv_gn_b", [D])
        if 'hgrn' in kinds:
            self.w_in_dram['hgrn'] = self.din("hgrn_w_in", [D, 4 * D])
            self.w_out_dram['hgrn'] = self.din("hgrn_w_out", [D, D])
            self.din("hgrn_gn_g", [D])
            self.din("hgrn_lbl", [128, 4, NKC])
        if 'gmlp' in kinds:
            self.w_in_dram['gmlp'] = self.din("gmlp_w_in", [D, 3 * D])
            self.w_out_dram['gmlp'] = self.din("gmlp_w_out", [D, D])
            self.din("gmlp_wsT", [128, 8, 128])
            self.din("gmlp_bs", [8, 128])
            self.din("gmlp_vg", [D])
        with ExitStack() as es:
            self.es = es
            nc.allow_low_precision("bf16 matmul operands, fp32 accumulation")
            self.p = p = Prog(nc, es)
            self.psums = [es.enter_context(nc.psum_tensor(f"ps{i}", [128, 512], F32)) for i in range(8)]
            self.ps_rr = 0
            self.ps_pool = list(range(8))
            self.ones_bf = self.sb("ones_bf", [128, 128], BF16)
            self.epsc = self.sb("epsc", [128, 4], F32)
            self.norm_g = self.sb("norm_g_sb", [128, 4, NKC], F32)
            self.final_g = self.sb("final_g_sb", [128, NKC], F32)
            p.op('pool', lambda e: e.memset(self.ones_bf[:], 1.0), [], ['ones_bf'])
            p.op('pool', lambda e: e.memset(self.epsc[:, 0:1], RMS_EPS), [], ['consts'])
            p.op('pool', lambda e: e.memset(self.epsc[:, 2:3], 1.0), ['consts'], ['consts'])
            p.dma('sp', self.norm_g[:], d_norm_g, [], ['consts'])
            p.dma('sp', self.final_g[:], d_final_g, [], ['consts'])
            if 'conv' in kinds:
                self.conv_w = self.sb("conv_w_sb", [128, NKC, 3], F32)
                p.dma('sp', self.conv_w[:], d_conv_w, [], ['consts'])
            self.first_layer = True
            for n, (li, kind) in enumerate(self.layers):
                is_last = n == len(self.layers) - 1
                if kind == 'conv':
                    self.conv_layer(li, is_last)
                elif kind == 'gmlp':
                    self.gmlp_layer(li, is_last)
                elif kind == 'hgrn':
                    self.hgrn_layer(li, is_last)
                elif kind == 'rwkv':
                    self.rwkv_layer(li, is_last)
                else:
                    raise ValueError(kind)
            p.finish('sp')
            self.stats = (p.n_ins, p.n_wait)
        return nc


def prep_inputs(inp, b, layers):
    f = np.float32
    m = {}
    m["xT"] = np.ascontiguousarray(np.asarray(inp["x"][b], f).T)
    m["norm_g"] = np.ascontiguousarray(np.asarray(inp["norm_g"], f).reshape(4, NKC, 128).transpose(2, 0, 1))
    m["final_g"] = np.ascontiguousarray(np.asarray(inp["final_g"], f).reshape(NKC, 128).T)
    kinds = [k for (_, k) in layers]
    if 'conv' in kinds:
        m["conv_w_in"] = np.ascontiguousarray(np.asarray(inp["conv_w_in"][0], f))
        m["conv_w_out"] = np.ascontiguousarray(np.asarray(inp["conv_w_out"][0], f))
        m["conv_w"] = np.ascontiguousarray(np.asarray(inp["conv_w"][0], f).reshape(3, NKC, 128).transpose(2, 1, 0))
    if 'rwkv' in kinds:
        m["rwkv_w_in"] = np.ascontiguousarray(np.asarray(inp["rwkv_w_in"][0], f))
        m["rwkv_w_out"] = np.ascontiguousarray(np.asarray(inp["rwkv_w_out"][0], f))
        mu = np.asarray(inp["rwkv_mu"][0], f)
        m["rwkv_mu"] = np.ascontiguousarray(mu)
        m["rwkv_mu_fm"] = np.ascontiguousarray(mu.reshape(33, 128).T)
        vecs = np.stack([np.asarray(inp[k][0], f).reshape(NKC, 128) for k in
                         ("rwkv_w0", "rwkv_a0", "rwkv_k_k", "rwkv_k_a", "rwkv_r_k")], axis=0)
        m["rwkv_vecs"] = np.ascontiguousarray(vecs.transpose(2, 0, 1))
        m["rwkv_lw2"] = np.ascontiguousarray(np.concatenate([np.asarray(inp["rwkv_w_w2"][0], f), np.asarray(inp["rwkv_w_a2"][0], f)], axis=0))
        m["rwkv_gn_g"] = np.ascontiguousarray(np.asarray(inp["rwkv_gn_g"][0], f))
        m["rwkv_gn_b"] = np.ascontiguousarray(np.asarray(inp["rwkv_gn_b"][0], f))
    if 'hgrn' in kinds:
        m["hgrn_w_in"] = np.ascontiguousarray(np.asarray(inp["hgrn_w_in"][0], f))
        m["hgrn_w_out"] = np.ascontiguousarray(np.asarray(inp["hgrn_w_out"][0], f))
        m["hgrn_gn_g"] = np.ascontiguousarray(np.asarray(inp["hgrn_gn_g"][0], f))
        m["hgrn_lbl"] = np.ascontiguousarray(np.asarray(inp["hgrn_lb_logits"], f).reshape(4, NKC, 128).transpose(2, 0, 1))
    if 'gmlp' in kinds:
        m["gmlp_w_in"] = np.ascontiguousarray(np.asarray(inp["gmlp_w_in"][0], f))
        m["gmlp_w_out"] = np.ascontiguousarray(np.asarray(inp["gmlp_w_out"][0], f))
        m["gmlp_wsT"] = np.ascontiguousarray(np.asarray(inp["gmlp_w_s"][0], f).transpose(2, 0, 1))
        m["gmlp_bs"] = np.ascontiguousarray(np.asarray(inp["gmlp_b_s"][0], f))
        m["gmlp_vg"] = np.ascontiguousarray(np.asarray(inp["gmlp_v_g"][0], f))
    return m


FULL_LAYERS = [(0, 'rwkv'), (1, 'hgrn'), (2, 'conv'), (3, 'gmlp')]


def kernel(**inputs):
    x = np.asarray(inputs["x"])
    B, T, _ = x.shape
    layers = FULL_LAYERS
    bld = Builder(T, layers)
    nc = bld.build()
    in_maps = []
    for c in range(8):
        in_maps.append(prep_inputs(inputs, c // 2, layers))
    res = run_bass_kernel_spmd(nc, in_maps, core_ids=list(range(8)))
    out = np.stack([np.asarray(res.results[2 * b]["yT"]).T for b in range(B)], axis=0)
    return out.astype(np.float32)
```

```python
import numpy as np
from contextlib import ExitStack
import concourse.bass as bass
import concourse.mybir as mybir
from concourse.bass_utils import run_bass_kernel_spmd

F32 = mybir.dt.float32
BF16 = mybir.dt.bfloat16
ALU = mybir.AluOpType
AF = mybir.ActivationFunctionType
AX = mybir.AxisListType

D = 1024
NKC = 8
RMS_EPS = 1e-6
GN_EPS = 64e-5


class Prog:
    LIMIT = 30000

    def __init__(self, nc, es, n_dma_sems=24):
        self.nc = nc
        self.es = es
        self.engs = {'pe': nc.tensor, 'act': nc.scalar, 'dve': nc.vector,
                     'pool': nc.gpsimd, 'sp': nc.sync}
        self.sems = {}
        self.epoch = {k: 0 for k in self.engs}
        self.cnt = {k: 0 for k in self.engs}
        for k in self.engs:
            self.sems[(k, 0)] = es.enter_context(nc.semaphore(f"s_{k}_0"))
        self.dma_sems = []
        for i in range(n_dma_sems):
            key = ('dma', i)
            self.sems[key] = es.enter_context(nc.semaphore(f"s_dma_{i}"))
            self.cnt[key] = 0
            self.dma_sems.append(key)
        self.dma_rr = 0
        self.waited = {k: {} for k in self.engs}
        self.bufs = {}
        self.n_wait = 0
        self.n_ins = 0

    def _deps(self, reads, writes):
        deps = set()
        for k in reads:
            b = self.bufs.get(k)
            if b and b['w']:
                deps.add(b['w'])
        for k in writes:
            b = self.bufs.get(k)
            if b:
                if b['w']:
                    deps.add(b['w'])
                deps.update(b['r'])
        return deps

    def _wait(self, eng, deps):
        e = self.engs[eng]
        best = {}
        for (sk, v) in deps:
            if sk[0] == eng and eng == 'pe':
                continue
            if best.get(sk, 0) < v:
                best[sk] = v
        for sk, v in best.items():
            if self.waited[eng].get(sk, 0) >= v:
                continue
            e.wait_ge(self.sems[sk], v)
            self.waited[eng][sk] = v
            self.n_wait += 1

    def _record(self, tok, reads, writes):
        for k in reads:
            b = self.bufs.setdefault(k, {'w': None, 'r': []})
            b['r'].append(tok)
            if len(b['r']) > 64:
                best = {}
                for (sk, v) in b['r']:
                    if best.get(sk, 0) < v:
                        best[sk] = v
                b['r'] = list(best.items())
        for k in writes:
            b = self.bufs.setdefault(k, {'w': None, 'r': []})
            b['w'] = tok
            b['r'] = []

    @staticmethod
    def _excl(reads, writes):
        ps = [k for k in reads if isinstance(k, str) and k.startswith('ps')]
        if ps:
            reads = [k for k in reads if k not in ps]
            writes = list(writes) + ps
        return reads, writes

    disabled = False
    recording = None
    SYNC_LAT = 0.45

    def begin_record(self):
        self.recording = []

    def flush(self):
        rec = self.recording
        self.recording = None
        if not rec:
            return
        n = len(rec)
        preds = [None] * n
        succs = [[] for _ in range(n)]
        last_w = {}
        readers = {}
        for i, (kind, eng, fn, reads, writes, cost, lat) in enumerate(rec):
            ps = set()
            for k in reads:
                w = last_w.get(k)
                if w is not None:
                    ps.add(w)
            for k in writes:
                w = last_w.get(k)
                if w is not None:
                    ps.add(w)
                ps.update(readers.get(k, ()))
            ps.discard(i)
            preds[i] = ps
            for pi in ps:
                succs[pi].append(i)
            for k in reads:
                readers.setdefault(k, []).append(i)
            for k in writes:
                last_w[k] = i
                readers[k] = []
        npred = [len(p_) for p_ in preds]
        ready = [i for i in range(n) if npred[i] == 0]
        eng_free = {}
        end_t = [0.0] * n
        done_t = [0.0] * n
        order = []
        blevel = [0.0] * n
        for i in range(n - 1, -1, -1):
            kind, eng, fn, reads, writes, cost, lat = rec[i]
            b = 0.0
            for si in succs[i]:
                v = blevel[si] + (self.SYNC_LAT if rec[si][1] != eng else 0.0)
                if v > b:
                    b = v
            blevel[i] = b + cost + lat

        def est(i):
            kind, eng, fn, reads, writes, cost, lat = rec[i]
            t = eng_free.get(eng, 0.0)
            for pi in preds[i]:
                tp = done_t[pi] + (self.SYNC_LAT if rec[pi][1] != eng else 0.0)
                if tp > t:
                    t = tp
            return t
        EPS = 0.25
        while ready:
            ests = [(est(i), i) for i in ready]
            tmin = min(ests)[0]
            best = None
            for (t, i) in ests:
                if t <= tmin + EPS:
                    if best is None or blevel[i] > blevel[best[1]] or (blevel[i] == blevel[best[1]] and i < best[1]):
                        best = (t, i)
            t1, i = best
            ready.remove(i)
            kind, eng, fn, reads, writes, cost, lat = rec[i]
            end_t[i] = t1 + cost
            done_t[i] = t1 + cost + lat
            eng_free[eng] = end_t[i]
            order.append(i)
            for si in succs[i]:
                npred[si] -= 1
                if npred[si] == 0:
                    ready.append(si)
        assert len(order) == n, (len(order), n)
        self.sched_span = max(done_t) if done_t else 0.0
        for i in order:
            kind, eng, fn, reads, writes, cost, lat = rec[i]
            if kind == 'op':
                self.op(eng, fn, reads, writes)
            else:
                out, in_, kw = fn
                self.dma(eng, out, in_, reads, writes, **kw)

    def op(self, eng, fn, reads=(), writes=(), cost=None):
        if self.disabled:
            return None
        if self.recording is not None:
            reads, writes = self._excl(reads, writes)
            if cost is None:
                cost = {'pe': 0.2, 'act': 0.45, 'dve': 0.45, 'pool': 0.7, 'sp': 0.1}[eng]
            self.recording.append(('op', eng, fn, list(reads), list(writes), cost, 0.0))
            return None
        reads, writes = self._excl(reads, writes)
        deps = self._deps(reads, writes)
        self._wait(eng, deps)
        ins = fn(self.engs[eng])
        if self.cnt[eng] >= self.LIMIT:
            self.epoch[eng] += 1
            ep = self.epoch[eng]
            self.sems[(eng, ep)] = self.es.enter_context(self.nc.semaphore(f"s_{eng}_{ep}"))
            self.cnt[eng] = 0
        sk = (eng, self.epoch[eng])
        self.cnt[eng] += 1
        ins.then_inc(self.sems[sk], 1)
        self._record((sk, self.cnt[eng]), reads, writes)
        self.n_ins += 1
        return ins

    def dma(self, eng, out, in_, reads=(), writes=(), **kw):
        if self.disabled:
            return None
        if self.recording is not None:
            self.recording.append(('dma', eng, (out, in_, kw), list(reads), list(writes), 0.15, 6.0))
            return None
        deps = self._deps(reads, writes)
        sk = self.dma_sems[self.dma_rr]
        self.dma_rr = (self.dma_rr + 1) % len(self.dma_sems)
        if self.cnt[sk] > 0:
            deps.add((sk, self.cnt[sk]))
        self._wait(eng, deps)
        ins = self.engs[eng].dma_start(out=out, in_=in_, **kw)
        self.cnt[sk] += 16
        ins.then_inc(self.sems[sk], 16)
        self._record((sk, self.cnt[sk]), reads, writes)
        self.n_ins += 1
        return ins

    def all_tokens(self):
        deps = set()
        for k, b in self.bufs.items():
            if b['w']:
                deps.add(b['w'])
            deps.update(b['r'])
        return deps

    def barrier(self):
        deps = self.all_tokens()
        for eng in self.engs:
            d = set(x for x in deps)
            self._wait(eng, d)

    def finish(self, eng='sp'):
        self._wait(eng, self.all_tokens())


class Builder:
    def __init__(self, T, layers, do_final=True, neu_dt=None):
        self.neu_dt = neu_dt if neu_dt is not None else BF16
        self.use_sched = True
        self.T = T
        self.layers = layers
        self.do_final = do_final
        self.nc = bass.Bass("TRN2", target_bir_lowering=False)
        self.inputs = {}

    def din(self, name, shape):
        t = self.nc.dram_tensor(name, list(shape), F32, kind="ExternalInput").ap()
        self.inputs[name] = t
        return t

    def sb(self, name, shape, dt=F32):
        return self.es.enter_context(self.nc.sbuf_tensor(name, list(shape), dt))

    def lsb(self, name, shape, dt=F32):
        return self.les.enter_context(self.nc.sbuf_tensor(f"{name}_{self.lname}", list(shape), dt))

    def next_ps(self):
        pool = self.ps_pool
        i = pool[self.ps_rr % len(pool)]
        self.ps_rr += 1
        return self.psums[i], f"ps{i}"

    @staticmethod
    def ecost(eng, ap):
        try:
            n = ap.free_size()
        except Exception:
            n = 256
        if eng == 'act':
            return 0.22 + n * 0.00075
        if eng == 'dve':
            return 0.2 + n * 0.00095
        if eng == 'pool':
            return 0.2 + n * 0.0021
        return 0.2

    def tt(self, eng, out, in0, in1, op, reads, writes):
        return self.p.op(eng, lambda e: e.tensor_tensor(out=out, in0=in0, in1=in1, op=op), reads, writes, cost=self.ecost(eng, out))

    def ts(self, eng, out, in0, s1, s2, op0, op1, reads, writes):
        if s2 is None:
            return self.p.op(eng, lambda e: e.tensor_scalar(out=out, in0=in0, scalar1=s1, scalar2=None, op0=op0), reads, writes, cost=self.ecost(eng, out))
        return self.p.op(eng, lambda e: e.tensor_scalar(out=out, in0=in0, scalar1=s1, scalar2=s2, op0=op0, op1=op1), reads, writes, cost=self.ecost(eng, out))

    def stt(self, eng, out, in0, scalar, in1, op0, op1, reads, writes):
        eng = 'dve'
        return self.p.op(eng, lambda e: e.scalar_tensor_tensor(out=out, in0=in0, scalar=scalar, in1=in1, op0=op0, op1=op1), reads, writes, cost=self.ecost(eng, out))

    def act(self, out, in_, func, reads, writes, bias=None, scale=1.0, accum_out=None):
        kw = {}
        if bias is not None:
            kw['bias'] = bias
        if accum_out is not None:
            kw['accum_out'] = accum_out
        return self.p.op('act', lambda e: e.activation(out=out, in_=in_, func=func, scale=scale, **kw), reads, writes, cost=self.ecost('act', in_))

    def mm(self, out, lhsT, rhs, start, stop, reads, writes):
        try:
            n = rhs.free_size()
        except Exception:
            n = 128
        c = 0.06 + n / 2400.0 * (1.0 if lhsT.dtype == BF16 else 2.4)
        return self.p.op('pe', lambda e: e.matmul(out, lhsT=lhsT, rhs=rhs, start=start, stop=stop), reads, writes, cost=c)

    def copy(self, eng, out, in_, reads, writes):
        if eng == 'act':
            return self.p.op('act', lambda e: e.copy(out=out, in_=in_), reads, writes, cost=self.ecost('act', out))
        return self.p.op(eng, lambda e: e.tensor_copy(out=out, in_=in_), reads, writes, cost=self.ecost(eng, out))

    def load_weight_bf16(self, dst, dst_key, src, ncols, src_c0=0, dst_c0=0, scale_bc=None):
        p = self.p
        CH = 1024 if ncols % 1024 == 0 else ncols
        for kc in range(NKC):
            for c0 in range(0, ncols, CH):
                i = self.stage_i
                self.stage_i += 1
                nst = len(self.stage)
                st = self.stage[i % nst]
                sk = f"stage{i % nst}"
                p.dma('sp', st[:, 0:CH], src[kc * 128:(kc + 1) * 128, src_c0 + c0:src_c0 + c0 + CH], reads=[], writes=[sk])
                eng = ['dve', 'act'][i % 2] if scale_bc is None else ['dve', 'pool'][i % 2]
                if scale_bc is None:
                    self.copy(eng, dst[:, kc, dst_c0 + c0:dst_c0 + c0 + CH], st[:, 0:CH], [sk], [dst_key])
                else:
                    self.tt(eng, dst[:, kc, dst_c0 + c0:dst_c0 + c0 + CH], st[:, 0:CH], scale_bc[:, c0:c0 + CH], ALU.mult,
                            [sk, 'bc_tiles'], [dst_key])

    def rms_rstd(self, src, src_key, TT, tag):
        bi = self.rms_i % len(self.sqb_l)
        self.rms_i += 1
        sqb, rstd = self.sqb_l[bi], self.rstd_l[bi]
        ksq, krs = f'sqb{bi}', f'rstd{bi}'
        if self.sqb_alias:
            ksq = 'yT0'
        for kc in range(NKC):
            if kc % 2 == 0:
                self.act(sqb[:, kc, :TT], src[:, kc, :TT], AF.Square, [src_key], [(ksq, kc) if not self.sqb_alias else ksq])
            else:
                self.tt('dve', sqb[:, kc, :TT], src[:, kc, :TT], src[:, kc, :TT], ALU.mult, [src_key], [(ksq, kc) if not self.sqb_alias else ksq])
        ps, pk = self.next_ps()
        for kc in range(NKC):
            self.mm(ps[:, :TT], self.ones_bf[:], sqb[:, kc, :TT], kc == 0, kc == NKC - 1, [(ksq, kc) if not self.sqb_alias else ksq, 'ones_bf'], [pk])
        self.act(rstd[:, :TT], ps[:, :TT], AF.Ln, [pk, 'consts'], [krs], bias=self.epsc[:, 0:1], scale=1.0 / D)
        self.act(rstd[:, :TT], rstd[:, :TT], AF.Exp, [krs], [krs], scale=-0.5)
        return rstd, krs

    def run_layer(self, li, kind, TT, w_in_cols, mixer_setup, mixer_tile, is_last):
        p = self.p
        T = self.T
        ntiles = T // TT
        with ExitStack() as les:
            self.les = les
            self.lname = f"L{li}"
            self.TT = TT
            self.W_in = self.lsb("W_in", [128, NKC, w_in_cols], BF16)
            self.W_out = self.lsb("W_out", [128, NKC, D], BF16)
            ndb = 2 if kind != 'rwkv' else 1
            self.hT = [self.lsb(f"hT{i}", [128, NKC, TT], F32) for i in range(ndb)]
            self.sqb_alias = (kind == 'rwkv')
            if not self.sqb_alias:
                self.sqb_l = [self.lsb(f"sqb{i}", [128, NKC, TT], BF16) for i in range(ndb)]
            self.rstd_l = [self.lsb(f"rstd{i}", [128, TT], F32) for i in range(ndb)]
            self.rms_i = 0
            self.hn_l = [self.lsb(f"hn{i}", [128, NKC, TT + 1], BF16) for i in range(ndb)]
            self.yTt_l = [self.lsb(f"yTt{i}", [128, NKC, TT], BF16) for i in range(ndb)]
            self.hn, self.hnk = self.hn_l[0], 'hn0'
            self.yTt, self.yTk = self.yTt_l[0], 'yT0'
            if self.sqb_alias:
                self.sqb_l = [self.yTt_l[0]]
            self.stage_i = 0
            loader = mixer_setup()
            with ExitStack() as ses:
                nst = 2 if kind == 'rwkv' else max(2, min(4, (self.nc.sbuf_bytes_remaining - 512) // 4096))
                self.stage = [ses.enter_context(self.nc.sbuf_tensor(f"stage{i}_{self.lname}", [128, 1024], F32)) for i in range(nst)]
                if loader is None:
                    self.load_weight_bf16(self.W_in, 'W_in', self.w_in_dram[kind], w_in_cols)
                    self.load_weight_bf16(self.W_out, 'W_out', self.w_out_dram[kind], D)
                else:
                    loader(ses)
                p.barrier()
            if getattr(self, 'post_setup', None) is not None:
                self.post_setup()
                self.post_setup = None
            hn0 = self.hn_l[0]
            p.op('pool', lambda e: e.memset(hn0[:, :, 0:1], 0.0), [], ['hn0'])

            def load(ti):
                buf = self.hT[ti % len(self.hT)]
                src = self.xT if self.first_layer else self.yT
                p.dma('sp', buf[:], src.rearrange("(c p) t -> p c t", p=128)[:, :, ti * TT:(ti + 1) * TT],
                      reads=[('hd', ti * TT // 128 + i) for i in range(TT // 128)], writes=[f"hT{ti % len(self.hT)}"])

            if self.use_sched:
                p.begin_record()
            load(0)
            for ti in range(ntiles):
                if len(self.hT) > 1:
                    if ti + 1 < ntiles:
                        load(ti + 1)
                elif ti > 0:
                    load(ti)
                h = self.hT[ti % len(self.hT)]
                hk = f"hT{ti % len(self.hT)}"
                rstd, rk = self.rms_rstd(h, hk, TT, 'in')
                g = self.norm_g
                prev_hn, prev_hnk = self.hn, self.hnk
                bi = ti % ndb
                self.hn, self.hnk = self.hn_l[bi], f'hn{bi}'
                self.yTt, self.yTk = self.yTt_l[bi], f'yT{bi}'
                if ti > 0:
                    self.copy('pool', self.hn[:, :, 0:1], prev_hn[:, :, TT:TT + 1], [prev_hnk], [self.hnk])
                for kc in range(NKC):
                    self.stt('dve', self.hn[:, kc, 1:TT + 1], h[:, kc, :], g[:, li, kc:kc + 1], rstd[:, :TT],
                             ALU.mult, ALU.mult, [hk, rk, 'consts'], [self.hnk])
                mixer_tile(ti)
                for j in range(NKC):
                    ps, pk = self.next_ps()
                    for kc in range(NKC):
                        self.mm(ps[:, :TT], self.W_out[:, kc, j * 128:(j + 1) * 128], self.yTt[:, kc, :TT],
                                kc == 0, kc == NKC - 1, ['W_out', self.yTk], [pk])
                    self.tt('dve', h[:, j, :], h[:, j, :], ps[:, :TT], ALU.add, [hk, pk], [hk])
                if is_last and self.do_final:
                    rstd, rk = self.rms_rstd(h, hk, TT, 'fin')
                    for kc in range(NKC):
                        self.stt('dve' if kc % 2 == 0 else 'pool', h[:, kc, :], h[:, kc, :], self.final_g[:, kc:kc + 1], rstd[:, :TT],
                                 ALU.mult, ALU.mult, [hk, rk, 'consts'], [hk])
                p.dma('sp', self.yT.rearrange("(c p) t -> p c t", p=128)[:, :, ti * TT:(ti + 1) * TT], h[:],
                      reads=[hk], writes=[('hd', (ti * TT) // 128 + i) for i in range(max(1, TT // 128))])
            if self.use_sched:
                p.flush()
            self.first_layer = False
            p.barrier()
        self.les = None

    def conv_layer(self, li, is_last):
        TT = 512

        def setup():
            self.yext = self.lsb("yext", [128, NKC, TT + 2], F32)
            self.zs = [self.lsb(f"zs{i}", [128, TT], F32) for i in range(2)]
            self.acc = [self.lsb(f"acc{i}", [128, TT], F32) for i in range(2)]
            self.sg = [self.lsb(f"sg{i}", [128, TT], F32) for i in range(2)]
            self.p.op('pool', lambda e: e.memset(self.yext[:, :, 0:2], 0.0), [], [('yext', j) for j in range(NKC)])

        def tile(ti):
            W = self.W_in
            cw = self.conv_w
            for j in range(NKC):
                zs, acc, sg = self.zs[j % 2], self.acc[j % 2], self.sg[j % 2]
                zk, ak, gk = f"zs{j % 2}", f"acc{j % 2}", f"sg{j % 2}"
                pss = []
                for blk in range(4):
                    ps, pk = self.next_ps()
                    col0 = blk * D + j * 128
                    for kc in range(NKC):
                        self.mm(ps[:, :TT], W[:, kc, col0:col0 + 128], self.hn[:, kc, 1:TT + 1], kc == 0, kc == NKC - 1,
                                ['W_in', self.hnk], [pk])
                    pss.append((ps, pk))
                (pb, pbk), (pc, pck), (pz, pzk), (pg, pgk) = pss
                yk = ('yext', j)
                self.copy('act', zs[:], pz[:, :TT], [pzk], [zk])
                if ti > 0:
                    self.copy('pool', self.yext[:, j, 0:2], self.yext[:, j, TT:TT + 2], [yk], [yk])
                self.tt('dve', self.yext[:, j, 2:TT + 2], pc[:, :TT], zs[:], ALU.mult, [pck, zk], [yk])
                self.act(acc[:], self.yext[:, j, 2:TT + 2], AF.Copy, [yk, 'consts'], [ak], scale=cw[:, j, 2:3])
                self.stt('pool', acc[:], self.yext[:, j, 1:TT + 1], cw[:, j, 1:2], acc[:], ALU.mult, ALU.add, [yk, ak, 'consts'], [ak])
                self.stt('pool', acc[:], self.yext[:, j, 0:TT], cw[:, j, 0:1], acc[:], ALU.mult, ALU.add, [yk, ak, 'consts'], [ak])
                self.act(sg[:], pg[:, :TT], AF.Silu, [pgk], [gk])
                self.tt('dve', acc[:], pb[:, :TT], acc[:], ALU.mult, [pbk, ak], [ak])
                self.tt('pool', self.yTt[:, j, :], acc[:], sg[:], ALU.mult, [ak, gk], [self.yTk])

        self.run_layer(li, 'conv', TT, 4 * D, setup, tile, is_last)

    def gmlp_layer(self, li, is_last):
        TT = 512

        def setup():
            p = self.p
            self.wsT = self.lsb("wsT", [128, 8, 128], F32)
            self.bs_bc = self.lsb("bs_bc", [128, 8, TT], F32)
            self.vg_bc = self.lsb("vg_bc", [128, D], F32)
            self.vn = [self.lsb(f"vn{i}", [128, D], F32) for i in range(TT // 128)]
            self.vss = self.lsb("vss", [128, 4], F32)
            self.junk = self.lsb("junk", [128, 512], F32)
            self.s_sb = [self.lsb(f"s_sb{i}", [128, TT], F32) for i in range(2)]
            self.sg = [self.lsb(f"sg{i}", [128, TT], F32) for i in range(2)]
            p.dma('sp', self.wsT[:], self.inputs['gmlp_wsT'], [], ['wsT'])
            for g in range(8):
                p.op('pool', lambda e: e.affine_select(out=self.wsT[:, g, :], in_=self.wsT[:, g, :], pattern=[[1, 128]],
                                                       compare_op=ALU.is_ge, fill=0.0, base=0, channel_multiplier=-1),
                     ['wsT'], ['wsT'])
            for r in range(TT // 128):
                p.dma('sp', self.bs_bc[:, :, r * 128:(r + 1) * 128],
                      self.inputs['gmlp_bs'].partition_broadcast(128), [], ['bs_bc'])
            p.dma('sp', self.vg_bc[:], self.inputs['gmlp_vg'].partition_broadcast(128), [], ['vg_bc'])

        def tile(ti):
            W = self.W_in
            nblk = TT // 128
            for blk in range(nblk):
                vn = self.vn[blk]
                vk = f"vn{blk}"
                halves = []
                for hf in range(2):
                    ps, pk = self.next_ps()
                    for kc in range(NKC):
                        self.mm(ps[:, :512], self.hn[:, kc, 1 + blk * 128:1 + (blk + 1) * 128],
                                W[:, kc, D + hf * 512:D + (hf + 1) * 512], kc == 0, kc == NKC - 1, ['W_in', self.hnk], [pk])
                    halves.append((ps, pk))
                for hf, (ps, pk) in enumerate(halves):
                    self.act(self.junk[:], ps[:, :512], AF.Square, [pk], ['junk', 'vss'], accum_out=self.vss[:, hf:hf + 1])
                self.tt('dve', self.vss[:, 2:3], self.vss[:, 0:1], self.vss[:, 1:2], ALU.add, ['vss'], ['vss'])
                self.act(self.vss[:, 3:4], self.vss[:, 2:3], AF.Ln, ['vss', 'consts'], ['vss'], bias=self.epsc[:, 0:1], scale=1.0 / D)
                self.act(self.vss[:, 3:4], self.vss[:, 3:4], AF.Exp, ['vss'], ['vss'], scale=-0.5)
                for hf, (ps, pk) in enumerate(halves):
                    self.stt('dve', vn[:, hf * 512:(hf + 1) * 512], ps[:, :512], self.vss[:, 3:4],
                             self.vg_bc[:, hf * 512:(hf + 1) * 512], ALU.mult, ALU.mult, [pk, 'vss', 'vg_bc'], [vk])
            for j in range(NKC):
                s_sb, sg = self.s_sb[j % 2], self.sg[j % 2]
                sk, gk = f"s_sb{j % 2}", f"sg{j % 2}"
                ps, pk = self.next_ps()
                for blk in range(nblk):
                    self.mm(ps[:, blk * 128:(blk + 1) * 128], self.vn[blk][:, j * 128:(j + 1) * 128], self.wsT[:, j, :], True, True,
                            [f"vn{blk}", 'wsT'], [pk])
                self.tt('dve', s_sb[:], ps[:, :TT], self.bs_bc[:, j, :], ALU.add, [pk, 'bs_bc'], [sk])
                pu, puk = self.next_ps()
                for kc in range(NKC):
                    self.mm(pu[:, :TT], W[:, kc, j * 128:(j + 1) * 128], self.hn[:, kc, 1:TT + 1], kc == 0, kc == NKC - 1,
                            ['W_in', self.hnk], [puk])
                pg, pgk = self.next_ps()
                for kc in range(NKC):
                    self.mm(pg[:, :TT], W[:, kc, 2 * D + j * 128:2 * D + (j + 1) * 128], self.hn[:, kc, 1:TT + 1], kc == 0,
                            kc == NKC - 1, ['W_in', self.hnk], [pgk])
                self.act(sg[:], pg[:, :TT], AF.Silu, [pgk], [gk])
                self.tt('dve', s_sb[:], pu[:, :TT], s_sb[:], ALU.mult, [puk, sk], [sk])
                self.tt('pool', self.yTt[:, j, :], s_sb[:], sg[:], ALU.mult, [sk, gk], [self.yTk])

        self.run_layer(li, 'gmlp', TT, 3 * D, setup, tile, is_last)


    def make_ident(self, ident, key):
        p = self.p
        p.op('pool', lambda e: e.memset(ident[:], 1.0), [], [key])
        p.op('pool', lambda e: e.affine_select(out=ident[:], in_=ident[:], pattern=[[-1, 128]], compare_op=ALU.is_equal,
                                               fill=0.0, base=0, channel_multiplier=1), [key], [key])

    def make_block_masks(self, C, maskT, colmask, rowmask, strict=False):
        p = self.p
        nch = 128 // C
        if maskT is not None:
            p.op('pool', lambda e: e.memset(maskT[:], 1.0), [], ['masks'])
            p.op('pool', lambda e: e.affine_select(out=maskT[:], in_=maskT[:], pattern=[[1, 128]], compare_op=ALU.is_ge if not strict else ALU.is_gt,
                                                   fill=0.0, base=0, channel_multiplier=-1), ['masks'], ['masks'])
            for c in range(1, nch):
                p.op('pool', lambda e, c=c: e.affine_select(out=maskT[:, c * C:(c + 1) * C], in_=maskT[:, c * C:(c + 1) * C], pattern=[[0, C]],
                                                            compare_op=ALU.is_ge, fill=0.0, base=-c * C, channel_multiplier=1), ['masks'], ['masks'])
        if colmask is not None:
            p.op('pool', lambda e: e.memset(colmask[:], 0.0), [], ['masks'])
            for c in range(nch):
                p.op('pool', lambda e, c=c: e.memset(colmask[:, c, c * C:(c + 1) * C], 1.0), ['masks'], ['masks'])
        if rowmask is not None:
            p.op('pool', lambda e: e.memset(rowmask[:], 1.0), [], ['masks'])
            for c in range(nch):
                p.op('pool', lambda e, c=c: e.affine_select(out=rowmask[:, c:c + 1], in_=rowmask[:, c:c + 1], pattern=[[0, 1]],
                                                            compare_op=ALU.is_ge, fill=0.0, base=-c * C, channel_multiplier=1), ['masks'], ['masks'])
                p.op('pool', lambda e, c=c: e.affine_select(out=rowmask[:, c:c + 1], in_=rowmask[:, c:c + 1], pattern=[[0, 1]],
                                                            compare_op=ALU.is_ge, fill=0.0, base=c * C + C - 1, channel_multiplier=-1), ['masks'], ['masks'])

    def hgrn_layer(self, li, is_last):
        TT = 256
        C = 32
        NB = TT // 128
        NCH = TT // C

        def setup():
            p = self.p
            L = self.lsb
            self.ident = L("ident", [128, 128], F32)
            self.make_ident(self.ident, 'ident')
            self.maskT = L("maskT", [128, 128], F32)
            self.colmask = L("colmask", [128, 4, 128], F32)
            self.rowmask = L("rowmask", [128, 4], F32)
            self.make_block_masks(C, self.maskT, self.colmask, self.rowmask)
            self.resetm = L("resetm", [128, TT], F32)
            self.ones_t = L("ones_t", [128, TT], F32)
            p.op('pool', lambda e: e.memset(self.ones_t[:], 1.0), [], ['masks'])
            p.op('pool', lambda e: e.memset(self.resetm[:], 1.0), [], ['masks'])
            p.op('pool', lambda e: e.memset(self.resetm[:].rearrange("p (n c) -> p n c", c=C)[:, :, 0:1], 0.0), ['masks'], ['masks'])
            self.gn_bc = L("gn_bc", [128, D], F32)
            p.dma('sp', self.gn_bc[:], self.inputs['hgrn_gn_g'].partition_broadcast(128), [], ['gn_bc'])
            self.lbl = L("lbl", [128, 4, NKC], F32)
            self.lbt = L("lbt", [128, 4, NKC], F32)
            p.dma('sp', self.lbl[:], self.inputs['hgrn_lbl'], [], ['lbl'])
            self.act(self.lbl[:], self.lbl[:], AF.Exp, ['lbl'], ['lbl'])
            self.tt('dve', self.lbt[:, 0, :], self.lbl[:, 0, :], self.lbl[:, 1, :], ALU.add, ['lbl'], ['lbt'])
            self.tt('dve', self.lbt[:, 0, :], self.lbt[:, 0, :], self.lbl[:, 2, :], ALU.add, ['lbl', 'lbt'], ['lbt'])
            self.tt('dve', self.lbt[:, 0, :], self.lbt[:, 0, :], self.lbl[:, 3, :], ALU.add, ['lbl', 'lbt'], ['lbt'])
            p.op('dve', lambda e: e.reciprocal(out=self.lbt[:, 3, :], in_=self.lbt[:, 0, :]), ['lbt'], ['lbt'])
            p.op('dve', lambda e: e.memset(self.lbt[:, 1, :], 0.0), ['lbt'], ['lbt'])
            for i in range(1, li + 1):
                self.tt('dve', self.lbt[:, 1, :], self.lbt[:, 1, :], self.lbl[:, i, :], ALU.add, ['lbl', 'lbt'], ['lbt'])
            self.tt('dve', self.lbt[:, 1, :], self.lbt[:, 1, :], self.lbt[:, 3, :], ALU.mult, ['lbt'], ['lbt'])
            self.ts('dve', self.lbt[:, 2, :], self.lbt[:, 1, :], -1.0, 1.0, ALU.mult, ALU.add, ['lbt'], ['lbt'])
            self.S = L("S_hgrn", [128, NKC, 128], F32)
            p.op('pool', lambda e: e.memset(self.S[:], 0.0), [], [('S', j) for j in range(NKC)])
            names = ['f', 'kk', 'bb', 'qe', 'dd', 'sg']
            self.tmps = []
            for q in range(2):
                tm = {n: L(f"h_{n}{q}", [128, TT], F32) for n in names}
                tm['e1'] = tm['f']
                tm['ko'] = tm['dd']
                tm['ke_bf'] = L(f"h_ke_bf{q}", [128, TT], BF16)
                tm['qe_bf'] = L(f"h_qe_bf{q}", [128, TT], BF16)
                tm['kom'] = L(f"kom{q}", [128, 4, NB, 128], BF16)
                self.tmps.append(tm)
            self.sgate = [L(f"sgate{q}", [128, TT], BF16) for q in range(3)]
            self.qem = [L(f"qem{q}", [128, 4, TT], BF16) for q in range(3)]
            self.v_bf = [L(f"v_bf{q}", [128, NB, 128], BF16) for q in range(3)]
            self.attm = [L(f"attm{q}", [128, NB, 128], BF16) for q in range(3)]
            self.u_sb = [L(f"u_sb{q}", [128, NCH, 128], F32) for q in range(3)]
            self.dec = [L(f"dec{q}", [128, NCH], F32) for q in range(3)]
            self.S_all2 = [L(f"S_all{q}", [128, 5, 128], F32) for q in range(2)]
            self.S_bf2 = [L(f"S_bf{q}", [128, NCH, 128], BF16) for q in range(2)]
            self.on2 = [L(f"on{q}", [128, NB, 128], F32) for q in range(2)]
            self.oss2 = [L(f"oss{q}", [128, 2 * NB], F32) for q in range(2)]
            self.junk2 = [L(f"junk{q}", [128, 128], F32) for q in range(2)]
            self.ps_pool = [0, 1, 2, 3]
            self.nm_rr = 0

        def nm_ps():
            i = [4, 5][self.nm_rr % 2]
            self.nm_rr += 1
            return self.psums[i], f"ps{i}"

        def proj(col0):
            ps, pk = self.next_ps()
            for kc in range(NKC):
                self.mm(ps[:, :TT], self.W_in[:, kc, col0:col0 + 128], self.hn[:, kc, 1:TT + 1], kc == 0, kc == NKC - 1, ['W_in', self.hnk], [pk])
            return ps, pk

        def A_gen(j):
            q2 = j % 2
            t = self.tmps[q2]
            kom = t['kom']
            W = self.W_in
            q = j % 3
            lb, oml = self.lbt[:, 1, :], self.lbt[:, 2, :]
            sgate, qem, v_bf, attm, u_sb, dec = self.sgate[q], self.qem[q], self.v_bf[q], self.attm[q], self.u_sb[q], self.dec[q]
            ksg, kqem, kv, katt, ku, kdec = f'sgate{q}', f'qem{q}', f'v_bf{q}', f'attm{q}', f'u_sb{q}', f'dec{q}'
            pf, pfk = proj(D + j * 128)
            self.act(t['f'][:], pf[:, :TT], AF.Exp, [pfk], [f't_f{q2}'], scale=-1.0)
            self.tt('pool', t['f'][:], t['f'][:], self.ones_t[:], ALU.add, [f't_f{q2}', 'masks'], [f't_f{q2}'])
            self.p.op('dve', lambda e: e.reciprocal(out=t['f'][:], in_=t['f'][:]), [f't_f{q2}'], [f't_f{q2}'], cost=0.45)
            self.ts('dve', t['f'][:], t['f'][:], oml[:, j:j + 1], lb[:, j:j + 1], ALU.mult, ALU.add, [f't_f{q2}', 'lbt'], [f't_f{q2}'])
            self.act(t['kk'][:], t['f'][:], AF.Identity, [f't_f{q2}'], [f't_kk{q2}'], scale=-1.0, bias=self.epsc[:, 2:3])
            self.act(t['dd'][:], t['f'][:], AF.Ln, [f't_f{q2}'], [f't_dd{q2}'])
            self.p.op('dve', lambda e: e.tensor_tensor_scan(out=t['bb'][:], data0=self.resetm[:], data1=t['dd'][:], initial=0.0,
                                                            op0=ALU.mult, op1=ALU.add), [f't_dd{q2}', 'masks'], [f't_bb{q2}'])
            yield
            pq, pqk = proj(j * 128)
            self.act(t['e1'][:], t['bb'][:], AF.Exp, [f't_bb{q2}'], [f't_f{q2}'])
            self.tt('dve', t['qe'][:], pq[:, :TT], t['e1'][:], ALU.mult, [pqk, f't_f{q2}'], [f't_qe{q2}'])
            self.copy('act', t['qe_bf'][:], t['qe'][:], [f't_qe{q2}'], [f't_qe_bf{q2}'])
            qe4 = t['qe'][:].rearrange("p (b t) -> p b t", t=128)
            for c in range(4):
                self.tt('pool', qem[:, c, :].rearrange("p (b t) -> p b t", t=128), qe4,
                        self.colmask[:, c:c + 1, :].to_broadcast([128, NB, 128]), ALU.mult, [f't_qe{q2}', 'masks'], [kqem])
            yield
            self.act(t['e1'][:], t['bb'][:], AF.Exp, [f't_bb{q2}'], [f't_f{q2}'], scale=-1.0)
            self.tt('pool', t['ke_bf'][:], t['kk'][:], t['e1'][:], ALU.mult, [f't_kk{q2}', f't_f{q2}'], [f't_ke_bf{q2}'])
            b3 = t['bb'][:].rearrange("p (n c) -> p n c", c=C)
            self.act(dec[:], b3[:, :, C - 1], AF.Exp, [f't_bb{q2}'], [kdec])
            self.tt('pool', t['dd'][:].rearrange("p (n c) -> p n c", c=C), b3[:, :, C - 1:C].to_broadcast([128, NCH, C]), b3, ALU.subtract,
                    [f't_bb{q2}'], [f't_dd{q2}'])
            self.act(t['dd'][:], t['dd'][:], AF.Exp, [f't_dd{q2}'], [f't_dd{q2}'])
            self.tt('pool', t['ko'][:], t['kk'][:], t['dd'][:], ALU.mult, [f't_kk{q2}', f't_dd{q2}'], [f't_dd{q2}'])
            pg, pgk = proj(3 * D + j * 128)
            self.act(t['sg'][:], pg[:, :TT], AF.Exp, [pgk], [f't_sg{q2}'], scale=-1.0)
            self.tt('pool', t['sg'][:], t['sg'][:], self.ones_t[:], ALU.add, [f't_sg{q2}', 'masks'], [f't_sg{q2}'])
            self.p.op('dve', lambda e: e.reciprocal(out=t['sg'][:], in_=t['sg'][:]), [f't_sg{q2}'], [f't_sg{q2}'], cost=0.45)
            self.tt('dve', sgate[:], pg[:, :TT], t['sg'][:], ALU.mult, [pgk, f't_sg{q2}'], [ksg])
            yield
            pv, pvk = self.next_ps()
            for blk in range(NB):
                for kc in range(NKC):
                    self.mm(pv[:, blk * 128:(blk + 1) * 128], self.hn[:, kc, 1 + blk * 128:1 + (blk + 1) * 128],
                            W[:, kc, 2 * D + j * 128:2 * D + (j + 1) * 128], kc == 0, kc == NKC - 1, ['W_in', self.hnk], [pvk])
            self.copy('act', v_bf[:].rearrange("p b v -> p (b v)"), pv[:, :TT], [pvk], [kv])
            yield
            ps, pk = nm_ps()
            for blk in range(NB):
                cs = slice(blk * 128, (blk + 1) * 128)
                self.mm(ps[:, cs], t['ke_bf'][:, cs], t['qe_bf'][:, cs], True, True, [f't_ke_bf{q2}', f't_qe_bf{q2}'], [pk])
            self.tt('dve', attm[:], ps[:, :TT].rearrange("p (b t) -> p b t", t=128), self.maskT[:, None, :].to_broadcast([128, NB, 128]),
                    ALU.mult, [pk, 'masks'], [katt])
            ps, pk = nm_ps()
            for blk in range(NB):
                cs = slice(blk * 128, (blk + 1) * 128)
                self.p.op('pe', lambda e, ps=ps, cs=cs: e.transpose(ps[:, cs], t['ko'][:, cs], self.ident[:]), [f't_dd{q2}', 'ident'], [pk])
            for c in range(4):
                self.act(kom[:, c, :, :].rearrange("p b k -> p (b k)"), ps[:, :TT], AF.Copy, [pk, 'masks'], [f'kom{q2}'],
                         scale=self.rowmask[:, c:c + 1])
            yield
            for blk in range(NB):
                ps, pk = nm_ps()
                for c in range(4):
                    self.mm(ps[:, c * 128:(c + 1) * 128], kom[:, c, blk, :], v_bf[:, blk, :], True, True, [f'kom{q2}', kv], [pk])
                self.copy('act' if blk % 2 else 'dve', u_sb[:, blk * 4:(blk + 1) * 4, :].rearrange("p c v -> p (c v)"), ps[:, 0:512], [pk], [ku])
                if blk % 2:
                    yield

        def B_gen(j):
            P = self.psums
            q = j % 3
            sgate, qem, v_bf, attm, u_sb, dec = self.sgate[q], self.qem[q], self.v_bf[q], self.attm[q], self.u_sb[q], self.dec[q]
            ksg, kqem, kv, katt, ku, kdec = f'sgate{q}', f'qem{q}', f'v_bf{q}', f'attm{q}', f'u_sb{q}', f'dec{q}'
            kS = ('S', j)
            q2 = j % 2
            SA = self.S_all2[q2]
            S_bf, on, oss, junk = self.S_bf2[q2], self.on2[q2], self.oss2[q2], self.junk2[q2]
            kSA, kSbf, kon, koss, kjunk = f'S_all{q2}', f'S_bf{q2}', f'on{q2}', f'oss{q2}', f'junk{q2}'
            self.copy('pool', SA[:, 0, :], self.S[:, j, :], [kS], [kSA])
            for blk in range(NB):
                for c in range(4):
                    n = blk * 4 + c
                    self.stt('dve', SA[:, c + 1, :], SA[:, c, :], dec[:, n:n + 1], u_sb[:, n, :], ALU.mult, ALU.add, [kSA, kdec, ku], [kSA])
                self.copy('act', S_bf[:, blk * 4:(blk + 1) * 4, :].rearrange("p c v -> p (c v)"),
                          SA[:, 0:4, :].rearrange("p c v -> p (c v)"), [kSA], [kSbf])
                if blk < NB - 1:
                    self.copy('dve', SA[:, 0, :], SA[:, 4, :], [kSA], [kSA])
                yield
            self.copy('pool', self.S[:, j, :], SA[:, 4, :], [kSA], [kS])
            po, pok = P[6], 'ps6'
            for blk in range(NB):
                cs = slice(blk * 128, (blk + 1) * 128)
                self.mm(po[:, cs], attm[:, blk, :], v_bf[:, blk, :], True, False, [katt, kv], [pok])
                for c in range(4):
                    self.mm(po[:, cs], qem[:, c, cs], S_bf[:, blk * 4 + c, :], False, c == 3, [kqem, kSbf], [pok])
                if blk % 2:
                    yield
            for blk in range(NB):
                cs = slice(blk * 128, (blk + 1) * 128)
                self.act(junk[:], po[:, cs], AF.Square, [pok], [kjunk, koss], accum_out=oss[:, blk:blk + 1])
            self.act(oss[:, NB:2 * NB], oss[:, 0:NB], AF.Ln, [koss, 'consts'], [koss], bias=self.epsc[:, 0:1], scale=1.0 / 128)
            self.act(oss[:, NB:2 * NB], oss[:, NB:2 * NB], AF.Exp, [koss], [koss], scale=-0.5)
            self.tt('dve', on[:], po[:, :TT].rearrange("p (b v) -> p b v", v=128),
                    oss[:, NB:2 * NB, None].to_broadcast([128, NB, 128]), ALU.mult, [pok, koss], [kon])
            self.tt('pool', on[:], on[:], self.gn_bc[:, None, j * 128:(j + 1) * 128].to_broadcast([128, NB, 128]), ALU.mult,
                    [kon, 'gn_bc'], [kon])
            yield
            py, pyk = P[7], 'ps7'
            for blk in range(NB):
                cs = slice(blk * 128, (blk + 1) * 128)
                self.p.op('pe', lambda e, cs=cs, blk=blk: e.transpose(py[:, cs], on[:, blk, :], self.ident[:]), [kon, 'ident'], [pyk])
            self.tt('dve', self.yTt[:, j, :], py[:, :TT], sgate[:], ALU.mult, [pyk, ksg], [self.yTk])
            yield

        def drive(gens):
            gens = [g for g in gens if g is not None]
            while gens:
                for g in list(gens):
                    try:
                        next(g)
                    except StopIteration:
                        gens.remove(g)

        def step(g):
            try:
                next(g)
                return True
            except StopIteration:
                return False

        def tile(ti):
            A = {0: A_gen(0), 1: A_gen(1)}
            while step(A[0]):
                step(A[1])
            for sl in range(NKC):
                must = [B_gen(sl)]
                if sl + 1 < NKC:
                    must.append(A[sl + 1])
                opt = None
                if sl + 2 < NKC:
                    A[sl + 2] = A_gen(sl + 2)
                    opt = A[sl + 2]
                while must:
                    for g in list(must):
                        if not step(g):
                            must.remove(g)
                    if opt is not None and not step(opt):
                        opt = None

        self.run_layer(li, 'hgrn', TT, 4 * D, setup, tile, is_last)
        self.ps_pool = list(range(8))

    def rwkv_layer(self, li, is_last):
        TT = 256
        NB = TT // 128
        WC = 3200
        NDT = self.neu_dt
        LC = -0.6065306597126334

        def setup():
            p = self.p
            L = self.lsb
            self.ident = L("ident", [128, 128], F32)
            self.make_ident(self.ident, 'ident')
            self.ident_n = L("ident_n", [128, 128], NDT)
            self.copy('dve', self.ident_n[:], self.ident[:], ['ident'], ['ident'])
            self.maskS = L("maskS", [128, 128], F32)
            self.maskI = L("maskI", [128, 128], F32)
            self.maskSL = L("maskSL", [128, 128], F32)
            for (m, pat, cm, cmp_) in ((self.maskS, 1, -1, ALU.is_gt), (self.maskI, 1, -1, ALU.is_ge), (self.maskSL, -1, 1, ALU.is_gt)):
                p.op('pool', lambda e, m=m: e.memset(m[:], 1.0), [], ['masks'])
                p.op('pool', lambda e, m=m, pat=pat, cm=cm, cmp_=cmp_: e.affine_select(
                    out=m[:], in_=m[:], pattern=[[pat, 128]], compare_op=cmp_, fill=0.0, base=0, channel_multiplier=cm), ['masks'], ['masks'])
            self.blockones = L("blockones", [128, 128], F32)
            p.op('pool', lambda e: e.memset(self.blockones[:], 1.0), [], ['masks'])
            p.op('pool', lambda e: e.memset(self.blockones[0:64, 64:128], 0.0), ['masks'], ['masks'])
            p.op('pool', lambda e: e.memset(self.blockones[64:128, 0:64], 0.0), ['masks'], ['masks'])
            self.resetm = L("resetm", [128, TT], F32)
            p.op('pool', lambda e: e.memset(self.resetm[:], 1.0), [], ['masks'])
            p.op('pool', lambda e: e.memset(self.resetm[:].rearrange("p (n c) -> p n c", c=128)[:, :, 0:1], 0.0), ['masks'], ['masks'])
            p.op('pool', lambda e: e.memset(self.epsc[:, 1:2], GN_EPS), [], ['consts'])
            self.mu_fm = L("mu_fm", [128, 33], F32)
            self.omu_fm = L("omu_fm", [128, 33], F32)
            p.dma('sp', self.mu_fm[:], self.inputs['rwkv_mu_fm'], [], ['rw_vecs'])
            self.ts('dve', self.omu_fm[:], self.mu_fm[:], -1.0, 1.0, ALU.mult, ALU.add, ['rw_vecs'], ['rw_vecs'])
            self.vecs = L("rw_vecs", [128, 5, NKC], F32)
            p.dma('sp', self.vecs[:], self.inputs['rwkv_vecs'], [], ['rw_vecs'])
            self.lw2 = L("lw2", [128, D], F32)
            p.dma('sp', self.lw2[:], self.inputs['rwkv_lw2'], [], ['lw2'])
            self.gng_bc = L("gng_bc", [128, D], BF16)
            self.gnb_bc = L("gnb_bc", [128, D], BF16)
            self.Wva = L("Wva", [128, NKC, D], BF16)
            self.Wvb = L("Wvb", [128, NKC, D], BF16)
            src = self.w_in_dram['rwkv']

            def loader(tes):
                muv = tes.enter_context(self.nc.sbuf_tensor("muv_bc", [128, D], F32))
                for (dst, nm) in ((self.gng_bc, 'rwkv_gn_g'), (self.gnb_bc, 'rwkv_gn_b')):
                    p.dma('sp', muv[:], self.inputs[nm].partition_broadcast(128), [], ['muv'])
                    self.copy('dve', dst[:], muv[:], ['muv'], ['bc_tiles'])
                p.dma('sp', muv[:], self.inputs['rwkv_mu'][2 * D:3 * D].partition_broadcast(128), ['muv'], ['bc_tiles', 'muv'])
                self.load_weight_bf16(self.Wvb, 'Wv', src, D, src_c0=2 * D, scale_bc=muv)
                self.ts('dve', muv[:], muv[:], -1.0, 1.0, ALU.mult, ALU.add, ['bc_tiles'], ['bc_tiles'])
                self.load_weight_bf16(self.Wva, 'Wv', src, D, src_c0=2 * D, scale_bc=muv)
                self.load_weight_bf16(self.W_in, 'W_in', src, 2 * D, src_c0=0, dst_c0=0)
                self.load_weight_bf16(self.W_in, 'W_in', src, 128, src_c0=3 * D, dst_c0=2 * D)
                self.load_weight_bf16(self.W_in, 'W_in', src, D, src_c0=3 * D + 128, dst_c0=2 * D + 128)
                self.load_weight_bf16(self.W_out, 'W_out', self.w_out_dram['rwkv'], D)
            self.S = L("S_rwkv", [128, NKC, 64], F32)
            p.op('pool', lambda e: e.memset(self.S[:], 0.0), [], [('S', j) for j in range(NKC)])
            self.pcar = L("pcar", [128, 25], F32)
            p.op('pool', lambda e: e.memset(self.pcar[:], 0.0), [], [('pcar', i) for i in range(25)])
            self.pm_ext = [L(f"pm_ext{i}", [128, TT + 1], F32) for i in range(2)]
            self.pm_i = 0
            names = ['r', 'k', 'tmp', 'sigw', 'a', 'kk', 'rn', 'kmod', 'bbv', 'c']
            self.tmp = {'lo': L("w_lo", [128, TT], F32)}
            self.tmpP = [{n: L(f"w_{n}0", [128, TT], F32) for n in names}, None]
            self.tmp2 = [dict(), dict()]
            for n in ['khat', 'bhat']:
                self.tmp2[0][n] = L(f"w_{n}0", [128, TT], F32)
            for n in ['rt_bf', 'bt_bf', 'at_bf', 'kt_h0', 'kt_h1', 'bt_h0', 'bt_h1', 'at_h0', 'at_h1']:
                self.tmp2[0][n] = L(f"w_{n}0", [128, TT], BF16)
            self.hm = L("hm", [128, 2], F32)
            p.op('pool', lambda e: e.memset(self.hm[:], 0.0), [], ['masks'])
            p.op('pool', lambda e: e.memset(self.hm[0:64, 0:1], 1.0), ['masks'], ['masks'])
            p.op('pool', lambda e: e.memset(self.hm[64:128, 1:2], 1.0), ['masks'], ['masks'])
            self.pt = {n: [L(f"wp_{n}{q}", [128, TT], F32 if n in ('at', 'rt') else BF16) for q in range(2)] for n in ['at', 'rt', 'rkr', 'sgate']}
            self.blockones_bf = L("blockones_bf", [128, 128], BF16)
            self.copy('dve', self.blockones_bf[:], self.blockones[:], ['masks'], ['masks'])
            self.v_sb = [L(f"v_sb{q}", [128, NB, 128], F32) for q in range(2)]
            self.v_bf = [L(f"v_bf{q}", [128, NB, 128], BF16) for q in range(2)]
            self.dec = [L(f"dec{q}", [128, NB], F32) for q in range(3)]

            def post_setup():
                self.tmpP[1] = {n: L(f"w_{n}1", [128, TT], F32) for n in names}
                self.PbP[1] = [L(f"Pb1{i}", [128, NCHN, 128], NDT) for i in range(2)]
                self.QbP[1] = [L(f"Qb1{i}", [128, NCHN, 128], NDT) for i in range(2)]
                for n in ['khat', 'bhat']:
                    self.tmp2[1][n] = L(f"w_{n}1", [128, TT], F32)
                for n in ['rt_bf', 'bt_bf', 'at_bf', 'kt_h0', 'kt_h1', 'bt_h0', 'bt_h1', 'at_h0', 'at_h1']:
                    self.tmp2[1][n] = L(f"w_{n}1", [128, TT], BF16)
                for n in ['rkr', 'sgate']:
                    self.pt[n].append(L(f"wp_{n}2", [128, TT], BF16))
                for n in ['at', 'rt']:
                    self.pt[n].append(self.pt[n][0])
                self.ysb = [L(f"ysb{i}", [128, 256], F32) for i in range(2)]
                self.yn2 = [self.yn, L("yn1", [128, 128], F32)]
                self.bon2 = [self.bon, L("bon1", [128, 128], F32)]
                self.gst2 = [self.gst, L("gst1", [128, 12], F32)]
                self.junk2 = [self.junk, L("junk1", [128, 64], F32)]
                self.bcount = 0
                self.v_sb.append(L("v_sb2", [128, NB, 128], F32))
                self.v_bf.append(L("v_bf2", [128, NB, 128], BF16))
            self.post_setup = post_setup
            NCHN = NB * 2
            self.PbP = [[L(f"Pb0{i}", [128, NCHN, 128], NDT) for i in range(2)], None]
            self.QbP = [[L(f"Qb0{i}", [128, NCHN, 128], NDT) for i in range(2)], None]
            self.NT = [L(f"NT{q}", [128, NCHN, 128], NDT) for q in range(2)]
            self.Aak = [L(f"Aak{q}", [128, NCHN, 128], BF16) for q in range(2)]
            self.Ark = [L(f"Ark{q}", [128, NCHN, 128], BF16) for q in range(2)]
            self.Arb = [L(f"Arb{q}", [128, NCHN, 128], BF16) for q in range(2)]
            self.ident4 = L("ident4", [128, NCHN, 128], NDT)
            for c in range(NCHN):
                self.copy('dve', self.ident4[:, c, :], self.ident[:], ['ident'], ['ident'])
            self.khm = [[L(f"khm{q}{b}", [128, 128], BF16) for b in range(NB)] for q in range(2)]
            self.bhm = [[L(f"bhm{q}{b}", [128, 128], BF16) for b in range(NB)] for q in range(2)]
            self.Z_sb = L("Z_sb", [128, 128], NDT)
            self.U_bf = L("U_bf", [128, 128], BF16)
            self.yn = L("yn", [128, 128], F32)
            self.bon = L("bon", [128, 128], F32)
            self.gst = L("gst", [128, 12], F32)
            self.junk = L("junk", [128, 64], F32)
            self.ps_pool = [0, 1]
            return loader

        NMB = [[2, 3], [4, 7]]
        self.nm_rrs = [0, 0]

        def nm_ps(q=0):
            i = NMB[q][self.nm_rrs[q] % 2]
            self.nm_rrs[q] += 1
            return self.psums[i], f"ps{i}"

        def shift(ps, pk, dst, dk, idx, mt):
            pm = self.pm_ext[self.pm_i % 2]
            pmk = f"pm_ext{self.pm_i % 2}"
            self.pm_i += 1
            ck = ('pcar', idx)
            self.copy('pool', pm[:, 0:1], self.pcar[:, idx:idx + 1], [ck], [pmk])
            self.act(pm[:, 1:TT + 1], ps[:, :TT], AF.Copy, [pk, 'rw_vecs'], [pmk], scale=self.mu_fm[:, mt:mt + 1])
            self.copy('pool', self.pcar[:, idx:idx + 1], pm[:, TT:TT + 1], [pmk], [ck])
            self.act(dst, ps[:, :TT], AF.Copy, [pk, 'rw_vecs'], [dk], scale=self.omu_fm[:, mt:mt + 1])
            self.tt('dve', dst, dst, pm[:, 0:TT], ALU.add, [dk, pmk], [dk])

        def proj(col0):
            ps, pk = self.next_ps()
            for kc in range(NKC):
                self.mm(ps[:, :TT], self.W_in[:, kc, col0:col0 + 128], self.hn[:, kc, 1:TT + 1], kc == 0, kc == NKC - 1, ['W_in', self.hnk], [pk])
            return ps, pk

        hsl = [slice(0, 64), slice(64, 128)]

        def A_gen(j):
            V = self.vecs
            q = j % 2
            q3 = j % 3
            t = dict(self.tmp)
            t.update(self.tmpP[q])
            t['e1'] = t['rn']
            t['e2'] = t['tmp']
            t.update(self.tmp2[q])
            self.Pb, self.Qb = self.PbP[q], self.QbP[q]
            PAR = set(self.tmp2[0].keys())
            jc = slice(j * 128, (j + 1) * 128)
            at, rt, rkr, sgate = self.pt['at'][q], self.pt['rt'][q], self.pt['rkr'][q3], self.pt['sgate'][q3]
            kat, krt, krkr, ksg = f'p_at{q}', f'p_rt{q}', f'p_rkr{q3}', f'p_sgate{q3}'
            v_sb, v_bf, dec = self.v_sb[q3], self.v_bf[q3], self.dec[q3]
            kv, kvb, kdec = f'v_sb{q3}', f'v_bf{q3}', f'dec{q3}'
            ps, pk = proj(j * 128)
            shift(ps, pk, t['r'][:], f't_r{q}', j, j)
            ps, pk = proj(D + j * 128)
            shift(ps, pk, t['k'][:], f't_k{q}', 8 + j, 8 + j)
            yield
            ps, pk = proj(2 * D + 128 + j * 128)
            shift(ps, pk, t['tmp'][:], f't_tmp{q}', 16 + j, 25 + j)
            self.act(sgate[:], t['tmp'][:], AF.Silu, [f't_tmp{q}'], [ksg])
            pv, pvk = self.next_ps()
            for blk in range(NB):
                n = 0
                for kc in range(NKC):
                    for (Wv, off) in ((self.Wva, 1), (self.Wvb, 0)):
                        self.mm(pv[:, blk * 128:(blk + 1) * 128], self.hn[:, kc, off + blk * 128:off + (blk + 1) * 128], Wv[:, kc, jc],
                                n == 0, n == 2 * NKC - 1, ['Wv', self.hnk], [pvk])
                        n += 1
            self.copy('act', v_sb[:].rearrange("p b v -> p (b v)"), pv[:, :TT], [pvk], [kv])
            self.copy('dve', v_bf[:].rearrange("p b v -> p (b v)"), pv[:, :TT], [pvk], [kvb])
            yield
            pw, pwk = self.next_ps()
            self.mm(pw[:, :TT], self.lw2[0:64, jc], t['lo'][0:64, :], True, True, ['lw2', 't_lo'], [pwk])
            self.act(t['sigw'][:], pw[:, :TT], AF.Sigmoid, [pwk, 'rw_vecs'], [f't_sigw{q}'], bias=V[:, 0, j:j + 1])
            pa, pak = self.next_ps()
            self.mm(pa[:, :TT], self.lw2[64:128, jc], t['lo'][64:128, :], True, True, ['lw2', 't_lo'], [pak])
            self.act(t['a'][:], pa[:, :TT], AF.Sigmoid, [pak, 'rw_vecs'], [f't_a{q}'], bias=V[:, 1, j:j + 1])
            self.ts('dve', t['kk'][:], t['k'][:], V[:, 2, j:j + 1], None, ALU.mult, None, [f't_k{q}', 'rw_vecs'], [f't_kk{q}'])
            self.tt('pool', t['tmp'][:], t['kk'][:], t['kk'][:], ALU.mult, [f't_kk{q}'], [f't_tmp{q}'])
            pn, pnk = self.next_ps()
            self.mm(pn[:, :TT], self.blockones[:], t['tmp'][:], True, True, ['masks', f't_tmp{q}'], [pnk])
            self.ts('dve', t['rn'][:], pn[:, :TT], 1e-24, None, ALU.max, None, [pnk], [f't_rn{q}'])
            self.act(t['rn'][:], t['rn'][:], AF.Ln, [f't_rn{q}'], [f't_rn{q}'])
            self.act(t['rn'][:], t['rn'][:], AF.Exp, [f't_rn{q}'], [f't_rn{q}'], scale=-0.5)
            self.tt('pool', t['kk'][:], t['kk'][:], t['rn'][:], ALU.mult, [f't_kk{q}', f't_rn{q}'], [f't_kk{q}'])
            self.ts('dve', t['tmp'][:], t['a'][:], -1.0, V[:, 3, j:j + 1], ALU.add, ALU.mult, [f't_a{q}', 'rw_vecs'], [f't_tmp{q}'])
            self.stt('dve', t['kmod'][:], t['tmp'][:], 1.0, t['k'][:], ALU.add, ALU.mult, [f't_tmp{q}', f't_k{q}'], [f't_kmod{q}'])
            self.tt('pool', t['bbv'][:], t['kk'][:], t['a'][:], ALU.mult, [f't_kk{q}', f't_a{q}'], [f't_bbv{q}'])
            yield
            self.p.op('dve', lambda e: e.tensor_tensor_scan(out=t['c'][:], data0=self.resetm[:], data1=t['sigw'][:], initial=0.0,
                                                            op0=ALU.mult, op1=ALU.add), [f't_sigw{q}', 'masks'], [f't_c{q}'])
            self.act(t['e1'][:], t['c'][:], AF.Exp, [f't_c{q}'], [f't_rn{q}'], scale=LC)
            self.tt('pool', rt[:], t['r'][:], t['e1'][:], ALU.mult, [f't_r{q}', f't_rn{q}'], [krt])
            self.copy('act', t['rt_bf'][:], rt[:], [krt], [f't_rt_bf{q}'])
            self.act(t['e2'][:], t['c'][:], AF.Exp, [f't_c{q}'], [f't_tmp{q}'], scale=-LC)
            for hd in range(2):
                self.stt('dve', t[f'kt_h{hd}'][:], t['kmod'][:], self.hm[:, hd:hd + 1], t['e2'][:], ALU.mult, ALU.mult,
                         [f't_kmod{q}', f't_tmp{q}', 'masks'], [f't_kt_h{hd}_{q}'])
                self.stt('dve', t[f'bt_h{hd}'][:], t['bbv'][:], self.hm[:, hd:hd + 1], t['e2'][:], ALU.mult, ALU.mult,
                         [f't_bbv{q}', f't_tmp{q}', 'masks'], [f't_bt_h{hd}_{q}'])
            self.tt('pool', t['bt_bf'][:], t['bbv'][:], t['e2'][:], ALU.mult, [f't_bbv{q}', f't_tmp{q}'], [f't_bt_bf{q}'])
            self.tt('pool', t['e1'][:], t['c'][:], t['sigw'][:], ALU.subtract, [f't_c{q}', f't_sigw{q}'], [f't_rn{q}'])
            self.act(t['e1'][:], t['e1'][:], AF.Exp, [f't_rn{q}'], [f't_rn{q}'], scale=LC)
            self.stt('dve', at[:], t['kk'][:], -1.0, t['e1'][:], ALU.mult, ALU.mult, [f't_kk{q}', f't_rn{q}'], [kat])
            self.copy('act', t['at_bf'][:], at[:], [kat], [f't_at_bf{q}'])
            for hd in range(2):
                self.act(t[f'at_h{hd}'][:], at[:], AF.Copy, [kat, 'masks'], [f't_at_h{hd}_{q}'], scale=self.hm[:, hd:hd + 1])
            yield
            c3 = t['c'][:].rearrange("p (n c) -> p n c", c=128)
            self.tt('pool', t['e2'][:].rearrange("p (n c) -> p n c", c=128), c3[:, :, 127:128].to_broadcast([128, NB, 128]), c3,
                    ALU.subtract, [f't_c{q}'], [f't_tmp{q}'])
            self.act(t['e2'][:], t['e2'][:], AF.Exp, [f't_tmp{q}'], [f't_tmp{q}'], scale=LC)
            self.act(dec[:], c3[:, :, 127], AF.Exp, [f't_c{q}'], [kdec], scale=LC)
            self.tt('pool', t['khat'][:], t['kmod'][:], t['e2'][:], ALU.mult, [f't_kmod{q}', f't_tmp{q}'], [f't_khat{q}'])
            self.tt('dve', t['bhat'][:], t['bbv'][:], t['e2'][:], ALU.mult, [f't_bbv{q}', f't_tmp{q}'], [f't_bhat{q}'])
            self.stt('dve', rkr[:], t['r'][:], V[:, 4, j:j + 1], t['kmod'][:], ALU.mult, ALU.mult, [f't_r{q}', 'rw_vecs', f't_kmod{q}'], [krkr])
            yield
            NCH = NB * 2
            specs = {'P': ('bt_h', 'at_bf', self.maskS, self.Pb[0], f'Pb{q}0'),
                     'Q': ('at_h', 'bt_bf', self.maskSL, self.Qb[0], f'Qb{q}0'),
                     'ak': ('kt_h', 'at_bf', self.maskS, self.Aak[q], f'Aak{q}'),
                     'rk': ('kt_h', 'rt_bf', self.maskI, self.Ark[q], f'Ark{q}'),
                     'rb': ('bt_h', 'rt_bf', self.maskI, self.Arb[q], f'Arb{q}')}
            for name in ('P', 'Q', 'ak', 'rk', 'rb'):
                lh, rh, mask, dst, dk = specs[name]
                ps, pk = nm_ps(q)
                for c in range(NCH):
                    blk, hd = c // 2, c % 2
                    cs = slice(blk * 128, (blk + 1) * 128)
                    self.mm(ps[:, c * 128:(c + 1) * 128], t[f'{lh}{hd}'][:, cs], t[rh][:, cs], True, True, [f't_{lh}{hd}_{q}', f't_{rh}{q}'], [pk])
                self.tt('dve', dst[:], ps[:, 0:NCH * 128].rearrange("p (c t) -> p c t", c=NCH),
                        mask[:, None, :].to_broadcast([128, NCH, 128]), ALU.mult, [pk, 'masks'], [dk])
                if name == 'Q':
                    self.tt('pool', self.NT[q][:], self.ident4[:], self.Pb[0][:], ALU.add, ['ident', f'Pb{q}0'], [f'NT{q}'])
                    yield
            for blk in range(NB):
                cs = slice(blk * 128, (blk + 1) * 128)
                for (srcn, dst, dk) in (('khat', self.khm[q][blk], f'khm{q}{blk}'), ('bhat', self.bhm[q][blk], f'bhm{q}{blk}')):
                    ps, pk = nm_ps(q)
                    self.p.op('pe', lambda e, ps=ps, srcn=srcn, cs=cs: e.transpose(ps[:, 0:128], t[srcn][:, cs], self.ident[:]),
                              [f't_{srcn}{q}', 'ident'], [pk])
                    self.copy('act', dst[:], ps[:, 0:128], [pk], [dk])
            yield
            NTq, kNT = self.NT[q], f'NT{q}'
            for i in range(6):
                a_, b_ = i % 2, (i + 1) % 2
                Pa, Qa, Pn, Qn = self.Pb[a_], self.Qb[a_], self.Pb[b_], self.Qb[b_]
                kPa, kQa, kPn, kQn = f'Pb{q}{a_}', f'Qb{q}{a_}', f'Pb{q}{b_}', f'Qb{q}{b_}'
                if i < 5:
                    ps, pk = nm_ps(q)
                    for c in range(NCH):
                        self.mm(ps[:, c * 128:(c + 1) * 128], Qa[:, c, :], Pa[:, c, :], True, True, [kQa, kPa], [pk])
                    self.copy('act', Pn[:].rearrange("p c t -> p (c t)"), ps[:, 0:NCH * 128], [pk], [kPn])
                ps, pk = nm_ps(q)
                for c in range(NCH):
                    self.mm(ps[:, c * 128:(c + 1) * 128], Pa[:, c, :], Qa[:, c, :], True, True, [kPa, kQa], [pk])
                self.copy('dve' if i % 2 == 0 else 'act', Qn[:].rearrange("p c t -> p (c t)"), ps[:, 0:NCH * 128], [pk], [kQn])
                yield
                ps, pk = nm_ps(q)
                for c in range(NCH):
                    self.mm(ps[:, c * 128:(c + 1) * 128], Qn[:, c, :], NTq[:, c, :], True, True, [kQn, kNT], [pk])
                self.tt('dve', NTq[:].rearrange("p c t -> p (c t)"), NTq[:].rearrange("p c t -> p (c t)"), ps[:, 0:NCH * 128], ALU.add,
                        [kNT, pk], [kNT])
                yield

        def B_gen(j):
            t = self.tmp
            P = self.psums
            q = j % 2
            q3 = j % 3
            jc = slice(j * 128, (j + 1) * 128)
            at, rt, rkr, sgate = self.pt['at'][q], self.pt['rt'][q], self.pt['rkr'][q3], self.pt['sgate'][q3]
            kat, krt, krkr, ksg = f'p_at{q}', f'p_rt{q}', f'p_rkr{q3}', f'p_sgate{q3}'
            v_sb, v_bf, dec = self.v_sb[q3], self.v_bf[q3], self.dec[q3]
            kv, kvb, kdec = f'v_sb{q3}', f'v_bf{q3}', f'dec{q3}'
            kS = ('S', j)
            for blk in range(NB):
                cs = slice(blk * 128, (blk + 1) * 128)
                khm, bhm = self.khm[q][blk], self.bhm[q][blk]
                kkh, kbh = f'khm{q}{blk}', f'bhm{q}{blk}'
                pz, pzk = P[5], 'ps5'
                for hd in range(2):
                    hs, hc, c = hsl[hd], slice(hd * 64, (hd + 1) * 64), blk * 2 + hd
                    self.mm(pz[:, hc], self.Aak[q][:, c, :], v_bf[:, blk, hc], True, False, [f'Aak{q}', kvb], [pzk])
                    self.mm(pz[:, hc], at[hs, cs], self.S[hs, j, :], False, True, [kat, kS], [pzk])
                self.copy('act', self.Z_sb[:], pz[:, 0:128], [pzk], ['Z_sb'])
                yield
                for hd in range(2):
                    hc, c = slice(hd * 64, (hd + 1) * 64), blk * 2 + hd
                    self.mm(pz[:, hc], self.NT[q][:, c, :], self.Z_sb[:, hc], True, True, [f'NT{q}', 'Z_sb'], [pzk])
                self.copy('act', self.U_bf[:], pz[:, 0:128], [pzk], ['U_bf'])
                yield
                self.mm(pz[:, 0:128], khm[:], v_bf[:, blk, :], True, False, [kkh, kvb], [pzk])
                self.mm(pz[:, 0:128], bhm[:], self.U_bf[:], False, True, [kbh, 'U_bf'], [pzk])
                py, pyk = P[6], 'ps6'
                for hd in range(2):
                    hs, hc, c = hsl[hd], slice(hd * 64, (hd + 1) * 64), blk * 2 + hd
                    yc = slice(hd * 128, hd * 128 + 64)
                    bc_ = slice(hd * 128 + 64, hd * 128 + 128)
                    self.mm(py[:, yc], self.Ark[q][:, c, :], v_bf[:, blk, hc], True, False, [f'Ark{q}', kvb], [pyk])
                    self.mm(py[:, yc], rt[hs, cs], self.S[hs, j, :], False, False, [krt, kS], [pyk])
                    self.mm(py[:, yc], self.Arb[q][:, c, :], self.U_bf[:, hc], False, True, [f'Arb{q}', 'U_bf'], [pyk])
                    self.mm(py[:, bc_], rkr[hs, cs], self.blockones_bf[hs, hs], True, True, [krkr, 'masks'], [pyk])
                for hd in range(2):
                    hs, hc = hsl[hd], slice(hd * 64, (hd + 1) * 64)
                    self.stt('dve', self.S[hs, j, :], self.S[hs, j, :], dec[hs, blk:blk + 1], pz[hs, hc], ALU.mult, ALU.add,
                             [kS, kdec, pzk], [kS])
                yield
                bp = self.bcount % 2
                self.bcount += 1
                ysb, kys = self.ysb[bp], f'ysb{bp}'
                g, kg = self.gst2[bp], f'gst{bp}'
                yn, kyn = self.yn2[bp], f'yn{bp}'
                bon, kbon = self.bon2[bp], f'bon{bp}'
                junk, kjunk = self.junk2[bp], f'junk{bp}'
                self.copy('act', ysb[:], py[:, 0:256], [pyk], [kys])
                for hd in range(2):
                    hc = slice(hd * 64, (hd + 1) * 64)
                    yc = slice(hd * 128, hd * 128 + 64)
                    bc_ = slice(hd * 128 + 64, hd * 128 + 128)
                    self.act(junk[:], ysb[:, yc], AF.Identity, [kys], [kjunk, kg], accum_out=g[:, hd:hd + 1])
                    self.act(junk[:], ysb[:, yc], AF.Square, [kys], [kjunk, kg], accum_out=g[:, 2 + hd:3 + hd])
                    self.tt('pool', bon[:, hc], ysb[:, bc_], v_sb[:, blk, hc], ALU.mult, [kys, kv], [kbon])
                self.ts('dve', g[:, 4:6], g[:, 0:2], 1.0 / 64, None, ALU.mult, None, [kg], [kg])
                self.tt('dve', g[:, 6:8], g[:, 4:6], g[:, 4:6], ALU.mult, [kg], [kg])
                self.stt('dve', g[:, 8:10], g[:, 2:4], 1.0 / 64, g[:, 6:8], ALU.mult, ALU.subtract, [kg], [kg])
                self.act(g[:, 8:10], g[:, 8:10], AF.Ln, [kg, 'consts'], [kg], bias=self.epsc[:, 1:2])
                self.act(g[:, 8:10], g[:, 8:10], AF.Exp, [kg], [kg], scale=-0.5)
                for hd in range(2):
                    hc = slice(hd * 64, (hd + 1) * 64)
                    yc = slice(hd * 128, hd * 128 + 64)
                    self.ts('dve', yn[:, hc], ysb[:, yc], g[:, 4 + hd:5 + hd], g[:, 8 + hd:9 + hd], ALU.subtract, ALU.mult,
                            [kys, kg], [kyn])
                yield
                self.tt('pool', yn[:], yn[:], self.gng_bc[:, jc], ALU.mult, [kyn, 'bc_tiles'], [kyn])
                self.tt('pool', yn[:], yn[:], self.gnb_bc[:, jc], ALU.add, [kyn, 'bc_tiles'], [kyn])
                self.tt('pool', yn[:], yn[:], bon[:], ALU.add, [kyn, kbon], [kyn])
                ps, pk = nm_ps(q)
                self.p.op('pe', lambda e, ps=ps, yn=yn: e.transpose(ps[:, 0:128], yn[:], self.ident[:]), [kyn, 'ident'], [pk])
                self.tt('dve', self.yTt[:, j, cs], ps[:, 0:128], sgate[:, cs], ALU.mult, [pk, ksg], [self.yTk])
                yield

        def drive(gens):
            gens = [g for g in gens if g is not None]
            while gens:
                for g in list(gens):
                    try:
                        next(g)
                    except StopIteration:
                        gens.remove(g)

        def tile(ti):
            t = self.tmp
            ps, pk = proj(2 * D)
            shift(ps, pk, t['lo'][:], 't_lo', 24, 24)
            self.act(t['lo'][0:64, :], t['lo'][0:64, :], AF.Tanh, ['t_lo'], ['t_lo'])
            for j in range(NKC):
                drive([A_gen(j)])
                drive([B_gen(j)])

        self.run_layer(li, 'rwkv', TT, WC, setup, tile, is_last)
        self.ps_pool = list(range(8))

    def build(self):
        nc = self.nc
        T = self.T
        self.xT = self.din("xT", [D, T])
        self.yT = nc.dram_tensor("yT", [D, T], F32, kind="ExternalOutput").ap()
        d_norm_g = self.din("norm_g", [128, 4, NKC])
        d_final_g = self.din("final_g", [128, NKC])
        self.w_in_dram, self.w_out_dram = {}, {}
        kinds = [k for (_, k) in self.layers]
        if 'conv' in kinds:
            self.w_in_dram['conv'] = self.din("conv_w_in", [D, 4 * D])
            self.w_out_dram['conv'] = self.din("conv_w_out", [D, D])
            d_conv_w = self.din("conv_w", [128, NKC, 3])
        if 'rwkv' in kinds:
            self.w_in_dram['rwkv'] = self.din("rwkv_w_in", [D, 4 * D + 128])
            self.w_out_dram['rwkv'] = self.din("rwkv_w_out", [D, D])
            self.din("rwkv_mu_fm", [128, 33])
            self.din("rwkv_mu", [4 * D + 128])
            self.din("rwkv_vecs", [128, 5, NKC])
            self.din("rwkv_lw2", [128, D])
            self.din("rwkv_gn_g", [D])
            self.din("rwkv_gn_b", [D])
        if 'hgrn' in kinds:
            self.w_in_dram['hgrn'] = self.din("hgrn_w_in", [D, 4 * D])
            self.w_out_dram['hgrn'] = self.din("hgrn_w_out", [D, D])
            self.din("hgrn_gn_g", [D])
            self.din("hgrn_lbl", [128, 4, NKC])
        if 'gmlp' in kinds:
            self.w_in_dram['gmlp'] = self.din("gmlp_w_in", [D, 3 * D])
            self.w_out_dram['gmlp'] = self.din("gmlp_w_out", [D, D])
            self.din("gmlp_wsT", [128, 8, 128])
            self.din("gmlp_bs", [8, 128])
            self.din("gmlp_vg", [D])
        with ExitStack() as es:
            self.es = es
            nc.allow_low_precision("bf16 matmul operands, fp32 accumulation")
            self.p = p = Prog(nc, es)
            self.psums = [es.enter_context(nc.psum_tensor(f"ps{i}", [128, 512], F32)) for i in range(8)]
            self.ps_rr = 0
            self.ps_pool = list(range(8))
            self.ones_bf = self.sb("ones_bf", [128, 128], BF16)
            self.epsc = self.sb("epsc", [128, 4], F32)
            self.norm_g = self.sb("norm_g_sb", [128, 4, NKC], F32)
            self.final_g = self.sb("final_g_sb", [128, NKC], F32)
            p.op('pool', lambda e: e.memset(self.ones_bf[:], 1.0), [], ['ones_bf'])
            p.op('pool', lambda e: e.memset(self.epsc[:, 0:1], RMS_EPS), [], ['consts'])
            p.op('pool', lambda e: e.memset(self.epsc[:, 2:3], 1.0), ['consts'], ['consts'])
            p.dma('sp', self.norm_g[:], d_norm_g, [], ['consts'])
            p.dma('sp', self.final_g[:], d_final_g, [], ['consts'])
            if 'conv' in kinds:
                self.conv_w = self.sb("conv_w_sb", [128, NKC, 3], F32)
                p.dma('sp', self.conv_w[:], d_conv_w, [], ['consts'])
            self.first_layer = True
            for n, (li, kind) in enumerate(self.layers):
                is_last = n == len(self.layers) - 1
                if kind == 'conv':
                    self.conv_layer(li, is_last)
                elif kind == 'gmlp':
                    self.gmlp_layer(li, is_last)
                elif kind == 'hgrn':
                    self.hgrn_layer(li, is_last)
                elif kind == 'rwkv':
                    self.rwkv_layer(li, is_last)
                else:
                    raise ValueError(kind)
            p.finish('sp')
            self.stats = (p.n_ins, p.n_wait)
        return nc


def prep_inputs(inp, b, layers):
    f = np.float32
    m = {}
    m["xT"] = np.ascontiguousarray(np.asarray(inp["x"][b], f).T)
    m["norm_g"] = np.ascontiguousarray(np.asarray(inp["norm_g"], f).reshape(4, NKC, 128).transpose(2, 0, 1))
    m["final_g"] = np.ascontiguousarray(np.asarray(inp["final_g"], f).reshape(NKC, 128).T)
    kinds = [k for (_, k) in layers]
    if 'conv' in kinds:
        m["conv_w_in"] = np.ascontiguousarray(np.asarray(inp["conv_w_in"][0], f))
        m["conv_w_out"] = np.ascontiguousarray(np.asarray(inp["conv_w_out"][0], f))
        m["conv_w"] = np.ascontiguousarray(np.asarray(inp["conv_w"][0], f).reshape(3, NKC, 128).transpose(2, 1, 0))
    if 'rwkv' in kinds:
        m["rwkv_w_in"] = np.ascontiguousarray(np.asarray(inp["rwkv_w_in"][0], f))
        m["rwkv_w_out"] = np.ascontiguousarray(np.asarray(inp["rwkv_w_out"][0], f))
        mu = np.asarray(inp["rwkv_mu"][0], f)
        m["rwkv_mu"] = np.ascontiguousarray(mu)
        m["rwkv_mu_fm"] = np.ascontiguousarray(mu.reshape(33, 128).T)
        vecs = np.stack([np.asarray(inp[k][0], f).reshape(NKC, 128) for k in
                         ("rwkv_w0", "rwkv_a0", "rwkv_k_k", "rwkv_k_a", "rwkv_r_k")], axis=0)
        m["rwkv_vecs"] = np.ascontiguousarray(vecs.transpose(2, 0, 1))
        m["rwkv_lw2"] = np.ascontiguousarray(np.concatenate([np.asarray(inp["rwkv_w_w2"][0], f), np.asarray(inp["rwkv_w_a2"][0], f)], axis=0))
        m["rwkv_gn_g"] = np.ascontiguousarray(np.asarray(inp["rwkv_gn_g"][0], f))
        m["rwkv_gn_b"] = np.ascontiguousarray(np.asarray(inp["rwkv_gn_b"][0], f))
    if 'hgrn' in kinds:
        m["hgrn_w_in"] = np.ascontiguousarray(np.asarray(inp["hgrn_w_in"][0], f))
        m["hgrn_w_out"] = np.ascontiguousarray(np.asarray(inp["hgrn_w_out"][0], f))
        m["hgrn_gn_g"] = np.ascontiguousarray(np.asarray(inp["hgrn_gn_g"][0], f))
        m["hgrn_lbl"] = np.ascontiguousarray(np.asarray(inp["hgrn_lb_logits"], f).reshape(4, NKC, 128).transpose(2, 0, 1))
    if 'gmlp' in kinds:
        m["gmlp_w_in"] = np.ascontiguousarray(np.asarray(inp["gmlp_w_in"][0], f))
        m["gmlp_w_out"] = np.ascontiguousarray(np.asarray(inp["gmlp_w_out"][0], f))
        m["gmlp_wsT"] = np.ascontiguousarray(np.asarray(inp["gmlp_w_s"][0], f).transpose(2, 0, 1))
        m["gmlp_bs"] = np.ascontiguousarray(np.asarray(inp["gmlp_b_s"][0], f))
        m["gmlp_vg"] = np.ascontiguousarray(np.asarray(inp["gmlp_v_g"][0], f))
    return m


FULL_LAYERS = [(0, 'rwkv'), (1, 'hgrn'), (2, 'conv'), (3, 'gmlp')]


def kernel(**inputs):
    x = np.asarray(inputs["x"])
    B, T, _ = x.shape
    layers = FULL_LAYERS
    bld = Builder(T, layers)
    nc = bld.build()
    in_maps = []
    zeros = None
    for c in range(8):
        if c % 2 == 0:
            in_maps.append(prep_inputs(inputs, c // 2, layers))
        else:
            if zeros is None:
                zeros = {k: np.zeros_like(v) for k, v in in_maps[0].items()}
            in_maps.append(zeros)
    res = run_bass_kernel_spmd(nc, in_maps, core_ids=list(range(8)))
    out = np.stack([np.asarray(res.results[2 * b]["yT"]).T for b in range(B)], axis=0)
    return out.astype(np.float32)
```

```python
import numpy as np
from contextlib import ExitStack
import concourse.bass as bass
import concourse.mybir as mybir
from concourse.bass_utils import run_bass_kernel_spmd

F32 = mybir.dt.float32
BF16 = mybir.dt.bfloat16
ALU = mybir.AluOpType
AF = mybir.ActivationFunctionType
AX = mybir.AxisListType

D = 1024
NKC = 8
RMS_EPS = 1e-6
GN_EPS = 64e-5


class Prog:
    LIMIT = 30000

    def __init__(self, nc, es, n_dma_sems=24):
        self.nc = nc
        self.es = es
        self.engs = {'pe': nc.tensor, 'act': nc.scalar, 'dve': nc.vector,
                     'pool': nc.gpsimd, 'sp': nc.sync}
        self.sems = {}
        self.epoch = {k: 0 for k in self.engs}
        self.cnt = {k: 0 for k in self.engs}
        for k in self.engs:
            self.sems[(k, 0)] = es.enter_context(nc.semaphore(f"s_{k}_0"))
        self.dma_sems = []
        for i in range(n_dma_sems):
            key = ('dma', i)
            self.sems[key] = es.enter_context(nc.semaphore(f"s_dma_{i}"))
            self.cnt[key] = 0
            self.dma_sems.append(key)
        self.dma_rr = 0
        self.waited = {k: {} for k in self.engs}
        self.bufs = {}
        self.n_wait = 0
        self.n_ins = 0

    def _deps(self, reads, writes):
        deps = set()
        for k in reads:
            b = self.bufs.get(k)
            if b and b['w']:
                deps.add(b['w'])
        for k in writes:
            b = self.bufs.get(k)
            if b:
                if b['w']:
                    deps.add(b['w'])
                deps.update(b['r'])
        return deps

    def _wait(self, eng, deps):
        e = self.engs[eng]
        best = {}
        for (sk, v) in deps:
            if sk[0] == eng and eng == 'pe':
                continue
            if best.get(sk, 0) < v:
                best[sk] = v
        for sk, v in best.items():
            if self.waited[eng].get(sk, 0) >= v:
                continue
            e.wait_ge(self.sems[sk], v)
            self.waited[eng][sk] = v
            self.n_wait += 1

    def _record(self, tok, reads, writes):
        for k in reads:
            b = self.bufs.setdefault(k, {'w': None, 'r': []})
            b['r'].append(tok)
            if len(b['r']) > 64:
                best = {}
                for (sk, v) in b['r']:
                    if best.get(sk, 0) < v:
                        best[sk] = v
                b['r'] = list(best.items())
        for k in writes:
            b = self.bufs.setdefault(k, {'w': None, 'r': []})
            b['w'] = tok
            b['r'] = []

    @staticmethod
    def _excl(reads, writes):
        ps = [k for k in reads if isinstance(k, str) and k.startswith('ps')]
        if ps:
            reads = [k for k in reads if k not in ps]
            writes = list(writes) + ps
        return reads, writes

    disabled = False
    recording = None
    SYNC_LAT = 0.45

    def begin_record(self):
        self.recording = []

    def flush(self):
        rec = self.recording
        self.recording = None
        if not rec:
            return
        n = len(rec)
        preds = [None] * n
        succs = [[] for _ in range(n)]
        last_w = {}
        readers = {}
        for i, (kind, eng, fn, reads, writes, cost, lat) in enumerate(rec):
            ps = set()
            for k in reads:
                w = last_w.get(k)
                if w is not None:
                    ps.add(w)
            for k in writes:
                w = last_w.get(k)
                if w is not None:
                    ps.add(w)
                ps.update(readers.get(k, ()))
            ps.discard(i)
            preds[i] = ps
            for pi in ps:
                succs[pi].append(i)
            for k in reads:
                readers.setdefault(k, []).append(i)
            for k in writes:
                last_w[k] = i
                readers[k] = []
        npred = [len(p_) for p_ in preds]
        ready = [i for i in range(n) if npred[i] == 0]
        eng_free = {}
        end_t = [0.0] * n
        done_t = [0.0] * n
        order = []
        blevel = [0.0] * n
        for i in range(n - 1, -1, -1):
            kind, eng, fn, reads, writes, cost, lat = rec[i]
            b = 0.0
            for si in succs[i]:
                v = blevel[si] + (self.SYNC_LAT if rec[si][1] != eng else 0.0)
                if v > b:
                    b = v
            blevel[i] = b + cost + lat

        def est(i):
            kind, eng, fn, reads, writes, cost, lat = rec[i]
            t = eng_free.get(eng, 0.0)
            for pi in preds[i]:
                tp = done_t[pi] + (self.SYNC_LAT if rec[pi][1] != eng else 0.0)
                if tp > t:
                    t = tp
            return t
        EPS = 0.25
        while ready:
            ests = [(est(i), i) for i in ready]
            tmin = min(ests)[0]
            best = None
            for (t, i) in ests:
                if t <= tmin + EPS:
                    if best is None or blevel[i] > blevel[best[1]] or (blevel[i] == blevel[best[1]] and i < best[1]):
                        best = (t, i)
            t1, i = best
            ready.remove(i)
            kind, eng, fn, reads, writes, cost, lat = rec[i]
            end_t[i] = t1 + cost
            done_t[i] = t1 + cost + lat
            eng_free[eng] = end_t[i]
            order.append(i)
            for si in succs[i]:
                npred[si] -= 1
                if npred[si] == 0:
                    ready.append(si)
        assert len(order) == n, (len(order), n)
        self.sched_span = max(done_t) if done_t else 0.0
        for i in order:
            kind, eng, fn, reads, writes, cost, lat = rec[i]
            if kind == 'op':
                self.op(eng, fn, reads, writes)
            else:
                out, in_, kw = fn
                self.dma(eng, out, in_, reads, writes, **kw)

    def op(self, eng, fn, reads=(), writes=(), cost=None):
        if self.disabled:
            return None
        if self.recording is not None:
            reads, writes = self._excl(reads, writes)
            if cost is None:
                cost = {'pe': 0.2, 'act': 0.45, 'dve': 0.45, 'pool': 0.7, 'sp': 0.1}[eng]
            self.recording.append(('op', eng, fn, list(reads), list(writes), cost, 0.0))
            return None
        reads, writes = self._excl(reads, writes)
        deps = self._deps(reads, writes)
        self._wait(eng, deps)
        ins = fn(self.engs[eng])
        if self.cnt[eng] >= self.LIMIT:
            self.epoch[eng] += 1
            ep = self.epoch[eng]
            self.sems[(eng, ep)] = self.es.enter_context(self.nc.semaphore(f"s_{eng}_{ep}"))
            self.cnt[eng] = 0
        sk = (eng, self.epoch[eng])
        self.cnt[eng] += 1
        ins.then_inc(self.sems[sk], 1)
        self._record((sk, self.cnt[eng]), reads, writes)
        self.n_ins += 1
        return ins

    def dma(self, eng, out, in_, reads=(), writes=(), **kw):
        if self.disabled:
            return None
        if self.recording is not None:
            self.recording.append(('dma', eng, (out, in_, kw), list(reads), list(writes), 0.15, 6.0))
            return None
        deps = self._deps(reads, writes)
        sk = self.dma_sems[self.dma_rr]
        self.dma_rr = (self.dma_rr + 1) % len(self.dma_sems)
        if self.cnt[sk] > 0:
            deps.add((sk, self.cnt[sk]))
        self._wait(eng, deps)
        ins = self.engs[eng].dma_start(out=out, in_=in_, **kw)
        self.cnt[sk] += 16
        ins.then_inc(self.sems[sk], 16)
        self._record((sk, self.cnt[sk]), reads, writes)
        self.n_ins += 1
        return ins

    def all_tokens(self):
        deps = set()
        for k, b in self.bufs.items():
            if b['w']:
                deps.add(b['w'])
            deps.update(b['r'])
        return deps

    def barrier(self):
        deps = self.all_tokens()
        for eng in self.engs:
            d = set(x for x in deps)
            self._wait(eng, d)

    def finish(self, eng='sp'):
        self._wait(eng, self.all_tokens())


class Builder:
    def __init__(self, T, layers, do_final=True, neu_dt=None):
        self.neu_dt = neu_dt if neu_dt is not None else BF16
        self.use_sched = True
        self.T = T
        self.layers = layers
        self.do_final = do_final
        self.nc = bass.Bass("TRN2", target_bir_lowering=False)
        self.inputs = {}

    def din(self, name, shape):
        t = self.nc.dram_tensor(name, list(shape), F32, kind="ExternalInput").ap()
        self.inputs[name] = t
        return t

    def sb(self, name, shape, dt=F32):
        return self.es.enter_context(self.nc.sbuf_tensor(name, list(shape), dt))

    def lsb(self, name, shape, dt=F32):
        return self.les.enter_context(self.nc.sbuf_tensor(f"{name}_{self.lname}", list(shape), dt))

    def next_ps(self):
        pool = self.ps_pool
        i = pool[self.ps_rr % len(pool)]
        self.ps_rr += 1
        return self.psums[i], f"ps{i}"

    @staticmethod
    def ecost(eng, ap):
        try:
            n = ap.free_size()
        except Exception:
            n = 256
        if eng == 'act':
            return 0.22 + n * 0.00075
        if eng == 'dve':
            return 0.2 + n * 0.00095
        if eng == 'pool':
            return 0.2 + n * 0.0021
        return 0.2

    def tt(self, eng, out, in0, in1, op, reads, writes):
        return self.p.op(eng, lambda e: e.tensor_tensor(out=out, in0=in0, in1=in1, op=op), reads, writes, cost=self.ecost(eng, out))

    def ts(self, eng, out, in0, s1, s2, op0, op1, reads, writes):
        if s2 is None:
            return self.p.op(eng, lambda e: e.tensor_scalar(out=out, in0=in0, scalar1=s1, scalar2=None, op0=op0), reads, writes, cost=self.ecost(eng, out))
        return self.p.op(eng, lambda e: e.tensor_scalar(out=out, in0=in0, scalar1=s1, scalar2=s2, op0=op0, op1=op1), reads, writes, cost=self.ecost(eng, out))

    def stt(self, eng, out, in0, scalar, in1, op0, op1, reads, writes):
        eng = 'dve'
        return self.p.op(eng, lambda e: e.scalar_tensor_tensor(out=out, in0=in0, scalar=scalar, in1=in1, op0=op0, op1=op1), reads, writes, cost=self.ecost(eng, out))

    def act(self, out, in_, func, reads, writes, bias=None, scale=1.0, accum_out=None):
        kw = {}
        if bias is not None:
            kw['bias'] = bias
        if accum_out is not None:
            kw['accum_out'] = accum_out
        return self.p.op('act', lambda e: e.activation(out=out, in_=in_, func=func, scale=scale, **kw), reads, writes, cost=self.ecost('act', in_))

    def mm(self, out, lhsT, rhs, start, stop, reads, writes):
        try:
            n = rhs.free_size()
        except Exception:
            n = 128
        c = 0.06 + n / 2400.0 * (1.0 if lhsT.dtype == BF16 else 2.4)
        return self.p.op('pe', lambda e: e.matmul(out, lhsT=lhsT, rhs=rhs, start=start, stop=stop), reads, writes, cost=c)

    def copy(self, eng, out, in_, reads, writes):
        if eng == 'act':
            return self.p.op('act', lambda e: e.copy(out=out, in_=in_), reads, writes, cost=self.ecost('act', out))
        return self.p.op(eng, lambda e: e.tensor_copy(out=out, in_=in_), reads, writes, cost=self.ecost(eng, out))

    def load_weight_bf16(self, dst, dst_key, src, ncols, src_c0=0, dst_c0=0, scale_bc=None):
        p = self.p
        CH = 1024 if ncols % 1024 == 0 else ncols
        for kc in range(NKC):
            for c0 in range(0, ncols, CH):
                i = self.stage_i
                self.stage_i += 1
                nst = len(self.stage)
                st = self.stage[i % nst]
                sk = f"stage{i % nst}"
                p.dma('sp', st[:, 0:CH], src[kc * 128:(kc + 1) * 128, src_c0 + c0:src_c0 + c0 + CH], reads=[], writes=[sk])
                eng = ['dve', 'act'][i % 2] if scale_bc is None else ['dve', 'pool'][i % 2]
                if scale_bc is None:
                    self.copy(eng, dst[:, kc, dst_c0 + c0:dst_c0 + c0 + CH], st[:, 0:CH], [sk], [dst_key])
                else:
                    self.tt(eng, dst[:, kc, dst_c0 + c0:dst_c0 + c0 + CH], st[:, 0:CH], scale_bc[:, c0:c0 + CH], ALU.mult,
                            [sk, 'bc_tiles'], [dst_key])

    def rms_rstd(self, src, src_key, TT, tag):
        bi = self.rms_i % len(self.sqb_l)
        self.rms_i += 1
        sqb, rstd = self.sqb_l[bi], self.rstd_l[bi]
        ksq, krs = f'sqb{bi}', f'rstd{bi}'
        if self.sqb_alias:
            ksq = 'yT0'
        for kc in range(NKC):
            if kc % 2 == 0:
                self.act(sqb[:, kc, :TT], src[:, kc, :TT], AF.Square, [src_key], [(ksq, kc) if not self.sqb_alias else ksq])
            else:
                self.tt('dve', sqb[:, kc, :TT], src[:, kc, :TT], src[:, kc, :TT], ALU.mult, [src_key], [(ksq, kc) if not self.sqb_alias else ksq])
        ps, pk = self.next_ps()
        for kc in range(NKC):
            self.mm(ps[:, :TT], self.ones_bf[:], sqb[:, kc, :TT], kc == 0, kc == NKC - 1, [(ksq, kc) if not self.sqb_alias else ksq, 'ones_bf'], [pk])
        self.act(rstd[:, :TT], ps[:, :TT], AF.Ln, [pk, 'consts'], [krs], bias=self.epsc[:, 0:1], scale=1.0 / D)
        self.act(rstd[:, :TT], rstd[:, :TT], AF.Exp, [krs], [krs], scale=-0.5)
        return rstd, krs

    def run_layer(self, li, kind, TT, w_in_cols, mixer_setup, mixer_tile, is_last):
        p = self.p
        T = self.T
        ntiles = T // TT
        with ExitStack() as les:
            self.les = les
            self.lname = f"L{li}"
            self.TT = TT
            self.W_in = self.lsb("W_in", [128, NKC, w_in_cols], BF16)
            self.W_out = self.lsb("W_out", [128, NKC, D], BF16)
            ndb = 2 if kind != 'rwkv' else 1
            self.hT = [self.lsb(f"hT{i}", [128, NKC, TT], F32) for i in range(ndb)]
            self.sqb_alias = (kind == 'rwkv')
            if not self.sqb_alias:
                self.sqb_l = [self.lsb(f"sqb{i}", [128, NKC, TT], BF16) for i in range(ndb)]
            self.rstd_l = [self.lsb(f"rstd{i}", [128, TT], F32) for i in range(ndb)]
            self.rms_i = 0
            self.hn_l = [self.lsb(f"hn{i}", [128, NKC, TT + 1], BF16) for i in range(ndb)]
            self.yTt_l = [self.lsb(f"yTt{i}", [128, NKC, TT], BF16) for i in range(ndb)]
            self.hn, self.hnk = self.hn_l[0], 'hn0'
            self.yTt, self.yTk = self.yTt_l[0], 'yT0'
            if self.sqb_alias:
                self.sqb_l = [self.yTt_l[0]]
            self.stage_i = 0
            loader = mixer_setup()
            with ExitStack() as ses:
                nst = 2 if kind == 'rwkv' else max(2, min(4, (self.nc.sbuf_bytes_remaining - 512) // 4096))
                self.stage = [ses.enter_context(self.nc.sbuf_tensor(f"stage{i}_{self.lname}", [128, 1024], F32)) for i in range(nst)]
                if loader is None:
                    self.load_weight_bf16(self.W_in, 'W_in', self.w_in_dram[kind], w_in_cols)
                    self.load_weight_bf16(self.W_out, 'W_out', self.w_out_dram[kind], D)
                else:
                    loader(ses)
                p.barrier()
            if getattr(self, 'post_setup', None) is not None:
                self.post_setup()
                self.post_setup = None
            hn0 = self.hn_l[0]
            p.op('pool', lambda e: e.memset(hn0[:, :, 0:1], 0.0), [], ['hn0'])

            def load(ti):
                buf = self.hT[ti % len(self.hT)]
                src = self.xT if self.first_layer else self.yT
                p.dma('sp', buf[:], src.rearrange("(c p) t -> p c t", p=128)[:, :, ti * TT:(ti + 1) * TT],
                      reads=[('hd', ti * TT // 128 + i) for i in range(TT // 128)], writes=[f"hT{ti % len(self.hT)}"])

            if self.use_sched:
                p.begin_record()
            load(0)
            for ti in range(ntiles):
                if len(self.hT) > 1:
                    if ti + 1 < ntiles:
                        load(ti + 1)
                elif ti > 0:
                    load(ti)
                h = self.hT[ti % len(self.hT)]
                hk = f"hT{ti % len(self.hT)}"
                rstd, rk = self.rms_rstd(h, hk, TT, 'in')
                g = self.norm_g
                prev_hn, prev_hnk = self.hn, self.hnk
                bi = ti % ndb
                self.hn, self.hnk = self.hn_l[bi], f'hn{bi}'
                self.yTt, self.yTk = self.yTt_l[bi], f'yT{bi}'
                if ti > 0:
                    self.copy('pool', self.hn[:, :, 0:1], prev_hn[:, :, TT:TT + 1], [prev_hnk], [self.hnk])
                for kc in range(NKC):
                    self.stt('dve', self.hn[:, kc, 1:TT + 1], h[:, kc, :], g[:, li, kc:kc + 1], rstd[:, :TT],
                             ALU.mult, ALU.mult, [hk, rk, 'consts'], [self.hnk])
                mixer_tile(ti)
                for j in range(NKC):
                    ps, pk = self.next_ps()
                    for kc in range(NKC):
                        self.mm(ps[:, :TT], self.W_out[:, kc, j * 128:(j + 1) * 128], self.yTt[:, kc, :TT],
                                kc == 0, kc == NKC - 1, ['W_out', self.yTk], [pk])
                    self.tt('dve', h[:, j, :], h[:, j, :], ps[:, :TT], ALU.add, [hk, pk], [hk])
                if is_last and self.do_final:
                    rstd, rk = self.rms_rstd(h, hk, TT, 'fin')
                    for kc in range(NKC):
                        self.stt('dve' if kc % 2 == 0 else 'pool', h[:, kc, :], h[:, kc, :], self.final_g[:, kc:kc + 1], rstd[:, :TT],
                                 ALU.mult, ALU.mult, [hk, rk, 'consts'], [hk])
                p.dma('sp', self.yT.rearrange("(c p) t -> p c t", p=128)[:, :, ti * TT:(ti + 1) * TT], h[:],
                      reads=[hk], writes=[('hd', (ti * TT) // 128 + i) for i in range(max(1, TT // 128))])
            if self.use_sched:
                p.flush()
            self.first_layer = False
            p.barrier()
        self.les = None

    def conv_layer(self, li, is_last):
        TT = 512

        def setup():
            self.yext = self.lsb("yext", [128, NKC, TT + 2], F32)
            self.zs = [self.lsb(f"zs{i}", [128, TT], F32) for i in range(2)]
            self.acc = [self.lsb(f"acc{i}", [128, TT], F32) for i in range(2)]
            self.sg = [self.lsb(f"sg{i}", [128, TT], F32) for i in range(2)]
            self.p.op('pool', lambda e: e.memset(self.yext[:, :, 0:2], 0.0), [], [('yext', j) for j in range(NKC)])

        def tile(ti):
            W = self.W_in
            cw = self.conv_w
            for j in range(NKC):
                zs, acc, sg = self.zs[j % 2], self.acc[j % 2], self.sg[j % 2]
                zk, ak, gk = f"zs{j % 2}", f"acc{j % 2}", f"sg{j % 2}"
                pss = []
                for blk in range(4):
                    ps, pk = self.next_ps()
                    col0 = blk * D + j * 128
                    for kc in range(NKC):
                        self.mm(ps[:, :TT], W[:, kc, col0:col0 + 128], self.hn[:, kc, 1:TT + 1], kc == 0, kc == NKC - 1,
                                ['W_in', self.hnk], [pk])
                    pss.append((ps, pk))
                (pb, pbk), (pc, pck), (pz, pzk), (pg, pgk) = pss
                yk = ('yext', j)
                self.copy('act', zs[:], pz[:, :TT], [pzk], [zk])
                if ti > 0:
                    self.copy('pool', self.yext[:, j, 0:2], self.yext[:, j, TT:TT + 2], [yk], [yk])
                self.tt('dve', self.yext[:, j, 2:TT + 2], pc[:, :TT], zs[:], ALU.mult, [pck, zk], [yk])
                self.act(acc[:], self.yext[:, j, 2:TT + 2], AF.Copy, [yk, 'consts'], [ak], scale=cw[:, j, 2:3])
                self.stt('pool', acc[:], self.yext[:, j, 1:TT + 1], cw[:, j, 1:2], acc[:], ALU.mult, ALU.add, [yk, ak, 'consts'], [ak])
                self.stt('pool', acc[:], self.yext[:, j, 0:TT], cw[:, j, 0:1], acc[:], ALU.mult, ALU.add, [yk, ak, 'consts'], [ak])
                self.act(sg[:], pg[:, :TT], AF.Silu, [pgk], [gk])
                self.tt('dve', acc[:], pb[:, :TT], acc[:], ALU.mult, [pbk, ak], [ak])
                self.tt('pool', self.yTt[:, j, :], acc[:], sg[:], ALU.mult, [ak, gk], [self.yTk])

        self.run_layer(li, 'conv', TT, 4 * D, setup, tile, is_last)

    def gmlp_layer(self, li, is_last):
        TT = 512

        def setup():
            p = self.p
            self.wsT = self.lsb("wsT", [128, 8, 128], F32)
            self.bs_bc = self.lsb("bs_bc", [128, 8, TT], F32)
            self.vg_bc = self.lsb("vg_bc", [128, D], F32)
            self.vn = [self.lsb(f"vn{i}", [128, D], F32) for i in range(TT // 128)]
            self.vss = self.lsb("vss", [128, 4], F32)
            self.junk = self.lsb("junk", [128, 512], F32)
            self.s_sb = [self.lsb(f"s_sb{i}", [128, TT], F32) for i in range(2)]
            self.sg = [self.lsb(f"sg{i}", [128, TT], F32) for i in range(2)]
            p.dma('sp', self.wsT[:], self.inputs['gmlp_wsT'], [], ['wsT'])
            for g in range(8):
                p.op('pool', lambda e: e.affine_select(out=self.wsT[:, g, :], in_=self.wsT[:, g, :], pattern=[[1, 128]],
                                                       compare_op=ALU.is_ge, fill=0.0, base=0, channel_multiplier=-1),
                     ['wsT'], ['wsT'])
            for r in range(TT // 128):
                p.dma('sp', self.bs_bc[:, :, r * 128:(r + 1) * 128],
                      self.inputs['gmlp_bs'].partition_broadcast(128), [], ['bs_bc'])
            p.dma('sp', self.vg_bc[:], self.inputs['gmlp_vg'].partition_broadcast(128), [], ['vg_bc'])

        def tile(ti):
            W = self.W_in
            nblk = TT // 128
            for blk in range(nblk):
                vn = self.vn[blk]
                vk = f"vn{blk}"
                halves = []
                for hf in range(2):
                    ps, pk = self.next_ps()
                    for kc in range(NKC):
                        self.mm(ps[:, :512], self.hn[:, kc, 1 + blk * 128:1 + (blk + 1) * 128],
                                W[:, kc, D + hf * 512:D + (hf + 1) * 512], kc == 0, kc == NKC - 1, ['W_in', self.hnk], [pk])
                    halves.append((ps, pk))
                for hf, (ps, pk) in enumerate(halves):
                    self.act(self.junk[:], ps[:, :512], AF.Square, [pk], ['junk', 'vss'], accum_out=self.vss[:, hf:hf + 1])
                self.tt('dve', self.vss[:, 2:3], self.vss[:, 0:1], self.vss[:, 1:2], ALU.add, ['vss'], ['vss'])
                self.act(self.vss[:, 3:4], self.vss[:, 2:3], AF.Ln, ['vss', 'consts'], ['vss'], bias=self.epsc[:, 0:1], scale=1.0 / D)
                self.act(self.vss[:, 3:4], self.vss[:, 3:4], AF.Exp, ['vss'], ['vss'], scale=-0.5)
                for hf, (ps, pk) in enumerate(halves):
                    self.stt('dve', vn[:, hf * 512:(hf + 1) * 512], ps[:, :512], self.vss[:, 3:4],
                             self.vg_bc[:, hf * 512:(hf + 1) * 512], ALU.mult, ALU.mult, [pk, 'vss', 'vg_bc'], [vk])
            for j in range(NKC):
                s_sb, sg = self.s_sb[j % 2], self.sg[j % 2]
                sk, gk = f"s_sb{j % 2}", f"sg{j % 2}"
                ps, pk = self.next_ps()
                for blk in range(nblk):
                    self.mm(ps[:, blk * 128:(blk + 1) * 128], self.vn[blk][:, j * 128:(j + 1) * 128], self.wsT[:, j, :], True, True,
                            [f"vn{blk}", 'wsT'], [pk])
                self.tt('dve', s_sb[:], ps[:, :TT], self.bs_bc[:, j, :], ALU.add, [pk, 'bs_bc'], [sk])
                pu, puk = self.next_ps()
                for kc in range(NKC):
                    self.mm(pu[:, :TT], W[:, kc, j * 128:(j + 1) * 128], self.hn[:, kc, 1:TT + 1], kc == 0, kc == NKC - 1,
                            ['W_in', self.hnk], [puk])
                pg, pgk = self.next_ps()
                for kc in range(NKC):
                    self.mm(pg[:, :TT], W[:, kc, 2 * D + j * 128:2 * D + (j + 1) * 128], self.hn[:, kc, 1:TT + 1], kc == 0,
                            kc == NKC - 1, ['W_in', self.hnk], [pgk])
                self.act(sg[:], pg[:, :TT], AF.Silu, [pgk], [gk])
                self.tt('dve', s_sb[:], pu[:, :TT], s_sb[:], ALU.mult, [puk, sk], [sk])
                self.tt('pool', self.yTt[:, j, :], s_sb[:], sg[:], ALU.mult, [sk, gk], [self.yTk])

        self.run_layer(li, 'gmlp', TT, 3 * D, setup, tile, is_last)


    def make_ident(self, ident, key):
        p = self.p
        p.op('pool', lambda e: e.memset(ident[:], 1.0), [], [key])
        p.op('pool', lambda e: e.affine_select(out=ident[:], in_=ident[:], pattern=[[-1, 128]], compare_op=ALU.is_equal,
                                               fill=0.0, base=0, channel_multiplier=1), [key], [key])

    def make_block_masks(self, C, maskT, colmask, rowmask, strict=False):
        p = self.p
        nch = 128 // C
        if maskT is not None:
            p.op('pool', lambda e: e.memset(maskT[:], 1.0), [], ['masks'])
            p.op('pool', lambda e: e.affine_select(out=maskT[:], in_=maskT[:], pattern=[[1, 128]], compare_op=ALU.is_ge if not strict else ALU.is_gt,
                                                   fill=0.0, base=0, channel_multiplier=-1), ['masks'], ['masks'])
            for c in range(1, nch):
                p.op('pool', lambda e, c=c: e.affine_select(out=maskT[:, c * C:(c + 1) * C], in_=maskT[:, c * C:(c + 1) * C], pattern=[[0, C]],
                                                            compare_op=ALU.is_ge, fill=0.0, base=-c * C, channel_multiplier=1), ['masks'], ['masks'])
        if colmask is not None:
            p.op('pool', lambda e: e.memset(colmask[:], 0.0), [], ['masks'])
            for c in range(nch):
                p.op('pool', lambda e, c=c: e.memset(colmask[:, c, c * C:(c + 1) * C], 1.0), ['masks'], ['masks'])
        if rowmask is not None:
            p.op('pool', lambda e: e.memset(rowmask[:], 1.0), [], ['masks'])
            for c in range(nch):
                p.op('pool', lambda e, c=c: e.affine_select(out=rowmask[:, c:c + 1], in_=rowmask[:, c:c + 1], pattern=[[0, 1]],
                                                            compare_op=ALU.is_ge, fill=0.0, base=-c * C, channel_multiplier=1), ['masks'], ['masks'])
                p.op('pool', lambda e, c=c: e.affine_select(out=rowmask[:, c:c + 1], in_=rowmask[:, c:c + 1], pattern=[[0, 1]],
                                                            compare_op=ALU.is_ge, fill=0.0, base=c * C + C - 1, channel_multiplier=-1), ['masks'], ['masks'])

    def hgrn_layer(self, li, is_last):
        TT = 256
        C = 32
        NB = TT // 128
        NCH = TT // C

        def setup():
            p = self.p
            L = self.lsb
            self.ident = L("ident", [128, 128], F32)
            self.make_ident(self.ident, 'ident')
            self.maskT = L("maskT", [128, 128], F32)
            self.colmask = L("colmask", [128, 4, 128], F32)
            self.rowmask = L("rowmask", [128, 4], F32)
            self.make_block_masks(C, self.maskT, self.colmask, self.rowmask)
            self.resetm = L("resetm", [128, TT], F32)
            self.ones_t = L("ones_t", [128, TT], F32)
            p.op('pool', lambda e: e.memset(self.ones_t[:], 1.0), [], ['masks'])
            p.op('pool', lambda e: e.memset(self.resetm[:], 1.0), [], ['masks'])
            p.op('pool', lambda e: e.memset(self.resetm[:].rearrange("p (n c) -> p n c", c=C)[:, :, 0:1], 0.0), ['masks'], ['masks'])
            self.gn_bc = L("gn_bc", [128, D], F32)
            p.dma('sp', self.gn_bc[:], self.inputs['hgrn_gn_g'].partition_broadcast(128), [], ['gn_bc'])
            self.lbl = L("lbl", [128, 4, NKC], F32)
            self.lbt = L("lbt", [128, 4, NKC], F32)
            p.dma('sp', self.lbl[:], self.inputs['hgrn_lbl'], [], ['lbl'])
            self.act(self.lbl[:], self.lbl[:], AF.Exp, ['lbl'], ['lbl'])
            self.tt('dve', self.lbt[:, 0, :], self.lbl[:, 0, :], self.lbl[:, 1, :], ALU.add, ['lbl'], ['lbt'])
            self.tt('dve', self.lbt[:, 0, :], self.lbt[:, 0, :], self.lbl[:, 2, :], ALU.add, ['lbl', 'lbt'], ['lbt'])
            self.tt('dve', self.lbt[:, 0, :], self.lbt[:, 0, :], self.lbl[:, 3, :], ALU.add, ['lbl', 'lbt'], ['lbt'])
            p.op('dve', lambda e: e.reciprocal(out=self.lbt[:, 3, :], in_=self.lbt[:, 0, :]), ['lbt'], ['lbt'])
            p.op('dve', lambda e: e.memset(self.lbt[:, 1, :], 0.0), ['lbt'], ['lbt'])
            for i in range(1, li + 1):
                self.tt('dve', self.lbt[:, 1, :], self.lbt[:, 1, :], self.lbl[:, i, :], ALU.add, ['lbl', 'lbt'], ['lbt'])
            self.tt('dve', self.lbt[:, 1, :], self.lbt[:, 1, :], self.lbt[:, 3, :], ALU.mult, ['lbt'], ['lbt'])
            self.ts('dve', self.lbt[:, 2, :], self.lbt[:, 1, :], -1.0, 1.0, ALU.mult, ALU.add, ['lbt'], ['lbt'])
            self.S = L("S_hgrn", [128, NKC, 128], F32)
            p.op('pool', lambda e: e.memset(self.S[:], 0.0), [], [('S', j) for j in range(NKC)])
            names = ['f', 'kk', 'bb', 'qe', 'dd', 'sg']
            self.tmps = []
            for q in range(2):
                tm = {n: L(f"h_{n}{q}", [128, TT], F32) for n in names}
                tm['e1'] = tm['f']
                tm['ko'] = tm['dd']
                tm['ke_bf'] = L(f"h_ke_bf{q}", [128, TT], BF16)
                tm['qe_bf'] = L(f"h_qe_bf{q}", [128, TT], BF16)
                tm['kom'] = L(f"kom{q}", [128, 4, NB, 128], BF16)
                self.tmps.append(tm)
            self.sgate = [L(f"sgate{q}", [128, TT], BF16) for q in range(3)]
            self.qem = [L(f"qem{q}", [128, 4, TT], BF16) for q in range(3)]
            self.v_bf = [L(f"v_bf{q}", [128, NB, 128], BF16) for q in range(3)]
            self.attm = [L(f"attm{q}", [128, NB, 128], BF16) for q in range(3)]
            self.u_sb = [L(f"u_sb{q}", [128, NCH, 128], F32) for q in range(3)]
            self.dec = [L(f"dec{q}", [128, NCH], F32) for q in range(3)]
            self.S_all2 = [L(f"S_all{q}", [128, 5, 128], F32) for q in range(2)]
            self.S_bf2 = [L(f"S_bf{q}", [128, NCH, 128], BF16) for q in range(2)]
            self.on2 = [L(f"on{q}", [128, NB, 128], F32) for q in range(2)]
            self.oss2 = [L(f"oss{q}", [128, 2 * NB], F32) for q in range(2)]
            self.junk2 = [L(f"junk{q}", [128, 128], F32) for q in range(2)]
            self.ps_pool = [0, 1, 2, 3]
            self.nm_rr = 0

        def nm_ps():
            i = [4, 5][self.nm_rr % 2]
            self.nm_rr += 1
            return self.psums[i], f"ps{i}"

        def proj(col0):
            ps, pk = self.next_ps()
            for kc in range(NKC):
                self.mm(ps[:, :TT], self.W_in[:, kc, col0:col0 + 128], self.hn[:, kc, 1:TT + 1], kc == 0, kc == NKC - 1, ['W_in', self.hnk], [pk])
            return ps, pk

        def A_gen(j):
            q2 = j % 2
            t = self.tmps[q2]
            kom = t['kom']
            W = self.W_in
            q = j % 3
            lb, oml = self.lbt[:, 1, :], self.lbt[:, 2, :]
            sgate, qem, v_bf, attm, u_sb, dec = self.sgate[q], self.qem[q], self.v_bf[q], self.attm[q], self.u_sb[q], self.dec[q]
            ksg, kqem, kv, katt, ku, kdec = f'sgate{q}', f'qem{q}', f'v_bf{q}', f'attm{q}', f'u_sb{q}', f'dec{q}'
            pf, pfk = proj(D + j * 128)
            self.act(t['f'][:], pf[:, :TT], AF.Exp, [pfk], [f't_f{q2}'], scale=-1.0)
            self.tt('pool', t['f'][:], t['f'][:], self.ones_t[:], ALU.add, [f't_f{q2}', 'masks'], [f't_f{q2}'])
            self.p.op('dve', lambda e: e.reciprocal(out=t['f'][:], in_=t['f'][:]), [f't_f{q2}'], [f't_f{q2}'], cost=0.45)
            self.ts('dve', t['f'][:], t['f'][:], oml[:, j:j + 1], lb[:, j:j + 1], ALU.mult, ALU.add, [f't_f{q2}', 'lbt'], [f't_f{q2}'])
            self.act(t['kk'][:], t['f'][:], AF.Identity, [f't_f{q2}'], [f't_kk{q2}'], scale=-1.0, bias=self.epsc[:, 2:3])
            self.act(t['dd'][:], t['f'][:], AF.Ln, [f't_f{q2}'], [f't_dd{q2}'])
            self.p.op('dve', lambda e: e.tensor_tensor_scan(out=t['bb'][:], data0=self.resetm[:], data1=t['dd'][:], initial=0.0,
                                                            op0=ALU.mult, op1=ALU.add), [f't_dd{q2}', 'masks'], [f't_bb{q2}'])
            yield
            pq, pqk = proj(j * 128)
            self.act(t['e1'][:], t['bb'][:], AF.Exp, [f't_bb{q2}'], [f't_f{q2}'])
            self.tt('dve', t['qe'][:], pq[:, :TT], t['e1'][:], ALU.mult, [pqk, f't_f{q2}'], [f't_qe{q2}'])
            self.copy('act', t['qe_bf'][:], t['qe'][:], [f't_qe{q2}'], [f't_qe_bf{q2}'])
            qe4 = t['qe'][:].rearrange("p (b t) -> p b t", t=128)
            for c in range(4):
                self.tt('pool', qem[:, c, :].rearrange("p (b t) -> p b t", t=128), qe4,
                        self.colmask[:, c:c + 1, :].to_broadcast([128, NB, 128]), ALU.mult, [f't_qe{q2}', 'masks'], [kqem])
            yield
            self.act(t['e1'][:], t['bb'][:], AF.Exp, [f't_bb{q2}'], [f't_f{q2}'], scale=-1.0)
            self.tt('pool', t['ke_bf'][:], t['kk'][:], t['e1'][:], ALU.mult, [f't_kk{q2}', f't_f{q2}'], [f't_ke_bf{q2}'])
            b3 = t['bb'][:].rearrange("p (n c) -> p n c", c=C)
            self.act(dec[:], b3[:, :, C - 1], AF.Exp, [f't_bb{q2}'], [kdec])
            self.tt('pool', t['dd'][:].rearrange("p (n c) -> p n c", c=C), b3[:, :, C - 1:C].to_broadcast([128, NCH, C]), b3, ALU.subtract,
                    [f't_bb{q2}'], [f't_dd{q2}'])
            self.act(t['dd'][:], t['dd'][:], AF.Exp, [f't_dd{q2}'], [f't_dd{q2}'])
            self.tt('pool', t['ko'][:], t['kk'][:], t['dd'][:], ALU.mult, [f't_kk{q2}', f't_dd{q2}'], [f't_dd{q2}'])
            pg, pgk = proj(3 * D + j * 128)
            self.act(t['sg'][:], pg[:, :TT], AF.Exp, [pgk], [f't_sg{q2}'], scale=-1.0)
            self.tt('pool', t['sg'][:], t['sg'][:], self.ones_t[:], ALU.add, [f't_sg{q2}', 'masks'], [f't_sg{q2}'])
            self.p.op('dve', lambda e: e.reciprocal(out=t['sg'][:], in_=t['sg'][:]), [f't_sg{q2}'], [f't_sg{q2}'], cost=0.45)
            self.tt('dve', sgate[:], pg[:, :TT], t['sg'][:], ALU.mult, [pgk, f't_sg{q2}'], [ksg])
            yield
            pv, pvk = self.next_ps()
            for blk in range(NB):
                for kc in range(NKC):
                    self.mm(pv[:, blk * 128:(blk + 1) * 128], self.hn[:, kc, 1 + blk * 128:1 + (blk + 1) * 128],
                            W[:, kc, 2 * D + j * 128:2 * D + (j + 1) * 128], kc == 0, kc == NKC - 1, ['W_in', self.hnk], [pvk])
            self.copy('act', v_bf[:].rearrange("p b v -> p (b v)"), pv[:, :TT], [pvk], [kv])
            yield
            ps, pk = nm_ps()
            for blk in range(NB):
                cs = slice(blk * 128, (blk + 1) * 128)
                self.mm(ps[:, cs], t['ke_bf'][:, cs], t['qe_bf'][:, cs], True, True, [f't_ke_bf{q2}', f't_qe_bf{q2}'], [pk])
            self.tt('dve', attm[:], ps[:, :TT].rearrange("p (b t) -> p b t", t=128), self.maskT[:, None, :].to_broadcast([128, NB, 128]),
                    ALU.mult, [pk, 'masks'], [katt])
            ps, pk = nm_ps()
            for blk in range(NB):
                cs = slice(blk * 128, (blk + 1) * 128)
                self.p.op('pe', lambda e, ps=ps, cs=cs: e.transpose(ps[:, cs], t['ko'][:, cs], self.ident[:]), [f't_dd{q2}', 'ident'], [pk])
            for c in range(4):
                self.act(kom[:, c, :, :].rearrange("p b k -> p (b k)"), ps[:, :TT], AF.Copy, [pk, 'masks'], [f'kom{q2}'],
                         scale=self.rowmask[:, c:c + 1])
            yield
            for blk in range(NB):
                ps, pk = nm_ps()
                for c in range(4):
                    self.mm(ps[:, c * 128:(c + 1) * 128], kom[:, c, blk, :], v_bf[:, blk, :], True, True, [f'kom{q2}', kv], [pk])
                self.copy('act' if blk % 2 else 'dve', u_sb[:, blk * 4:(blk + 1) * 4, :].rearrange("p c v -> p (c v)"), ps[:, 0:512], [pk], [ku])
                if blk % 2:
                    yield

        def B_gen(j):
            P = self.psums
            q = j % 3
            sgate, qem, v_bf, attm, u_sb, dec = self.sgate[q], self.qem[q], self.v_bf[q], self.attm[q], self.u_sb[q], self.dec[q]
            ksg, kqem, kv, katt, ku, kdec = f'sgate{q}', f'qem{q}', f'v_bf{q}', f'attm{q}', f'u_sb{q}', f'dec{q}'
            kS = ('S', j)
            q2 = j % 2
            SA = self.S_all2[q2]
            S_bf, on, oss, junk = self.S_bf2[q2], self.on2[q2], self.oss2[q2], self.junk2[q2]
            kSA, kSbf, kon, koss, kjunk = f'S_all{q2}', f'S_bf{q2}', f'on{q2}', f'oss{q2}', f'junk{q2}'
            self.copy('pool', SA[:, 0, :], self.S[:, j, :], [kS], [kSA])
            for blk in range(NB):
                for c in range(4):
                    n = blk * 4 + c
                    self.stt('dve', SA[:, c + 1, :], SA[:, c, :], dec[:, n:n + 1], u_sb[:, n, :], ALU.mult, ALU.add, [kSA, kdec, ku], [kSA])
                self.copy('act', S_bf[:, blk * 4:(blk + 1) * 4, :].rearrange("p c v -> p (c v)"),
                          SA[:, 0:4, :].rearrange("p c v -> p (c v)"), [kSA], [kSbf])
                if blk < NB - 1:
                    self.copy('dve', SA[:, 0, :], SA[:, 4, :], [kSA], [kSA])
                yield
            self.copy('pool', self.S[:, j, :], SA[:, 4, :], [kSA], [kS])
            po, pok = P[6], 'ps6'
            for blk in range(NB):
                cs = slice(blk * 128, (blk + 1) * 128)
                self.mm(po[:, cs], attm[:, blk, :], v_bf[:, blk, :], True, False, [katt, kv], [pok])
                for c in range(4):
                    self.mm(po[:, cs], qem[:, c, cs], S_bf[:, blk * 4 + c, :], False, c == 3, [kqem, kSbf], [pok])
                if blk % 2:
                    yield
            for blk in range(NB):
                cs = slice(blk * 128, (blk + 1) * 128)
                self.act(junk[:], po[:, cs], AF.Square, [pok], [kjunk, koss], accum_out=oss[:, blk:blk + 1])
            self.act(oss[:, NB:2 * NB], oss[:, 0:NB], AF.Ln, [koss, 'consts'], [koss], bias=self.epsc[:, 0:1], scale=1.0 / 128)
            self.act(oss[:, NB:2 * NB], oss[:, NB:2 * NB], AF.Exp, [koss], [koss], scale=-0.5)
            self.tt('dve', on[:], po[:, :TT].rearrange("p (b v) -> p b v", v=128),
                    oss[:, NB:2 * NB, None].to_broadcast([128, NB, 128]), ALU.mult, [pok, koss], [kon])
            self.tt('pool', on[:], on[:], self.gn_bc[:, None, j * 128:(j + 1) * 128].to_broadcast([128, NB, 128]), ALU.mult,
                    [kon, 'gn_bc'], [kon])
            yield
            py, pyk = P[7], 'ps7'
            for blk in range(NB):
                cs = slice(blk * 128, (blk + 1) * 128)
                self.p.op('pe', lambda e, cs=cs, blk=blk: e.transpose(py[:, cs], on[:, blk, :], self.ident[:]), [kon, 'ident'], [pyk])
            self.tt('dve', self.yTt[:, j, :], py[:, :TT], sgate[:], ALU.mult, [pyk, ksg], [self.yTk])
            yield

        def drive(gens):
            gens = [g for g in gens if g is not None]
            while gens:
                for g in list(gens):
                    try:
                        next(g)
                    except StopIteration:
                        gens.remove(g)

        def step(g):
            try:
                next(g)
                return True
            except StopIteration:
                return False

        def tile(ti):
            A = {0: A_gen(0), 1: A_gen(1)}
            while step(A[0]):
                step(A[1])
            for sl in range(NKC):
                must = [B_gen(sl)]
                if sl + 1 < NKC:
                    must.append(A[sl + 1])
                opt = None
                if sl + 2 < NKC:
                    A[sl + 2] = A_gen(sl + 2)
                    opt = A[sl + 2]
                while must:
                    for g in list(must):
                        if not step(g):
                            must.remove(g)
                    if opt is not None and not step(opt):
                        opt = None

        self.run_layer(li, 'hgrn', TT, 4 * D, setup, tile, is_last)
        self.ps_pool = list(range(8))

    def rwkv_layer(self, li, is_last):
        TT = 256
        NB = TT // 128
        WC = 3200
        NDT = self.neu_dt
        LC = -0.6065306597126334

        def setup():
            p = self.p
            L = self.lsb
            self.ident = L("ident", [128, 128], F32)
            self.make_ident(self.ident, 'ident')
            self.ident_n = L("ident_n", [128, 128], NDT)
            self.copy('dve', self.ident_n[:], self.ident[:], ['ident'], ['ident'])
            self.maskS = L("maskS", [128, 128], F32)
            self.maskI = L("maskI", [128, 128], F32)
            self.maskSL = L("maskSL", [128, 128], F32)
            for (m, pat, cm, cmp_) in ((self.maskS, 1, -1, ALU.is_gt), (self.maskI, 1, -1, ALU.is_ge), (self.maskSL, -1, 1, ALU.is_gt)):
                p.op('pool', lambda e, m=m: e.memset(m[:], 1.0), [], ['masks'])
                p.op('pool', lambda e, m=m, pat=pat, cm=cm, cmp_=cmp_: e.affine_select(
                    out=m[:], in_=m[:], pattern=[[pat, 128]], compare_op=cmp_, fill=0.0, base=0, channel_multiplier=cm), ['masks'], ['masks'])
            self.blockones = L("blockones", [128, 128], F32)
            p.op('pool', lambda e: e.memset(self.blockones[:], 1.0), [], ['masks'])
            p.op('pool', lambda e: e.memset(self.blockones[0:64, 64:128], 0.0), ['masks'], ['masks'])
            p.op('pool', lambda e: e.memset(self.blockones[64:128, 0:64], 0.0), ['masks'], ['masks'])
            self.resetm = L("resetm", [128, TT], F32)
            p.op('pool', lambda e: e.memset(self.resetm[:], 1.0), [], ['masks'])
            p.op('pool', lambda e: e.memset(self.resetm[:].rearrange("p (n c) -> p n c", c=128)[:, :, 0:1], 0.0), ['masks'], ['masks'])
            p.op('pool', lambda e: e.memset(self.epsc[:, 1:2], GN_EPS), [], ['consts'])
            self.mu_fm = L("mu_fm", [128, 33], F32)
            self.omu_fm = L("omu_fm", [128, 33], F32)
            p.dma('sp', self.mu_fm[:], self.inputs['rwkv_mu_fm'], [], ['rw_vecs'])
            self.ts('dve', self.omu_fm[:], self.mu_fm[:], -1.0, 1.0, ALU.mult, ALU.add, ['rw_vecs'], ['rw_vecs'])
            self.vecs = L("rw_vecs", [128, 5, NKC], F32)
            p.dma('sp', self.vecs[:], self.inputs['rwkv_vecs'], [], ['rw_vecs'])
            self.lw2 = L("lw2", [128, D], BF16)
            self.lo_bf = L("lo_bf", [128, TT], BF16)
            self.gng_bc = L("gng_bc", [128, D], BF16)
            self.gnb_bc = L("gnb_bc", [128, D], BF16)
            self.Wva = L("Wva", [128, NKC, D], BF16)
            self.Wvb = L("Wvb", [128, NKC, D], BF16)
            src = self.w_in_dram['rwkv']

            def loader(tes):
                muv = tes.enter_context(self.nc.sbuf_tensor("muv_bc", [128, D], F32))
                p.dma('sp', muv[:], self.inputs['rwkv_lw2'], [], ['muv'])
                self.copy('dve', self.lw2[:], muv[:], ['muv'], ['lw2'])
                for (dst, nm) in ((self.gng_bc, 'rwkv_gn_g'), (self.gnb_bc, 'rwkv_gn_b')):
                    p.dma('sp', muv[:], self.inputs[nm].partition_broadcast(128), ['muv'], ['muv'])
                    self.copy('dve', dst[:], muv[:], ['muv'], ['bc_tiles'])
                p.dma('sp', muv[:], self.inputs['rwkv_mu'][2 * D:3 * D].partition_broadcast(128), ['muv'], ['bc_tiles', 'muv'])
                self.load_weight_bf16(self.Wvb, 'Wv', src, D, src_c0=2 * D, scale_bc=muv)
                self.ts('dve', muv[:], muv[:], -1.0, 1.0, ALU.mult, ALU.add, ['bc_tiles'], ['bc_tiles'])
                self.load_weight_bf16(self.Wva, 'Wv', src, D, src_c0=2 * D, scale_bc=muv)
                self.load_weight_bf16(self.W_in, 'W_in', src, 2 * D, src_c0=0, dst_c0=0)
                self.load_weight_bf16(self.W_in, 'W_in', src, 128, src_c0=3 * D, dst_c0=2 * D)
                self.load_weight_bf16(self.W_in, 'W_in', src, D, src_c0=3 * D + 128, dst_c0=2 * D + 128)
                self.load_weight_bf16(self.W_out, 'W_out', self.w_out_dram['rwkv'], D)
            self.S = L("S_rwkv", [128, NKC, 64], F32)
            p.op('pool', lambda e: e.memset(self.S[:], 0.0), [], [('S', j) for j in range(NKC)])
            self.pcar = L("pcar", [128, 25], F32)
            p.op('pool', lambda e: e.memset(self.pcar[:], 0.0), [], [('pcar', i) for i in range(25)])
            self.pm_ext = [L(f"pm_ext{i}", [128, TT + 1], F32) for i in range(2)]
            self.pm_i = 0
            names = ['r', 'k', 'tmp', 'sigw', 'a', 'kk', 'rn', 'kmod', 'bbv', 'c']
            self.tmp = {'lo': L("w_lo", [128, TT], F32)}
            self.tmpP = [{n: L(f"w_{n}0", [128, TT], F32) for n in names}, None]
            self.tmp2 = [dict(), dict()]
            for n in ['khat', 'bhat']:
                self.tmp2[0][n] = L(f"w_{n}0", [128, TT], F32)
            for n in ['rt_bf', 'bt_bf', 'at_bf', 'kt_h0', 'kt_h1', 'bt_h0', 'bt_h1', 'at_h0', 'at_h1']:
                self.tmp2[0][n] = L(f"w_{n}0", [128, TT], BF16)
            self.hm = L("hm", [128, 2], F32)
            p.op('pool', lambda e: e.memset(self.hm[:], 0.0), [], ['masks'])
            p.op('pool', lambda e: e.memset(self.hm[0:64, 0:1], 1.0), ['masks'], ['masks'])
            p.op('pool', lambda e: e.memset(self.hm[64:128, 1:2], 1.0), ['masks'], ['masks'])
            self.pt = {n: [L(f"wp_{n}{q}", [128, TT], F32 if n in ('at', 'rt') else BF16) for q in range(2)] for n in ['at', 'rt', 'rkr', 'sgate']}
            self.blockones_bf = L("blockones_bf", [128, 128], BF16)
            self.copy('dve', self.blockones_bf[:], self.blockones[:], ['masks'], ['masks'])
            self.v_bf = [L(f"v_bf{q}", [128, NB, 128], BF16) for q in range(2)]
            self.dec = [L(f"dec{q}", [128, NB], F32) for q in range(3)]

            def post_setup():
                self.tmpP[1] = {n: L(f"w_{n}1", [128, TT], F32) for n in names}
                for qq in range(2, self.NDEEP):
                    self.NT.append(L(f"NT{qq}", [128, NCHN, 128], NDT))
                    self.Aak.append(L(f"Aak{qq}", [128, NCHN, 128], BF16))
                    self.Ark.append(L(f"Ark{qq}", [128, NCHN, 128], BF16))
                    self.Arb.append(L(f"Arb{qq}", [128, NCHN, 128], BF16))
                    self.khm.append([L(f"khm{qq}{b}", [128, 128], BF16) for b in range(NB)])
                    self.bhm.append([L(f"bhm{qq}{b}", [128, 128], BF16) for b in range(NB)])
                self.PbP[1] = [L(f"Pb1{i}", [128, NCHN, 128], NDT) for i in range(2)]
                self.QbP[1] = [L(f"Qb1{i}", [128, NCHN, 128], NDT) for i in range(2)]
                for n in ['khat', 'bhat']:
                    self.tmp2[1][n] = L(f"w_{n}1", [128, TT], F32)
                for n in ['rt_bf', 'bt_bf', 'at_bf', 'kt_h0', 'kt_h1', 'bt_h0', 'bt_h1', 'at_h0', 'at_h1']:
                    self.tmp2[1][n] = L(f"w_{n}1", [128, TT], BF16)
                for n in ['rkr', 'sgate']:
                    self.pt[n].append(L(f"wp_{n}2", [128, TT], BF16))
                for n in ['at', 'rt']:
                    self.pt[n].append(self.pt[n][0])
                self.ysb = [L(f"ysb{i}", [128, 256], F32) for i in range(2)]
                self.yn2 = [self.yn, L("yn1", [128, 128], F32)]
                self.bon2 = [self.bon, L("bon1", [128, 128], F32)]
                self.gst2 = [self.gst, L("gst1", [128, 12], F32)]
                self.junk2 = [self.junk, L("junk1", [128, 64], F32)]
                self.bcount = 0
                self.v_bf.append(L("v_bf2", [128, NB, 128], BF16))
            self.post_setup = post_setup
            NCHN = NB * 2
            self.PbP = [[L(f"Pb0{i}", [128, NCHN, 128], NDT) for i in range(2)], None]
            self.QbP = [[L(f"Qb0{i}", [128, NCHN, 128], NDT) for i in range(2)], None]
            self.NT = [L(f"NT{q}", [128, NCHN, 128], NDT) for q in range(2)]
            self.Aak = [L(f"Aak{q}", [128, NCHN, 128], BF16) for q in range(2)]
            self.Ark = [L(f"Ark{q}", [128, NCHN, 128], BF16) for q in range(2)]
            self.Arb = [L(f"Arb{q}", [128, NCHN, 128], BF16) for q in range(2)]
            self.NDEEP = 4
            self.ident4 = L("ident4", [128, NCHN, 128], NDT)
            for c in range(NCHN):
                self.copy('dve', self.ident4[:, c, :], self.ident[:], ['ident'], ['ident'])
            self.khm = [[L(f"khm{q}{b}", [128, 128], BF16) for b in range(NB)] for q in range(2)]
            self.bhm = [[L(f"bhm{q}{b}", [128, 128], BF16) for b in range(NB)] for q in range(2)]
            self.Z_sb = L("Z_sb", [128, 128], NDT)
            self.U_bf = L("U_bf", [128, 128], BF16)
            self.yn = L("yn", [128, 128], F32)
            self.bon = L("bon", [128, 128], F32)
            self.gst = L("gst", [128, 12], F32)
            self.junk = L("junk", [128, 64], F32)
            self.ps_pool = [0, 1]
            return loader

        NMB = [[2, 3], [4, 7]]
        self.nm_rrs = [0, 0]

        def nm_ps(q=0):
            i = NMB[q][self.nm_rrs[q] % 2]
            self.nm_rrs[q] += 1
            return self.psums[i], f"ps{i}"

        def shift(ps, pk, dst, dk, idx, mt):
            pm = self.pm_ext[self.pm_i % 2]
            pmk = f"pm_ext{self.pm_i % 2}"
            self.pm_i += 1
            ck = ('pcar', idx)
            self.copy('pool', pm[:, 0:1], self.pcar[:, idx:idx + 1], [ck], [pmk])
            self.act(pm[:, 1:TT + 1], ps[:, :TT], AF.Copy, [pk, 'rw_vecs'], [pmk], scale=self.mu_fm[:, mt:mt + 1])
            self.copy('pool', self.pcar[:, idx:idx + 1], pm[:, TT:TT + 1], [pmk], [ck])
            self.act(dst, ps[:, :TT], AF.Copy, [pk, 'rw_vecs'], [dk], scale=self.omu_fm[:, mt:mt + 1])
            self.tt('dve', dst, dst, pm[:, 0:TT], ALU.add, [dk, pmk], [dk])

        def proj(col0):
            ps, pk = self.next_ps()
            for kc in range(NKC):
                self.mm(ps[:, :TT], self.W_in[:, kc, col0:col0 + 128], self.hn[:, kc, 1:TT + 1], kc == 0, kc == NKC - 1, ['W_in', self.hnk], [pk])
            return ps, pk

        hsl = [slice(0, 64), slice(64, 128)]

        def A_gen(j):
            V = self.vecs
            q = j % 2
            q3 = j % 3
            q4 = j % self.NDEEP
            t = dict(self.tmp)
            t.update(self.tmpP[q])
            t['e1'] = t['rn']
            t['e2'] = t['tmp']
            t.update(self.tmp2[q])
            self.Pb, self.Qb = self.PbP[q], self.QbP[q]
            PAR = set(self.tmp2[0].keys())
            jc = slice(j * 128, (j + 1) * 128)
            at, rt, rkr, sgate = self.pt['at'][q], self.pt['rt'][q], self.pt['rkr'][q3], self.pt['sgate'][q3]
            kat, krt, krkr, ksg = f'p_at{q}', f'p_rt{q}', f'p_rkr{q3}', f'p_sgate{q3}'
            v_bf, dec = self.v_bf[q3], self.dec[q3]
            kv, kvb, kdec = f'v_sb{q3}', f'v_bf{q3}', f'dec{q3}'
            ps, pk = proj(j * 128)
            shift(ps, pk, t['r'][:], f't_r{q}', j, j)
            ps, pk = proj(D + j * 128)
            shift(ps, pk, t['k'][:], f't_k{q}', 8 + j, 8 + j)
            yield
            ps, pk = proj(2 * D + 128 + j * 128)
            shift(ps, pk, t['tmp'][:], f't_tmp{q}', 16 + j, 25 + j)
            self.act(sgate[:], t['tmp'][:], AF.Silu, [f't_tmp{q}'], [ksg])
            pv, pvk = self.next_ps()
            for blk in range(NB):
                n = 0
                for kc in range(NKC):
                    for (Wv, off) in ((self.Wva, 1), (self.Wvb, 0)):
                        self.mm(pv[:, blk * 128:(blk + 1) * 128], self.hn[:, kc, off + blk * 128:off + (blk + 1) * 128], Wv[:, kc, jc],
                                n == 0, n == 2 * NKC - 1, ['Wv', self.hnk], [pvk])
                        n += 1
            self.copy('dve', v_bf[:].rearrange("p b v -> p (b v)"), pv[:, :TT], [pvk], [kvb])
            yield
            pw, pwk = self.next_ps()
            self.mm(pw[:, :TT], self.lw2[0:64, jc], self.lo_bf[0:64, :], True, True, ['lw2', 't_lo_bf'], [pwk])
            self.act(t['sigw'][:], pw[:, :TT], AF.Sigmoid, [pwk, 'rw_vecs'], [f't_sigw{q}'], bias=V[:, 0, j:j + 1])
            pa, pak = self.next_ps()
            self.mm(pa[:, :TT], self.lw2[64:128, jc], self.lo_bf[64:128, :], True, True, ['lw2', 't_lo_bf'], [pak])
            self.act(t['a'][:], pa[:, :TT], AF.Sigmoid, [pak, 'rw_vecs'], [f't_a{q}'], bias=V[:, 1, j:j + 1])
            self.ts('dve', t['kk'][:], t['k'][:], V[:, 2, j:j + 1], None, ALU.mult, None, [f't_k{q}', 'rw_vecs'], [f't_kk{q}'])
            self.tt('pool', t['tmp'][:], t['kk'][:], t['kk'][:], ALU.mult, [f't_kk{q}'], [f't_tmp{q}'])
            pn, pnk = self.next_ps()
            self.mm(pn[:, :TT], self.blockones[:], t['tmp'][:], True, True, ['masks', f't_tmp{q}'], [pnk])
            self.ts('dve', t['rn'][:], pn[:, :TT], 1e-24, None, ALU.max, None, [pnk], [f't_rn{q}'])
            self.act(t['rn'][:], t['rn'][:], AF.Ln, [f't_rn{q}'], [f't_rn{q}'])
            self.act(t['rn'][:], t['rn'][:], AF.Exp, [f't_rn{q}'], [f't_rn{q}'], scale=-0.5)
            self.tt('pool', t['kk'][:], t['kk'][:], t['rn'][:], ALU.mult, [f't_kk{q}', f't_rn{q}'], [f't_kk{q}'])
            self.ts('dve', t['tmp'][:], t['a'][:], -1.0, V[:, 3, j:j + 1], ALU.add, ALU.mult, [f't_a{q}', 'rw_vecs'], [f't_tmp{q}'])
            self.stt('dve', t['kmod'][:], t['tmp'][:], 1.0, t['k'][:], ALU.add, ALU.mult, [f't_tmp{q}', f't_k{q}'], [f't_kmod{q}'])
            self.tt('pool', t['bbv'][:], t['kk'][:], t['a'][:], ALU.mult, [f't_kk{q}', f't_a{q}'], [f't_bbv{q}'])
            yield
            self.p.op('dve', lambda e: e.tensor_tensor_scan(out=t['c'][:], data0=self.resetm[:], data1=t['sigw'][:], initial=0.0,
                                                            op0=ALU.mult, op1=ALU.add), [f't_sigw{q}', 'masks'], [f't_c{q}'])
            self.act(t['e1'][:], t['c'][:], AF.Exp, [f't_c{q}'], [f't_rn{q}'], scale=LC)
            self.tt('pool', rt[:], t['r'][:], t['e1'][:], ALU.mult, [f't_r{q}', f't_rn{q}'], [krt])
            self.copy('act', t['rt_bf'][:], rt[:], [krt], [f't_rt_bf{q}'])
            self.act(t['e2'][:], t['c'][:], AF.Exp, [f't_c{q}'], [f't_tmp{q}'], scale=-LC)
            for hd in range(2):
                self.stt('dve', t[f'kt_h{hd}'][:], t['kmod'][:], self.hm[:, hd:hd + 1], t['e2'][:], ALU.mult, ALU.mult,
                         [f't_kmod{q}', f't_tmp{q}', 'masks'], [f't_kt_h{hd}_{q}'])
                self.stt('dve', t[f'bt_h{hd}'][:], t['bbv'][:], self.hm[:, hd:hd + 1], t['e2'][:], ALU.mult, ALU.mult,
                         [f't_bbv{q}', f't_tmp{q}', 'masks'], [f't_bt_h{hd}_{q}'])
            self.tt('pool', t['bt_bf'][:], t['bbv'][:], t['e2'][:], ALU.mult, [f't_bbv{q}', f't_tmp{q}'], [f't_bt_bf{q}'])
            self.tt('pool', t['e1'][:], t['c'][:], t['sigw'][:], ALU.subtract, [f't_c{q}', f't_sigw{q}'], [f't_rn{q}'])
            self.act(t['e1'][:], t['e1'][:], AF.Exp, [f't_rn{q}'], [f't_rn{q}'], scale=LC)
            self.stt('dve', at[:], t['kk'][:], -1.0, t['e1'][:], ALU.mult, ALU.mult, [f't_kk{q}', f't_rn{q}'], [kat])
            self.copy('act', t['at_bf'][:], at[:], [kat], [f't_at_bf{q}'])
            for hd in range(2):
                self.act(t[f'at_h{hd}'][:], at[:], AF.Copy, [kat, 'masks'], [f't_at_h{hd}_{q}'], scale=self.hm[:, hd:hd + 1])
            yield
            c3 = t['c'][:].rearrange("p (n c) -> p n c", c=128)
            self.tt('pool', t['e2'][:].rearrange("p (n c) -> p n c", c=128), c3[:, :, 127:128].to_broadcast([128, NB, 128]), c3,
                    ALU.subtract, [f't_c{q}'], [f't_tmp{q}'])
            self.act(t['e2'][:], t['e2'][:], AF.Exp, [f't_tmp{q}'], [f't_tmp{q}'], scale=LC)
            self.act(dec[:], c3[:, :, 127], AF.Exp, [f't_c{q}'], [kdec], scale=LC)
            self.tt('pool', t['khat'][:], t['kmod'][:], t['e2'][:], ALU.mult, [f't_kmod{q}', f't_tmp{q}'], [f't_khat{q}'])
            self.tt('dve', t['bhat'][:], t['bbv'][:], t['e2'][:], ALU.mult, [f't_bbv{q}', f't_tmp{q}'], [f't_bhat{q}'])
            self.stt('dve', rkr[:], t['r'][:], V[:, 4, j:j + 1], t['kmod'][:], ALU.mult, ALU.mult, [f't_r{q}', 'rw_vecs', f't_kmod{q}'], [krkr])
            yield
            NCH = NB * 2
            specs = {'P': ('bt_h', 'at_bf', self.maskS, self.Pb[0], f'Pb{q}0'),
                     'Q': ('at_h', 'bt_bf', self.maskSL, self.Qb[0], f'Qb{q}0'),
                     'ak': ('kt_h', 'at_bf', self.maskS, self.Aak[q4], f'Aak{q4}'),
                     'rk': ('kt_h', 'rt_bf', self.maskI, self.Ark[q4], f'Ark{q4}'),
                     'rb': ('bt_h', 'rt_bf', self.maskI, self.Arb[q4], f'Arb{q4}')}
            for name in ('P', 'Q', 'ak', 'rk', 'rb'):
                lh, rh, mask, dst, dk = specs[name]
                ps, pk = nm_ps(q)
                for c in range(NCH):
                    blk, hd = c // 2, c % 2
                    cs = slice(blk * 128, (blk + 1) * 128)
                    self.mm(ps[:, c * 128:(c + 1) * 128], t[f'{lh}{hd}'][:, cs], t[rh][:, cs], True, True, [f't_{lh}{hd}_{q}', f't_{rh}{q}'], [pk])
                self.tt('dve', dst[:], ps[:, 0:NCH * 128].rearrange("p (c t) -> p c t", c=NCH),
                        mask[:, None, :].to_broadcast([128, NCH, 128]), ALU.mult, [pk, 'masks'], [dk])
                if name == 'Q':
                    self.tt('pool', self.NT[q4][:], self.ident4[:], self.Pb[0][:], ALU.add, ['ident', f'Pb{q}0'], [f'NT{q4}'])
                    yield
            for blk in range(NB):
                cs = slice(blk * 128, (blk + 1) * 128)
                for (srcn, dst, dk) in (('khat', self.khm[q4][blk], f'khm{q4}{blk}'), ('bhat', self.bhm[q4][blk], f'bhm{q4}{blk}')):
                    ps, pk = nm_ps(q)
                    self.p.op('pe', lambda e, ps=ps, srcn=srcn, cs=cs: e.transpose(ps[:, 0:128], t[srcn][:, cs], self.ident[:]),
                              [f't_{srcn}{q}', 'ident'], [pk])
                    self.copy('act', dst[:], ps[:, 0:128], [pk], [dk])
            yield
            NTq, kNT = self.NT[q4], f'NT{q4}'
            for i in range(6):
                a_, b_ = i % 2, (i + 1) % 2
                Pa, Qa, Pn, Qn = self.Pb[a_], self.Qb[a_], self.Pb[b_], self.Qb[b_]
                kPa, kQa, kPn, kQn = f'Pb{q}{a_}', f'Qb{q}{a_}', f'Pb{q}{b_}', f'Qb{q}{b_}'
                if i < 5:
                    ps, pk = nm_ps(q)
                    for c in range(NCH):
                        self.mm(ps[:, c * 128:(c + 1) * 128], Qa[:, c, :], Pa[:, c, :], True, True, [kQa, kPa], [pk])
                    self.copy('act', Pn[:].rearrange("p c t -> p (c t)"), ps[:, 0:NCH * 128], [pk], [kPn])
                ps, pk = nm_ps(q)
                for c in range(NCH):
                    self.mm(ps[:, c * 128:(c + 1) * 128], Pa[:, c, :], Qa[:, c, :], True, True, [kPa, kQa], [pk])
                self.copy('dve' if i % 2 == 0 else 'act', Qn[:].rearrange("p c t -> p (c t)"), ps[:, 0:NCH * 128], [pk], [kQn])
                yield
                ps, pk = nm_ps(q)
                for c in range(NCH):
                    self.mm(ps[:, c * 128:(c + 1) * 128], Qn[:, c, :], NTq[:, c, :], True, True, [kQn, kNT], [pk])
                self.tt('dve', NTq[:].rearrange("p c t -> p (c t)"), NTq[:].rearrange("p c t -> p (c t)"), ps[:, 0:NCH * 128], ALU.add,
                        [kNT, pk], [kNT])
                yield

        def B_gen(j):
            t = self.tmp
            P = self.psums
            q = j % 2
            q3 = j % 3
            q4 = j % self.NDEEP
            jc = slice(j * 128, (j + 1) * 128)
            at, rt, rkr, sgate = self.pt['at'][q], self.pt['rt'][q], self.pt['rkr'][q3], self.pt['sgate'][q3]
            kat, krt, krkr, ksg = f'p_at{q}', f'p_rt{q}', f'p_rkr{q3}', f'p_sgate{q3}'
            v_bf, dec = self.v_bf[q3], self.dec[q3]
            kv, kvb, kdec = f'v_sb{q3}', f'v_bf{q3}', f'dec{q3}'
            kS = ('S', j)
            for blk in range(NB):
                cs = slice(blk * 128, (blk + 1) * 128)
                khm, bhm = self.khm[q4][blk], self.bhm[q4][blk]
                kkh, kbh = f'khm{q4}{blk}', f'bhm{q4}{blk}'
                pz, pzk = P[5], 'ps5'
                for hd in range(2):
                    hs, hc, c = hsl[hd], slice(hd * 64, (hd + 1) * 64), blk * 2 + hd
                    self.mm(pz[:, hc], self.Aak[q4][:, c, :], v_bf[:, blk, hc], True, False, [f'Aak{q4}', kvb], [pzk])
                    self.mm(pz[:, hc], at[hs, cs], self.S[hs, j, :], False, True, [kat, kS], [pzk])
                self.copy('act', self.Z_sb[:], pz[:, 0:128], [pzk], ['Z_sb'])
                yield
                for hd in range(2):
                    hc, c = slice(hd * 64, (hd + 1) * 64), blk * 2 + hd
                    self.mm(pz[:, hc], self.NT[q4][:, c, :], self.Z_sb[:, hc], True, True, [f'NT{q4}', 'Z_sb'], [pzk])
                self.copy('act', self.U_bf[:], pz[:, 0:128], [pzk], ['U_bf'])
                yield
                self.mm(pz[:, 0:128], khm[:], v_bf[:, blk, :], True, False, [kkh, kvb], [pzk])
                self.mm(pz[:, 0:128], bhm[:], self.U_bf[:], False, True, [kbh, 'U_bf'], [pzk])
                py, pyk = P[6], 'ps6'
                for hd in range(2):
                    hs, hc, c = hsl[hd], slice(hd * 64, (hd + 1) * 64), blk * 2 + hd
                    yc = slice(hd * 128, hd * 128 + 64)
                    bc_ = slice(hd * 128 + 64, hd * 128 + 128)
                    self.mm(py[:, yc], self.Ark[q4][:, c, :], v_bf[:, blk, hc], True, False, [f'Ark{q4}', kvb], [pyk])
                    self.mm(py[:, yc], rt[hs, cs], self.S[hs, j, :], False, False, [krt, kS], [pyk])
                    self.mm(py[:, yc], self.Arb[q4][:, c, :], self.U_bf[:, hc], False, True, [f'Arb{q4}', 'U_bf'], [pyk])
                    self.mm(py[:, bc_], rkr[hs, cs], self.blockones_bf[hs, hs], True, True, [krkr, 'masks'], [pyk])
                for hd in range(2):
                    hs, hc = hsl[hd], slice(hd * 64, (hd + 1) * 64)
                    self.stt('dve', self.S[hs, j, :], self.S[hs, j, :], dec[hs, blk:blk + 1], pz[hs, hc], ALU.mult, ALU.add,
                             [kS, kdec, pzk], [kS])
                yield
                bp = self.bcount % 2
                self.bcount += 1
                ysb, kys = self.ysb[bp], f'ysb{bp}'
                g, kg = self.gst2[bp], f'gst{bp}'
                yn, kyn = self.yn2[bp], f'yn{bp}'
                bon, kbon = self.bon2[bp], f'bon{bp}'
                junk, kjunk = self.junk2[bp], f'junk{bp}'
                self.copy('act', ysb[:], py[:, 0:256], [pyk], [kys])
                for hd in range(2):
                    hc = slice(hd * 64, (hd + 1) * 64)
                    yc = slice(hd * 128, hd * 128 + 64)
                    bc_ = slice(hd * 128 + 64, hd * 128 + 128)
                    self.act(junk[:], ysb[:, yc], AF.Identity, [kys], [kjunk, kg], accum_out=g[:, hd:hd + 1])
                    self.act(junk[:], ysb[:, yc], AF.Square, [kys], [kjunk, kg], accum_out=g[:, 2 + hd:3 + hd])
                    self.tt('pool', bon[:, hc], ysb[:, bc_], v_bf[:, blk, hc], ALU.mult, [kys, kvb], [kbon])
                self.ts('dve', g[:, 4:6], g[:, 0:2], 1.0 / 64, None, ALU.mult, None, [kg], [kg])
                self.tt('dve', g[:, 6:8], g[:, 4:6], g[:, 4:6], ALU.mult, [kg], [kg])
                self.stt('dve', g[:, 8:10], g[:, 2:4], 1.0 / 64, g[:, 6:8], ALU.mult, ALU.subtract, [kg], [kg])
                self.act(g[:, 8:10], g[:, 8:10], AF.Ln, [kg, 'consts'], [kg], bias=self.epsc[:, 1:2])
                self.act(g[:, 8:10], g[:, 8:10], AF.Exp, [kg], [kg], scale=-0.5)
                for hd in range(2):
                    hc = slice(hd * 64, (hd + 1) * 64)
                    yc = slice(hd * 128, hd * 128 + 64)
                    self.ts('dve', yn[:, hc], ysb[:, yc], g[:, 4 + hd:5 + hd], g[:, 8 + hd:9 + hd], ALU.subtract, ALU.mult,
                            [kys, kg], [kyn])
                yield
                self.tt('pool', yn[:], yn[:], self.gng_bc[:, jc], ALU.mult, [kyn, 'bc_tiles'], [kyn])
                self.tt('pool', yn[:], yn[:], self.gnb_bc[:, jc], ALU.add, [kyn, 'bc_tiles'], [kyn])
                self.tt('pool', yn[:], yn[:], bon[:], ALU.add, [kyn, kbon], [kyn])
                ps, pk = nm_ps(q)
                self.p.op('pe', lambda e, ps=ps, yn=yn: e.transpose(ps[:, 0:128], yn[:], self.ident[:]), [kyn, 'ident'], [pk])
                self.tt('dve', self.yTt[:, j, cs], ps[:, 0:128], sgate[:, cs], ALU.mult, [pk, ksg], [self.yTk])
                yield

        def drive(gens):
            gens = [g for g in gens if g is not None]
            while gens:
                for g in list(gens):
                    try:
                        next(g)
                    except StopIteration:
                        gens.remove(g)

        def tile(ti):
            t = self.tmp
            ps, pk = proj(2 * D)
            shift(ps, pk, t['lo'][:], 't_lo', 24, 24)
            self.act(t['lo'][0:64, :], t['lo'][0:64, :], AF.Tanh, ['t_lo'], ['t_lo'])
            self.copy('dve', self.lo_bf[:], t['lo'][:], ['t_lo'], ['t_lo_bf'])
            for j in range(NKC):
                drive([A_gen(j)])
                drive([B_gen(j)])

        self.run_layer(li, 'rwkv', TT, WC, setup, tile, is_last)
        self.ps_pool = list(range(8))

    def build(self):
        nc = self.nc
        T = self.T
        self.xT = self.din("xT", [D, T])
        self.yT = nc.dram_tensor("yT", [D, T], F32, kind="ExternalOutput").ap()
        d_norm_g = self.din("norm_g", [128, 4, NKC])
        d_final_g = self.din("final_g", [128, NKC])
        self.w_in_dram, self.w_out_dram = {}, {}
        kinds = [k for (_, k) in self.layers]
        if 'conv' in kinds:
            self.w_in_dram['conv'] = self.din("conv_w_in", [D, 4 * D])
            self.w_out_dram['conv'] = self.din("conv_w_out", [D, D])
            d_conv_w = self.din("conv_w", [128, NKC, 3])
        if 'rwkv' in kinds:
            self.w_in_dram['rwkv'] = self.din("rwkv_w_in", [D, 4 * D + 128])
            self.w_out_dram['rwkv'] = self.din("rwkv_w_out", [D, D])
            self.din("rwkv_mu_fm", [128, 33])
            self.din("rwkv_mu", [4 * D + 128])
            self.din("rwkv_vecs", [128, 5, NKC])
            self.din("rwkv_lw2", [128, D])
            self.din("rwkv_gn_g", [D])
            self.din("rwkv_gn_b", [D])
        if 'hgrn' in kinds:
            self.w_in_dram['hgrn'] = self.din("hgrn_w_in", [D, 4 * D])
            self.w_out_dram['hgrn'] = self.din("hgrn_w_out", [D, D])
            self.din("hgrn_gn_g", [D])
            self.din("hgrn_lbl", [128, 4, NKC])
        if 'gmlp' in kinds:
            self.w_in_dram['gmlp'] = self.din("gmlp_w_in", [D, 3 * D])
            self.w_out_dram['gmlp'] = self.din("gmlp_w_out", [D, D])
            self.din("gmlp_wsT", [128, 8, 128])
            self.din("gmlp_bs", [8, 128])
            self.din("gmlp_vg", [D])
        with ExitStack() as es:
            self.es = es
            nc.allow_low_precision("bf16 matmul operands, fp32 accumulation")
            self.p = p = Prog(nc, es)
            self.psums = [es.enter_context(nc.psum_tensor(f"ps{i}", [128, 512], F32)) for i in range(8)]
            self.ps_rr = 0
            self.ps_pool = list(range(8))
            self.ones_bf = self.sb("ones_bf", [128, 128], BF16)
            self.epsc = self.sb("epsc", [128, 4], F32)
            self.norm_g = self.sb("norm_g_sb", [128, 4, NKC], F32)
            self.final_g = self.sb("final_g_sb", [128, NKC], F32)
            p.op('pool', lambda e: e.memset(self.ones_bf[:], 1.0), [], ['ones_bf'])
            p.op('pool', lambda e: e.memset(self.epsc[:, 0:1], RMS_EPS), [], ['consts'])
            p.op('pool', lambda e: e.memset(self.epsc[:, 2:3], 1.0), ['consts'], ['consts'])
            p.dma('sp', self.norm_g[:], d_norm_g, [], ['consts'])
            p.dma('sp', self.final_g[:], d_final_g, [], ['consts'])
            if 'conv' in kinds:
                self.conv_w = self.sb("conv_w_sb", [128, NKC, 3], F32)
                p.dma('sp', self.conv_w[:], d_conv_w, [], ['consts'])
            self.first_layer = True
            for n, (li, kind) in enumerate(self.layers):
                is_last = n == len(self.layers) - 1
                if kind == 'conv':
                    self.conv_layer(li, is_last)
                elif kind == 'gmlp':
                    self.gmlp_layer(li, is_last)
                elif kind == 'hgrn':
                    self.hgrn_layer(li, is_last)
                elif kind == 'rwkv':
                    self.rwkv_layer(li, is_last)
                else:
                    raise ValueError(kind)
            p.finish('sp')
            self.stats = (p.n_ins, p.n_wait)
        return nc


def prep_inputs(inp, b, layers):
    f = np.float32
    m = {}
    m["xT"] = np.ascontiguousarray(np.asarray(inp["x"][b], f).T)
    m["norm_g"] = np.ascontiguousarray(np.asarray(inp["norm_g"], f).reshape(4, NKC, 128).transpose(2, 0, 1))
    m["final_g"] = np.ascontiguousarray(np.asarray(inp["final_g"], f).reshape(NKC, 128).T)
    kinds = [k for (_, k) in layers]
    if 'conv' in kinds:
        m["conv_w_in"] = np.ascontiguousarray(np.asarray(inp["conv_w_in"][0], f))
        m["conv_w_out"] = np.ascontiguousarray(np.asarray(inp["conv_w_out"][0], f))
        m["conv_w"] = np.ascontiguousarray(np.asarray(inp["conv_w"][0], f).reshape(3, NKC, 128).transpose(2, 1, 0))
    if 'rwkv' in kinds:
        m["rwkv_w_in"] = np.ascontiguousarray(np.asarray(inp["rwkv_w_in"][0], f))
        m["rwkv_w_out"] = np.ascontiguousarray(np.asarray(inp["rwkv_w_out"][0], f))
        mu = np.asarray(inp["rwkv_mu"][0], f)
        m["rwkv_mu"] = np.ascontiguousarray(mu)
        m["rwkv_mu_fm"] = np.ascontiguousarray(mu.reshape(33, 128).T)
        vecs = np.stack([np.asarray(inp[k][0], f).reshape(NKC, 128) for k in
                         ("rwkv_w0", "rwkv_a0", "rwkv_k_k", "rwkv_k_a", "rwkv_r_k")], axis=0)
        m["rwkv_vecs"] = np.ascontiguousarray(vecs.transpose(2, 0, 1))
        m["rwkv_lw2"] = np.ascontiguousarray(np.concatenate([np.asarray(inp["rwkv_w_w2"][0], f), np.asarray(inp["rwkv_w_a2"][0], f)], axis=0))
        m["rwkv_gn_g"] = np.ascontiguousarray(np.asarray(inp["rwkv_gn_g"][0], f))
        m["rwkv_gn_b"] = np.ascontiguousarray(np.asarray(inp["rwkv_gn_b"][0], f))
    if 'hgrn' in kinds:
        m["hgrn_w_in"] = np.ascontiguousarray(np.asarray(inp["hgrn_w_in"][0], f))
        m["hgrn_w_out"] = np.ascontiguousarray(np.asarray(inp["hgrn_w_out"][0], f))
        m["hgrn_gn_g"] = np.ascontiguousarray(np.asarray(inp["hgrn_gn_g"][0], f))
        m["hgrn_lbl"] = np.ascontiguousarray(np.asarray(inp["hgrn_lb_logits"], f).reshape(4, NKC, 128).transpose(2, 0, 1))
    if 'gmlp' in kinds:
        m["gmlp_w_in"] = np.ascontiguousarray(np.asarray(inp["gmlp_w_in"][0], f))
        m["gmlp_w_out"] = np.ascontiguousarray(np.asarray(inp["gmlp_w_out"][0], f))
        m["gmlp_wsT"] = np.ascontiguousarray(np.asarray(inp["gmlp_w_s"][0], f).transpose(2, 0, 1))
        m["gmlp_bs"] = np.ascontiguousarray(np.asarray(inp["gmlp_b_s"][0], f))
        m["gmlp_vg"] = np.ascontiguousarray(np.asarray(inp["gmlp_v_g"][0], f))
    return m


FULL_LAYERS = [(0, 'rwkv'), (1, 'hgrn'), (2, 'conv'), (3, 'gmlp')]


def kernel(**inputs):
    x = np.asarray(inputs["x"])
    B, T, _ = x.shape
    layers = FULL_LAYERS
    bld = Builder(T, layers)
    nc = bld.build()
    in_maps = []
    zeros = None
    for c in range(8):
        if c % 2 == 0:
            in_maps.append(prep_inputs(inputs, c // 2, layers))
        else:
            if zeros is None:
                zeros = {k: np.zeros_like(v) for k, v in in_maps[0].items()}
            in_maps.append(zeros)
    res = run_bass_kernel_spmd(nc, in_maps, core_ids=list(range(8)))
    out = np.stack([np.asarray(res.results[2 * b]["yT"]).T for b in range(B)], axis=0)
    return out.astype(np.float32)
```

```python
import numpy as np
from contextlib import ExitStack
import concourse.bass as bass
import concourse.mybir as mybir
from concourse.bass_utils import run_bass_kernel_spmd

F32 = mybir.dt.float32
BF16 = mybir.dt.bfloat16
ALU = mybir.AluOpType
AF = mybir.ActivationFunctionType
AX = mybir.AxisListType

D = 1024
NKC = 8
RMS_EPS = 1e-6
GN_EPS = 64e-5


class Prog:
    LIMIT = 30000

    def __init__(self, nc, es, n_dma_sems=24):
        self.nc = nc
        self.es = es
        self.engs = {'pe': nc.tensor, 'act': nc.scalar, 'dve': nc.vector,
                     'pool': nc.gpsimd, 'sp': nc.sync}
        self.sems = {}
        self.epoch = {k: 0 for k in self.engs}
        self.cnt = {k: 0 for k in self.engs}
        for k in self.engs:
            self.sems[(k, 0)] = es.enter_context(nc.semaphore(f"s_{k}_0"))
        self.dma_sems = []
        for i in range(n_dma_sems):
            key = ('dma', i)
            self.sems[key] = es.enter_context(nc.semaphore(f"s_dma_{i}"))
            self.cnt[key] = 0
            self.dma_sems.append(key)
        self.dma_rr = 0
        self.waited = {k: {} for k in self.engs}
        self.bufs = {}
        self.n_wait = 0
        self.n_ins = 0

    def _deps(self, reads, writes):
        deps = set()
        for k in reads:
            b = self.bufs.get(k)
            if b and b['w']:
                deps.add(b['w'])
        for k in writes:
            b = self.bufs.get(k)
            if b:
                if b['w']:
                    deps.add(b['w'])
                deps.update(b['r'])
        return deps

    def _wait(self, eng, deps):
        e = self.engs[eng]
        best = {}
        for (sk, v) in deps:
            if sk[0] == eng and eng == 'pe':
                continue
            if best.get(sk, 0) < v:
                best[sk] = v
        for sk, v in best.items():
            if self.waited[eng].get(sk, 0) >= v:
                continue
            e.wait_ge(self.sems[sk], v)
            self.waited[eng][sk] = v
            self.n_wait += 1

    def _record(self, tok, reads, writes):
        for k in reads:
            b = self.bufs.setdefault(k, {'w': None, 'r': []})
            b['r'].append(tok)
            if len(b['r']) > 64:
                best = {}
                for (sk, v) in b['r']:
                    if best.get(sk, 0) < v:
                        best[sk] = v
                b['r'] = list(best.items())
        for k in writes:
            b = self.bufs.setdefault(k, {'w': None, 'r': []})
            b['w'] = tok
            b['r'] = []

    @staticmethod
    def _excl(reads, writes):
        ps = [k for k in reads if isinstance(k, str) and k.startswith('ps')]
        if ps:
            reads = [k for k in reads if k not in ps]
            writes = list(writes) + ps
        return reads, writes

    disabled = False
    recording = None
    SYNC_LAT = 0.45

    def begin_record(self):
        self.recording = []

    def flush(self):
        rec = self.recording
        self.recording = None
        if not rec:
            return
        n = len(rec)
        preds = [None] * n
        succs = [[] for _ in range(n)]
        last_w = {}
        readers = {}
        for i, (kind, eng, fn, reads, writes, cost, lat) in enumerate(rec):
            ps = set()
            for k in reads:
                w = last_w.get(k)
                if w is not None:
                    ps.add(w)
            for k in writes:
                w = last_w.get(k)
                if w is not None:
                    ps.add(w)
                ps.update(readers.get(k, ()))
            ps.discard(i)
            preds[i] = ps
            for pi in ps:
                succs[pi].append(i)
            for k in reads:
                readers.setdefault(k, []).append(i)
            for k in writes:
                last_w[k] = i
                readers[k] = []
        npred = [len(p_) for p_ in preds]
        ready = [i for i in range(n) if npred[i] == 0]
        eng_free = {}
        end_t = [0.0] * n
        done_t = [0.0] * n
        order = []
        blevel = [0.0] * n
        for i in range(n - 1, -1, -1):
            kind, eng, fn, reads, writes, cost, lat = rec[i]
            b = 0.0
            for si in succs[i]:
                v = blevel[si] + (self.SYNC_LAT if rec[si][1] != eng else 0.0)
                if v > b:
                    b = v
            blevel[i] = b + cost + lat

        def est(i):
            kind, eng, fn, reads, writes, cost, lat = rec[i]
            t = eng_free.get(eng, 0.0)
            for pi in preds[i]:
                tp = done_t[pi] + (self.SYNC_LAT if rec[pi][1] != eng else 0.0)
                if tp > t:
                    t = tp
            return t
        EPS = 0.1
        while ready:
            ests = [(est(i), i) for i in ready]
            tmin = min(ests)[0]
            best = None
            for (t, i) in ests:
                if t <= tmin + EPS:
                    if best is None or blevel[i] > blevel[best[1]] or (blevel[i] == blevel[best[1]] and i < best[1]):
                        best = (t, i)
            t1, i = best
            ready.remove(i)
            kind, eng, fn, reads, writes, cost, lat = rec[i]
            end_t[i] = t1 + cost
            done_t[i] = t1 + cost + lat
            eng_free[eng] = end_t[i]
            order.append(i)
            for si in succs[i]:
                npred[si] -= 1
                if npred[si] == 0:
                    ready.append(si)
        assert len(order) == n, (len(order), n)
        self.sched_span = max(done_t) if done_t else 0.0
        for i in order:
            kind, eng, fn, reads, writes, cost, lat = rec[i]
            if kind == 'op':
                self.op(eng, fn, reads, writes)
            else:
                out, in_, kw = fn
                self.dma(eng, out, in_, reads, writes, **kw)

    def op(self, eng, fn, reads=(), writes=(), cost=None):
        if self.disabled:
            return None
        if self.recording is not None:
            reads, writes = self._excl(reads, writes)
            if cost is None:
                cost = {'pe': 0.2, 'act': 0.45, 'dve': 0.45, 'pool': 0.7, 'sp': 0.1}[eng]
            self.recording.append(('op', eng, fn, list(reads), list(writes), cost, 0.0))
            return None
        reads, writes = self._excl(reads, writes)
        deps = self._deps(reads, writes)
        self._wait(eng, deps)
        ins = fn(self.engs[eng])
        if self.cnt[eng] >= self.LIMIT:
            self.epoch[eng] += 1
            ep = self.epoch[eng]
            self.sems[(eng, ep)] = self.es.enter_context(self.nc.semaphore(f"s_{eng}_{ep}"))
            self.cnt[eng] = 0
        sk = (eng, self.epoch[eng])
        self.cnt[eng] += 1
        ins.then_inc(self.sems[sk], 1)
        self._record((sk, self.cnt[eng]), reads, writes)
        self.n_ins += 1
        return ins

    def dma(self, eng, out, in_, reads=(), writes=(), **kw):
        if self.disabled:
            return None
        if self.recording is not None:
            self.recording.append(('dma', eng, (out, in_, kw), list(reads), list(writes), 0.15, 6.0))
            return None
        deps = self._deps(reads, writes)
        sk = self.dma_sems[self.dma_rr]
        self.dma_rr = (self.dma_rr + 1) % len(self.dma_sems)
        if self.cnt[sk] > 0:
            deps.add((sk, self.cnt[sk]))
        self._wait(eng, deps)
        ins = self.engs[eng].dma_start(out=out, in_=in_, **kw)
        self.cnt[sk] += 16
        ins.then_inc(self.sems[sk], 16)
        self._record((sk, self.cnt[sk]), reads, writes)
        self.n_ins += 1
        return ins

    def all_tokens(self):
        deps = set()
        for k, b in self.bufs.items():
            if b['w']:
                deps.add(b['w'])
            deps.update(b['r'])
        return deps

    def barrier(self):
        deps = self.all_tokens()
        for eng in self.engs:
            d = set(x for x in deps)
            self._wait(eng, d)

    def finish(self, eng='sp'):
        self._wait(eng, self.all_tokens())


class Builder:
    def __init__(self, T, layers, do_final=True, neu_dt=None):
        self.neu_dt = neu_dt if neu_dt is not None else BF16
        self.use_sched = True
        self.T = T
        self.layers = layers
        self.do_final = do_final
        self.nc = bass.Bass("TRN2", target_bir_lowering=False)
        self.inputs = {}

    def din(self, name, shape):
        t = self.nc.dram_tensor(name, list(shape), F32, kind="ExternalInput").ap()
        self.inputs[name] = t
        return t

    def sb(self, name, shape, dt=F32):
        return self.es.enter_context(self.nc.sbuf_tensor(name, list(shape), dt))

    def lsb(self, name, shape, dt=F32):
        return self.les.enter_context(self.nc.sbuf_tensor(f"{name}_{self.lname}", list(shape), dt))

    def next_ps(self):
        pool = self.ps_pool
        i = pool[self.ps_rr % len(pool)]
        self.ps_rr += 1
        return self.psums[i], f"ps{i}"

    @staticmethod
    def ecost(eng, ap):
        try:
            n = ap.free_size()
        except Exception:
            n = 256
        if eng == 'act':
            return 0.22 + n * 0.00075
        if eng == 'dve':
            return 0.2 + n * 0.00095
        if eng == 'pool':
            return 0.2 + n * 0.0021
        return 0.2

    def tt(self, eng, out, in0, in1, op, reads, writes):
        return self.p.op(eng, lambda e: e.tensor_tensor(out=out, in0=in0, in1=in1, op=op), reads, writes, cost=self.ecost(eng, out))

    def ts(self, eng, out, in0, s1, s2, op0, op1, reads, writes):
        if s2 is None:
            return self.p.op(eng, lambda e: e.tensor_scalar(out=out, in0=in0, scalar1=s1, scalar2=None, op0=op0), reads, writes, cost=self.ecost(eng, out))
        return self.p.op(eng, lambda e: e.tensor_scalar(out=out, in0=in0, scalar1=s1, scalar2=s2, op0=op0, op1=op1), reads, writes, cost=self.ecost(eng, out))

    def stt(self, eng, out, in0, scalar, in1, op0, op1, reads, writes):
        eng = 'dve'
        return self.p.op(eng, lambda e: e.scalar_tensor_tensor(out=out, in0=in0, scalar=scalar, in1=in1, op0=op0, op1=op1), reads, writes, cost=self.ecost(eng, out))

    def act(self, out, in_, func, reads, writes, bias=None, scale=1.0, accum_out=None):
        kw = {}
        if bias is not None:
            kw['bias'] = bias
        if accum_out is not None:
            kw['accum_out'] = accum_out
        return self.p.op('act', lambda e: e.activation(out=out, in_=in_, func=func, scale=scale, **kw), reads, writes, cost=self.ecost('act', in_))

    def mm(self, out, lhsT, rhs, start, stop, reads, writes):
        try:
            n = rhs.free_size()
        except Exception:
            n = 128
        c = 0.06 + n / 2400.0 * (1.0 if lhsT.dtype == BF16 else 2.4)
        return self.p.op('pe', lambda e: e.matmul(out, lhsT=lhsT, rhs=rhs, start=start, stop=stop), reads, writes, cost=c)

    def copy(self, eng, out, in_, reads, writes):
        if eng == 'act':
            return self.p.op('act', lambda e: e.copy(out=out, in_=in_), reads, writes, cost=self.ecost('act', out))
        return self.p.op(eng, lambda e: e.tensor_copy(out=out, in_=in_), reads, writes, cost=self.ecost(eng, out))

    def load_weight_bf16(self, dst, dst_key, src, ncols, src_c0=0, dst_c0=0, scale_bc=None):
        p = self.p
        CH = 1024 if ncols % 1024 == 0 else ncols
        for kc in range(NKC):
            for c0 in range(0, ncols, CH):
                i = self.stage_i
                self.stage_i += 1
                nst = len(self.stage)
                st = self.stage[i % nst]
                sk = f"stage{i % nst}"
                p.dma('sp', st[:, 0:CH], src[kc * 128:(kc + 1) * 128, src_c0 + c0:src_c0 + c0 + CH], reads=[], writes=[sk])
                eng = ['dve', 'act'][i % 2] if scale_bc is None else ['dve', 'pool'][i % 2]
                if scale_bc is None:
                    self.copy(eng, dst[:, kc, dst_c0 + c0:dst_c0 + c0 + CH], st[:, 0:CH], [sk], [dst_key])
                else:
                    self.tt(eng, dst[:, kc, dst_c0 + c0:dst_c0 + c0 + CH], st[:, 0:CH], scale_bc[:, c0:c0 + CH], ALU.mult,
                            [sk, 'bc_tiles'], [dst_key])

    def rms_rstd(self, src, src_key, TT, tag):
        bi = self.rms_i % len(self.sqb_l)
        self.rms_i += 1
        sqb, rstd = self.sqb_l[bi], self.rstd_l[bi]
        ksq, krs = f'sqb{bi}', f'rstd{bi}'
        if self.sqb_alias:
            ksq = 'yT0'
        for kc in range(NKC):
            if kc % 2 == 0:
                self.act(sqb[:, kc, :TT], src[:, kc, :TT], AF.Square, [src_key], [(ksq, kc) if not self.sqb_alias else ksq])
            else:
                self.tt('dve', sqb[:, kc, :TT], src[:, kc, :TT], src[:, kc, :TT], ALU.mult, [src_key], [(ksq, kc) if not self.sqb_alias else ksq])
        ps, pk = self.next_ps()
        for kc in range(NKC):
            self.mm(ps[:, :TT], self.ones_bf[:], sqb[:, kc, :TT], kc == 0, kc == NKC - 1, [(ksq, kc) if not self.sqb_alias else ksq, 'ones_bf'], [pk])
        self.act(rstd[:, :TT], ps[:, :TT], AF.Ln, [pk, 'consts'], [krs], bias=self.epsc[:, 0:1], scale=1.0 / D)
        self.act(rstd[:, :TT], rstd[:, :TT], AF.Exp, [krs], [krs], scale=-0.5)
        return rstd, krs

    def run_layer(self, li, kind, TT, w_in_cols, mixer_setup, mixer_tile, is_last):
        p = self.p
        T = self.T
        ntiles = T // TT
        with ExitStack() as les:
            self.les = les
            self.lname = f"L{li}"
            self.TT = TT
            self.W_in = self.lsb("W_in", [128, NKC, w_in_cols], BF16)
            self.W_out = self.lsb("W_out", [128, NKC, D], BF16)
            ndb = 2 if kind != 'rwkv' else 1
            self.hT = [self.lsb(f"hT{i}", [128, NKC, TT], F32) for i in range(ndb)]
            self.sqb_alias = (kind == 'rwkv')
            if not self.sqb_alias:
                self.sqb_l = [self.lsb(f"sqb{i}", [128, NKC, TT], BF16) for i in range(ndb)]
            self.rstd_l = [self.lsb(f"rstd{i}", [128, TT], F32) for i in range(ndb)]
            self.rms_i = 0
            self.hn_l = [self.lsb(f"hn{i}", [128, NKC, TT + 1], BF16) for i in range(ndb)]
            self.yTt_l = [self.lsb(f"yTt{i}", [128, NKC, TT], BF16) for i in range(ndb)]
            self.hn, self.hnk = self.hn_l[0], 'hn0'
            self.yTt, self.yTk = self.yTt_l[0], 'yT0'
            if self.sqb_alias:
                self.sqb_l = [self.yTt_l[0]]
            self.stage_i = 0
            loader = mixer_setup()
            with ExitStack() as ses:
                nst = 2 if kind == 'rwkv' else max(2, min(4, (self.nc.sbuf_bytes_remaining - 512) // 4096))
                self.stage = [ses.enter_context(self.nc.sbuf_tensor(f"stage{i}_{self.lname}", [128, 1024], F32)) for i in range(nst)]
                if loader is None:
                    self.load_weight_bf16(self.W_in, 'W_in', self.w_in_dram[kind], w_in_cols)
                    self.load_weight_bf16(self.W_out, 'W_out', self.w_out_dram[kind], D)
                else:
                    loader(ses)
                p.barrier()
            if getattr(self, 'post_setup', None) is not None:
                self.post_setup()
                self.post_setup = None
            hn0 = self.hn_l[0]
            p.op('pool', lambda e: e.memset(hn0[:, :, 0:1], 0.0), [], ['hn0'])

            def load(ti):
                buf = self.hT[ti % len(self.hT)]
                src = self.xT if self.first_layer else self.yT
                p.dma('sp', buf[:], src.rearrange("(c p) t -> p c t", p=128)[:, :, ti * TT:(ti + 1) * TT],
                      reads=[('hd', ti * TT // 128 + i) for i in range(TT // 128)], writes=[f"hT{ti % len(self.hT)}"])

            if self.use_sched:
                p.begin_record()
            load(0)
            for ti in range(ntiles):
                if len(self.hT) > 1:
                    if ti + 1 < ntiles:
                        load(ti + 1)
                elif ti > 0:
                    load(ti)
                h = self.hT[ti % len(self.hT)]
                hk = f"hT{ti % len(self.hT)}"
                rstd, rk = self.rms_rstd(h, hk, TT, 'in')
                g = self.norm_g
                prev_hn, prev_hnk = self.hn, self.hnk
                bi = ti % ndb
                self.hn, self.hnk = self.hn_l[bi], f'hn{bi}'
                self.yTt, self.yTk = self.yTt_l[bi], f'yT{bi}'
                if ti > 0:
                    self.copy('pool', self.hn[:, :, 0:1], prev_hn[:, :, TT:TT + 1], [prev_hnk], [self.hnk])
                for kc in range(NKC):
                    self.stt('dve', self.hn[:, kc, 1:TT + 1], h[:, kc, :], g[:, li, kc:kc + 1], rstd[:, :TT],
                             ALU.mult, ALU.mult, [hk, rk, 'consts'], [self.hnk])
                mixer_tile(ti)
                for j in range(NKC):
                    ps, pk = self.next_ps()
                    for kc in range(NKC):
                        self.mm(ps[:, :TT], self.W_out[:, kc, j * 128:(j + 1) * 128], self.yTt[:, kc, :TT],
                                kc == 0, kc == NKC - 1, ['W_out', self.yTk], [pk])
                    self.tt('dve', h[:, j, :], h[:, j, :], ps[:, :TT], ALU.add, [hk, pk], [hk])
                if is_last and self.do_final:
                    rstd, rk = self.rms_rstd(h, hk, TT, 'fin')
                    for kc in range(NKC):
                        self.stt('dve' if kc % 2 == 0 else 'pool', h[:, kc, :], h[:, kc, :], self.final_g[:, kc:kc + 1], rstd[:, :TT],
                                 ALU.mult, ALU.mult, [hk, rk, 'consts'], [hk])
                p.dma('sp', self.yT.rearrange("(c p) t -> p c t", p=128)[:, :, ti * TT:(ti + 1) * TT], h[:],
                      reads=[hk], writes=[('hd', (ti * TT) // 128 + i) for i in range(max(1, TT // 128))])
            if self.use_sched:
                p.flush()
            self.first_layer = False
            p.barrier()
        self.les = None

    def conv_layer(self, li, is_last):
        TT = 512

        def setup():
            self.yext = self.lsb("yext", [128, NKC, TT + 2], F32)
            self.zs = [self.lsb(f"zs{i}", [128, TT], F32) for i in range(2)]
            self.acc = [self.lsb(f"acc{i}", [128, TT], F32) for i in range(2)]
            self.sg = [self.lsb(f"sg{i}", [128, TT], F32) for i in range(2)]
            self.p.op('pool', lambda e: e.memset(self.yext[:, :, 0:2], 0.0), [], [('yext', j) for j in range(NKC)])

        def tile(ti):
            W = self.W_in
            cw = self.conv_w
            for j in range(NKC):
                zs, acc, sg = self.zs[j % 2], self.acc[j % 2], self.sg[j % 2]
                zk, ak, gk = f"zs{j % 2}", f"acc{j % 2}", f"sg{j % 2}"
                pss = []
                for blk in range(4):
                    ps, pk = self.next_ps()
                    col0 = blk * D + j * 128
                    for kc in range(NKC):
                        self.mm(ps[:, :TT], W[:, kc, col0:col0 + 128], self.hn[:, kc, 1:TT + 1], kc == 0, kc == NKC - 1,
                                ['W_in', self.hnk], [pk])
                    pss.append((ps, pk))
                (pb, pbk), (pc, pck), (pz, pzk), (pg, pgk) = pss
                yk = ('yext', j)
                self.copy('act', zs[:], pz[:, :TT], [pzk], [zk])
                if ti > 0:
                    self.copy('pool', self.yext[:, j, 0:2], self.yext[:, j, TT:TT + 2], [yk], [yk])
                self.tt('dve', self.yext[:, j, 2:TT + 2], pc[:, :TT], zs[:], ALU.mult, [pck, zk], [yk])
                self.act(acc[:], self.yext[:, j, 2:TT + 2], AF.Copy, [yk, 'consts'], [ak], scale=cw[:, j, 2:3])
                self.stt('pool', acc[:], self.yext[:, j, 1:TT + 1], cw[:, j, 1:2], acc[:], ALU.mult, ALU.add, [yk, ak, 'consts'], [ak])
                self.stt('pool', acc[:], self.yext[:, j, 0:TT], cw[:, j, 0:1], acc[:], ALU.mult, ALU.add, [yk, ak, 'consts'], [ak])
                self.act(sg[:], pg[:, :TT], AF.Silu, [pgk], [gk])
                self.tt('dve', acc[:], pb[:, :TT], acc[:], ALU.mult, [pbk, ak], [ak])
                self.tt('pool', self.yTt[:, j, :], acc[:], sg[:], ALU.mult, [ak, gk], [self.yTk])

        self.run_layer(li, 'conv', TT, 4 * D, setup, tile, is_last)

    def gmlp_layer(self, li, is_last):
        TT = 512

        def setup():
            p = self.p
            self.wsT = self.lsb("wsT", [128, 8, 128], F32)
            self.bs_bc = self.lsb("bs_bc", [128, 8, TT], F32)
            self.vg_bc = self.lsb("vg_bc", [128, D], F32)
            self.vn = [self.lsb(f"vn{i}", [128, D], F32) for i in range(TT // 128)]
            self.vss = self.lsb("vss", [128, 4], F32)
            self.junk = self.lsb("junk", [128, 512], F32)
            self.s_sb = [self.lsb(f"s_sb{i}", [128, TT], F32) for i in range(2)]
            self.sg = [self.lsb(f"sg{i}", [128, TT], F32) for i in range(2)]
            p.dma('sp', self.wsT[:], self.inputs['gmlp_wsT'], [], ['wsT'])
            for g in range(8):
                p.op('pool', lambda e: e.affine_select(out=self.wsT[:, g, :], in_=self.wsT[:, g, :], pattern=[[1, 128]],
                                                       compare_op=ALU.is_ge, fill=0.0, base=0, channel_multiplier=-1),
                     ['wsT'], ['wsT'])
            for r in range(TT // 128):
                p.dma('sp', self.bs_bc[:, :, r * 128:(r + 1) * 128],
                      self.inputs['gmlp_bs'].partition_broadcast(128), [], ['bs_bc'])
            p.dma('sp', self.vg_bc[:], self.inputs['gmlp_vg'].partition_broadcast(128), [], ['vg_bc'])

        def tile(ti):
            W = self.W_in
            nblk = TT // 128
            for blk in range(nblk):
                vn = self.vn[blk]
                vk = f"vn{blk}"
                halves = []
                for hf in range(2):
                    ps, pk = self.next_ps()
                    for kc in range(NKC):
                        self.mm(ps[:, :512], self.hn[:, kc, 1 + blk * 128:1 + (blk + 1) * 128],
                                W[:, kc, D + hf * 512:D + (hf + 1) * 512], kc == 0, kc == NKC - 1, ['W_in', self.hnk], [pk])
                    halves.append((ps, pk))
                for hf, (ps, pk) in enumerate(halves):
                    self.act(self.junk[:], ps[:, :512], AF.Square, [pk], ['junk', 'vss'], accum_out=self.vss[:, hf:hf + 1])
                self.tt('dve', self.vss[:, 2:3], self.vss[:, 0:1], self.vss[:, 1:2], ALU.add, ['vss'], ['vss'])
                self.act(self.vss[:, 3:4], self.vss[:, 2:3], AF.Ln, ['vss', 'consts'], ['vss'], bias=self.epsc[:, 0:1], scale=1.0 / D)
                self.act(self.vss[:, 3:4], self.vss[:, 3:4], AF.Exp, ['vss'], ['vss'], scale=-0.5)
                for hf, (ps, pk) in enumerate(halves):
                    self.stt('dve', vn[:, hf * 512:(hf + 1) * 512], ps[:, :512], self.vss[:, 3:4],
                             self.vg_bc[:, hf * 512:(hf + 1) * 512], ALU.mult, ALU.mult, [pk, 'vss', 'vg_bc'], [vk])
            for j in range(NKC):
                s_sb, sg = self.s_sb[j % 2], self.sg[j % 2]
                sk, gk = f"s_sb{j % 2}", f"sg{j % 2}"
                ps, pk = self.next_ps()
                for blk in range(nblk):
                    self.mm(ps[:, blk * 128:(blk + 1) * 128], self.vn[blk][:, j * 128:(j + 1) * 128], self.wsT[:, j, :], True, True,
                            [f"vn{blk}", 'wsT'], [pk])
                self.tt('dve', s_sb[:], ps[:, :TT], self.bs_bc[:, j, :], ALU.add, [pk, 'bs_bc'], [sk])
                pu, puk = self.next_ps()
                for kc in range(NKC):
                    self.mm(pu[:, :TT], W[:, kc, j * 128:(j + 1) * 128], self.hn[:, kc, 1:TT + 1], kc == 0, kc == NKC - 1,
                            ['W_in', self.hnk], [puk])
                pg, pgk = self.next_ps()
                for kc in range(NKC):
                    self.mm(pg[:, :TT], W[:, kc, 2 * D + j * 128:2 * D + (j + 1) * 128], self.hn[:, kc, 1:TT + 1], kc == 0,
                            kc == NKC - 1, ['W_in', self.hnk], [pgk])
                self.act(sg[:], pg[:, :TT], AF.Silu, [pgk], [gk])
                self.tt('dve', s_sb[:], pu[:, :TT], s_sb[:], ALU.mult, [puk, sk], [sk])
                self.tt('pool', self.yTt[:, j, :], s_sb[:], sg[:], ALU.mult, [sk, gk], [self.yTk])

        self.run_layer(li, 'gmlp', TT, 3 * D, setup, tile, is_last)


    def make_ident(self, ident, key):
        p = self.p
        p.op('pool', lambda e: e.memset(ident[:], 1.0), [], [key])
        p.op('pool', lambda e: e.affine_select(out=ident[:], in_=ident[:], pattern=[[-1, 128]], compare_op=ALU.is_equal,
                                               fill=0.0, base=0, channel_multiplier=1), [key], [key])

    def make_block_masks(self, C, maskT, colmask, rowmask, strict=False):
        p = self.p
        nch = 128 // C
        if maskT is not None:
            p.op('pool', lambda e: e.memset(maskT[:], 1.0), [], ['masks'])
            p.op('pool', lambda e: e.affine_select(out=maskT[:], in_=maskT[:], pattern=[[1, 128]], compare_op=ALU.is_ge if not strict else ALU.is_gt,
                                                   fill=0.0, base=0, channel_multiplier=-1), ['masks'], ['masks'])
            for c in range(1, nch):
                p.op('pool', lambda e, c=c: e.affine_select(out=maskT[:, c * C:(c + 1) * C], in_=maskT[:, c * C:(c + 1) * C], pattern=[[0, C]],
                                                            compare_op=ALU.is_ge, fill=0.0, base=-c * C, channel_multiplier=1), ['masks'], ['masks'])
        if colmask is not None:
            p.op('pool', lambda e: e.memset(colmask[:], 0.0), [], ['masks'])
            for c in range(nch):
                p.op('pool', lambda e, c=c: e.memset(colmask[:, c, c * C:(c + 1) * C], 1.0), ['masks'], ['masks'])
        if rowmask is not None:
            p.op('pool', lambda e: e.memset(rowmask[:], 1.0), [], ['masks'])
            for c in range(nch):
                p.op('pool', lambda e, c=c: e.affine_select(out=rowmask[:, c:c + 1], in_=rowmask[:, c:c + 1], pattern=[[0, 1]],
                                                            compare_op=ALU.is_ge, fill=0.0, base=-c * C, channel_multiplier=1), ['masks'], ['masks'])
                p.op('pool', lambda e, c=c: e.affine_select(out=rowmask[:, c:c + 1], in_=rowmask[:, c:c + 1], pattern=[[0, 1]],
                                                            compare_op=ALU.is_ge, fill=0.0, base=c * C + C - 1, channel_multiplier=-1), ['masks'], ['masks'])

    def hgrn_layer(self, li, is_last):
        TT = 256
        C = 32
        NB = TT // 128
        NCH = TT // C

        def setup():
            p = self.p
            L = self.lsb
            self.ident = L("ident", [128, 128], F32)
            self.make_ident(self.ident, 'ident')
            self.maskT = L("maskT", [128, 128], F32)
            self.colmask = L("colmask", [128, 4, 128], F32)
            self.rowmask = L("rowmask", [128, 4], F32)
            self.make_block_masks(C, self.maskT, self.colmask, self.rowmask)
            self.resetm = L("resetm", [128, TT], F32)
            self.ones_t = L("ones_t", [128, TT], F32)
            p.op('pool', lambda e: e.memset(self.ones_t[:], 1.0), [], ['masks'])
            p.op('pool', lambda e: e.memset(self.resetm[:], 1.0), [], ['masks'])
            p.op('pool', lambda e: e.memset(self.resetm[:].rearrange("p (n c) -> p n c", c=C)[:, :, 0:1], 0.0), ['masks'], ['masks'])
            self.gn_bc = L("gn_bc", [128, D], F32)
            p.dma('sp', self.gn_bc[:], self.inputs['hgrn_gn_g'].partition_broadcast(128), [], ['gn_bc'])
            self.lbl = L("lbl", [128, 4, NKC], F32)
            self.lbt = L("lbt", [128, 4, NKC], F32)
            p.dma('sp', self.lbl[:], self.inputs['hgrn_lbl'], [], ['lbl'])
            self.act(self.lbl[:], self.lbl[:], AF.Exp, ['lbl'], ['lbl'])
            self.tt('dve', self.lbt[:, 0, :], self.lbl[:, 0, :], self.lbl[:, 1, :], ALU.add, ['lbl'], ['lbt'])
            self.tt('dve', self.lbt[:, 0, :], self.lbt[:, 0, :], self.lbl[:, 2, :], ALU.add, ['lbl', 'lbt'], ['lbt'])
            self.tt('dve', self.lbt[:, 0, :], self.lbt[:, 0, :], self.lbl[:, 3, :], ALU.add, ['lbl', 'lbt'], ['lbt'])
            p.op('dve', lambda e: e.reciprocal(out=self.lbt[:, 3, :], in_=self.lbt[:, 0, :]), ['lbt'], ['lbt'])
            p.op('dve', lambda e: e.memset(self.lbt[:, 1, :], 0.0), ['lbt'], ['lbt'])
            for i in range(1, li + 1):
                self.tt('dve', self.lbt[:, 1, :], self.lbt[:, 1, :], self.lbl[:, i, :], ALU.add, ['lbl', 'lbt'], ['lbt'])
            self.tt('dve', self.lbt[:, 1, :], self.lbt[:, 1, :], self.lbt[:, 3, :], ALU.mult, ['lbt'], ['lbt'])
            self.ts('dve', self.lbt[:, 2, :], self.lbt[:, 1, :], -1.0, 1.0, ALU.mult, ALU.add, ['lbt'], ['lbt'])
            self.S = L("S_hgrn", [128, NKC, 128], F32)
            p.op('pool', lambda e: e.memset(self.S[:], 0.0), [], [('S', j) for j in range(NKC)])
            names = ['f', 'kk', 'bb', 'qe', 'dd', 'sg']
            self.tmps = []
            for q in range(2):
                tm = {n: L(f"h_{n}{q}", [128, TT], F32) for n in names}
                tm['e1'] = tm['f']
                tm['ko'] = tm['dd']
                tm['ke_bf'] = L(f"h_ke_bf{q}", [128, TT], BF16)
                tm['qe_bf'] = L(f"h_qe_bf{q}", [128, TT], BF16)
                tm['kom'] = L(f"kom{q}", [128, 4, NB, 128], BF16)
                self.tmps.append(tm)
            self.sgate = [L(f"sgate{q}", [128, TT], BF16) for q in range(3)]
            self.qem = [L(f"qem{q}", [128, 4, TT], BF16) for q in range(3)]
            self.v_bf = [L(f"v_bf{q}", [128, NB, 128], BF16) for q in range(3)]
            self.attm = [L(f"attm{q}", [128, NB, 128], BF16) for q in range(3)]
            self.u_sb = [L(f"u_sb{q}", [128, NCH, 128], F32) for q in range(3)]
            self.dec = [L(f"dec{q}", [128, NCH], F32) for q in range(3)]
            self.S_all2 = [L(f"S_all{q}", [128, 5, 128], F32) for q in range(2)]
            self.S_bf2 = [L(f"S_bf{q}", [128, NCH, 128], BF16) for q in range(2)]
            self.on2 = [L(f"on{q}", [128, NB, 128], F32) for q in range(2)]
            self.oss2 = [L(f"oss{q}", [128, 2 * NB], F32) for q in range(2)]
            self.junk2 = [L(f"junk{q}", [128, 128], F32) for q in range(2)]
            self.ps_pool = [0, 1, 2, 3]
            self.nm_rr = 0

        def nm_ps():
            i = [4, 5][self.nm_rr % 2]
            self.nm_rr += 1
            return self.psums[i], f"ps{i}"

        def proj(col0):
            ps, pk = self.next_ps()
            for kc in range(NKC):
                self.mm(ps[:, :TT], self.W_in[:, kc, col0:col0 + 128], self.hn[:, kc, 1:TT + 1], kc == 0, kc == NKC - 1, ['W_in', self.hnk], [pk])
            return ps, pk

        def A_gen(j):
            q2 = j % 2
            t = self.tmps[q2]
            kom = t['kom']
            W = self.W_in
            q = j % 3
            lb, oml = self.lbt[:, 1, :], self.lbt[:, 2, :]
            sgate, qem, v_bf, attm, u_sb, dec = self.sgate[q], self.qem[q], self.v_bf[q], self.attm[q], self.u_sb[q], self.dec[q]
            ksg, kqem, kv, katt, ku, kdec = f'sgate{q}', f'qem{q}', f'v_bf{q}', f'attm{q}', f'u_sb{q}', f'dec{q}'
            pf, pfk = proj(D + j * 128)
            self.act(t['f'][:], pf[:, :TT], AF.Exp, [pfk], [f't_f{q2}'], scale=-1.0)
            self.tt('pool', t['f'][:], t['f'][:], self.ones_t[:], ALU.add, [f't_f{q2}', 'masks'], [f't_f{q2}'])
            self.p.op('dve', lambda e: e.reciprocal(out=t['f'][:], in_=t['f'][:]), [f't_f{q2}'], [f't_f{q2}'], cost=0.45)
            self.ts('dve', t['f'][:], t['f'][:], oml[:, j:j + 1], lb[:, j:j + 1], ALU.mult, ALU.add, [f't_f{q2}', 'lbt'], [f't_f{q2}'])
            self.act(t['kk'][:], t['f'][:], AF.Identity, [f't_f{q2}'], [f't_kk{q2}'], scale=-1.0, bias=self.epsc[:, 2:3])
            self.act(t['dd'][:], t['f'][:], AF.Ln, [f't_f{q2}'], [f't_dd{q2}'])
            self.p.op('dve', lambda e: e.tensor_tensor_scan(out=t['bb'][:], data0=self.resetm[:], data1=t['dd'][:], initial=0.0,
                                                            op0=ALU.mult, op1=ALU.add), [f't_dd{q2}', 'masks'], [f't_bb{q2}'])
            yield
            pq, pqk = proj(j * 128)
            self.act(t['e1'][:], t['bb'][:], AF.Exp, [f't_bb{q2}'], [f't_f{q2}'])
            self.tt('dve', t['qe'][:], pq[:, :TT], t['e1'][:], ALU.mult, [pqk, f't_f{q2}'], [f't_qe{q2}'])
            self.copy('act', t['qe_bf'][:], t['qe'][:], [f't_qe{q2}'], [f't_qe_bf{q2}'])
            qe4 = t['qe'][:].rearrange("p (b t) -> p b t", t=128)
            for c in range(4):
                self.tt('pool', qem[:, c, :].rearrange("p (b t) -> p b t", t=128), qe4,
                        self.colmask[:, c:c + 1, :].to_broadcast([128, NB, 128]), ALU.mult, [f't_qe{q2}', 'masks'], [kqem])
            yield
            self.act(t['e1'][:], t['bb'][:], AF.Exp, [f't_bb{q2}'], [f't_f{q2}'], scale=-1.0)
            self.tt('pool', t['ke_bf'][:], t['kk'][:], t['e1'][:], ALU.mult, [f't_kk{q2}', f't_f{q2}'], [f't_ke_bf{q2}'])
            b3 = t['bb'][:].rearrange("p (n c) -> p n c", c=C)
            self.act(dec[:], b3[:, :, C - 1], AF.Exp, [f't_bb{q2}'], [kdec])
            self.tt('pool', t['dd'][:].rearrange("p (n c) -> p n c", c=C), b3[:, :, C - 1:C].to_broadcast([128, NCH, C]), b3, ALU.subtract,
                    [f't_bb{q2}'], [f't_dd{q2}'])
            self.act(t['dd'][:], t['dd'][:], AF.Exp, [f't_dd{q2}'], [f't_dd{q2}'])
            self.tt('pool', t['ko'][:], t['kk'][:], t['dd'][:], ALU.mult, [f't_kk{q2}', f't_dd{q2}'], [f't_dd{q2}'])
            pg, pgk = proj(3 * D + j * 128)
            self.act(t['sg'][:], pg[:, :TT], AF.Exp, [pgk], [f't_sg{q2}'], scale=-1.0)
            self.tt('pool', t['sg'][:], t['sg'][:], self.ones_t[:], ALU.add, [f't_sg{q2}', 'masks'], [f't_sg{q2}'])
            self.p.op('dve', lambda e: e.reciprocal(out=t['sg'][:], in_=t['sg'][:]), [f't_sg{q2}'], [f't_sg{q2}'], cost=0.45)
            self.tt('dve', sgate[:], pg[:, :TT], t['sg'][:], ALU.mult, [pgk, f't_sg{q2}'], [ksg])
            yield
            pv, pvk = self.next_ps()
            for blk in range(NB):
                for kc in range(NKC):
                    self.mm(pv[:, blk * 128:(blk + 1) * 128], self.hn[:, kc, 1 + blk * 128:1 + (blk + 1) * 128],
                            W[:, kc, 2 * D + j * 128:2 * D + (j + 1) * 128], kc == 0, kc == NKC - 1, ['W_in', self.hnk], [pvk])
            self.copy('act', v_bf[:].rearrange("p b v -> p (b v)"), pv[:, :TT], [pvk], [kv])
            yield
            ps, pk = nm_ps()
            for blk in range(NB):
                cs = slice(blk * 128, (blk + 1) * 128)
                self.mm(ps[:, cs], t['ke_bf'][:, cs], t['qe_bf'][:, cs], True, True, [f't_ke_bf{q2}', f't_qe_bf{q2}'], [pk])
            self.tt('dve', attm[:], ps[:, :TT].rearrange("p (b t) -> p b t", t=128), self.maskT[:, None, :].to_broadcast([128, NB, 128]),
                    ALU.mult, [pk, 'masks'], [katt])
            ps, pk = nm_ps()
            for blk in range(NB):
                cs = slice(blk * 128, (blk + 1) * 128)
                self.p.op('pe', lambda e, ps=ps, cs=cs: e.transpose(ps[:, cs], t['ko'][:, cs], self.ident[:]), [f't_dd{q2}', 'ident'], [pk])
            for c in range(4):
                self.act(kom[:, c, :, :].rearrange("p b k -> p (b k)"), ps[:, :TT], AF.Copy, [pk, 'masks'], [f'kom{q2}'],
                         scale=self.rowmask[:, c:c + 1])
            yield
            for blk in range(NB):
                ps, pk = nm_ps()
                for c in range(4):
                    self.mm(ps[:, c * 128:(c + 1) * 128], kom[:, c, blk, :], v_bf[:, blk, :], True, True, [f'kom{q2}', kv], [pk])
                self.copy('act' if blk % 2 else 'dve', u_sb[:, blk * 4:(blk + 1) * 4, :].rearrange("p c v -> p (c v)"), ps[:, 0:512], [pk], [ku])
                if blk % 2:
                    yield

        def B_gen(j):
            P = self.psums
            q = j % 3
            sgate, qem, v_bf, attm, u_sb, dec = self.sgate[q], self.qem[q], self.v_bf[q], self.attm[q], self.u_sb[q], self.dec[q]
            ksg, kqem, kv, katt, ku, kdec = f'sgate{q}', f'qem{q}', f'v_bf{q}', f'attm{q}', f'u_sb{q}', f'dec{q}'
            kS = ('S', j)
            q2 = j % 2
            SA = self.S_all2[q2]
            S_bf, on, oss, junk = self.S_bf2[q2], self.on2[q2], self.oss2[q2], self.junk2[q2]
            kSA, kSbf, kon, koss, kjunk = f'S_all{q2}', f'S_bf{q2}', f'on{q2}', f'oss{q2}', f'junk{q2}'
            self.copy('pool', SA[:, 0, :], self.S[:, j, :], [kS], [kSA])
            for blk in range(NB):
                for c in range(4):
                    n = blk * 4 + c
                    self.stt('dve', SA[:, c + 1, :], SA[:, c, :], dec[:, n:n + 1], u_sb[:, n, :], ALU.mult, ALU.add, [kSA, kdec, ku], [kSA])
                self.copy('act', S_bf[:, blk * 4:(blk + 1) * 4, :].rearrange("p c v -> p (c v)"),
                          SA[:, 0:4, :].rearrange("p c v -> p (c v)"), [kSA], [kSbf])
                if blk < NB - 1:
                    self.copy('dve', SA[:, 0, :], SA[:, 4, :], [kSA], [kSA])
                yield
            self.copy('pool', self.S[:, j, :], SA[:, 4, :], [kSA], [kS])
            po, pok = P[6], 'ps6'
            for blk in range(NB):
                cs = slice(blk * 128, (blk + 1) * 128)
                self.mm(po[:, cs], attm[:, blk, :], v_bf[:, blk, :], True, False, [katt, kv], [pok])
                for c in range(4):
                    self.mm(po[:, cs], qem[:, c, cs], S_bf[:, blk * 4 + c, :], False, c == 3, [kqem, kSbf], [pok])
                if blk % 2:
                    yield
            for blk in range(NB):
                cs = slice(blk * 128, (blk + 1) * 128)
                self.act(junk[:], po[:, cs], AF.Square, [pok], [kjunk, koss], accum_out=oss[:, blk:blk + 1])
            self.act(oss[:, NB:2 * NB], oss[:, 0:NB], AF.Ln, [koss, 'consts'], [koss], bias=self.epsc[:, 0:1], scale=1.0 / 128)
            self.act(oss[:, NB:2 * NB], oss[:, NB:2 * NB], AF.Exp, [koss], [koss], scale=-0.5)
            self.tt('dve', on[:], po[:, :TT].rearrange("p (b v) -> p b v", v=128),
                    oss[:, NB:2 * NB, None].to_broadcast([128, NB, 128]), ALU.mult, [pok, koss], [kon])
            self.tt('pool', on[:], on[:], self.gn_bc[:, None, j * 128:(j + 1) * 128].to_broadcast([128, NB, 128]), ALU.mult,
                    [kon, 'gn_bc'], [kon])
            yield
            py, pyk = P[7], 'ps7'
            for blk in range(NB):
                cs = slice(blk * 128, (blk + 1) * 128)
                self.p.op('pe', lambda e, cs=cs, blk=blk: e.transpose(py[:, cs], on[:, blk, :], self.ident[:]), [kon, 'ident'], [pyk])
            self.tt('dve', self.yTt[:, j, :], py[:, :TT], sgate[:], ALU.mult, [pyk, ksg], [self.yTk])
            yield

        def drive(gens):
            gens = [g for g in gens if g is not None]
            while gens:
                for g in list(gens):
                    try:
                        next(g)
                    except StopIteration:
                        gens.remove(g)

        def step(g):
            try:
                next(g)
                return True
            except StopIteration:
                return False

        def tile(ti):
            A = {0: A_gen(0), 1: A_gen(1)}
            while step(A[0]):
                step(A[1])
            for sl in range(NKC):
                must = [B_gen(sl)]
                if sl + 1 < NKC:
                    must.append(A[sl + 1])
                opt = None
                if sl + 2 < NKC:
                    A[sl + 2] = A_gen(sl + 2)
                    opt = A[sl + 2]
                while must:
                    for g in list(must):
                        if not step(g):
                            must.remove(g)
                    if opt is not None and not step(opt):
                        opt = None

        self.run_layer(li, 'hgrn', TT, 4 * D, setup, tile, is_last)
        self.ps_pool = list(range(8))

    def rwkv_layer(self, li, is_last):
        TT = 256
        NB = TT // 128
        WC = 3200
        NDT = self.neu_dt
        LC = -0.6065306597126334

        def setup():
            p = self.p
            L = self.lsb
            self.ident = L("ident", [128, 128], F32)
            self.make_ident(self.ident, 'ident')
            self.ident_n = L("ident_n", [128, 128], NDT)
            self.copy('dve', self.ident_n[:], self.ident[:], ['ident'], ['ident'])
            self.maskS = L("maskS", [128, 128], F32)
            self.maskI = L("maskI", [128, 128], F32)
            self.maskSL = L("maskSL", [128, 128], F32)
            for (m, pat, cm, cmp_) in ((self.maskS, 1, -1, ALU.is_gt), (self.maskI, 1, -1, ALU.is_ge), (self.maskSL, -1, 1, ALU.is_gt)):
                p.op('pool', lambda e, m=m: e.memset(m[:], 1.0), [], ['masks'])
                p.op('pool', lambda e, m=m, pat=pat, cm=cm, cmp_=cmp_: e.affine_select(
                    out=m[:], in_=m[:], pattern=[[pat, 128]], compare_op=cmp_, fill=0.0, base=0, channel_multiplier=cm), ['masks'], ['masks'])
            self.blockones = L("blockones", [128, 128], F32)
            p.op('pool', lambda e: e.memset(self.blockones[:], 1.0), [], ['masks'])
            p.op('pool', lambda e: e.memset(self.blockones[0:64, 64:128], 0.0), ['masks'], ['masks'])
            p.op('pool', lambda e: e.memset(self.blockones[64:128, 0:64], 0.0), ['masks'], ['masks'])
            self.resetm = L("resetm", [128, TT], F32)
            p.op('pool', lambda e: e.memset(self.resetm[:], 1.0), [], ['masks'])
            p.op('pool', lambda e: e.memset(self.resetm[:].rearrange("p (n c) -> p n c", c=128)[:, :, 0:1], 0.0), ['masks'], ['masks'])
            p.op('pool', lambda e: e.memset(self.epsc[:, 1:2], GN_EPS), [], ['consts'])
            self.mu_fm = L("mu_fm", [128, 33], F32)
            self.omu_fm = L("omu_fm", [128, 33], F32)
            p.dma('sp', self.mu_fm[:], self.inputs['rwkv_mu_fm'], [], ['rw_vecs'])
            self.ts('dve', self.omu_fm[:], self.mu_fm[:], -1.0, 1.0, ALU.mult, ALU.add, ['rw_vecs'], ['rw_vecs'])
            self.vecs = L("rw_vecs", [128, 5, NKC], F32)
            p.dma('sp', self.vecs[:], self.inputs['rwkv_vecs'], [], ['rw_vecs'])
            self.lw2 = L("lw2", [128, D], BF16)
            self.lo_bf = L("lo_bf", [128, TT], BF16)
            self.gng_bc = L("gng_bc", [128, D], BF16)
            self.gnb_bc = L("gnb_bc", [128, D], BF16)
            self.Wva = L("Wva", [128, NKC, D], BF16)
            self.Wvb = L("Wvb", [128, NKC, D], BF16)
            src = self.w_in_dram['rwkv']

            def loader(tes):
                muv = tes.enter_context(self.nc.sbuf_tensor("muv_bc", [128, D], F32))
                p.dma('sp', muv[:], self.inputs['rwkv_lw2'], [], ['muv'])
                self.copy('dve', self.lw2[:], muv[:], ['muv'], ['lw2'])
                for (dst, nm) in ((self.gng_bc, 'rwkv_gn_g'), (self.gnb_bc, 'rwkv_gn_b')):
                    p.dma('sp', muv[:], self.inputs[nm].partition_broadcast(128), ['muv'], ['muv'])
                    self.copy('dve', dst[:], muv[:], ['muv'], ['bc_tiles'])
                p.dma('sp', muv[:], self.inputs['rwkv_mu'][2 * D:3 * D].partition_broadcast(128), ['muv'], ['bc_tiles', 'muv'])
                self.load_weight_bf16(self.Wvb, 'Wv', src, D, src_c0=2 * D, scale_bc=muv)
                self.ts('dve', muv[:], muv[:], -1.0, 1.0, ALU.mult, ALU.add, ['bc_tiles'], ['bc_tiles'])
                self.load_weight_bf16(self.Wva, 'Wv', src, D, src_c0=2 * D, scale_bc=muv)
                self.load_weight_bf16(self.W_in, 'W_in', src, 2 * D, src_c0=0, dst_c0=0)
                self.load_weight_bf16(self.W_in, 'W_in', src, 128, src_c0=3 * D, dst_c0=2 * D)
                self.load_weight_bf16(self.W_in, 'W_in', src, D, src_c0=3 * D + 128, dst_c0=2 * D + 128)
                self.load_weight_bf16(self.W_out, 'W_out', self.w_out_dram['rwkv'], D)
            self.S = L("S_rwkv", [128, NKC, 64], F32)
            p.op('pool', lambda e: e.memset(self.S[:], 0.0), [], [('S', j) for j in range(NKC)])
            self.pcar = L("pcar", [128, 25], F32)
            p.op('pool', lambda e: e.memset(self.pcar[:], 0.0), [], [('pcar', i) for i in range(25)])
            self.pm_ext = [L(f"pm_ext{i}", [128, TT + 1], F32) for i in range(2)]
            self.pm_i = 0
            names = ['r', 'k', 'tmp', 'sigw', 'a', 'kk', 'rn', 'kmod', 'bbv', 'c']
            self.tmp = {'lo': L("w_lo", [128, TT], F32)}
            self.tmpP = [{n: L(f"w_{n}0", [128, TT], F32) for n in names}, None]
            self.tmp2 = [dict(), dict()]
            for n in ['khat', 'bhat']:
                self.tmp2[0][n] = L(f"w_{n}0", [128, TT], F32)
            for n in ['rt_bf', 'bt_bf', 'at_bf', 'kt_h0', 'kt_h1', 'bt_h0', 'bt_h1', 'at_h0', 'at_h1']:
                self.tmp2[0][n] = L(f"w_{n}0", [128, TT], BF16)
            self.hm = L("hm", [128, 2], F32)
            p.op('pool', lambda e: e.memset(self.hm[:], 0.0), [], ['masks'])
            p.op('pool', lambda e: e.memset(self.hm[0:64, 0:1], 1.0), ['masks'], ['masks'])
            p.op('pool', lambda e: e.memset(self.hm[64:128, 1:2], 1.0), ['masks'], ['masks'])
            self.pt = {n: [L(f"wp_{n}{q}", [128, TT], F32 if n in ('at', 'rt') else BF16) for q in range(2)] for n in ['at', 'rt', 'rkr', 'sgate']}
            self.blockones_bf = L("blockones_bf", [128, 128], BF16)
            self.copy('dve', self.blockones_bf[:], self.blockones[:], ['masks'], ['masks'])
            self.v_bf = [L(f"v_bf{q}", [128, NB, 128], BF16) for q in range(2)]
            self.dec = [L(f"dec{q}", [128, NB], F32) for q in range(3)]

            def post_setup():
                self.tmpP[1] = {n: L(f"w_{n}1", [128, TT], F32) for n in names}
                for qq in range(2, self.NDEEP):
                    self.NT.append(L(f"NT{qq}", [128, NCHN, 128], NDT))
                    self.Aak.append(L(f"Aak{qq}", [128, NCHN, 128], BF16))
                    self.Ark.append(L(f"Ark{qq}", [128, NCHN, 128], BF16))
                    self.Arb.append(L(f"Arb{qq}", [128, NCHN, 128], BF16))
                    self.khm.append([L(f"khm{qq}{b}", [128, 128], BF16) for b in range(NB)])
                    self.bhm.append([L(f"bhm{qq}{b}", [128, 128], BF16) for b in range(NB)])
                self.PbP[1] = [L(f"Pb1{i}", [128, NCHN, 128], NDT) for i in range(2)]
                self.QbP[1] = [L(f"Qb1{i}", [128, NCHN, 128], NDT) for i in range(2)]
                for n in ['khat', 'bhat']:
                    self.tmp2[1][n] = L(f"w_{n}1", [128, TT], F32)
                for n in ['rt_bf', 'bt_bf', 'at_bf', 'kt_h0', 'kt_h1', 'bt_h0', 'bt_h1', 'at_h0', 'at_h1']:
                    self.tmp2[1][n] = L(f"w_{n}1", [128, TT], BF16)
                for n in ['rkr', 'sgate']:
                    self.pt[n].append(L(f"wp_{n}2", [128, TT], BF16))
                for n in ['at', 'rt']:
                    self.pt[n].append(self.pt[n][0])
                self.ysb = [L(f"ysb{i}", [128, 256], F32) for i in range(2)]
                self.yn2 = [self.yn, L("yn1", [128, 128], F32)]
                self.bon2 = [self.bon, L("bon1", [128, 128], F32)]
                self.gst2 = [self.gst, L("gst1", [128, 12], F32)]
                self.junk2 = [self.junk, L("junk1", [128, 64], F32)]
                self.bcount = 0
                self.v_bf.append(L("v_bf2", [128, NB, 128], BF16))
            self.post_setup = post_setup
            NCHN = NB * 2
            self.PbP = [[L(f"Pb0{i}", [128, NCHN, 128], NDT) for i in range(2)], None]
            self.QbP = [[L(f"Qb0{i}", [128, NCHN, 128], NDT) for i in range(2)], None]
            self.NT = [L(f"NT{q}", [128, NCHN, 128], NDT) for q in range(2)]
            self.Aak = [L(f"Aak{q}", [128, NCHN, 128], BF16) for q in range(2)]
            self.Ark = [L(f"Ark{q}", [128, NCHN, 128], BF16) for q in range(2)]
            self.Arb = [L(f"Arb{q}", [128, NCHN, 128], BF16) for q in range(2)]
            self.NDEEP = 4
            self.ident4 = L("ident4", [128, NCHN, 128], NDT)
            for c in range(NCHN):
                self.copy('dve', self.ident4[:, c, :], self.ident[:], ['ident'], ['ident'])
            self.khm = [[L(f"khm{q}{b}", [128, 128], BF16) for b in range(NB)] for q in range(2)]
            self.bhm = [[L(f"bhm{q}{b}", [128, 128], BF16) for b in range(NB)] for q in range(2)]
            self.Z_sb = L("Z_sb", [128, 128], NDT)
            self.U_bf = L("U_bf", [128, 128], BF16)
            self.yn = L("yn", [128, 128], F32)
            self.bon = L("bon", [128, 128], F32)
            self.gst = L("gst", [128, 12], F32)
            self.junk = L("junk", [128, 64], F32)
            self.ps_pool = [0, 1]
            return loader

        NMB = [[2, 3], [4, 7]]
        self.nm_rrs = [0, 0]

        def nm_ps(q=0):
            i = NMB[q][self.nm_rrs[q] % 2]
            self.nm_rrs[q] += 1
            return self.psums[i], f"ps{i}"

        def shift(ps, pk, dst, dk, idx, mt):
            pm = self.pm_ext[self.pm_i % 2]
            pmk = f"pm_ext{self.pm_i % 2}"
            self.pm_i += 1
            ck = ('pcar', idx)
            self.copy('pool', pm[:, 0:1], self.pcar[:, idx:idx + 1], [ck], [pmk])
            self.act(pm[:, 1:TT + 1], ps[:, :TT], AF.Copy, [pk, 'rw_vecs'], [pmk], scale=self.mu_fm[:, mt:mt + 1])
            self.copy('pool', self.pcar[:, idx:idx + 1], pm[:, TT:TT + 1], [pmk], [ck])
            self.act(dst, ps[:, :TT], AF.Copy, [pk, 'rw_vecs'], [dk], scale=self.omu_fm[:, mt:mt + 1])
            self.tt('dve', dst, dst, pm[:, 0:TT], ALU.add, [dk, pmk], [dk])

        def proj(col0):
            ps, pk = self.next_ps()
            for kc in range(NKC):
                self.mm(ps[:, :TT], self.W_in[:, kc, col0:col0 + 128], self.hn[:, kc, 1:TT + 1], kc == 0, kc == NKC - 1, ['W_in', self.hnk], [pk])
            return ps, pk

        hsl = [slice(0, 64), slice(64, 128)]

        def A_gen(j):
            V = self.vecs
            q = j % 2
            q3 = j % 3
            q4 = j % self.NDEEP
            t = dict(self.tmp)
            t.update(self.tmpP[q])
            t['e1'] = t['rn']
            t['e2'] = t['tmp']
            t.update(self.tmp2[q])
            self.Pb, self.Qb = self.PbP[q], self.QbP[q]
            PAR = set(self.tmp2[0].keys())
            jc = slice(j * 128, (j + 1) * 128)
            at, rt, rkr, sgate = self.pt['at'][q], self.pt['rt'][q], self.pt['rkr'][q3], self.pt['sgate'][q3]
            kat, krt, krkr, ksg = f'p_at{q}', f'p_rt{q}', f'p_rkr{q3}', f'p_sgate{q3}'
            v_bf, dec = self.v_bf[q3], self.dec[q3]
            kv, kvb, kdec = f'v_sb{q3}', f'v_bf{q3}', f'dec{q3}'
            ps, pk = proj(j * 128)
            shift(ps, pk, t['r'][:], f't_r{q}', j, j)
            ps, pk = proj(D + j * 128)
            shift(ps, pk, t['k'][:], f't_k{q}', 8 + j, 8 + j)
            yield
            ps, pk = proj(2 * D + 128 + j * 128)
            shift(ps, pk, t['tmp'][:], f't_tmp{q}', 16 + j, 25 + j)
            self.act(sgate[:], t['tmp'][:], AF.Silu, [f't_tmp{q}'], [ksg])
            pv, pvk = self.next_ps()
            for blk in range(NB):
                n = 0
                for kc in range(NKC):
                    for (Wv, off) in ((self.Wva, 1), (self.Wvb, 0)):
                        self.mm(pv[:, blk * 128:(blk + 1) * 128], self.hn[:, kc, off + blk * 128:off + (blk + 1) * 128], Wv[:, kc, jc],
                                n == 0, n == 2 * NKC - 1, ['Wv', self.hnk], [pvk])
                        n += 1
            self.copy('dve', v_bf[:].rearrange("p b v -> p (b v)"), pv[:, :TT], [pvk], [kvb])
            yield
            pw, pwk = self.next_ps()
            self.mm(pw[:, :TT], self.lw2[0:64, jc], self.lo_bf[0:64, :], True, True, ['lw2', 't_lo_bf'], [pwk])
            self.act(t['sigw'][:], pw[:, :TT], AF.Sigmoid, [pwk, 'rw_vecs'], [f't_sigw{q}'], bias=V[:, 0, j:j + 1])
            pa, pak = self.next_ps()
            self.mm(pa[:, :TT], self.lw2[64:128, jc], self.lo_bf[64:128, :], True, True, ['lw2', 't_lo_bf'], [pak])
            self.act(t['a'][:], pa[:, :TT], AF.Sigmoid, [pak, 'rw_vecs'], [f't_a{q}'], bias=V[:, 1, j:j + 1])
            self.ts('dve', t['kk'][:], t['k'][:], V[:, 2, j:j + 1], None, ALU.mult, None, [f't_k{q}', 'rw_vecs'], [f't_kk{q}'])
            self.tt('pool', t['tmp'][:], t['kk'][:], t['kk'][:], ALU.mult, [f't_kk{q}'], [f't_tmp{q}'])
            pn, pnk = self.next_ps()
            self.mm(pn[:, :TT], self.blockones[:], t['tmp'][:], True, True, ['masks', f't_tmp{q}'], [pnk])
            self.ts('dve', t['rn'][:], pn[:, :TT], 1e-24, None, ALU.max, None, [pnk], [f't_rn{q}'])
            self.act(t['rn'][:], t['rn'][:], AF.Ln, [f't_rn{q}'], [f't_rn{q}'])
            self.act(t['rn'][:], t['rn'][:], AF.Exp, [f't_rn{q}'], [f't_rn{q}'], scale=-0.5)
            self.tt('pool', t['kk'][:], t['kk'][:], t['rn'][:], ALU.mult, [f't_kk{q}', f't_rn{q}'], [f't_kk{q}'])
            self.ts('dve', t['tmp'][:], t['a'][:], -1.0, V[:, 3, j:j + 1], ALU.add, ALU.mult, [f't_a{q}', 'rw_vecs'], [f't_tmp{q}'])
            self.stt('dve', t['kmod'][:], t['tmp'][:], 1.0, t['k'][:], ALU.add, ALU.mult, [f't_tmp{q}', f't_k{q}'], [f't_kmod{q}'])
            self.tt('pool', t['bbv'][:], t['kk'][:], t['a'][:], ALU.mult, [f't_kk{q}', f't_a{q}'], [f't_bbv{q}'])
            yield
            self.p.op('dve', lambda e: e.tensor_tensor_scan(out=t['c'][:], data0=self.resetm[:], data1=t['sigw'][:], initial=0.0,
                                                            op0=ALU.mult, op1=ALU.add), [f't_sigw{q}', 'masks'], [f't_c{q}'])
            self.act(t['e1'][:], t['c'][:], AF.Exp, [f't_c{q}'], [f't_rn{q}'], scale=LC)
            self.tt('pool', rt[:], t['r'][:], t['e1'][:], ALU.mult, [f't_r{q}', f't_rn{q}'], [krt])
            self.copy('act', t['rt_bf'][:], rt[:], [krt], [f't_rt_bf{q}'])
            self.act(t['e2'][:], t['c'][:], AF.Exp, [f't_c{q}'], [f't_tmp{q}'], scale=-LC)
            for hd in range(2):
                self.stt('dve', t[f'kt_h{hd}'][:], t['kmod'][:], self.hm[:, hd:hd + 1], t['e2'][:], ALU.mult, ALU.mult,
                         [f't_kmod{q}', f't_tmp{q}', 'masks'], [f't_kt_h{hd}_{q}'])
                self.stt('dve', t[f'bt_h{hd}'][:], t['bbv'][:], self.hm[:, hd:hd + 1], t['e2'][:], ALU.mult, ALU.mult,
                         [f't_bbv{q}', f't_tmp{q}', 'masks'], [f't_bt_h{hd}_{q}'])
            self.tt('pool', t['bt_bf'][:], t['bbv'][:], t['e2'][:], ALU.mult, [f't_bbv{q}', f't_tmp{q}'], [f't_bt_bf{q}'])
            self.tt('pool', t['e1'][:], t['c'][:], t['sigw'][:], ALU.subtract, [f't_c{q}', f't_sigw{q}'], [f't_rn{q}'])
            self.act(t['e1'][:], t['e1'][:], AF.Exp, [f't_rn{q}'], [f't_rn{q}'], scale=LC)
            self.stt('dve', at[:], t['kk'][:], -1.0, t['e1'][:], ALU.mult, ALU.mult, [f't_kk{q}', f't_rn{q}'], [kat])
            self.copy('act', t['at_bf'][:], at[:], [kat], [f't_at_bf{q}'])
            for hd in range(2):
                self.act(t[f'at_h{hd}'][:], at[:], AF.Copy, [kat, 'masks'], [f't_at_h{hd}_{q}'], scale=self.hm[:, hd:hd + 1])
            yield
            c3 = t['c'][:].rearrange("p (n c) -> p n c", c=128)
            self.tt('pool', t['e2'][:].rearrange("p (n c) -> p n c", c=128), c3[:, :, 127:128].to_broadcast([128, NB, 128]), c3,
                    ALU.subtract, [f't_c{q}'], [f't_tmp{q}'])
            self.act(t['e2'][:], t['e2'][:], AF.Exp, [f't_tmp{q}'], [f't_tmp{q}'], scale=LC)
            self.act(dec[:], c3[:, :, 127], AF.Exp, [f't_c{q}'], [kdec], scale=LC)
            self.tt('pool', t['khat'][:], t['kmod'][:], t['e2'][:], ALU.mult, [f't_kmod{q}', f't_tmp{q}'], [f't_khat{q}'])
            self.tt('dve', t['bhat'][:], t['bbv'][:], t['e2'][:], ALU.mult, [f't_bbv{q}', f't_tmp{q}'], [f't_bhat{q}'])
            self.stt('dve', rkr[:], t['r'][:], V[:, 4, j:j + 1], t['kmod'][:], ALU.mult, ALU.mult, [f't_r{q}', 'rw_vecs', f't_kmod{q}'], [krkr])
            yield
            NCH = NB * 2
            specs = {'P': ('bt_h', 'at_bf', self.maskS, self.Pb[0], f'Pb{q}0'),
                     'Q': ('at_h', 'bt_bf', self.maskSL, self.Qb[0], f'Qb{q}0'),
                     'ak': ('kt_h', 'at_bf', self.maskS, self.Aak[q4], f'Aak{q4}'),
                     'rk': ('kt_h', 'rt_bf', self.maskI, self.Ark[q4], f'Ark{q4}'),
                     'rb': ('bt_h', 'rt_bf', self.maskI, self.Arb[q4], f'Arb{q4}')}
            for name in ('P', 'Q', 'ak', 'rk', 'rb'):
                lh, rh, mask, dst, dk = specs[name]
                ps, pk = nm_ps(q)
                for c in range(NCH):
                    blk, hd = c // 2, c % 2
                    cs = slice(blk * 128, (blk + 1) * 128)
                    self.mm(ps[:, c * 128:(c + 1) * 128], t[f'{lh}{hd}'][:, cs], t[rh][:, cs], True, True, [f't_{lh}{hd}_{q}', f't_{rh}{q}'], [pk])
                self.tt('dve', dst[:], ps[:, 0:NCH * 128].rearrange("p (c t) -> p c t", c=NCH),
                        mask[:, None, :].to_broadcast([128, NCH, 128]), ALU.mult, [pk, 'masks'], [dk])
                if name == 'Q':
                    self.tt('pool', self.NT[q4][:], self.ident4[:], self.Pb[0][:], ALU.add, ['ident', f'Pb{q}0'], [f'NT{q4}'])
                    yield
            for blk in range(NB):
                cs = slice(blk * 128, (blk + 1) * 128)
                for (srcn, dst, dk) in (('khat', self.khm[q4][blk], f'khm{q4}{blk}'), ('bhat', self.bhm[q4][blk], f'bhm{q4}{blk}')):
                    ps, pk = nm_ps(q)
                    self.p.op('pe', lambda e, ps=ps, srcn=srcn, cs=cs: e.transpose(ps[:, 0:128], t[srcn][:, cs], self.ident[:]),
                              [f't_{srcn}{q}', 'ident'], [pk])
                    self.copy('act', dst[:], ps[:, 0:128], [pk], [dk])
            yield
            NTq, kNT = self.NT[q4], f'NT{q4}'
            for i in range(6):
                a_, b_ = i % 2, (i + 1) % 2
                Pa, Qa, Pn, Qn = self.Pb[a_], self.Qb[a_], self.Pb[b_], self.Qb[b_]
                kPa, kQa, kPn, kQn = f'Pb{q}{a_}', f'Qb{q}{a_}', f'Pb{q}{b_}', f'Qb{q}{b_}'
                if i < 5:
                    ps, pk = nm_ps(q)
                    for c in range(NCH):
                        self.mm(ps[:, c * 128:(c + 1) * 128], Qa[:, c, :], Pa[:, c, :], True, True, [kQa, kPa], [pk])
                    self.copy('act', Pn[:].rearrange("p c t -> p (c t)"), ps[:, 0:NCH * 128], [pk], [kPn])
                ps, pk = nm_ps(q)
                for c in range(NCH):
                    self.mm(ps[:, c * 128:(c + 1) * 128], Pa[:, c, :], Qa[:, c, :], True, True, [kPa, kQa], [pk])
                self.copy('dve' if i % 2 == 0 else 'act', Qn[:].rearrange("p c t -> p (c t)"), ps[:, 0:NCH * 128], [pk], [kQn])
                yield
                ps, pk = nm_ps(q)
                for c in range(NCH):
                    self.mm(ps[:, c * 128:(c + 1) * 128], Qn[:, c, :], NTq[:, c, :], True, True, [kQn, kNT], [pk])
                self.tt('dve', NTq[:].rearrange("p c t -> p (c t)"), NTq[:].rearrange("p c t -> p (c t)"), ps[:, 0:NCH * 128], ALU.add,
                        [kNT, pk], [kNT])
                yield

        def B_gen(j):
            t = self.tmp
            P = self.psums
            q = j % 2
            q3 = j % 3
            q4 = j % self.NDEEP
            jc = slice(j * 128, (j + 1) * 128)
            at, rt, rkr, sgate = self.pt['at'][q], self.pt['rt'][q], self.pt['rkr'][q3], self.pt['sgate'][q3]
            kat, krt, krkr, ksg = f'p_at{q}', f'p_rt{q}', f'p_rkr{q3}', f'p_sgate{q3}'
            v_bf, dec = self.v_bf[q3], self.dec[q3]
            kv, kvb, kdec = f'v_sb{q3}', f'v_bf{q3}', f'dec{q3}'
            kS = ('S', j)
            for blk in range(NB):
                cs = slice(blk * 128, (blk + 1) * 128)
                khm, bhm = self.khm[q4][blk], self.bhm[q4][blk]
                kkh, kbh = f'khm{q4}{blk}', f'bhm{q4}{blk}'
                pz, pzk = P[5], 'ps5'
                for hd in range(2):
                    hs, hc, c = hsl[hd], slice(hd * 64, (hd + 1) * 64), blk * 2 + hd
                    self.mm(pz[:, hc], self.Aak[q4][:, c, :], v_bf[:, blk, hc], True, False, [f'Aak{q4}', kvb], [pzk])
                    self.mm(pz[:, hc], at[hs, cs], self.S[hs, j, :], False, True, [kat, kS], [pzk])
                self.copy('act', self.Z_sb[:], pz[:, 0:128], [pzk], ['Z_sb'])
                yield
                for hd in range(2):
                    hc, c = slice(hd * 64, (hd + 1) * 64), blk * 2 + hd
                    self.mm(pz[:, hc], self.NT[q4][:, c, :], self.Z_sb[:, hc], True, True, [f'NT{q4}', 'Z_sb'], [pzk])
                self.copy('act', self.U_bf[:], pz[:, 0:128], [pzk], ['U_bf'])
                yield
                self.mm(pz[:, 0:128], khm[:], v_bf[:, blk, :], True, False, [kkh, kvb], [pzk])
                self.mm(pz[:, 0:128], bhm[:], self.U_bf[:], False, True, [kbh, 'U_bf'], [pzk])
                py, pyk = P[6], 'ps6'
                for hd in range(2):
                    hs, hc, c = hsl[hd], slice(hd * 64, (hd + 1) * 64), blk * 2 + hd
                    yc = slice(hd * 128, hd * 128 + 64)
                    bc_ = slice(hd * 128 + 64, hd * 128 + 128)
                    self.mm(py[:, yc], self.Ark[q4][:, c, :], v_bf[:, blk, hc], True, False, [f'Ark{q4}', kvb], [pyk])
                    self.mm(py[:, yc], rt[hs, cs], self.S[hs, j, :], False, False, [krt, kS], [pyk])
                    self.mm(py[:, yc], self.Arb[q4][:, c, :], self.U_bf[:, hc], False, True, [f'Arb{q4}', 'U_bf'], [pyk])
                    self.mm(py[:, bc_], rkr[hs, cs], self.blockones_bf[hs, hs], True, True, [krkr, 'masks'], [pyk])
                for hd in range(2):
                    hs, hc = hsl[hd], slice(hd * 64, (hd + 1) * 64)
                    self.stt('dve', self.S[hs, j, :], self.S[hs, j, :], dec[hs, blk:blk + 1], pz[hs, hc], ALU.mult, ALU.add,
                             [kS, kdec, pzk], [kS])
                yield
                bp = self.bcount % 2
                self.bcount += 1
                ysb, kys = self.ysb[bp], f'ysb{bp}'
                g, kg = self.gst2[bp], f'gst{bp}'
                yn, kyn = self.yn2[bp], f'yn{bp}'
                bon, kbon = self.bon2[bp], f'bon{bp}'
                junk, kjunk = self.junk2[bp], f'junk{bp}'
                self.copy('act', ysb[:], py[:, 0:256], [pyk], [kys])
                for hd in range(2):
                    hc = slice(hd * 64, (hd + 1) * 64)
                    yc = slice(hd * 128, hd * 128 + 64)
                    bc_ = slice(hd * 128 + 64, hd * 128 + 128)
                    self.act(junk[:], ysb[:, yc], AF.Identity, [kys], [kjunk, kg], accum_out=g[:, hd:hd + 1])
                    self.act(junk[:], ysb[:, yc], AF.Square, [kys], [kjunk, kg], accum_out=g[:, 2 + hd:3 + hd])
                    self.tt('pool', bon[:, hc], ysb[:, bc_], v_bf[:, blk, hc], ALU.mult, [kys, kvb], [kbon])
                self.ts('dve', g[:, 4:6], g[:, 0:2], 1.0 / 64, None, ALU.mult, None, [kg], [kg])
                self.tt('dve', g[:, 6:8], g[:, 4:6], g[:, 4:6], ALU.mult, [kg], [kg])
                self.stt('dve', g[:, 8:10], g[:, 2:4], 1.0 / 64, g[:, 6:8], ALU.mult, ALU.subtract, [kg], [kg])
                self.act(g[:, 8:10], g[:, 8:10], AF.Ln, [kg, 'consts'], [kg], bias=self.epsc[:, 1:2])
                self.act(g[:, 8:10], g[:, 8:10], AF.Exp, [kg], [kg], scale=-0.5)
                for hd in range(2):
                    hc = slice(hd * 64, (hd + 1) * 64)
                    yc = slice(hd * 128, hd * 128 + 64)
                    self.ts('dve', yn[:, hc], ysb[:, yc], g[:, 4 + hd:5 + hd], g[:, 8 + hd:9 + hd], ALU.subtract, ALU.mult,
                            [kys, kg], [kyn])
                yield
                self.tt('pool', yn[:], yn[:], self.gng_bc[:, jc], ALU.mult, [kyn, 'bc_tiles'], [kyn])
                self.tt('pool', yn[:], yn[:], self.gnb_bc[:, jc], ALU.add, [kyn, 'bc_tiles'], [kyn])
                self.tt('pool', yn[:], yn[:], bon[:], ALU.add, [kyn, kbon], [kyn])
                ps, pk = nm_ps(q)
                self.p.op('pe', lambda e, ps=ps, yn=yn: e.transpose(ps[:, 0:128], yn[:], self.ident[:]), [kyn, 'ident'], [pk])
                self.tt('dve', self.yTt[:, j, cs], ps[:, 0:128], sgate[:, cs], ALU.mult, [pk, ksg], [self.yTk])
                yield

        def drive(gens):
            gens = [g for g in gens if g is not None]
            while gens:
                for g in list(gens):
                    try:
                        next(g)
                    except StopIteration:
                        gens.remove(g)

        def tile(ti):
            t = self.tmp
            ps, pk = proj(2 * D)
            shift(ps, pk, t['lo'][:], 't_lo', 24, 24)
            self.act(t['lo'][0:64, :], t['lo'][0:64, :], AF.Tanh, ['t_lo'], ['t_lo'])
            self.copy('dve', self.lo_bf[:], t['lo'][:], ['t_lo'], ['t_lo_bf'])
            for j in range(NKC):
                drive([A_gen(j)])
                drive([B_gen(j)])

        self.run_layer(li, 'rwkv', TT, WC, setup, tile, is_last)
        self.ps_pool = list(range(8))

    def build(self):
        nc = self.nc
        T = self.T
        self.xT = self.din("xT", [D, T])
        self.yT = nc.dram_tensor("yT", [D, T], F32, kind="ExternalOutput").ap()
        d_norm_g = self.din("norm_g", [128, 4, NKC])
        d_final_g = self.din("final_g", [128, NKC])
        self.w_in_dram, self.w_out_dram = {}, {}
        kinds = [k for (_, k) in self.layers]
        if 'conv' in kinds:
            self.w_in_dram['conv'] = self.din("conv_w_in", [D, 4 * D])
            self.w_out_dram['conv'] = self.din("conv_w_out", [D, D])
            d_conv_w = self.din("conv_w", [128, NKC, 3])
        if 'rwkv' in kinds:
            self.w_in_dram['rwkv'] = self.din("rwkv_w_in", [D, 4 * D + 128])
            self.w_out_dram['rwkv'] = self.din("rwkv_w_out", [D, D])
            self.din("rwkv_mu_fm", [128, 33])
            self.din("rwkv_mu", [4 * D + 128])
            self.din("rwkv_vecs", [128, 5, NKC])
            self.din("rwkv_lw2", [128, D])
            self.din("rwkv_gn_g", [D])
            self.din("rwkv_gn_b", [D])
        if 'hgrn' in kinds:
            self.w_in_dram['hgrn'] = self.din("hgrn_w_in", [D, 4 * D])
            self.w_out_dram['hgrn'] = self.din("hgrn_w_out", [D, D])
            self.din("hgrn_gn_g", [D])
            self.din("hgrn_lbl", [128, 4, NKC])
        if 'gmlp' in kinds:
            self.w_in_dram['gmlp'] = self.din("gmlp_w_in", [D, 3 * D])
            self.w_out_dram['gmlp'] = self.din("gmlp_w_out", [D, D])
            self.din("gmlp_wsT", [128, 8, 128])
            self.din("gmlp_bs", [8, 128])
            self.din("gmlp_vg", [D])
        with ExitStack() as es:
            self.es = es
            nc.allow_low_precision("bf16 matmul operands, fp32 accumulation")
            self.p = p = Prog(nc, es)
            self.psums = [es.enter_context(nc.psum_tensor(f"ps{i}", [128, 512], F32)) for i in range(8)]
            self.ps_rr = 0
            self.ps_pool = list(range(8))
            self.ones_bf = self.sb("ones_bf", [128, 128], BF16)
            self.epsc = self.sb("epsc", [128, 4], F32)
            self.norm_g = self.sb("norm_g_sb", [128, 4, NKC], F32)
            self.final_g = self.sb("final_g_sb", [128, NKC], F32)
            p.op('pool', lambda e: e.memset(self.ones_bf[:], 1.0), [], ['ones_bf'])
            p.op('pool', lambda e: e.memset(self.epsc[:, 0:1], RMS_EPS), [], ['consts'])
            p.op('pool', lambda e: e.memset(self.epsc[:, 2:3], 1.0), ['consts'], ['consts'])
            p.dma('sp', self.norm_g[:], d_norm_g, [], ['consts'])
            p.dma('sp', self.final_g[:], d_final_g, [], ['consts'])
            if 'conv' in kinds:
                self.conv_w = self.sb("conv_w_sb", [128, NKC, 3], F32)
                p.dma('sp', self.conv_w[:], d_conv_w, [], ['consts'])
            self.first_layer = True
            for n, (li, kind) in enumerate(self.layers):
                is_last = n == len(self.layers) - 1
                if kind == 'conv':
                    self.conv_layer(li, is_last)
                elif kind == 'gmlp':
                    self.gmlp_layer(li, is_last)
                elif kind == 'hgrn':
                    self.hgrn_layer(li, is_last)
                elif kind == 'rwkv':
                    self.rwkv_layer(li, is_last)
                else:
                    raise ValueError(kind)
            p.finish('sp')
            self.stats = (p.n_ins, p.n_wait)
        return nc


def prep_inputs(inp, b, layers):
    f = np.float32
    m = {}
    m["xT"] = np.ascontiguousarray(np.asarray(inp["x"][b], f).T)
    m["norm_g"] = np.ascontiguousarray(np.asarray(inp["norm_g"], f).reshape(4, NKC, 128).transpose(2, 0, 1))
    m["final_g"] = np.ascontiguousarray(np.asarray(inp["final_g"], f).reshape(NKC, 128).T)
    kinds = [k for (_, k) in layers]
    if 'conv' in kinds:
        m["conv_w_in"] = np.ascontiguousarray(np.asarray(inp["conv_w_in"][0], f))
        m["conv_w_out"] = np.ascontiguousarray(np.asarray(inp["conv_w_out"][0], f))
        m["conv_w"] = np.ascontiguousarray(np.asarray(inp["conv_w"][0], f).reshape(3, NKC, 128).transpose(2, 1, 0))
    if 'rwkv' in kinds:
        m["rwkv_w_in"] = np.ascontiguousarray(np.asarray(inp["rwkv_w_in"][0], f))
        m["rwkv_w_out"] = np.ascontiguousarray(np.asarray(inp["rwkv_w_out"][0], f))
        mu = np.asarray(inp["rwkv_mu"][0], f)
        m["rwkv_mu"] = np.ascontiguousarray(mu)
        m["rwkv_mu_fm"] = np.ascontiguousarray(mu.reshape(33, 128).T)
        vecs = np.stack([np.asarray(inp[k][0], f).reshape(NKC, 128) for k in
                         ("rwkv_w0", "rwkv_a0", "rwkv_k_k", "rwkv_k_a", "rwkv_r_k")], axis=0)
        m["rwkv_vecs"] = np.ascontiguousarray(vecs.transpose(2, 0, 1))
        m["rwkv_lw2"] = np.ascontiguousarray(np.concatenate([np.asarray(inp["rwkv_w_w2"][0], f), np.asarray(inp["rwkv_w_a2"][0], f)], axis=0))
        m["rwkv_gn_g"] = np.ascontiguousarray(np.asarray(inp["rwkv_gn_g"][0], f))
        m["rwkv_gn_b"] = np.ascontiguousarray(np.asarray(inp["rwkv_gn_b"][0], f))
    if 'hgrn' in kinds:
        m["hgrn_w_in"] = np.ascontiguousarray(np.asarray(inp["hgrn_w_in"][0], f))
        m["hgrn_w_out"] = np.ascontiguousarray(np.asarray(inp["hgrn_w_out"][0], f))
        m["hgrn_gn_g"] = np.ascontiguousarray(np.asarray(inp["hgrn_gn_g"][0], f))
        m["hgrn_lbl"] = np.ascontiguousarray(np.asarray(inp["hgrn_lb_logits"], f).reshape(4, NKC, 128).transpose(2, 0, 1))
    if 'gmlp' in kinds:
        m["gmlp_w_in"] = np.ascontiguousarray(np.asarray(inp["gmlp_w_in"][0], f))
        m["gmlp_w_out"] = np.ascontiguousarray(np.asarray(inp["gmlp_w_out"][0], f))
        m["gmlp_wsT"] = np.ascontiguousarray(np.asarray(inp["gmlp_w_s"][0], f).transpose(2, 0, 1))
        m["gmlp_bs"] = np.ascontiguousarray(np.asarray(inp["gmlp_b_s"][0], f))
        m["gmlp_vg"] = np.ascontiguousarray(np.asarray(inp["gmlp_v_g"][0], f))
    return m


FULL_LAYERS = [(0, 'rwkv'), (1, 'hgrn'), (2, 'conv'), (3, 'gmlp')]


def kernel(**inputs):
    x = np.asarray(inputs["x"])
    B, T, _ = x.shape
    layers = FULL_LAYERS
    bld = Builder(T, layers)
    nc = bld.build()
    in_maps = []
    zeros = None
    for c in range(8):
        if c % 2 == 0:
            in_maps.append(prep_inputs(inputs, c // 2, layers))
        else:
            if zeros is None:
                zeros = {k: np.zeros_like(v) for k, v in in_maps[0].items()}
            in_maps.append(zeros)
    res = run_bass_kernel_spmd(nc, in_maps, core_ids=list(range(8)))
    out = np.stack([np.asarray(res.results[2 * b]["yT"]).T for b in range(B)], axis=0)
    return out.astype(np.float32)
```

```python
import numpy as np
from contextlib import ExitStack
import concourse.bass as bass
import concourse.mybir as mybir
from concourse.bass_utils import run_bass_kernel_spmd

F32 = mybir.dt.float32
BF16 = mybir.dt.bfloat16
ALU = mybir.AluOpType
AF = mybir.ActivationFunctionType
AX = mybir.AxisListType

D = 1024
NKC = 8
RMS_EPS = 1e-6
GN_EPS = 64e-5


class Prog:
    LIMIT = 30000

    def __init__(self, nc, es, n_dma_sems=24):
        self.nc = nc
        self.es = es
        self.engs = {'pe': nc.tensor, 'act': nc.scalar, 'dve': nc.vector,
                     'pool': nc.gpsimd, 'sp': nc.sync}
        self.sems = {}
        self.epoch = {k: 0 for k in self.engs}
        self.cnt = {k: 0 for k in self.engs}
        for k in self.engs:
            self.sems[(k, 0)] = es.enter_context(nc.semaphore(f"s_{k}_0"))
        self.dma_sems = []
        for i in range(n_dma_sems):
            key = ('dma', i)
            self.sems[key] = es.enter_context(nc.semaphore(f"s_dma_{i}"))
            self.cnt[key] = 0
            self.dma_sems.append(key)
        self.dma_rr = 0
        self.waited = {k: {} for k in self.engs}
        self.bufs = {}
        self.n_wait = 0
        self.n_ins = 0

    def _deps(self, reads, writes):
        deps = set()
        for k in reads:
            b = self.bufs.get(k)
            if b and b['w']:
                deps.add(b['w'])
        for k in writes:
            b = self.bufs.get(k)
            if b:
                if b['w']:
                    deps.add(b['w'])
                deps.update(b['r'])
        return deps

    def _wait(self, eng, deps):
        e = self.engs[eng]
        best = {}
        for (sk, v) in deps:
            if sk[0] == eng and eng == 'pe':
                continue
            if best.get(sk, 0) < v:
                best[sk] = v
        for sk, v in best.items():
            if self.waited[eng].get(sk, 0) >= v:
                continue
            e.wait_ge(self.sems[sk], v)
            self.waited[eng][sk] = v
            self.n_wait += 1

    def _record(self, tok, reads, writes):
        for k in reads:
            b = self.bufs.setdefault(k, {'w': None, 'r': []})
            b['r'].append(tok)
            if len(b['r']) > 64:
                best = {}
                for (sk, v) in b['r']:
                    if best.get(sk, 0) < v:
                        best[sk] = v
                b['r'] = list(best.items())
        for k in writes:
            b = self.bufs.setdefault(k, {'w': None, 'r': []})
            b['w'] = tok
            b['r'] = []

    @staticmethod
    def _excl(reads, writes):
        ps = [k for k in reads if isinstance(k, str) and k.startswith('ps')]
        if ps:
            reads = [k for k in reads if k not in ps]
            writes = list(writes) + ps
        return reads, writes

    disabled = False
    recording = None
    SYNC_LAT = 0.45

    def begin_record(self):
        self.recording = []

    def flush(self):
        rec = self.recording
        self.recording = None
        if not rec:
            return
        n = len(rec)
        preds = [None] * n
        succs = [[] for _ in range(n)]
        last_w = {}
        readers = {}
        for i, (kind, eng, fn, reads, writes, cost, lat) in enumerate(rec):
            ps = set()
            for k in reads:
                w = last_w.get(k)
                if w is not None:
                    ps.add(w)
            for k in writes:
                w = last_w.get(k)
                if w is not None:
                    ps.add(w)
                ps.update(readers.get(k, ()))
            ps.discard(i)
            preds[i] = ps
            for pi in ps:
                succs[pi].append(i)
            for k in reads:
                readers.setdefault(k, []).append(i)
            for k in writes:
                last_w[k] = i
                readers[k] = []
        npred = [len(p_) for p_ in preds]
        ready = [i for i in range(n) if npred[i] == 0]
        eng_free = {}
        end_t = [0.0] * n
        done_t = [0.0] * n
        order = []
        blevel = [0.0] * n
        for i in range(n - 1, -1, -1):
            kind, eng, fn, reads, writes, cost, lat = rec[i]
            b = 0.0
            for si in succs[i]:
                v = blevel[si] + (self.SYNC_LAT if rec[si][1] != eng else 0.0)
                if v > b:
                    b = v
            blevel[i] = b + cost + lat

        def est(i):
            kind, eng, fn, reads, writes, cost, lat = rec[i]
            t = eng_free.get(eng, 0.0)
            for pi in preds[i]:
                tp = done_t[pi] + (self.SYNC_LAT if rec[pi][1] != eng else 0.0)
                if tp > t:
                    t = tp
            return t
        EPS = 0.1
        while ready:
            ests = [(est(i), i) for i in ready]
            tmin = min(ests)[0]
            best = None
            for (t, i) in ests:
                if t <= tmin + EPS:
                    if best is None or blevel[i] > blevel[best[1]] or (blevel[i] == blevel[best[1]] and i < best[1]):
                        best = (t, i)
            t1, i = best
            ready.remove(i)
            kind, eng, fn, reads, writes, cost, lat = rec[i]
            end_t[i] = t1 + cost
            done_t[i] = t1 + cost + lat
            eng_free[eng] = end_t[i]
            order.append(i)
            for si in succs[i]:
                npred[si] -= 1
                if npred[si] == 0:
                    ready.append(si)
        assert len(order) == n, (len(order), n)
        self.sched_span = max(done_t) if done_t else 0.0
        for i in order:
            kind, eng, fn, reads, writes, cost, lat = rec[i]
            if kind == 'op':
                self.op(eng, fn, reads, writes)
            else:
                out, in_, kw = fn
                self.dma(eng, out, in_, reads, writes, **kw)

    def op(self, eng, fn, reads=(), writes=(), cost=None):
        if self.disabled:
            return None
        if self.recording is not None:
            reads, writes = self._excl(reads, writes)
            if cost is None:
                cost = {'pe': 0.2, 'act': 0.45, 'dve': 0.45, 'pool': 0.7, 'sp': 0.1}[eng]
            self.recording.append(('op', eng, fn, list(reads), list(writes), cost, 0.0))
            return None
        reads, writes = self._excl(reads, writes)
        deps = self._deps(reads, writes)
        self._wait(eng, deps)
        ins = fn(self.engs[eng])
        if self.cnt[eng] >= self.LIMIT:
            self.epoch[eng] += 1
            ep = self.epoch[eng]
            self.sems[(eng, ep)] = self.es.enter_context(self.nc.semaphore(f"s_{eng}_{ep}"))
            self.cnt[eng] = 0
        sk = (eng, self.epoch[eng])
        self.cnt[eng] += 1
        ins.then_inc(self.sems[sk], 1)
        self._record((sk, self.cnt[eng]), reads, writes)
        self.n_ins += 1
        return ins

    def dma(self, eng, out, in_, reads=(), writes=(), **kw):
        if self.disabled:
            return None
        if self.recording is not None:
            self.recording.append(('dma', eng, (out, in_, kw), list(reads), list(writes), 0.15, 6.0))
            return None
        deps = self._deps(reads, writes)
        sk = self.dma_sems[self.dma_rr]
        self.dma_rr = (self.dma_rr + 1) % len(self.dma_sems)
        if self.cnt[sk] > 0:
            deps.add((sk, self.cnt[sk]))
        self._wait(eng, deps)
        ins = self.engs[eng].dma_start(out=out, in_=in_, **kw)
        self.cnt[sk] += 16
        ins.then_inc(self.sems[sk], 16)
        self._record((sk, self.cnt[sk]), reads, writes)
        self.n_ins += 1
        return ins

    def all_tokens(self):
        deps = set()
        for k, b in self.bufs.items():
            if b['w']:
                deps.add(b['w'])
            deps.update(b['r'])
        return deps

    def barrier(self):
        deps = self.all_tokens()
        for eng in self.engs:
            d = set(x for x in deps)
            self._wait(eng, d)

    def finish(self, eng='sp'):
        self._wait(eng, self.all_tokens())


class Builder:
    def __init__(self, T, layers, do_final=True, neu_dt=None):
        self.neu_dt = neu_dt if neu_dt is not None else BF16
        self.use_sched = True
        self.T = T
        self.layers = layers
        self.do_final = do_final
        self.nc = bass.Bass("TRN2", target_bir_lowering=False)
        self.inputs = {}

    def din(self, name, shape):
        t = self.nc.dram_tensor(name, list(shape), F32, kind="ExternalInput").ap()
        self.inputs[name] = t
        return t

    def sb(self, name, shape, dt=F32):
        return self.es.enter_context(self.nc.sbuf_tensor(name, list(shape), dt))

    def lsb(self, name, shape, dt=F32):
        return self.les.enter_context(self.nc.sbuf_tensor(f"{name}_{self.lname}", list(shape), dt))

    def next_ps(self):
        pool = self.ps_pool
        i = pool[self.ps_rr % len(pool)]
        self.ps_rr += 1
        return self.psums[i], f"ps{i}"

    @staticmethod
    def ecost(eng, ap):
        try:
            n = ap.free_size()
        except Exception:
            n = 256
        if eng == 'act':
            return 0.22 + n * 0.00075
        if eng == 'dve':
            return 0.2 + n * 0.00095
        if eng == 'pool':
            return 0.2 + n * 0.0021
        return 0.2

    def tt(self, eng, out, in0, in1, op, reads, writes):
        return self.p.op(eng, lambda e: e.tensor_tensor(out=out, in0=in0, in1=in1, op=op), reads, writes, cost=self.ecost(eng, out))

    def ts(self, eng, out, in0, s1, s2, op0, op1, reads, writes):
        if s2 is None:
            return self.p.op(eng, lambda e: e.tensor_scalar(out=out, in0=in0, scalar1=s1, scalar2=None, op0=op0), reads, writes, cost=self.ecost(eng, out))
        return self.p.op(eng, lambda e: e.tensor_scalar(out=out, in0=in0, scalar1=s1, scalar2=s2, op0=op0, op1=op1), reads, writes, cost=self.ecost(eng, out))

    def stt(self, eng, out, in0, scalar, in1, op0, op1, reads, writes):
        eng = 'dve'
        return self.p.op(eng, lambda e: e.scalar_tensor_tensor(out=out, in0=in0, scalar=scalar, in1=in1, op0=op0, op1=op1), reads, writes, cost=self.ecost(eng, out))

    def act(self, out, in_, func, reads, writes, bias=None, scale=1.0, accum_out=None):
        kw = {}
        if bias is not None:
            kw['bias'] = bias
        if accum_out is not None:
            kw['accum_out'] = accum_out
        return self.p.op('act', lambda e: e.activation(out=out, in_=in_, func=func, scale=scale, **kw), reads, writes, cost=self.ecost('act', in_))

    def mm(self, out, lhsT, rhs, start, stop, reads, writes):
        try:
            n = rhs.free_size()
        except Exception:
            n = 128
        c = 0.06 + n / 2400.0 * (1.0 if lhsT.dtype == BF16 else 2.4)
        return self.p.op('pe', lambda e: e.matmul(out, lhsT=lhsT, rhs=rhs, start=start, stop=stop), reads, writes, cost=c)

    def copy(self, eng, out, in_, reads, writes):
        if eng == 'act':
            return self.p.op('act', lambda e: e.copy(out=out, in_=in_), reads, writes, cost=self.ecost('act', out))
        return self.p.op(eng, lambda e: e.tensor_copy(out=out, in_=in_), reads, writes, cost=self.ecost(eng, out))

    def load_weight_bf16(self, dst, dst_key, src, ncols, src_c0=0, dst_c0=0, scale_bc=None):
        p = self.p
        CH = 1024 if ncols % 1024 == 0 else ncols
        for kc in range(NKC):
            for c0 in range(0, ncols, CH):
                i = self.stage_i
                self.stage_i += 1
                nst = len(self.stage)
                st = self.stage[i % nst]
                sk = f"stage{i % nst}"
                p.dma('sp', st[:, 0:CH], src[kc * 128:(kc + 1) * 128, src_c0 + c0:src_c0 + c0 + CH], reads=[], writes=[sk])
                eng = ['dve', 'act'][i % 2] if scale_bc is None else ['dve', 'pool'][i % 2]
                if scale_bc is None:
                    self.copy(eng, dst[:, kc, dst_c0 + c0:dst_c0 + c0 + CH], st[:, 0:CH], [sk], [dst_key])
                else:
                    self.tt(eng, dst[:, kc, dst_c0 + c0:dst_c0 + c0 + CH], st[:, 0:CH], scale_bc[:, c0:c0 + CH], ALU.mult,
                            [sk, 'bc_tiles'], [dst_key])

    def rms_rstd(self, src, src_key, TT, tag):
        bi = self.rms_i % len(self.sqb_l)
        self.rms_i += 1
        sqb, rstd = self.sqb_l[bi], self.rstd_l[bi]
        ksq, krs = f'sqb{bi}', f'rstd{bi}'
        if self.sqb_alias:
            ksq = 'yT0'
        ps, pk = self.next_ps()
        if self.sqb_alias:
            for kc in range(NKC):
                sl = kc % 2
                if kc % 2 == 0:
                    self.act(sqb[:, sl, :TT], src[:, kc, :TT], AF.Square, [src_key], [('sqs', sl)])
                else:
                    self.tt('dve', sqb[:, sl, :TT], src[:, kc, :TT], src[:, kc, :TT], ALU.mult, [src_key], [('sqs', sl)])
                self.mm(ps[:, :TT], self.ones_bf[:], sqb[:, sl, :TT], kc == 0, kc == NKC - 1, [('sqs', sl), 'ones_bf'], [pk])
        else:
            for kc in range(NKC):
                if kc % 2 == 0:
                    self.act(sqb[:, kc, :TT], src[:, kc, :TT], AF.Square, [src_key], [(ksq, kc)])
                else:
                    self.tt('dve', sqb[:, kc, :TT], src[:, kc, :TT], src[:, kc, :TT], ALU.mult, [src_key], [(ksq, kc)])
            for kc in range(NKC):
                self.mm(ps[:, :TT], self.ones_bf[:], sqb[:, kc, :TT], kc == 0, kc == NKC - 1, [(ksq, kc), 'ones_bf'], [pk])
        self.act(rstd[:, :TT], ps[:, :TT], AF.Ln, [pk, 'consts'], [krs], bias=self.epsc[:, 0:1], scale=1.0 / D)
        self.act(rstd[:, :TT], rstd[:, :TT], AF.Exp, [krs], [krs], scale=-0.5)
        return rstd, krs

    def run_layer(self, li, kind, TT, w_in_cols, mixer_setup, mixer_tile, is_last):
        p = self.p
        T = self.T
        ntiles = T // TT
        with ExitStack() as les:
            self.les = les
            self.lname = f"L{li}"
            self.TT = TT
            self.W_in = self.lsb("W_in", [128, NKC, w_in_cols], BF16)
            self.W_out = self.lsb("W_out", [128, NKC, D], BF16)
            ndb = 2 if kind != 'rwkv' else 1
            self.hT = [self.lsb(f"hT{i}", [128, NKC, TT], F32) for i in range(2)]
            self.sqb_alias = (kind == 'rwkv')
            if not self.sqb_alias:
                self.sqb_l = [self.lsb(f"sqb{i}", [128, NKC, TT], BF16) for i in range(ndb)]
            self.rstd_l = [self.lsb(f"rstd{i}", [128, TT], F32) for i in range(ndb)]
            self.rms_i = 0
            self.hn_l = [self.lsb(f"hn{i}", [128, NKC, TT + 1], BF16) for i in range(ndb)]
            self.yTt_l = [self.lsb(f"yTt{i}", [128, NKC, TT], BF16) for i in range(ndb)]
            self.hn, self.hnk = self.hn_l[0], 'hn0'
            self.yTt, self.yTk = self.yTt_l[0], 'yT0'
            if self.sqb_alias:
                self.sqb_l = [self.lsb("sqs", [128, 2, TT], BF16)]
            self.stage_i = 0
            loader = mixer_setup()
            with ExitStack() as ses:
                nst = 2 if kind == 'rwkv' else max(2, min(4, (self.nc.sbuf_bytes_remaining - 512) // 4096))
                self.stage = [ses.enter_context(self.nc.sbuf_tensor(f"stage{i}_{self.lname}", [128, 1024], F32)) for i in range(nst)]
                if loader is None:
                    self.load_weight_bf16(self.W_in, 'W_in', self.w_in_dram[kind], w_in_cols)
                    self.load_weight_bf16(self.W_out, 'W_out', self.w_out_dram[kind], D)
                else:
                    loader(ses)
                p.barrier()
            if getattr(self, 'post_setup', None) is not None:
                self.post_setup()
                self.post_setup = None
            hn0 = self.hn_l[0]
            p.op('pool', lambda e: e.memset(hn0[:, :, 0:1], 0.0), [], ['hn0'])

            def load(ti):
                buf = self.hT[ti % len(self.hT)]
                src = self.xT if self.first_layer else self.yT
                p.dma('sp', buf[:], src.rearrange("(c p) t -> p c t", p=128)[:, :, ti * TT:(ti + 1) * TT],
                      reads=[('hd', ti * TT // 128 + i) for i in range(TT // 128)], writes=[f"hT{ti % len(self.hT)}"])

            if self.use_sched:
                p.begin_record()
            load(0)
            for ti in range(ntiles):
                if len(self.hT) > 1:
                    if ti + 1 < ntiles:
                        load(ti + 1)
                elif ti > 0:
                    load(ti)
                h = self.hT[ti % len(self.hT)]
                hk = f"hT{ti % len(self.hT)}"
                rstd, rk = self.rms_rstd(h, hk, TT, 'in')
                g = self.norm_g
                prev_hn, prev_hnk = self.hn, self.hnk
                bi = ti % ndb
                self.hn, self.hnk = self.hn_l[bi], f'hn{bi}'
                self.yTt, self.yTk = self.yTt_l[bi], f'yT{bi}'
                if ti > 0:
                    self.copy('pool', self.hn[:, :, 0:1], prev_hn[:, :, TT:TT + 1], [prev_hnk], [self.hnk])
                for kc in range(NKC):
                    self.stt('dve', self.hn[:, kc, 1:TT + 1], h[:, kc, :], g[:, li, kc:kc + 1], rstd[:, :TT],
                             ALU.mult, ALU.mult, [hk, rk, 'consts'], [self.hnk])
                mixer_tile(ti)
                for j in range(NKC):
                    ps, pk = self.next_ps()
                    for kc in range(NKC):
                        self.mm(ps[:, :TT], self.W_out[:, kc, j * 128:(j + 1) * 128], self.yTt[:, kc, :TT],
                                kc == 0, kc == NKC - 1, ['W_out', self.yTk], [pk])
                    self.tt('dve', h[:, j, :], h[:, j, :], ps[:, :TT], ALU.add, [hk, pk], [hk])
                if is_last and self.do_final:
                    rstd, rk = self.rms_rstd(h, hk, TT, 'fin')
                    for kc in range(NKC):
                        self.stt('dve' if kc % 2 == 0 else 'pool', h[:, kc, :], h[:, kc, :], self.final_g[:, kc:kc + 1], rstd[:, :TT],
                                 ALU.mult, ALU.mult, [hk, rk, 'consts'], [hk])
                p.dma('sp', self.yT.rearrange("(c p) t -> p c t", p=128)[:, :, ti * TT:(ti + 1) * TT], h[:],
                      reads=[hk], writes=[('hd', (ti * TT) // 128 + i) for i in range(max(1, TT // 128))])
            if self.use_sched:
                p.flush()
            self.first_layer = False
            p.barrier()
        self.les = None

    def conv_layer(self, li, is_last):
        TT = 512

        def setup():
            self.yext = self.lsb("yext", [128, NKC, TT + 2], F32)
            self.zs = [self.lsb(f"zs{i}", [128, TT], F32) for i in range(2)]
            self.acc = [self.lsb(f"acc{i}", [128, TT], F32) for i in range(2)]
            self.sg = [self.lsb(f"sg{i}", [128, TT], F32) for i in range(2)]
            self.p.op('pool', lambda e: e.memset(self.yext[:, :, 0:2], 0.0), [], [('yext', j) for j in range(NKC)])

        def tile(ti):
            W = self.W_in
            cw = self.conv_w
            for j in range(NKC):
                zs, acc, sg = self.zs[j % 2], self.acc[j % 2], self.sg[j % 2]
                zk, ak, gk = f"zs{j % 2}", f"acc{j % 2}", f"sg{j % 2}"
                pss = []
                for blk in range(4):
                    ps, pk = self.next_ps()
                    col0 = blk * D + j * 128
                    for kc in range(NKC):
                        self.mm(ps[:, :TT], W[:, kc, col0:col0 + 128], self.hn[:, kc, 1:TT + 1], kc == 0, kc == NKC - 1,
                                ['W_in', self.hnk], [pk])
                    pss.append((ps, pk))
                (pb, pbk), (pc, pck), (pz, pzk), (pg, pgk) = pss
                yk = ('yext', j)
                self.copy('act', zs[:], pz[:, :TT], [pzk], [zk])
                if ti > 0:
                    self.copy('pool', self.yext[:, j, 0:2], self.yext[:, j, TT:TT + 2], [yk], [yk])
                self.tt('dve', self.yext[:, j, 2:TT + 2], pc[:, :TT], zs[:], ALU.mult, [pck, zk], [yk])
                self.act(acc[:], self.yext[:, j, 2:TT + 2], AF.Copy, [yk, 'consts'], [ak], scale=cw[:, j, 2:3])
                self.stt('pool', acc[:], self.yext[:, j, 1:TT + 1], cw[:, j, 1:2], acc[:], ALU.mult, ALU.add, [yk, ak, 'consts'], [ak])
                self.stt('pool', acc[:], self.yext[:, j, 0:TT], cw[:, j, 0:1], acc[:], ALU.mult, ALU.add, [yk, ak, 'consts'], [ak])
                self.act(sg[:], pg[:, :TT], AF.Silu, [pgk], [gk])
                self.tt('dve', acc[:], pb[:, :TT], acc[:], ALU.mult, [pbk, ak], [ak])
                self.tt('pool', self.yTt[:, j, :], acc[:], sg[:], ALU.mult, [ak, gk], [self.yTk])

        self.run_layer(li, 'conv', TT, 4 * D, setup, tile, is_last)

    def gmlp_layer(self, li, is_last):
        TT = 512

        def setup():
            p = self.p
            self.wsT = self.lsb("wsT", [128, 8, 128], F32)
            self.bs_bc = self.lsb("bs_bc", [128, 8, TT], F32)
            self.vg_bc = self.lsb("vg_bc", [128, D], F32)
            self.vn = [self.lsb(f"vn{i}", [128, D], F32) for i in range(TT // 128)]
            self.vss = self.lsb("vss", [128, 4], F32)
            self.junk = self.lsb("junk", [128, 512], F32)
            self.s_sb = [self.lsb(f"s_sb{i}", [128, TT], F32) for i in range(2)]
            self.sg = [self.lsb(f"sg{i}", [128, TT], F32) for i in range(2)]
            p.dma('sp', self.wsT[:], self.inputs['gmlp_wsT'], [], ['wsT'])
            for g in range(8):
                p.op('pool', lambda e: e.affine_select(out=self.wsT[:, g, :], in_=self.wsT[:, g, :], pattern=[[1, 128]],
                                                       compare_op=ALU.is_ge, fill=0.0, base=0, channel_multiplier=-1),
                     ['wsT'], ['wsT'])
            for r in range(TT // 128):
                p.dma('sp', self.bs_bc[:, :, r * 128:(r + 1) * 128],
                      self.inputs['gmlp_bs'].partition_broadcast(128), [], ['bs_bc'])
            p.dma('sp', self.vg_bc[:], self.inputs['gmlp_vg'].partition_broadcast(128), [], ['vg_bc'])

        def tile(ti):
            W = self.W_in
            nblk = TT // 128
            for blk in range(nblk):
                vn = self.vn[blk]
                vk = f"vn{blk}"
                halves = []
                for hf in range(2):
                    ps, pk = self.next_ps()
                    for kc in range(NKC):
                        self.mm(ps[:, :512], self.hn[:, kc, 1 + blk * 128:1 + (blk + 1) * 128],
                                W[:, kc, D + hf * 512:D + (hf + 1) * 512], kc == 0, kc == NKC - 1, ['W_in', self.hnk], [pk])
                    halves.append((ps, pk))
                for hf, (ps, pk) in enumerate(halves):
                    self.act(self.junk[:], ps[:, :512], AF.Square, [pk], ['junk', 'vss'], accum_out=self.vss[:, hf:hf + 1])
                self.tt('dve', self.vss[:, 2:3], self.vss[:, 0:1], self.vss[:, 1:2], ALU.add, ['vss'], ['vss'])
                self.act(self.vss[:, 3:4], self.vss[:, 2:3], AF.Ln, ['vss', 'consts'], ['vss'], bias=self.epsc[:, 0:1], scale=1.0 / D)
                self.act(self.vss[:, 3:4], self.vss[:, 3:4], AF.Exp, ['vss'], ['vss'], scale=-0.5)
                for hf, (ps, pk) in enumerate(halves):
                    self.stt('dve', vn[:, hf * 512:(hf + 1) * 512], ps[:, :512], self.vss[:, 3:4],
                             self.vg_bc[:, hf * 512:(hf + 1) * 512], ALU.mult, ALU.mult, [pk, 'vss', 'vg_bc'], [vk])
            for j in range(NKC):
                s_sb, sg = self.s_sb[j % 2], self.sg[j % 2]
                sk, gk = f"s_sb{j % 2}", f"sg{j % 2}"
                ps, pk = self.next_ps()
                for blk in range(nblk):
                    self.mm(ps[:, blk * 128:(blk + 1) * 128], self.vn[blk][:, j * 128:(j + 1) * 128], self.wsT[:, j, :], True, True,
                            [f"vn{blk}", 'wsT'], [pk])
                self.tt('dve', s_sb[:], ps[:, :TT], self.bs_bc[:, j, :], ALU.add, [pk, 'bs_bc'], [sk])
                pu, puk = self.next_ps()
                for kc in range(NKC):
                    self.mm(pu[:, :TT], W[:, kc, j * 128:(j + 1) * 128], self.hn[:, kc, 1:TT + 1], kc == 0, kc == NKC - 1,
                            ['W_in', self.hnk], [puk])
                pg, pgk = self.next_ps()
                for kc in range(NKC):
                    self.mm(pg[:, :TT], W[:, kc, 2 * D + j * 128:2 * D + (j + 1) * 128], self.hn[:, kc, 1:TT + 1], kc == 0,
                            kc == NKC - 1, ['W_in', self.hnk], [pgk])
                self.act(sg[:], pg[:, :TT], AF.Silu, [pgk], [gk])
                self.tt('dve', s_sb[:], pu[:, :TT], s_sb[:], ALU.mult, [puk, sk], [sk])
                self.tt('pool', self.yTt[:, j, :], s_sb[:], sg[:], ALU.mult, [sk, gk], [self.yTk])

        self.run_layer(li, 'gmlp', TT, 3 * D, setup, tile, is_last)


    def make_ident(self, ident, key):
        p = self.p
        p.op('pool', lambda e: e.memset(ident[:], 1.0), [], [key])
        p.op('pool', lambda e: e.affine_select(out=ident[:], in_=ident[:], pattern=[[-1, 128]], compare_op=ALU.is_equal,
                                               fill=0.0, base=0, channel_multiplier=1), [key], [key])

    def make_block_masks(self, C, maskT, colmask, rowmask, strict=False):
        p = self.p
        nch = 128 // C
        if maskT is not None:
            p.op('pool', lambda e: e.memset(maskT[:], 1.0), [], ['masks'])
            p.op('pool', lambda e: e.affine_select(out=maskT[:], in_=maskT[:], pattern=[[1, 128]], compare_op=ALU.is_ge if not strict else ALU.is_gt,
                                                   fill=0.0, base=0, channel_multiplier=-1), ['masks'], ['masks'])
            for c in range(1, nch):
                p.op('pool', lambda e, c=c: e.affine_select(out=maskT[:, c * C:(c + 1) * C], in_=maskT[:, c * C:(c + 1) * C], pattern=[[0, C]],
                                                            compare_op=ALU.is_ge, fill=0.0, base=-c * C, channel_multiplier=1), ['masks'], ['masks'])
        if colmask is not None:
            p.op('pool', lambda e: e.memset(colmask[:], 0.0), [], ['masks'])
            for c in range(nch):
                p.op('pool', lambda e, c=c: e.memset(colmask[:, c, c * C:(c + 1) * C], 1.0), ['masks'], ['masks'])
        if rowmask is not None:
            p.op('pool', lambda e: e.memset(rowmask[:], 1.0), [], ['masks'])
            for c in range(nch):
                p.op('pool', lambda e, c=c: e.affine_select(out=rowmask[:, c:c + 1], in_=rowmask[:, c:c + 1], pattern=[[0, 1]],
                                                            compare_op=ALU.is_ge, fill=0.0, base=-c * C, channel_multiplier=1), ['masks'], ['masks'])
                p.op('pool', lambda e, c=c: e.affine_select(out=rowmask[:, c:c + 1], in_=rowmask[:, c:c + 1], pattern=[[0, 1]],
                                                            compare_op=ALU.is_ge, fill=0.0, base=c * C + C - 1, channel_multiplier=-1), ['masks'], ['masks'])

    def hgrn_layer(self, li, is_last):
        TT = 256
        C = 32
        NB = TT // 128
        NCH = TT // C

        def setup():
            p = self.p
            L = self.lsb
            self.ident = L("ident", [128, 128], F32)
            self.make_ident(self.ident, 'ident')
            self.maskT = L("maskT", [128, 128], F32)
            self.colmask = L("colmask", [128, 4, 128], F32)
            self.rowmask = L("rowmask", [128, 4], F32)
            self.make_block_masks(C, self.maskT, self.colmask, self.rowmask)
            self.resetm = L("resetm", [128, TT], F32)
            self.ones_t = L("ones_t", [128, TT], F32)
            p.op('pool', lambda e: e.memset(self.ones_t[:], 1.0), [], ['masks'])
            p.op('pool', lambda e: e.memset(self.resetm[:], 1.0), [], ['masks'])
            p.op('pool', lambda e: e.memset(self.resetm[:].rearrange("p (n c) -> p n c", c=C)[:, :, 0:1], 0.0), ['masks'], ['masks'])
            self.gn_bc = L("gn_bc", [128, D], F32)
            p.dma('sp', self.gn_bc[:], self.inputs['hgrn_gn_g'].partition_broadcast(128), [], ['gn_bc'])
            self.lbl = L("lbl", [128, 4, NKC], F32)
            self.lbt = L("lbt", [128, 4, NKC], F32)
            p.dma('sp', self.lbl[:], self.inputs['hgrn_lbl'], [], ['lbl'])
            self.act(self.lbl[:], self.lbl[:], AF.Exp, ['lbl'], ['lbl'])
            self.tt('dve', self.lbt[:, 0, :], self.lbl[:, 0, :], self.lbl[:, 1, :], ALU.add, ['lbl'], ['lbt'])
            self.tt('dve', self.lbt[:, 0, :], self.lbt[:, 0, :], self.lbl[:, 2, :], ALU.add, ['lbl', 'lbt'], ['lbt'])
            self.tt('dve', self.lbt[:, 0, :], self.lbt[:, 0, :], self.lbl[:, 3, :], ALU.add, ['lbl', 'lbt'], ['lbt'])
            p.op('dve', lambda e: e.reciprocal(out=self.lbt[:, 3, :], in_=self.lbt[:, 0, :]), ['lbt'], ['lbt'])
            p.op('dve', lambda e: e.memset(self.lbt[:, 1, :], 0.0), ['lbt'], ['lbt'])
            for i in range(1, li + 1):
                self.tt('dve', self.lbt[:, 1, :], self.lbt[:, 1, :], self.lbl[:, i, :], ALU.add, ['lbl', 'lbt'], ['lbt'])
            self.tt('dve', self.lbt[:, 1, :], self.lbt[:, 1, :], self.lbt[:, 3, :], ALU.mult, ['lbt'], ['lbt'])
            self.ts('dve', self.lbt[:, 2, :], self.lbt[:, 1, :], -1.0, 1.0, ALU.mult, ALU.add, ['lbt'], ['lbt'])
            self.S = L("S_hgrn", [128, NKC, 128], F32)
            p.op('pool', lambda e: e.memset(self.S[:], 0.0), [], [('S', j) for j in range(NKC)])
            names = ['f', 'kk', 'bb', 'qe', 'dd', 'sg']
            self.tmps = []
            for q in range(2):
                tm = {n: L(f"h_{n}{q}", [128, TT], F32) for n in names}
                tm['e1'] = tm['f']
                tm['ko'] = tm['dd']
                tm['ke_bf'] = L(f"h_ke_bf{q}", [128, TT], BF16)
                tm['qe_bf'] = L(f"h_qe_bf{q}", [128, TT], BF16)
                tm['kom'] = L(f"kom{q}", [128, 4, NB, 128], BF16)
                self.tmps.append(tm)
            self.sgate = [L(f"sgate{q}", [128, TT], BF16) for q in range(3)]
            self.qem = [L(f"qem{q}", [128, 4, TT], BF16) for q in range(3)]
            self.v_bf = [L(f"v_bf{q}", [128, NB, 128], BF16) for q in range(3)]
            self.attm = [L(f"attm{q}", [128, NB, 128], BF16) for q in range(3)]
            self.u_sb = [L(f"u_sb{q}", [128, NCH, 128], F32) for q in range(3)]
            self.dec = [L(f"dec{q}", [128, NCH], F32) for q in range(3)]
            self.S_all2 = [L(f"S_all{q}", [128, 5, 128], F32) for q in range(2)]
            self.S_bf2 = [L(f"S_bf{q}", [128, NCH, 128], BF16) for q in range(2)]
            self.on2 = [L(f"on{q}", [128, NB, 128], F32) for q in range(2)]
            self.oss2 = [L(f"oss{q}", [128, 2 * NB], F32) for q in range(2)]
            self.junk2 = [L(f"junk{q}", [128, 128], F32) for q in range(2)]
            self.ps_pool = [0, 1, 2, 3]
            self.nm_rr = 0

        def nm_ps():
            i = [4, 5][self.nm_rr % 2]
            self.nm_rr += 1
            return self.psums[i], f"ps{i}"

        def proj(col0):
            ps, pk = self.next_ps()
            for kc in range(NKC):
                self.mm(ps[:, :TT], self.W_in[:, kc, col0:col0 + 128], self.hn[:, kc, 1:TT + 1], kc == 0, kc == NKC - 1, ['W_in', self.hnk], [pk])
            return ps, pk

        def A_gen(j):
            q2 = j % 2
            t = self.tmps[q2]
            kom = t['kom']
            W = self.W_in
            q = j % 3
            lb, oml = self.lbt[:, 1, :], self.lbt[:, 2, :]
            sgate, qem, v_bf, attm, u_sb, dec = self.sgate[q], self.qem[q], self.v_bf[q], self.attm[q], self.u_sb[q], self.dec[q]
            ksg, kqem, kv, katt, ku, kdec = f'sgate{q}', f'qem{q}', f'v_bf{q}', f'attm{q}', f'u_sb{q}', f'dec{q}'
            pf, pfk = proj(D + j * 128)
            self.act(t['f'][:], pf[:, :TT], AF.Exp, [pfk], [f't_f{q2}'], scale=-1.0)
            self.tt('pool', t['f'][:], t['f'][:], self.ones_t[:], ALU.add, [f't_f{q2}', 'masks'], [f't_f{q2}'])
            self.p.op('dve', lambda e: e.reciprocal(out=t['f'][:], in_=t['f'][:]), [f't_f{q2}'], [f't_f{q2}'], cost=0.45)
            self.ts('dve', t['f'][:], t['f'][:], oml[:, j:j + 1], lb[:, j:j + 1], ALU.mult, ALU.add, [f't_f{q2}', 'lbt'], [f't_f{q2}'])
            self.act(t['kk'][:], t['f'][:], AF.Identity, [f't_f{q2}'], [f't_kk{q2}'], scale=-1.0, bias=self.epsc[:, 2:3])
            self.act(t['dd'][:], t['f'][:], AF.Ln, [f't_f{q2}'], [f't_dd{q2}'])
            self.p.op('dve', lambda e: e.tensor_tensor_scan(out=t['bb'][:], data0=self.resetm[:], data1=t['dd'][:], initial=0.0,
                                                            op0=ALU.mult, op1=ALU.add), [f't_dd{q2}', 'masks'], [f't_bb{q2}'])
            yield
            pq, pqk = proj(j * 128)
            self.act(t['e1'][:], t['bb'][:], AF.Exp, [f't_bb{q2}'], [f't_f{q2}'])
            self.tt('dve', t['qe'][:], pq[:, :TT], t['e1'][:], ALU.mult, [pqk, f't_f{q2}'], [f't_qe{q2}'])
            self.copy('act', t['qe_bf'][:], t['qe'][:], [f't_qe{q2}'], [f't_qe_bf{q2}'])
            qe4 = t['qe'][:].rearrange("p (b t) -> p b t", t=128)
            for c in range(4):
                self.tt('pool', qem[:, c, :].rearrange("p (b t) -> p b t", t=128), qe4,
                        self.colmask[:, c:c + 1, :].to_broadcast([128, NB, 128]), ALU.mult, [f't_qe{q2}', 'masks'], [kqem])
            yield
            self.act(t['e1'][:], t['bb'][:], AF.Exp, [f't_bb{q2}'], [f't_f{q2}'], scale=-1.0)
            self.tt('pool', t['ke_bf'][:], t['kk'][:], t['e1'][:], ALU.mult, [f't_kk{q2}', f't_f{q2}'], [f't_ke_bf{q2}'])
            b3 = t['bb'][:].rearrange("p (n c) -> p n c", c=C)
            self.act(dec[:], b3[:, :, C - 1], AF.Exp, [f't_bb{q2}'], [kdec])
            self.tt('pool', t['dd'][:].rearrange("p (n c) -> p n c", c=C), b3[:, :, C - 1:C].to_broadcast([128, NCH, C]), b3, ALU.subtract,
                    [f't_bb{q2}'], [f't_dd{q2}'])
            self.act(t['dd'][:], t['dd'][:], AF.Exp, [f't_dd{q2}'], [f't_dd{q2}'])
            self.tt('pool', t['ko'][:], t['kk'][:], t['dd'][:], ALU.mult, [f't_kk{q2}', f't_dd{q2}'], [f't_dd{q2}'])
            pg, pgk = proj(3 * D + j * 128)
            self.act(t['sg'][:], pg[:, :TT], AF.Exp, [pgk], [f't_sg{q2}'], scale=-1.0)
            self.tt('pool', t['sg'][:], t['sg'][:], self.ones_t[:], ALU.add, [f't_sg{q2}', 'masks'], [f't_sg{q2}'])
            self.p.op('dve', lambda e: e.reciprocal(out=t['sg'][:], in_=t['sg'][:]), [f't_sg{q2}'], [f't_sg{q2}'], cost=0.45)
            self.tt('dve', sgate[:], pg[:, :TT], t['sg'][:], ALU.mult, [pgk, f't_sg{q2}'], [ksg])
            yield
            pv, pvk = self.next_ps()
            for blk in range(NB):
                for kc in range(NKC):
                    self.mm(pv[:, blk * 128:(blk + 1) * 128], self.hn[:, kc, 1 + blk * 128:1 + (blk + 1) * 128],
                            W[:, kc, 2 * D + j * 128:2 * D + (j + 1) * 128], kc == 0, kc == NKC - 1, ['W_in', self.hnk], [pvk])
            self.copy('act', v_bf[:].rearrange("p b v -> p (b v)"), pv[:, :TT], [pvk], [kv])
            yield
            ps, pk = nm_ps()
            for blk in range(NB):
                cs = slice(blk * 128, (blk + 1) * 128)
                self.mm(ps[:, cs], t['ke_bf'][:, cs], t['qe_bf'][:, cs], True, True, [f't_ke_bf{q2}', f't_qe_bf{q2}'], [pk])
            self.tt('dve', attm[:], ps[:, :TT].rearrange("p (b t) -> p b t", t=128), self.maskT[:, None, :].to_broadcast([128, NB, 128]),
                    ALU.mult, [pk, 'masks'], [katt])
            ps, pk = nm_ps()
            for blk in range(NB):
                cs = slice(blk * 128, (blk + 1) * 128)
                self.p.op('pe', lambda e, ps=ps, cs=cs: e.transpose(ps[:, cs], t['ko'][:, cs], self.ident[:]), [f't_dd{q2}', 'ident'], [pk])
            for c in range(4):
                self.act(kom[:, c, :, :].rearrange("p b k -> p (b k)"), ps[:, :TT], AF.Copy, [pk, 'masks'], [f'kom{q2}'],
                         scale=self.rowmask[:, c:c + 1])
            yield
            for blk in range(NB):
                ps, pk = nm_ps()
                for c in range(4):
                    self.mm(ps[:, c * 128:(c + 1) * 128], kom[:, c, blk, :], v_bf[:, blk, :], True, True, [f'kom{q2}', kv], [pk])
                self.copy('act' if blk % 2 else 'dve', u_sb[:, blk * 4:(blk + 1) * 4, :].rearrange("p c v -> p (c v)"), ps[:, 0:512], [pk], [ku])
                if blk % 2:
                    yield

        def B_gen(j):
            P = self.psums
            q = j % 3
            sgate, qem, v_bf, attm, u_sb, dec = self.sgate[q], self.qem[q], self.v_bf[q], self.attm[q], self.u_sb[q], self.dec[q]
            ksg, kqem, kv, katt, ku, kdec = f'sgate{q}', f'qem{q}', f'v_bf{q}', f'attm{q}', f'u_sb{q}', f'dec{q}'
            kS = ('S', j)
            q2 = j % 2
            SA = self.S_all2[q2]
            S_bf, on, oss, junk = self.S_bf2[q2], self.on2[q2], self.oss2[q2], self.junk2[q2]
            kSA, kSbf, kon, koss, kjunk = f'S_all{q2}', f'S_bf{q2}', f'on{q2}', f'oss{q2}', f'junk{q2}'
            self.copy('pool', SA[:, 0, :], self.S[:, j, :], [kS], [kSA])
            for blk in range(NB):
                for c in range(4):
                    n = blk * 4 + c
                    self.stt('dve', SA[:, c + 1, :], SA[:, c, :], dec[:, n:n + 1], u_sb[:, n, :], ALU.mult, ALU.add, [kSA, kdec, ku], [kSA])
                self.copy('act', S_bf[:, blk * 4:(blk + 1) * 4, :].rearrange("p c v -> p (c v)"),
                          SA[:, 0:4, :].rearrange("p c v -> p (c v)"), [kSA], [kSbf])
                if blk < NB - 1:
                    self.copy('dve', SA[:, 0, :], SA[:, 4, :], [kSA], [kSA])
                yield
            self.copy('pool', self.S[:, j, :], SA[:, 4, :], [kSA], [kS])
            po, pok = P[6], 'ps6'
            for blk in range(NB):
                cs = slice(blk * 128, (blk + 1) * 128)
                self.mm(po[:, cs], attm[:, blk, :], v_bf[:, blk, :], True, False, [katt, kv], [pok])
                for c in range(4):
                    self.mm(po[:, cs], qem[:, c, cs], S_bf[:, blk * 4 + c, :], False, c == 3, [kqem, kSbf], [pok])
                if blk % 2:
                    yield
            for blk in range(NB):
                cs = slice(blk * 128, (blk + 1) * 128)
                self.act(junk[:], po[:, cs], AF.Square, [pok], [kjunk, koss], accum_out=oss[:, blk:blk + 1])
            self.act(oss[:, NB:2 * NB], oss[:, 0:NB], AF.Ln, [koss, 'consts'], [koss], bias=self.epsc[:, 0:1], scale=1.0 / 128)
            self.act(oss[:, NB:2 * NB], oss[:, NB:2 * NB], AF.Exp, [koss], [koss], scale=-0.5)
            self.tt('dve', on[:], po[:, :TT].rearrange("p (b v) -> p b v", v=128),
                    oss[:, NB:2 * NB, None].to_broadcast([128, NB, 128]), ALU.mult, [pok, koss], [kon])
            self.tt('pool', on[:], on[:], self.gn_bc[:, None, j * 128:(j + 1) * 128].to_broadcast([128, NB, 128]), ALU.mult,
                    [kon, 'gn_bc'], [kon])
            yield
            py, pyk = P[7], 'ps7'
            for blk in range(NB):
                cs = slice(blk * 128, (blk + 1) * 128)
                self.p.op('pe', lambda e, cs=cs, blk=blk: e.transpose(py[:, cs], on[:, blk, :], self.ident[:]), [kon, 'ident'], [pyk])
            self.tt('dve', self.yTt[:, j, :], py[:, :TT], sgate[:], ALU.mult, [pyk, ksg], [self.yTk])
            yield

        def drive(gens):
            gens = [g for g in gens if g is not None]
            while gens:
                for g in list(gens):
                    try:
                        next(g)
                    except StopIteration:
                        gens.remove(g)

        def step(g):
            try:
                next(g)
                return True
            except StopIteration:
                return False

        def tile(ti):
            A = {0: A_gen(0), 1: A_gen(1)}
            while step(A[0]):
                step(A[1])
            for sl in range(NKC):
                must = [B_gen(sl)]
                if sl + 1 < NKC:
                    must.append(A[sl + 1])
                opt = None
                if sl + 2 < NKC:
                    A[sl + 2] = A_gen(sl + 2)
                    opt = A[sl + 2]
                while must:
                    for g in list(must):
                        if not step(g):
                            must.remove(g)
                    if opt is not None and not step(opt):
                        opt = None

        self.run_layer(li, 'hgrn', TT, 4 * D, setup, tile, is_last)
        self.ps_pool = list(range(8))

    def rwkv_layer(self, li, is_last):
        TT = 256
        NB = TT // 128
        WC = 3200
        NDT = self.neu_dt
        LC = -0.6065306597126334

        def setup():
            p = self.p
            L = self.lsb
            self.ident = L("ident", [128, 128], F32)
            self.make_ident(self.ident, 'ident')
            self.ident_n = L("ident_n", [128, 128], NDT)
            self.copy('dve', self.ident_n[:], self.ident[:], ['ident'], ['ident'])
            self.maskS = L("maskS", [128, 128], F32)
            self.maskI = L("maskI", [128, 128], F32)
            self.maskSL = L("maskSL", [128, 128], F32)
            for (m, pat, cm, cmp_) in ((self.maskS, 1, -1, ALU.is_gt), (self.maskI, 1, -1, ALU.is_ge), (self.maskSL, -1, 1, ALU.is_gt)):
                p.op('pool', lambda e, m=m: e.memset(m[:], 1.0), [], ['masks'])
                p.op('pool', lambda e, m=m, pat=pat, cm=cm, cmp_=cmp_: e.affine_select(
                    out=m[:], in_=m[:], pattern=[[pat, 128]], compare_op=cmp_, fill=0.0, base=0, channel_multiplier=cm), ['masks'], ['masks'])
            self.blockones = L("blockones", [128, 128], F32)
            p.op('pool', lambda e: e.memset(self.blockones[:], 1.0), [], ['masks'])
            p.op('pool', lambda e: e.memset(self.blockones[0:64, 64:128], 0.0), ['masks'], ['masks'])
            p.op('pool', lambda e: e.memset(self.blockones[64:128, 0:64], 0.0), ['masks'], ['masks'])
            self.resetm = L("resetm", [128, TT], F32)
            p.op('pool', lambda e: e.memset(self.resetm[:], 1.0), [], ['masks'])
            p.op('pool', lambda e: e.memset(self.resetm[:].rearrange("p (n c) -> p n c", c=128)[:, :, 0:1], 0.0), ['masks'], ['masks'])
            p.op('pool', lambda e: e.memset(self.epsc[:, 1:2], GN_EPS), [], ['consts'])
            self.mu_fm = L("mu_fm", [128, 33], F32)
            self.omu_fm = L("omu_fm", [128, 33], F32)
            p.dma('sp', self.mu_fm[:], self.inputs['rwkv_mu_fm'], [], ['rw_vecs'])
            self.ts('dve', self.omu_fm[:], self.mu_fm[:], -1.0, 1.0, ALU.mult, ALU.add, ['rw_vecs'], ['rw_vecs'])
            self.vecs = L("rw_vecs", [128, 5, NKC], F32)
            p.dma('sp', self.vecs[:], self.inputs['rwkv_vecs'], [], ['rw_vecs'])
            self.lw2 = L("lw2", [128, D], BF16)
            self.lo_bf = L("lo_bf", [128, TT], BF16)
            self.gng_bc = L("gng_bc", [128, D], BF16)
            self.gnb_bc = L("gnb_bc", [128, D], BF16)
            self.Wva = L("Wva", [128, NKC, D], BF16)
            self.Wvb = L("Wvb", [128, NKC, D], BF16)
            src = self.w_in_dram['rwkv']

            def loader(tes):
                muv = tes.enter_context(self.nc.sbuf_tensor("muv_bc", [128, D], F32))
                p.dma('sp', muv[:], self.inputs['rwkv_lw2'], [], ['muv'])
                self.copy('dve', self.lw2[:], muv[:], ['muv'], ['lw2'])
                for (dst, nm) in ((self.gng_bc, 'rwkv_gn_g'), (self.gnb_bc, 'rwkv_gn_b')):
                    p.dma('sp', muv[:], self.inputs[nm].partition_broadcast(128), ['muv'], ['muv'])
                    self.copy('dve', dst[:], muv[:], ['muv'], ['bc_tiles'])
                p.dma('sp', muv[:], self.inputs['rwkv_mu'][2 * D:3 * D].partition_broadcast(128), ['muv'], ['bc_tiles', 'muv'])
                self.load_weight_bf16(self.Wvb, 'Wv', src, D, src_c0=2 * D, scale_bc=muv)
                self.ts('dve', muv[:], muv[:], -1.0, 1.0, ALU.mult, ALU.add, ['bc_tiles'], ['bc_tiles'])
                self.load_weight_bf16(self.Wva, 'Wv', src, D, src_c0=2 * D, scale_bc=muv)
                self.load_weight_bf16(self.W_in, 'W_in', src, 2 * D, src_c0=0, dst_c0=0)
                self.load_weight_bf16(self.W_in, 'W_in', src, 128, src_c0=3 * D, dst_c0=2 * D)
                self.load_weight_bf16(self.W_in, 'W_in', src, D, src_c0=3 * D + 128, dst_c0=2 * D + 128)
                self.load_weight_bf16(self.W_out, 'W_out', self.w_out_dram['rwkv'], D)
            self.S = L("S_rwkv", [128, NKC, 64], F32)
            p.op('pool', lambda e: e.memset(self.S[:], 0.0), [], [('S', j) for j in range(NKC)])
            self.pcar = L("pcar", [128, 25], F32)
            p.op('pool', lambda e: e.memset(self.pcar[:], 0.0), [], [('pcar', i) for i in range(25)])
            self.pm_ext = [L(f"pm_ext{i}", [128, TT + 1], F32) for i in range(2)]
            self.pm_i = 0
            names = ['r', 'k', 'tmp', 'sigw', 'a', 'kk', 'rn', 'kmod', 'bbv', 'c']
            self.tmp = {'lo': L("w_lo", [128, TT], F32)}
            self.tmpP = [{n: L(f"w_{n}0", [128, TT], F32) for n in names}, None]
            self.tmp2 = [dict(), dict()]
            for n in ['khat', 'bhat']:
                self.tmp2[0][n] = L(f"w_{n}0", [128, TT], F32)
            for n in ['rt_bf', 'bt_bf', 'at_bf', 'kt_h0', 'kt_h1', 'bt_h0', 'bt_h1', 'at_h0', 'at_h1']:
                self.tmp2[0][n] = L(f"w_{n}0", [128, TT], BF16)
            self.hm = L("hm", [128, 2], F32)
            p.op('pool', lambda e: e.memset(self.hm[:], 0.0), [], ['masks'])
            p.op('pool', lambda e: e.memset(self.hm[0:64, 0:1], 1.0), ['masks'], ['masks'])
            p.op('pool', lambda e: e.memset(self.hm[64:128, 1:2], 1.0), ['masks'], ['masks'])
            self.pt = {n: [L(f"wp_{n}{q}", [128, TT], F32 if n in ('at', 'rt') else BF16) for q in range(2)] for n in ['at', 'rt', 'rkr', 'sgate']}
            self.blockones_bf = L("blockones_bf", [128, 128], BF16)
            self.copy('dve', self.blockones_bf[:], self.blockones[:], ['masks'], ['masks'])
            self.v_bf = [L(f"v_bf{q}", [128, NB, 128], BF16) for q in range(2)]
            self.dec = [L(f"dec{q}", [128, NB], F32) for q in range(3)]

            def post_setup():
                self.tmpP[1] = {n: L(f"w_{n}1", [128, TT], F32) for n in names}
                for qq in range(2, self.NDEEP):
                    self.NT.append(L(f"NT{qq}", [128, NCHN, 128], NDT))
                    self.Aak.append(L(f"Aak{qq}", [128, NCHN, 128], BF16))
                    self.Ark.append(L(f"Ark{qq}", [128, NCHN, 128], BF16))
                    self.Arb.append(L(f"Arb{qq}", [128, NCHN, 128], BF16))
                    self.khm.append([L(f"khm{qq}{b}", [128, 128], BF16) for b in range(NB)])
                    self.bhm.append([L(f"bhm{qq}{b}", [128, 128], BF16) for b in range(NB)])
                self.PbP[1] = [L(f"Pb1{i}", [128, NCHN, 128], NDT) for i in range(2)]
                self.QbP[1] = [L(f"Qb1{i}", [128, NCHN, 128], NDT) for i in range(2)]
                for n in ['khat', 'bhat']:
                    self.tmp2[1][n] = L(f"w_{n}1", [128, TT], F32)
                for n in ['rt_bf', 'bt_bf', 'at_bf', 'kt_h0', 'kt_h1', 'bt_h0', 'bt_h1', 'at_h0', 'at_h1']:
                    self.tmp2[1][n] = L(f"w_{n}1", [128, TT], BF16)
                for n in ['rkr', 'sgate']:
                    self.pt[n].append(L(f"wp_{n}2", [128, TT], BF16))
                for n in ['at', 'rt']:
                    self.pt[n].append(self.pt[n][0])
                self.ysb = [L(f"ysb{i}", [128, 256], F32) for i in range(2)]
                self.yn2 = [self.yn, L("yn1", [128, 128], F32)]
                self.bon2 = [self.bon, L("bon1", [128, 128], F32)]
                self.gst2 = [self.gst, L("gst1", [128, 12], F32)]
                self.junk2 = [self.junk, L("junk1", [128, 64], F32)]
                self.bcount = 0
                self.v_bf.append(L("v_bf2", [128, NB, 128], BF16))
            self.post_setup = post_setup
            NCHN = NB * 2
            self.PbP = [[L(f"Pb0{i}", [128, NCHN, 128], NDT) for i in range(2)], None]
            self.QbP = [[L(f"Qb0{i}", [128, NCHN, 128], NDT) for i in range(2)], None]
            self.NT = [L(f"NT{q}", [128, NCHN, 128], NDT) for q in range(2)]
            self.Aak = [L(f"Aak{q}", [128, NCHN, 128], BF16) for q in range(2)]
            self.Ark = [L(f"Ark{q}", [128, NCHN, 128], BF16) for q in range(2)]
            self.Arb = [L(f"Arb{q}", [128, NCHN, 128], BF16) for q in range(2)]
            self.NDEEP = 2
            self.ident4 = L("ident4", [128, NCHN, 128], NDT)
            for c in range(NCHN):
                self.copy('dve', self.ident4[:, c, :], self.ident[:], ['ident'], ['ident'])
            self.khm = [[L(f"khm{q}{b}", [128, 128], BF16) for b in range(NB)] for q in range(2)]
            self.bhm = [[L(f"bhm{q}{b}", [128, 128], BF16) for b in range(NB)] for q in range(2)]
            self.Z_sb = L("Z_sb", [128, 128], NDT)
            self.U_bf = L("U_bf", [128, 128], BF16)
            self.yn = L("yn", [128, 128], F32)
            self.bon = L("bon", [128, 128], F32)
            self.gst = L("gst", [128, 12], F32)
            self.junk = L("junk", [128, 64], F32)
            self.ps_pool = [0, 1]
            return loader

        NMB = [[2, 3], [4, 7]]
        self.nm_rrs = [0, 0]

        def nm_ps(q=0):
            i = NMB[q][self.nm_rrs[q] % 2]
            self.nm_rrs[q] += 1
            return self.psums[i], f"ps{i}"

        def shift(ps, pk, dst, dk, idx, mt):
            pm = self.pm_ext[self.pm_i % 2]
            pmk = f"pm_ext{self.pm_i % 2}"
            self.pm_i += 1
            ck = ('pcar', idx)
            self.copy('pool', pm[:, 0:1], self.pcar[:, idx:idx + 1], [ck], [pmk])
            self.act(pm[:, 1:TT + 1], ps[:, :TT], AF.Copy, [pk, 'rw_vecs'], [pmk], scale=self.mu_fm[:, mt:mt + 1])
            self.copy('pool', self.pcar[:, idx:idx + 1], pm[:, TT:TT + 1], [pmk], [ck])
            self.act(dst, ps[:, :TT], AF.Copy, [pk, 'rw_vecs'], [dk], scale=self.omu_fm[:, mt:mt + 1])
            self.tt('dve', dst, dst, pm[:, 0:TT], ALU.add, [dk, pmk], [dk])

        def proj(col0):
            ps, pk = self.next_ps()
            for kc in range(NKC):
                self.mm(ps[:, :TT], self.W_in[:, kc, col0:col0 + 128], self.hn[:, kc, 1:TT + 1], kc == 0, kc == NKC - 1, ['W_in', self.hnk], [pk])
            return ps, pk

        hsl = [slice(0, 64), slice(64, 128)]

        def A_gen(j):
            V = self.vecs
            q = j % 2
            q3 = j % 3
            q4 = j % self.NDEEP
            t = dict(self.tmp)
            t.update(self.tmpP[q])
            t['e1'] = t['rn']
            t['e2'] = t['tmp']
            t.update(self.tmp2[q])
            self.Pb, self.Qb = self.PbP[q], self.QbP[q]
            PAR = set(self.tmp2[0].keys())
            jc = slice(j * 128, (j + 1) * 128)
            at, rt, rkr, sgate = self.pt['at'][q], self.pt['rt'][q], self.pt['rkr'][q3], self.pt['sgate'][q3]
            kat, krt, krkr, ksg = f'p_at{q}', f'p_rt{q}', f'p_rkr{q3}', f'p_sgate{q3}'
            v_bf, dec = self.v_bf[q3], self.dec[q3]
            kv, kvb, kdec = f'v_sb{q3}', f'v_bf{q3}', f'dec{q3}'
            ps, pk = proj(j * 128)
            shift(ps, pk, t['r'][:], f't_r{q}', j, j)
            ps, pk = proj(D + j * 128)
            shift(ps, pk, t['k'][:], f't_k{q}', 8 + j, 8 + j)
            yield
            ps, pk = proj(2 * D + 128 + j * 128)
            shift(ps, pk, t['tmp'][:], f't_tmp{q}', 16 + j, 25 + j)
            self.act(sgate[:], t['tmp'][:], AF.Silu, [f't_tmp{q}'], [ksg])
            pv, pvk = self.next_ps()
            for blk in range(NB):
                n = 0
                for kc in range(NKC):
                    for (Wv, off) in ((self.Wva, 1), (self.Wvb, 0)):
                        self.mm(pv[:, blk * 128:(blk + 1) * 128], self.hn[:, kc, off + blk * 128:off + (blk + 1) * 128], Wv[:, kc, jc],
                                n == 0, n == 2 * NKC - 1, ['Wv', self.hnk], [pvk])
                        n += 1
            self.copy('dve', v_bf[:].rearrange("p b v -> p (b v)"), pv[:, :TT], [pvk], [kvb])
            yield
            pw, pwk = self.next_ps()
            self.mm(pw[:, :TT], self.lw2[0:64, jc], self.lo_bf[0:64, :], True, True, ['lw2', 't_lo_bf'], [pwk])
            self.act(t['sigw'][:], pw[:, :TT], AF.Sigmoid, [pwk, 'rw_vecs'], [f't_sigw{q}'], bias=V[:, 0, j:j + 1])
            pa, pak = self.next_ps()
            self.mm(pa[:, :TT], self.lw2[64:128, jc], self.lo_bf[64:128, :], True, True, ['lw2', 't_lo_bf'], [pak])
            self.act(t['a'][:], pa[:, :TT], AF.Sigmoid, [pak, 'rw_vecs'], [f't_a{q}'], bias=V[:, 1, j:j + 1])
            self.ts('dve', t['kk'][:], t['k'][:], V[:, 2, j:j + 1], None, ALU.mult, None, [f't_k{q}', 'rw_vecs'], [f't_kk{q}'])
            self.tt('pool', t['tmp'][:], t['kk'][:], t['kk'][:], ALU.mult, [f't_kk{q}'], [f't_tmp{q}'])
            pn, pnk = self.next_ps()
            self.mm(pn[:, :TT], self.blockones[:], t['tmp'][:], True, True, ['masks', f't_tmp{q}'], [pnk])
            self.ts('dve', t['rn'][:], pn[:, :TT], 1e-24, None, ALU.max, None, [pnk], [f't_rn{q}'])
            self.act(t['rn'][:], t['rn'][:], AF.Ln, [f't_rn{q}'], [f't_rn{q}'])
            self.act(t['rn'][:], t['rn'][:], AF.Exp, [f't_rn{q}'], [f't_rn{q}'], scale=-0.5)
            self.tt('pool', t['kk'][:], t['kk'][:], t['rn'][:], ALU.mult, [f't_kk{q}', f't_rn{q}'], [f't_kk{q}'])
            self.ts('dve', t['tmp'][:], t['a'][:], -1.0, V[:, 3, j:j + 1], ALU.add, ALU.mult, [f't_a{q}', 'rw_vecs'], [f't_tmp{q}'])
            self.stt('dve', t['kmod'][:], t['tmp'][:], 1.0, t['k'][:], ALU.add, ALU.mult, [f't_tmp{q}', f't_k{q}'], [f't_kmod{q}'])
            self.tt('pool', t['bbv'][:], t['kk'][:], t['a'][:], ALU.mult, [f't_kk{q}', f't_a{q}'], [f't_bbv{q}'])
            yield
            self.p.op('dve', lambda e: e.tensor_tensor_scan(out=t['c'][:], data0=self.resetm[:], data1=t['sigw'][:], initial=0.0,
                                                            op0=ALU.mult, op1=ALU.add), [f't_sigw{q}', 'masks'], [f't_c{q}'])
            self.act(t['e1'][:], t['c'][:], AF.Exp, [f't_c{q}'], [f't_rn{q}'], scale=LC)
            self.tt('pool', rt[:], t['r'][:], t['e1'][:], ALU.mult, [f't_r{q}', f't_rn{q}'], [krt])
            self.copy('act', t['rt_bf'][:], rt[:], [krt], [f't_rt_bf{q}'])
            self.act(t['e2'][:], t['c'][:], AF.Exp, [f't_c{q}'], [f't_tmp{q}'], scale=-LC)
            for hd in range(2):
                self.stt('dve', t[f'kt_h{hd}'][:], t['kmod'][:], self.hm[:, hd:hd + 1], t['e2'][:], ALU.mult, ALU.mult,
                         [f't_kmod{q}', f't_tmp{q}', 'masks'], [f't_kt_h{hd}_{q}'])
                self.stt('dve', t[f'bt_h{hd}'][:], t['bbv'][:], self.hm[:, hd:hd + 1], t['e2'][:], ALU.mult, ALU.mult,
                         [f't_bbv{q}', f't_tmp{q}', 'masks'], [f't_bt_h{hd}_{q}'])
            self.tt('pool', t['bt_bf'][:], t['bbv'][:], t['e2'][:], ALU.mult, [f't_bbv{q}', f't_tmp{q}'], [f't_bt_bf{q}'])
            self.tt('pool', t['e1'][:], t['c'][:], t['sigw'][:], ALU.subtract, [f't_c{q}', f't_sigw{q}'], [f't_rn{q}'])
            self.act(t['e1'][:], t['e1'][:], AF.Exp, [f't_rn{q}'], [f't_rn{q}'], scale=LC)
            self.stt('dve', at[:], t['kk'][:], -1.0, t['e1'][:], ALU.mult, ALU.mult, [f't_kk{q}', f't_rn{q}'], [kat])
            self.copy('act', t['at_bf'][:], at[:], [kat], [f't_at_bf{q}'])
            for hd in range(2):
                self.act(t[f'at_h{hd}'][:], at[:], AF.Copy, [kat, 'masks'], [f't_at_h{hd}_{q}'], scale=self.hm[:, hd:hd + 1])
            yield
            c3 = t['c'][:].rearrange("p (n c) -> p n c", c=128)
            self.tt('pool', t['e2'][:].rearrange("p (n c) -> p n c", c=128), c3[:, :, 127:128].to_broadcast([128, NB, 128]), c3,
                    ALU.subtract, [f't_c{q}'], [f't_tmp{q}'])
            self.act(t['e2'][:], t['e2'][:], AF.Exp, [f't_tmp{q}'], [f't_tmp{q}'], scale=LC)
            self.act(dec[:], c3[:, :, 127], AF.Exp, [f't_c{q}'], [kdec], scale=LC)
            self.tt('pool', t['khat'][:], t['kmod'][:], t['e2'][:], ALU.mult, [f't_kmod{q}', f't_tmp{q}'], [f't_khat{q}'])
            self.tt('dve', t['bhat'][:], t['bbv'][:], t['e2'][:], ALU.mult, [f't_bbv{q}', f't_tmp{q}'], [f't_bhat{q}'])
            self.stt('dve', rkr[:], t['r'][:], V[:, 4, j:j + 1], t['kmod'][:], ALU.mult, ALU.mult, [f't_r{q}', 'rw_vecs', f't_kmod{q}'], [krkr])
            yield
            NCH = NB * 2
            specs = {'P': ('bt_h', 'at_bf', self.maskS, self.Pb[0], f'Pb{q}0'),
                     'Q': ('at_h', 'bt_bf', self.maskSL, self.Qb[0], f'Qb{q}0'),
                     'ak': ('kt_h', 'at_bf', self.maskS, self.Aak[q4], f'Aak{q4}'),
                     'rk': ('kt_h', 'rt_bf', self.maskI, self.Ark[q4], f'Ark{q4}'),
                     'rb': ('bt_h', 'rt_bf', self.maskI, self.Arb[q4], f'Arb{q4}')}
            for name in ('P', 'Q', 'ak', 'rk', 'rb'):
                lh, rh, mask, dst, dk = specs[name]
                ps, pk = nm_ps(q)
                for c in range(NCH):
                    blk, hd = c // 2, c % 2
                    cs = slice(blk * 128, (blk + 1) * 128)
                    self.mm(ps[:, c * 128:(c + 1) * 128], t[f'{lh}{hd}'][:, cs], t[rh][:, cs], True, True, [f't_{lh}{hd}_{q}', f't_{rh}{q}'], [pk])
                self.tt('dve', dst[:], ps[:, 0:NCH * 128].rearrange("p (c t) -> p c t", c=NCH),
                        mask[:, None, :].to_broadcast([128, NCH, 128]), ALU.mult, [pk, 'masks'], [dk])
                if name == 'Q':
                    self.tt('pool', self.NT[q4][:], self.ident4[:], self.Pb[0][:], ALU.add, ['ident', f'Pb{q}0'], [f'NT{q4}'])
                    yield
            for blk in range(NB):
                cs = slice(blk * 128, (blk + 1) * 128)
                for (srcn, dst, dk) in (('khat', self.khm[q4][blk], f'khm{q4}{blk}'), ('bhat', self.bhm[q4][blk], f'bhm{q4}{blk}')):
                    ps, pk = nm_ps(q)
                    self.p.op('pe', lambda e, ps=ps, srcn=srcn, cs=cs: e.transpose(ps[:, 0:128], t[srcn][:, cs], self.ident[:]),
                              [f't_{srcn}{q}', 'ident'], [pk])
                    self.copy('act', dst[:], ps[:, 0:128], [pk], [dk])
            yield
            NTq, kNT = self.NT[q4], f'NT{q4}'
            for i in range(6):
                a_, b_ = i % 2, (i + 1) % 2
                Pa, Qa, Pn, Qn = self.Pb[a_], self.Qb[a_], self.Pb[b_], self.Qb[b_]
                kPa, kQa, kPn, kQn = f'Pb{q}{a_}', f'Qb{q}{a_}', f'Pb{q}{b_}', f'Qb{q}{b_}'
                if i < 5:
                    ps, pk = nm_ps(q)
                    for c in range(NCH):
                        self.mm(ps[:, c * 128:(c + 1) * 128], Qa[:, c, :], Pa[:, c, :], True, True, [kQa, kPa], [pk])
                    self.copy('act', Pn[:].rearrange("p c t -> p (c t)"), ps[:, 0:NCH * 128], [pk], [kPn])
                ps, pk = nm_ps(q)
                for c in range(NCH):
                    self.mm(ps[:, c * 128:(c + 1) * 128], Pa[:, c, :], Qa[:, c, :], True, True, [kPa, kQa], [pk])
                self.copy('dve' if i % 2 == 0 else 'act', Qn[:].rearrange("p c t -> p (c t)"), ps[:, 0:NCH * 128], [pk], [kQn])
                yield
                ps, pk = nm_ps(q)
                for c in range(NCH):
                    self.mm(ps[:, c * 128:(c + 1) * 128], Qn[:, c, :], NTq[:, c, :], True, True, [kQn, kNT], [pk])
                self.tt('dve', NTq[:].rearrange("p c t -> p (c t)"), NTq[:].rearrange("p c t -> p (c t)"), ps[:, 0:NCH * 128], ALU.add,
                        [kNT, pk], [kNT])
                yield

        def B_gen(j):
            t = self.tmp
            P = self.psums
            q = j % 2
            q3 = j % 3
            q4 = j % self.NDEEP
            jc = slice(j * 128, (j + 1) * 128)
            at, rt, rkr, sgate = self.pt['at'][q], self.pt['rt'][q], self.pt['rkr'][q3], self.pt['sgate'][q3]
            kat, krt, krkr, ksg = f'p_at{q}', f'p_rt{q}', f'p_rkr{q3}', f'p_sgate{q3}'
            v_bf, dec = self.v_bf[q3], self.dec[q3]
            kv, kvb, kdec = f'v_sb{q3}', f'v_bf{q3}', f'dec{q3}'
            kS = ('S', j)
            for blk in range(NB):
                cs = slice(blk * 128, (blk + 1) * 128)
                khm, bhm = self.khm[q4][blk], self.bhm[q4][blk]
                kkh, kbh = f'khm{q4}{blk}', f'bhm{q4}{blk}'
                pz, pzk = P[5], 'ps5'
                for hd in range(2):
                    hs, hc, c = hsl[hd], slice(hd * 64, (hd + 1) * 64), blk * 2 + hd
                    self.mm(pz[:, hc], self.Aak[q4][:, c, :], v_bf[:, blk, hc], True, False, [f'Aak{q4}', kvb], [pzk])
                    self.mm(pz[:, hc], at[hs, cs], self.S[hs, j, :], False, True, [kat, kS], [pzk])
                self.copy('act', self.Z_sb[:], pz[:, 0:128], [pzk], ['Z_sb'])
                yield
                for hd in range(2):
                    hc, c = slice(hd * 64, (hd + 1) * 64), blk * 2 + hd
                    self.mm(pz[:, hc], self.NT[q4][:, c, :], self.Z_sb[:, hc], True, True, [f'NT{q4}', 'Z_sb'], [pzk])
                self.copy('act', self.U_bf[:], pz[:, 0:128], [pzk], ['U_bf'])
                yield
                self.mm(pz[:, 0:128], khm[:], v_bf[:, blk, :], True, False, [kkh, kvb], [pzk])
                self.mm(pz[:, 0:128], bhm[:], self.U_bf[:], False, True, [kbh, 'U_bf'], [pzk])
                py, pyk = P[6], 'ps6'
                for hd in range(2):
                    hs, hc, c = hsl[hd], slice(hd * 64, (hd + 1) * 64), blk * 2 + hd
                    yc = slice(hd * 128, hd * 128 + 64)
                    bc_ = slice(hd * 128 + 64, hd * 128 + 128)
                    self.mm(py[:, yc], self.Ark[q4][:, c, :], v_bf[:, blk, hc], True, False, [f'Ark{q4}', kvb], [pyk])
                    self.mm(py[:, yc], rt[hs, cs], self.S[hs, j, :], False, False, [krt, kS], [pyk])
                    self.mm(py[:, yc], self.Arb[q4][:, c, :], self.U_bf[:, hc], False, True, [f'Arb{q4}', 'U_bf'], [pyk])
                    self.mm(py[:, bc_], rkr[hs, cs], self.blockones_bf[hs, hs], True, True, [krkr, 'masks'], [pyk])
                for hd in range(2):
                    hs, hc = hsl[hd], slice(hd * 64, (hd + 1) * 64)
                    self.stt('dve', self.S[hs, j, :], self.S[hs, j, :], dec[hs, blk:blk + 1], pz[hs, hc], ALU.mult, ALU.add,
                             [kS, kdec, pzk], [kS])
                yield
                bp = self.bcount % 2
                self.bcount += 1
                ysb, kys = self.ysb[bp], f'ysb{bp}'
                g, kg = self.gst2[bp], f'gst{bp}'
                yn, kyn = self.yn2[bp], f'yn{bp}'
                bon, kbon = self.bon2[bp], f'bon{bp}'
                junk, kjunk = self.junk2[bp], f'junk{bp}'
                self.copy('act', ysb[:], py[:, 0:256], [pyk], [kys])
                for hd in range(2):
                    hc = slice(hd * 64, (hd + 1) * 64)
                    yc = slice(hd * 128, hd * 128 + 64)
                    bc_ = slice(hd * 128 + 64, hd * 128 + 128)
                    self.act(junk[:], ysb[:, yc], AF.Identity, [kys], [kjunk, kg], accum_out=g[:, hd:hd + 1])
                    self.act(junk[:], ysb[:, yc], AF.Square, [kys], [kjunk, kg], accum_out=g[:, 2 + hd:3 + hd])
                    self.tt('pool', bon[:, hc], ysb[:, bc_], v_bf[:, blk, hc], ALU.mult, [kys, kvb], [kbon])
                self.ts('dve', g[:, 4:6], g[:, 0:2], 1.0 / 64, None, ALU.mult, None, [kg], [kg])
                self.tt('dve', g[:, 6:8], g[:, 4:6], g[:, 4:6], ALU.mult, [kg], [kg])
                self.stt('dve', g[:, 8:10], g[:, 2:4], 1.0 / 64, g[:, 6:8], ALU.mult, ALU.subtract, [kg], [kg])
                self.act(g[:, 8:10], g[:, 8:10], AF.Ln, [kg, 'consts'], [kg], bias=self.epsc[:, 1:2])
                self.act(g[:, 8:10], g[:, 8:10], AF.Exp, [kg], [kg], scale=-0.5)
                for hd in range(2):
                    hc = slice(hd * 64, (hd + 1) * 64)
                    yc = slice(hd * 128, hd * 128 + 64)
                    self.ts('dve', yn[:, hc], ysb[:, yc], g[:, 4 + hd:5 + hd], g[:, 8 + hd:9 + hd], ALU.subtract, ALU.mult,
                            [kys, kg], [kyn])
                yield
                self.tt('pool', yn[:], yn[:], self.gng_bc[:, jc], ALU.mult, [kyn, 'bc_tiles'], [kyn])
                self.tt('pool', yn[:], yn[:], self.gnb_bc[:, jc], ALU.add, [kyn, 'bc_tiles'], [kyn])
                self.tt('pool', yn[:], yn[:], bon[:], ALU.add, [kyn, kbon], [kyn])
                ps, pk = nm_ps(q)
                self.p.op('pe', lambda e, ps=ps, yn=yn: e.transpose(ps[:, 0:128], yn[:], self.ident[:]), [kyn, 'ident'], [pk])
                self.tt('dve', self.yTt[:, j, cs], ps[:, 0:128], sgate[:, cs], ALU.mult, [pk, ksg], [self.yTk])
                yield

        def drive(gens):
            gens = [g for g in gens if g is not None]
            while gens:
                for g in list(gens):
                    try:
                        next(g)
                    except StopIteration:
                        gens.remove(g)

        def tile(ti):
            t = self.tmp
            ps, pk = proj(2 * D)
            shift(ps, pk, t['lo'][:], 't_lo', 24, 24)
            self.act(t['lo'][0:64, :], t['lo'][0:64, :], AF.Tanh, ['t_lo'], ['t_lo'])
            self.copy('dve', self.lo_bf[:], t['lo'][:], ['t_lo'], ['t_lo_bf'])
            for j in range(NKC):
                drive([A_gen(j)])
                drive([B_gen(j)])

        self.run_layer(li, 'rwkv', TT, WC, setup, tile, is_last)
        self.ps_pool = list(range(8))

    def build(self):
        nc = self.nc
        T = self.T
        self.xT = self.din("xT", [D, T])
        self.yT = nc.dram_tensor("yT", [D, T], F32, kind="ExternalOutput").ap()
        d_norm_g = self.din("norm_g", [128, 4, NKC])
        d_final_g = self.din("final_g", [128, NKC])
        self.w_in_dram, self.w_out_dram = {}, {}
        kinds = [k for (_, k) in self.layers]
        if 'conv' in kinds:
            self.w_in_dram['conv'] = self.din("conv_w_in", [D, 4 * D])
            self.w_out_dram['conv'] = self.din("conv_w_out", [D, D])
            d_conv_w = self.din("conv_w", [128, NKC, 3])
        if 'rwkv' in kinds:
            self.w_in_dram['rwkv'] = self.din("rwkv_w_in", [D, 4 * D + 128])
            self.w_out_dram['rwkv'] = self.din("rwkv_w_out", [D, D])
            self.din("rwkv_mu_fm", [128, 33])
            self.din("rwkv_mu", [4 * D + 128])
            self.din("rwkv_vecs", [128, 5, NKC])
            self.din("rwkv_lw2", [128, D])
            self.din("rwkv_gn_g", [D])
            self.din("rwkv_gn_b", [D])
        if 'hgrn' in kinds:
            self.w_in_dram['hgrn'] = self.din("hgrn_w_in", [D, 4 * D])
            self.w_out_dram['hgrn'] = self.din("hgrn_w_out", [D, D])
            self.din("hgrn_gn_g", [D])
            self.din("hgrn_lbl", [128, 4, NKC])
        if 'gmlp' in kinds:
            self.w_in_dram['gmlp'] = self.din("gmlp_w_in", [D, 3 * D])
            self.w_out_dram['gmlp'] = self.din("gmlp_w_out", [D, D])
            self.din("gmlp_wsT", [128, 8, 128])
            self.din("gmlp_bs", [8, 128])
            self.din("gmlp_vg", [D])
        with ExitStack() as es:
            self.es = es
            nc.allow_low_precision("bf16 matmul operands, fp32 accumulation")
            self.p = p = Prog(nc, es)
            self.psums = [es.enter_context(nc.psum_tensor(f"ps{i}", [128, 512], F32)) for i in range(8)]
            self.ps_rr = 0
            self.ps_pool = list(range(8))
            self.ones_bf = self.sb("ones_bf", [128, 128], BF16)
            self.epsc = self.sb("epsc", [128, 4], F32)
            self.norm_g = self.sb("norm_g_sb", [128, 4, NKC], F32)
            self.final_g = self.sb("final_g_sb", [128, NKC], F32)
            p.op('pool', lambda e: e.memset(self.ones_bf[:], 1.0), [], ['ones_bf'])
            p.op('pool', lambda e: e.memset(self.epsc[:, 0:1], RMS_EPS), [], ['consts'])
            p.op('pool', lambda e: e.memset(self.epsc[:, 2:3], 1.0), ['consts'], ['consts'])
            p.dma('sp', self.norm_g[:], d_norm_g, [], ['consts'])
            p.dma('sp', self.final_g[:], d_final_g, [], ['consts'])
            if 'conv' in kinds:
                self.conv_w = self.sb("conv_w_sb", [128, NKC, 3], F32)
                p.dma('sp', self.conv_w[:], d_conv_w, [], ['consts'])
            self.first_layer = True
            for n, (li, kind) in enumerate(self.layers):
                is_last = n == len(self.layers) - 1
                if kind == 'conv':
                    self.conv_layer(li, is_last)
                elif kind == 'gmlp':
                    self.gmlp_layer(li, is_last)
                elif kind == 'hgrn':
                    self.hgrn_layer(li, is_last)
                elif kind == 'rwkv':
                    self.rwkv_layer(li, is_last)
                else:
                    raise ValueError(kind)
            p.finish('sp')
            self.stats = (p.n_ins, p.n_wait)
        return nc


def prep_inputs(inp, b, layers):
    f = np.float32
    m = {}
    m["xT"] = np.ascontiguousarray(np.asarray(inp["x"][b], f).T)
    m["norm_g"] = np.ascontiguousarray(np.asarray(inp["norm_g"], f).reshape(4, NKC, 128).transpose(2, 0, 1))
    m["final_g"] = np.ascontiguousarray(np.asarray(inp["final_g"], f).reshape(NKC, 128).T)
    kinds = [k for (_, k) in layers]
    if 'conv' in kinds:
        m["conv_w_in"] = np.ascontiguousarray(np.asarray(inp["conv_w_in"][0], f))
        m["conv_w_out"] = np.ascontiguousarray(np.asarray(inp["conv_w_out"][0], f))
        m["conv_w"] = np.ascontiguousarray(np.asarray(inp["conv_w"][0], f).reshape(3, NKC, 128).transpose(2, 1, 0))
    if 'rwkv' in kinds:
        m["rwkv_w_in"] = np.ascontiguousarray(np.asarray(inp["rwkv_w_in"][0], f))
        m["rwkv_w_out"] = np.ascontiguousarray(np.asarray(inp["rwkv_w_out"][0], f))
        mu = np.asarray(inp["rwkv_mu"][0], f)
        m["rwkv_mu"] = np.ascontiguousarray(mu)
        m["rwkv_mu_fm"] = np.ascontiguousarray(mu.reshape(33, 128).T)
        vecs = np.stack([np.asarray(inp[k][0], f).reshape(NKC, 128) for k in
                         ("rwkv_w0", "rwkv_a0", "rwkv_k_k", "rwkv_k_a", "rwkv_r_k")], axis=0)
        m["rwkv_vecs"] = np.ascontiguousarray(vecs.transpose(2, 0, 1))
        m["rwkv_lw2"] = np.ascontiguousarray(np.concatenate([np.asarray(inp["rwkv_w_w2"][0], f), np.asarray(inp["rwkv_w_a2"][0], f)], axis=0))
        m["rwkv_gn_g"] = np.ascontiguousarray(np.asarray(inp["rwkv_gn_g"][0], f))
        m["rwkv_gn_b"] = np.ascontiguousarray(np.asarray(inp["rwkv_gn_b"][0], f))
    if 'hgrn' in kinds:
        m["hgrn_w_in"] = np.ascontiguousarray(np.asarray(inp["hgrn_w_in"][0], f))
        m["hgrn_w_out"] = np.ascontiguousarray(np.asarray(inp["hgrn_w_out"][0], f))
        m["hgrn_gn_g"] = np.ascontiguousarray(np.asarray(inp["hgrn_gn_g"][0], f))
        m["hgrn_lbl"] = np.ascontiguousarray(np.asarray(inp["hgrn_lb_logits"], f).reshape(4, NKC, 128).transpose(2, 0, 1))
    if 'gmlp' in kinds:
        m["gmlp_w_in"] = np.ascontiguousarray(np.asarray(inp["gmlp_w_in"][0], f))
        m["gmlp_w_out"] = np.ascontiguousarray(np.asarray(inp["gmlp_w_out"][0], f))
        m["gmlp_wsT"] = np.ascontiguousarray(np.asarray(inp["gmlp_w_s"][0], f).transpose(2, 0, 1))
        m["gmlp_bs"] = np.ascontiguousarray(np.asarray(inp["gmlp_b_s"][0], f))
        m["gmlp_vg"] = np.ascontiguousarray(np.asarray(inp["gmlp_v_g"][0], f))
    return m


FULL_LAYERS = [(0, 'rwkv'), (1, 'hgrn'), (2, 'conv'), (3, 'gmlp')]


def kernel(**inputs):
    x = np.asarray(inputs["x"])
    B, T, _ = x.shape
    layers = FULL_LAYERS
    bld = Builder(T, layers)
    nc = bld.build()
    in_maps = []
    zeros = None
    for c in range(8):
        if c % 2 == 0:
            in_maps.append(prep_inputs(inputs, c // 2, layers))
        else:
            if zeros is None:
                zeros = {k: np.zeros_like(v) for k, v in in_maps[0].items()}
            in_maps.append(zeros)
    res = run_bass_kernel_spmd(nc, in_maps, core_ids=list(range(8)))
    out = np.stack([np.asarray(res.results[2 * b]["yT"]).T for b in range(B)], axis=0)
    return out.astype(np.float32)
```

```python
import numpy as np
from contextlib import ExitStack
import concourse.bass as bass
import concourse.mybir as mybir
from concourse.bass_utils import run_bass_kernel_spmd

F32 = mybir.dt.float32
BF16 = mybir.dt.bfloat16
ALU = mybir.AluOpType
AF = mybir.ActivationFunctionType
AX = mybir.AxisListType

D = 1024
NKC = 8
RMS_EPS = 1e-6
GN_EPS = 64e-5


class Prog:
    LIMIT = 30000

    def __init__(self, nc, es, n_dma_sems=24):
        self.nc = nc
        self.es = es
        self.engs = {'pe': nc.tensor, 'act': nc.scalar, 'dve': nc.vector,
                     'pool': nc.gpsimd, 'sp': nc.sync}
        self.sems = {}
        self.epoch = {k: 0 for k in self.engs}
        self.cnt = {k: 0 for k in self.engs}
        for k in self.engs:
            self.sems[(k, 0)] = es.enter_context(nc.semaphore(f"s_{k}_0"))
        self.dma_sems = []
        for i in range(n_dma_sems):
            key = ('dma', i)
            self.sems[key] = es.enter_context(nc.semaphore(f"s_dma_{i}"))
            self.cnt[key] = 0
            self.dma_sems.append(key)
        self.dma_rr = 0
        self.waited = {k: {} for k in self.engs}
        self.bufs = {}
        self.n_wait = 0
        self.n_ins = 0

    def _deps(self, reads, writes):
        deps = set()
        for k in reads:
            b = self.bufs.get(k)
            if b and b['w']:
                deps.add(b['w'])
        for k in writes:
            b = self.bufs.get(k)
            if b:
                if b['w']:
                    deps.add(b['w'])
                deps.update(b['r'])
        return deps

    def _wait(self, eng, deps):
        e = self.engs[eng]
        best = {}
        for (sk, v) in deps:
            if sk[0] == eng and eng == 'pe':
                continue
            if best.get(sk, 0) < v:
                best[sk] = v
        for sk, v in best.items():
            if self.waited[eng].get(sk, 0) >= v:
                continue
            e.wait_ge(self.sems[sk], v)
            self.waited[eng][sk] = v
            self.n_wait += 1

    def _record(self, tok, reads, writes):
        for k in reads:
            b = self.bufs.setdefault(k, {'w': None, 'r': []})
            b['r'].append(tok)
            if len(b['r']) > 64:
                best = {}
                for (sk, v) in b['r']:
                    if best.get(sk, 0) < v:
                        best[sk] = v
                b['r'] = list(best.items())
        for k in writes:
            b = self.bufs.setdefault(k, {'w': None, 'r': []})
            b['w'] = tok
            b['r'] = []

    @staticmethod
    def _excl(reads, writes):
        ps = [k for k in reads if isinstance(k, str) and k.startswith('ps')]
        if ps:
            reads = [k for k in reads if k not in ps]
            writes = list(writes) + ps
        return reads, writes

    disabled = False
    recording = None
    SYNC_LAT = 0.45

    def begin_record(self):
        self.recording = []

    def flush(self):
        rec = self.recording
        self.recording = None
        if not rec:
            return
        n = len(rec)
        preds = [None] * n
        succs = [[] for _ in range(n)]
        last_w = {}
        readers = {}
        for i, (kind, eng, fn, reads, writes, cost, lat) in enumerate(rec):
            ps = set()
            for k in reads:
                w = last_w.get(k)
                if w is not None:
                    ps.add(w)
            for k in writes:
                w = last_w.get(k)
                if w is not None:
                    ps.add(w)
                ps.update(readers.get(k, ()))
            ps.discard(i)
            preds[i] = ps
            for pi in ps:
                succs[pi].append(i)
            for k in reads:
                readers.setdefault(k, []).append(i)
            for k in writes:
                last_w[k] = i
                readers[k] = []
        npred = [len(p_) for p_ in preds]
        ready = [i for i in range(n) if npred[i] == 0]
        eng_free = {}
        end_t = [0.0] * n
        done_t = [0.0] * n
        order = []
        blevel = [0.0] * n
        for i in range(n - 1, -1, -1):
            kind, eng, fn, reads, writes, cost, lat = rec[i]
            b = 0.0
            for si in succs[i]:
                v = blevel[si] + (self.SYNC_LAT if rec[si][1] != eng else 0.0)
                if v > b:
                    b = v
            blevel[i] = b + cost + lat

        def est(i):
            kind, eng, fn, reads, writes, cost, lat = rec[i]
            t = eng_free.get(eng, 0.0)
            for pi in preds[i]:
                tp = done_t[pi] + (self.SYNC_LAT if rec[pi][1] != eng else 0.0)
                if tp > t:
                    t = tp
            return t
        EPS = 0.1
        while ready:
            ests = [(est(i), i) for i in ready]
            tmin = min(ests)[0]
            best = None
            for (t, i) in ests:
                if t <= tmin + EPS:
                    if best is None or blevel[i] > blevel[best[1]] or (blevel[i] == blevel[best[1]] and i < best[1]):
                        best = (t, i)
            t1, i = best
            ready.remove(i)
            kind, eng, fn, reads, writes, cost, lat = rec[i]
            end_t[i] = t1 + cost
            done_t[i] = t1 + cost + lat
            eng_free[eng] = end_t[i]
            order.append(i)
            for si in succs[i]:
                npred[si] -= 1
                if npred[si] == 0:
                    ready.append(si)
        assert len(order) == n, (len(order), n)
        self.sched_span = max(done_t) if done_t else 0.0
        for i in order:
            kind, eng, fn, reads, writes, cost, lat = rec[i]
            if kind == 'op':
                self.op(eng, fn, reads, writes)
            else:
                out, in_, kw = fn
                self.dma(eng, out, in_, reads, writes, **kw)

    def op(self, eng, fn, reads=(), writes=(), cost=None):
        if self.disabled:
            return None
        if self.recording is not None:
            reads, writes = self._excl(reads, writes)
            if cost is None:
                cost = {'pe': 0.2, 'act': 0.45, 'dve': 0.45, 'pool': 0.7, 'sp': 0.1}[eng]
            self.recording.append(('op', eng, fn, list(reads), list(writes), cost, 0.0))
            return None
        reads, writes = self._excl(reads, writes)
        deps = self._deps(reads, writes)
        self._wait(eng, deps)
        ins = fn(self.engs[eng])
        if self.cnt[eng] >= self.LIMIT:
            self.epoch[eng] += 1
            ep = self.epoch[eng]
            self.sems[(eng, ep)] = self.es.enter_context(self.nc.semaphore(f"s_{eng}_{ep}"))
            self.cnt[eng] = 0
        sk = (eng, self.epoch[eng])
        self.cnt[eng] += 1
        ins.then_inc(self.sems[sk], 1)
        self._record((sk, self.cnt[eng]), reads, writes)
        self.n_ins += 1
        return ins

    def dma(self, eng, out, in_, reads=(), writes=(), **kw):
        if self.disabled:
            return None
        if self.recording is not None:
            self.recording.append(('dma', eng, (out, in_, kw), list(reads), list(writes), 0.15, 6.0))
            return None
        deps = self._deps(reads, writes)
        sk = self.dma_sems[self.dma_rr]
        self.dma_rr = (self.dma_rr + 1) % len(self.dma_sems)
        if self.cnt[sk] > 0:
            deps.add((sk, self.cnt[sk]))
        self._wait(eng, deps)
        ins = self.engs[eng].dma_start(out=out, in_=in_, **kw)
        self.cnt[sk] += 16
        ins.then_inc(self.sems[sk], 16)
        self._record((sk, self.cnt[sk]), reads, writes)
        self.n_ins += 1
        return ins

    def all_tokens(self):
        deps = set()
        for k, b in self.bufs.items():
            if b['w']:
                deps.add(b['w'])
            deps.update(b['r'])
        return deps

    def barrier(self):
        deps = self.all_tokens()
        for eng in self.engs:
            d = set(x for x in deps)
            self._wait(eng, d)

    def finish(self, eng='sp'):
        self._wait(eng, self.all_tokens())


class Builder:
    def __init__(self, T, layers, do_final=True, neu_dt=None):
        self.neu_dt = neu_dt if neu_dt is not None else BF16
        self.use_sched = True
        self.T = T
        self.layers = layers
        self.do_final = do_final
        self.nc = bass.Bass("TRN2", target_bir_lowering=False)
        self.inputs = {}

    def din(self, name, shape):
        t = self.nc.dram_tensor(name, list(shape), F32, kind="ExternalInput").ap()
        self.inputs[name] = t
        return t

    def sb(self, name, shape, dt=F32):
        return self.es.enter_context(self.nc.sbuf_tensor(name, list(shape), dt))

    def lsb(self, name, shape, dt=F32):
        return self.les.enter_context(self.nc.sbuf_tensor(f"{name}_{self.lname}", list(shape), dt))

    def next_ps(self):
        pool = self.ps_pool
        i = pool[self.ps_rr % len(pool)]
        self.ps_rr += 1
        return self.psums[i], f"ps{i}"

    @staticmethod
    def ecost(eng, ap):
        try:
            n = ap.free_size()
        except Exception:
            n = 256
        if eng == 'act':
            return 0.22 + n * 0.00075
        if eng == 'dve':
            return 0.2 + n * 0.00095
        if eng == 'pool':
            return 0.2 + n * 0.0021
        return 0.2

    def tt(self, eng, out, in0, in1, op, reads, writes):
        return self.p.op(eng, lambda e: e.tensor_tensor(out=out, in0=in0, in1=in1, op=op), reads, writes, cost=self.ecost(eng, out))

    def ts(self, eng, out, in0, s1, s2, op0, op1, reads, writes):
        if s2 is None:
            return self.p.op(eng, lambda e: e.tensor_scalar(out=out, in0=in0, scalar1=s1, scalar2=None, op0=op0), reads, writes, cost=self.ecost(eng, out))
        return self.p.op(eng, lambda e: e.tensor_scalar(out=out, in0=in0, scalar1=s1, scalar2=s2, op0=op0, op1=op1), reads, writes, cost=self.ecost(eng, out))

    def stt(self, eng, out, in0, scalar, in1, op0, op1, reads, writes):
        eng = 'dve'
        return self.p.op(eng, lambda e: e.scalar_tensor_tensor(out=out, in0=in0, scalar=scalar, in1=in1, op0=op0, op1=op1), reads, writes, cost=self.ecost(eng, out))

    def act(self, out, in_, func, reads, writes, bias=None, scale=1.0, accum_out=None):
        kw = {}
        if bias is not None:
            kw['bias'] = bias
        if accum_out is not None:
            kw['accum_out'] = accum_out
        return self.p.op('act', lambda e: e.activation(out=out, in_=in_, func=func, scale=scale, **kw), reads, writes, cost=self.ecost('act', in_))

    def mm(self, out, lhsT, rhs, start, stop, reads, writes):
        try:
            n = rhs.free_size()
        except Exception:
            n = 128
        c = 0.06 + n / 2400.0 * (1.0 if lhsT.dtype == BF16 else 2.4)
        return self.p.op('pe', lambda e: e.matmul(out, lhsT=lhsT, rhs=rhs, start=start, stop=stop), reads, writes, cost=c)

    def copy(self, eng, out, in_, reads, writes):
        if eng == 'act':
            return self.p.op('act', lambda e: e.copy(out=out, in_=in_), reads, writes, cost=self.ecost('act', out))
        return self.p.op(eng, lambda e: e.tensor_copy(out=out, in_=in_), reads, writes, cost=self.ecost(eng, out))

    def load_weight_bf16(self, dst, dst_key, src, ncols, src_c0=0, dst_c0=0, scale_bc=None):
        p = self.p
        CH = 1024 if ncols % 1024 == 0 else ncols
        for kc in range(NKC):
            for c0 in range(0, ncols, CH):
                i = self.stage_i
                self.stage_i += 1
                nst = len(self.stage)
                st = self.stage[i % nst]
                sk = f"stage{i % nst}"
                p.dma('sp', st[:, 0:CH], src[kc * 128:(kc + 1) * 128, src_c0 + c0:src_c0 + c0 + CH], reads=[], writes=[sk])
                eng = ['dve', 'act'][i % 2] if scale_bc is None else ['dve', 'pool'][i % 2]
                if scale_bc is None:
                    self.copy(eng, dst[:, kc, dst_c0 + c0:dst_c0 + c0 + CH], st[:, 0:CH], [sk], [dst_key])
                else:
                    self.tt(eng, dst[:, kc, dst_c0 + c0:dst_c0 + c0 + CH], st[:, 0:CH], scale_bc[:, c0:c0 + CH], ALU.mult,
                            [sk, 'bc_tiles'], [dst_key])

    def rms_rstd(self, src, src_key, TT, tag):
        bi = self.rms_i % len(self.sqb_l)
        self.rms_i += 1
        sqb, rstd = self.sqb_l[bi], self.rstd_l[bi]
        ksq, krs = f'sqb{bi}', f'rstd{bi}'
        if self.sqb_alias:
            ksq = 'yT0'
        ps, pk = self.next_ps()
        if self.sqb_alias:
            for kc in range(NKC):
                sl = kc % 2
                if kc % 2 == 0:
                    self.act(sqb[:, sl, :TT], src[:, kc, :TT], AF.Square, [src_key], [('sqs', sl)])
                else:
                    self.tt('dve', sqb[:, sl, :TT], src[:, kc, :TT], src[:, kc, :TT], ALU.mult, [src_key], [('sqs', sl)])
                self.mm(ps[:, :TT], self.ones_bf[:], sqb[:, sl, :TT], kc == 0, kc == NKC - 1, [('sqs', sl), 'ones_bf'], [pk])
        else:
            for kc in range(NKC):
                if kc % 2 == 0:
                    self.act(sqb[:, kc, :TT], src[:, kc, :TT], AF.Square, [src_key], [(ksq, kc)])
                else:
                    self.tt('dve', sqb[:, kc, :TT], src[:, kc, :TT], src[:, kc, :TT], ALU.mult, [src_key], [(ksq, kc)])
            for kc in range(NKC):
                self.mm(ps[:, :TT], self.ones_bf[:], sqb[:, kc, :TT], kc == 0, kc == NKC - 1, [(ksq, kc), 'ones_bf'], [pk])
        self.act(rstd[:, :TT], ps[:, :TT], AF.Ln, [pk, 'consts'], [krs], bias=self.epsc[:, 0:1], scale=1.0 / D)
        self.act(rstd[:, :TT], rstd[:, :TT], AF.Exp, [krs], [krs], scale=-0.5)
        return rstd, krs

    def run_layer(self, li, kind, TT, w_in_cols, mixer_setup, mixer_tile, is_last):
        p = self.p
        T = self.T
        ntiles = T // TT
        with ExitStack() as les:
            self.les = les
            self.lname = f"L{li}"
            self.TT = TT
            self.W_in = self.lsb("W_in", [128, NKC, w_in_cols], BF16)
            self.W_out = self.lsb("W_out", [128, NKC, D], BF16)
            ndb = 2 if kind != 'rwkv' else 1
            self.hT = [self.lsb(f"hT{i}", [128, NKC, TT], F32) for i in range(2)]
            self.sqb_alias = (kind == 'rwkv')
            if not self.sqb_alias:
                self.sqb_l = [self.lsb(f"sqb{i}", [128, NKC, TT], BF16) for i in range(ndb)]
            self.rstd_l = [self.lsb(f"rstd{i}", [128, TT], F32) for i in range(ndb)]
            self.rms_i = 0
            self.hn_l = [self.lsb(f"hn{i}", [128, NKC, TT + 1], BF16) for i in range(ndb)]
            self.yTt_l = [self.lsb(f"yTt{i}", [128, NKC, TT], BF16) for i in range(ndb)]
            self.hn, self.hnk = self.hn_l[0], 'hn0'
            self.yTt, self.yTk = self.yTt_l[0], 'yT0'
            if self.sqb_alias:
                self.sqb_l = [self.lsb("sqs", [128, 2, TT], BF16)]
            self.stage_i = 0
            loader = mixer_setup()
            with ExitStack() as ses:
                nst = 2 if kind == 'rwkv' else max(2, min(4, (self.nc.sbuf_bytes_remaining - 512) // 4096))
                self.stage = [ses.enter_context(self.nc.sbuf_tensor(f"stage{i}_{self.lname}", [128, 1024], F32)) for i in range(nst)]
                if loader is None:
                    self.load_weight_bf16(self.W_in, 'W_in', self.w_in_dram[kind], w_in_cols)
                    self.load_weight_bf16(self.W_out, 'W_out', self.w_out_dram[kind], D)
                else:
                    loader(ses)
                p.barrier()
            if getattr(self, 'post_setup', None) is not None:
                self.post_setup()
                self.post_setup = None
            hn0 = self.hn_l[0]
            p.op('pool', lambda e: e.memset(hn0[:, :, 0:1], 0.0), [], ['hn0'])

            def load(ti):
                buf = self.hT[ti % len(self.hT)]
                src = self.xT if self.first_layer else self.yT
                p.dma('sp', buf[:], src.rearrange("(c p) t -> p c t", p=128)[:, :, ti * TT:(ti + 1) * TT],
                      reads=[('hd', ti * TT // 128 + i) for i in range(TT // 128)], writes=[f"hT{ti % len(self.hT)}"])

            if self.use_sched:
                p.begin_record()
            load(0)
            for ti in range(ntiles):
                if len(self.hT) > 1:
                    if ti + 1 < ntiles:
                        load(ti + 1)
                elif ti > 0:
                    load(ti)
                h = self.hT[ti % len(self.hT)]
                hk = f"hT{ti % len(self.hT)}"
                rstd, rk = self.rms_rstd(h, hk, TT, 'in')
                g = self.norm_g
                prev_hn, prev_hnk = self.hn, self.hnk
                bi = ti % ndb
                self.hn, self.hnk = self.hn_l[bi], f'hn{bi}'
                self.yTt, self.yTk = self.yTt_l[bi], f'yT{bi}'
                if ti > 0:
                    self.copy('pool', self.hn[:, :, 0:1], prev_hn[:, :, TT:TT + 1], [prev_hnk], [self.hnk])
                for kc in range(NKC):
                    self.stt('dve', self.hn[:, kc, 1:TT + 1], h[:, kc, :], g[:, li, kc:kc + 1], rstd[:, :TT],
                             ALU.mult, ALU.mult, [hk, rk, 'consts'], [self.hnk])
                mixer_tile(ti)
                for j in range(NKC):
                    ps, pk = self.next_ps()
                    for kc in range(NKC):
                        self.mm(ps[:, :TT], self.W_out[:, kc, j * 128:(j + 1) * 128], self.yTt[:, kc, :TT],
                                kc == 0, kc == NKC - 1, ['W_out', self.yTk], [pk])
                    self.tt('dve', h[:, j, :], h[:, j, :], ps[:, :TT], ALU.add, [hk, pk], [hk])
                if is_last and self.do_final:
                    rstd, rk = self.rms_rstd(h, hk, TT, 'fin')
                    for kc in range(NKC):
                        self.stt('dve' if kc % 2 == 0 else 'pool', h[:, kc, :], h[:, kc, :], self.final_g[:, kc:kc + 1], rstd[:, :TT],
                                 ALU.mult, ALU.mult, [hk, rk, 'consts'], [hk])
                p.dma('sp', self.yT.rearrange("(c p) t -> p c t", p=128)[:, :, ti * TT:(ti + 1) * TT], h[:],
                      reads=[hk], writes=[('hd', (ti * TT) // 128 + i) for i in range(max(1, TT // 128))])
            if self.use_sched:
                p.flush()
            self.first_layer = False
            p.barrier()
        self.les = None

    def conv_layer(self, li, is_last):
        TT = 512

        def setup():
            self.yext = self.lsb("yext", [128, NKC, TT + 2], F32)
            self.zs = [self.lsb(f"zs{i}", [128, TT], F32) for i in range(2)]
            self.acc = [self.lsb(f"acc{i}", [128, TT], F32) for i in range(2)]
            self.sg = [self.lsb(f"sg{i}", [128, TT], F32) for i in range(2)]
            self.p.op('pool', lambda e: e.memset(self.yext[:, :, 0:2], 0.0), [], [('yext', j) for j in range(NKC)])

        def tile(ti):
            W = self.W_in
            cw = self.conv_w
            for j in range(NKC):
                zs, acc, sg = self.zs[j % 2], self.acc[j % 2], self.sg[j % 2]
                zk, ak, gk = f"zs{j % 2}", f"acc{j % 2}", f"sg{j % 2}"
                pss = []
                for blk in range(4):
                    ps, pk = self.next_ps()
                    col0 = blk * D + j * 128
                    for kc in range(NKC):
                        self.mm(ps[:, :TT], W[:, kc, col0:col0 + 128], self.hn[:, kc, 1:TT + 1], kc == 0, kc == NKC - 1,
                                ['W_in', self.hnk], [pk])
                    pss.append((ps, pk))
                (pb, pbk), (pc, pck), (pz, pzk), (pg, pgk) = pss
                yk = ('yext', j)
                self.copy('act', zs[:], pz[:, :TT], [pzk], [zk])
                if ti > 0:
                    self.copy('pool', self.yext[:, j, 0:2], self.yext[:, j, TT:TT + 2], [yk], [yk])
                self.tt('dve', self.yext[:, j, 2:TT + 2], pc[:, :TT], zs[:], ALU.mult, [pck, zk], [yk])
                self.act(acc[:], self.yext[:, j, 2:TT + 2], AF.Copy, [yk, 'consts'], [ak], scale=cw[:, j, 2:3])
                self.stt('pool', acc[:], self.yext[:, j, 1:TT + 1], cw[:, j, 1:2], acc[:], ALU.mult, ALU.add, [yk, ak, 'consts'], [ak])
                self.stt('pool', acc[:], self.yext[:, j, 0:TT], cw[:, j, 0:1], acc[:], ALU.mult, ALU.add, [yk, ak, 'consts'], [ak])
                self.act(sg[:], pg[:, :TT], AF.Silu, [pgk], [gk])
                self.tt('dve', acc[:], pb[:, :TT], acc[:], ALU.mult, [pbk, ak], [ak])
                self.tt('pool', self.yTt[:, j, :], acc[:], sg[:], ALU.mult, [ak, gk], [self.yTk])

        self.run_layer(li, 'conv', TT, 4 * D, setup, tile, is_last)

    def gmlp_layer(self, li, is_last):
        TT = 512

        def setup():
            p = self.p
            self.wsT = self.lsb("wsT", [128, 8, 128], F32)
            self.bs_bc = self.lsb("bs_bc", [128, 8, TT], F32)
            self.vg_bc = self.lsb("vg_bc", [128, D], F32)
            self.vn = [self.lsb(f"vn{i}", [128, D], F32) for i in range(TT // 128)]
            self.vss = self.lsb("vss", [128, 4], F32)
            self.junk = self.lsb("junk", [128, 512], F32)
            self.s_sb = [self.lsb(f"s_sb{i}", [128, TT], F32) for i in range(2)]
            self.sg = [self.lsb(f"sg{i}", [128, TT], F32) for i in range(2)]
            p.dma('sp', self.wsT[:], self.inputs['gmlp_wsT'], [], ['wsT'])
            for g in range(8):
                p.op('pool', lambda e: e.affine_select(out=self.wsT[:, g, :], in_=self.wsT[:, g, :], pattern=[[1, 128]],
                                                       compare_op=ALU.is_ge, fill=0.0, base=0, channel_multiplier=-1),
                     ['wsT'], ['wsT'])
            for r in range(TT // 128):
                p.dma('sp', self.bs_bc[:, :, r * 128:(r + 1) * 128],
                      self.inputs['gmlp_bs'].partition_broadcast(128), [], ['bs_bc'])
            p.dma('sp', self.vg_bc[:], self.inputs['gmlp_vg'].partition_broadcast(128), [], ['vg_bc'])

        def tile(ti):
            W = self.W_in
            nblk = TT // 128
            for blk in range(nblk):
                vn = self.vn[blk]
                vk = f"vn{blk}"
                halves = []
                for hf in range(2):
                    ps, pk = self.next_ps()
                    for kc in range(NKC):
                        self.mm(ps[:, :512], self.hn[:, kc, 1 + blk * 128:1 + (blk + 1) * 128],
                                W[:, kc, D + hf * 512:D + (hf + 1) * 512], kc == 0, kc == NKC - 1, ['W_in', self.hnk], [pk])
                    halves.append((ps, pk))
                for hf, (ps, pk) in enumerate(halves):
                    self.act(self.junk[:], ps[:, :512], AF.Square, [pk], ['junk', 'vss'], accum_out=self.vss[:, hf:hf + 1])
                self.tt('dve', self.vss[:, 2:3], self.vss[:, 0:1], self.vss[:, 1:2], ALU.add, ['vss'], ['vss'])
                self.act(self.vss[:, 3:4], self.vss[:, 2:3], AF.Ln, ['vss', 'consts'], ['vss'], bias=self.epsc[:, 0:1], scale=1.0 / D)
                self.act(self.vss[:, 3:4], self.vss[:, 3:4], AF.Exp, ['vss'], ['vss'], scale=-0.5)
                for hf, (ps, pk) in enumerate(halves):
                    self.stt('dve', vn[:, hf * 512:(hf + 1) * 512], ps[:, :512], self.vss[:, 3:4],
                             self.vg_bc[:, hf * 512:(hf + 1) * 512], ALU.mult, ALU.mult, [pk, 'vss', 'vg_bc'], [vk])
            for j in range(NKC):
                s_sb, sg = self.s_sb[j % 2], self.sg[j % 2]
                sk, gk = f"s_sb{j % 2}", f"sg{j % 2}"
                ps, pk = self.next_ps()
                for blk in range(nblk):
                    self.mm(ps[:, blk * 128:(blk + 1) * 128], self.vn[blk][:, j * 128:(j + 1) * 128], self.wsT[:, j, :], True, True,
                            [f"vn{blk}", 'wsT'], [pk])
                self.tt('dve', s_sb[:], ps[:, :TT], self.bs_bc[:, j, :], ALU.add, [pk, 'bs_bc'], [sk])
                pu, puk = self.next_ps()
                for kc in range(NKC):
                    self.mm(pu[:, :TT], W[:, kc, j * 128:(j + 1) * 128], self.hn[:, kc, 1:TT + 1], kc == 0, kc == NKC - 1,
                            ['W_in', self.hnk], [puk])
                pg, pgk = self.next_ps()
                for kc in range(NKC):
                    self.mm(pg[:, :TT], W[:, kc, 2 * D + j * 128:2 * D + (j + 1) * 128], self.hn[:, kc, 1:TT + 1], kc == 0,
                            kc == NKC - 1, ['W_in', self.hnk], [pgk])
                self.act(sg[:], pg[:, :TT], AF.Silu, [pgk], [gk])
                self.tt('dve', s_sb[:], pu[:, :TT], s_sb[:], ALU.mult, [puk, sk], [sk])
                self.tt('pool', self.yTt[:, j, :], s_sb[:], sg[:], ALU.mult, [sk, gk], [self.yTk])

        self.run_layer(li, 'gmlp', TT, 3 * D, setup, tile, is_last)


    def make_ident(self, ident, key):
        p = self.p
        p.op('pool', lambda e: e.memset(ident[:], 1.0), [], [key])
        p.op('pool', lambda e: e.affine_select(out=ident[:], in_=ident[:], pattern=[[-1, 128]], compare_op=ALU.is_equal,
                                               fill=0.0, base=0, channel_multiplier=1), [key], [key])

    def make_block_masks(self, C, maskT, colmask, rowmask, strict=False):
        p = self.p
        nch = 128 // C
        if maskT is not None:
            p.op('pool', lambda e: e.memset(maskT[:], 1.0), [], ['masks'])
            p.op('pool', lambda e: e.affine_select(out=maskT[:], in_=maskT[:], pattern=[[1, 128]], compare_op=ALU.is_ge if not strict else ALU.is_gt,
                                                   fill=0.0, base=0, channel_multiplier=-1), ['masks'], ['masks'])
            for c in range(1, nch):
                p.op('pool', lambda e, c=c: e.affine_select(out=maskT[:, c * C:(c + 1) * C], in_=maskT[:, c * C:(c + 1) * C], pattern=[[0, C]],
                                                            compare_op=ALU.is_ge, fill=0.0, base=-c * C, channel_multiplier=1), ['masks'], ['masks'])
        if colmask is not None:
            p.op('pool', lambda e: e.memset(colmask[:], 0.0), [], ['masks'])
            for c in range(nch):
                p.op('pool', lambda e, c=c: e.memset(colmask[:, c, c * C:(c + 1) * C], 1.0), ['masks'], ['masks'])
        if rowmask is not None:
            p.op('pool', lambda e: e.memset(rowmask[:], 1.0), [], ['masks'])
            for c in range(nch):
                p.op('pool', lambda e, c=c: e.affine_select(out=rowmask[:, c:c + 1], in_=rowmask[:, c:c + 1], pattern=[[0, 1]],
                                                            compare_op=ALU.is_ge, fill=0.0, base=-c * C, channel_multiplier=1), ['masks'], ['masks'])
                p.op('pool', lambda e, c=c: e.affine_select(out=rowmask[:, c:c + 1], in_=rowmask[:, c:c + 1], pattern=[[0, 1]],
                                                            compare_op=ALU.is_ge, fill=0.0, base=c * C + C - 1, channel_multiplier=-1), ['masks'], ['masks'])

    def hgrn_layer(self, li, is_last):
        TT = 256
        C = 32
        NB = TT // 128
        NCH = TT // C

        def setup():
            p = self.p
            L = self.lsb
            self.ident = L("ident", [128, 128], F32)
            self.make_ident(self.ident, 'ident')
            self.maskT = L("maskT", [128, 128], F32)
            self.colmask = L("colmask", [128, 4, 128], F32)
            self.rowmask = L("rowmask", [128, 4], F32)
            self.make_block_masks(C, self.maskT, self.colmask, self.rowmask)
            self.resetm = L("resetm", [128, TT], F32)
            self.ones_t = L("ones_t", [128, TT], F32)
            p.op('pool', lambda e: e.memset(self.ones_t[:], 1.0), [], ['masks'])
            p.op('pool', lambda e: e.memset(self.resetm[:], 1.0), [], ['masks'])
            p.op('pool', lambda e: e.memset(self.resetm[:].rearrange("p (n c) -> p n c", c=C)[:, :, 0:1], 0.0), ['masks'], ['masks'])
            self.gn_bc = L("gn_bc", [128, D], F32)
            p.dma('sp', self.gn_bc[:], self.inputs['hgrn_gn_g'].partition_broadcast(128), [], ['gn_bc'])
            self.lbl = L("lbl", [128, 4, NKC], F32)
            self.lbt = L("lbt", [128, 4, NKC], F32)
            p.dma('sp', self.lbl[:], self.inputs['hgrn_lbl'], [], ['lbl'])
            self.act(self.lbl[:], self.lbl[:], AF.Exp, ['lbl'], ['lbl'])
            self.tt('dve', self.lbt[:, 0, :], self.lbl[:, 0, :], self.lbl[:, 1, :], ALU.add, ['lbl'], ['lbt'])
            self.tt('dve', self.lbt[:, 0, :], self.lbt[:, 0, :], self.lbl[:, 2, :], ALU.add, ['lbl', 'lbt'], ['lbt'])
            self.tt('dve', self.lbt[:, 0, :], self.lbt[:, 0, :], self.lbl[:, 3, :], ALU.add, ['lbl', 'lbt'], ['lbt'])
            p.op('dve', lambda e: e.reciprocal(out=self.lbt[:, 3, :], in_=self.lbt[:, 0, :]), ['lbt'], ['lbt'])
            p.op('dve', lambda e: e.memset(self.lbt[:, 1, :], 0.0), ['lbt'], ['lbt'])
            for i in range(1, li + 1):
                self.tt('dve', self.lbt[:, 1, :], self.lbt[:, 1, :], self.lbl[:, i, :], ALU.add, ['lbl', 'lbt'], ['lbt'])
            self.tt('dve', self.lbt[:, 1, :], self.lbt[:, 1, :], self.lbt[:, 3, :], ALU.mult, ['lbt'], ['lbt'])
            self.ts('dve', self.lbt[:, 2, :], self.lbt[:, 1, :], -1.0, 1.0, ALU.mult, ALU.add, ['lbt'], ['lbt'])
            self.S = L("S_hgrn", [128, NKC, 128], F32)
            p.op('pool', lambda e: e.memset(self.S[:], 0.0), [], [('S', j) for j in range(NKC)])
            names = ['f', 'kk', 'bb', 'qe', 'dd', 'sg']
            self.tmps = []
            for q in range(2):
                tm = {n: L(f"h_{n}{q}", [128, TT], F32) for n in names}
                tm['e1'] = tm['f']
                tm['ko'] = tm['dd']
                tm['ke_bf'] = L(f"h_ke_bf{q}", [128, TT], BF16)
                tm['qe_bf'] = L(f"h_qe_bf{q}", [128, TT], BF16)
                tm['kom'] = L(f"kom{q}", [128, 4, NB, 128], BF16)
                self.tmps.append(tm)
            self.sgate = [L(f"sgate{q}", [128, TT], BF16) for q in range(3)]
            self.qem = [L(f"qem{q}", [128, 4, TT], BF16) for q in range(3)]
            self.v_bf = [L(f"v_bf{q}", [128, NB, 128], BF16) for q in range(3)]
            self.attm = [L(f"attm{q}", [128, NB, 128], BF16) for q in range(3)]
            self.u_sb = [L(f"u_sb{q}", [128, NCH, 128], F32) for q in range(3)]
            self.dec = [L(f"dec{q}", [128, NCH], F32) for q in range(3)]
            self.S_all2 = [L(f"S_all{q}", [128, 5, 128], F32) for q in range(2)]
            self.S_bf2 = [L(f"S_bf{q}", [128, NCH, 128], BF16) for q in range(2)]
            self.on2 = [L(f"on{q}", [128, NB, 128], F32) for q in range(2)]
            self.oss2 = [L(f"oss{q}", [128, 2 * NB], F32) for q in range(2)]
            self.junk2 = [L(f"junk{q}", [128, 128], F32) for q in range(2)]
            self.ps_pool = [0, 1, 2, 3]
            self.nm_rr = 0

        def nm_ps():
            i = [4, 5][self.nm_rr % 2]
            self.nm_rr += 1
            return self.psums[i], f"ps{i}"

        def proj(col0):
            ps, pk = self.next_ps()
            for kc in range(NKC):
                self.mm(ps[:, :TT], self.W_in[:, kc, col0:col0 + 128], self.hn[:, kc, 1:TT + 1], kc == 0, kc == NKC - 1, ['W_in', self.hnk], [pk])
            return ps, pk

        def A_gen(j):
            q2 = j % 2
            t = self.tmps[q2]
            kom = t['kom']
            W = self.W_in
            q = j % 3
            lb, oml = self.lbt[:, 1, :], self.lbt[:, 2, :]
            sgate, qem, v_bf, attm, u_sb, dec = self.sgate[q], self.qem[q], self.v_bf[q], self.attm[q], self.u_sb[q], self.dec[q]
            ksg, kqem, kv, katt, ku, kdec = f'sgate{q}', f'qem{q}', f'v_bf{q}', f'attm{q}', f'u_sb{q}', f'dec{q}'
            pf, pfk = proj(D + j * 128)
            self.act(t['f'][:], pf[:, :TT], AF.Exp, [pfk], [f't_f{q2}'], scale=-1.0)
            self.tt('pool', t['f'][:], t['f'][:], self.ones_t[:], ALU.add, [f't_f{q2}', 'masks'], [f't_f{q2}'])
            self.p.op('dve', lambda e: e.reciprocal(out=t['f'][:], in_=t['f'][:]), [f't_f{q2}'], [f't_f{q2}'], cost=0.45)
            self.ts('dve', t['f'][:], t['f'][:], oml[:, j:j + 1], lb[:, j:j + 1], ALU.mult, ALU.add, [f't_f{q2}', 'lbt'], [f't_f{q2}'])
            self.act(t['kk'][:], t['f'][:], AF.Identity, [f't_f{q2}'], [f't_kk{q2}'], scale=-1.0, bias=self.epsc[:, 2:3])
            self.act(t['dd'][:], t['f'][:], AF.Ln, [f't_f{q2}'], [f't_dd{q2}'])
            self.p.op('dve', lambda e: e.tensor_tensor_scan(out=t['bb'][:], data0=self.resetm[:], data1=t['dd'][:], initial=0.0,
                                                            op0=ALU.mult, op1=ALU.add), [f't_dd{q2}', 'masks'], [f't_bb{q2}'])
            yield
            pq, pqk = proj(j * 128)
            self.act(t['e1'][:], t['bb'][:], AF.Exp, [f't_bb{q2}'], [f't_f{q2}'])
            self.tt('dve', t['qe'][:], pq[:, :TT], t['e1'][:], ALU.mult, [pqk, f't_f{q2}'], [f't_qe{q2}'])
            self.copy('act', t['qe_bf'][:], t['qe'][:], [f't_qe{q2}'], [f't_qe_bf{q2}'])
            qe4 = t['qe'][:].rearrange("p (b t) -> p b t", t=128)
            for c in range(4):
                self.tt('pool', qem[:, c, :].rearrange("p (b t) -> p b t", t=128), qe4,
                        self.colmask[:, c:c + 1, :].to_broadcast([128, NB, 128]), ALU.mult, [f't_qe{q2}', 'masks'], [kqem])
            yield
            self.act(t['e1'][:], t['bb'][:], AF.Exp, [f't_bb{q2}'], [f't_f{q2}'], scale=-1.0)
            self.tt('pool', t['ke_bf'][:], t['kk'][:], t['e1'][:], ALU.mult, [f't_kk{q2}', f't_f{q2}'], [f't_ke_bf{q2}'])
            b3 = t['bb'][:].rearrange("p (n c) -> p n c", c=C)
            self.act(dec[:], b3[:, :, C - 1], AF.Exp, [f't_bb{q2}'], [kdec])
            self.tt('pool', t['dd'][:].rearrange("p (n c) -> p n c", c=C), b3[:, :, C - 1:C].to_broadcast([128, NCH, C]), b3, ALU.subtract,
                    [f't_bb{q2}'], [f't_dd{q2}'])
            self.act(t['dd'][:], t['dd'][:], AF.Exp, [f't_dd{q2}'], [f't_dd{q2}'])
            self.tt('pool', t['ko'][:], t['kk'][:], t['dd'][:], ALU.mult, [f't_kk{q2}', f't_dd{q2}'], [f't_dd{q2}'])
            pg, pgk = proj(3 * D + j * 128)
            self.act(t['sg'][:], pg[:, :TT], AF.Exp, [pgk], [f't_sg{q2}'], scale=-1.0)
            self.tt('pool', t['sg'][:], t['sg'][:], self.ones_t[:], ALU.add, [f't_sg{q2}', 'masks'], [f't_sg{q2}'])
            self.p.op('dve', lambda e: e.reciprocal(out=t['sg'][:], in_=t['sg'][:]), [f't_sg{q2}'], [f't_sg{q2}'], cost=0.45)
            self.tt('dve', sgate[:], pg[:, :TT], t['sg'][:], ALU.mult, [pgk, f't_sg{q2}'], [ksg])
            yield
            pv, pvk = self.next_ps()
            for blk in range(NB):
                for kc in range(NKC):
                    self.mm(pv[:, blk * 128:(blk + 1) * 128], self.hn[:, kc, 1 + blk * 128:1 + (blk + 1) * 128],
                            W[:, kc, 2 * D + j * 128:2 * D + (j + 1) * 128], kc == 0, kc == NKC - 1, ['W_in', self.hnk], [pvk])
            self.copy('act', v_bf[:].rearrange("p b v -> p (b v)"), pv[:, :TT], [pvk], [kv])
            yield
            ps, pk = nm_ps()
            for blk in range(NB):
                cs = slice(blk * 128, (blk + 1) * 128)
                self.mm(ps[:, cs], t['ke_bf'][:, cs], t['qe_bf'][:, cs], True, True, [f't_ke_bf{q2}', f't_qe_bf{q2}'], [pk])
            self.tt('dve', attm[:], ps[:, :TT].rearrange("p (b t) -> p b t", t=128), self.maskT[:, None, :].to_broadcast([128, NB, 128]),
                    ALU.mult, [pk, 'masks'], [katt])
            ps, pk = nm_ps()
            for blk in range(NB):
                cs = slice(blk * 128, (blk + 1) * 128)
                self.p.op('pe', lambda e, ps=ps, cs=cs: e.transpose(ps[:, cs], t['ko'][:, cs], self.ident[:]), [f't_dd{q2}', 'ident'], [pk])
            for c in range(4):
                self.act(kom[:, c, :, :].rearrange("p b k -> p (b k)"), ps[:, :TT], AF.Copy, [pk, 'masks'], [f'kom{q2}'],
                         scale=self.rowmask[:, c:c + 1])
            yield
            for blk in range(NB):
                ps, pk = nm_ps()
                for c in range(4):
                    self.mm(ps[:, c * 128:(c + 1) * 128], kom[:, c, blk, :], v_bf[:, blk, :], True, True, [f'kom{q2}', kv], [pk])
                self.copy('act' if blk % 2 else 'dve', u_sb[:, blk * 4:(blk + 1) * 4, :].rearrange("p c v -> p (c v)"), ps[:, 0:512], [pk], [ku])
                if blk % 2:
                    yield

        def B_gen(j):
            P = self.psums
            q = j % 3
            sgate, qem, v_bf, attm, u_sb, dec = self.sgate[q], self.qem[q], self.v_bf[q], self.attm[q], self.u_sb[q], self.dec[q]
            ksg, kqem, kv, katt, ku, kdec = f'sgate{q}', f'qem{q}', f'v_bf{q}', f'attm{q}', f'u_sb{q}', f'dec{q}'
            kS = ('S', j)
            q2 = j % 2
            SA = self.S_all2[q2]
            S_bf, on, oss, junk = self.S_bf2[q2], self.on2[q2], self.oss2[q2], self.junk2[q2]
            kSA, kSbf, kon, koss, kjunk = f'S_all{q2}', f'S_bf{q2}', f'on{q2}', f'oss{q2}', f'junk{q2}'
            self.copy('pool', SA[:, 0, :], self.S[:, j, :], [kS], [kSA])
            for blk in range(NB):
                for c in range(4):
                    n = blk * 4 + c
                    self.stt('dve', SA[:, c + 1, :], SA[:, c, :], dec[:, n:n + 1], u_sb[:, n, :], ALU.mult, ALU.add, [kSA, kdec, ku], [kSA])
                self.copy('act', S_bf[:, blk * 4:(blk + 1) * 4, :].rearrange("p c v -> p (c v)"),
                          SA[:, 0:4, :].rearrange("p c v -> p (c v)"), [kSA], [kSbf])
                if blk < NB - 1:
                    self.copy('dve', SA[:, 0, :], SA[:, 4, :], [kSA], [kSA])
                yield
            self.copy('pool', self.S[:, j, :], SA[:, 4, :], [kSA], [kS])
            po, pok = P[6], 'ps6'
            for blk in range(NB):
                cs = slice(blk * 128, (blk + 1) * 128)
                self.mm(po[:, cs], attm[:, blk, :], v_bf[:, blk, :], True, False, [katt, kv], [pok])
                for c in range(4):
                    self.mm(po[:, cs], qem[:, c, cs], S_bf[:, blk * 4 + c, :], False, c == 3, [kqem, kSbf], [pok])
                if blk % 2:
                    yield
            for blk in range(NB):
                cs = slice(blk * 128, (blk + 1) * 128)
                self.act(junk[:], po[:, cs], AF.Square, [pok], [kjunk, koss], accum_out=oss[:, blk:blk + 1])
            self.act(oss[:, NB:2 * NB], oss[:, 0:NB], AF.Ln, [koss, 'consts'], [koss], bias=self.epsc[:, 0:1], scale=1.0 / 128)
            self.act(oss[:, NB:2 * NB], oss[:, NB:2 * NB], AF.Exp, [koss], [koss], scale=-0.5)
            self.tt('dve', on[:], po[:, :TT].rearrange("p (b v) -> p b v", v=128),
                    oss[:, NB:2 * NB, None].to_broadcast([128, NB, 128]), ALU.mult, [pok, koss], [kon])
            self.tt('pool', on[:], on[:], self.gn_bc[:, None, j * 128:(j + 1) * 128].to_broadcast([128, NB, 128]), ALU.mult,
                    [kon, 'gn_bc'], [kon])
            yield
            py, pyk = P[7], 'ps7'
            for blk in range(NB):
                cs = slice(blk * 128, (blk + 1) * 128)
                self.p.op('pe', lambda e, cs=cs, blk=blk: e.transpose(py[:, cs], on[:, blk, :], self.ident[:]), [kon, 'ident'], [pyk])
            self.tt('dve', self.yTt[:, j, :], py[:, :TT], sgate[:], ALU.mult, [pyk, ksg], [self.yTk])
            yield

        def drive(gens):
            gens = [g for g in gens if g is not None]
            while gens:
                for g in list(gens):
                    try:
                        next(g)
                    except StopIteration:
                        gens.remove(g)

        def step(g):
            try:
                next(g)
                return True
            except StopIteration:
                return False

        def tile(ti):
            A = {0: A_gen(0), 1: A_gen(1)}
            while step(A[0]):
                step(A[1])
            for sl in range(NKC):
                must = [B_gen(sl)]
                if sl + 1 < NKC:
                    must.append(A[sl + 1])
                opt = None
                if sl + 2 < NKC:
                    A[sl + 2] = A_gen(sl + 2)
                    opt = A[sl + 2]
                while must:
                    for g in list(must):
                        if not step(g):
                            must.remove(g)
                    if opt is not None and not step(opt):
                        opt = None

        self.run_layer(li, 'hgrn', TT, 4 * D, setup, tile, is_last)
        self.ps_pool = list(range(8))

    def rwkv_layer(self, li, is_last):
        TT = 256
        NB = TT // 128
        WC = 3200
        NDT = self.neu_dt
        LC = -0.6065306597126334

        def setup():
            p = self.p
            L = self.lsb
            self.ident = L("ident", [128, 128], F32)
            self.make_ident(self.ident, 'ident')
            self.ident_n = L("ident_n", [128, 128], NDT)
            self.copy('dve', self.ident_n[:], self.ident[:], ['ident'], ['ident'])
            self.maskS = L("maskS", [128, 128], F32)
            self.maskI = L("maskI", [128, 128], F32)
            self.maskSL = L("maskSL", [128, 128], F32)
            for (m, pat, cm, cmp_) in ((self.maskS, 1, -1, ALU.is_gt), (self.maskI, 1, -1, ALU.is_ge), (self.maskSL, -1, 1, ALU.is_gt)):
                p.op('pool', lambda e, m=m: e.memset(m[:], 1.0), [], ['masks'])
                p.op('pool', lambda e, m=m, pat=pat, cm=cm, cmp_=cmp_: e.affine_select(
                    out=m[:], in_=m[:], pattern=[[pat, 128]], compare_op=cmp_, fill=0.0, base=0, channel_multiplier=cm), ['masks'], ['masks'])
            self.blockones = L("blockones", [128, 128], F32)
            p.op('pool', lambda e: e.memset(self.blockones[:], 1.0), [], ['masks'])
            p.op('pool', lambda e: e.memset(self.blockones[0:64, 64:128], 0.0), ['masks'], ['masks'])
            p.op('pool', lambda e: e.memset(self.blockones[64:128, 0:64], 0.0), ['masks'], ['masks'])
            self.resetm = L("resetm", [128, TT], F32)
            p.op('pool', lambda e: e.memset(self.resetm[:], 1.0), [], ['masks'])
            p.op('pool', lambda e: e.memset(self.resetm[:].rearrange("p (n c) -> p n c", c=128)[:, :, 0:1], 0.0), ['masks'], ['masks'])
            p.op('pool', lambda e: e.memset(self.epsc[:, 1:2], GN_EPS), [], ['consts'])
            self.mu_fm = L("mu_fm", [128, 33], F32)
            self.omu_fm = L("omu_fm", [128, 33], F32)
            p.dma('sp', self.mu_fm[:], self.inputs['rwkv_mu_fm'], [], ['rw_vecs'])
            self.ts('dve', self.omu_fm[:], self.mu_fm[:], -1.0, 1.0, ALU.mult, ALU.add, ['rw_vecs'], ['rw_vecs'])
            self.vecs = L("rw_vecs", [128, 5, NKC], F32)
            p.dma('sp', self.vecs[:], self.inputs['rwkv_vecs'], [], ['rw_vecs'])
            self.nvecs = L("rw_nvecs", [128, 2, NKC], F32)
            self.ts('dve', self.nvecs[:], self.vecs[:, 0:2, :], -1.0, None, ALU.mult, None, ['rw_vecs'], ['rw_vecs'])
            self.lw2 = L("lw2", [128, D], BF16)
            self.lo_bf = L("lo_bf", [128, TT], BF16)
            self.gng_bc = L("gng_bc", [128, D], BF16)
            self.gnb_bc = L("gnb_bc", [128, D], BF16)
            self.Wva = L("Wva", [128, NKC, D], BF16)
            self.Wvb = L("Wvb", [128, NKC, D], BF16)
            src = self.w_in_dram['rwkv']

            def loader(tes):
                muv = tes.enter_context(self.nc.sbuf_tensor("muv_bc", [128, D], F32))
                p.dma('sp', muv[:], self.inputs['rwkv_lw2'], [], ['muv'])
                self.copy('dve', self.lw2[:], muv[:], ['muv'], ['lw2'])
                for (dst, nm) in ((self.gng_bc, 'rwkv_gn_g'), (self.gnb_bc, 'rwkv_gn_b')):
                    p.dma('sp', muv[:], self.inputs[nm].partition_broadcast(128), ['muv'], ['muv'])
                    self.copy('dve', dst[:], muv[:], ['muv'], ['bc_tiles'])
                p.dma('sp', muv[:], self.inputs['rwkv_mu'][2 * D:3 * D].partition_broadcast(128), ['muv'], ['bc_tiles', 'muv'])
                self.load_weight_bf16(self.Wvb, 'Wv', src, D, src_c0=2 * D, scale_bc=muv)
                self.ts('dve', muv[:], muv[:], -1.0, 1.0, ALU.mult, ALU.add, ['bc_tiles'], ['bc_tiles'])
                self.load_weight_bf16(self.Wva, 'Wv', src, D, src_c0=2 * D, scale_bc=muv)
                self.load_weight_bf16(self.W_in, 'W_in', src, 2 * D, src_c0=0, dst_c0=0)
                self.load_weight_bf16(self.W_in, 'W_in', src, 128, src_c0=3 * D, dst_c0=2 * D)
                self.load_weight_bf16(self.W_in, 'W_in', src, D, src_c0=3 * D + 128, dst_c0=2 * D + 128)
                self.load_weight_bf16(self.W_out, 'W_out', self.w_out_dram['rwkv'], D)
            self.S = L("S_rwkv", [128, NKC, 64], F32)
            p.op('pool', lambda e: e.memset(self.S[:], 0.0), [], [('S', j) for j in range(NKC)])
            self.pcar = L("pcar", [128, 25], F32)
            p.op('pool', lambda e: e.memset(self.pcar[:], 0.0), [], [('pcar', i) for i in range(25)])
            self.pm_ext = [L(f"pm_ext{i}", [128, TT + 1], F32) for i in range(2)]
            self.pm_i = 0
            names = ['r', 'k', 'tmp', 'sigw', 'a', 'kk', 'rn', 'kmod', 'bbv', 'c']
            self.tmp = {'lo': L("w_lo", [128, TT], F32)}
            self.tmpP = [{n: L(f"w_{n}0", [128, TT], F32) for n in names}, None]
            self.tmp2 = [dict(), dict()]
            for n in ['khat', 'bhat']:
                self.tmp2[0][n] = L(f"w_{n}0", [128, TT], F32)
            for n in ['rt_bf', 'bt_bf', 'at_bf', 'kt_h0', 'kt_h1', 'bt_h0', 'bt_h1', 'at_h0', 'at_h1']:
                self.tmp2[0][n] = L(f"w_{n}0", [128, TT], BF16)
            self.hm = L("hm", [128, 2], F32)
            p.op('pool', lambda e: e.memset(self.hm[:], 0.0), [], ['masks'])
            p.op('pool', lambda e: e.memset(self.hm[0:64, 0:1], 1.0), ['masks'], ['masks'])
            p.op('pool', lambda e: e.memset(self.hm[64:128, 1:2], 1.0), ['masks'], ['masks'])
            self.pt = {n: [L(f"wp_{n}{q}", [128, TT], F32 if n in ('at', 'rt') else BF16) for q in range(2)] for n in ['at', 'rt', 'rkr', 'sgate']}
            self.blockones_bf = L("blockones_bf", [128, 128], BF16)
            self.copy('dve', self.blockones_bf[:], self.blockones[:], ['masks'], ['masks'])
            self.v_bf = [L(f"v_bf{q}", [128, NB, 128], BF16) for q in range(2)]
            self.dec = [L(f"dec{q}", [128, NB], F32) for q in range(3)]

            def post_setup():
                self.tmpP[1] = {n: L(f"w_{n}1", [128, TT], F32) for n in names}
                for qq in range(2, self.NDEEP):
                    self.NT.append(L(f"NT{qq}", [128, NCHN, 128], NDT))
                    self.Aak.append(L(f"Aak{qq}", [128, NCHN, 128], BF16))
                    self.Ark.append(L(f"Ark{qq}", [128, NCHN, 128], BF16))
                    self.Arb.append(L(f"Arb{qq}", [128, NCHN, 128], BF16))
                    self.khm.append([L(f"khm{qq}{b}", [128, 128], BF16) for b in range(NB)])
                    self.bhm.append([L(f"bhm{qq}{b}", [128, 128], BF16) for b in range(NB)])
                self.PbP[1] = [L(f"Pb1{i}", [128, NCHN, 128], NDT) for i in range(2)]
                self.QbP[1] = [L(f"Qb1{i}", [128, NCHN, 128], NDT) for i in range(2)]
                for n in ['khat', 'bhat']:
                    self.tmp2[1][n] = L(f"w_{n}1", [128, TT], F32)
                for n in ['rt_bf', 'bt_bf', 'at_bf', 'kt_h0', 'kt_h1', 'bt_h0', 'bt_h1', 'at_h0', 'at_h1']:
                    self.tmp2[1][n] = L(f"w_{n}1", [128, TT], BF16)
                for n in ['rkr', 'sgate']:
                    self.pt[n].append(L(f"wp_{n}2", [128, TT], BF16))
                for n in ['at', 'rt']:
                    self.pt[n].append(self.pt[n][0])
                self.ysb = [L(f"ysb{i}", [128, 256], F32) for i in range(2)]
                self.yn2 = [self.yn, L("yn1", [128, 128], F32)]
                self.bon2 = [self.bon, L("bon1", [128, 128], F32)]
                self.gst2 = [self.gst, L("gst1", [128, 12], F32)]
                self.junk2 = [self.junk, L("junk1", [128, 64], F32)]
                self.bcount = 0
                self.v_bf.append(L("v_bf2", [128, NB, 128], BF16))
            self.post_setup = post_setup
            NCHN = NB * 2
            self.PbP = [[L(f"Pb0{i}", [128, NCHN, 128], NDT) for i in range(2)], None]
            self.QbP = [[L(f"Qb0{i}", [128, NCHN, 128], NDT) for i in range(2)], None]
            self.NT = [L(f"NT{q}", [128, NCHN, 128], NDT) for q in range(2)]
            self.Aak = [L(f"Aak{q}", [128, NCHN, 128], BF16) for q in range(2)]
            self.Ark = [L(f"Ark{q}", [128, NCHN, 128], BF16) for q in range(2)]
            self.Arb = [L(f"Arb{q}", [128, NCHN, 128], BF16) for q in range(2)]
            self.NDEEP = 2
            self.ident4 = L("ident4", [128, NCHN, 128], NDT)
            for c in range(NCHN):
                self.copy('dve', self.ident4[:, c, :], self.ident[:], ['ident'], ['ident'])
            self.khm = [[L(f"khm{q}{b}", [128, 128], BF16) for b in range(NB)] for q in range(2)]
            self.bhm = [[L(f"bhm{q}{b}", [128, 128], BF16) for b in range(NB)] for q in range(2)]
            self.Z_sb = L("Z_sb", [128, 128], NDT)
            self.U_bf = L("U_bf", [128, 128], BF16)
            self.yn = L("yn", [128, 128], F32)
            self.bon = L("bon", [128, 128], F32)
            self.gst = L("gst", [128, 12], F32)
            self.junk = L("junk", [128, 64], F32)
            self.ps_pool = [0, 1]
            return loader

        NMB = [[2, 3], [4, 7]]
        self.nm_rrs = [0, 0]

        def nm_ps(q=0):
            i = NMB[q][self.nm_rrs[q] % 2]
            self.nm_rrs[q] += 1
            return self.psums[i], f"ps{i}"

        def shift(ps, pk, dst, dk, idx, mt):
            pm = self.pm_ext[self.pm_i % 2]
            pmk = f"pm_ext{self.pm_i % 2}"
            self.pm_i += 1
            ck = ('pcar', idx)
            self.copy('pool', pm[:, 0:1], self.pcar[:, idx:idx + 1], [ck], [pmk])
            self.act(pm[:, 1:TT + 1], ps[:, :TT], AF.Copy, [pk, 'rw_vecs'], [pmk], scale=self.mu_fm[:, mt:mt + 1])
            self.copy('pool', self.pcar[:, idx:idx + 1], pm[:, TT:TT + 1], [pmk], [ck])
            self.act(dst, ps[:, :TT], AF.Copy, [pk, 'rw_vecs'], [dk], scale=self.omu_fm[:, mt:mt + 1])
            self.tt('dve', dst, dst, pm[:, 0:TT], ALU.add, [dk, pmk], [dk])

        def proj(col0):
            ps, pk = self.next_ps()
            for kc in range(NKC):
                self.mm(ps[:, :TT], self.W_in[:, kc, col0:col0 + 128], self.hn[:, kc, 1:TT + 1], kc == 0, kc == NKC - 1, ['W_in', self.hnk], [pk])
            return ps, pk

        hsl = [slice(0, 64), slice(64, 128)]

        def A_gen(j):
            V = self.vecs
            q = j % 2
            q3 = j % 3
            q4 = j % self.NDEEP
            t = dict(self.tmp)
            t.update(self.tmpP[q])
            t['e1'] = t['rn']
            t['e2'] = t['tmp']
            t.update(self.tmp2[q])
            self.Pb, self.Qb = self.PbP[q], self.QbP[q]
            PAR = set(self.tmp2[0].keys())
            jc = slice(j * 128, (j + 1) * 128)
            at, rt, rkr, sgate = self.pt['at'][q], self.pt['rt'][q], self.pt['rkr'][q3], self.pt['sgate'][q3]
            kat, krt, krkr, ksg = f'p_at{q}', f'p_rt{q}', f'p_rkr{q3}', f'p_sgate{q3}'
            v_bf, dec = self.v_bf[q3], self.dec[q3]
            kv, kvb, kdec = f'v_sb{q3}', f'v_bf{q3}', f'dec{q3}'
            ps, pk = proj(j * 128)
            shift(ps, pk, t['r'][:], f't_r{q}', j, j)
            ps, pk = proj(D + j * 128)
            shift(ps, pk, t['k'][:], f't_k{q}', 8 + j, 8 + j)
            yield
            ps, pk = proj(2 * D + 128 + j * 128)
            shift(ps, pk, t['tmp'][:], f't_tmp{q}', 16 + j, 25 + j)
            self.act(t['sigw'][:], t['tmp'][:], AF.Exp, [f't_tmp{q}'], [f't_sigw{q}'], scale=-1.0)
            self.act(t['sigw'][:], t['sigw'][:], AF.Ln, [f't_sigw{q}', 'consts'], [f't_sigw{q}'], bias=self.epsc[:, 2:3])
            self.act(t['sigw'][:], t['sigw'][:], AF.Exp, [f't_sigw{q}'], [f't_sigw{q}'], scale=-1.0)
            self.tt('pool', sgate[:], t['tmp'][:], t['sigw'][:], ALU.mult, [f't_tmp{q}', f't_sigw{q}'], [ksg])
            pv, pvk = self.next_ps()
            for blk in range(NB):
                n = 0
                for kc in range(NKC):
                    for (Wv, off) in ((self.Wva, 1), (self.Wvb, 0)):
                        self.mm(pv[:, blk * 128:(blk + 1) * 128], self.hn[:, kc, off + blk * 128:off + (blk + 1) * 128], Wv[:, kc, jc],
                                n == 0, n == 2 * NKC - 1, ['Wv', self.hnk], [pvk])
                        n += 1
            self.copy('dve', v_bf[:].rearrange("p b v -> p (b v)"), pv[:, :TT], [pvk], [kvb])
            yield
            pw, pwk = self.next_ps()
            self.mm(pw[:, :TT], self.lw2[0:64, jc], self.lo_bf[0:64, :], True, True, ['lw2', 't_lo_bf'], [pwk])
            self.act(t['sigw'][:], pw[:, :TT], AF.Exp, [pwk, 'rw_vecs'], [f't_sigw{q}'], bias=self.nvecs[:, 0, j:j + 1], scale=-1.0)
            self.act(t['sigw'][:], t['sigw'][:], AF.Ln, [f't_sigw{q}', 'consts'], [f't_sigw{q}'], bias=self.epsc[:, 2:3])
            self.act(t['sigw'][:], t['sigw'][:], AF.Exp, [f't_sigw{q}'], [f't_sigw{q}'], scale=-1.0)
            pa, pak = self.next_ps()
            self.mm(pa[:, :TT], self.lw2[64:128, jc], self.lo_bf[64:128, :], True, True, ['lw2', 't_lo_bf'], [pak])
            self.act(t['a'][:], pa[:, :TT], AF.Exp, [pak, 'rw_vecs'], [f't_a{q}'], bias=self.nvecs[:, 1, j:j + 1], scale=-1.0)
            self.act(t['a'][:], t['a'][:], AF.Ln, [f't_a{q}', 'consts'], [f't_a{q}'], bias=self.epsc[:, 2:3])
            self.act(t['a'][:], t['a'][:], AF.Exp, [f't_a{q}'], [f't_a{q}'], scale=-1.0)
            self.ts('dve', t['kk'][:], t['k'][:], V[:, 2, j:j + 1], None, ALU.mult, None, [f't_k{q}', 'rw_vecs'], [f't_kk{q}'])
            self.tt('pool', t['tmp'][:], t['kk'][:], t['kk'][:], ALU.mult, [f't_kk{q}'], [f't_tmp{q}'])
            pn, pnk = self.next_ps()
            self.mm(pn[:, :TT], self.blockones[:], t['tmp'][:], True, True, ['masks', f't_tmp{q}'], [pnk])
            self.ts('dve', t['rn'][:], pn[:, :TT], 1e-24, None, ALU.max, None, [pnk], [f't_rn{q}'])
            self.act(t['rn'][:], t['rn'][:], AF.Ln, [f't_rn{q}'], [f't_rn{q}'])
            self.act(t['rn'][:], t['rn'][:], AF.Exp, [f't_rn{q}'], [f't_rn{q}'], scale=-0.5)
            self.tt('pool', t['kk'][:], t['kk'][:], t['rn'][:], ALU.mult, [f't_kk{q}', f't_rn{q}'], [f't_kk{q}'])
            self.ts('dve', t['tmp'][:], t['a'][:], -1.0, V[:, 3, j:j + 1], ALU.add, ALU.mult, [f't_a{q}', 'rw_vecs'], [f't_tmp{q}'])
            self.stt('dve', t['kmod'][:], t['tmp'][:], 1.0, t['k'][:], ALU.add, ALU.mult, [f't_tmp{q}', f't_k{q}'], [f't_kmod{q}'])
            self.tt('pool', t['bbv'][:], t['kk'][:], t['a'][:], ALU.mult, [f't_kk{q}', f't_a{q}'], [f't_bbv{q}'])
            yield
            self.p.op('dve', lambda e: e.tensor_tensor_scan(out=t['c'][:], data0=self.resetm[:], data1=t['sigw'][:], initial=0.0,
                                                            op0=ALU.mult, op1=ALU.add), [f't_sigw{q}', 'masks'], [f't_c{q}'])
            self.act(t['e1'][:], t['c'][:], AF.Exp, [f't_c{q}'], [f't_rn{q}'], scale=LC)
            self.tt('pool', rt[:], t['r'][:], t['e1'][:], ALU.mult, [f't_r{q}', f't_rn{q}'], [krt])
            self.copy('act', t['rt_bf'][:], rt[:], [krt], [f't_rt_bf{q}'])
            self.act(t['e2'][:], t['c'][:], AF.Exp, [f't_c{q}'], [f't_tmp{q}'], scale=-LC)
            for hd in range(2):
                self.stt('dve', t[f'kt_h{hd}'][:], t['kmod'][:], self.hm[:, hd:hd + 1], t['e2'][:], ALU.mult, ALU.mult,
                         [f't_kmod{q}', f't_tmp{q}', 'masks'], [f't_kt_h{hd}_{q}'])
                self.stt('dve', t[f'bt_h{hd}'][:], t['bbv'][:], self.hm[:, hd:hd + 1], t['e2'][:], ALU.mult, ALU.mult,
                         [f't_bbv{q}', f't_tmp{q}', 'masks'], [f't_bt_h{hd}_{q}'])
            self.tt('pool', t['bt_bf'][:], t['bbv'][:], t['e2'][:], ALU.mult, [f't_bbv{q}', f't_tmp{q}'], [f't_bt_bf{q}'])
            self.tt('pool', t['e1'][:], t['c'][:], t['sigw'][:], ALU.subtract, [f't_c{q}', f't_sigw{q}'], [f't_rn{q}'])
            self.act(t['e1'][:], t['e1'][:], AF.Exp, [f't_rn{q}'], [f't_rn{q}'], scale=LC)
            self.stt('dve', at[:], t['kk'][:], -1.0, t['e1'][:], ALU.mult, ALU.mult, [f't_kk{q}', f't_rn{q}'], [kat])
            self.copy('act', t['at_bf'][:], at[:], [kat], [f't_at_bf{q}'])
            for hd in range(2):
                self.act(t[f'at_h{hd}'][:], at[:], AF.Copy, [kat, 'masks'], [f't_at_h{hd}_{q}'], scale=self.hm[:, hd:hd + 1])
            yield
            c3 = t['c'][:].rearrange("p (n c) -> p n c", c=128)
            self.tt('pool', t['e2'][:].rearrange("p (n c) -> p n c", c=128), c3[:, :, 127:128].to_broadcast([128, NB, 128]), c3,
                    ALU.subtract, [f't_c{q}'], [f't_tmp{q}'])
            self.act(t['e2'][:], t['e2'][:], AF.Exp, [f't_tmp{q}'], [f't_tmp{q}'], scale=LC)
            self.act(dec[:], c3[:, :, 127], AF.Exp, [f't_c{q}'], [kdec], scale=LC)
            self.tt('pool', t['khat'][:], t['kmod'][:], t['e2'][:], ALU.mult, [f't_kmod{q}', f't_tmp{q}'], [f't_khat{q}'])
            self.tt('dve', t['bhat'][:], t['bbv'][:], t['e2'][:], ALU.mult, [f't_bbv{q}', f't_tmp{q}'], [f't_bhat{q}'])
            self.stt('dve', rkr[:], t['r'][:], V[:, 4, j:j + 1], t['kmod'][:], ALU.mult, ALU.mult, [f't_r{q}', 'rw_vecs', f't_kmod{q}'], [krkr])
            yield
            NCH = NB * 2
            specs = {'P': ('bt_h', 'at_bf', self.maskS, self.Pb[0], f'Pb{q}0'),
                     'Q': ('at_h', 'bt_bf', self.maskSL, self.Qb[0], f'Qb{q}0'),
                     'ak': ('kt_h', 'at_bf', self.maskS, self.Aak[q4], f'Aak{q4}'),
                     'rk': ('kt_h', 'rt_bf', self.maskI, self.Ark[q4], f'Ark{q4}'),
                     'rb': ('bt_h', 'rt_bf', self.maskI, self.Arb[q4], f'Arb{q4}')}
            for name in ('P', 'Q', 'ak', 'rk', 'rb'):
                lh, rh, mask, dst, dk = specs[name]
                ps, pk = nm_ps(q)
                for c in range(NCH):
                    blk, hd = c // 2, c % 2
                    cs = slice(blk * 128, (blk + 1) * 128)
                    self.mm(ps[:, c * 128:(c + 1) * 128], t[f'{lh}{hd}'][:, cs], t[rh][:, cs], True, True, [f't_{lh}{hd}_{q}', f't_{rh}{q}'], [pk])
                self.tt('dve', dst[:], ps[:, 0:NCH * 128].rearrange("p (c t) -> p c t", c=NCH),
                        mask[:, None, :].to_broadcast([128, NCH, 128]), ALU.mult, [pk, 'masks'], [dk])
                if name == 'Q':
                    self.tt('pool', self.NT[q4][:], self.ident4[:], self.Pb[0][:], ALU.add, ['ident', f'Pb{q}0'], [f'NT{q4}'])
                    yield
            for blk in range(NB):
                cs = slice(blk * 128, (blk + 1) * 128)
                for (srcn, dst, dk) in (('khat', self.khm[q4][blk], f'khm{q4}{blk}'), ('bhat', self.bhm[q4][blk], f'bhm{q4}{blk}')):
                    ps, pk = nm_ps(q)
                    self.p.op('pe', lambda e, ps=ps, srcn=srcn, cs=cs: e.transpose(ps[:, 0:128], t[srcn][:, cs], self.ident[:]),
                              [f't_{srcn}{q}', 'ident'], [pk])
                    self.copy('act', dst[:], ps[:, 0:128], [pk], [dk])
            yield
            NTq, kNT = self.NT[q4], f'NT{q4}'
            for i in range(6):
                a_, b_ = i % 2, (i + 1) % 2
                Pa, Qa, Pn, Qn = self.Pb[a_], self.Qb[a_], self.Pb[b_], self.Qb[b_]
                kPa, kQa, kPn, kQn = f'Pb{q}{a_}', f'Qb{q}{a_}', f'Pb{q}{b_}', f'Qb{q}{b_}'
                if i < 5:
                    ps, pk = nm_ps(q)
                    for c in range(NCH):
                        self.mm(ps[:, c * 128:(c + 1) * 128], Qa[:, c, :], Pa[:, c, :], True, True, [kQa, kPa], [pk])
                    self.copy('act', Pn[:].rearrange("p c t -> p (c t)"), ps[:, 0:NCH * 128], [pk], [kPn])
                ps, pk = nm_ps(q)
                for c in range(NCH):
                    self.mm(ps[:, c * 128:(c + 1) * 128], Pa[:, c, :], Qa[:, c, :], True, True, [kPa, kQa], [pk])
                self.copy('dve' if i % 2 == 0 else 'act', Qn[:].rearrange("p c t -> p (c t)"), ps[:, 0:NCH * 128], [pk], [kQn])
                yield
                ps, pk = nm_ps(q)
                for c in range(NCH):
                    self.mm(ps[:, c * 128:(c + 1) * 128], Qn[:, c, :], NTq[:, c, :], True, True, [kQn, kNT], [pk])
                self.tt('dve', NTq[:].rearrange("p c t -> p (c t)"), NTq[:].rearrange("p c t -> p (c t)"), ps[:, 0:NCH * 128], ALU.add,
                        [kNT, pk], [kNT])
                yield

        def B_gen(j):
            t = self.tmp
            P = self.psums
            q = j % 2
            q3 = j % 3
            q4 = j % self.NDEEP
            jc = slice(j * 128, (j + 1) * 128)
            at, rt, rkr, sgate = self.pt['at'][q], self.pt['rt'][q], self.pt['rkr'][q3], self.pt['sgate'][q3]
            kat, krt, krkr, ksg = f'p_at{q}', f'p_rt{q}', f'p_rkr{q3}', f'p_sgate{q3}'
            v_bf, dec = self.v_bf[q3], self.dec[q3]
            kv, kvb, kdec = f'v_sb{q3}', f'v_bf{q3}', f'dec{q3}'
            kS = ('S', j)
            for blk in range(NB):
                cs = slice(blk * 128, (blk + 1) * 128)
                khm, bhm = self.khm[q4][blk], self.bhm[q4][blk]
                kkh, kbh = f'khm{q4}{blk}', f'bhm{q4}{blk}'
                pz, pzk = P[5], 'ps5'
                for hd in range(2):
                    hs, hc, c = hsl[hd], slice(hd * 64, (hd + 1) * 64), blk * 2 + hd
                    self.mm(pz[:, hc], self.Aak[q4][:, c, :], v_bf[:, blk, hc], True, False, [f'Aak{q4}', kvb], [pzk])
                    self.mm(pz[:, hc], at[hs, cs], self.S[hs, j, :], False, True, [kat, kS], [pzk])
                self.copy('act', self.Z_sb[:], pz[:, 0:128], [pzk], ['Z_sb'])
                yield
                for hd in range(2):
                    hc, c = slice(hd * 64, (hd + 1) * 64), blk * 2 + hd
                    self.mm(pz[:, hc], self.NT[q4][:, c, :], self.Z_sb[:, hc], True, True, [f'NT{q4}', 'Z_sb'], [pzk])
                self.copy('act', self.U_bf[:], pz[:, 0:128], [pzk], ['U_bf'])
                yield
                self.mm(pz[:, 0:128], khm[:], v_bf[:, blk, :], True, False, [kkh, kvb], [pzk])
                self.mm(pz[:, 0:128], bhm[:], self.U_bf[:], False, True, [kbh, 'U_bf'], [pzk])
                py, pyk = P[6], 'ps6'
                for hd in range(2):
                    hs, hc, c = hsl[hd], slice(hd * 64, (hd + 1) * 64), blk * 2 + hd
                    yc = slice(hd * 128, hd * 128 + 64)
                    bc_ = slice(hd * 128 + 64, hd * 128 + 128)
                    self.mm(py[:, yc], self.Ark[q4][:, c, :], v_bf[:, blk, hc], True, False, [f'Ark{q4}', kvb], [pyk])
                    self.mm(py[:, yc], rt[hs, cs], self.S[hs, j, :], False, False, [krt, kS], [pyk])
                    self.mm(py[:, yc], self.Arb[q4][:, c, :], self.U_bf[:, hc], False, True, [f'Arb{q4}', 'U_bf'], [pyk])
                    self.mm(py[:, bc_], rkr[hs, cs], self.blockones_bf[hs, hs], True, True, [krkr, 'masks'], [pyk])
                for hd in range(2):
                    hs, hc = hsl[hd], slice(hd * 64, (hd + 1) * 64)
                    self.stt('dve', self.S[hs, j, :], self.S[hs, j, :], dec[hs, blk:blk + 1], pz[hs, hc], ALU.mult, ALU.add,
                             [kS, kdec, pzk], [kS])
                yield
                bp = self.bcount % 2
                self.bcount += 1
                ysb, kys = self.ysb[bp], f'ysb{bp}'
                g, kg = self.gst2[bp], f'gst{bp}'
                yn, kyn = self.yn2[bp], f'yn{bp}'
                bon, kbon = self.bon2[bp], f'bon{bp}'
                junk, kjunk = self.junk2[bp], f'junk{bp}'
                self.copy('act', ysb[:], py[:, 0:256], [pyk], [kys])
                for hd in range(2):
                    hc = slice(hd * 64, (hd + 1) * 64)
                    yc = slice(hd * 128, hd * 128 + 64)
                    bc_ = slice(hd * 128 + 64, hd * 128 + 128)
                    self.act(junk[:], ysb[:, yc], AF.Identity, [kys], [kjunk, kg], accum_out=g[:, hd:hd + 1])
                    self.act(junk[:], ysb[:, yc], AF.Square, [kys], [kjunk, kg], accum_out=g[:, 2 + hd:3 + hd])
                    self.tt('pool', bon[:, hc], ysb[:, bc_], v_bf[:, blk, hc], ALU.mult, [kys, kvb], [kbon])
                self.ts('dve', g[:, 4:6], g[:, 0:2], 1.0 / 64, None, ALU.mult, None, [kg], [kg])
                self.tt('dve', g[:, 6:8], g[:, 4:6], g[:, 4:6], ALU.mult, [kg], [kg])
                self.stt('dve', g[:, 8:10], g[:, 2:4], 1.0 / 64, g[:, 6:8], ALU.mult, ALU.subtract, [kg], [kg])
                self.act(g[:, 8:10], g[:, 8:10], AF.Ln, [kg, 'consts'], [kg], bias=self.epsc[:, 1:2])
                self.act(g[:, 8:10], g[:, 8:10], AF.Exp, [kg], [kg], scale=-0.5)
                for hd in range(2):
                    hc = slice(hd * 64, (hd + 1) * 64)
                    yc = slice(hd * 128, hd * 128 + 64)
                    self.ts('dve', yn[:, hc], ysb[:, yc], g[:, 4 + hd:5 + hd], g[:, 8 + hd:9 + hd], ALU.subtract, ALU.mult,
                            [kys, kg], [kyn])
                yield
                self.tt('pool', yn[:], yn[:], self.gng_bc[:, jc], ALU.mult, [kyn, 'bc_tiles'], [kyn])
                self.tt('pool', yn[:], yn[:], self.gnb_bc[:, jc], ALU.add, [kyn, 'bc_tiles'], [kyn])
                self.tt('pool', yn[:], yn[:], bon[:], ALU.add, [kyn, kbon], [kyn])
                ps, pk = nm_ps(q)
                self.p.op('pe', lambda e, ps=ps, yn=yn: e.transpose(ps[:, 0:128], yn[:], self.ident[:]), [kyn, 'ident'], [pk])
                self.tt('dve', self.yTt[:, j, cs], ps[:, 0:128], sgate[:, cs], ALU.mult, [pk, ksg], [self.yTk])
                yield

        def drive(gens):
            gens = [g for g in gens if g is not None]
            while gens:
                for g in list(gens):
                    try:
                        next(g)
                    except StopIteration:
                        gens.remove(g)

        def tile(ti):
            t = self.tmp
            ps, pk = proj(2 * D)
            shift(ps, pk, t['lo'][:], 't_lo', 24, 24)
            lo64 = t['lo'][0:64, :]
            self.act(lo64, lo64, AF.Exp, ['t_lo'], ['t_lo'], scale=-2.0)
            self.act(lo64, lo64, AF.Ln, ['t_lo', 'consts'], ['t_lo'], bias=self.epsc[0:64, 2:3])
            self.act(lo64, lo64, AF.Exp, ['t_lo'], ['t_lo'], scale=-1.0)
            self.ts('dve', lo64, lo64, 2.0, -1.0, ALU.mult, ALU.add, ['t_lo'], ['t_lo'])
            self.copy('dve', self.lo_bf[:], t['lo'][:], ['t_lo'], ['t_lo_bf'])
            for j in range(NKC):
                drive([A_gen(j)])
                drive([B_gen(j)])

        self.run_layer(li, 'rwkv', TT, WC, setup, tile, is_last)
        self.ps_pool = list(range(8))

    def build(self):
        nc = self.nc
        T = self.T
        self.xT = self.din("xT", [D, T])
        self.yT = nc.dram_tensor("yT", [D, T], F32, kind="ExternalOutput").ap()
        d_norm_g = self.din("norm_g", [128, 4, NKC])
        d_final_g = self.din("final_g", [128, NKC])
        self.w_in_dram, self.w_out_dram = {}, {}
        kinds = [k for (_, k) in self.layers]
        if 'conv' in kinds:
            self.w_in_dram['conv'] = self.din("conv_w_in", [D, 4 * D])
            self.w_out_dram['conv'] = self.din("conv_w_out", [D, D])
            d_conv_w = self.din("conv_w", [128, NKC, 3])
        if 'rwkv' in kinds:
            self.w_in_dram['rwkv'] = self.din("rwkv_w_in", [D, 4 * D + 128])
            self.w_out_dram['rwkv'] = self.din("rwkv_w_out", [D, D])
            self.din("rwkv_mu_fm", [128, 33])
            self.din("rwkv_mu", [4 * D + 128])
            self.din("rwkv_vecs", [128, 5, NKC])
            self.din("rwkv_lw2", [128, D])
            self.din("rwkv_gn_g", [D])
            self.din("rwkv_gn_b", [D])
        if 'hgrn' in kinds:
            self.w_in_dram['hgrn'] = self.din("hgrn_w_in", [D, 4 * D])
            self.w_out_dram['hgrn'] = self.din("hgrn_w_out", [D, D])
            self.din("hgrn_gn_g", [D])
            self.din("hgrn_lbl", [128, 4, NKC])
        if 'gmlp' in kinds:
            self.w_in_dram['gmlp'] = self.din("gmlp_w_in", [D, 3 * D])
            self.w_out_dram['gmlp'] = self.din("gmlp_w_out", [D, D])
            self.din("gmlp_wsT", [128, 8, 128])
            self.din("gmlp_bs", [8, 128])
            self.din("gmlp_vg", [D])
        with ExitStack() as es:
            self.es = es
            nc.allow_low_precision("bf16 matmul operands, fp32 accumulation")
            self.p = p = Prog(nc, es)
            self.psums = [es.enter_context(nc.psum_tensor(f"ps{i}", [128, 512], F32)) for i in range(8)]
            self.ps_rr = 0
            self.ps_pool = list(range(8))
            self.ones_bf = self.sb("ones_bf", [128, 128], BF16)
            self.epsc = self.sb("epsc", [128, 4], F32)
            self.norm_g = self.sb("norm_g_sb", [128, 4, NKC], F32)
            self.final_g = self.sb("final_g_sb", [128, NKC], F32)
            p.op('pool', lambda e: e.memset(self.ones_bf[:], 1.0), [], ['ones_bf'])
            p.op('pool', lambda e: e.memset(self.epsc[:, 0:1], RMS_EPS), [], ['consts'])
            p.op('pool', lambda e: e.memset(self.epsc[:, 2:3], 1.0), ['consts'], ['consts'])
            p.dma('sp', self.norm_g[:], d_norm_g, [], ['consts'])
            p.dma('sp', self.final_g[:], d_final_g, [], ['consts'])
            if 'conv' in kinds:
                self.conv_w = self.sb("conv_w_sb", [128, NKC, 3], F32)
                p.dma('sp', self.conv_w[:], d_conv_w, [], ['consts'])
            self.first_layer = True
            for n, (li, kind) in enumerate(self.layers):
                is_last = n == len(self.layers) - 1
                if kind == 'conv':
                    self.conv_layer(li, is_last)
                elif kind == 'gmlp':
                    self.gmlp_layer(li, is_last)
                elif kind == 'hgrn':
                    self.hgrn_layer(li, is_last)
                elif kind == 'rwkv':
                    self.rwkv_layer(li, is_last)
                else:
                    raise ValueError(kind)
            p.finish('sp')
            self.stats = (p.n_ins, p.n_wait)
        return nc


def prep_inputs(inp, b, layers):
    f = np.float32
    m = {}
    m["xT"] = np.ascontiguousarray(np.asarray(inp["x"][b], f).T)
    m["norm_g"] = np.ascontiguousarray(np.asarray(inp["norm_g"], f).reshape(4, NKC, 128).transpose(2, 0, 1))
    m["final_g"] = np.ascontiguousarray(np.asarray(inp["final_g"], f).reshape(NKC, 128).T)
    kinds = [k for (_, k) in layers]
    if 'conv' in kinds:
        m["conv_w_in"] = np.ascontiguousarray(np.asarray(inp["conv_w_in"][0], f))
        m["conv_w_out"] = np.ascontiguousarray(np.asarray(inp["conv_w_out"][0], f))
        m["conv_w"] = np.ascontiguousarray(np.asarray(inp["conv_w"][0], f).reshape(3, NKC, 128).transpose(2, 1, 0))
    if 'rwkv' in kinds:
        m["rwkv_w_in"] = np.ascontiguousarray(np.asarray(inp["rwkv_w_in"][0], f))
        m["rwkv_w_out"] = np.ascontiguousarray(np.asarray(inp["rwkv_w_out"][0], f))
        mu = np.asarray(inp["rwkv_mu"][0], f)
        m["rwkv_mu"] = np.ascontiguousarray(mu)
        m["rwkv_mu_fm"] = np.ascontiguousarray(mu.reshape(33, 128).T)
        vecs = np.stack([np.asarray(inp[k][0], f).reshape(NKC, 128) for k in
                         ("rwkv_w0", "rwkv_a0", "rwkv_k_k", "rwkv_k_a", "rwkv_r_k")], axis=0)
        m["rwkv_vecs"] = np.ascontiguousarray(vecs.transpose(2, 0, 1))
        m["rwkv_lw2"] = np.ascontiguousarray(np.concatenate([np.asarray(inp["rwkv_w_w2"][0], f), np.asarray(inp["rwkv_w_a2"][0], f)], axis=0))
        m["rwkv_gn_g"] = np.ascontiguousarray(np.asarray(inp["rwkv_gn_g"][0], f))
        m["rwkv_gn_b"] = np.ascontiguousarray(np.asarray(inp["rwkv_gn_b"][0], f))
    if 'hgrn' in kinds:
        m["hgrn_w_in"] = np.ascontiguousarray(np.asarray(inp["hgrn_w_in"][0], f))
        m["hgrn_w_out"] = np.ascontiguousarray(np.asarray(inp["hgrn_w_out"][0], f))
        m["hgrn_gn_g"] = np.ascontiguousarray(np.asarray(inp["hgrn_gn_g"][0], f))
        m["hgrn_lbl"] = np.ascontiguousarray(np.asarray(inp["hgrn_lb_logits"], f).reshape(4, NKC, 128).transpose(2, 0, 1))
    if 'gmlp' in kinds:
        m["gmlp_w_in"] = np.ascontiguousarray(np.asarray(inp["gmlp_w_in"][0], f))
        m["gmlp_w_out"] = np.ascontiguousarray(np.asarray(inp["gmlp_w_out"][0], f))
        m["gmlp_wsT"] = np.ascontiguousarray(np.asarray(inp["gmlp_w_s"][0], f).transpose(2, 0, 1))
        m["gmlp_bs"] = np.ascontiguousarray(np.asarray(inp["gmlp_b_s"][0], f))
        m["gmlp_vg"] = np.ascontiguousarray(np.asarray(inp["gmlp_v_g"][0], f))
    return m


FULL_LAYERS = [(0, 'rwkv'), (1, 'hgrn'), (2, 'conv'), (3, 'gmlp')]


def kernel(**inputs):
    x = np.asarray(inputs["x"])
    B, T, _ = x.shape
    layers = FULL_LAYERS
    bld = Builder(T, layers)
    nc = bld.build()
    in_maps = []
    zeros = None
    for c in range(8):
        if c % 2 == 0:
            in_maps.append(prep_inputs(inputs, c // 2, layers))
        else:
            if zeros is None:
                zeros = {k: np.zeros_like(v) for k, v in in_maps[0].items()}
            in_maps.append(zeros)
    res = run_bass_kernel_spmd(nc, in_maps, core_ids=list(range(8)))
    out = np.stack([np.asarray(res.results[2 * b]["yT"]).T for b in range(B)], axis=0)
    return out.astype(np.float32)
```

```python
import numpy as np
from contextlib import ExitStack
import concourse.bass as bass
import concourse.mybir as mybir
from concourse.bass_utils import run_bass_kernel_spmd

F32 = mybir.dt.float32
BF16 = mybir.dt.bfloat16
ALU = mybir.AluOpType
AF = mybir.ActivationFunctionType
AX = mybir.AxisListType

D = 1024
NKC = 8
RMS_EPS = 1e-6
GN_EPS = 64e-5


class Prog:
    LIMIT = 30000

    def __init__(self, nc, es, n_dma_sems=24):
        self.nc = nc
        self.es = es
        self.engs = {'pe': nc.tensor, 'act': nc.scalar, 'dve': nc.vector,
                     'pool': nc.gpsimd, 'sp': nc.sync}
        self.sems = {}
        self.epoch = {k: 0 for k in self.engs}
        self.cnt = {k: 0 for k in self.engs}
        for k in self.engs:
            self.sems[(k, 0)] = es.enter_context(nc.semaphore(f"s_{k}_0"))
        self.dma_sems = []
        for i in range(n_dma_sems):
            key = ('dma', i)
            self.sems[key] = es.enter_context(nc.semaphore(f"s_dma_{i}"))
            self.cnt[key] = 0
            self.dma_sems.append(key)
        self.dma_rr = 0
        self.waited = {k: {} for k in self.engs}
        self.bufs = {}
        self.n_wait = 0
        self.n_ins = 0

    def _deps(self, reads, writes):
        deps = set()
        for k in reads:
            b = self.bufs.get(k)
            if b and b['w']:
                deps.add(b['w'])
        for k in writes:
            b = self.bufs.get(k)
            if b:
                if b['w']:
                    deps.add(b['w'])
                deps.update(b['r'])
        return deps

    def _wait(self, eng, deps):
        e = self.engs[eng]
        best = {}
        for (sk, v) in deps:
            if sk[0] == eng and eng == 'pe':
                continue
            if best.get(sk, 0) < v:
                best[sk] = v
        for sk, v in best.items():
            if self.waited[eng].get(sk, 0) >= v:
                continue
            e.wait_ge(self.sems[sk], v)
            self.waited[eng][sk] = v
            self.n_wait += 1

    def _record(self, tok, reads, writes):
        for k in reads:
            b = self.bufs.setdefault(k, {'w': None, 'r': []})
            b['r'].append(tok)
            if len(b['r']) > 64:
                best = {}
                for (sk, v) in b['r']:
                    if best.get(sk, 0) < v:
                        best[sk] = v
                b['r'] = list(best.items())
        for k in writes:
            b = self.bufs.setdefault(k, {'w': None, 'r': []})
            b['w'] = tok
            b['r'] = []

    @staticmethod
    def _excl(reads, writes):
        ps = [k for k in reads if isinstance(k, str) and k.startswith('ps')]
        if ps:
            reads = [k for k in reads if k not in ps]
            writes = list(writes) + ps
        return reads, writes

    disabled = False
    recording = None
    SYNC_LAT = 0.45

    def begin_record(self):
        self.recording = []

    def flush(self):
        rec = self.recording
        self.recording = None
        if not rec:
            return
        n = len(rec)
        preds = [None] * n
        succs = [[] for _ in range(n)]
        last_w = {}
        readers = {}
        for i, (kind, eng, fn, reads, writes, cost, lat) in enumerate(rec):
            ps = set()
            for k in reads:
                w = last_w.get(k)
                if w is not None:
                    ps.add(w)
            for k in writes:
                w = last_w.get(k)
                if w is not None:
                    ps.add(w)
                ps.update(readers.get(k, ()))
            ps.discard(i)
            preds[i] = ps
            for pi in ps:
                succs[pi].append(i)
            for k in reads:
                readers.setdefault(k, []).append(i)
            for k in writes:
                last_w[k] = i
                readers[k] = []
        npred = [len(p_) for p_ in preds]
        ready = [i for i in range(n) if npred[i] == 0]
        eng_free = {}
        end_t = [0.0] * n
        done_t = [0.0] * n
        order = []
        blevel = [0.0] * n
        for i in range(n - 1, -1, -1):
            kind, eng, fn, reads, writes, cost, lat = rec[i]
            b = 0.0
            for si in succs[i]:
                v = blevel[si] + (self.SYNC_LAT if rec[si][1] != eng else 0.0)
                if v > b:
                    b = v
            blevel[i] = b + cost + lat

        def est(i):
            kind, eng, fn, reads, writes, cost, lat = rec[i]
            t = eng_free.get(eng, 0.0)
            for pi in preds[i]:
                tp = done_t[pi] + (self.SYNC_LAT if rec[pi][1] != eng else 0.0)
                if tp > t:
                    t = tp
            return t
        EPS = 0.1
        while ready:
            ests = [(est(i), i) for i in ready]
            tmin = min(ests)[0]
            best = None
            for (t, i) in ests:
                if t <= tmin + EPS:
                    if best is None or blevel[i] > blevel[best[1]] or (blevel[i] == blevel[best[1]] and i < best[1]):
                        best = (t, i)
            t1, i = best
            ready.remove(i)
            kind, eng, fn, reads, writes, cost, lat = rec[i]
            end_t[i] = t1 + cost
            done_t[i] = t1 + cost + lat
            eng_free[eng] = end_t[i]
            order.append(i)
            for si in succs[i]:
                npred[si] -= 1
                if npred[si] == 0:
                    ready.append(si)
        assert len(order) == n, (len(order), n)
        self.sched_span = max(done_t) if done_t else 0.0
        for i in order:
            kind, eng, fn, reads, writes, cost, lat = rec[i]
            if kind == 'op':
                self.op(eng, fn, reads, writes)
            else:
                out, in_, kw = fn
                self.dma(eng, out, in_, reads, writes, **kw)

    def op(self, eng, fn, reads=(), writes=(), cost=None):
        if self.disabled:
            return None
        if self.recording is not None:
            reads, writes = self._excl(reads, writes)
            if cost is None:
                cost = {'pe': 0.2, 'act': 0.45, 'dve': 0.45, 'pool': 0.7, 'sp': 0.1}[eng]
            self.recording.append(('op', eng, fn, list(reads), list(writes), cost, 0.0))
            return None
        reads, writes = self._excl(reads, writes)
        deps = self._deps(reads, writes)
        self._wait(eng, deps)
        ins = fn(self.engs[eng])
        if self.cnt[eng] >= self.LIMIT:
            self.epoch[eng] += 1
            ep = self.epoch[eng]
            self.sems[(eng, ep)] = self.es.enter_context(self.nc.semaphore(f"s_{eng}_{ep}"))
            self.cnt[eng] = 0
        sk = (eng, self.epoch[eng])
        self.cnt[eng] += 1
        ins.then_inc(self.sems[sk], 1)
        self._record((sk, self.cnt[eng]), reads, writes)
        self.n_ins += 1
        return ins

    def dma(self, eng, out, in_, reads=(), writes=(), **kw):
        if self.disabled:
            return None
        if self.recording is not None:
            self.recording.append(('dma', eng, (out, in_, kw), list(reads), list(writes), 0.15, 6.0))
            return None
        deps = self._deps(reads, writes)
        sk = self.dma_sems[self.dma_rr]
        self.dma_rr = (self.dma_rr + 1) % len(self.dma_sems)
        if self.cnt[sk] > 0:
            deps.add((sk, self.cnt[sk]))
        self._wait(eng, deps)
        ins = self.engs[eng].dma_start(out=out, in_=in_, **kw)
        self.cnt[sk] += 16
        ins.then_inc(self.sems[sk], 16)
        self._record((sk, self.cnt[sk]), reads, writes)
        self.n_ins += 1
        return ins

    def all_tokens(self):
        deps = set()
        for k, b in self.bufs.items():
            if b['w']:
                deps.add(b['w'])
            deps.update(b['r'])
        return deps

    def barrier(self):
        deps = self.all_tokens()
        for eng in self.engs:
            d = set(x for x in deps)
            self._wait(eng, d)

    def finish(self, eng='sp'):
        self._wait(eng, self.all_tokens())


class Builder:
    def __init__(self, T, layers, do_final=True, neu_dt=None):
        self.neu_dt = neu_dt if neu_dt is not None else BF16
        self.use_sched = True
        self.T = T
        self.layers = layers
        self.do_final = do_final
        self.nc = bass.Bass("TRN2", target_bir_lowering=False)
        self.inputs = {}

    def din(self, name, shape):
        t = self.nc.dram_tensor(name, list(shape), F32, kind="ExternalInput").ap()
        self.inputs[name] = t
        return t

    def sb(self, name, shape, dt=F32):
        return self.es.enter_context(self.nc.sbuf_tensor(name, list(shape), dt))

    def lsb(self, name, shape, dt=F32):
        return self.les.enter_context(self.nc.sbuf_tensor(f"{name}_{self.lname}", list(shape), dt))

    def next_ps(self):
        pool = self.ps_pool
        i = pool[self.ps_rr % len(pool)]
        self.ps_rr += 1
        return self.psums[i], f"ps{i}"

    @staticmethod
    def ecost(eng, ap):
        try:
            n = ap.free_size()
        except Exception:
            n = 256
        if eng == 'act':
            return 0.22 + n * 0.00075
        if eng == 'dve':
            return 0.2 + n * 0.00095
        if eng == 'pool':
            return 0.2 + n * 0.0021
        return 0.2

    def tt(self, eng, out, in0, in1, op, reads, writes):
        return self.p.op(eng, lambda e: e.tensor_tensor(out=out, in0=in0, in1=in1, op=op), reads, writes, cost=self.ecost(eng, out))

    def ts(self, eng, out, in0, s1, s2, op0, op1, reads, writes):
        if s2 is None:
            return self.p.op(eng, lambda e: e.tensor_scalar(out=out, in0=in0, scalar1=s1, scalar2=None, op0=op0), reads, writes, cost=self.ecost(eng, out))
        return self.p.op(eng, lambda e: e.tensor_scalar(out=out, in0=in0, scalar1=s1, scalar2=s2, op0=op0, op1=op1), reads, writes, cost=self.ecost(eng, out))

    def stt(self, eng, out, in0, scalar, in1, op0, op1, reads, writes):
        eng = 'dve'
        return self.p.op(eng, lambda e: e.scalar_tensor_tensor(out=out, in0=in0, scalar=scalar, in1=in1, op0=op0, op1=op1), reads, writes, cost=self.ecost(eng, out))

    def act(self, out, in_, func, reads, writes, bias=None, scale=1.0, accum_out=None):
        kw = {}
        if bias is not None:
            kw['bias'] = bias
        if accum_out is not None:
            kw['accum_out'] = accum_out
        return self.p.op('act', lambda e: e.activation(out=out, in_=in_, func=func, scale=scale, **kw), reads, writes, cost=self.ecost('act', in_))

    def mm(self, out, lhsT, rhs, start, stop, reads, writes):
        try:
            n = rhs.free_size()
        except Exception:
            n = 128
        c = 0.06 + n / 2400.0 * (1.0 if lhsT.dtype == BF16 else 2.4)
        return self.p.op('pe', lambda e: e.matmul(out, lhsT=lhsT, rhs=rhs, start=start, stop=stop), reads, writes, cost=c)

    def copy(self, eng, out, in_, reads, writes):
        if eng == 'act':
            return self.p.op('act', lambda e: e.copy(out=out, in_=in_), reads, writes, cost=self.ecost('act', out))
        return self.p.op(eng, lambda e: e.tensor_copy(out=out, in_=in_), reads, writes, cost=self.ecost(eng, out))

    def load_weight_bf16(self, dst, dst_key, src, ncols, src_c0=0, dst_c0=0, scale_bc=None):
        p = self.p
        CH = 1024 if ncols % 1024 == 0 else ncols
        for kc in range(NKC):
            for c0 in range(0, ncols, CH):
                i = self.stage_i
                self.stage_i += 1
                nst = len(self.stage)
                st = self.stage[i % nst]
                sk = f"stage{i % nst}"
                p.dma('sp', st[:, 0:CH], src[kc * 128:(kc + 1) * 128, src_c0 + c0:src_c0 + c0 + CH], reads=[], writes=[sk])
                eng = ['dve', 'act'][i % 2] if scale_bc is None else ['dve', 'pool'][i % 2]
                if scale_bc is None:
                    self.copy(eng, dst[:, kc, dst_c0 + c0:dst_c0 + c0 + CH], st[:, 0:CH], [sk], [dst_key])
                else:
                    self.tt(eng, dst[:, kc, dst_c0 + c0:dst_c0 + c0 + CH], st[:, 0:CH], scale_bc[:, c0:c0 + CH], ALU.mult,
                            [sk, 'bc_tiles'], [dst_key])

    def rms_rstd(self, src, src_key, TT, tag):
        bi = self.rms_i % len(self.sqb_l)
        self.rms_i += 1
        sqb, rstd = self.sqb_l[bi], self.rstd_l[bi]
        ksq, krs = f'sqb{bi}', f'rstd{bi}'
        if self.sqb_alias:
            ksq = 'yT0'
        ps, pk = self.next_ps()
        if self.sqb_alias:
            for kc in range(NKC):
                sl = kc % 3
                if kc % 2 == 0:
                    self.act(sqb[:, sl, :TT], src[:, kc, :TT], AF.Square, [src_key], [('sqs', sl)])
                else:
                    self.tt('dve', sqb[:, sl, :TT], src[:, kc, :TT], src[:, kc, :TT], ALU.mult, [src_key], [('sqs', sl)])
                self.mm(ps[:, :TT], self.ones_bf[:], sqb[:, sl, :TT], kc == 0, kc == NKC - 1, [('sqs', sl), 'ones_bf'], [pk])
        else:
            for kc in range(NKC):
                if kc % 2 == 0:
                    self.act(sqb[:, kc, :TT], src[:, kc, :TT], AF.Square, [src_key], [(ksq, kc)])
                else:
                    self.tt('dve', sqb[:, kc, :TT], src[:, kc, :TT], src[:, kc, :TT], ALU.mult, [src_key], [(ksq, kc)])
            for kc in range(NKC):
                self.mm(ps[:, :TT], self.ones_bf[:], sqb[:, kc, :TT], kc == 0, kc == NKC - 1, [(ksq, kc), 'ones_bf'], [pk])
        self.act(rstd[:, :TT], ps[:, :TT], AF.Ln, [pk, 'consts'], [krs], bias=self.epsc[:, 0:1], scale=1.0 / D)
        self.act(rstd[:, :TT], rstd[:, :TT], AF.Exp, [krs], [krs], scale=-0.5)
        return rstd, krs

    def run_layer(self, li, kind, TT, w_in_cols, mixer_setup, mixer_tile, is_last):
        p = self.p
        T = self.T
        ntiles = T // TT
        with ExitStack() as les:
            self.les = les
            self.lname = f"L{li}"
            self.TT = TT
            self.W_in = self.lsb("W_in", [128, NKC, w_in_cols], BF16)
            self.W_out = self.lsb("W_out", [128, NKC, D], BF16)
            ndb = 2 if kind != 'rwkv' else 1
            self.hT = [self.lsb(f"hT{i}", [128, NKC, TT], F32) for i in range(2)]
            self.sqb_alias = (kind == 'rwkv')
            if not self.sqb_alias:
                self.sqb_l = [self.lsb(f"sqb{i}", [128, NKC, TT], BF16) for i in range(ndb)]
            self.rstd_l = [self.lsb(f"rstd{i}", [128, TT], F32) for i in range(ndb)]
            self.rms_i = 0
            self.hn_l = [self.lsb(f"hn{i}", [128, NKC, TT + 1], BF16) for i in range(ndb)]
            self.yTt_l = [self.lsb(f"yTt{i}", [128, NKC, TT], BF16) for i in range(ndb)]
            self.hn, self.hnk = self.hn_l[0], 'hn0'
            self.yTt, self.yTk = self.yTt_l[0], 'yT0'
            if self.sqb_alias:
                self.sqb_l = [self.lsb("sqs", [128, 3, TT], BF16)]
            self.stage_i = 0
            loader = mixer_setup()
            with ExitStack() as ses:
                nst = 2 if kind == 'rwkv' else max(2, min(4, (self.nc.sbuf_bytes_remaining - 512) // 4096))
                self.stage = [ses.enter_context(self.nc.sbuf_tensor(f"stage{i}_{self.lname}", [128, 1024], F32)) for i in range(nst)]
                if loader is None:
                    self.load_weight_bf16(self.W_in, 'W_in', self.w_in_dram[kind], w_in_cols)
                    self.load_weight_bf16(self.W_out, 'W_out', self.w_out_dram[kind], D)
                else:
                    loader(ses)
                p.barrier()
            if getattr(self, 'post_setup', None) is not None:
                self.post_setup()
                self.post_setup = None
            hn0 = self.hn_l[0]
            p.op('pool', lambda e: e.memset(hn0[:, :, 0:1], 0.0), [], ['hn0'])

            def load(ti):
                buf = self.hT[ti % len(self.hT)]
                src = self.xT if self.first_layer else self.yT
                p.dma('sp', buf[:], src.rearrange("(c p) t -> p c t", p=128)[:, :, ti * TT:(ti + 1) * TT],
                      reads=[('hd', ti * TT // 128 + i) for i in range(TT // 128)], writes=[f"hT{ti % len(self.hT)}"])

            if self.use_sched:
                p.begin_record()
            load(0)
            for ti in range(ntiles):
                if len(self.hT) > 1:
                    if ti + 1 < ntiles:
                        load(ti + 1)
                elif ti > 0:
                    load(ti)
                h = self.hT[ti % len(self.hT)]
                hk = f"hT{ti % len(self.hT)}"
                rstd, rk = self.rms_rstd(h, hk, TT, 'in')
                g = self.norm_g
                prev_hn, prev_hnk = self.hn, self.hnk
                bi = ti % ndb
                self.hn, self.hnk = self.hn_l[bi], f'hn{bi}'
                self.yTt, self.yTk = self.yTt_l[bi], f'yT{bi}'
                if ti > 0:
                    self.copy('pool', self.hn[:, :, 0:1], prev_hn[:, :, TT:TT + 1], [prev_hnk], [self.hnk])
                for kc in range(NKC):
                    self.stt('dve', self.hn[:, kc, 1:TT + 1], h[:, kc, :], g[:, li, kc:kc + 1], rstd[:, :TT],
                             ALU.mult, ALU.mult, [hk, rk, 'consts'], [self.hnk])
                mixer_tile(ti)
                for j in range(NKC):
                    ps, pk = self.next_ps()
                    for kc in range(NKC):
                        self.mm(ps[:, :TT], self.W_out[:, kc, j * 128:(j + 1) * 128], self.yTt[:, kc, :TT],
                                kc == 0, kc == NKC - 1, ['W_out', self.yTk], [pk])
                    self.tt('dve', h[:, j, :], h[:, j, :], ps[:, :TT], ALU.add, [hk, pk], [hk])
                if is_last and self.do_final:
                    rstd, rk = self.rms_rstd(h, hk, TT, 'fin')
                    for kc in range(NKC):
                        self.stt('dve' if kc % 2 == 0 else 'pool', h[:, kc, :], h[:, kc, :], self.final_g[:, kc:kc + 1], rstd[:, :TT],
                                 ALU.mult, ALU.mult, [hk, rk, 'consts'], [hk])
                p.dma('sp', self.yT.rearrange("(c p) t -> p c t", p=128)[:, :, ti * TT:(ti + 1) * TT], h[:],
                      reads=[hk], writes=[('hd', (ti * TT) // 128 + i) for i in range(max(1, TT // 128))])
            if self.use_sched:
                p.flush()
            self.first_layer = False
            p.barrier()
        self.les = None

    def conv_layer(self, li, is_last):
        TT = 512

        def setup():
            self.yext = self.lsb("yext", [128, NKC, TT + 2], F32)
            self.zs = [self.lsb(f"zs{i}", [128, TT], F32) for i in range(2)]
            self.acc = [self.lsb(f"acc{i}", [128, TT], F32) for i in range(2)]
            self.sg = [self.lsb(f"sg{i}", [128, TT], F32) for i in range(2)]
            self.p.op('pool', lambda e: e.memset(self.yext[:, :, 0:2], 0.0), [], [('yext', j) for j in range(NKC)])

        def tile(ti):
            W = self.W_in
            cw = self.conv_w
            for j in range(NKC):
                zs, acc, sg = self.zs[j % 2], self.acc[j % 2], self.sg[j % 2]
                zk, ak, gk = f"zs{j % 2}", f"acc{j % 2}", f"sg{j % 2}"
                pss = []
                for blk in range(4):
                    ps, pk = self.next_ps()
                    col0 = blk * D + j * 128
                    for kc in range(NKC):
                        self.mm(ps[:, :TT], W[:, kc, col0:col0 + 128], self.hn[:, kc, 1:TT + 1], kc == 0, kc == NKC - 1,
                                ['W_in', self.hnk], [pk])
                    pss.append((ps, pk))
                (pb, pbk), (pc, pck), (pz, pzk), (pg, pgk) = pss
                yk = ('yext', j)
                self.copy('act', zs[:], pz[:, :TT], [pzk], [zk])
                if ti > 0:
                    self.copy('pool', self.yext[:, j, 0:2], self.yext[:, j, TT:TT + 2], [yk], [yk])
                self.tt('dve', self.yext[:, j, 2:TT + 2], pc[:, :TT], zs[:], ALU.mult, [pck, zk], [yk])
                self.act(acc[:], self.yext[:, j, 2:TT + 2], AF.Copy, [yk, 'consts'], [ak], scale=cw[:, j, 2:3])
                self.stt('pool', acc[:], self.yext[:, j, 1:TT + 1], cw[:, j, 1:2], acc[:], ALU.mult, ALU.add, [yk, ak, 'consts'], [ak])
                self.stt('pool', acc[:], self.yext[:, j, 0:TT], cw[:, j, 0:1], acc[:], ALU.mult, ALU.add, [yk, ak, 'consts'], [ak])
                self.act(sg[:], pg[:, :TT], AF.Silu, [pgk], [gk])
                self.tt('dve', acc[:], pb[:, :TT], acc[:], ALU.mult, [pbk, ak], [ak])
                self.tt('pool', self.yTt[:, j, :], acc[:], sg[:], ALU.mult, [ak, gk], [self.yTk])

        self.run_layer(li, 'conv', TT, 4 * D, setup, tile, is_last)

    def gmlp_layer(self, li, is_last):
        TT = 512

        def setup():
            p = self.p
            self.wsT = self.lsb("wsT", [128, 8, 128], F32)
            self.bs_bc = self.lsb("bs_bc", [128, 8, TT], F32)
            self.vg_bc = self.lsb("vg_bc", [128, D], F32)
            self.vn = [self.lsb(f"vn{i}", [128, D], F32) for i in range(TT // 128)]
            self.vss = self.lsb("vss", [128, 4], F32)
            self.junk = self.lsb("junk", [128, 512], F32)
            self.s_sb = [self.lsb(f"s_sb{i}", [128, TT], F32) for i in range(2)]
            self.sg = [self.lsb(f"sg{i}", [128, TT], F32) for i in range(2)]
            p.dma('sp', self.wsT[:], self.inputs['gmlp_wsT'], [], ['wsT'])
            for g in range(8):
                p.op('pool', lambda e: e.affine_select(out=self.wsT[:, g, :], in_=self.wsT[:, g, :], pattern=[[1, 128]],
                                                       compare_op=ALU.is_ge, fill=0.0, base=0, channel_multiplier=-1),
                     ['wsT'], ['wsT'])
            for r in range(TT // 128):
                p.dma('sp', self.bs_bc[:, :, r * 128:(r + 1) * 128],
                      self.inputs['gmlp_bs'].partition_broadcast(128), [], ['bs_bc'])
            p.dma('sp', self.vg_bc[:], self.inputs['gmlp_vg'].partition_broadcast(128), [], ['vg_bc'])

        def tile(ti):
            W = self.W_in
            nblk = TT // 128
            for blk in range(nblk):
                vn = self.vn[blk]
                vk = f"vn{blk}"
                halves = []
                for hf in range(2):
                    ps, pk = self.next_ps()
                    for kc in range(NKC):
                        self.mm(ps[:, :512], self.hn[:, kc, 1 + blk * 128:1 + (blk + 1) * 128],
                                W[:, kc, D + hf * 512:D + (hf + 1) * 512], kc == 0, kc == NKC - 1, ['W_in', self.hnk], [pk])
                    halves.append((ps, pk))
                for hf, (ps, pk) in enumerate(halves):
                    self.act(self.junk[:], ps[:, :512], AF.Square, [pk], ['junk', 'vss'], accum_out=self.vss[:, hf:hf + 1])
                self.tt('dve', self.vss[:, 2:3], self.vss[:, 0:1], self.vss[:, 1:2], ALU.add, ['vss'], ['vss'])
                self.act(self.vss[:, 3:4], self.vss[:, 2:3], AF.Ln, ['vss', 'consts'], ['vss'], bias=self.epsc[:, 0:1], scale=1.0 / D)
                self.act(self.vss[:, 3:4], self.vss[:, 3:4], AF.Exp, ['vss'], ['vss'], scale=-0.5)
                for hf, (ps, pk) in enumerate(halves):
                    self.stt('dve', vn[:, hf * 512:(hf + 1) * 512], ps[:, :512], self.vss[:, 3:4],
                             self.vg_bc[:, hf * 512:(hf + 1) * 512], ALU.mult, ALU.mult, [pk, 'vss', 'vg_bc'], [vk])
            for j in range(NKC):
                s_sb, sg = self.s_sb[j % 2], self.sg[j % 2]
                sk, gk = f"s_sb{j % 2}", f"sg{j % 2}"
                ps, pk = self.next_ps()
                for blk in range(nblk):
                    self.mm(ps[:, blk * 128:(blk + 1) * 128], self.vn[blk][:, j * 128:(j + 1) * 128], self.wsT[:, j, :], True, True,
                            [f"vn{blk}", 'wsT'], [pk])
                self.tt('dve', s_sb[:], ps[:, :TT], self.bs_bc[:, j, :], ALU.add, [pk, 'bs_bc'], [sk])
                pu, puk = self.next_ps()
                for kc in range(NKC):
                    self.mm(pu[:, :TT], W[:, kc, j * 128:(j + 1) * 128], self.hn[:, kc, 1:TT + 1], kc == 0, kc == NKC - 1,
                            ['W_in', self.hnk], [puk])
                pg, pgk = self.next_ps()
                for kc in range(NKC):
                    self.mm(pg[:, :TT], W[:, kc, 2 * D + j * 128:2 * D + (j + 1) * 128], self.hn[:, kc, 1:TT + 1], kc == 0,
                            kc == NKC - 1, ['W_in', self.hnk], [pgk])
                self.act(sg[:], pg[:, :TT], AF.Silu, [pgk], [gk])
                self.tt('dve', s_sb[:], pu[:, :TT], s_sb[:], ALU.mult, [puk, sk], [sk])
                self.tt('pool', self.yTt[:, j, :], s_sb[:], sg[:], ALU.mult, [sk, gk], [self.yTk])

        self.run_layer(li, 'gmlp', TT, 3 * D, setup, tile, is_last)


    def make_ident(self, ident, key):
        p = self.p
        p.op('pool', lambda e: e.memset(ident[:], 1.0), [], [key])
        p.op('pool', lambda e: e.affine_select(out=ident[:], in_=ident[:], pattern=[[-1, 128]], compare_op=ALU.is_equal,
                                               fill=0.0, base=0, channel_multiplier=1), [key], [key])

    def make_block_masks(self, C, maskT, colmask, rowmask, strict=False):
        p = self.p
        nch = 128 // C
        if maskT is not None:
            p.op('pool', lambda e: e.memset(maskT[:], 1.0), [], ['masks'])
            p.op('pool', lambda e: e.affine_select(out=maskT[:], in_=maskT[:], pattern=[[1, 128]], compare_op=ALU.is_ge if not strict else ALU.is_gt,
                                                   fill=0.0, base=0, channel_multiplier=-1), ['masks'], ['masks'])
            for c in range(1, nch):
                p.op('pool', lambda e, c=c: e.affine_select(out=maskT[:, c * C:(c + 1) * C], in_=maskT[:, c * C:(c + 1) * C], pattern=[[0, C]],
                                                            compare_op=ALU.is_ge, fill=0.0, base=-c * C, channel_multiplier=1), ['masks'], ['masks'])
        if colmask is not None:
            p.op('pool', lambda e: e.memset(colmask[:], 0.0), [], ['masks'])
            for c in range(nch):
                p.op('pool', lambda e, c=c: e.memset(colmask[:, c, c * C:(c + 1) * C], 1.0), ['masks'], ['masks'])
        if rowmask is not None:
            p.op('pool', lambda e: e.memset(rowmask[:], 1.0), [], ['masks'])
            for c in range(nch):
                p.op('pool', lambda e, c=c: e.affine_select(out=rowmask[:, c:c + 1], in_=rowmask[:, c:c + 1], pattern=[[0, 1]],
                                                            compare_op=ALU.is_ge, fill=0.0, base=-c * C, channel_multiplier=1), ['masks'], ['masks'])
                p.op('pool', lambda e, c=c: e.affine_select(out=rowmask[:, c:c + 1], in_=rowmask[:, c:c + 1], pattern=[[0, 1]],
                                                            compare_op=ALU.is_ge, fill=0.0, base=c * C + C - 1, channel_multiplier=-1), ['masks'], ['masks'])

    def hgrn_layer(self, li, is_last):
        TT = 256
        C = 32
        NB = TT // 128
        NCH = TT // C

        def setup():
            p = self.p
            L = self.lsb
            self.ident = L("ident", [128, 128], F32)
            self.make_ident(self.ident, 'ident')
            self.maskT = L("maskT", [128, 128], F32)
            self.colmask = L("colmask", [128, 4, 128], F32)
            self.rowmask = L("rowmask", [128, 4], F32)
            self.make_block_masks(C, self.maskT, self.colmask, self.rowmask)
            self.resetm = L("resetm", [128, TT], F32)
            self.ones_t = L("ones_t", [128, TT], F32)
            p.op('pool', lambda e: e.memset(self.ones_t[:], 1.0), [], ['masks'])
            p.op('pool', lambda e: e.memset(self.resetm[:], 1.0), [], ['masks'])
            p.op('pool', lambda e: e.memset(self.resetm[:].rearrange("p (n c) -> p n c", c=C)[:, :, 0:1], 0.0), ['masks'], ['masks'])
            self.gn_bc = L("gn_bc", [128, D], F32)
            p.dma('sp', self.gn_bc[:], self.inputs['hgrn_gn_g'].partition_broadcast(128), [], ['gn_bc'])
            self.lbl = L("lbl", [128, 4, NKC], F32)
            self.lbt = L("lbt", [128, 4, NKC], F32)
            p.dma('sp', self.lbl[:], self.inputs['hgrn_lbl'], [], ['lbl'])
            self.act(self.lbl[:], self.lbl[:], AF.Exp, ['lbl'], ['lbl'])
            self.tt('dve', self.lbt[:, 0, :], self.lbl[:, 0, :], self.lbl[:, 1, :], ALU.add, ['lbl'], ['lbt'])
            self.tt('dve', self.lbt[:, 0, :], self.lbt[:, 0, :], self.lbl[:, 2, :], ALU.add, ['lbl', 'lbt'], ['lbt'])
            self.tt('dve', self.lbt[:, 0, :], self.lbt[:, 0, :], self.lbl[:, 3, :], ALU.add, ['lbl', 'lbt'], ['lbt'])
            p.op('dve', lambda e: e.reciprocal(out=self.lbt[:, 3, :], in_=self.lbt[:, 0, :]), ['lbt'], ['lbt'])
            p.op('dve', lambda e: e.memset(self.lbt[:, 1, :], 0.0), ['lbt'], ['lbt'])
            for i in range(1, li + 1):
                self.tt('dve', self.lbt[:, 1, :], self.lbt[:, 1, :], self.lbl[:, i, :], ALU.add, ['lbl', 'lbt'], ['lbt'])
            self.tt('dve', self.lbt[:, 1, :], self.lbt[:, 1, :], self.lbt[:, 3, :], ALU.mult, ['lbt'], ['lbt'])
            self.ts('dve', self.lbt[:, 2, :], self.lbt[:, 1, :], -1.0, 1.0, ALU.mult, ALU.add, ['lbt'], ['lbt'])
            self.S = L("S_hgrn", [128, NKC, 128], F32)
            p.op('pool', lambda e: e.memset(self.S[:], 0.0), [], [('S', j) for j in range(NKC)])
            names = ['f', 'kk', 'bb', 'qe', 'dd', 'sg']
            self.tmps = []
            for q in range(2):
                tm = {n: L(f"h_{n}{q}", [128, TT], F32) for n in names}
                tm['e1'] = tm['f']
                tm['ko'] = tm['dd']
                tm['ke_bf'] = L(f"h_ke_bf{q}", [128, TT], BF16)
                tm['qe_bf'] = L(f"h_qe_bf{q}", [128, TT], BF16)
                tm['kom'] = L(f"kom{q}", [128, 4, NB, 128], BF16)
                self.tmps.append(tm)
            self.sgate = [L(f"sgate{q}", [128, TT], BF16) for q in range(3)]
            self.qem = [L(f"qem{q}", [128, 4, TT], BF16) for q in range(3)]
            self.v_bf = [L(f"v_bf{q}", [128, NB, 128], BF16) for q in range(3)]
            self.attm = [L(f"attm{q}", [128, NB, 128], BF16) for q in range(3)]
            self.u_sb = [L(f"u_sb{q}", [128, NCH, 128], F32) for q in range(3)]
            self.dec = [L(f"dec{q}", [128, NCH], F32) for q in range(3)]
            self.S_all2 = [L(f"S_all{q}", [128, 5, 128], F32) for q in range(2)]
            self.S_bf2 = [L(f"S_bf{q}", [128, NCH, 128], BF16) for q in range(2)]
            self.on2 = [L(f"on{q}", [128, NB, 128], F32) for q in range(2)]
            self.oss2 = [L(f"oss{q}", [128, 2 * NB], F32) for q in range(2)]
            self.junk2 = [L(f"junk{q}", [128, 128], F32) for q in range(2)]
            self.ps_pool = [0, 1, 2, 3]
            self.nm_rr = 0

        def nm_ps():
            i = [4, 5][self.nm_rr % 2]
            self.nm_rr += 1
            return self.psums[i], f"ps{i}"

        def proj(col0):
            ps, pk = self.next_ps()
            for kc in range(NKC):
                self.mm(ps[:, :TT], self.W_in[:, kc, col0:col0 + 128], self.hn[:, kc, 1:TT + 1], kc == 0, kc == NKC - 1, ['W_in', self.hnk], [pk])
            return ps, pk

        def A_gen(j):
            q2 = j % 2
            t = self.tmps[q2]
            kom = t['kom']
            W = self.W_in
            q = j % 3
            lb, oml = self.lbt[:, 1, :], self.lbt[:, 2, :]
            sgate, qem, v_bf, attm, u_sb, dec = self.sgate[q], self.qem[q], self.v_bf[q], self.attm[q], self.u_sb[q], self.dec[q]
            ksg, kqem, kv, katt, ku, kdec = f'sgate{q}', f'qem{q}', f'v_bf{q}', f'attm{q}', f'u_sb{q}', f'dec{q}'
            pf, pfk = proj(D + j * 128)
            self.act(t['f'][:], pf[:, :TT], AF.Exp, [pfk], [f't_f{q2}'], scale=-1.0)
            self.tt('pool', t['f'][:], t['f'][:], self.ones_t[:], ALU.add, [f't_f{q2}', 'masks'], [f't_f{q2}'])
            self.p.op('dve', lambda e: e.reciprocal(out=t['f'][:], in_=t['f'][:]), [f't_f{q2}'], [f't_f{q2}'], cost=0.45)
            self.ts('dve', t['f'][:], t['f'][:], oml[:, j:j + 1], lb[:, j:j + 1], ALU.mult, ALU.add, [f't_f{q2}', 'lbt'], [f't_f{q2}'])
            self.act(t['kk'][:], t['f'][:], AF.Identity, [f't_f{q2}'], [f't_kk{q2}'], scale=-1.0, bias=self.epsc[:, 2:3])
            self.act(t['dd'][:], t['f'][:], AF.Ln, [f't_f{q2}'], [f't_dd{q2}'])
            self.p.op('dve', lambda e: e.tensor_tensor_scan(out=t['bb'][:], data0=self.resetm[:], data1=t['dd'][:], initial=0.0,
                                                            op0=ALU.mult, op1=ALU.add), [f't_dd{q2}', 'masks'], [f't_bb{q2}'])
            yield
            pq, pqk = proj(j * 128)
            self.act(t['e1'][:], t['bb'][:], AF.Exp, [f't_bb{q2}'], [f't_f{q2}'])
            self.tt('dve', t['qe'][:], pq[:, :TT], t['e1'][:], ALU.mult, [pqk, f't_f{q2}'], [f't_qe{q2}'])
            self.copy('act', t['qe_bf'][:], t['qe'][:], [f't_qe{q2}'], [f't_qe_bf{q2}'])
            qe4 = t['qe'][:].rearrange("p (b t) -> p b t", t=128)
            for c in range(4):
                self.tt('pool', qem[:, c, :].rearrange("p (b t) -> p b t", t=128), qe4,
                        self.colmask[:, c:c + 1, :].to_broadcast([128, NB, 128]), ALU.mult, [f't_qe{q2}', 'masks'], [kqem])
            yield
            self.act(t['e1'][:], t['bb'][:], AF.Exp, [f't_bb{q2}'], [f't_f{q2}'], scale=-1.0)
            self.tt('pool', t['ke_bf'][:], t['kk'][:], t['e1'][:], ALU.mult, [f't_kk{q2}', f't_f{q2}'], [f't_ke_bf{q2}'])
            b3 = t['bb'][:].rearrange("p (n c) -> p n c", c=C)
            self.act(dec[:], b3[:, :, C - 1], AF.Exp, [f't_bb{q2}'], [kdec])
            self.tt('pool', t['dd'][:].rearrange("p (n c) -> p n c", c=C), b3[:, :, C - 1:C].to_broadcast([128, NCH, C]), b3, ALU.subtract,
                    [f't_bb{q2}'], [f't_dd{q2}'])
            self.act(t['dd'][:], t['dd'][:], AF.Exp, [f't_dd{q2}'], [f't_dd{q2}'])
            self.tt('pool', t['ko'][:], t['kk'][:], t['dd'][:], ALU.mult, [f't_kk{q2}', f't_dd{q2}'], [f't_dd{q2}'])
            pg, pgk = proj(3 * D + j * 128)
            self.act(t['sg'][:], pg[:, :TT], AF.Exp, [pgk], [f't_sg{q2}'], scale=-1.0)
            self.tt('pool', t['sg'][:], t['sg'][:], self.ones_t[:], ALU.add, [f't_sg{q2}', 'masks'], [f't_sg{q2}'])
            self.p.op('dve', lambda e: e.reciprocal(out=t['sg'][:], in_=t['sg'][:]), [f't_sg{q2}'], [f't_sg{q2}'], cost=0.45)
            self.tt('dve', sgate[:], pg[:, :TT], t['sg'][:], ALU.mult, [pgk, f't_sg{q2}'], [ksg])
            yield
            pv, pvk = self.next_ps()
            for blk in range(NB):
                for kc in range(NKC):
                    self.mm(pv[:, blk * 128:(blk + 1) * 128], self.hn[:, kc, 1 + blk * 128:1 + (blk + 1) * 128],
                            W[:, kc, 2 * D + j * 128:2 * D + (j + 1) * 128], kc == 0, kc == NKC - 1, ['W_in', self.hnk], [pvk])
            self.copy('act', v_bf[:].rearrange("p b v -> p (b v)"), pv[:, :TT], [pvk], [kv])
            yield
            ps, pk = nm_ps()
            for blk in range(NB):
                cs = slice(blk * 128, (blk + 1) * 128)
                self.mm(ps[:, cs], t['ke_bf'][:, cs], t['qe_bf'][:, cs], True, True, [f't_ke_bf{q2}', f't_qe_bf{q2}'], [pk])
            self.tt('dve', attm[:], ps[:, :TT].rearrange("p (b t) -> p b t", t=128), self.maskT[:, None, :].to_broadcast([128, NB, 128]),
                    ALU.mult, [pk, 'masks'], [katt])
            ps, pk = nm_ps()
            for blk in range(NB):
                cs = slice(blk * 128, (blk + 1) * 128)
                self.p.op('pe', lambda e, ps=ps, cs=cs: e.transpose(ps[:, cs], t['ko'][:, cs], self.ident[:]), [f't_dd{q2}', 'ident'], [pk])
            for c in range(4):
                self.act(kom[:, c, :, :].rearrange("p b k -> p (b k)"), ps[:, :TT], AF.Copy, [pk, 'masks'], [f'kom{q2}'],
                         scale=self.rowmask[:, c:c + 1])
            yield
            for blk in range(NB):
                ps, pk = nm_ps()
                for c in range(4):
                    self.mm(ps[:, c * 128:(c + 1) * 128], kom[:, c, blk, :], v_bf[:, blk, :], True, True, [f'kom{q2}', kv], [pk])
                self.copy('act' if blk % 2 else 'dve', u_sb[:, blk * 4:(blk + 1) * 4, :].rearrange("p c v -> p (c v)"), ps[:, 0:512], [pk], [ku])
                if blk % 2:
                    yield

        def B_gen(j):
            P = self.psums
            q = j % 3
            sgate, qem, v_bf, attm, u_sb, dec = self.sgate[q], self.qem[q], self.v_bf[q], self.attm[q], self.u_sb[q], self.dec[q]
            ksg, kqem, kv, katt, ku, kdec = f'sgate{q}', f'qem{q}', f'v_bf{q}', f'attm{q}', f'u_sb{q}', f'dec{q}'
            kS = ('S', j)
            q2 = j % 2
            SA = self.S_all2[q2]
            S_bf, on, oss, junk = self.S_bf2[q2], self.on2[q2], self.oss2[q2], self.junk2[q2]
            kSA, kSbf, kon, koss, kjunk = f'S_all{q2}', f'S_bf{q2}', f'on{q2}', f'oss{q2}', f'junk{q2}'
            self.copy('pool', SA[:, 0, :], self.S[:, j, :], [kS], [kSA])
            for blk in range(NB):
                for c in range(4):
                    n = blk * 4 + c
                    self.stt('dve', SA[:, c + 1, :], SA[:, c, :], dec[:, n:n + 1], u_sb[:, n, :], ALU.mult, ALU.add, [kSA, kdec, ku], [kSA])
                self.copy('act', S_bf[:, blk * 4:(blk + 1) * 4, :].rearrange("p c v -> p (c v)"),
                          SA[:, 0:4, :].rearrange("p c v -> p (c v)"), [kSA], [kSbf])
                if blk < NB - 1:
                    self.copy('dve', SA[:, 0, :], SA[:, 4, :], [kSA], [kSA])
                yield
            self.copy('pool', self.S[:, j, :], SA[:, 4, :], [kSA], [kS])
            po, pok = P[6], 'ps6'
            for blk in range(NB):
                cs = slice(blk * 128, (blk + 1) * 128)
                self.mm(po[:, cs], attm[:, blk, :], v_bf[:, blk, :], True, False, [katt, kv], [pok])
                for c in range(4):
                    self.mm(po[:, cs], qem[:, c, cs], S_bf[:, blk * 4 + c, :], False, c == 3, [kqem, kSbf], [pok])
                if blk % 2:
                    yield
            for blk in range(NB):
                cs = slice(blk * 128, (blk + 1) * 128)
                self.act(junk[:], po[:, cs], AF.Square, [pok], [kjunk, koss], accum_out=oss[:, blk:blk + 1])
            self.act(oss[:, NB:2 * NB], oss[:, 0:NB], AF.Ln, [koss, 'consts'], [koss], bias=self.epsc[:, 0:1], scale=1.0 / 128)
            self.act(oss[:, NB:2 * NB], oss[:, NB:2 * NB], AF.Exp, [koss], [koss], scale=-0.5)
            self.tt('dve', on[:], po[:, :TT].rearrange("p (b v) -> p b v", v=128),
                    oss[:, NB:2 * NB, None].to_broadcast([128, NB, 128]), ALU.mult, [pok, koss], [kon])
            self.tt('pool', on[:], on[:], self.gn_bc[:, None, j * 128:(j + 1) * 128].to_broadcast([128, NB, 128]), ALU.mult,
                    [kon, 'gn_bc'], [kon])
            yield
            py, pyk = P[7], 'ps7'
            for blk in range(NB):
                cs = slice(blk * 128, (blk + 1) * 128)
                self.p.op('pe', lambda e, cs=cs, blk=blk: e.transpose(py[:, cs], on[:, blk, :], self.ident[:]), [kon, 'ident'], [pyk])
            self.tt('dve', self.yTt[:, j, :], py[:, :TT], sgate[:], ALU.mult, [pyk, ksg], [self.yTk])
            yield

        def drive(gens):
            gens = [g for g in gens if g is not None]
            while gens:
                for g in list(gens):
                    try:
                        next(g)
                    except StopIteration:
                        gens.remove(g)

        def step(g):
            try:
                next(g)
                return True
            except StopIteration:
                return False

        def tile(ti):
            A = {0: A_gen(0), 1: A_gen(1)}
            while step(A[0]):
                step(A[1])
            for sl in range(NKC):
                must = [B_gen(sl)]
                if sl + 1 < NKC:
                    must.append(A[sl + 1])
                opt = None
                if sl + 2 < NKC:
                    A[sl + 2] = A_gen(sl + 2)
                    opt = A[sl + 2]
                while must:
                    for g in list(must):
                        if not step(g):
                            must.remove(g)
                    if opt is not None and not step(opt):
                        opt = None

        self.run_layer(li, 'hgrn', TT, 4 * D, setup, tile, is_last)
        self.ps_pool = list(range(8))

    def rwkv_layer(self, li, is_last):
        TT = 256
        NB = TT // 128
        WC = 3200
        NDT = self.neu_dt
        LC = -0.6065306597126334

        def setup():
            p = self.p
            L = self.lsb
            self.ident = L("ident", [128, 128], F32)
            self.make_ident(self.ident, 'ident')
            self.ident_n = L("ident_n", [128, 128], NDT)
            self.copy('dve', self.ident_n[:], self.ident[:], ['ident'], ['ident'])
            self.maskS = L("maskS", [128, 128], F32)
            self.maskI = L("maskI", [128, 128], F32)
            self.maskSL = L("maskSL", [128, 128], F32)
            for (m, pat, cm, cmp_) in ((self.maskS, 1, -1, ALU.is_gt), (self.maskI, 1, -1, ALU.is_ge), (self.maskSL, -1, 1, ALU.is_gt)):
                p.op('pool', lambda e, m=m: e.memset(m[:], 1.0), [], ['masks'])
                p.op('pool', lambda e, m=m, pat=pat, cm=cm, cmp_=cmp_: e.affine_select(
                    out=m[:], in_=m[:], pattern=[[pat, 128]], compare_op=cmp_, fill=0.0, base=0, channel_multiplier=cm), ['masks'], ['masks'])
            self.blockones = L("blockones", [128, 128], F32)
            p.op('pool', lambda e: e.memset(self.blockones[:], 1.0), [], ['masks'])
            p.op('pool', lambda e: e.memset(self.blockones[0:64, 64:128], 0.0), ['masks'], ['masks'])
            p.op('pool', lambda e: e.memset(self.blockones[64:128, 0:64], 0.0), ['masks'], ['masks'])
            self.resetm = L("resetm", [128, TT], F32)
            p.op('pool', lambda e: e.memset(self.resetm[:], 1.0), [], ['masks'])
            p.op('pool', lambda e: e.memset(self.resetm[:].rearrange("p (n c) -> p n c", c=128)[:, :, 0:1], 0.0), ['masks'], ['masks'])
            p.op('pool', lambda e: e.memset(self.epsc[:, 1:2], GN_EPS), [], ['consts'])
            self.mu_fm = L("mu_fm", [128, 33], F32)
            self.omu_fm = L("omu_fm", [128, 33], F32)
            p.dma('sp', self.mu_fm[:], self.inputs['rwkv_mu_fm'], [], ['rw_vecs'])
            self.ts('dve', self.omu_fm[:], self.mu_fm[:], -1.0, 1.0, ALU.mult, ALU.add, ['rw_vecs'], ['rw_vecs'])
            self.vecs = L("rw_vecs", [128, 5, NKC], F32)
            p.dma('sp', self.vecs[:], self.inputs['rwkv_vecs'], [], ['rw_vecs'])
            self.nvecs = L("rw_nvecs", [128, 2, NKC], F32)
            self.ts('dve', self.nvecs[:], self.vecs[:, 0:2, :], -1.0, None, ALU.mult, None, ['rw_vecs'], ['rw_vecs'])
            self.lw2 = L("lw2", [128, D], BF16)
            self.lo_bf = L("lo_bf", [128, TT], BF16)
            self.gng_bc = L("gng_bc", [128, D], BF16)
            self.gnb_bc = L("gnb_bc", [128, D], BF16)
            self.Wva = L("Wva", [128, NKC, D], BF16)
            self.Wvb = L("Wvb", [128, NKC, D], BF16)
            src = self.w_in_dram['rwkv']

            def loader(tes):
                muv = tes.enter_context(self.nc.sbuf_tensor("muv_bc", [128, D], F32))
                p.dma('sp', muv[:], self.inputs['rwkv_lw2'], [], ['muv'])
                self.copy('dve', self.lw2[:], muv[:], ['muv'], ['lw2'])
                for (dst, nm) in ((self.gng_bc, 'rwkv_gn_g'), (self.gnb_bc, 'rwkv_gn_b')):
                    p.dma('sp', muv[:], self.inputs[nm].partition_broadcast(128), ['muv'], ['muv'])
                    self.copy('dve', dst[:], muv[:], ['muv'], ['bc_tiles'])
                p.dma('sp', muv[:], self.inputs['rwkv_mu'][2 * D:3 * D].partition_broadcast(128), ['muv'], ['bc_tiles', 'muv'])
                self.load_weight_bf16(self.Wvb, 'Wv', src, D, src_c0=2 * D, scale_bc=muv)
                self.ts('dve', muv[:], muv[:], -1.0, 1.0, ALU.mult, ALU.add, ['bc_tiles'], ['bc_tiles'])
                self.load_weight_bf16(self.Wva, 'Wv', src, D, src_c0=2 * D, scale_bc=muv)
                self.load_weight_bf16(self.W_in, 'W_in', src, 2 * D, src_c0=0, dst_c0=0)
                self.load_weight_bf16(self.W_in, 'W_in', src, 128, src_c0=3 * D, dst_c0=2 * D)
                self.load_weight_bf16(self.W_in, 'W_in', src, D, src_c0=3 * D + 128, dst_c0=2 * D + 128)
                self.load_weight_bf16(self.W_out, 'W_out', self.w_out_dram['rwkv'], D)
            self.S = L("S_rwkv", [128, NKC, 64], F32)
            p.op('pool', lambda e: e.memset(self.S[:], 0.0), [], [('S', j) for j in range(NKC)])
            self.pcar = L("pcar", [128, 25], F32)
            p.op('pool', lambda e: e.memset(self.pcar[:], 0.0), [], [('pcar', i) for i in range(25)])
            self.pm_ext = [L(f"pm_ext{i}", [128, TT + 1], F32) for i in range(2)]
            self.pm_i = 0
            names = ['r', 'k', 'tmp', 'sigw', 'a', 'kk', 'rn', 'kmod', 'bbv', 'c']
            self.tmp = {'lo': L("w_lo", [128, TT], F32)}
            self.tmpP = [{n: L(f"w_{n}0", [128, TT], F32) for n in names}, None]
            self.tmp2 = [dict(), dict()]
            for n in ['khat', 'bhat']:
                self.tmp2[0][n] = L(f"w_{n}0", [128, TT], F32)
            for n in ['rt_bf', 'bt_bf', 'at_bf', 'kt_h0', 'kt_h1', 'bt_h0', 'bt_h1', 'at_h0', 'at_h1']:
                self.tmp2[0][n] = L(f"w_{n}0", [128, TT], BF16)
            self.hm = L("hm", [128, 2], F32)
            p.op('pool', lambda e: e.memset(self.hm[:], 0.0), [], ['masks'])
            p.op('pool', lambda e: e.memset(self.hm[0:64, 0:1], 1.0), ['masks'], ['masks'])
            p.op('pool', lambda e: e.memset(self.hm[64:128, 1:2], 1.0), ['masks'], ['masks'])
            self.pt = {n: [L(f"wp_{n}{q}", [128, TT], F32 if n in ('at', 'rt') else BF16) for q in range(2)] for n in ['at', 'rt', 'rkr', 'sgate']}
            self.blockones_bf = L("blockones_bf", [128, 128], BF16)
            self.copy('dve', self.blockones_bf[:], self.blockones[:], ['masks'], ['masks'])
            self.v_bf = [L(f"v_bf{q}", [128, NB, 128], BF16) for q in range(2)]
            self.dec = [L(f"dec{q}", [128, NB], F32) for q in range(3)]

            def post_setup():
                self.tmpP[1] = {n: L(f"w_{n}1", [128, TT], F32) for n in names}
                for qq in range(2, self.NDEEP):
                    self.NT.append(L(f"NT{qq}", [128, NCHN, 128], NDT))
                    self.Aak.append(L(f"Aak{qq}", [128, NCHN, 128], BF16))
                    self.Ark.append(L(f"Ark{qq}", [128, NCHN, 128], BF16))
                    self.Arb.append(L(f"Arb{qq}", [128, NCHN, 128], BF16))
                    self.khm.append([L(f"khm{qq}{b}", [128, 128], BF16) for b in range(NB)])
                    self.bhm.append([L(f"bhm{qq}{b}", [128, 128], BF16) for b in range(NB)])
                self.PbP[1] = [L(f"Pb1{i}", [128, NCHN, 128], NDT) for i in range(2)]
                self.QbP[1] = [L(f"Qb1{i}", [128, NCHN, 128], NDT) for i in range(2)]
                for n in ['khat', 'bhat']:
                    self.tmp2[1][n] = L(f"w_{n}1", [128, TT], F32)
                for n in ['rt_bf', 'bt_bf', 'at_bf', 'kt_h0', 'kt_h1', 'bt_h0', 'bt_h1', 'at_h0', 'at_h1']:
                    self.tmp2[1][n] = L(f"w_{n}1", [128, TT], BF16)
                for n in ['rkr', 'sgate']:
                    self.pt[n].append(L(f"wp_{n}2", [128, TT], BF16))
                for n in ['at', 'rt']:
                    self.pt[n].append(self.pt[n][0])
                self.ysb = [L(f"ysb{i}", [128, 256], F32) for i in range(2)]
                self.yn2 = [self.yn, L("yn1", [128, 128], F32)]
                self.bon2 = [self.bon, L("bon1", [128, 128], F32)]
                self.gst2 = [self.gst, L("gst1", [128, 12], F32)]
                self.junk2 = [self.junk, L("junk1", [128, 64], F32)]
                self.bcount = 0
                self.v_bf.append(L("v_bf2", [128, NB, 128], BF16))
            self.post_setup = post_setup
            NCHN = NB * 2
            self.PbP = [[L(f"Pb0{i}", [128, NCHN, 128], NDT) for i in range(2)], None]
            self.QbP = [[L(f"Qb0{i}", [128, NCHN, 128], NDT) for i in range(2)], None]
            self.NT = [L(f"NT{q}", [128, NCHN, 128], NDT) for q in range(2)]
            self.Aak = [L(f"Aak{q}", [128, NCHN, 128], BF16) for q in range(2)]
            self.Ark = [L(f"Ark{q}", [128, NCHN, 128], BF16) for q in range(2)]
            self.Arb = [L(f"Arb{q}", [128, NCHN, 128], BF16) for q in range(2)]
            self.NDEEP = 2
            self.ident4 = L("ident4", [128, NCHN, 128], NDT)
            for c in range(NCHN):
                self.copy('dve', self.ident4[:, c, :], self.ident[:], ['ident'], ['ident'])
            self.khm = [[L(f"khm{q}{b}", [128, 128], BF16) for b in range(NB)] for q in range(2)]
            self.bhm = [[L(f"bhm{q}{b}", [128, 128], BF16) for b in range(NB)] for q in range(2)]
            self.Z_sb = L("Z_sb", [128, 128], NDT)
            self.U_bf = L("U_bf", [128, 128], BF16)
            self.yn = L("yn", [128, 128], F32)
            self.bon = L("bon", [128, 128], F32)
            self.gst = L("gst", [128, 12], F32)
            self.junk = L("junk", [128, 64], F32)
            self.ps_pool = [0, 1]
            return loader

        NMB = [[2, 3], [4, 7]]
        self.nm_rrs = [0, 0]

        def nm_ps(q=0):
            i = NMB[q][self.nm_rrs[q] % 2]
            self.nm_rrs[q] += 1
            return self.psums[i], f"ps{i}"

        def shift(ps, pk, dst, dk, idx, mt):
            pm = self.pm_ext[self.pm_i % 2]
            pmk = f"pm_ext{self.pm_i % 2}"
            self.pm_i += 1
            ck = ('pcar', idx)
            self.copy('pool', pm[:, 0:1], self.pcar[:, idx:idx + 1], [ck], [pmk])
            self.act(pm[:, 1:TT + 1], ps[:, :TT], AF.Copy, [pk, 'rw_vecs'], [pmk], scale=self.mu_fm[:, mt:mt + 1])
            self.copy('pool', self.pcar[:, idx:idx + 1], pm[:, TT:TT + 1], [pmk], [ck])
            self.act(dst, ps[:, :TT], AF.Copy, [pk, 'rw_vecs'], [dk], scale=self.omu_fm[:, mt:mt + 1])
            self.tt('dve', dst, dst, pm[:, 0:TT], ALU.add, [dk, pmk], [dk])

        def proj(col0):
            ps, pk = self.next_ps()
            for kc in range(NKC):
                self.mm(ps[:, :TT], self.W_in[:, kc, col0:col0 + 128], self.hn[:, kc, 1:TT + 1], kc == 0, kc == NKC - 1, ['W_in', self.hnk], [pk])
            return ps, pk

        hsl = [slice(0, 64), slice(64, 128)]

        def A_gen(j):
            V = self.vecs
            q = j % 2
            q3 = j % 3
            q4 = j % self.NDEEP
            t = dict(self.tmp)
            t.update(self.tmpP[q])
            t['e1'] = t['rn']
            t['e2'] = t['tmp']
            t.update(self.tmp2[q])
            self.Pb, self.Qb = self.PbP[q], self.QbP[q]
            PAR = set(self.tmp2[0].keys())
            jc = slice(j * 128, (j + 1) * 128)
            at, rt, rkr, sgate = self.pt['at'][q], self.pt['rt'][q], self.pt['rkr'][q3], self.pt['sgate'][q3]
            kat, krt, krkr, ksg = f'p_at{q}', f'p_rt{q}', f'p_rkr{q3}', f'p_sgate{q3}'
            v_bf, dec = self.v_bf[q3], self.dec[q3]
            kv, kvb, kdec = f'v_sb{q3}', f'v_bf{q3}', f'dec{q3}'
            ps, pk = proj(j * 128)
            shift(ps, pk, t['r'][:], f't_r{q}', j, j)
            ps, pk = proj(D + j * 128)
            shift(ps, pk, t['k'][:], f't_k{q}', 8 + j, 8 + j)
            yield
            ps, pk = proj(2 * D + 128 + j * 128)
            shift(ps, pk, t['tmp'][:], f't_tmp{q}', 16 + j, 25 + j)
            self.act(t['sigw'][:], t['tmp'][:], AF.Exp, [f't_tmp{q}'], [f't_sigw{q}'], scale=-1.0)
            self.act(t['sigw'][:], t['sigw'][:], AF.Ln, [f't_sigw{q}', 'consts'], [f't_sigw{q}'], bias=self.epsc[:, 2:3])
            self.act(t['sigw'][:], t['sigw'][:], AF.Exp, [f't_sigw{q}'], [f't_sigw{q}'], scale=-1.0)
            self.tt('pool', sgate[:], t['tmp'][:], t['sigw'][:], ALU.mult, [f't_tmp{q}', f't_sigw{q}'], [ksg])
            pv, pvk = self.next_ps()
            for blk in range(NB):
                n = 0
                for kc in range(NKC):
                    for (Wv, off) in ((self.Wva, 1), (self.Wvb, 0)):
                        self.mm(pv[:, blk * 128:(blk + 1) * 128], self.hn[:, kc, off + blk * 128:off + (blk + 1) * 128], Wv[:, kc, jc],
                                n == 0, n == 2 * NKC - 1, ['Wv', self.hnk], [pvk])
                        n += 1
            self.copy('dve', v_bf[:].rearrange("p b v -> p (b v)"), pv[:, :TT], [pvk], [kvb])
            yield
            pw, pwk = self.next_ps()
            self.mm(pw[:, :TT], self.lw2[0:64, jc], self.lo_bf[0:64, :], True, True, ['lw2', 't_lo_bf'], [pwk])
            self.act(t['sigw'][:], pw[:, :TT], AF.Exp, [pwk, 'rw_vecs'], [f't_sigw{q}'], bias=self.nvecs[:, 0, j:j + 1], scale=-1.0)
            self.act(t['sigw'][:], t['sigw'][:], AF.Ln, [f't_sigw{q}', 'consts'], [f't_sigw{q}'], bias=self.epsc[:, 2:3])
            self.act(t['sigw'][:], t['sigw'][:], AF.Exp, [f't_sigw{q}'], [f't_sigw{q}'], scale=-1.0)
            pa, pak = self.next_ps()
            self.mm(pa[:, :TT], self.lw2[64:128, jc], self.lo_bf[64:128, :], True, True, ['lw2', 't_lo_bf'], [pak])
            self.act(t['a'][:], pa[:, :TT], AF.Exp, [pak, 'rw_vecs'], [f't_a{q}'], bias=self.nvecs[:, 1, j:j + 1], scale=-1.0)
            self.act(t['a'][:], t['a'][:], AF.Ln, [f't_a{q}', 'consts'], [f't_a{q}'], bias=self.epsc[:, 2:3])
            self.act(t['a'][:], t['a'][:], AF.Exp, [f't_a{q}'], [f't_a{q}'], scale=-1.0)
            self.ts('dve', t['kk'][:], t['k'][:], V[:, 2, j:j + 1], None, ALU.mult, None, [f't_k{q}', 'rw_vecs'], [f't_kk{q}'])
            self.tt('pool', t['tmp'][:], t['kk'][:], t['kk'][:], ALU.mult, [f't_kk{q}'], [f't_tmp{q}'])
            pn, pnk = self.next_ps()
            self.mm(pn[:, :TT], self.blockones[:], t['tmp'][:], True, True, ['masks', f't_tmp{q}'], [pnk])
            self.ts('dve', t['rn'][:], pn[:, :TT], 1e-24, None, ALU.max, None, [pnk], [f't_rn{q}'])
            self.act(t['rn'][:], t['rn'][:], AF.Ln, [f't_rn{q}'], [f't_rn{q}'])
            self.act(t['rn'][:], t['rn'][:], AF.Exp, [f't_rn{q}'], [f't_rn{q}'], scale=-0.5)
            self.tt('pool', t['kk'][:], t['kk'][:], t['rn'][:], ALU.mult, [f't_kk{q}', f't_rn{q}'], [f't_kk{q}'])
            self.ts('dve', t['tmp'][:], t['a'][:], -1.0, V[:, 3, j:j + 1], ALU.add, ALU.mult, [f't_a{q}', 'rw_vecs'], [f't_tmp{q}'])
            self.stt('dve', t['kmod'][:], t['tmp'][:], 1.0, t['k'][:], ALU.add, ALU.mult, [f't_tmp{q}', f't_k{q}'], [f't_kmod{q}'])
            self.tt('pool', t['bbv'][:], t['kk'][:], t['a'][:], ALU.mult, [f't_kk{q}', f't_a{q}'], [f't_bbv{q}'])
            yield
            self.p.op('dve', lambda e: e.tensor_tensor_scan(out=t['c'][:], data0=self.resetm[:], data1=t['sigw'][:], initial=0.0,
                                                            op0=ALU.mult, op1=ALU.add), [f't_sigw{q}', 'masks'], [f't_c{q}'])
            self.act(t['e1'][:], t['c'][:], AF.Exp, [f't_c{q}'], [f't_rn{q}'], scale=LC)
            self.tt('pool', rt[:], t['r'][:], t['e1'][:], ALU.mult, [f't_r{q}', f't_rn{q}'], [krt])
            self.copy('act', t['rt_bf'][:], rt[:], [krt], [f't_rt_bf{q}'])
            self.act(t['e2'][:], t['c'][:], AF.Exp, [f't_c{q}'], [f't_tmp{q}'], scale=-LC)
            for hd in range(2):
                self.stt('dve', t[f'kt_h{hd}'][:], t['kmod'][:], self.hm[:, hd:hd + 1], t['e2'][:], ALU.mult, ALU.mult,
                         [f't_kmod{q}', f't_tmp{q}', 'masks'], [f't_kt_h{hd}_{q}'])
                self.stt('dve', t[f'bt_h{hd}'][:], t['bbv'][:], self.hm[:, hd:hd + 1], t['e2'][:], ALU.mult, ALU.mult,
                         [f't_bbv{q}', f't_tmp{q}', 'masks'], [f't_bt_h{hd}_{q}'])
            self.tt('pool', t['bt_bf'][:], t['bbv'][:], t['e2'][:], ALU.mult, [f't_bbv{q}', f't_tmp{q}'], [f't_bt_bf{q}'])
            self.tt('pool', t['e1'][:], t['c'][:], t['sigw'][:], ALU.subtract, [f't_c{q}', f't_sigw{q}'], [f't_rn{q}'])
            self.act(t['e1'][:], t['e1'][:], AF.Exp, [f't_rn{q}'], [f't_rn{q}'], scale=LC)
            self.stt('dve', at[:], t['kk'][:], -1.0, t['e1'][:], ALU.mult, ALU.mult, [f't_kk{q}', f't_rn{q}'], [kat])
            self.copy('act', t['at_bf'][:], at[:], [kat], [f't_at_bf{q}'])
            for hd in range(2):
                self.act(t[f'at_h{hd}'][:], at[:], AF.Copy, [kat, 'masks'], [f't_at_h{hd}_{q}'], scale=self.hm[:, hd:hd + 1])
            yield
            c3 = t['c'][:].rearrange("p (n c) -> p n c", c=128)
            self.tt('pool', t['e2'][:].rearrange("p (n c) -> p n c", c=128), c3[:, :, 127:128].to_broadcast([128, NB, 128]), c3,
                    ALU.subtract, [f't_c{q}'], [f't_tmp{q}'])
            self.act(t['e2'][:], t['e2'][:], AF.Exp, [f't_tmp{q}'], [f't_tmp{q}'], scale=LC)
            self.act(dec[:], c3[:, :, 127], AF.Exp, [f't_c{q}'], [kdec], scale=LC)
            self.tt('pool', t['khat'][:], t['kmod'][:], t['e2'][:], ALU.mult, [f't_kmod{q}', f't_tmp{q}'], [f't_khat{q}'])
            self.tt('dve', t['bhat'][:], t['bbv'][:], t['e2'][:], ALU.mult, [f't_bbv{q}', f't_tmp{q}'], [f't_bhat{q}'])
            self.stt('dve', rkr[:], t['r'][:], V[:, 4, j:j + 1], t['kmod'][:], ALU.mult, ALU.mult, [f't_r{q}', 'rw_vecs', f't_kmod{q}'], [krkr])
            yield
            NCH = NB * 2
            specs = {'P': ('bt_h', 'at_bf', self.maskS, self.Pb[0], f'Pb{q}0'),
                     'Q': ('at_h', 'bt_bf', self.maskSL, self.Qb[0], f'Qb{q}0'),
                     'ak': ('kt_h', 'at_bf', self.maskS, self.Aak[q4], f'Aak{q4}'),
                     'rk': ('kt_h', 'rt_bf', self.maskI, self.Ark[q4], f'Ark{q4}'),
                     'rb': ('bt_h', 'rt_bf', self.maskI, self.Arb[q4], f'Arb{q4}')}
            for name in ('P', 'Q', 'ak', 'rk', 'rb'):
                lh, rh, mask, dst, dk = specs[name]
                ps, pk = nm_ps(q)
                for c in range(NCH):
                    blk, hd = c // 2, c % 2
                    cs = slice(blk * 128, (blk + 1) * 128)
                    self.mm(ps[:, c * 128:(c + 1) * 128], t[f'{lh}{hd}'][:, cs], t[rh][:, cs], True, True, [f't_{lh}{hd}_{q}', f't_{rh}{q}'], [pk])
                self.tt('dve', dst[:], ps[:, 0:NCH * 128].rearrange("p (c t) -> p c t", c=NCH),
                        mask[:, None, :].to_broadcast([128, NCH, 128]), ALU.mult, [pk, 'masks'], [dk])
                if name == 'Q':
                    self.tt('pool', self.NT[q4][:], self.ident4[:], self.Pb[0][:], ALU.add, ['ident', f'Pb{q}0'], [f'NT{q4}'])
                    yield
            for blk in range(NB):
                cs = slice(blk * 128, (blk + 1) * 128)
                for (srcn, dst, dk) in (('khat', self.khm[q4][blk], f'khm{q4}{blk}'), ('bhat', self.bhm[q4][blk], f'bhm{q4}{blk}')):
                    ps, pk = nm_ps(q)
                    self.p.op('pe', lambda e, ps=ps, srcn=srcn, cs=cs: e.transpose(ps[:, 0:128], t[srcn][:, cs], self.ident[:]),
                              [f't_{srcn}{q}', 'ident'], [pk])
                    self.copy('act', dst[:], ps[:, 0:128], [pk], [dk])
            yield
            NTq, kNT = self.NT[q4], f'NT{q4}'
            for i in range(6):
                a_, b_ = i % 2, (i + 1) % 2
                Pa, Qa, Pn, Qn = self.Pb[a_], self.Qb[a_], self.Pb[b_], self.Qb[b_]
                kPa, kQa, kPn, kQn = f'Pb{q}{a_}', f'Qb{q}{a_}', f'Pb{q}{b_}', f'Qb{q}{b_}'
                if i < 5:
                    ps, pk = nm_ps(q)
                    for c in range(NCH):
                        self.mm(ps[:, c * 128:(c + 1) * 128], Qa[:, c, :], Pa[:, c, :], True, True, [kQa, kPa], [pk])
                    self.copy('act', Pn[:].rearrange("p c t -> p (c t)"), ps[:, 0:NCH * 128], [pk], [kPn])
                ps, pk = nm_ps(q)
                for c in range(NCH):
                    self.mm(ps[:, c * 128:(c + 1) * 128], Pa[:, c, :], Qa[:, c, :], True, True, [kPa, kQa], [pk])
                self.copy('dve' if i % 2 == 0 else 'act', Qn[:].rearrange("p c t -> p (c t)"), ps[:, 0:NCH * 128], [pk], [kQn])
                yield
                ps, pk = nm_ps(q)
                for c in range(NCH):
                    self.mm(ps[:, c * 128:(c + 1) * 128], Qn[:, c, :], NTq[:, c, :], True, True, [kQn, kNT], [pk])
                self.tt('dve', NTq[:].rearrange("p c t -> p (c t)"), NTq[:].rearrange("p c t -> p (c t)"), ps[:, 0:NCH * 128], ALU.add,
                        [kNT, pk], [kNT])
                yield

        def B_gen(j):
            t = self.tmp
            P = self.psums
            q = j % 2
            q3 = j % 3
            q4 = j % self.NDEEP
            jc = slice(j * 128, (j + 1) * 128)
            at, rt, rkr, sgate = self.pt['at'][q], self.pt['rt'][q], self.pt['rkr'][q3], self.pt['sgate'][q3]
            kat, krt, krkr, ksg = f'p_at{q}', f'p_rt{q}', f'p_rkr{q3}', f'p_sgate{q3}'
            v_bf, dec = self.v_bf[q3], self.dec[q3]
            kv, kvb, kdec = f'v_sb{q3}', f'v_bf{q3}', f'dec{q3}'
            kS = ('S', j)
            for blk in range(NB):
                cs = slice(blk * 128, (blk + 1) * 128)
                khm, bhm = self.khm[q4][blk], self.bhm[q4][blk]
                kkh, kbh = f'khm{q4}{blk}', f'bhm{q4}{blk}'
                pz, pzk = P[5], 'ps5'
                for hd in range(2):
                    hs, hc, c = hsl[hd], slice(hd * 64, (hd + 1) * 64), blk * 2 + hd
                    self.mm(pz[:, hc], self.Aak[q4][:, c, :], v_bf[:, blk, hc], True, False, [f'Aak{q4}', kvb], [pzk])
                    self.mm(pz[:, hc], at[hs, cs], self.S[hs, j, :], False, True, [kat, kS], [pzk])
                self.copy('act', self.Z_sb[:], pz[:, 0:128], [pzk], ['Z_sb'])
                yield
                for hd in range(2):
                    hc, c = slice(hd * 64, (hd + 1) * 64), blk * 2 + hd
                    self.mm(pz[:, hc], self.NT[q4][:, c, :], self.Z_sb[:, hc], True, True, [f'NT{q4}', 'Z_sb'], [pzk])
                self.copy('act', self.U_bf[:], pz[:, 0:128], [pzk], ['U_bf'])
                yield
                self.mm(pz[:, 0:128], khm[:], v_bf[:, blk, :], True, False, [kkh, kvb], [pzk])
                self.mm(pz[:, 0:128], bhm[:], self.U_bf[:], False, True, [kbh, 'U_bf'], [pzk])
                py, pyk = P[6], 'ps6'
                for hd in range(2):
                    hs, hc, c = hsl[hd], slice(hd * 64, (hd + 1) * 64), blk * 2 + hd
                    yc = slice(hd * 128, hd * 128 + 64)
                    bc_ = slice(hd * 128 + 64, hd * 128 + 128)
                    self.mm(py[:, yc], self.Ark[q4][:, c, :], v_bf[:, blk, hc], True, False, [f'Ark{q4}', kvb], [pyk])
                    self.mm(py[:, yc], rt[hs, cs], self.S[hs, j, :], False, False, [krt, kS], [pyk])
                    self.mm(py[:, yc], self.Arb[q4][:, c, :], self.U_bf[:, hc], False, True, [f'Arb{q4}', 'U_bf'], [pyk])
                    self.mm(py[:, bc_], rkr[hs, cs], self.blockones_bf[hs, hs], True, True, [krkr, 'masks'], [pyk])
                for hd in range(2):
                    hs, hc = hsl[hd], slice(hd * 64, (hd + 1) * 64)
                    self.stt('dve', self.S[hs, j, :], self.S[hs, j, :], dec[hs, blk:blk + 1], pz[hs, hc], ALU.mult, ALU.add,
                             [kS, kdec, pzk], [kS])
                yield
                bp = self.bcount % 2
                self.bcount += 1
                ysb, kys = self.ysb[bp], f'ysb{bp}'
                g, kg = self.gst2[bp], f'gst{bp}'
                yn, kyn = self.yn2[bp], f'yn{bp}'
                bon, kbon = self.bon2[bp], f'bon{bp}'
                junk, kjunk = self.junk2[bp], f'junk{bp}'
                self.copy('act', ysb[:], py[:, 0:256], [pyk], [kys])
                for hd in range(2):
                    hc = slice(hd * 64, (hd + 1) * 64)
                    yc = slice(hd * 128, hd * 128 + 64)
                    bc_ = slice(hd * 128 + 64, hd * 128 + 128)
                    self.act(junk[:], ysb[:, yc], AF.Identity, [kys], [kjunk, kg], accum_out=g[:, hd:hd + 1])
                    self.act(junk[:], ysb[:, yc], AF.Square, [kys], [kjunk, kg], accum_out=g[:, 2 + hd:3 + hd])
                    self.tt('pool', bon[:, hc], ysb[:, bc_], v_bf[:, blk, hc], ALU.mult, [kys, kvb], [kbon])
                self.ts('dve', g[:, 4:6], g[:, 0:2], 1.0 / 64, None, ALU.mult, None, [kg], [kg])
                self.tt('dve', g[:, 6:8], g[:, 4:6], g[:, 4:6], ALU.mult, [kg], [kg])
                self.stt('dve', g[:, 8:10], g[:, 2:4], 1.0 / 64, g[:, 6:8], ALU.mult, ALU.subtract, [kg], [kg])
                self.act(g[:, 8:10], g[:, 8:10], AF.Ln, [kg, 'consts'], [kg], bias=self.epsc[:, 1:2])
                self.act(g[:, 8:10], g[:, 8:10], AF.Exp, [kg], [kg], scale=-0.5)
                for hd in range(2):
                    hc = slice(hd * 64, (hd + 1) * 64)
                    yc = slice(hd * 128, hd * 128 + 64)
                    self.ts('dve', yn[:, hc], ysb[:, yc], g[:, 4 + hd:5 + hd], g[:, 8 + hd:9 + hd], ALU.subtract, ALU.mult,
                            [kys, kg], [kyn])
                yield
                self.tt('pool', yn[:], yn[:], self.gng_bc[:, jc], ALU.mult, [kyn, 'bc_tiles'], [kyn])
                self.tt('pool', yn[:], yn[:], self.gnb_bc[:, jc], ALU.add, [kyn, 'bc_tiles'], [kyn])
                self.tt('pool', yn[:], yn[:], bon[:], ALU.add, [kyn, kbon], [kyn])
                ps, pk = nm_ps(q)
                self.p.op('pe', lambda e, ps=ps, yn=yn: e.transpose(ps[:, 0:128], yn[:], self.ident[:]), [kyn, 'ident'], [pk])
                self.tt('dve', self.yTt[:, j, cs], ps[:, 0:128], sgate[:, cs], ALU.mult, [pk, ksg], [self.yTk])
                yield

        def drive(gens):
            gens = [g for g in gens if g is not None]
            while gens:
                for g in list(gens):
                    try:
                        next(g)
                    except StopIteration:
                        gens.remove(g)

        def tile(ti):
            t = self.tmp
            ps, pk = proj(2 * D)
            shift(ps, pk, t['lo'][:], 't_lo', 24, 24)
            lo64 = t['lo'][0:64, :]
            self.act(lo64, lo64, AF.Exp, ['t_lo'], ['t_lo'], scale=-2.0)
            self.act(lo64, lo64, AF.Ln, ['t_lo', 'consts'], ['t_lo'], bias=self.epsc[0:64, 2:3])
            self.act(lo64, lo64, AF.Exp, ['t_lo'], ['t_lo'], scale=-1.0)
            self.ts('dve', lo64, lo64, 2.0, -1.0, ALU.mult, ALU.add, ['t_lo'], ['t_lo'])
            self.copy('dve', self.lo_bf[:], t['lo'][:], ['t_lo'], ['t_lo_bf'])
            for j in range(NKC):
                drive([A_gen(j)])
                drive([B_gen(j)])

        self.run_layer(li, 'rwkv', TT, WC, setup, tile, is_last)
        self.ps_pool = list(range(8))

    def build(self):
        nc = self.nc
        T = self.T
        self.xT = self.din("xT", [D, T])
        self.yT = nc.dram_tensor("yT", [D, T], F32, kind="ExternalOutput").ap()
        d_norm_g = self.din("norm_g", [128, 4, NKC])
        d_final_g = self.din("final_g", [128, NKC])
        self.w_in_dram, self.w_out_dram = {}, {}
        kinds = [k for (_, k) in self.layers]
        if 'conv' in kinds:
            self.w_in_dram['conv'] = self.din("conv_w_in", [D, 4 * D])
            self.w_out_dram['conv'] = self.din("conv_w_out", [D, D])
            d_conv_w = self.din("conv_w", [128, NKC, 3])
        if 'rwkv' in kinds:
            self.w_in_dram['rwkv'] = self.din("rwkv_w_in", [D, 4 * D + 128])
            self.w_out_dram['rwkv'] = self.din("rwkv_w_out", [D, D])
            self.din("rwkv_mu_fm", [128, 33])
            self.din("rwkv_mu", [4 * D + 128])
            self.din("rwkv_vecs", [128, 5, NKC])
            self.din("rwkv_lw2", [128, D])
            self.din("rwkv_gn_g", [D])
            self.din("rwkv_gn_b", [D])
        if 'hgrn' in kinds:
            self.w_in_dram['hgrn'] = self.din("hgrn_w_in", [D, 4 * D])
            self.w_out_dram['hgrn'] = self.din("hgrn_w_out", [D, D])
            self.din("hgrn_gn_g", [D])
            self.din("hgrn_lbl", [128, 4, NKC])
        if 'gmlp' in kinds:
            self.w_in_dram['gmlp'] = self.din("gmlp_w_in", [D, 3 * D])
            self.w_out_dram['gmlp'] = self.din("gmlp_w_out", [D, D])
            self.din("gmlp_wsT", [128, 8, 128])
            self.din("gmlp_bs", [8, 128])
            self.din("gmlp_vg", [D])
        with ExitStack() as es:
            self.es = es
            nc.allow_low_precision("bf16 matmul operands, fp32 accumulation")
            self.p = p = Prog(nc, es)
            self.psums = [es.enter_context(nc.psum_tensor(f"ps{i}", [128, 512], F32)) for i in range(8)]
            self.ps_rr = 0
            self.ps_pool = list(range(8))
            self.ones_bf = self.sb("ones_bf", [128, 128], BF16)
            self.epsc = self.sb("epsc", [128, 4], F32)
            self.norm_g = self.sb("norm_g_sb", [128, 4, NKC], F32)
            self.final_g = self.sb("final_g_sb", [128, NKC], F32)
            p.op('pool', lambda e: e.memset(self.ones_bf[:], 1.0), [], ['ones_bf'])
            p.op('pool', lambda e: e.memset(self.epsc[:, 0:1], RMS_EPS), [], ['consts'])
            p.op('pool', lambda e: e.memset(self.epsc[:, 2:3], 1.0), ['consts'], ['consts'])
            p.dma('sp', self.norm_g[:], d_norm_g, [], ['consts'])
            p.dma('sp', self.final_g[:], d_final_g, [], ['consts'])
            if 'conv' in kinds:
                self.conv_w = self.sb("conv_w_sb", [128, NKC, 3], F32)
                p.dma('sp', self.conv_w[:], d_conv_w, [], ['consts'])
            self.first_layer = True
            for n, (li, kind) in enumerate(self.layers):
                is_last = n == len(self.layers) - 1
                if kind == 'conv':
                    self.conv_layer(li, is_last)
                elif kind == 'gmlp':
                    self.gmlp_layer(li, is_last)
                elif kind == 'hgrn':
                    self.hgrn_layer(li, is_last)
                elif kind == 'rwkv':
                    self.rwkv_layer(li, is_last)
                else:
                    raise ValueError(kind)
            p.finish('sp')
            self.stats = (p.n_ins, p.n_wait)
        return nc


def prep_inputs(inp, b, layers):
    f = np.float32
    m = {}
    m["xT"] = np.ascontiguousarray(np.asarray(inp["x"][b], f).T)
    m["norm_g"] = np.ascontiguousarray(np.asarray(inp["norm_g"], f).reshape(4, NKC, 128).transpose(2, 0, 1))
    m["final_g"] = np.ascontiguousarray(np.asarray(inp["final_g"], f).reshape(NKC, 128).T)
    kinds = [k for (_, k) in layers]
    if 'conv' in kinds:
        m["conv_w_in"] = np.ascontiguousarray(np.asarray(inp["conv_w_in"][0], f))
        m["conv_w_out"] = np.ascontiguousarray(np.asarray(inp["conv_w_out"][0], f))
        m["conv_w"] = np.ascontiguousarray(np.asarray(inp["conv_w"][0], f).reshape(3, NKC, 128).transpose(2, 1, 0))
    if 'rwkv' in kinds:
        m["rwkv_w_in"] = np.ascontiguousarray(np.asarray(inp["rwkv_w_in"][0], f))
        m["rwkv_w_out"] = np.ascontiguousarray(np.asarray(inp["rwkv_w_out"][0], f))
        mu = np.asarray(inp["rwkv_mu"][0], f)
        m["rwkv_mu"] = np.ascontiguousarray(mu)
        m["rwkv_mu_fm"] = np.ascontiguousarray(mu.reshape(33, 128).T)
        vecs = np.stack([np.asarray(inp[k][0], f).reshape(NKC, 128) for k in
                         ("rwkv_w0", "rwkv_a0", "rwkv_k_k", "rwkv_k_a", "rwkv_r_k")], axis=0)
        m["rwkv_vecs"] = np.ascontiguousarray(vecs.transpose(2, 0, 1))
        m["rwkv_lw2"] = np.ascontiguousarray(np.concatenate([np.asarray(inp["rwkv_w_w2"][0], f), np.asarray(inp["rwkv_w_a2"][0], f)], axis=0))
        m["rwkv_gn_g"] = np.ascontiguousarray(np.asarray(inp["rwkv_gn_g"][0], f))
        m["rwkv_gn_b"] = np.ascontiguousarray(np.asarray(inp["rwkv_gn_b"][0], f))
    if 'hgrn' in kinds:
        m["hgrn_w_in"] = np.ascontiguousarray(np.asarray(inp["hgrn_w_in"][0], f))
        m["hgrn_w_out"] = np.ascontiguousarray(np.asarray(inp["hgrn_w_out"][0], f))
        m["hgrn_gn_g"] = np.ascontiguousarray(np.asarray(inp["hgrn_gn_g"][0], f))
        m["hgrn_lbl"] = np.ascontiguousarray(np.asarray(inp["hgrn_lb_logits"], f).reshape(4, NKC, 128).transpose(2, 0, 1))
    if 'gmlp' in kinds:
        m["gmlp_w_in"] = np.ascontiguousarray(np.asarray(inp["gmlp_w_in"][0], f))
        m["gmlp_w_out"] = np.ascontiguousarray(np.asarray(inp["gmlp_w_out"][0], f))
        m["gmlp_wsT"] = np.ascontiguousarray(np.asarray(inp["gmlp_w_s"][0], f).transpose(2, 0, 1))
        m["gmlp_bs"] = np.ascontiguousarray(np.asarray(inp["gmlp_b_s"][0], f))
        m["gmlp_vg"] = np.ascontiguousarray(np.asarray(inp["gmlp_v_g"][0], f))
    return m


FULL_LAYERS = [(0, 'rwkv'), (1, 'hgrn'), (2, 'conv'), (3, 'gmlp')]


def kernel(**inputs):
    x = np.asarray(inputs["x"])
    B, T, _ = x.shape
    layers = FULL_LAYERS
    bld = Builder(T, layers)
    nc = bld.build()
    in_maps = []
    zeros = None
    for c in range(8):
        if c % 2 == 0:
            in_maps.append(prep_inputs(inputs, c // 2, layers))
        else:
            if zeros is None:
                zeros = {k: np.zeros_like(v) for k, v in in_maps[0].items()}
            in_maps.append(zeros)
    res = run_bass_kernel_spmd(nc, in_maps, core_ids=list(range(8)))
    out = np.stack([np.asarray(res.results[2 * b]["yT"]).T for b in range(B)], axis=0)
    return out.astype(np.float32)
```
